# Optimizing a Trainium2 kernel written in Bass

```python
import jax, jax.numpy as jnp
from jax import lax
import numpy as np

D_MODEL = 1024
BATCH = 8
SEQ = 2048
DEPTH = 2
DEC_BATCH = 128
DEC_SEQ = 8
PAST_LEN = 16384
PAGE_SIZE = 128

N_AB = (DEPTH + 1) // 2
N_C = DEPTH // 2
N_SUB = 3
D_MIX = D_MODEL
MLSTM_WIDTH = D_MIX // 2
MLSTM_HEADS = 4
MLSTM_HD = MLSTM_WIDTH // MLSTM_HEADS
MLSTM_CHUNK = 128
RG_WIDTH = D_MIX - MLSTM_WIDTH
RG_BLOCKS = 8
RG_BD = RG_WIDTH // RG_BLOCKS
RG_CONV = 4
RG_C = 8.0
D_IN_AB = 4 * MLSTM_WIDTH + 2 * MLSTM_HEADS + 2 * RG_WIDTH
AB_SPLITS = (MLSTM_WIDTH, 2 * MLSTM_WIDTH, 3 * MLSTM_WIDTH, 4 * MLSTM_WIDTH,
             4 * MLSTM_WIDTH + MLSTM_HEADS, 4 * MLSTM_WIDTH + 2 * MLSTM_HEADS,
             4 * MLSTM_WIDTH + 2 * MLSTM_HEADS + RG_WIDTH)
RWKV_HS = 64
RWKV_HEADS = D_MODEL // RWKV_HS
RWKV_DECAY_LORA = 64
RWKV_A_LORA = 64
RWKV_GATE_LORA = 128
RWKV_LN_EPS = 64e-5
D_FF = ((8 * D_MODEL // 3 + 127) // 128) * 128
ALPHA = (2.0 * DEPTH) ** 0.25
BETA = (8.0 * DEPTH) ** -0.25
LN_EPS = 1e-5
HEAD_NORM_EPS = 1e-6

kernel_name = 'hybrid_mlstm_rglru_rwkv7_decoder_step'


def layer_norm(x, g, b):
    xf = x.astype(jnp.float32)
    mu = jnp.mean(xf, axis=-1, keepdims=True)
    var = jnp.mean(jnp.square(xf - mu), axis=-1, keepdims=True)
    return ((xf - mu) * lax.rsqrt(var + LN_EPS) * g + b).astype(x.dtype)


def modulate(x, shift, scale):
    return x * (1.0 + scale[:, None, :]) + shift[:, None, :]


def post_norm_residual(x, y, gate, res_w, g, b):
    return layer_norm(ALPHA * x + res_w * (1.0 + gate[:, None, :]) * y, g, b)


def swiglu(h, w1, w3, w2):
    return (jax.nn.silu(h @ w1) * (h @ w3)) @ w2


def mlstm_chunk(carry, inp):
    C0, n0, m0 = carry
    q, k, v, ig, lf = inp
    L = q.shape[2]
    b = jnp.cumsum(lf, axis=-1)
    causal = jnp.tril(jnp.ones((L, L), dtype=bool))
    dmat = jnp.where(causal, b[..., :, None] - b[..., None, :] + ig[..., None, :], -jnp.inf)
    m_inter = b + m0[..., None]
    m = jnp.maximum(m_inter, jnp.max(dmat, axis=-1))
    w_inter = jnp.exp(m_inter - m)
    scores = jnp.einsum('bhtd,bhsd->bhts', q, k) * jnp.exp(dmat - m[..., None])
    num = jnp.einsum('bhts,bhsd->bhtd', scores, v) + w_inter[..., None] * jnp.einsum('bhvk,bhtk->bhtv', C0, q)
    den = jnp.sum(scores, axis=-1) + w_inter * jnp.einsum('bhk,bhtk->bht', n0, q)
    h = num / jnp.maximum(jnp.abs(den), jnp.exp(-m))[..., None]
    m_end = m[..., -1]
    w_state = jnp.exp(b[..., -1] + m0 - m_end)
    w_rows = jnp.exp(b[..., -1:] - b + ig - m_end[..., None])
    C_new = w_state[..., None, None] * C0 + jnp.einsum('bhs,bhsv,bhsk->bhvk', w_rows, v, k)
    n_new = w_state[..., None] * n0 + jnp.einsum('bhs,bhsk->bhk', w_rows, k)
    return (C_new, n_new, m_end), h


def mlstm(q, k, v, ig, lf, C0, n0, m0):
    B, S = q.shape[0], q.shape[1]
    L = MLSTM_CHUNK if S % MLSTM_CHUNK == 0 else S
    nc = S // L

    def chunks(t):
        t = t.astype(jnp.float32).reshape((B, nc, L) + t.shape[2:])
        return jnp.moveaxis(jnp.moveaxis(t, 3, 2), 1, 0)

    carry0 = (C0.astype(jnp.float32), n0.astype(jnp.float32), m0.astype(jnp.float32))
    (C1, n1, m1), h = lax.scan(mlstm_chunk, carry0,
                               (chunks(q), chunks(k), chunks(v), chunks(ig), chunks(lf)))
    h = jnp.swapaxes(jnp.moveaxis(h, 0, 1), 2, 3).reshape(B, S, MLSTM_HEADS, MLSTM_HD)
    return h, C1, n1, m1


def head_norm(h, g):
    mu = jnp.mean(h, axis=-1, keepdims=True)
    var = jnp.mean(jnp.square(h - mu), axis=-1, keepdims=True)
    hn = (h - mu) * lax.rsqrt(var + HEAD_NORM_EPS)
    return hn.reshape(h.shape[0], h.shape[1], -1) * g


def causal_dwconv(xr, buf, w, b):
    S = xr.shape[1]
    xp = jnp.concatenate([buf.astype(xr.dtype), xr], axis=1)
    out = sum(w[j] * xp[:, j:j + S] for j in range(RG_CONV)) + b
    return out, xp[:, S:].astype(jnp.float32)


def lin_combine(left, right):
    a1, b1 = left
    a2, b2 = right
    return a1 * a2, a2 * b1 + b2


def rg_lru(xc, h0, w_a, b_a, w_x, b_x, lam):
    B, S, W = xc.shape
    xb = xc.reshape(B, S, RG_BLOCKS, RG_BD)
    r = jax.nn.sigmoid(jnp.einsum('bsgi,gij->bsgj', xb, w_a).reshape(B, S, W) + b_a).astype(jnp.float32)
    i = jax.nn.sigmoid(jnp.einsum('bsgi,gij->bsgj', xb, w_x).reshape(B, S, W) + b_x).astype(jnp.float32)
    log_a = -RG_C * jax.nn.softplus(-lam.astype(jnp.float32)) * r
    a = jnp.exp(log_a)
    u = jnp.sqrt(-jnp.expm1(2.0 * log_a)) * i * xc.astype(jnp.float32)
    u = u.at[:, 0].add(a[:, 0] * h0.astype(jnp.float32))
    _, h = lax.associative_scan(lin_combine, (a, u), axis=1)
    return h, h[:, -1]


def ab_mixer(h, C0, n0, m0, rh0, buf0, w_in, b_gates, mnorm_g, conv_w, conv_b,
             w_a, b_a, w_x, b_x, lam, w_out):
    B, S, _ = h.shape
    z = h @ w_in
    q, k, v, o, ig, fg, xr, gr = jnp.split(z, AB_SPLITS, axis=-1)
    heads = lambda t: t.reshape(B, S, MLSTM_HEADS, MLSTM_HD)
    ig = ig.astype(jnp.float32) + b_gates[0]
    lf = jax.nn.log_sigmoid(fg.astype(jnp.float32) + b_gates[1])
    hm, C1, n1, m1 = mlstm(heads(q), heads(k) * (MLSTM_HD ** -0.5), heads(v), ig, lf, C0, n0, m0)
    hm = head_norm(hm, mnorm_g) * jax.nn.sigmoid(o.astype(jnp.float32))
    xc, buf1 = causal_dwconv(xr, buf0, conv_w, conv_b)
    hr, rh1 = rg_lru(xc, rh0, w_a, b_a, w_x, b_x, lam)
    hr = hr.astype(h.dtype) * jax.nn.gelu(gr)
    y = jnp.concatenate([hm.astype(h.dtype), hr], axis=-1) @ w_out
    return y, (C1, n1, m1, rh1, buf1)


def rwkv7_step(S_, inp):
    r_t, dec_t, k_t, v_t, kk_t, a_t = inp
    sa = jnp.einsum('bhvk,bhk->bhv', S_, -kk_t)
    S_ = (S_ * dec_t[:, :, None, :] + sa[..., None] * (kk_t * a_t)[:, :, None, :]
          + v_t[..., None] * k_t[:, :, None, :])
    return S_, jnp.einsum('bhvk,bhk->bhv', S_, r_t)


def rwkv7_mixer(h, wkv0, prev0, mu, wr, wk, wv, w0, w1, w2, a0, a1, a2, g1, g2,
                k_k, k_a, r_k, lnx_g, lnx_b, wo):
    B, S, D = h.shape
    f32 = jnp.float32
    h_prev = jnp.concatenate([prev0[:, None].astype(h.dtype), h[:, :-1]], axis=1)
    dx = h_prev - h
    xr, xw, xk, xv, xa, xg = [h + dx * mu[j] for j in range(6)]
    heads = lambda t: t.astype(f32).reshape(B, S, RWKV_HEADS, RWKV_HS)
    r = heads(xr @ wr)
    k = heads(xk @ wk)
    v = heads(xv @ wv)
    w_log = -jax.nn.softplus(-(w0 + jnp.tanh(xw @ w1) @ w2).astype(f32)) - 0.5
    decay = heads(jnp.exp(-jnp.exp(w_log)))
    a = heads(jax.nn.sigmoid(a0 + (xa @ a1) @ a2))
    g = jax.nn.sigmoid(xg @ g1) @ g2
    kk = k * k_k.astype(f32).reshape(RWKV_HEADS, RWKV_HS)
    kk = kk / jnp.maximum(jnp.sqrt(jnp.sum(kk * kk, axis=-1, keepdims=True)), 1e-12)
    k = k * (1.0 + (a - 1.0) * k_a.astype(f32).reshape(RWKV_HEADS, RWKV_HS))
    tm = lambda t: jnp.moveaxis(t, 1, 0)
    wkv1, y = lax.scan(rwkv7_step, wkv0.astype(f32), (tm(r), tm(decay), tm(k), tm(v), tm(kk), tm(a)))
    y = jnp.moveaxis(y, 0, 1)
    ym = jnp.mean(y, axis=-1, keepdims=True)
    yv = jnp.mean(jnp.square(y - ym), axis=-1, keepdims=True)
    y = ((y - ym) * lax.rsqrt(yv + RWKV_LN_EPS)).reshape(B, S, D) * lnx_g + lnx_b
    bonus = jnp.sum(r * k * r_k.astype(f32), axis=-1, keepdims=True) * v
    y = y + bonus.reshape(B, S, D)
    out = (y.astype(h.dtype) * g) @ wo
    return out, (wkv1, h[:, -1].astype(f32))


def run_trunk(x, c, states, p):
    mC, mn, mm, rh, rconv, wkv, shift = states
    B = x.shape[0]
    new_ab, new_c = [], []
    for layer in range(DEPTH):
        mod = (jax.nn.silu(c) @ p['ada_w'][layer] + p['ada_b'][layer]).reshape(B, 3 * N_SUB, D_MODEL)
        lg, lb = p['ln_g'][layer], p['ln_b'][layer]
        y = swiglu(modulate(x, mod[:, 0], mod[:, 1]),
                   p['ffn_w1'][layer, 0], p['ffn_w3'][layer, 0], p['ffn_w2'][layer, 0])
        x = post_norm_residual(x, y, mod[:, 2], 0.5, lg[0], lb[0])
        hin = modulate(x, mod[:, 3], mod[:, 4])
        j = layer // 2
        if layer % 2 == 0:
            y, st = ab_mixer(hin, mC[j], mn[j], mm[j], rh[j], rconv[j],
                             p['ab_w_in'][j], p['mlstm_b_gates'][j], p['mlstm_norm_g'][j],
                             p['rg_conv_w'][j], p['rg_conv_b'][j], p['rg_w_a'][j], p['rg_b_a'][j],
                             p['rg_w_x'][j], p['rg_b_x'][j], p['rg_lambda'][j], p['ab_w_out'][j])
            new_ab.append(st)
        else:
            y, st = rwkv7_mixer(hin, wkv[j], shift[j], p['rw_mu'][j], p['rw_wr'][j], p['rw_wk'][j],
                                p['rw_wv'][j], p['rw_w0'][j], p['rw_w1'][j], p['rw_w2'][j],
                                p['rw_a0'][j], p['rw_a1'][j], p['rw_a2'][j], p['rw_g1'][j], p['rw_g2'][j],
                                p['rw_k_k'][j], p['rw_k_a'][j], p['rw_r_k'][j],
                                p['rw_lnx_g'][j], p['rw_lnx_b'][j], p['rw_wo'][j])
            new_c.append(st)
        x = post_norm_residual(x, y, mod[:, 5], 1.0, lg[1], lb[1])
        y = swiglu(modulate(x, mod[:, 6], mod[:, 7]),
                   p['ffn_w1'][layer, 1], p['ffn_w3'][layer, 1], p['ffn_w2'][layer, 1])
        x = post_norm_residual(x, y, mod[:, 8], 0.5, lg[2], lb[2])
    stk = lambda sts, i: jnp.stack([s[i] for s in sts], axis=0)
    return x, (stk(new_ab, 0), stk(new_ab, 1), stk(new_ab, 2), stk(new_ab, 3), stk(new_ab, 4),
               stk(new_c, 0), stk(new_c, 1))


def zero_states(B):
    f = jnp.float32
    return (jnp.zeros((N_AB, B, MLSTM_HEADS, MLSTM_HD, MLSTM_HD), f),
            jnp.zeros((N_AB, B, MLSTM_HEADS, MLSTM_HD), f),
            jnp.zeros((N_AB, B, MLSTM_HEADS), f),
            jnp.zeros((N_AB, B, RG_WIDTH), f),
            jnp.zeros((N_AB, B, RG_CONV - 1, RG_WIDTH), f),
            jnp.zeros((N_C, B, RWKV_HEADS, RWKV_HS, RWKV_HS), f),
            jnp.zeros((N_C, B, D_MODEL), f))


def setup_inputs(seed: int = 0) -> dict:
    key = jax.random.key(seed)
    it = iter(jax.random.split(key, 64))
    f32 = jnp.float32

    def nrm(shape, scale):
        return jax.random.normal(next(it), shape, f32) * scale

    def unif(shape, lo, hi):
        return jax.random.uniform(next(it), shape, f32, lo, hi)

    D = D_MODEL
    inv = D ** -0.5
    d = {}
    d['x_prompt'] = nrm((BATCH, SEQ, D), 1.0)
    d['x_sample'] = nrm((DEC_BATCH, DEC_SEQ, D), 1.0)
    d['c_prompt'] = nrm((BATCH, D), 1.0)
    d['c_sample'] = nrm((DEC_BATCH, D), 1.0)
    d['state_mlstm_C'] = nrm((N_AB, DEC_BATCH, MLSTM_HEADS, MLSTM_HD, MLSTM_HD), 0.1)
    d['state_mlstm_n'] = nrm((N_AB, DEC_BATCH, MLSTM_HEADS, MLSTM_HD), 0.1)
    d['state_mlstm_m'] = unif((N_AB, DEC_BATCH, MLSTM_HEADS), 0.0, 3.0)
    d['state_rglru_h'] = nrm((N_AB, DEC_BATCH, RG_WIDTH), 0.5)
    d['state_rglru_conv'] = nrm((N_AB, DEC_BATCH, RG_CONV - 1, RG_WIDTH), 0.5)
    d['state_rwkv_wkv'] = nrm((N_C, DEC_BATCH, RWKV_HEADS, RWKV_HS, RWKV_HS), 0.1)
    d['state_rwkv_shift'] = nrm((N_C, DEC_BATCH, D), 1.0)
    d['ada_w'] = nrm((DEPTH, D, 3 * N_SUB * D), 0.5 * inv)
    d['ada_b'] = nrm((DEPTH, 3 * N_SUB * D), 0.02)
    d['ln_g'] = 1.0 + nrm((DEPTH, N_SUB, D), 0.05)
    d['ln_b'] = nrm((DEPTH, N_SUB, D), 0.02)
    d['ffn_w1'] = nrm((DEPTH, 2, D, D_FF), inv)
    d['ffn_w3'] = nrm((DEPTH, 2, D, D_FF), inv)
    d['ffn_w2'] = nrm((DEPTH, 2, D_FF, D), BETA * D_FF ** -0.5)
    d['ab_w_in'] = nrm((N_AB, D, D_IN_AB), inv)
    d['mlstm_b_gates'] = jnp.stack([nrm((N_AB, MLSTM_HEADS), 0.1),
                                    unif((N_AB, MLSTM_HEADS), 3.0, 6.0)], axis=1)
    d['mlstm_norm_g'] = 1.0 + nrm((N_AB, MLSTM_WIDTH), 0.05)
    d['rg_conv_w'] = nrm((N_AB, RG_CONV, RG_WIDTH), RG_CONV ** -0.5)
    d['rg_conv_b'] = nrm((N_AB, RG_WIDTH), 0.02)
    d['rg_w_a'] = nrm((N_AB, RG_BLOCKS, RG_BD, RG_BD), RG_BD ** -0.5)
    d['rg_b_a'] = nrm((N_AB, RG_WIDTH), 0.02)
    d['rg_w_x'] = nrm((N_AB, RG_BLOCKS, RG_BD, RG_BD), RG_BD ** -0.5)
    d['rg_b_x'] = nrm((N_AB, RG_WIDTH), 0.02)
    a_base = unif((N_AB, RG_WIDTH), 0.9, 0.999) ** (1.0 / RG_C)
    d['rg_lambda'] = jnp.log(a_base) - jnp.log1p(-a_base)
    d['ab_w_out'] = nrm((N_AB, D_MIX, D), BETA * D_MIX ** -0.5)
    d['rw_mu'] = unif((N_C, 6, D), 0.0, 1.0)
    d['rw_wr'] = nrm((N_C, D, D), inv)
    d['rw_wk'] = nrm((N_C, D, D), inv)
    d['rw_wv'] = nrm((N_C, D, D), inv)
    d['rw_w0'] = unif((N_C, D), -5.0, 1.0)
    d['rw_w1'] = nrm((N_C, D, RWKV_DECAY_LORA), inv)
    d['rw_w2'] = nrm((N_C, RWKV_DECAY_LORA, D), 0.1)
    d['rw_a0'] = nrm((N_C, D), 0.1)
    d['rw_a1'] = nrm((N_C, D, RWKV_A_LORA), inv)
    d['rw_a2'] = nrm((N_C, RWKV_A_LORA, D), 0.1)
    d['rw_g1'] = nrm((N_C, D, RWKV_GATE_LORA), inv)
    d['rw_g2'] = nrm((N_C, RWKV_GATE_LORA, D), RWKV_GATE_LORA ** -0.5)
    d['rw_k_k'] = 0.85 + nrm((N_C, D), 0.05)
    d['rw_k_a'] = 1.0 + nrm((N_C, D), 0.05)
    d['rw_r_k'] = nrm((N_C, RWKV_HEADS, RWKV_HS), 0.1)
    d['rw_lnx_g'] = 1.0 + nrm((N_C, D), 0.05)
    d['rw_lnx_b'] = nrm((N_C, D), 0.02)
    d['rw_wo'] = nrm((N_C, D, D), BETA * inv)
    return d


def reference(x_prompt, x_sample, c_prompt, c_sample,
              state_mlstm_C, state_mlstm_n, state_mlstm_m, state_rglru_h, state_rglru_conv,
              state_rwkv_wkv, state_rwkv_shift,
              ada_w, ada_b, ln_g, ln_b, ffn_w1, ffn_w3, ffn_w2,
              ab_w_in, mlstm_b_gates, mlstm_norm_g, rg_conv_w, rg_conv_b, rg_w_a, rg_b_a,
              rg_w_x, rg_b_x, rg_lambda, ab_w_out,
              rw_mu, rw_wr, rw_wk, rw_wv, rw_w0, rw_w1, rw_w2, rw_a0, rw_a1, rw_a2,
              rw_g1, rw_g2, rw_k_k, rw_k_a, rw_r_k, rw_lnx_g, rw_lnx_b, rw_wo):
    p = dict(ada_w=ada_w, ada_b=ada_b, ln_g=ln_g, ln_b=ln_b, ffn_w1=ffn_w1, ffn_w3=ffn_w3,
             ffn_w2=ffn_w2, ab_w_in=ab_w_in, mlstm_b_gates=mlstm_b_gates, mlstm_norm_g=mlstm_norm_g,
             rg_conv_w=rg_conv_w, rg_conv_b=rg_conv_b, rg_w_a=rg_w_a, rg_b_a=rg_b_a, rg_w_x=rg_w_x,
             rg_b_x=rg_b_x, rg_lambda=rg_lambda, ab_w_out=ab_w_out, rw_mu=rw_mu, rw_wr=rw_wr,
             rw_wk=rw_wk, rw_wv=rw_wv, rw_w0=rw_w0, rw_w1=rw_w1, rw_w2=rw_w2, rw_a0=rw_a0,
             rw_a1=rw_a1, rw_a2=rw_a2, rw_g1=rw_g1, rw_g2=rw_g2, rw_k_k=rw_k_k, rw_k_a=rw_k_a,
             rw_r_k=rw_r_k, rw_lnx_g=rw_lnx_g, rw_lnx_b=rw_lnx_b, rw_wo=rw_wo)
    y_prompt, sp = run_trunk(x_prompt, c_prompt, zero_states(x_prompt.shape[0]), p)
    y_sample, ss = run_trunk(x_sample, c_sample,
                             (state_mlstm_C, state_mlstm_n, state_mlstm_m, state_rglru_h,
                              state_rglru_conv, state_rwkv_wkv, state_rwkv_shift), p)
    p_mC, p_mn, p_mm, p_rh, p_rconv, p_wkv, p_shift = sp
    s_mC, s_mn, s_mm, s_rh, s_rconv, s_wkv, s_shift = ss
    return (y_prompt, y_sample,
            p_mC, p_mn, p_mm, p_rh, p_rconv, p_wkv, p_shift,
            s_mC, s_mn, s_mm, s_rh, s_rconv, s_wkv, s_shift)
```

```python
import contextlib
import numpy as np
import concourse.bass as bass
import concourse.mybir as mybir
from concourse.bass_utils import run_bass_kernel_spmd

F32 = mybir.dt.float32
BF16 = mybir.dt.bfloat16
F32R = mybir.dt.float32r
AF = mybir.ActivationFunctionType
ALU = mybir.AluOpType
AX = mybir.AxisListType

D = 1024
DFF = 2816
NT = 17
NTOK = NT * 128
ALPHA = 4.0 ** 0.25
LN_EPS = 1e-5
NCORES = 8


class Op:
    __slots__ = ("eng", "fn", "deps", "sig", "idx", "dma", "slot", "slot_total", "sigcount", "waits", "tag")


class Sched:
    ENGS = ["pe", "act", "dve", "pool", "sp"]

    def __init__(self, n_slots=40):
        self.q = {e: [] for e in self.ENGS}
        self.last_w = {}
        self.readers = {}
        self.n_slots = n_slots
        self.slot_rr = 0
        self.slot_total = [0] * n_slots
        self.slot_last = [None] * n_slots
        self.all_dma = []

    skip = False

    def add(self, eng, fn, r=(), w=(), sig=True, dma=False, tag=""):
        if self.skip:
            return None
        op = Op()
        op.eng, op.fn, op.sig, op.dma, op.tag = eng, fn, sig, dma, tag
        op.slot = None
        deps = []
        seen = set()

        def dep(o):
            if o is None or id(o) in seen:
                return
            seen.add(id(o))
            if (not dma) and eng == "pe" and o.eng == "pe" and not o.dma:
                return
            deps.append(o)

        for k in r:
            dep(self.last_w.get(k))
        for k in w:
            dep(self.last_w.get(k))
            for o in self.readers.get(k, {}).values():
                dep(o)
        if dma:
            slot = self.slot_rr % self.n_slots
            self.slot_rr += 1
            dep(self.slot_last[slot])
            self.slot_total[slot] += 16
            op.slot = slot
            op.slot_total = self.slot_total[slot]
            self.slot_last[slot] = op
            self.all_dma.append(op)
        op.deps = deps
        self.q[eng].append(op)
        op.idx = len(self.q[eng]) - 1
        key = ("dma", id(op)) if dma else eng
        for k in r:
            self.readers.setdefault(k, {})[key] = op
        for k in w:
            self.last_w[k] = op
            self.readers[k] = {}
        return op

    def barrier(self):
        if self.skip:
            return
        lasts = []
        for e in self.ENGS:
            comp = [o for o in self.q[e] if (not o.dma) and o.fn is not None]
            if comp:
                comp[-1].sig = True
                lasts.append(comp[-1])
        lasts += [o for o in self.slot_last if o is not None]
        for e in self.ENGS:
            op = Op()
            op.eng, op.sig, op.dma, op.tag, op.slot, op.fn = e, False, False, "barrier", None, None
            op.deps = list(lasts)
            self.q[e].append(op)
            op.idx = len(self.q[e]) - 1
        self.last_w = {}
        self.readers = {}

    def finalize(self):
        for e in self.ENGS:
            for o in reversed(self.q[e]):
                if not o.dma and o.fn is not None and e != "sp":
                    o.sig = True
                    break
        self.sigtot = {}
        for e in self.ENGS:
            cnt = 0
            ops = self.q[e]
            pref = []
            for o in ops:
                if (not o.dma) and o.sig:
                    cnt += 1
                pref.append(cnt)
            self.sigtot[e] = cnt
            nxt = None
            for i in range(len(ops) - 1, -1, -1):
                o = ops[i]
                if (not o.dma) and o.sig:
                    nxt = pref[i]
                o.sigcount = nxt if not o.dma else None
        for e in self.ENGS:
            known = {}
            for o in self.q[e]:
                need = {}
                for d in o.deps:
                    if d.dma:
                        key, val = ("slot", d.slot), d.slot_total
                    else:
                        if d.sigcount is None:
                            raise RuntimeError("dependency on op with no later signal: %s" % d.tag)
                        key, val = ("eng", d.eng), d.sigcount
                    if val > need.get(key, 0):
                        need[key] = val
                o.waits = []
                for key, val in need.items():
                    if known.get(key, 0) < val:
                        known[key] = val
                        o.waits.append((key, val))

    def simulate(self):
        pc = {e: 0 for e in self.ENGS}
        sem = {}
        sigc = {e: 0 for e in self.ENGS}
        progress = True
        while progress:
            progress = False
            for e in self.ENGS:
                while pc[e] < len(self.q[e]):
                    o = self.q[e][pc[e]]
                    ok = all(sem.get(k, 0) >= v for k, v in o.waits)
                    if not ok:
                        break
                    if o.dma:
                        sem[("slot", o.slot)] = sem.get(("slot", o.slot), 0) + 16
                    elif o.sig:
                        sem[("eng", e)] = sem.get(("eng", e), 0) + 1
                    pc[e] += 1
                    progress = True
        stuck = {e: (pc[e], len(self.q[e])) for e in self.ENGS if pc[e] < len(self.q[e])}
        if stuck:
            msg = []
            for e, (p, n) in stuck.items():
                o = self.q[e][p]
                msg.append("%s stuck at %d/%d tag=%s waits=%s" % (e, p, n, o.tag, [(k, v, sem.get(k, 0)) for k, v in o.waits]))
            raise RuntimeError("DEADLOCK in wait graph:\n" + "\n".join(msg))

    def emit(self, nc, es):
        self.finalize()
        self.simulate()
        engsem = {e: es.enter_context(nc.semaphore("sem_" + e)) for e in ["pe", "act", "dve", "pool"]}
        slotsem = [es.enter_context(nc.semaphore("slot%d" % i)) for i in range(self.n_slots)]

        def semof(key):
            return engsem[key[1]] if key[0] == "eng" else slotsem[key[1]]

        def run(e, g):
            for o in self.q[e]:
                for key, val in o.waits:
                    g.wait_ge(semof(key), val)
                if o.fn is None:
                    continue
                ins = o.fn(g)
                if o.dma:
                    ins.then_inc(slotsem[o.slot], 16)
                elif o.sig:
                    ins.then_inc(engsem[e], 1)
            if e == "sp":
                for s in range(self.n_slots):
                    if self.slot_total[s] > 0:
                        g.wait_ge(slotsem[s], self.slot_total[s])

        with nc.Block() as blk:
            blk.tensor(lambda g: run("pe", g))
            blk.scalar(lambda g: run("act", g))
            blk.vector(lambda g: run("dve", g))
            blk.gpsimd(lambda g: run("pool", g))
            blk.sync(lambda g: run("sp", g))


class KB:
    def __init__(self, stop_after=None, debug=False):
        self.nc = bass.Bass("TRN2", target_bir_lowering=False)
        self.s = Sched()
        self.stop_after = stop_after
        self.debug = debug
        self.dram = {}

    def din(self, name, shape, dt=F32):
        t = self.nc.dram_tensor(name, list(shape), dt, kind="ExternalInput")
        self.dram[name] = t
        return t.ap()

    def dout(self, name, shape, dt=F32):
        t = self.nc.dram_tensor(name, list(shape), dt, kind="ExternalOutput")
        self.dram[name] = t
        return t.ap()

    def sb(self, es, name, shape, dt=F32):
        self.uid = getattr(self, "uid", 0) + 1
        return es.enter_context(self.nc.sbuf_tensor("%s_%d" % (name, self.uid), list(shape), dt))

    def mm(self, out, lhsT, rhs, start, stop, r, w, sig=None, tag="mm"):
        if sig is None:
            sig = stop
        return self.s.add("pe", lambda g: g.matmul(out, lhsT=lhsT, rhs=rhs, start=start, stop=stop), r=r, w=w, sig=sig, tag=tag)

    def tr(self, out, in_, ident, r, w, sig=True, tag="tr"):
        return self.s.add("pe", lambda g: g.transpose(out, in_, ident), r=r, w=w, sig=sig, tag=tag)

    def act(self, out, in_, func, r, w, bias=None, scale=None, eng="act", tag="act"):
        kw = {}
        if bias is not None:
            kw["bias"] = bias
        if scale is not None:
            kw["scale"] = scale
        return self.s.add("act", lambda g: g.activation(out=out, in_=in_, func=func, **kw), r=r, w=w, tag=tag)

    def tt(self, eng, out, in0, in1, op, r, w, tag="tt"):
        return self.s.add(eng, lambda g: g.tensor_tensor(out=out, in0=in0, in1=in1, op=op), r=r, w=w, tag=tag)

    def ts(self, eng, out, in0, s1, s2, op0, op1, r, w, tag="ts"):
        if op1 is None:
            return self.s.add(eng, lambda g: g.tensor_scalar(out=out, in0=in0, scalar1=s1, scalar2=None, op0=op0), r=r, w=w, tag=tag)
        return self.s.add(eng, lambda g: g.tensor_scalar(out=out, in0=in0, scalar1=s1, scalar2=s2, op0=op0, op1=op1), r=r, w=w, tag=tag)

    def stt(self, out, in0, scalar, in1, op0, op1, r, w, tag="stt"):
        return self.s.add("dve", lambda g: g.scalar_tensor_tensor(out=out, in0=in0, scalar=scalar, in1=in1, op0=op0, op1=op1), r=r, w=w, tag=tag)

    def cp(self, eng, out, in_, r, w, tag="cp"):
        if eng == "act":
            return self.s.add("act", lambda g: g.copy(out=out, in_=in_), r=r, w=w, tag=tag)
        return self.s.add(eng, lambda g: g.tensor_copy(out=out, in_=in_), r=r, w=w, tag=tag)

    def memset(self, eng, ap, val, w, tag="memset"):
        return self.s.add(eng, lambda g: g.memset(ap, val), r=(), w=w, tag=tag)

    def dma(self, q, out, in_, r, w, tag="dma", **kw):
        return self.s.add(q, lambda g: g.dma_start(out=out, in_=in_, **kw), r=r, w=w, dma=True, tag=tag)


def make_consts():
    c = {}
    c["ident"] = np.eye(128, dtype=np.float32)
    selP = np.zeros((128, 128), np.float32)
    selP[0, :] = 1.0
    selS = np.zeros((128, 128), np.float32)
    for p in range(128):
        selS[1 + p // 8, p] = 1.0
    c["selP"] = selP
    c["selS"] = selS
    st = np.arange(128)
    c["maskP"] = (st[:, None] <= st[None, :]).astype(np.float32)
    c["maskS"] = ((st[:, None] <= st[None, :]) & (st[:, None] // 8 == st[None, :] // 8)).astype(np.float32)
    c["rst"] = np.tile((st % 8 != 0).astype(np.float32)[None, :], (128, 1))
    c["rstm"] = np.tile(np.where(st % 8 == 0, -1e30, 0.0).astype(np.float32)[None, :], (128, 1))
    bms = np.zeros((128, 128), np.float32)
    bms[st, st // 8] = 1.0
    c["bms"] = bms
    c["ones"] = np.ones((128, 128), np.float32)
    same = (st[:, None] // 8 == st[None, :] // 8)
    c["upP"] = (st[:, None] < st[None, :]).astype(np.float32)
    c["lowP"] = (st[:, None] > st[None, :]).astype(np.float32)
    c["upS"] = ((st[:, None] < st[None, :]) & same).astype(np.float32)
    c["lowS"] = ((st[:, None] > st[None, :]) & same).astype(np.float32)
    c["blkS"] = same.astype(np.float32)
    names = list(c.keys())
    arr = np.concatenate([c[k] for k in names], axis=1)
    offs = {}
    o = 0
    for k in names:
        offs[k] = o
        o += c[k].shape[1]
    return arr, offs


CST_ARR, CST_OFF = make_consts()
NCST = CST_ARR.shape[1]

FFN_PARTS = [(0, 4), (4, 4), (8, 4), (12, 4), (16, 4), (20, 2)]
TGS = [(0, 512), (512, 512), (1024, 512), (1536, 512), (2048, 128)]


def build_program(kb, upto=99):
    nc, s = kb.nc, kb.s
    es = kb.es
    xin = kb.din("x", [NTOK, D])
    cin = kb.din("c", [NT, D])
    cst = kb.din("cst", [128, NCST])
    ada_w = kb.din("ada_w", [2, D, 9 * D])
    ada_b = kb.din("ada_b", [2, 9 * D])
    ln_g = kb.din("ln_g", [2, 3, D])
    ln_b = kb.din("ln_b", [2, 3, D])
    ffn_w1 = kb.din("ffn_w1", [2, 2, D, DFF])
    ffn_w3 = kb.din("ffn_w3", [2, 2, D, DFF])
    ffn_w2 = kb.din("ffn_w2", [2, 2, DFF, D])
    yout = kb.dout("y", [NTOK, D])

    X = kb.sb(es, "X", [128, NT, D], F32)
    C = kb.sb(es, "cst_sb", [128, NCST], F32)
    cT = kb.sb(es, "cT", [128, 8, NT], BF16)
    onesb = kb.sb(es, "onesb", [1, 32], F32)
    modT = kb.sb(es, "modT", [128, 16, NT], F32)
    gl = {}

    def alloc_gl(stack):
        gl["Gp"] = kb.sb(stack, "Gp", [128, D], F32)
        gl["Gs"] = kb.sb(stack, "Gs", [128, D], F32)
        gl["LNg"] = kb.sb(stack, "LNg", [128, D], F32)
        gl["LNb"] = kb.sb(stack, "LNb", [128, D], F32)
    tA = [kb.sb(es, "tA%d" % i, [128, 512], F32) for i in range(4)]
    tB = [kb.sb(es, "tB%d" % i, [128, D], F32) for i in range(2)]
    stt_ = [kb.sb(es, "bnst%d" % i, [128, 2, 6], F32) for i in range(2)]
    mv = [kb.sb(es, "mv%d" % i, [128, 2], F32) for i in range(2)]
    rstd = [kb.sb(es, "rstd%d" % i, [128, 1], F32) for i in range(2)]
    nmr = [kb.sb(es, "nmr%d" % i, [128, 1], F32) for i in range(2)]
    tmpS = kb.sb(es, "tmpS", [128, 128], F32)
    ps = [es.enter_context(nc.psum_tensor("ps%d" % i, [128, 512], F32)) for i in range(8)]
    PS = ["ps%d" % i for i in range(8)]

    ident = C[:, CST_OFF["ident"]:CST_OFF["ident"] + 128]
    selP = C[0:NT, CST_OFF["selP"]:CST_OFF["selP"] + 128]
    selS = C[0:NT, CST_OFF["selS"]:CST_OFF["selS"] + 128]

    kb.dma("sp", C[:], cst, r=(), w=["C"])
    for i in range(NT):
        kb.dma("sp", X[:, i, :], xin[i * 128:(i + 1) * 128, :], r=(), w=["X%d" % i])
    kb.memset("pool", onesb[:], 1.0, w=["onesb"])

    with contextlib.ExitStack() as ph0:
        c_sb = kb.sb(ph0, "c_sb", [NT, D], F32)
        cs_sb = kb.sb(ph0, "cs_sb", [NT, D], F32)
        kb.dma("sp", c_sb[:], cin, r=(), w=["c_sb"])
        kb.act(cs_sb[:], c_sb[:], AF.Silu, r=["c_sb"], w=["cs_sb"])
        for kc in range(8):
            kb.tr(ps[0][:, kc * NT:(kc + 1) * NT], cs_sb[0:NT, kc * 128:(kc + 1) * 128], C[0:NT, 0:NT],
                  r=["cs_sb", "C"], w=[PS[0]], sig=(kc == 7))
        kb.cp("dve", cT[:].rearrange("p a b -> p (a b)"), ps[0][:, 0:8 * NT], r=[PS[0]], w=["cT"])
    s.barrier()

    state = {"ada_i": 0, "ada_i2": 0, "psr": 0}

    def mod_prepare(l, sub, res_w, blocks=range(6)):
        if 5 in blocks:
            Gp, Gs = gl["Gp"], gl["Gs"]
            kb.dma("sp", gl["LNg"][:], ln_g[l, sub:sub + 1, :].to_broadcast([128, D]), r=(), w=["LNg"])
            kb.dma("sp", gl["LNb"][:], ln_b[l, sub:sub + 1, :].to_broadcast([128, D]), r=(), w=["LNb"])
        phm = contextlib.ExitStack()
        modst = [kb.sb(phm, "modst%d" % i, [NT, 512], F32) for i in range(2)]
        adaw = [kb.sb(phm, "adaw%d" % i, [128, 8, 256], BF16) for i in range(2)]
        adab = [kb.sb(phm, "adab%d" % i, [1, 512], F32) for i in range(2)]
        for b in blocks:
            i = state["ada_i"]
            state["ada_i"] += 1
            buf = i % 2
            co = sub * 3 * D + b * 512
            kb.dma("sp", adab[buf][:], ada_b[l:l + 1, co:co + 512], r=(), w=["adab%d" % buf])
            pm = 4 + (i % 2)
            for sbk in range(2):
                i2 = state["ada_i2"]
                state["ada_i2"] += 1
                wb = i2 % 2
                kb.dma("pool", adaw[wb][:], ada_w[l].rearrange("(kc p) n -> p kc n", p=128)[:, :, co + sbk * 256:co + (sbk + 1) * 256],
                       r=(), w=["adaw%d" % wb])
                for kc in range(8):
                    kb.mm(ps[pm][0:NT, sbk * 256:(sbk + 1) * 256], lhsT=cT[:, kc, :], rhs=adaw[wb][:, kc, :], start=(kc == 0), stop=False,
                          r=["cT", "adaw%d" % wb], w=[PS[pm]], sig=False)
                kb.mm(ps[pm][0:NT, sbk * 256:(sbk + 1) * 256], lhsT=onesb[0:1, 0:NT], rhs=adab[buf][:, sbk * 256:(sbk + 1) * 256], start=False, stop=True,
                      r=["onesb", "adab%d" % buf], w=[PS[pm]], sig=True)
            kb.cp("act", modst[buf][:], ps[pm][0:NT, :], r=[PS[pm]], w=["modst%d" % buf])
            if b < 4:
                for cc in range(4):
                    j = b * 4 + cc
                    kb.tr(ps[6][:, j * NT:(j + 1) * NT], modst[buf][0:NT, cc * 128:(cc + 1) * 128], C[0:NT, 0:NT],
                          r=["modst%d" % buf, "C"], w=[PS[6]], sig=(cc == 3))
                if b == 1:
                    kb.cp("dve", modT[:, 0:8, :].rearrange("p a b -> p (a b)"), ps[6][:, 0:8 * NT], r=[PS[6]], w=["modT"])
                if b == 3:
                    kb.ts("dve", modT[:, 8:16, :].rearrange("p a b -> p (a b)"), ps[6][:, 8 * NT:16 * NT], 1.0, None,
                          ALU.add, None, r=[PS[6]], w=["modT"])
            else:
                h = b - 4
                kb.mm(ps[7][:, :], lhsT=selP, rhs=modst[buf][:], start=True, stop=True, r=["C", "modst%d" % buf], w=[PS[7]])
                kb.act(Gp[:, h * 512:(h + 1) * 512], ps[7][:, :], AF.Identity, r=[PS[7]], w=["Gp"], bias=float(res_w), scale=float(res_w))
                kb.mm(ps[7][:, :], lhsT=selS, rhs=modst[buf][:], start=True, stop=True, r=["C", "modst%d" % buf], w=[PS[7]])
                kb.act(Gs[:, h * 512:(h + 1) * 512], ps[7][:, :], AF.Identity, r=[PS[7]], w=["Gs"], bias=float(res_w), scale=float(res_w))
        s.barrier()
        phm.close()

    def make_hT(hT, i, prescale=True, col0=None, res=None, hook=None):
        g = res if res is not None else "hT_g%d" % (i // 4)
        if col0 is None:
            col0 = i * 128
        for half in range(2):
            pb = state["psr"] % 4
            state["psr"] += 1
            for cc in range(4):
                c = half * 4 + cc
                kb.tr(ps[pb][:, cc * 128:(cc + 1) * 128], X[:, i, c * 128:(c + 1) * 128], ident,
                      r=["X%d" % i, "C"], w=[PS[pb]], sig=(cc == 3))
            for cc in range(4):
                c = half * 4 + cc
                src = ps[pb][:, cc * 128:(cc + 1) * 128]
                if hook is not None:
                    hook(i, c, src, PS[pb])
                if i < 16:
                    if cc % 2 == 0:
                        kb.act(hT[:, c, col0:col0 + 128], src, AF.Identity, r=[PS[pb], "modT"], w=[g],
                               bias=modT[:, c, 0:1], scale=modT[:, 8 + c, 0:1])
                    else:
                        kb.ts("dve", hT[:, c, col0:col0 + 128], src, modT[:, 8 + c, 0:1], modT[:, c, 0:1],
                              ALU.mult, ALU.add, r=[PS[pb], "modT"], w=[g])
                else:
                    sc = modT[:, 8 + c, 1:NT].unsqueeze(2).to_broadcast([128, 16, 8])
                    sh = modT[:, c, 1:NT].unsqueeze(2).to_broadcast([128, 16, 8])
                    kb.tt("dve", tmpS[:].rearrange("p (q t) -> p q t", t=8), src.rearrange("p (q t) -> p q t", t=8), sc,
                          ALU.mult, r=[PS[pb], "modT"], w=["tmpS"])
                    kb.tt("dve", hT[:, c, col0:col0 + 128].rearrange("p (q t) -> p q t", t=8),
                          tmpS[:].rearrange("p (q t) -> p q t", t=8), sh, ALU.add, r=["tmpS", "modT"], w=[g])
        if prescale:
            kb.ts("pool", X[:, i, :], X[:, i, :], float(ALPHA), None, ALU.mult, None, r=["X%d" % i], w=["X%d" % i])

    def layer_norm(i):
        k = i % 2
        for h in range(2):
            s.add("dve", lambda g_, h=h, k=k, i=i: g_.bn_stats(out=stt_[k][:, h, :], in_=X[:, i, h * 512:(h + 1) * 512]),
                  r=["X%d" % i], w=["bnst%d" % k], tag="bnstats")
        s.add("dve", lambda g_, k=k: g_.bn_aggr(out=mv[k][:], in_=stt_[k][:].rearrange("p a b -> p (a b)")),
              r=["bnst%d" % k], w=["mv%d" % k], tag="bnaggr")
        kb.act(rstd[k][:], mv[k][:, 1:2], AF.Sqrt, r=["mv%d" % k], w=["rstd%d" % k], bias=float(LN_EPS), scale=1.0)
        s.add("dve", lambda g_, k=k: g_.reciprocal(out=rstd[k][:], in_=rstd[k][:]), r=["rstd%d" % k], w=["rstd%d" % k], tag="recip")
        kb.ts("dve", nmr[k][:], mv[k][:, 0:1], rstd[k][:, 0:1], -1.0, ALU.mult, ALU.mult, r=["mv%d" % k, "rstd%d" % k], w=["nmr%d" % k])
        kb.act(tB[k][:], X[:, i, :], AF.Identity, r=["X%d" % i, "rstd%d" % k, "nmr%d" % k], w=["tB%d" % k],
               bias=nmr[k][:, 0:1], scale=rstd[k][:, 0:1])
        kb.tt("pool", tB[k][:], tB[k][:], gl["LNg"][:], ALU.mult, r=["tB%d" % k, "LNg"], w=["tB%d" % k])
        kb.tt("pool", X[:, i, :], tB[k][:], gl["LNb"][:], ALU.add, r=["tB%d" % k, "LNb"], w=["X%d" % i])

    def ffn(l, f, sub, res_w):
        with contextlib.ExitStack() as ph:
            alloc_gl(ph)
            mod_prepare(l, sub, res_w)
            Gp, Gs = gl["Gp"], gl["Gs"]
            hT = kb.sb(ph, "hT", [128, 8, NTOK], BF16)
            for i in range(NT):
                make_hT(hT, i)
            w1p = kb.sb(ph, "w1p", [128, 8, 512], BF16)
            w3p = kb.sb(ph, "w3p", [128, 8, 512], BF16)
            w2p = kb.sb(ph, "w2p", [128, 4, D], BF16)
            gbuf = kb.sb(ph, "gbuf", [128, 4, NTOK], BF16)
            sil = [kb.sb(ph, "sil%d" % i, [128, 512], F32) for i in range(2)]
            w1v = ffn_w1[l, f].rearrange("(kc p) n -> p kc n", p=128)
            w3v = ffn_w3[l, f].rearrange("(kc p) n -> p kc n", p=128)
            w2v = ffn_w2[l, f].rearrange("(j p) n -> p j n", p=128)
            u = 0
            v = 0
            for pi, (j0, ncn) in enumerate(FFN_PARTS):
                kb.dma("pool", w1p[:, :, 0:ncn * 128], w1v[:, :, j0 * 128:(j0 + ncn) * 128], r=(), w=["w1p"])
                kb.dma("pool", w3p[:, :, 0:ncn * 128], w3v[:, :, j0 * 128:(j0 + ncn) * 128], r=(), w=["w3p"])
                kb.dma("pool", w2p[:, 0:ncn, :], w2v[:, j0:j0 + ncn, :], r=(), w=["w2p"])
                for tg, (t0, nt_) in enumerate(TGS):
                    for jj in range(ncn):
                        pa, pb = (2 * u) % 4, (2 * u + 1) % 4
                        for kc in range(8):
                            kb.mm(ps[pa][:, 0:nt_], lhsT=w1p[:, kc, jj * 128:(jj + 1) * 128], rhs=hT[:, kc, t0:t0 + nt_],
                                  start=(kc == 0), stop=(kc == 7), r=["w1p", "hT_g%d" % tg], w=[PS[pa]])
                        for kc in range(8):
                            kb.mm(ps[pb][:, 0:nt_], lhsT=w3p[:, kc, jj * 128:(jj + 1) * 128], rhs=hT[:, kc, t0:t0 + nt_],
                                  start=(kc == 0), stop=(kc == 7), r=["w3p", "hT_g%d" % tg], w=[PS[pb]])
                        kb.act(sil[u % 2][:, 0:nt_], ps[pa][:, 0:nt_], AF.Silu, r=[PS[pa]], w=["sil%d" % (u % 2)])
                        kb.tt("dve", gbuf[:, jj, t0:t0 + nt_], sil[u % 2][:, 0:nt_], ps[pb][:, 0:nt_], ALU.mult,
                              r=["sil%d" % (u % 2), PS[pb]], w=["g_g%d" % tg])
                        u += 1
                for i in range(NT):
                    G = Gp if i < 16 else Gs
                    for half in range(2):
                        py = 4 + (v % 4)
                        for jj in range(ncn):
                            kb.mm(ps[py][:, :], lhsT=gbuf[:, jj, i * 128:(i + 1) * 128], rhs=w2p[:, jj, half * 512:(half + 1) * 512],
                                  start=(jj == 0), stop=(jj == ncn - 1), r=["g_g%d" % (i // 4), "w2p"], w=[PS[py]])
                        kb.tt("dve", tA[v % 4][:], ps[py][:, :], G[:, half * 512:(half + 1) * 512], ALU.mult,
                              r=[PS[py], "Gp", "Gs"], w=["tA%d" % (v % 4)])
                        kb.tt("pool", X[:, i, half * 512:(half + 1) * 512], X[:, i, half * 512:(half + 1) * 512], tA[v % 4][:],
                              ALU.add, r=["X%d" % i, "tA%d" % (v % 4)], w=["X%d" % i])
                        v += 1
                    if pi == len(FFN_PARTS) - 1:
                        layer_norm(i)
        s.barrier()

    DKS = float(128 ** -0.5)

    class _Stop(Exception):
        pass

    def stop_at(n):
        if getattr(kb, "stop_point", None) == n:
            s.barrier()
            s.skip = True

    def ab_mixer(l):
        ab_mixer_(l)
        s.skip = False
        s.barrier()

    def ab_mixer_(l):
        ab_w_in = kb.din("ab_w_in", [1, D, 3080])
        ab_w_out = kb.din("ab_w_out", [1, D, D])
        mnorm_g = kb.din("mlstm_norm_g", [1, 512])
        vecA_d = kb.din("vecA", [128, 32])
        bgT_d = kb.din("bgT", [4, 2])
        minitT_d = kb.din("minitT", [4, 16])
        rg_w_a = kb.din("rg_w_a", [1, 8, 64, 64])
        rg_w_x = kb.din("rg_w_x", [1, 8, 64, 64])
        smC = kb.din("smC", [16, 4, 128, 128])
        smn = kb.din("smn", [16, 4, 128])
        srh = kb.din("srh", [16, 512])
        srconv = kb.din("srconv", [48, 512])
        o_pmC = kb.dout("o_pmC", [4, 128, 128])
        o_pmn = kb.dout("o_pmn", [4, 128])
        o_pmm = kb.dout("o_pmm", [4, 1])
        o_prh = kb.dout("o_prh", [4, 128])
        o_prconv = kb.dout("o_prconv", [3, 512])
        o_smC = kb.dout("o_smC", [16, 4, 128, 128])
        o_smn = kb.dout("o_smn", [16, 4, 128])
        o_smm = kb.dout("o_smm", [4, 16])
        o_srh = kb.dout("o_srh", [16, 512])
        o_srconv = kb.dout("o_srconv", [48, 512])

        maskP = C[:, CST_OFF["maskP"]:CST_OFF["maskP"] + 128]
        maskS = C[:, CST_OFF["maskS"]:CST_OFF["maskS"] + 128]
        rst = C[:, CST_OFF["rst"]:CST_OFF["rst"] + 128]
        rstm = C[:, CST_OFF["rstm"]:CST_OFF["rstm"] + 128]
        bms = C[:, CST_OFF["bms"]:CST_OFF["bms"] + 16]
        ones = C[:, CST_OFF["ones"]:CST_OFF["ones"] + 128]

        mod_prepare(l, 1, 1.0, blocks=range(4))
        win_v = ab_w_in[0].rearrange("(kc p) n -> p kc n", p=128)
        with contextlib.ExitStack() as ph:
            hmT = kb.sb(ph, "hmT", [128, 4, NTOK], BF16)
            vecA = kb.sb(ph, "vecA_sb", [128, 32], F32)
            kb.dma("sp", vecA[:], vecA_d, r=(), w=["vecA"])
            sigo = tA[0]
            hmf = tB[0][:, 0:512]
            with contextlib.ExitStack() as pa:
                winA = kb.sb(pa, "winA", [128, 8, 2056], BF16)
                kb.dma("pool", winA[:, :, 0:1024], win_v[:, :, 0:1024], r=(), w=["winA"])
                kb.dma("pool", winA[:, :, 1024:2056], win_v[:, :, 1024:2056], r=(), w=["winA"])
                qkT = kb.sb(pa, "qkT", [128, 8, 512], BF16)
                hTg = [kb.sb(pa, "hTgA%d" % i, [128, 8, 512], BF16) for i in range(2)]
                bg = kb.sb(pa, "bg", [4, 2], F32)
                nbg1 = kb.sb(pa, "nbg1", [4, 1], F32)
                minitT = kb.sb(pa, "minitT_sb", [4, 16], F32)
                mng = kb.sb(pa, "mng", [128, 512], F32)
                kb.dma("sp", bg[:], bgT_d, r=(), w=["bg"])
                kb.dma("sp", minitT[:], minitT_d, r=(), w=["minitT"])
                kb.dma("sp", mng[:], mnorm_g[0:1, :].to_broadcast([128, 512]), r=(), w=["mng"])
                kb.ts("dve", nbg1[:], bg[:, 1:2], -1.0, None, ALU.mult, None, r=["bg"], w=["nbg1"])
                R4 = lambda nm: kb.sb(pa, nm, [4, 128], F32)
                t1, IGa, Rt, t3, t4 = R4("r_t1"), R4("r_ig"), R4("r_rt"), R4("r_t3"), R4("r_t4")
                Bc = [R4("r_bc0"), R4("r_bc1")]
                Mx = [R4("r_mx0"), R4("r_mx1")]
                dd = kb.sb(pa, "r_dd", [4, 16], F32)
                DDm = kb.sb(pa, "r_DD", [4, 64], F32)
                mout = kb.sb(pa, "r_mout", [4, 16], F32)
                colq = [kb.sb(pa, "colq%d" % i, [128, 16], F32) for i in range(2)]
                decsb = kb.sb(pa, "decsb", [128, 64], F32)
                kw = kb.sb(pa, "kw", [128, 4, 128], BF16)
                ktok = kb.sb(pa, "ktok", [128, 4, 128], BF16)
                vext = [kb.sb(pa, "vext%d" % i, [128, 4, 130], BF16) for i in range(2)]
                PT = kb.sb(pa, "PT", [128, 4, 128], BF16)
                Cst = kb.sb(pa, "Cst", [128, 4, 130], F32)
                Cb = kb.sb(pa, "Cb", [128, 4, 130], BF16)
                dmax = kb.sb(pa, "dmax", [128, 4], F32)
                hst6 = kb.sb(pa, "hst6", [128, 4, 6], F32)
                hmv = kb.sb(pa, "hmv", [128, 4, 2], F32)
                hrs = kb.sb(pa, "hrs", [128, 4], F32)
                hnm = kb.sb(pa, "hnm", [128, 4], F32)
                for vv in vext:
                    kb.memset("pool", vv[:], 1.0, w=["vext0", "vext1"])
                kb.memset("pool", Cst[:], 0.0, w=["Cst"])
                kb.memset("pool", Cb[:], 0.0, w=["Cb"])

                def rows(i):
                    k = i % 2
                    hb = (i // 4) % 2
                    hT = hTg[hb]
                    tc0 = (i % 4) * 128
                    pg = ps[7]
                    for kc in range(8):
                        kb.mm(pg[0:4, 0:128], lhsT=winA[:, kc, 2048:2052], rhs=hT[:, kc, tc0:tc0 + 128], start=(kc == 0), stop=(kc == 7),
                              r=["winA", "hTg%d" % hb], w=[PS[7]])
                    for kc in range(8):
                        kb.mm(pg[0:4, 128:256], lhsT=winA[:, kc, 2052:2056], rhs=hT[:, kc, tc0:tc0 + 128], start=(kc == 0), stop=(kc == 7),
                              r=["winA", "hTg%d" % hb], w=[PS[7]])
                    kb.act(IGa[:], pg[0:4, 0:128], AF.Identity, r=[PS[7], "bg"], w=["r_ig"], bias=bg[:, 0:1], scale=1.0)
                    kb.act(t1[:], pg[0:4, 128:256], AF.Exp, r=[PS[7], "nbg1"], w=["r_t1"], bias=nbg1[:, 0:1], scale=-1.0)
                    kb.act(t1[:], t1[:], AF.Ln, r=["r_t1"], w=["r_t1"], bias=1.0, scale=1.0)
                    kb.ts("dve", t1[:], t1[:], -1.0, None, ALU.mult, None, r=["r_t1"], w=["r_t1"])
                    prompt = i < 16
                    if prompt:
                        binit = 0.0 if i == 0 else Bc[1 - k][:, 127:128]
                        minit = 0.0 if i == 0 else Mx[1 - k][:, 127:128]
                        s.add("dve", lambda g_: g_.tensor_tensor_scan(out=Bc[k][:], data0=ones[0:4, :], data1=t1[:], initial=binit,
                                                                       op0=ALU.mult, op1=ALU.add),
                              r=["r_t1", "r_bc%d" % (1 - k), "C"], w=["r_bc%d" % k], tag="scanB")
                        kb.tt("dve", IGa[:], IGa[:], Bc[k][:], ALU.subtract, r=["r_ig", "r_bc%d" % k], w=["r_ig"])
                        kb.memset("dve", t3[:], 0.0, w=["r_t3"])
                        s.add("dve", lambda g_: g_.tensor_tensor_scan(out=Mx[k][:], data0=t3[:], data1=IGa[:], initial=minit,
                                                                       op0=ALU.add, op1=ALU.max),
                              r=["r_t3", "r_ig", "r_mx%d" % (1 - k)], w=["r_mx%d" % k], tag="scanM")
                        if i == 0:
                            kb.memset("dve", Rt[:], 0.0, w=["r_rt"])
                        else:
                            kb.cp("dve", Rt[:], Mx[1 - k][:, 127:128].to_broadcast([4, 128]), r=["r_mx%d" % (1 - k)], w=["r_rt"])
                    else:
                        s.add("dve", lambda g_: g_.tensor_tensor_scan(out=Bc[k][:], data0=rst[0:4, :], data1=t1[:], initial=0.0,
                                                                       op0=ALU.mult, op1=ALU.add),
                              r=["r_t1", "C"], w=["r_bc%d" % k], tag="scanB")
                        kb.tt("dve", IGa[:], IGa[:], Bc[k][:], ALU.subtract, r=["r_ig", "r_bc%d" % k], w=["r_ig"])
                        kb.cp("dve", t3[:], IGa[:], r=["r_ig"], w=["r_t3"])
                        kb.tt("dve", t3[:].rearrange("p (q t) -> p q t", t=8)[:, :, 0:1], IGa[:].rearrange("p (q t) -> p q t", t=8)[:, :, 0:1],
                              minitT[:].unsqueeze(2), ALU.max, r=["r_ig", "minitT"], w=["r_t3"])
                        s.add("dve", lambda g_: g_.tensor_tensor_scan(out=Mx[k][:], data0=rstm[0:4, :], data1=t3[:], initial=0.0,
                                                                       op0=ALU.add, op1=ALU.max),
                              r=["r_t3", "C"], w=["r_mx%d" % k], tag="scanM")
                        kb.cp("dve", Rt[:].rearrange("p (q t) -> p q t", t=8), minitT[:].unsqueeze(2).to_broadcast([4, 16, 8]),
                              r=["minitT"], w=["r_rt"])
                    kb.tt("dve", t3[:], IGa[:], Rt[:], ALU.subtract, r=["r_ig", "r_rt"], w=["r_t3"])
                    kb.act(t3[:], t3[:], AF.Exp, r=["r_t3"], w=["r_t3"])
                    kb.tt("dve", t4[:], Bc[k][:], Rt[:], ALU.add, r=["r_bc%d" % k, "r_rt"], w=["r_t4"])
                    kb.act(t4[:], t4[:], AF.Exp, r=["r_t4"], w=["r_t4"], scale=-1.0)
                    kb.tr(pg[:, 256:260], t3[0:4, :], C[0:4, 0:4], r=["r_t3", "C"], w=[PS[7]])
                    kb.tr(pg[:, 260:264], t4[0:4, :], C[0:4, 0:4], r=["r_t4", "C"], w=[PS[7]])
                    if prompt:
                        kb.tt("dve", dd[:, 0:1], Rt[:, 0:1], Mx[k][:, 127:128], ALU.subtract, r=["r_rt", "r_mx%d" % k], w=["r_dd"])
                        kb.act(dd[:, 0:1], dd[:, 0:1], AF.Exp, r=["r_dd"], w=["r_dd"])
                        kb.ts("dve", DDm[:, 0:4], C[0:4, 0:4], dd[:, 0:1], None, ALU.mult, None, r=["r_dd", "C"], w=["r_DD"])
                        kb.mm(pg[:, 264:268], lhsT=ones[0:4, :], rhs=DDm[:, 0:4], start=True, stop=True, r=["C", "r_DD"], w=[PS[7]])
                        kb.cp("dve", colq[k][:, 0:12], pg[:, 256:268], r=[PS[7]], w=["colq%d" % k])
                        if i == 15:
                            kb.tt("dve", mout[:, 0:1], Bc[k][:, 127:128], Mx[k][:, 127:128], ALU.add, r=["r_bc%d" % k, "r_mx%d" % k], w=["r_mout"])
                            kb.dma("sp", o_pmm, mout[:, 0:1], r=["r_mout"], w=())
                    else:
                        MT = Mx[k][:].rearrange("p (q t) -> p q t", t=8)[:, :, 7:8]
                        kb.tt("dve", t4[:].rearrange("p (q t) -> p q t", t=8), IGa[:].rearrange("p (q t) -> p q t", t=8),
                              MT.to_broadcast([4, 16, 8]), ALU.subtract, r=["r_ig", "r_mx%d" % k, PS[7]], w=["r_t4"])
                        kb.act(t4[:], t4[:], AF.Exp, r=["r_t4"], w=["r_t4"])
                        kb.tr(pg[:, 264:268], t4[0:4, :], C[0:4, 0:4], r=["r_t4", "C"], w=[PS[7]])
                        kb.cp("dve", colq[k][:, 0:12], pg[:, 256:268], r=[PS[7]], w=["colq%d" % k])
                        kb.tt("dve", dd[:].unsqueeze(2), minitT[:].unsqueeze(2), MT, ALU.subtract, r=["minitT", "r_mx%d" % k], w=["r_dd"])
                        kb.act(dd[:], dd[:], AF.Exp, r=["r_dd"], w=["r_dd"])
                        kb.tt("dve", DDm[:].rearrange("p (q h) -> p q h", h=4), dd[:].unsqueeze(2).to_broadcast([4, 16, 4]),
                              C[0:4, 0:4].unsqueeze(1).to_broadcast([4, 16, 4]), ALU.mult, r=["r_dd", "C"], w=["r_DD"])
                        kb.mm(pg[:, 272:336], lhsT=ones[0:4, :], rhs=DDm[:], start=True, stop=True, r=["C", "r_DD"], w=[PS[7]])
                        kb.cp("dve", decsb[:], pg[:, 272:336], r=[PS[7]], w=["decsb"])
                        kb.tt("dve", mout[:].unsqueeze(2), Bc[k][:].rearrange("p (q t) -> p q t", t=8)[:, :, 7:8], MT, ALU.add,
                              r=["r_bc%d" % k, "r_mx%d" % k], w=["r_mout"])
                        kb.dma("sp", o_smm, mout[:], r=["r_mout"], w=())

                def mlstm_tile(i):
                    k = i % 2
                    tg = i // 4
                    hT = hTg[tg % 2]
                    tc0 = (i % 4) * 128
                    lc0 = (i % 4) * 128 if i < 16 else 0
                    cq = colq[k]
                    vx = vext[k]
                    grp = "hTg%d" % (tg % 2)
                    for bi, c0 in enumerate((512, 1024, 1536)):
                        bank = 2 + (bi % 2)
                        for kc in range(8):
                            kb.mm(ps[bank][:, :], lhsT=hT[:, kc, tc0:tc0 + 128], rhs=winA[:, kc, c0:c0 + 512], start=(kc == 0), stop=(kc == 7),
                                  r=["winA", grp], w=[PS[bank]])
                        if bi == 0:
                            kb.act(ktok[:].rearrange("p a b -> p (a b)"), ps[bank][:, :], AF.Identity, r=[PS[bank]], w=["ktok"], scale=DKS)
                            for h in range(4):
                                kb.ts("dve", kw[:, h, :], ktok[:, h, :], cq[:, h:h + 1], None, ALU.mult, None,
                                      r=["ktok", "colq%d" % k], w=["kw"])
                        elif bi == 1:
                            kb.cp("act", vx[:, :, 0:128], ps[bank][:, :].rearrange("p (h d) -> p h d", d=128), r=[PS[bank]], w=["vext%d" % k])
                        else:
                            kb.act(sigo[:], ps[bank][:, :], AF.Sigmoid, r=[PS[bank]], w=["tA0"])
                    if i == 0:
                        stop_at(31)
                    for h in range(4):
                        kb.mm(ps[4][:, h * 128:(h + 1) * 128], lhsT=qkT[:, 4 + h, lc0:lc0 + 128], rhs=qkT[:, h, lc0:lc0 + 128],
                              start=True, stop=True, r=["qkT"], w=[PS[4]], sig=(h == 3))
                    if i == 0:
                        stop_at(32)
                    msk = maskP if i < 16 else maskS
                    for h in range(4):
                        kb.stt(PT[:, h, :], ps[4][:, h * 128:(h + 1) * 128], cq[:, h:h + 1], msk, ALU.mult, ALU.mult,
                               r=[PS[4], "colq%d" % k, "C"], w=["PT"])
                    return cq, vx, lc0

                def numden_finish(i, cq):
                    for half in range(2):
                        bank = ps[5 + half]
                        den = bank[:, 0:260].rearrange("p (h d) -> p h d", d=130)[:, :, 128:129]
                        kb.act(dmax[:, 2 * half:2 * half + 2].unsqueeze(2), den, AF.Abs, r=[PS[5 + half]], w=["dmax"])
                        kb.tt("dve", dmax[:, 2 * half:2 * half + 2], dmax[:, 2 * half:2 * half + 2], cq[:, 4 + 2 * half:6 + 2 * half], ALU.max,
                              r=["dmax", "colq%d" % (i % 2)], w=["dmax"])
                    s.add("dve", lambda g_: g_.reciprocal(out=dmax[:], in_=dmax[:]), r=["dmax"], w=["dmax"], tag="recip")
                    for h in range(4):
                        bank = ps[5 + h // 2]
                        o0 = (h % 2) * 130
                        kb.act(hmf[:, h * 128:(h + 1) * 128], bank[:, o0:o0 + 128], AF.Identity, r=[PS[5 + h // 2], "dmax"], w=["tB0"],
                               scale=dmax[:, h:h + 1])
                    for h in range(4):
                        s.add("dve", lambda g_, h=h: g_.bn_stats(out=hst6[:, h, :], in_=hmf[:, h * 128:(h + 1) * 128]), r=["tB0"], w=["hst6"], tag="bnst")
                    for h in range(4):
                        s.add("dve", lambda g_, h=h: g_.bn_aggr(out=hmv[:, h, :], in_=hst6[:, h, :]), r=["hst6"], w=["hmv"], tag="bnag")
                    kb.act(hrs[:].unsqueeze(2), hmv[:, :, 1:2], AF.Sqrt, r=["hmv"], w=["hrs"], bias=1e-6, scale=1.0)
                    s.add("dve", lambda g_: g_.reciprocal(out=hrs[:], in_=hrs[:]), r=["hrs"], w=["hrs"], tag="recip")
                    kb.tt("dve", hnm[:].unsqueeze(2), hmv[:, :, 0:1], hrs[:].unsqueeze(2), ALU.mult, r=["hmv", "hrs"], w=["hnm"])
                    kb.ts("dve", hnm[:], hnm[:], -1.0, None, ALU.mult, None, r=["hnm"], w=["hnm"])
                    for h in range(4):
                        kb.act(hmf[:, h * 128:(h + 1) * 128], hmf[:, h * 128:(h + 1) * 128], AF.Identity, r=["tB0", "hrs", "hnm"], w=["tB0"],
                               bias=hnm[:, h:h + 1], scale=hrs[:, h:h + 1])
                    kb.tt("pool", hmf[:], hmf[:], mng[:], ALU.mult, r=["tB0", "mng"], w=["tB0"])
                    kb.tt("pool", hmf[:], hmf[:], sigo[:], ALU.mult, r=["tB0", "tA0"], w=["tB0"])
                    for h in range(4):
                        kb.tr(ps[4][:, h * 128:(h + 1) * 128], hmf[:, h * 128:(h + 1) * 128], ident, r=["tB0", "C"], w=[PS[4]], sig=(h == 3))
                    kb.cp("act", hmT[:, :, i * 128:(i + 1) * 128], ps[4][:, :].rearrange("p (h d) -> p h d", d=128), r=[PS[4]], w=["hmT%d" % i])

                for tg, (t0, nt_) in enumerate(TGS):
                    tiles = range(4 * tg, 4 * tg + 4) if tg < 4 else [16]
                    hT = hTg[tg % 2]
                    for i in tiles:
                        make_hT(hT, i, prescale=False, col0=(i % 4) * 128, res="hTg%d" % (tg % 2))
                    for j in range(8):
                        bank = j % 2
                        for kc in range(8):
                            kb.mm(ps[bank][:, 0:nt_], lhsT=winA[:, kc, j * 128:(j + 1) * 128], rhs=hT[:, kc, 0:nt_], start=(kc == 0), stop=(kc == 7),
                                  r=["winA", "hTg%d" % (tg % 2)], w=[PS[bank]])
                        if j < 4:
                            kb.cp("act", qkT[:, j, 0:nt_], ps[bank][:, 0:nt_], r=[PS[bank]], w=["qkT"])
                        else:
                            kb.ts("dve", qkT[:, j, 0:nt_], ps[bank][:, 0:nt_], DKS, None, ALU.mult, None, r=[PS[bank]], w=["qkT"])
                    for i in tiles:
                        if i == 0:
                            stop_at(1)
                        rows(i)
                        if i == 0:
                            stop_at(2)
                        cq, vx, lc0 = mlstm_tile(i)
                        if i == 0:
                            stop_at(3)
                        if i == 16:
                            stop_at(5)
                        if i < 16:
                            for h in range(4):
                                bank = ps[5 + h // 2]
                                o0 = (h % 2) * 130
                                kb.mm(bank[:, o0:o0 + 130], lhsT=PT[:, h, :], rhs=vx[:, h, :], start=True, stop=False, r=["PT", "vext%d" % (i % 2)], w=[PS[5 + h // 2]], sig=False)
                                kb.mm(bank[:, o0:o0 + 130], lhsT=qkT[:, h, lc0:lc0 + 128], rhs=Cb[:, h, :], start=False, stop=True, r=["qkT", "Cb"], w=[PS[5 + h // 2]], sig=True)
                            numden_finish(i, cq)
                            for h in range(4):
                                bank = ps[5 + h // 2]
                                o0 = (h % 2) * 130
                                kb.mm(bank[:, o0:o0 + 130], lhsT=kw[:, h, :], rhs=vx[:, h, :], start=True, stop=True, r=["kw", "vext%d" % (i % 2)], w=[PS[5 + h // 2]])
                            for h in range(4):
                                bank = ps[5 + h // 2]
                                o0 = (h % 2) * 130
                                kb.ts("dve", Cst[:, h, :], Cst[:, h, :], cq[:, 8 + h:9 + h], None, ALU.mult, None, r=["Cst", "colq%d" % (i % 2)], w=["Cst"])
                                kb.stt(Cst[:, h, :], bank[:, o0:o0 + 130], cq[:, 8 + h:9 + h], Cst[:, h, :], ALU.mult, ALU.add,
                                       r=[PS[5 + h // 2], "Cst", "colq%d" % (i % 2)], w=["Cst"])
                            kb.cp("act", Cb[:], Cst[:], r=["Cst"], w=["Cb"])
                            if i == 0:
                                stop_at(4)
                            if i == 15:
                                for h in range(4):
                                    kb.tr(ps[4][:, h * 128:(h + 1) * 128], Cst[:, h, 0:128], ident, r=["Cst", "C"], w=[PS[4]], sig=(h == 3))
                                kb.cp("act", hmf[:], ps[4][:, :], r=[PS[4]], w=["tB0"])
                                kb.dma("sp", o_pmC.rearrange("h v k -> v h k"), hmf[:].rearrange("p (h k) -> p h k", k=128), r=["tB0"], w=())
                                kb.dma("sp", o_pmn.rearrange("h k -> k h"), Cst[:, :, 128], r=["Cst"], w=(), allow_slow_non_contiguous=True)
                        else:
                            with contextlib.ExitStack() as psm:
                                Cin = kb.sb(psm, "Cin", [128, 16, 128], F32)
                                CsT = kb.sb(psm, "CsT", [128, 16, 130], BF16)
                                qTm = kb.sb(psm, "qTm", [128, 16, 128], BF16)
                                VWm = kb.sb(psm, "VWm", [128, 16, 128], BF16)
                                nin = kb.sb(psm, "nin", [16, 4, 128], F32)
                                ninT = kb.sb(psm, "ninT", [128, 4, 16], F32)
                                BMW = kb.sb(psm, "BMW", [128, 16], BF16)
                                decc = kb.sb(psm, "decc", [16, 4], F32)
                                nout = kb.sb(psm, "nout", [16, 4, 128], F32)
                                kb.memset("pool", qTm[:], 0.0, w=["qTm"])
                                kb.dma("sp", nin[:], smn, r=(), w=["nin"])
                                kb.tr(ps[7][0:16, 400:404], dd[0:4, :], C[0:4, 0:4], r=["r_dd", "C"], w=[PS[7]])
                                kb.cp("dve", decc[:], ps[7][0:16, 400:404], r=[PS[7]], w=["decc"])
                                for h in range(4):
                                    kb.tr(ps[7][:, 416 + h * 16:432 + h * 16], nin[0:16, h, :], C[0:16, 0:16], r=["nin", "C"], w=[PS[7]], sig=(h == 3))
                                kb.cp("dve", ninT[:].rearrange("p a b -> p (a b)"), ps[7][:, 416:480], r=[PS[7]], w=["ninT"])
                                for h in range(4):
                                    bank = ps[5 + h // 2]
                                    o0 = (h % 2) * 130
                                    kb.dma("sp", Cin[:], smC[:, h].rearrange("q v k -> v q k"), r=(), w=["Cin"])
                                    for q4 in range(4):
                                        pb = q4 % 2
                                        for qq in range(4):
                                            q = q4 * 4 + qq
                                            kb.tr(ps[pb][:, qq * 128:(qq + 1) * 128], Cin[:, q, :], ident, r=["Cin", "C"], w=[PS[pb]], sig=(qq == 3))
                                        kb.cp("act", CsT[:, q4 * 4:q4 * 4 + 4, 0:128], ps[pb][:, :].rearrange("p (a b) -> p a b", b=128), r=[PS[pb]], w=["CsT"])
                                    kb.cp("dve", CsT[:, :, 128:129], ninT[:, h, :].unsqueeze(2), r=["ninT"], w=["CsT"])
                                    kb.cp("pool", bass.AP(qTm, 0, [[2048, 128], [136, 16], [1, 8]]),
                                          qkT[:, h, 0:128].rearrange("p (q t) -> p q t", t=8), r=["qkT"], w=["qTm"])
                                    kb.mm(bank[:, o0:o0 + 130], lhsT=PT[:, h, :], rhs=vx[:, h, :], start=True, stop=False, r=["PT", "vext%d" % (i % 2)], w=[PS[5 + h // 2]], sig=False)
                                    for q in range(16):
                                        kb.mm(bank[:, o0:o0 + 130], lhsT=qTm[:, q, :], rhs=CsT[:, q, :], start=False, stop=(q == 15), r=["qTm", "CsT"], w=[PS[5 + h // 2]], sig=(q == 15))
                                    kb.ts("dve", BMW[:], bms, cq[:, 8 + h:9 + h], None, ALU.mult, None, r=["C", "colq%d" % (i % 2)], w=["BMW"])
                                    kb.tt("dve", VWm[:], vx[:, h, 0:128].unsqueeze(1).to_broadcast([128, 16, 128]), BMW[:].unsqueeze(2).to_broadcast([128, 16, 128]),
                                          ALU.mult, r=["vext%d" % (i % 2), "BMW"], w=["VWm"])
                                    for q4 in range(4):
                                        pb = q4 % 2
                                        for qq in range(4):
                                            q = q4 * 4 + qq
                                            kb.mm(ps[pb][:, qq * 128:(qq + 1) * 128], lhsT=VWm[:, q, :], rhs=ktok[:, h, :], start=True, stop=True,
                                                  r=["VWm", "ktok"], w=[PS[pb]], sig=(qq == 3))
                                        for qq in range(4):
                                            q = q4 * 4 + qq
                                            kb.stt(Cin[:, q, :], Cin[:, q, :], decsb[:, q * 4 + h:q * 4 + h + 1], ps[pb][:, qq * 128:(qq + 1) * 128], ALU.mult, ALU.add,
                                                   r=["Cin", "decsb", PS[pb]], w=["Cin"])
                                    kb.dma("sp", o_smC[:, h].rearrange("q v k -> v q k"), Cin[:], r=["Cin"], w=())
                                    kb.mm(ps[7][0:16, 0:128], lhsT=BMW[:], rhs=ktok[:, h, :], start=True, stop=True, r=["BMW", "ktok"], w=[PS[7]])
                                    kb.stt(nout[:, h, :], nin[:, h, :], decc[:, h:h + 1], ps[7][0:16, 0:128], ALU.mult, ALU.add, r=["nin", "decc", PS[7]], w=["nout"])
                                numden_finish(i, cq)
                                kb.dma("sp", o_smn, nout[:], r=["nout"], w=())
                                s.barrier()
            s.barrier()
            stop_at(6)
            hrT = kb.sb(ph, "hrT", [128, 4, NTOK], BF16)
            with contextlib.ExitStack() as pb_:
                winB = kb.sb(pb_, "winB", [128, 8, 1024], BF16)
                kb.dma("pool", winB[:], win_v[:, :, 2056:3080], r=(), w=["winB"])
                hTgB = [kb.sb(pb_, "hTgB%d" % i, [128, 8, 512], BF16) for i in range(2)]
                WA = kb.sb(pb_, "WA", [128, 4, 128], F32)
                WX = kb.sb(pb_, "WX", [128, 4, 128], F32)
                kb.memset("pool", WA[:], 0.0, w=["WA"])
                kb.memset("pool", WX[:], 0.0, w=["WX"])
                for c in range(4):
                    for hp in range(2):
                        kb.dma("sp", WA[hp * 64:(hp + 1) * 64, c, hp * 64:(hp + 1) * 64], rg_w_a[0, 2 * c + hp], r=(), w=["WA"])
                        kb.dma("sp", WX[hp * 64:(hp + 1) * 64, c, hp * 64:(hp + 1) * 64], rg_w_x[0, 2 * c + hp], r=(), w=["WX"])
                cl = kb.sb(pb_, "cl", [128, 4], F32)
                cl2 = kb.sb(pb_, "cl2", [128, 4], F32)
                kb.act(cl[:], vecA[:, 28:32], AF.Exp, r=["vecA"], w=["cl"], scale=-1.0)
                kb.act(cl[:], cl[:], AF.Ln, r=["cl"], w=["cl"], bias=1.0, scale=1.0)
                kb.ts("dve", cl2[:], cl[:], -16.0, None, ALU.mult, None, r=["cl"], w=["cl2"])
                kb.ts("dve", cl[:], cl[:], -8.0, None, ALU.mult, None, r=["cl", "cl2"], w=["cl"])
                xp = [kb.sb(pb_, "xp%d" % c, [128, 515], F32) for c in range(4)]
                xps = kb.sb(pb_, "xps", [128, 16, 11], F32)
                hst = kb.sb(pb_, "hst", [128, 4], F32)
                h0T = kb.sb(pb_, "h0T", [128, 4, 16], F32)
                cvT = kb.sb(pb_, "cvT", [128, 4, 48], F32)
                hl = kb.sb(pb_, "hl", [128, 4, 16], F32)
                srh_sb = kb.sb(pb_, "srh_sb", [16, 512], F32)
                src_sb = kb.sb(pb_, "src_sb", [48, 512], F32)
                F5 = lambda nm: kb.sb(pb_, nm, [128, 512], F32)
                xc, rr, ii, aa, a2, uu, hh_, t5 = F5("xc"), F5("rr"), F5("ii"), F5("aa"), F5("a2"), F5("uu"), F5("hh"), F5("t5")
                for c in range(4):
                    kb.memset("pool", xp[c][:, 0:3], 0.0, w=["xp%d" % c])
                kb.memset("pool", hst[:], 0.0, w=["hst"])
                kb.dma("sp", srh_sb[:], srh, r=(), w=["srh_sb"])
                kb.dma("sp", src_sb[:], srconv, r=(), w=["src_sb"])
                for c in range(4):
                    kb.tr(ps[6][:, c * 16:(c + 1) * 16], srh_sb[0:16, c * 128:(c + 1) * 128], C[0:16, 0:16], r=["srh_sb", "C"], w=[PS[6]], sig=(c == 3))
                kb.cp("dve", h0T[:].rearrange("p a b -> p (a b)"), ps[6][:, 0:64], r=[PS[6]], w=["h0T"])
                for c in range(4):
                    kb.tr(ps[6][:, 64 + c * 48:64 + (c + 1) * 48], src_sb[0:48, c * 128:(c + 1) * 128], C[0:48, 0:48], r=["src_sb", "C"], w=[PS[6]], sig=(c == 3))
                kb.cp("dve", cvT[:].rearrange("p a b -> p (a b)"), ps[6][:, 64:256], r=[PS[6]], w=["cvT"])
                u_ = 0
                for tg, (t0, n) in enumerate(TGS):
                    sample = tg == 4
                    hT = hTgB[tg % 2]
                    for i in (range(4 * tg, 4 * tg + 4) if tg < 4 else [16]):
                        make_hT(hT, i, prescale=True, col0=(i % 4) * 128, res="hTg%d" % (tg % 2))
                    for c in range(4):
                        px, pgr = ps[(2 * u_) % 4], ps[(2 * u_ + 1) % 4]
                        PX, PGR = PS[(2 * u_) % 4], PS[(2 * u_ + 1) % 4]
                        u_ += 1
                        for kc in range(8):
                            kb.mm(px[:, 0:n], lhsT=winB[:, kc, c * 128:(c + 1) * 128], rhs=hT[:, kc, 0:n], start=(kc == 0), stop=(kc == 7),
                                  r=["winB", "hTg%d" % (tg % 2)], w=[PX])
                        for kc in range(8):
                            kb.mm(pgr[:, 0:n], lhsT=winB[:, kc, 512 + c * 128:512 + (c + 1) * 128], rhs=hT[:, kc, 0:n], start=(kc == 0), stop=(kc == 7),
                                  r=["winB", "hTg%d" % (tg % 2)], w=[PGR])
                        cw = lambda j: vecA[:, c * 4 + j:c * 4 + j + 1]
                        cb = vecA[:, 16 + c:17 + c]
                        if not sample:
                            kb.cp("act", xp[c][:, 3:3 + n], px[:, 0:n], r=[PX], w=["xp%d" % c])
                            kb.ts("dve", xc[:, 0:n], xp[c][:, 0:n], cw(0), cb, ALU.mult, ALU.add, r=["xp%d" % c, "vecA"], w=["xc"])
                            for j in range(1, 4):
                                kb.stt(xc[:, 0:n], xp[c][:, j:j + n], cw(j), xc[:, 0:n], ALU.mult, ALU.add, r=["xp%d" % c, "vecA", "xc"], w=["xc"])
                            if tg == 3:
                                kb.tr(ps[6][0:3, c * 128:(c + 1) * 128], xp[c][:, n:n + 3], ident, r=["xp%d" % c, "C"], w=[PS[6]])
                            else:
                                kb.cp("pool", xp[c][:, 0:3], xp[c][:, n:n + 3], r=["xp%d" % c], w=["xp%d" % c])
                        else:
                            kb.cp("dve", xps[:, :, 0:3], cvT[:, c, :].rearrange("p (q j) -> p q j", j=3), r=["cvT"], w=["xps"])
                            kb.cp("act", xps[:, :, 3:11], px[:, 0:n].rearrange("p (q t) -> p q t", t=8), r=[PX], w=["xps"])
                            xc3 = xc[:, 0:n].rearrange("p (q t) -> p q t", t=8)
                            kb.ts("dve", xc3, xps[:, :, 0:8], cw(0), cb, ALU.mult, ALU.add, r=["xps", "vecA"], w=["xc"])
                            for j in range(1, 4):
                                kb.stt(xc3, xps[:, :, j:j + 8], cw(j), xc3, ALU.mult, ALU.add, r=["xps", "vecA", "xc"], w=["xc"])
                            for j in range(3):
                                kb.tr(ps[5][0:16, j * 128:(j + 1) * 128], xps[:, :, 8 + j], ident, r=["xps", "C"], w=[PS[5]], sig=(j == 2))
                            kb.cp("dve", t5[0:16, 0:384], ps[5][0:16, 0:384], r=[PS[5]], w=["t5"])
                            kb.dma("sp", o_srconv.rearrange("(q j) f -> q j f", j=3)[:, :, c * 128:(c + 1) * 128],
                                   t5[0:16, 0:384].rearrange("q (j f) -> q j f", f=128), r=["t5"], w=())
                        kb.mm(ps[4][:, 0:n], lhsT=WA[:, c, :], rhs=xc[:, 0:n], start=True, stop=True, r=["WA", "xc"], w=[PS[4]])
                        kb.mm(ps[5][:, 0:n], lhsT=WX[:, c, :], rhs=xc[:, 0:n], start=True, stop=True, r=["WX", "xc"], w=[PS[5]])
                        kb.act(rr[:, 0:n], ps[4][:, 0:n], AF.Sigmoid, r=[PS[4], "vecA"], w=["rr"], bias=vecA[:, 20 + c:21 + c], scale=1.0)
                        kb.act(ii[:, 0:n], ps[5][:, 0:n], AF.Sigmoid, r=[PS[5], "vecA"], w=["ii"], bias=vecA[:, 24 + c:25 + c], scale=1.0)
                        kb.act(aa[:, 0:n], rr[:, 0:n], AF.Exp, r=["rr", "cl"], w=["aa"], scale=cl[:, c:c + 1])
                        kb.act(a2[:, 0:n], rr[:, 0:n], AF.Exp, r=["rr", "cl2"], w=["a2"], scale=cl2[:, c:c + 1])
                        kb.act(a2[:, 0:n], a2[:, 0:n], AF.Sqrt, r=["a2"], w=["a2"], bias=1.0, scale=-1.0)
                        kb.tt("dve", uu[:, 0:n], a2[:, 0:n], ii[:, 0:n], ALU.mult, r=["a2", "ii"], w=["uu"])
                        kb.tt("dve", uu[:, 0:n], uu[:, 0:n], xc[:, 0:n], ALU.mult, r=["uu", "xc"], w=["uu"])
                        if not sample:
                            s.add("dve", lambda g_, c=c, n=n: g_.tensor_tensor_scan(out=hh_[:, 0:n], data0=aa[:, 0:n], data1=uu[:, 0:n], initial=hst[:, c:c + 1],
                                                                                     op0=ALU.mult, op1=ALU.add),
                                  r=["aa", "uu", "hst"], w=["hh"], tag="scanH")
                            kb.cp("dve", hst[:, c:c + 1], hh_[:, n - 1:n], r=["hh"], w=["hst"])
                        else:
                            aa3 = aa[:, 0:n].rearrange("p (q t) -> p q t", t=8)
                            uu3 = uu[:, 0:n].rearrange("p (q t) -> p q t", t=8)
                            kb.tt("dve", t5[:, 0:16].unsqueeze(2), aa3[:, :, 0:1], h0T[:, c, :].unsqueeze(2), ALU.mult, r=["aa", "h0T", "t5"], w=["t5"])
                            kb.tt("dve", uu3[:, :, 0:1], uu3[:, :, 0:1], t5[:, 0:16].unsqueeze(2), ALU.add, r=["uu", "t5"], w=["uu"])
                            kb.tt("dve", aa[:, 0:n], aa[:, 0:n], rst, ALU.mult, r=["aa", "C"], w=["aa"])
                            s.add("dve", lambda g_, n=n: g_.tensor_tensor_scan(out=hh_[:, 0:n], data0=aa[:, 0:n], data1=uu[:, 0:n], initial=0.0,
                                                                                op0=ALU.mult, op1=ALU.add),
                                  r=["aa", "uu"], w=["hh"], tag="scanH")
                            kb.cp("dve", hl[:, c, :].unsqueeze(2), hh_[:, 0:n].rearrange("p (q t) -> p q t", t=8)[:, :, 7:8], r=["hh"], w=["hl"])
                        kb.act(t5[:, 0:n], pgr[:, 0:n], AF.Square, r=[PGR, "t5"], w=["t5"])
                        kb.ts("dve", t5[:, 0:n], t5[:, 0:n], 0.044715, 1.0, ALU.mult, ALU.add, r=["t5"], w=["t5"])
                        kb.tt("dve", t5[:, 0:n], t5[:, 0:n], pgr[:, 0:n], ALU.mult, r=["t5", PGR], w=["t5"])
                        kb.act(t5[:, 0:n], t5[:, 0:n], AF.Tanh, r=["t5"], w=["t5"], scale=0.7978845608028654)
                        kb.ts("dve", t5[:, 0:n], t5[:, 0:n], 1.0, 0.5, ALU.add, ALU.mult, r=["t5"], w=["t5"])
                        kb.tt("dve", t5[:, 0:n], t5[:, 0:n], pgr[:, 0:n], ALU.mult, r=["t5", PGR], w=["t5"])
                        kb.tt("dve", hrT[:, c, t0:t0 + n], t5[:, 0:n], hh_[:, 0:n], ALU.mult, r=["t5", "hh"], w=["hrT_g%d" % tg])
                    if tg == 3:
                        kb.cp("dve", t5[0:3, :], ps[6][0:3, 0:512], r=[PS[6]], w=["t5"])
                        kb.dma("sp", o_prconv, t5[0:3, :], r=["t5"], w=())
                kb.tr(ps[7][0:4, 0:128], hst[:, 0:4], ident, r=["hst", "C"], w=[PS[7]])
                kb.cp("dve", rr[0:4, 0:128], ps[7][0:4, 0:128], r=[PS[7]], w=["rr"])
                kb.dma("sp", o_prh, rr[0:4, 0:128], r=["rr"], w=())
                for c in range(4):
                    kb.tr(ps[4][0:16, c * 128:(c + 1) * 128], hl[:, c, :], ident, r=["hl", "C"], w=[PS[4]], sig=(c == 3))
                kb.cp("dve", ii[0:16, :], ps[4][0:16, :], r=[PS[4]], w=["ii"])
                kb.dma("sp", o_srh, ii[0:16, :], r=["ii"], w=())
                s.barrier()
            stop_at(7)
            with contextlib.ExitStack() as pc_:
                alloc_gl(pc_)
                mod_prepare(l, 1, 1.0, blocks=[4, 5])
                Gp, Gs = gl["Gp"], gl["Gs"]
                wout = kb.sb(pc_, "wout", [128, 8, D], BF16)
                kb.dma("pool", wout[:], ab_w_out[0].rearrange("(kc p) n -> p kc n", p=128), r=(), w=["wout"])
                v_ = 0
                for i in range(NT):
                    G = Gp if i < 16 else Gs
                    for half in range(2):
                        py = 4 + (v_ % 4)
                        for kc in range(8):
                            src = hmT[:, kc, i * 128:(i + 1) * 128] if kc < 4 else hrT[:, kc - 4, i * 128:(i + 1) * 128]
                            rn = ("hmT%d" % i) if kc < 4 else ("hrT_g%d" % (i // 4))
                            kb.mm(ps[py][:, :], lhsT=src, rhs=wout[:, kc, half * 512:(half + 1) * 512], start=(kc == 0), stop=(kc == 7),
                                  r=[rn, "wout"], w=[PS[py]])
                        kb.tt("dve", tA[v_ % 4][:], ps[py][:, :], G[:, half * 512:(half + 1) * 512], ALU.mult, r=[PS[py], "Gp", "Gs"], w=["tA%d" % (v_ % 4)])
                        kb.tt("pool", X[:, i, half * 512:(half + 1) * 512], X[:, i, half * 512:(half + 1) * 512], tA[v_ % 4][:], ALU.add,
                              r=["X%d" % i, "tA%d" % (v_ % 4)], w=["X%d" % i])
                        v_ += 1
                    layer_norm(i)
                s.barrier()
        s.barrier()

    CW = -0.6065306597126334
    RT = F32

    def rwkv_mixer(l):
        rwkv_mixer_(l)
        s.skip = False
        s.barrier()

    def rwkv_mixer_(l):
        rw_mu = kb.din("muT", [128, 48])
        rw_wr = kb.din("rw_wr", [1, D, D])
        rw_wk = kb.din("rw_wk", [1, D, D])
        rw_wv = kb.din("rw_wv", [1, D, D])
        rw_wo = kb.din("rw_wo", [1, D, D])
        rw_w0 = kb.din("rw_w0", [1, D])
        rw_w1 = kb.din("rw_w1", [1, D, 64])
        rw_w2 = kb.din("rw_w2", [1, 64, D])
        rw_a0 = kb.din("rw_a0", [1, D])
        rw_a1 = kb.din("rw_a1", [1, D, 64])
        rw_a2 = kb.din("rw_a2", [1, 64, D])
        rw_g1 = kb.din("rw_g1", [1, D, 128])
        rw_g2 = kb.din("rw_g2", [1, 128, D])
        rw_kk = kb.din("rw_k_k", [1, D])
        rw_ka = kb.din("rw_k_a", [1, D])
        rw_rk = kb.din("rk_flat", [1, D])
        rw_lng = kb.din("rw_lnx_g", [1, D])
        rw_lnb = kb.din("rw_lnx_b", [1, D])
        swkv = kb.din("swkv", [16, 16, 64, 64])
        sshift = kb.din("sshift", [16, D])
        o_pwkv = kb.dout("o_pwkv", [16, 64, 64])
        o_pshift = kb.dout("o_pshift", [1, D])
        o_swkv = kb.dout("o_swkv", [16, 16, 64, 64])
        o_sshift = kb.dout("o_sshift", [16, D])

        cm = lambda nm: C[:, CST_OFF[nm]:CST_OFF[nm] + 128]
        maskP, maskS, upP, lowP, upS, lowS, blkS, ones, bms = cm("maskP"), cm("maskS"), cm("upP"), cm("lowP"), cm("upS"), cm("lowS"), cm("blkS"), cm("ones"), C[:, CST_OFF["bms"]:CST_OFF["bms"] + 16]

        mod_prepare(l, 1, 1.0, blocks=range(4))
        with contextlib.ExitStack() as ph:
            ygT = kb.sb(ph, "ygT", [128, 8, NTOK], BF16)
            muT = kb.sb(ph, "muT_sb", [128, 48], F32)
            kb.dma("sp", muT[:], rw_mu, r=(), w=["muT"])
            identR = kb.sb(ph, "identR", [128, 128], RT)
            kb.cp("dve", identR[:], ident, r=["C"], w=["identR"])
            w1b = kb.sb(ph, "w1b", [128, 8, 64], BF16)
            a1b = kb.sb(ph, "a1b", [128, 8, 64], BF16)
            g1b = kb.sb(ph, "g1b", [128, 8, 128], BF16)
            kb.dma("pool", w1b[:], rw_w1[0].rearrange("(kc p) n -> p kc n", p=128), r=(), w=["w1b"])
            kb.dma("pool", a1b[:], rw_a1[0].rearrange("(kc p) n -> p kc n", p=128), r=(), w=["a1b"])
            kb.dma("pool", g1b[:], rw_g1[0].rearrange("(kc p) n -> p kc n", p=128), r=(), w=["g1b"])
            hlast = kb.sb(ph, "hlast", [128, 8, 17], F32)
            sh0T = kb.sb(ph, "sh0T", [128, 8, 16], BF16)
            with contextlib.ExitStack() as p0:
                shs = kb.sb(p0, "shs", [16, D], F32)
                kb.dma("sp", shs[:], sshift, r=(), w=["shs"])
                for c in range(8):
                    kb.tr(ps[0][:, c * 16:(c + 1) * 16], shs[0:16, c * 128:(c + 1) * 128], C[0:16, 0:16], r=["shs", "C"], w=[PS[0]], sig=(c == 7))
                kb.cp("dve", sh0T[:].rearrange("p a b -> p (a b)"), ps[0][:, 0:128], r=[PS[0]], w=["sh0T"])
                s.barrier()

            def hook_last(i, c, src, psres):
                if i == 15:
                    kb.ts("dve", hlast[:, c, 0:1], src[:, 127:128], modT[:, 8 + c, 0:1], modT[:, c, 0:1], ALU.mult, ALU.add, r=["modT", psres], w=["hlast"])
                elif i == 16:
                    v3 = src.rearrange("p (q t) -> p q t", t=8)[:, :, 7:8]
                    kb.tt("dve", hlast[:, c, 1:17].unsqueeze(2), v3, modT[:, 8 + c, 1:NT].unsqueeze(2), ALU.mult, r=["modT", psres], w=["hlast"])
                    kb.tt("dve", hlast[:, c, 1:17], hlast[:, c, 1:17], modT[:, c, 1:NT], ALU.add, r=["hlast", "modT"], w=["hlast"])

            stop_at(51)
            for hg in range(4):
                c0 = hg * 256
                with contextlib.ExitStack() as pp:
                    wrs = kb.sb(pp, "wrs", [128, 8, 256], BF16)
                    wks = kb.sb(pp, "wks", [128, 8, 256], BF16)
                    wvs = kb.sb(pp, "wvs", [128, 8, 256], BF16)
                    for wt, src in ((wrs, rw_wr), (wks, rw_wk), (wvs, rw_wv)):
                        kb.dma("pool", wt[:], src[0].rearrange("(kc p) n -> p kc n", p=128)[:, :, c0:c0 + 256], r=(), w=["wqkv"])
                    w2s = kb.sb(pp, "w2s", [64, 256], BF16)
                    a2s = kb.sb(pp, "a2s", [64, 256], BF16)
                    g2s = kb.sb(pp, "g2s", [128, 256], BF16)
                    kb.dma("pool", w2s[:], rw_w2[0, :, c0:c0 + 256], r=(), w=["w2s"])
                    kb.dma("pool", a2s[:], rw_a2[0, :, c0:c0 + 256], r=(), w=["a2s"])
                    kb.dma("pool", g2s[:], rw_g2[0, :, c0:c0 + 256], r=(), w=["g2s"])
                    w0r = kb.sb(pp, "w0r", [1, 256], F32)
                    a0r = kb.sb(pp, "a0r", [1, 256], F32)
                    kb.dma("sp", w0r[:], rw_w0[0:1, c0:c0 + 256], r=(), w=["w0r"])
                    kb.dma("sp", a0r[:], rw_a0[0:1, c0:c0 + 256], r=(), w=["a0r"])
                    bcs = {}
                    alias = {"kkb": tA[2][:, 0:256], "kab": tA[2][:, 256:512], "rkb": tA[3][:, 0:256], "lngb": tA[3][:, 256:512]}
                    for nm, src in (("kkb", rw_kk), ("kab", rw_ka), ("rkb", rw_rk), ("lngb", rw_lng), ("lnbb", rw_lnb)):
                        bcs[nm] = alias[nm] if nm in alias else kb.sb(pp, nm, [128, 256], F32)
                        kb.dma("sp", bcs[nm][:], src[0:1, c0:c0 + 256].to_broadcast([128, 256]), r=(), w=[nm])
                    hTg = [kb.sb(pp, "hTr%d" % i, [128, 8, 130], BF16) for i in range(2)]
                    dx = kb.sb(pp, "dx", [128, 8, 128], BF16)
                    xj = [kb.sb(pp, "xj%d" % i, [128, 8, 128], BF16) for i in range(2)]
                    loT = kb.sb(pp, "loT", [128, 3, 128], BF16)
                    TB = lambda j: tB[j // 4][:, (j % 4) * 256:(j % 4) * 256 + 256]
                    Rr, Kk, KKn, Aa, SG, CSs, Ee, Tt = [TB(j) for j in range(8)]
                    tbres = lambda j: "tB%d" % (j // 4)
                    Gt = tA[0][:, 0:256]
                    small = tA[1][:, 256:320]
                    F3 = lambda nm: kb.sb(pp, nm, [128, 256], RT)
                    Vv, AL, KT, RB, KH, BH = F3("Vv"), F3("AL"), F3("KT"), F3("RB"), F3("KH"), F3("BH")
                    BT = tA[0][:, 256:512]
                    fmT = {nm: kb.sb(pp, nm, [64, 4, 128], RT) for nm in ("alT", "btT", "ktT", "rbT")}
                    chainall = kb.sb(pp, "chainall", [128, 6, 4, 128], RT)
                    chn = {nm: chainall[:, j] for j, nm in enumerate(("ApA", "ApB", "BpA", "BpB", "TTa", "TTb"))}
                    S0nat = kb.sb(pp, "S0nat", [64, 16, 64], F32)
                    S0Tq = kb.sb(pp, "S0Tq", [64, 16, 64], RT)
                    SLo = S0nat
                    AakT = kb.sb(pp, "AakT", [128, 4, 128], RT)
                    YTs = AakT[0:64, :, :]
                    ArbT = kb.sb(pp, "ArbT", [128, 4, 128], RT)
                    ArkT = kb.sb(pp, "ArkT", [128, 4, 128], RT)
                    Ahat = kb.sb(pp, "Ahat", [128, 4, 64], RT)
                    X1 = kb.sb(pp, "X1", [128, 4, 64], RT)
                    U0 = kb.sb(pp, "U0", [128, 4, 64], RT)
                    Gm = kb.sb(pp, "Gm", [64, 4, 64], RT)
                    RhT = kb.sb(pp, "RhT", [64, 4, 128], RT)
                    S0T = kb.sb(pp, "S0T", [64, 4, 64], RT)
                    PLc = tA[1][0:64, 320:384]
                    yb = tA[1][:, 0:256]
                    kb.ts("dve", S0T[:].rearrange("p a b -> p (a b)"), C[0:64, 0:256], 0.0, None, ALU.mult, None, r=["C"], w=["S0T"])
                    kb.memset("pool", hTg[1][:, :, 128:129], 0.0, w=["hTr1"])

                    algb = [4]

                    def nb():
                        b_ = algb[0]
                        algb[0] = 4 + (algb[0] - 3) % 4
                        return b_

                    def grp4(mmf, n_cols, m_rows=128):
                        b_ = nb()
                        for hl in range(4):
                            items = mmf(hl)
                            for j, (lt, rh, rd) in enumerate(items):
                                kb.mm(ps[b_][0:m_rows, hl * n_cols:(hl + 1) * n_cols], lhsT=lt, rhs=rh, start=(j == 0), stop=(j == len(items) - 1),
                                      r=rd, w=[PS[b_]], sig=(hl == 3 and j == len(items) - 1))
                        return b_

                    fsl = lambda t_, hl: t_[:, hl, :]
                    tsl = lambda t_, hl: t_[:, hl * 64:(hl + 1) * 64]

                    def sample_states():
                        Bhm = chainall[:, 0:2].rearrange("p a h t -> p (a h t)").rearrange("p (q k) -> p q k", k=64)
                        Khm = chainall[:, 2:4].rearrange("p a h t -> p (a h t)").rearrange("p (q k) -> p q k", k=64)
                        Gq = chainall[0:64, 4:6].rearrange("p a h t -> p (a h t)").rearrange("p (q k) -> p q k", k=64)
                        bmq = bms.unsqueeze(2).to_broadcast([128, 16, 64])
                        for hl in range(4):
                            h = hg * 4 + hl
                            kb.tt("dve", Bhm, tsl(BH, hl).unsqueeze(1).to_broadcast([128, 16, 64]), bmq, ALU.mult, r=["BH", "C"], w=["ApA", "ApB"])
                            kb.tt("pool", Khm, tsl(KH, hl).unsqueeze(1).to_broadcast([128, 16, 64]), bmq, ALU.mult, r=["KH", "C"], w=["BpA", "BpB"])
                            kb.dma("sp", S0nat[:], swkv[:, h].rearrange("q v k -> v q k"), r=(), w=["S0nat"])
                            b0, b1 = nb(), nb()
                            for q in range(16):
                                bb = b0 if q < 8 else b1
                                kb.tr(ps[bb][0:64, (q % 8) * 64:(q % 8 + 1) * 64], S0nat[:, q, :], ident[0:64, 0:64], r=["S0nat", "C"], w=[PS[bb]], sig=(q % 8 == 7))
                            kb.cp("act", S0Tq[:, 0:8, :], ps[b0][0:64, :].rearrange("p (q k) -> p q k", k=64), r=[PS[b0]], w=["S0Tq"])
                            kb.cp("dve", S0Tq[:, 8:16, :], ps[b1][0:64, :].rearrange("p (q k) -> p q k", k=64), r=[PS[b1]], w=["S0Tq"])
                            b_ = nb()
                            for q in range(16):
                                kb.mm(ps[b_][0:64, q * 8:(q + 1) * 8], lhsT=S0Tq[:, q, :], rhs=RhT[:, hl, q * 8:(q + 1) * 8], start=True, stop=True,
                                      r=["S0Tq", "RhT"], w=[PS[b_]], sig=(q == 15))
                            kb.cp("act", YTs[:, hl, :], ps[b_][0:64, 0:128], r=[PS[b_]], w=["AakT"])
                            g0, g1 = nb(), nb()
                            for half, bb in ((0, g0), (1, g1)):
                                kb.mm(ps[bb][0:64, :], lhsT=Ahat[:, hl, :], rhs=Bhm[:, half * 8:(half + 1) * 8, :], start=True, stop=True,
                                      r=["tA2", "ApA", "ApB"], w=[PS[bb]])
                            for q in range(16):
                                bb = g0 if q < 8 else g1
                                kb.stt(Gq[:, q, :], ident[0:64, 0:64], PLc[:, hl * 16 + q:hl * 16 + q + 1], ps[bb][0:64, (q % 8) * 64:(q % 8 + 1) * 64], ALU.mult, ALU.add,
                                       r=["C", "tA1", PS[bb]], w=["TTa", "TTb"])
                            for half in range(2):
                                bb = nb()
                                kb.mm(ps[bb][0:64, :], lhsT=tsl(Vv, hl), rhs=Khm[:, half * 8:(half + 1) * 8, :], start=True, stop=False, r=["Vv", "BpA", "BpB"], w=[PS[bb]], sig=False)
                                kb.mm(ps[bb][0:64, :], lhsT=U0[:, hl, :], rhs=Bhm[:, half * 8:(half + 1) * 8, :], start=False, stop=False, r=["tA3", "ApA", "ApB"], w=[PS[bb]], sig=False)
                                for qq in range(8):
                                    q = half * 8 + qq
                                    kb.mm(ps[bb][0:64, qq * 64:(qq + 1) * 64], lhsT=S0Tq[:, q, :], rhs=Gq[:, q, :], start=False, stop=(qq == 7),
                                          r=["S0Tq", "TTa", "TTb"], w=[PS[bb]], sig=(qq == 7))
                                kb.cp("act" if half else "dve", SLo[:, half * 8:(half + 1) * 8, :], ps[bb][0:64, :].rearrange("p (q k) -> p q k", k=64), r=[PS[bb]], w=["S0nat"])
                            kb.dma("sp", o_swkv[:, h].rearrange("q v k -> v q k"), SLo[:], r=["S0nat"], w=())
                        by = grp4(lambda hl: [(ArkT[:, hl, :], tsl(Vv, hl), ["ArkT", "Vv"]), (ArbT[:, hl, :], U0[:, hl, :], ["ArbT", "tA3"]),
                                               (YTs[:, hl, :], identR[0:64, 0:64], ["AakT", "identR"])], 64)
                        kb.cp("act", yb[:], ps[by][:, 0:256], r=[PS[by]], w=["tA1"])

                    for i in range(NT):
                        sample = i == 16
                        k_ = i % 2
                        hT = hTg[k_]
                        hres = "hTr%d" % k_
                        last_pass = hg == 3
                        if hg == 0 and i == 1:
                            stop_at(58)
                        if hg == 0 and i == 16:
                            stop_at(59)
                        make_hT(hT, i, prescale=last_pass, col0=1, res=hres, hook=(hook_last if hg == 0 else None))
                        if i == 0:
                            kb.memset("pool", hT[:, :, 0:1], 0.0, w=[hres])
                        elif not sample:
                            kb.cp("pool", hT[:, :, 0:1], hTg[1 - k_][:, :, 128:129], r=["hTr%d" % (1 - k_)], w=[hres])
                        cur = hT[:, :, 1:129]
                        if not sample:
                            kb.tt("pool", dx[:], hT[:, :, 0:128], cur, ALU.subtract, r=[hres], w=["dx"])
                        else:
                            kb.cp("pool", dx[:], hT[:, :, 0:128], r=[hres], w=["dx"])
                            kb.cp("pool", dx[:].rearrange("p c (q t) -> p c q t", t=8)[:, :, :, 0], sh0T[:], r=["sh0T"], w=["dx"])
                            kb.tt("pool", dx[:], dx[:], cur, ALU.subtract, r=["dx", hres], w=["dx"])
                        def mix(j, buf):
                            e = "dve" if j % 2 == 0 else "pool"
                            kb.tt(e, xj[buf][:], dx[:], muT[:, j * 8:(j + 1) * 8].unsqueeze(2).to_broadcast([128, 8, 128]), ALU.mult, r=["dx", "muT"], w=["xj%d" % buf])
                            kb.tt(e, xj[buf][:], xj[buf][:], cur, ALU.add, r=["xj%d" % buf, hres], w=["xj%d" % buf])
                            return xj[buf], "xj%d" % buf
                        for j, (wt, bank, off) in ((0, (wrs, 0, 0)), (2, (wks, 0, 256)), (3, (wvs, 1, 0))):
                            xx, xr_ = mix(j, j % 2)
                            for kc in range(8):
                                kb.mm(ps[bank][:, off:off + 256], lhsT=xx[:, kc, :], rhs=wt[:, kc, :], start=(kc == 0), stop=(kc == 7), r=[xr_, "wqkv"], w=[PS[bank]])
                        for j, (wt, wr_, m_, off3) in ((1, (w1b, "w1b", 64, 0)), (4, (a1b, "a1b", 64, 128)), (5, (g1b, "g1b", 128, 256))):
                            xx, xr_ = mix(j, j % 2)
                            for kc in range(8):
                                kb.mm(ps[3][0:m_, off3:off3 + 128], lhsT=wt[:, kc, :], rhs=xx[:, kc, :], start=(kc == 0), stop=(kc == 7), r=[xr_, wr_], w=[PS[3]])
                        kb.act(loT[0:64, 0, :], ps[3][0:64, 0:128], AF.Tanh, r=[PS[3]], w=["loT"])
                        kb.cp("act", loT[0:64, 1, :], ps[3][0:64, 128:256], r=[PS[3]], w=["loT"])
                        kb.act(loT[:, 2, :], ps[3][:, 256:384], AF.Sigmoid, r=[PS[3]], w=["loT"])
                        kb.mm(ps[1][:, 256:512], lhsT=loT[0:64, 0, :], rhs=w2s[:], start=True, stop=False, r=["loT", "w2s"], w=[PS[1]], sig=False)
                        kb.mm(ps[1][:, 256:512], lhsT=ones[0:1, :], rhs=w0r[:], start=False, stop=True, r=["C", "w0r"], w=[PS[1]])
                        kb.mm(ps[2][:, 0:256], lhsT=loT[0:64, 1, :], rhs=a2s[:], start=True, stop=False, r=["loT", "a2s"], w=[PS[2]], sig=False)
                        kb.mm(ps[2][:, 0:256], lhsT=ones[0:1, :], rhs=a0r[:], start=False, stop=True, r=["C", "a0r"], w=[PS[2]])
                        kb.mm(ps[2][:, 256:512], lhsT=loT[:, 2, :], rhs=g2s[:], start=True, stop=True, r=["loT", "g2s"], w=[PS[2]])
                        if hg == 0 and i == 0:
                            stop_at(52)
                        kb.cp("act", Rr, ps[0][:, 0:256], r=[PS[0]], w=[tbres(0)])
                        kb.cp("act", Kk, ps[0][:, 256:512], r=[PS[0]], w=[tbres(1)])
                        kb.cp("act", Vv[:], ps[1][:, 0:256], r=[PS[1]], w=["Vv"])
                        kb.act(SG, ps[1][:, 256:512], AF.Sigmoid, r=[PS[1]], w=[tbres(4)])
                        kb.act(Aa, ps[2][:, 0:256], AF.Sigmoid, r=[PS[2]], w=[tbres(3)])
                        kb.cp("act", Gt, ps[2][:, 256:512], r=[PS[2]], w=["tA0"])
                        kb.tt("dve", KKn, Kk, bcs["kkb"][:], ALU.mult, r=[tbres(1), "kkb"], w=[tbres(2)])
                        kb.tt("dve", Tt, KKn, KKn, ALU.mult, r=[tbres(2)], w=[tbres(7)])
                        s.add("dve", lambda g_: g_.tensor_reduce(out=small[:, 0:4], in_=Tt.rearrange("p (h k) -> p h k", k=64), axis=AX.X, op=ALU.add),
                              r=[tbres(7)], w=["tA1"], tag="red")
                        kb.act(small[:, 0:4], small[:, 0:4], AF.Sqrt, r=["tA1"], w=["tA1"])
                        kb.ts("dve", small[:, 0:4], small[:, 0:4], 1e-12, None, ALU.max, None, r=["tA1"], w=["tA1"])
                        s.add("dve", lambda g_: g_.reciprocal(out=small[:, 0:4], in_=small[:, 0:4]), r=["tA1"], w=["tA1"], tag="recip")
                        kb.tt("dve", KKn.rearrange("p (h k) -> p h k", k=64), KKn.rearrange("p (h k) -> p h k", k=64),
                              small[:, 0:4].unsqueeze(2).to_broadcast([128, 4, 64]), ALU.mult, r=[tbres(2), "tA1"], w=[tbres(2)])
                        kb.stt(Tt, Aa, -1.0, bcs["kab"][:], ALU.add, ALU.mult, r=[tbres(3), "kab"], w=[tbres(7)])
                        kb.tt("dve", Tt, Tt, Kk, ALU.mult, r=[tbres(7), tbres(1)], w=[tbres(7)])
                        kb.tt("dve", Kk, Kk, Tt, ALU.add, r=[tbres(1), tbres(7)], w=[tbres(1)])
                        kb.tt("pool", Tt, Rr, Kk, ALU.mult, r=[tbres(0), tbres(1)], w=[tbres(7)])
                        kb.tt("pool", Tt, Tt, bcs["rkb"][:], ALU.mult, r=[tbres(7), "rkb"], w=[tbres(7)])
                        s.add("dve", lambda g_: g_.tensor_reduce(out=small[:, 4:8], in_=Tt.rearrange("p (h k) -> p h k", k=64), axis=AX.X, op=ALU.add),
                              r=[tbres(7)], w=["tA1"], tag="red")
                        kb.tt("pool", Aa, KKn, Aa, ALU.mult, r=[tbres(2), tbres(3)], w=[tbres(3)])
                        if hg == 0 and i == 0:
                            stop_at(53)
                        Um, Jm = (maskP, ones) if not sample else (maskS, blkS)
                        kb.mm(ps[4][:, 0:256], lhsT=Um, rhs=SG, start=True, stop=True, r=["C", tbres(4)], w=[PS[4]])
                        kb.mm(ps[4][:, 256:512], lhsT=Jm, rhs=SG, start=True, stop=True, r=["C", tbres(4)], w=[PS[4]])
                        nq = 1 if not sample else 16
                        for hl in range(4):
                            kb.mm(ps[3][0:64, 384 + hl * nq:384 + (hl + 1) * nq], lhsT=SG[:, hl * 64:(hl + 1) * 64], rhs=(ones[:, 0:1] if not sample else bms),
                                  start=True, stop=True, r=[tbres(4), "C"], w=[PS[3]], sig=(hl == 3))
                        kb.act(PLc[:, 0:4 * nq], ps[3][0:64, 384:384 + 4 * nq], AF.Exp, r=[PS[3]], w=["tA1"], scale=CW)
                        kb.cp("act", CSs, ps[4][:, 0:256], r=[PS[4]], w=[tbres(5)])
                        kb.tt("dve", Tt, CSs, SG, ALU.subtract, r=[tbres(5), tbres(4)], w=[tbres(7)])
                        kb.act(Ee, Tt, AF.Exp, r=[tbres(7)], w=[tbres(6)], scale=CW)
                        kb.stt(AL[:], KKn, -1.0, Ee, ALU.mult, ALU.mult, r=[tbres(2), tbres(6)], w=["AL"])
                        kb.act(Ee, CSs, AF.Exp, r=[tbres(5), "AL"], w=[tbres(6)], scale=-CW)
                        kb.tt("dve", BT[:], Aa, Ee, ALU.mult, r=[tbres(3), tbres(6)], w=["tA0"])
                        kb.tt("pool", KT[:], Kk, Ee, ALU.mult, r=[tbres(1), tbres(6)], w=["KT"])
                        kb.act(Tt, CSs, AF.Exp, r=[tbres(5)], w=[tbres(7)], scale=CW)
                        kb.tt("dve", RB[:], Rr, Tt, ALU.mult, r=[tbres(0), tbres(7)], w=["RB"])
                        kb.tt("dve", Ee, ps[4][:, 256:512], CSs, ALU.subtract, r=[PS[4], tbres(5), "tA0", "KT"], w=[tbres(6)])
                        kb.act(Ee, Ee, AF.Exp, r=[tbres(6)], w=[tbres(6)], scale=CW)
                        kb.tt("dve", KH[:], Kk, Ee, ALU.mult, r=[tbres(1), tbres(6)], w=["KH"])
                        kb.tt("pool", BH[:], Aa, Ee, ALU.mult, r=[tbres(3), tbres(6)], w=["BH"])
                        for qi, (nm, src, sr) in enumerate((("alT", AL, "AL"), ("btT", BT, "tA0"), ("ktT", KT, "KT"), ("rbT", RB, "RB"))):
                            b_ = 5 + qi % 2
                            for hl in range(4):
                                sv = src[:, hl * 64:(hl + 1) * 64]
                                kb.tr(ps[b_][0:64, hl * 128:(hl + 1) * 128], sv.bitcast(F32) if nm != "btT" else sv, ident,
                                      r=[sr, "C"], w=[PS[b_]], sig=(hl == 3))
                            kb.cp("act" if qi % 2 else "dve", fmT[nm][:].rearrange("p a b -> p (a b)"), ps[b_][0:64, :], r=[PS[b_]], w=[nm])
                        if hg == 0 and i == 0:
                            stop_at(54)
                        alT, btT, ktT, rbT = fmT["alT"], fmT["btT"], fmT["ktT"], fmT["rbT"]
                        mlow, mup, minc = (lowP, upP, maskP) if not sample else (lowS, upS, maskS)
                        bc4 = lambda m_: m_.unsqueeze(1).to_broadcast([128, 4, 128])
                        v4 = lambda b_: ps[b_][:, :].rearrange("p (h t) -> p h t", t=128)
                        b_ = grp4(lambda hl: [(fsl(alT, hl), fsl(btT, hl), ["alT", "btT"])], 128)
                        kb.tt("dve", chn["ApA"][:], v4(b_), bc4(mlow), ALU.mult, r=[PS[b_], "C"], w=["ApA"])
                        b_ = grp4(lambda hl: [(fsl(btT, hl), fsl(alT, hl), ["alT", "btT"])], 128)
                        kb.tt("dve", chn["BpA"][:], v4(b_), bc4(mup), ALU.mult, r=[PS[b_], "C"], w=["BpA"])
                        b_ = grp4(lambda hl: [(fsl(ktT, hl), fsl(alT, hl), ["alT", "ktT"])], 128)
                        kb.tt("dve", AakT[:], v4(b_), bc4(mup), ALU.mult, r=[PS[b_], "C"], w=["AakT"])
                        b_ = grp4(lambda hl: [(fsl(btT, hl), fsl(rbT, hl), ["rbT", "btT"])], 128)
                        kb.tt("dve", ArbT[:], v4(b_), bc4(minc), ALU.mult, r=[PS[b_], "C"], w=["ArbT"])
                        b_ = grp4(lambda hl: [(fsl(ktT, hl), fsl(rbT, hl), ["rbT", "ktT"])], 128)
                        kb.tt("dve", ArkT[:], v4(b_), bc4(minc), ALU.mult, r=[PS[b_], "C"], w=["ArkT"])
                        if hg == 0 and i == 0:
                            stop_at(55)
                        kb.tt("pool", chn["TTa"][:], chn["BpA"][:], bc4(ident), ALU.add, r=["BpA", "C"], w=["TTa"])
                        Ap, Bp, TT = "ApA", "BpA", "TTa"
                        n_it = 6 if not sample else 2
                        for it in range(n_it):
                            Ap2 = "ApB" if Ap == "ApA" else "ApA"
                            Bp2 = "BpB" if Bp == "BpA" else "BpA"
                            TT2 = "TTb" if TT == "TTa" else "TTa"
                            b_ = grp4(lambda hl: [(chn[Bp][:, hl, :], chn[Ap][:, hl, :], [Ap, Bp])], 128)
                            kb.cp("act", chn[Ap2][:], v4(b_), r=[PS[b_]], w=[Ap2])
                            if it < n_it - 1:
                                b_ = grp4(lambda hl: [(chn[Ap][:, hl, :], chn[Bp][:, hl, :], [Ap, Bp])], 128)
                                kb.cp("act", chn[Bp2][:], v4(b_), r=[PS[b_]], w=[Bp2])
                            b_ = grp4(lambda hl: [(chn[Ap2][:, hl, :], chn[TT][:, hl, :], [Ap2, TT])], 128)
                            kb.tt("dve", chn[TT2][:], v4(b_), chn[TT][:], ALU.add, r=[PS[b_], TT], w=[TT2])
                            Ap, Bp, TT = Ap2, Bp2, TT2
                        if hg == 0 and i == 0:
                            stop_at(56)
                        TTt = chn[TT]
                        v64 = lambda b_, m_=128: ps[b_][0:m_, 0:256].rearrange("p (h k) -> p h k", k=64)
                        b_ = grp4(lambda hl: [(TTt[:, hl, :], tsl(AL, hl), [TT, "AL"])], 64)
                        kb.cp("act", Ahat[:], v64(b_), r=[PS[b_]], w=["tA2"])
                        b_ = grp4(lambda hl: [(AakT[:, hl, :], tsl(Vv, hl), ["AakT", "Vv"])], 64)
                        kb.cp("dve", X1[:], v64(b_), r=[PS[b_]], w=["tA2"])
                        b_ = grp4(lambda hl: [(TTt[:, hl, :], X1[:, hl, :], [TT, "tA2"])], 64)
                        kb.cp("act", U0[:], v64(b_), r=[PS[b_]], w=["tA3"])
                        b_ = grp4(lambda hl: [(tsl(RB, hl), identR[:], ["RB", "identR"]), (Ahat[:, hl, :], ArbT[:, hl, :], ["tA2", "ArbT"])], 128, m_rows=64)
                        kb.cp("dve", RhT[:], ps[b_][0:64, :].rearrange("p (h t) -> p h t", t=128), r=[PS[b_]], w=["RhT"])
                        if not sample:
                            b_ = grp4(lambda hl: [(Ahat[:, hl, :], tsl(BH, hl), ["tA2", "BH"])], 64, m_rows=64)
                            for hl in range(4):
                                kb.stt(Gm[:, hl, :], ident[0:64, 0:64], PLc[:, hl:hl + 1], ps[b_][0:64, hl * 64:(hl + 1) * 64], ALU.mult, ALU.add,
                                       r=["C", "tA1", PS[b_]], w=["tA3"])
                            by = grp4(lambda hl: [(ArkT[:, hl, :], tsl(Vv, hl), ["ArkT", "Vv"]), (ArbT[:, hl, :], U0[:, hl, :], ["ArbT", "tA3"]),
                                                   (RhT[:, hl, :], S0T[:, hl, :], ["RhT", "S0T"])], 64)
                            kb.cp("act", yb[:], ps[by][:, 0:256], r=[PS[by]], w=["tA1"])
                            if i == 15:
                                b_ = grp4(lambda hl: [(tsl(Vv, hl), tsl(KH, hl), ["Vv", "KH"]), (U0[:, hl, :], tsl(BH, hl), ["tA3", "BH"]),
                                                       (S0T[:, hl, :], Gm[:, hl, :], ["S0T", "tA3"])], 64, m_rows=64)
                                kb.cp("dve", Tt[0:64, :], ps[b_][0:64, 0:256], r=[PS[b_]], w=[tbres(7)])
                                kb.dma("sp", o_pwkv[hg * 4:hg * 4 + 4].rearrange("h v k -> v h k"), Tt[0:64, :].rearrange("p (h k) -> p h k", k=64), r=[tbres(7)], w=())
                            b_ = grp4(lambda hl: [(tsl(KH, hl), tsl(Vv, hl), ["Vv", "KH"]), (tsl(BH, hl), U0[:, hl, :], ["tA3", "BH"]),
                                                   (Gm[:, hl, :], S0T[:, hl, :], ["S0T", "tA3"])], 64, m_rows=64)
                            kb.cp("dve", S0T[:], v64(b_, 64), r=[PS[b_]], w=["S0T"])
                        else:
                            sample_states()
                        if hg == 0 and i == 0:
                            stop_at(57)
                        if hg == 0 and i == 16:
                            stop_at(60)
                        for hl in range(4):
                            s.add("dve", lambda g_, hl=hl: g_.bn_stats(out=small[:, 8 + hl * 6:14 + hl * 6], in_=yb[:, hl * 64:(hl + 1) * 64]), r=["tA1"], w=["tA1"], tag="bnst")
                        for hl in range(4):
                            s.add("dve", lambda g_, hl=hl: g_.bn_aggr(out=small[:, 32 + hl * 2:34 + hl * 2], in_=small[:, 8 + hl * 6:14 + hl * 6]), r=["tA1"], w=["tA1"], tag="bnag")
                        mvv = small[:, 32:40].rearrange("p (h two) -> p h two", two=2)
                        kb.act(small[:, 40:44].unsqueeze(2), mvv[:, :, 1:2], AF.Sqrt, r=["tA1"], w=["tA1"], bias=64e-5, scale=1.0)
                        s.add("dve", lambda g_: g_.reciprocal(out=small[:, 40:44], in_=small[:, 40:44]), r=["tA1"], w=["tA1"], tag="recip")
                        kb.tt("dve", small[:, 44:48].unsqueeze(2), mvv[:, :, 0:1], small[:, 40:44].unsqueeze(2), ALU.mult, r=["tA1"], w=["tA1"])
                        kb.ts("dve", small[:, 44:48], small[:, 44:48], -1.0, None, ALU.mult, None, r=["tA1"], w=["tA1"])
                        for hl in range(4):
                            kb.act(yb[:, hl * 64:(hl + 1) * 64], yb[:, hl * 64:(hl + 1) * 64], AF.Identity, r=["tA1", "tA1"], w=["tA1"],
                                   bias=small[:, 44 + hl:45 + hl], scale=small[:, 40 + hl:41 + hl])
                        kb.tt("pool", yb[:], yb[:], bcs["lngb"][:], ALU.mult, r=["tA1", "lngb"], w=["tA1"])
                        kb.tt("pool", yb[:], yb[:], bcs["lnbb"][:], ALU.add, r=["tA1", "lnbb"], w=["tA1"])
                        kb.tt("dve", Tt.rearrange("p (h k) -> p h k", k=64), Vv[:].bitcast(F32).rearrange("p (h k) -> p h k", k=64),
                              small[:, 4:8].unsqueeze(2).to_broadcast([128, 4, 64]), ALU.mult, r=["Vv", "tA1"], w=[tbres(7)])
                        kb.tt("dve", yb[:], yb[:], Tt, ALU.add, r=["tA1", tbres(7)], w=["tA1"])
                        kb.tt("dve", yb[:], yb[:], Gt, ALU.mult, r=["tA1", "tA0"], w=["tA1"])
                        for cc in range(2):
                            kb.tr(ps[7][:, cc * 128:(cc + 1) * 128], yb[:, cc * 128:(cc + 1) * 128], ident, r=["tA1", "C"], w=[PS[7]], sig=(cc == 1))
                        kb.cp("act", ygT[:, 2 * hg:2 * hg + 2, i * 128:(i + 1) * 128], ps[7][:, 0:256].rearrange("p (c t) -> p c t", t=128), r=[PS[7]], w=["ygT%d" % i])
                    s.barrier()
            for c in range(8):
                kb.tr(ps[0][0:17, c * 128:(c + 1) * 128] if c < 4 else ps[1][0:17, (c - 4) * 128:(c - 3) * 128], hlast[:, c, :], ident, r=["hlast", "C"],
                      w=[PS[0] if c < 4 else PS[1]], sig=(c in (3, 7)))
            kb.cp("dve", tB[0][0:17, 0:512], ps[0][0:17, :], r=[PS[0]], w=["tB0"])
            kb.cp("dve", tB[0][0:17, 512:1024], ps[1][0:17, :], r=[PS[1]], w=["tB0"])
            kb.dma("sp", o_pshift, tB[0][0:1, :], r=["tB0"], w=())
            kb.dma("sp", o_sshift, tB[0][1:17, :], r=["tB0"], w=())
            s.barrier()
            with contextlib.ExitStack() as pc_:
                alloc_gl(pc_)
                mod_prepare(l, 1, 1.0, blocks=[4, 5])
                Gp, Gs = gl["Gp"], gl["Gs"]
                wout = kb.sb(pc_, "wo_sb", [128, 8, D], BF16)
                kb.dma("pool", wout[:], rw_wo[0].rearrange("(kc p) n -> p kc n", p=128), r=(), w=["wout"])
                v_ = 0
                for i in range(NT):
                    G = Gp if i < 16 else Gs
                    for half in range(2):
                        py = 4 + (v_ % 4)
                        for kc in range(8):
                            kb.mm(ps[py][:, :], lhsT=ygT[:, kc, i * 128:(i + 1) * 128], rhs=wout[:, kc, half * 512:(half + 1) * 512], start=(kc == 0), stop=(kc == 7),
                                  r=["ygT%d" % i, "wout"], w=[PS[py]])
                        kb.tt("dve", tA[v_ % 4][:], ps[py][:, :], G[:, half * 512:(half + 1) * 512], ALU.mult, r=[PS[py], "Gp", "Gs"], w=["tA%d" % (v_ % 4)])
                        kb.tt("pool", X[:, i, half * 512:(half + 1) * 512], X[:, i, half * 512:(half + 1) * 512], tA[v_ % 4][:], ALU.add,
                              r=["X%d" % i, "tA%d" % (v_ % 4)], w=["X%d" % i])
                        v_ += 1
                    layer_norm(i)
                s.barrier()
        s.barrier()

    def dump_and_finish():
        for i in range(NT):
            kb.dma("sp", yout[i * 128:(i + 1) * 128, :], X[:, i, :], r=["X%d" % i], w=())

    stage = 0
    for l in range(2):
        for sub in range(3):
            if sub == 0:
                ffn(l, 0, 0, 0.5)
            elif sub == 2:
                ffn(l, 1, 2, 0.5)
            elif l == 0:
                ab_mixer(l)
            else:
                rwkv_mixer(l)
            stage += 1
            if stage >= upto:
                return dump_and_finish()
    dump_and_finish()


def build(upto=99, stop_point=None):
    kb = KB()
    kb.stop_point = stop_point
    with contextlib.ExitStack() as es:
        kb.es = es
        build_program(kb, upto)
        kb.s.emit(kb.nc, es)
    return kb


def core_inputs(inp, core, kb):
    sl = slice(16 * core, 16 * core + 16)
    xs = inp["x_sample"][sl].reshape(128, D)
    f32 = np.float32

    def fm(v):
        return np.ascontiguousarray(np.asarray(v, f32).reshape(-1, 128).T)

    m = {
        "x": np.concatenate([inp["x_prompt"][core], xs], axis=0),
        "c": np.concatenate([inp["c_prompt"][core:core + 1], inp["c_sample"][sl]], axis=0),
        "cst": CST_ARR,
    }
    cw = inp["rg_conv_w"][0]
    vecA = np.zeros((128, 32), f32)
    for c in range(4):
        for j in range(4):
            vecA[:, c * 4 + j] = cw[j, c * 128:(c + 1) * 128]
    vecA[:, 16:20] = fm(inp["rg_conv_b"][0])
    vecA[:, 20:24] = fm(inp["rg_b_a"][0])
    vecA[:, 24:28] = fm(inp["rg_b_x"][0])
    vecA[:, 28:32] = fm(inp["rg_lambda"][0])
    m["vecA"] = vecA
    m["bgT"] = np.ascontiguousarray(inp["mlstm_b_gates"][0].T)
    m["minitT"] = np.ascontiguousarray(inp["state_mlstm_m"][0, sl].T)
    m["smC"] = inp["state_mlstm_C"][0, sl]
    m["smn"] = inp["state_mlstm_n"][0, sl]
    m["srh"] = inp["state_rglru_h"][0, sl]
    m["srconv"] = inp["state_rglru_conv"][0, sl].reshape(48, 512)
    mu = inp["rw_mu"][0]
    muT = np.zeros((128, 48), f32)
    for j in range(6):
        muT[:, j * 8:(j + 1) * 8] = fm(mu[j])
    m["muT"] = muT
    m["rk_flat"] = inp["rw_r_k"].reshape(1, D)
    m["swkv"] = inp["state_rwkv_wkv"][0, sl]
    m["sshift"] = inp["state_rwkv_shift"][0, sl]
    for k in kb.dram:
        if k not in m and k in inp:
            m[k] = inp[k]
    return {k: np.ascontiguousarray(v, dtype=f32) for k, v in m.items() if k in kb.dram}


_CACHE = {}


def kernel(**inputs):
    inp = {k: np.asarray(v) for k, v in inputs.items()}
    if "kb" not in _CACHE:
        _CACHE["kb"] = build()
    kb = _CACHE["kb"]
    in_maps = [core_inputs(inp, c, kb) for c in range(NCORES)]
    res = run_bass_kernel_spmd(kb.nc, in_maps, core_ids=list(range(NCORES)))
    R = res.results
    f32 = np.float32
    cat = lambda key, f=(lambda a: a): np.stack([f(np.asarray(R[c][key], f32)) for c in range(NCORES)], axis=0)
    cats = lambda key, f=(lambda a: a): np.concatenate([f(np.asarray(R[c][key], f32)) for c in range(NCORES)], axis=0)
    y_prompt = cat("y", lambda a: a[:2048])
    y_sample = cats("y", lambda a: a[2048:].reshape(16, 8, D))
    outs = (
        y_prompt, y_sample,
        cat("o_pmC")[None], cat("o_pmn")[None], cat("o_pmm", lambda a: a[:, 0])[None],
        cat("o_prh", lambda a: a.reshape(512))[None], cat("o_prconv")[None],
        cat("o_pwkv")[None], cat("o_pshift", lambda a: a[0])[None],
        cats("o_smC")[None], cats("o_smn")[None], cats("o_smm", lambda a: a.T)[None],
        cats("o_srh")[None], cats("o_srconv", lambda a: a.reshape(16, 3, 512))[None],
        cats("o_swkv")[None], cats("o_sshift")[None],
    )
    return tuple(np.ascontiguousarray(o, dtype=f32) for o in outs)
```

```python
import contextlib
import numpy as np
import concourse.bass as bass
import concourse.mybir as mybir
from concourse.bass_utils import run_bass_kernel_spmd

F32 = mybir.dt.float32
BF16 = mybir.dt.bfloat16
F32R = mybir.dt.float32r
AF = mybir.ActivationFunctionType
ALU = mybir.AluOpType
AX = mybir.AxisListType

D = 1024
DFF = 2816
NT = 17
NTOK = NT * 128
ALPHA = 4.0 ** 0.25
LN_EPS = 1e-5
NCORES = 8


class Op:
    __slots__ = ("eng", "fn", "deps", "sig", "idx", "dma", "slot", "slot_total", "sigcount", "waits", "tag")


class Sched:
    ENGS = ["pe", "act", "dve", "pool", "sp"]

    def __init__(self, n_slots=40):
        self.q = {e: [] for e in self.ENGS}
        self.last_w = {}
        self.readers = {}
        self.n_slots = n_slots
        self.slot_rr = 0
        self.slot_total = [0] * n_slots
        self.slot_last = [None] * n_slots
        self.all_dma = []

    skip = False

    def add(self, eng, fn, r=(), w=(), sig=True, dma=False, tag=""):
        if self.skip:
            return None
        op = Op()
        op.eng, op.fn, op.sig, op.dma, op.tag = eng, fn, sig, dma, tag
        op.slot = None
        deps = []
        seen = set()

        def dep(o):
            if o is None or id(o) in seen:
                return
            seen.add(id(o))
            if (not dma) and eng == "pe" and o.eng == "pe" and not o.dma:
                return
            deps.append(o)

        for k in r:
            dep(self.last_w.get(k))
        for k in w:
            dep(self.last_w.get(k))
            for o in self.readers.get(k, {}).values():
                dep(o)
        if dma:
            slot = self.slot_rr % self.n_slots
            self.slot_rr += 1
            dep(self.slot_last[slot])
            self.slot_total[slot] += 16
            op.slot = slot
            op.slot_total = self.slot_total[slot]
            self.slot_last[slot] = op
            self.all_dma.append(op)
        op.deps = deps
        self.q[eng].append(op)
        op.idx = len(self.q[eng]) - 1
        key = ("dma", id(op)) if dma else eng
        for k in r:
            self.readers.setdefault(k, {})[key] = op
        for k in w:
            self.last_w[k] = op
            self.readers[k] = {}
        return op

    def barrier(self):
        if self.skip:
            return
        lasts = []
        for e in self.ENGS:
            comp = [o for o in self.q[e] if (not o.dma) and o.fn is not None]
            if comp:
                comp[-1].sig = True
                lasts.append(comp[-1])
        lasts += [o for o in self.slot_last if o is not None]
        for e in self.ENGS:
            op = Op()
            op.eng, op.sig, op.dma, op.tag, op.slot, op.fn = e, False, False, "barrier", None, None
            op.deps = list(lasts)
            self.q[e].append(op)
            op.idx = len(self.q[e]) - 1
        self.last_w = {}
        self.readers = {}

    def finalize(self):
        for e in self.ENGS:
            for o in reversed(self.q[e]):
                if not o.dma and o.fn is not None and e != "sp":
                    o.sig = True
                    break
        self.sigtot = {}
        for e in self.ENGS:
            cnt = 0
            ops = self.q[e]
            pref = []
            for o in ops:
                if (not o.dma) and o.sig:
                    cnt += 1
                pref.append(cnt)
            self.sigtot[e] = cnt
            nxt = None
            for i in range(len(ops) - 1, -1, -1):
                o = ops[i]
                if (not o.dma) and o.sig:
                    nxt = pref[i]
                o.sigcount = nxt if not o.dma else None
        for e in self.ENGS:
            known = {}
            for o in self.q[e]:
                need = {}
                for d in o.deps:
                    if d.dma:
                        key, val = ("slot", d.slot), d.slot_total
                    else:
                        if d.sigcount is None:
                            raise RuntimeError("dependency on op with no later signal: %s" % d.tag)
                        key, val = ("eng", d.eng), d.sigcount
                    if val > need.get(key, 0):
                        need[key] = val
                o.waits = []
                for key, val in need.items():
                    if known.get(key, 0) < val:
                        known[key] = val
                        o.waits.append((key, val))

    def simulate(self):
        pc = {e: 0 for e in self.ENGS}
        sem = {}
        sigc = {e: 0 for e in self.ENGS}
        progress = True
        while progress:
            progress = False
            for e in self.ENGS:
                while pc[e] < len(self.q[e]):
                    o = self.q[e][pc[e]]
                    ok = all(sem.get(k, 0) >= v for k, v in o.waits)
                    if not ok:
                        break
                    if o.dma:
                        sem[("slot", o.slot)] = sem.get(("slot", o.slot), 0) + 16
                    elif o.sig:
                        sem[("eng", e)] = sem.get(("eng", e), 0) + 1
                    pc[e] += 1
                    progress = True
        stuck = {e: (pc[e], len(self.q[e])) for e in self.ENGS if pc[e] < len(self.q[e])}
        if stuck:
            msg = []
            for e, (p, n) in stuck.items():
                o = self.q[e][p]
                msg.append("%s stuck at %d/%d tag=%s waits=%s" % (e, p, n, o.tag, [(k, v, sem.get(k, 0)) for k, v in o.waits]))
            raise RuntimeError("DEADLOCK in wait graph:\n" + "\n".join(msg))

    def emit(self, nc, es):
        self.finalize()
        self.simulate()
        engsem = {e: es.enter_context(nc.semaphore("sem_" + e)) for e in ["pe", "act", "dve", "pool"]}
        slotsem = [es.enter_context(nc.semaphore("slot%d" % i)) for i in range(self.n_slots)]

        def semof(key):
            return engsem[key[1]] if key[0] == "eng" else slotsem[key[1]]

        def run(e, g):
            for o in self.q[e]:
                for key, val in o.waits:
                    g.wait_ge(semof(key), val)
                if o.fn is None:
                    continue
                ins = o.fn(g)
                if o.dma:
                    ins.then_inc(slotsem[o.slot], 16)
                elif o.sig:
                    ins.then_inc(engsem[e], 1)
            if e == "sp":
                for s in range(self.n_slots):
                    if self.slot_total[s] > 0:
                        g.wait_ge(slotsem[s], self.slot_total[s])

        with nc.Block() as blk:
            blk.tensor(lambda g: run("pe", g))
            blk.scalar(lambda g: run("act", g))
            blk.vector(lambda g: run("dve", g))
            blk.gpsimd(lambda g: run("pool", g))
            blk.sync(lambda g: run("sp", g))


class KB:
    def __init__(self, stop_after=None, debug=False):
        self.nc = bass.Bass("TRN2", target_bir_lowering=False)
        self.s = Sched()
        self.stop_after = stop_after
        self.debug = debug
        self.dram = {}

    def din(self, name, shape, dt=F32):
        t = self.nc.dram_tensor(name, list(shape), dt, kind="ExternalInput")
        self.dram[name] = t
        return t.ap()

    def dout(self, name, shape, dt=F32):
        t = self.nc.dram_tensor(name, list(shape), dt, kind="ExternalOutput")
        self.dram[name] = t
        return t.ap()

    def sb(self, es, name, shape, dt=F32):
        self.uid = getattr(self, "uid", 0) + 1
        return es.enter_context(self.nc.sbuf_tensor("%s_%d" % (name, self.uid), list(shape), dt))

    def mm(self, out, lhsT, rhs, start, stop, r, w, sig=None, tag="mm"):
        if sig is None:
            sig = stop
        return self.s.add("pe", lambda g: g.matmul(out, lhsT=lhsT, rhs=rhs, start=start, stop=stop), r=r, w=w, sig=sig, tag=tag)

    def tr(self, out, in_, ident, r, w, sig=True, tag="tr"):
        return self.s.add("pe", lambda g: g.transpose(out, in_, ident), r=r, w=w, sig=sig, tag=tag)

    def act(self, out, in_, func, r, w, bias=None, scale=None, eng="act", tag="act"):
        kw = {}
        if bias is not None:
            kw["bias"] = bias
        if scale is not None:
            kw["scale"] = scale
        return self.s.add("act", lambda g: g.activation(out=out, in_=in_, func=func, **kw), r=r, w=w, tag=tag)

    def tt(self, eng, out, in0, in1, op, r, w, tag="tt"):
        return self.s.add(eng, lambda g: g.tensor_tensor(out=out, in0=in0, in1=in1, op=op), r=r, w=w, tag=tag)

    def ts(self, eng, out, in0, s1, s2, op0, op1, r, w, tag="ts"):
        if op1 is None:
            return self.s.add(eng, lambda g: g.tensor_scalar(out=out, in0=in0, scalar1=s1, scalar2=None, op0=op0), r=r, w=w, tag=tag)
        return self.s.add(eng, lambda g: g.tensor_scalar(out=out, in0=in0, scalar1=s1, scalar2=s2, op0=op0, op1=op1), r=r, w=w, tag=tag)

    def stt(self, out, in0, scalar, in1, op0, op1, r, w, tag="stt"):
        return self.s.add("dve", lambda g: g.scalar_tensor_tensor(out=out, in0=in0, scalar=scalar, in1=in1, op0=op0, op1=op1), r=r, w=w, tag=tag)

    def cp(self, eng, out, in_, r, w, tag="cp"):
        if eng == "act":
            return self.s.add("act", lambda g: g.copy(out=out, in_=in_), r=r, w=w, tag=tag)
        return self.s.add(eng, lambda g: g.tensor_copy(out=out, in_=in_), r=r, w=w, tag=tag)

    def memset(self, eng, ap, val, w, tag="memset"):
        return self.s.add(eng, lambda g: g.memset(ap, val), r=(), w=w, tag=tag)

    def dma(self, q, out, in_, r, w, tag="dma", **kw):
        return self.s.add(q, lambda g: g.dma_start(out=out, in_=in_, **kw), r=r, w=w, dma=True, tag=tag)


def make_consts():
    c = {}
    c["ident"] = np.eye(128, dtype=np.float32)
    selP = np.zeros((128, 128), np.float32)
    selP[0, :] = 1.0
    selS = np.zeros((128, 128), np.float32)
    for p in range(128):
        selS[1 + p // 8, p] = 1.0
    c["selP"] = selP
    c["selS"] = selS
    st = np.arange(128)
    c["maskP"] = (st[:, None] <= st[None, :]).astype(np.float32)
    c["maskS"] = ((st[:, None] <= st[None, :]) & (st[:, None] // 8 == st[None, :] // 8)).astype(np.float32)
    c["rst"] = np.tile((st % 8 != 0).astype(np.float32)[None, :], (128, 1))
    c["rstm"] = np.tile(np.where(st % 8 == 0, -1e30, 0.0).astype(np.float32)[None, :], (128, 1))
    bms = np.zeros((128, 128), np.float32)
    bms[st, st // 8] = 1.0
    c["bms"] = bms
    c["ones"] = np.ones((128, 128), np.float32)
    same = (st[:, None] // 8 == st[None, :] // 8)
    c["upP"] = (st[:, None] < st[None, :]).astype(np.float32)
    c["lowP"] = (st[:, None] > st[None, :]).astype(np.float32)
    c["upS"] = ((st[:, None] < st[None, :]) & same).astype(np.float32)
    c["lowS"] = ((st[:, None] > st[None, :]) & same).astype(np.float32)
    c["blkS"] = same.astype(np.float32)
    names = list(c.keys())
    arr = np.concatenate([c[k] for k in names], axis=1)
    offs = {}
    o = 0
    for k in names:
        offs[k] = o
        o += c[k].shape[1]
    return arr, offs


CST_ARR, CST_OFF = make_consts()
NCST = CST_ARR.shape[1]

FFN_PARTS = [(0, 4), (4, 4), (8, 4), (12, 4), (16, 4), (20, 2)]
TGS = [(0, 512), (512, 512), (1024, 512), (1536, 512), (2048, 128)]


def build_program(kb, upto=99):
    nc, s = kb.nc, kb.s
    es = kb.es
    xin = kb.din("x", [NTOK, D])
    cin = kb.din("c", [NT, D])
    cst = kb.din("cst", [128, NCST])
    ada_w = kb.din("ada_w", [2, D, 9 * D])
    ada_b = kb.din("ada_b", [2, 9 * D])
    ln_g = kb.din("ln_g", [2, 3, D])
    ln_b = kb.din("ln_b", [2, 3, D])
    ffn_w1 = kb.din("ffn_w1", [2, 2, D, DFF])
    ffn_w3 = kb.din("ffn_w3", [2, 2, D, DFF])
    ffn_w2 = kb.din("ffn_w2", [2, 2, DFF, D])
    yout = kb.dout("y", [NTOK, D])

    X = kb.sb(es, "X", [128, NT, D], F32)
    C = kb.sb(es, "cst_sb", [128, NCST], F32)
    cT = kb.sb(es, "cT", [128, 8, NT], BF16)
    onesb = kb.sb(es, "onesb", [1, 32], F32)
    modT = kb.sb(es, "modT", [128, 16, NT], F32)
    gl = {}

    def alloc_gl(stack):
        gl["Gp"] = kb.sb(stack, "Gp", [128, D], F32)
        gl["Gs"] = kb.sb(stack, "Gs", [128, D], F32)
        gl["LNg"] = kb.sb(stack, "LNg", [128, D], F32)
        gl["LNb"] = kb.sb(stack, "LNb", [128, D], F32)
    tA = [kb.sb(es, "tA%d" % i, [128, 512], F32) for i in range(4)]
    tB = [kb.sb(es, "tB%d" % i, [128, D], F32) for i in range(2)]
    stt_ = [kb.sb(es, "bnst%d" % i, [128, 2, 6], F32) for i in range(2)]
    mv = [kb.sb(es, "mv%d" % i, [128, 2], F32) for i in range(2)]
    rstd = [kb.sb(es, "rstd%d" % i, [128, 1], F32) for i in range(2)]
    nmr = [kb.sb(es, "nmr%d" % i, [128, 1], F32) for i in range(2)]
    tmpS = kb.sb(es, "tmpS", [128, 128], F32)
    ps = [es.enter_context(nc.psum_tensor("ps%d" % i, [128, 512], F32)) for i in range(8)]
    PS = ["ps%d" % i for i in range(8)]

    ident = C[:, CST_OFF["ident"]:CST_OFF["ident"] + 128]
    selP = C[0:NT, CST_OFF["selP"]:CST_OFF["selP"] + 128]
    selS = C[0:NT, CST_OFF["selS"]:CST_OFF["selS"] + 128]

    kb.dma("sp", C[:], cst, r=(), w=["C"])
    for i in range(NT):
        kb.dma("sp", X[:, i, :], xin[i * 128:(i + 1) * 128, :], r=(), w=["X%d" % i])
    kb.memset("pool", onesb[:], 1.0, w=["onesb"])

    with contextlib.ExitStack() as ph0:
        c_sb = kb.sb(ph0, "c_sb", [NT, D], F32)
        cs_sb = kb.sb(ph0, "cs_sb", [NT, D], F32)
        kb.dma("sp", c_sb[:], cin, r=(), w=["c_sb"])
        kb.act(cs_sb[:], c_sb[:], AF.Silu, r=["c_sb"], w=["cs_sb"])
        for kc in range(8):
            kb.tr(ps[0][:, kc * NT:(kc + 1) * NT], cs_sb[0:NT, kc * 128:(kc + 1) * 128], C[0:NT, 0:NT],
                  r=["cs_sb", "C"], w=[PS[0]], sig=(kc == 7))
        kb.cp("dve", cT[:].rearrange("p a b -> p (a b)"), ps[0][:, 0:8 * NT], r=[PS[0]], w=["cT"])
    s.barrier()

    state = {"ada_i": 0, "ada_i2": 0, "psr": 0}

    def mod_prepare(l, sub, res_w, blocks=range(6)):
        if 5 in blocks:
            Gp, Gs = gl["Gp"], gl["Gs"]
            kb.dma("sp", gl["LNg"][:], ln_g[l, sub:sub + 1, :].to_broadcast([128, D]), r=(), w=["LNg"])
            kb.dma("sp", gl["LNb"][:], ln_b[l, sub:sub + 1, :].to_broadcast([128, D]), r=(), w=["LNb"])
        phm = contextlib.ExitStack()
        modst = [kb.sb(phm, "modst%d" % i, [NT, 512], F32) for i in range(2)]
        adaw = [kb.sb(phm, "adaw%d" % i, [128, 8, 256], BF16) for i in range(2)]
        adab = [kb.sb(phm, "adab%d" % i, [1, 512], F32) for i in range(2)]
        for b in blocks:
            i = state["ada_i"]
            state["ada_i"] += 1
            buf = i % 2
            co = sub * 3 * D + b * 512
            kb.dma("sp", adab[buf][:], ada_b[l:l + 1, co:co + 512], r=(), w=["adab%d" % buf])
            pm = 4 + (i % 2)
            for sbk in range(2):
                i2 = state["ada_i2"]
                state["ada_i2"] += 1
                wb = i2 % 2
                kb.dma("pool", adaw[wb][:], ada_w[l].rearrange("(kc p) n -> p kc n", p=128)[:, :, co + sbk * 256:co + (sbk + 1) * 256],
                       r=(), w=["adaw%d" % wb])
                for kc in range(8):
                    kb.mm(ps[pm][0:NT, sbk * 256:(sbk + 1) * 256], lhsT=cT[:, kc, :], rhs=adaw[wb][:, kc, :], start=(kc == 0), stop=False,
                          r=["cT", "adaw%d" % wb], w=[PS[pm]], sig=False)
                kb.mm(ps[pm][0:NT, sbk * 256:(sbk + 1) * 256], lhsT=onesb[0:1, 0:NT], rhs=adab[buf][:, sbk * 256:(sbk + 1) * 256], start=False, stop=True,
                      r=["onesb", "adab%d" % buf], w=[PS[pm]], sig=True)
            kb.cp("act", modst[buf][:], ps[pm][0:NT, :], r=[PS[pm]], w=["modst%d" % buf])
            if b < 4:
                for cc in range(4):
                    j = b * 4 + cc
                    kb.tr(ps[6][:, j * NT:(j + 1) * NT], modst[buf][0:NT, cc * 128:(cc + 1) * 128], C[0:NT, 0:NT],
                          r=["modst%d" % buf, "C"], w=[PS[6]], sig=(cc == 3))
                if b == 1:
                    kb.cp("dve", modT[:, 0:8, :].rearrange("p a b -> p (a b)"), ps[6][:, 0:8 * NT], r=[PS[6]], w=["modT"])
                if b == 3:
                    kb.ts("dve", modT[:, 8:16, :].rearrange("p a b -> p (a b)"), ps[6][:, 8 * NT:16 * NT], 1.0, None,
                          ALU.add, None, r=[PS[6]], w=["modT"])
            else:
                h = b - 4
                kb.mm(ps[7][:, :], lhsT=selP, rhs=modst[buf][:], start=True, stop=True, r=["C", "modst%d" % buf], w=[PS[7]])
                kb.act(Gp[:, h * 512:(h + 1) * 512], ps[7][:, :], AF.Identity, r=[PS[7]], w=["Gp"], bias=float(res_w), scale=float(res_w))
                kb.mm(ps[7][:, :], lhsT=selS, rhs=modst[buf][:], start=True, stop=True, r=["C", "modst%d" % buf], w=[PS[7]])
                kb.act(Gs[:, h * 512:(h + 1) * 512], ps[7][:, :], AF.Identity, r=[PS[7]], w=["Gs"], bias=float(res_w), scale=float(res_w))
        s.barrier()
        phm.close()

    def make_hT(hT, i, prescale=True, col0=None, res=None, hook=None):
        g = res if res is not None else "hT_g%d" % (i // 4)
        if col0 is None:
            col0 = i * 128
        for half in range(2):
            pb = state["psr"] % 4
            state["psr"] += 1
            for cc in range(4):
                c = half * 4 + cc
                kb.tr(ps[pb][:, cc * 128:(cc + 1) * 128], X[:, i, c * 128:(c + 1) * 128], ident,
                      r=["X%d" % i, "C"], w=[PS[pb]], sig=(cc == 3))
            for cc in range(4):
                c = half * 4 + cc
                src = ps[pb][:, cc * 128:(cc + 1) * 128]
                if hook is not None:
                    hook(i, c, src, PS[pb])
                if i < 16:
                    if cc % 2 == 0:
                        kb.act(hT[:, c, col0:col0 + 128], src, AF.Identity, r=[PS[pb], "modT"], w=[g],
                               bias=modT[:, c, 0:1], scale=modT[:, 8 + c, 0:1])
                    else:
                        kb.ts("dve", hT[:, c, col0:col0 + 128], src, modT[:, 8 + c, 0:1], modT[:, c, 0:1],
                              ALU.mult, ALU.add, r=[PS[pb], "modT"], w=[g])
                else:
                    sc = modT[:, 8 + c, 1:NT].unsqueeze(2).to_broadcast([128, 16, 8])
                    sh = modT[:, c, 1:NT].unsqueeze(2).to_broadcast([128, 16, 8])
                    kb.tt("dve", tmpS[:].rearrange("p (q t) -> p q t", t=8), src.rearrange("p (q t) -> p q t", t=8), sc,
                          ALU.mult, r=[PS[pb], "modT"], w=["tmpS"])
                    kb.tt("dve", hT[:, c, col0:col0 + 128].rearrange("p (q t) -> p q t", t=8),
                          tmpS[:].rearrange("p (q t) -> p q t", t=8), sh, ALU.add, r=["tmpS", "modT"], w=[g])

    def layer_norm(i):
        k = i % 2
        for h in range(2):
            s.add("dve", lambda g_, h=h, k=k, i=i: g_.bn_stats(out=stt_[k][:, h, :], in_=X[:, i, h * 512:(h + 1) * 512]),
                  r=["X%d" % i], w=["bnst%d" % k], tag="bnstats")
        s.add("dve", lambda g_, k=k: g_.bn_aggr(out=mv[k][:], in_=stt_[k][:].rearrange("p a b -> p (a b)")),
              r=["bnst%d" % k], w=["mv%d" % k], tag="bnaggr")
        kb.act(rstd[k][:], mv[k][:, 1:2], AF.Sqrt, r=["mv%d" % k], w=["rstd%d" % k], bias=float(LN_EPS), scale=1.0)
        s.add("dve", lambda g_, k=k: g_.reciprocal(out=rstd[k][:], in_=rstd[k][:]), r=["rstd%d" % k], w=["rstd%d" % k], tag="recip")
        kb.ts("dve", nmr[k][:], mv[k][:, 0:1], rstd[k][:, 0:1], -1.0, ALU.mult, ALU.mult, r=["mv%d" % k, "rstd%d" % k], w=["nmr%d" % k])
        kb.act(tB[k][:], X[:, i, :], AF.Identity, r=["X%d" % i, "rstd%d" % k, "nmr%d" % k], w=["tB%d" % k],
               bias=nmr[k][:, 0:1], scale=rstd[k][:, 0:1])
        kb.tt("dve", tB[k][:], tB[k][:], gl["LNg"][:], ALU.mult, r=["tB%d" % k, "LNg"], w=["tB%d" % k])
        kb.tt("dve", X[:, i, :], tB[k][:], gl["LNb"][:], ALU.add, r=["tB%d" % k, "LNb"], w=["X%d" % i])

    pacc = {"v": 0}

    def proj_acc(wt, wres, nk, lhs_of, do_ln, first):
        Gp, Gs = gl["Gp"], gl["Gs"]
        for i in [16] + list(range(16)):
            if i == 0:
                kb.tt("dve", wt[:, 0:nk, :], wt[:, 0:nk, :], Gp[:].unsqueeze(1).to_broadcast([128, nk, D]), ALU.mult, r=[wres, "Gp"], w=[wres])
            for half in range(2):
                v = pacc["v"]
                pacc["v"] += 1
                py = 4 + (v % 4)
                for kc in range(nk):
                    lt, lres = lhs_of(i, kc)
                    kb.mm(ps[py][:, :], lhsT=lt, rhs=wt[:, kc, half * 512:(half + 1) * 512], start=(kc == 0), stop=(kc == nk - 1),
                          r=[lres, wres], w=[PS[py]])
                xs = X[:, i, half * 512:(half + 1) * 512]
                src = ps[py][:, :]
                rsrc = PS[py]
                if i == 16:
                    kb.tt("dve", tA[v % 4][:], ps[py][:, :], Gs[:, half * 512:(half + 1) * 512], ALU.mult, r=[PS[py], "Gs"], w=["tA%d" % (v % 4)])
                    src, rsrc = tA[v % 4][:], "tA%d" % (v % 4)
                if first:
                    kb.stt(xs, xs, float(ALPHA), src, ALU.mult, ALU.add, r=["X%d" % i, rsrc], w=["X%d" % i])
                else:
                    kb.tt("dve", xs, xs, src, ALU.add, r=["X%d" % i, rsrc], w=["X%d" % i])
            if do_ln:
                layer_norm(i)

    def ffn(l, f, sub, res_w):
        with contextlib.ExitStack() as ph:
            alloc_gl(ph)
            mod_prepare(l, sub, res_w)
            Gp, Gs = gl["Gp"], gl["Gs"]
            hT = kb.sb(ph, "hT", [128, 8, NTOK], BF16)
            w1p = kb.sb(ph, "w1p", [128, 8, 512], BF16)
            w3p = kb.sb(ph, "w3p", [128, 8, 512], BF16)
            w2p = kb.sb(ph, "w2p", [128, 4, D], BF16)
            gbuf = kb.sb(ph, "gbuf", [128, 4, NTOK], BF16)
            sil = [kb.sb(ph, "sil%d" % i, [128, 512], F32) for i in range(2)]
            w1v = ffn_w1[l, f].rearrange("(kc p) n -> p kc n", p=128)
            w3v = ffn_w3[l, f].rearrange("(kc p) n -> p kc n", p=128)
            w2v = ffn_w2[l, f].rearrange("(j p) n -> p j n", p=128)
            u = 0
            v = 0
            def load_up(pi_):
                j0_, n_ = FFN_PARTS[pi_]
                kb.dma("pool", w1p[:, :, 0:n_ * 128], w1v[:, :, j0_ * 128:(j0_ + n_) * 128], r=(), w=["w1p"])
                kb.dma("pool", w3p[:, :, 0:n_ * 128], w3v[:, :, j0_ * 128:(j0_ + n_) * 128], r=(), w=["w3p"])

            def load_dn(pi_):
                j0_, n_ = FFN_PARTS[pi_]
                kb.dma("pool", w2p[:, 0:n_, :], w2v[:, j0_:j0_ + n_, :], r=(), w=["w2p"])

            load_up(0)
            load_dn(0)
            for i in range(NT):
                make_hT(hT, i)
            for pi, (j0, ncn) in enumerate(FFN_PARTS):
                for tg, (t0, nt_) in enumerate(TGS):
                    for jj in range(ncn):
                        pa, pb = (2 * u) % 4, (2 * u + 1) % 4
                        for kc in range(8):
                            kb.mm(ps[pa][:, 0:nt_], lhsT=w1p[:, kc, jj * 128:(jj + 1) * 128], rhs=hT[:, kc, t0:t0 + nt_],
                                  start=(kc == 0), stop=(kc == 7), r=["w1p", "hT_g%d" % tg], w=[PS[pa]])
                        for kc in range(8):
                            kb.mm(ps[pb][:, 0:nt_], lhsT=w3p[:, kc, jj * 128:(jj + 1) * 128], rhs=hT[:, kc, t0:t0 + nt_],
                                  start=(kc == 0), stop=(kc == 7), r=["w3p", "hT_g%d" % tg], w=[PS[pb]])
                        kb.act(sil[u % 2][:, 0:nt_], ps[pa][:, 0:nt_], AF.Silu, r=[PS[pa]], w=["sil%d" % (u % 2)])
                        kb.tt("dve", gbuf[:, jj, t0:t0 + nt_], sil[u % 2][:, 0:nt_], ps[pb][:, 0:nt_], ALU.mult,
                              r=["sil%d" % (u % 2), PS[pb]], w=["g_g%d" % tg])
                        u += 1
                if pi + 1 < len(FFN_PARTS):
                    load_up(pi + 1)
                proj_acc(w2p, "w2p", ncn, lambda i, jj: (gbuf[:, jj, i * 128:(i + 1) * 128], "g_g%d" % (i // 4)), pi == len(FFN_PARTS) - 1, pi == 0)
                if pi + 1 < len(FFN_PARTS):
                    load_dn(pi + 1)
        s.barrier()

    DKS = float(128 ** -0.5)

    class _Stop(Exception):
        pass

    def stop_at(n):
        if getattr(kb, "stop_point", None) == n:
            s.barrier()
            s.skip = True

    def ab_mixer(l):
        ab_mixer_(l)
        s.skip = False
        s.barrier()

    def ab_mixer_(l):
        ab_w_in = kb.din("ab_w_in", [1, D, 3080])
        ab_w_out = kb.din("ab_w_out", [1, D, D])
        mnorm_g = kb.din("mlstm_norm_g", [1, 512])
        vecA_d = kb.din("vecA", [128, 32])
        bgT_d = kb.din("bgT", [4, 2])
        minitT_d = kb.din("minitT", [4, 16])
        rg_w_a = kb.din("rg_w_a", [1, 8, 64, 64])
        rg_w_x = kb.din("rg_w_x", [1, 8, 64, 64])
        smC = kb.din("smC", [16, 4, 128, 128])
        smn = kb.din("smn", [16, 4, 128])
        srh = kb.din("srh", [16, 512])
        srconv = kb.din("srconv", [48, 512])
        o_pmC = kb.dout("o_pmC", [4, 128, 128])
        o_pmn = kb.dout("o_pmn", [4, 128])
        o_pmm = kb.dout("o_pmm", [4, 1])
        o_prh = kb.dout("o_prh", [4, 128])
        o_prconv = kb.dout("o_prconv", [3, 512])
        o_smC = kb.dout("o_smC", [16, 4, 128, 128])
        o_smn = kb.dout("o_smn", [16, 4, 128])
        o_smm = kb.dout("o_smm", [4, 16])
        o_srh = kb.dout("o_srh", [16, 512])
        o_srconv = kb.dout("o_srconv", [48, 512])

        maskP = C[:, CST_OFF["maskP"]:CST_OFF["maskP"] + 128]
        maskS = C[:, CST_OFF["maskS"]:CST_OFF["maskS"] + 128]
        rst = C[:, CST_OFF["rst"]:CST_OFF["rst"] + 128]
        rstm = C[:, CST_OFF["rstm"]:CST_OFF["rstm"] + 128]
        bms = C[:, CST_OFF["bms"]:CST_OFF["bms"] + 16]
        ones = C[:, CST_OFF["ones"]:CST_OFF["ones"] + 128]

        mod_prepare(l, 1, 1.0, blocks=range(4))
        win_v = ab_w_in[0].rearrange("(kc p) n -> p kc n", p=128)
        with contextlib.ExitStack() as ph:
            hmT = kb.sb(ph, "hmT", [128, 4, NTOK], BF16)
            vecA = kb.sb(ph, "vecA_sb", [128, 32], F32)
            kb.dma("sp", vecA[:], vecA_d, r=(), w=["vecA"])
            sigo = tA[0]
            hmf = tB[0][:, 0:512]
            with contextlib.ExitStack() as pa:
                winA = kb.sb(pa, "winA", [128, 8, 2056], BF16)
                kb.dma("pool", winA[:, :, 0:1024], win_v[:, :, 0:1024], r=(), w=["winA"])
                kb.dma("pool", winA[:, :, 1024:2056], win_v[:, :, 1024:2056], r=(), w=["winA"])
                qkT = kb.sb(pa, "qkT", [128, 8, 512], BF16)
                hTg = [kb.sb(pa, "hTgA%d" % i, [128, 8, 512], BF16) for i in range(2)]
                bg = kb.sb(pa, "bg", [4, 2], F32)
                nbg1 = kb.sb(pa, "nbg1", [4, 1], F32)
                minitT = kb.sb(pa, "minitT_sb", [4, 16], F32)
                mng = kb.sb(pa, "mng", [128, 512], F32)
                kb.dma("sp", bg[:], bgT_d, r=(), w=["bg"])
                kb.dma("sp", minitT[:], minitT_d, r=(), w=["minitT"])
                kb.dma("sp", mng[:], mnorm_g[0:1, :].to_broadcast([128, 512]), r=(), w=["mng"])
                kb.ts("dve", nbg1[:], bg[:, 1:2], -1.0, None, ALU.mult, None, r=["bg"], w=["nbg1"])
                R4 = lambda nm: kb.sb(pa, nm, [4, 128], F32)
                t1, IGa, Rt, t3, t4 = R4("r_t1"), R4("r_ig"), R4("r_rt"), R4("r_t3"), R4("r_t4")
                Bc = [R4("r_bc0"), R4("r_bc1")]
                Mx = [R4("r_mx0"), R4("r_mx1")]
                dd = kb.sb(pa, "r_dd", [4, 16], F32)
                DDm = kb.sb(pa, "r_DD", [4, 64], F32)
                mout = kb.sb(pa, "r_mout", [4, 16], F32)
                colq = [kb.sb(pa, "colq%d" % i, [128, 16], F32) for i in range(2)]
                decsb = kb.sb(pa, "decsb", [128, 64], F32)
                kw = kb.sb(pa, "kw", [128, 4, 128], BF16)
                ktok = kb.sb(pa, "ktok", [128, 4, 128], BF16)
                vext = [kb.sb(pa, "vext%d" % i, [128, 4, 130], BF16) for i in range(2)]
                PT = kb.sb(pa, "PT", [128, 4, 128], BF16)
                Cst = kb.sb(pa, "Cst", [128, 4, 130], F32)
                Cb = kb.sb(pa, "Cb", [128, 4, 130], BF16)
                dmax = kb.sb(pa, "dmax", [128, 4], F32)
                hst6 = kb.sb(pa, "hst6", [128, 4, 6], F32)
                hmv = kb.sb(pa, "hmv", [128, 4, 2], F32)
                hrs = kb.sb(pa, "hrs", [128, 4], F32)
                hnm = kb.sb(pa, "hnm", [128, 4], F32)
                for vv in vext:
                    kb.memset("pool", vv[:], 1.0, w=["vext0", "vext1"])
                kb.memset("pool", Cst[:], 0.0, w=["Cst"])
                kb.memset("pool", Cb[:], 0.0, w=["Cb"])

                def rows(i):
                    k = i % 2
                    hb = (i // 4) % 2
                    hT = hTg[hb]
                    tc0 = (i % 4) * 128
                    pg = ps[7]
                    for kc in range(8):
                        kb.mm(pg[0:4, 0:128], lhsT=winA[:, kc, 2048:2052], rhs=hT[:, kc, tc0:tc0 + 128], start=(kc == 0), stop=(kc == 7),
                              r=["winA", "hTg%d" % hb], w=[PS[7]])
                    for kc in range(8):
                        kb.mm(pg[0:4, 128:256], lhsT=winA[:, kc, 2052:2056], rhs=hT[:, kc, tc0:tc0 + 128], start=(kc == 0), stop=(kc == 7),
                              r=["winA", "hTg%d" % hb], w=[PS[7]])
                    kb.act(IGa[:], pg[0:4, 0:128], AF.Identity, r=[PS[7], "bg"], w=["r_ig"], bias=bg[:, 0:1], scale=1.0)
                    kb.act(t1[:], pg[0:4, 128:256], AF.Exp, r=[PS[7], "nbg1"], w=["r_t1"], bias=nbg1[:, 0:1], scale=-1.0)
                    kb.act(t1[:], t1[:], AF.Ln, r=["r_t1"], w=["r_t1"], bias=1.0, scale=1.0)
                    kb.ts("dve", t1[:], t1[:], -1.0, None, ALU.mult, None, r=["r_t1"], w=["r_t1"])
                    prompt = i < 16
                    if prompt:
                        binit = 0.0 if i == 0 else Bc[1 - k][:, 127:128]
                        minit = 0.0 if i == 0 else Mx[1 - k][:, 127:128]
                        s.add("dve", lambda g_: g_.tensor_tensor_scan(out=Bc[k][:], data0=ones[0:4, :], data1=t1[:], initial=binit,
                                                                       op0=ALU.mult, op1=ALU.add),
                              r=["r_t1", "r_bc%d" % (1 - k), "C"], w=["r_bc%d" % k], tag="scanB")
                        kb.tt("dve", IGa[:], IGa[:], Bc[k][:], ALU.subtract, r=["r_ig", "r_bc%d" % k], w=["r_ig"])
                        kb.memset("dve", t3[:], 0.0, w=["r_t3"])
                        s.add("dve", lambda g_: g_.tensor_tensor_scan(out=Mx[k][:], data0=t3[:], data1=IGa[:], initial=minit,
                                                                       op0=ALU.add, op1=ALU.max),
                              r=["r_t3", "r_ig", "r_mx%d" % (1 - k)], w=["r_mx%d" % k], tag="scanM")
                        if i == 0:
                            kb.memset("dve", Rt[:], 0.0, w=["r_rt"])
                        else:
                            kb.cp("dve", Rt[:], Mx[1 - k][:, 127:128].to_broadcast([4, 128]), r=["r_mx%d" % (1 - k)], w=["r_rt"])
                    else:
                        s.add("dve", lambda g_: g_.tensor_tensor_scan(out=Bc[k][:], data0=rst[0:4, :], data1=t1[:], initial=0.0,
                                                                       op0=ALU.mult, op1=ALU.add),
                              r=["r_t1", "C"], w=["r_bc%d" % k], tag="scanB")
                        kb.tt("dve", IGa[:], IGa[:], Bc[k][:], ALU.subtract, r=["r_ig", "r_bc%d" % k], w=["r_ig"])
                        kb.cp("dve", t3[:], IGa[:], r=["r_ig"], w=["r_t3"])
                        kb.tt("dve", t3[:].rearrange("p (q t) -> p q t", t=8)[:, :, 0:1], IGa[:].rearrange("p (q t) -> p q t", t=8)[:, :, 0:1],
                              minitT[:].unsqueeze(2), ALU.max, r=["r_ig", "minitT"], w=["r_t3"])
                        s.add("dve", lambda g_: g_.tensor_tensor_scan(out=Mx[k][:], data0=rstm[0:4, :], data1=t3[:], initial=0.0,
                                                                       op0=ALU.add, op1=ALU.max),
                              r=["r_t3", "C"], w=["r_mx%d" % k], tag="scanM")
                        kb.cp("dve", Rt[:].rearrange("p (q t) -> p q t", t=8), minitT[:].unsqueeze(2).to_broadcast([4, 16, 8]),
                              r=["minitT"], w=["r_rt"])
                    kb.tt("dve", t3[:], IGa[:], Rt[:], ALU.subtract, r=["r_ig", "r_rt"], w=["r_t3"])
                    kb.act(t3[:], t3[:], AF.Exp, r=["r_t3"], w=["r_t3"])
                    kb.tt("dve", t4[:], Bc[k][:], Rt[:], ALU.add, r=["r_bc%d" % k, "r_rt"], w=["r_t4"])
                    kb.act(t4[:], t4[:], AF.Exp, r=["r_t4"], w=["r_t4"], scale=-1.0)
                    kb.tr(pg[:, 256:260], t3[0:4, :], C[0:4, 0:4], r=["r_t3", "C"], w=[PS[7]])
                    kb.tr(pg[:, 260:264], t4[0:4, :], C[0:4, 0:4], r=["r_t4", "C"], w=[PS[7]])
                    if prompt:
                        kb.tt("dve", dd[:, 0:1], Rt[:, 0:1], Mx[k][:, 127:128], ALU.subtract, r=["r_rt", "r_mx%d" % k], w=["r_dd"])
                        kb.act(dd[:, 0:1], dd[:, 0:1], AF.Exp, r=["r_dd"], w=["r_dd"])
                        kb.ts("dve", DDm[:, 0:4], C[0:4, 0:4], dd[:, 0:1], None, ALU.mult, None, r=["r_dd", "C"], w=["r_DD"])
                        kb.mm(pg[:, 264:268], lhsT=ones[0:4, :], rhs=DDm[:, 0:4], start=True, stop=True, r=["C", "r_DD"], w=[PS[7]])
                        kb.cp("dve", colq[k][:, 0:12], pg[:, 256:268], r=[PS[7]], w=["colq%d" % k])
                        if i == 15:
                            kb.tt("dve", mout[:, 0:1], Bc[k][:, 127:128], Mx[k][:, 127:128], ALU.add, r=["r_bc%d" % k, "r_mx%d" % k], w=["r_mout"])
                            kb.dma("sp", o_pmm, mout[:, 0:1], r=["r_mout"], w=())
                    else:
                        MT = Mx[k][:].rearrange("p (q t) -> p q t", t=8)[:, :, 7:8]
                        kb.tt("dve", t4[:].rearrange("p (q t) -> p q t", t=8), IGa[:].rearrange("p (q t) -> p q t", t=8),
                              MT.to_broadcast([4, 16, 8]), ALU.subtract, r=["r_ig", "r_mx%d" % k, PS[7]], w=["r_t4"])
                        kb.act(t4[:], t4[:], AF.Exp, r=["r_t4"], w=["r_t4"])
                        kb.tr(pg[:, 264:268], t4[0:4, :], C[0:4, 0:4], r=["r_t4", "C"], w=[PS[7]])
                        kb.cp("dve", colq[k][:, 0:12], pg[:, 256:268], r=[PS[7]], w=["colq%d" % k])
                        kb.tt("dve", dd[:].unsqueeze(2), minitT[:].unsqueeze(2), MT, ALU.subtract, r=["minitT", "r_mx%d" % k], w=["r_dd"])
                        kb.act(dd[:], dd[:], AF.Exp, r=["r_dd"], w=["r_dd"])
                        kb.tt("dve", DDm[:].rearrange("p (q h) -> p q h", h=4), dd[:].unsqueeze(2).to_broadcast([4, 16, 4]),
                              C[0:4, 0:4].unsqueeze(1).to_broadcast([4, 16, 4]), ALU.mult, r=["r_dd", "C"], w=["r_DD"])
                        kb.mm(pg[:, 272:336], lhsT=ones[0:4, :], rhs=DDm[:], start=True, stop=True, r=["C", "r_DD"], w=[PS[7]])
                        kb.cp("dve", decsb[:], pg[:, 272:336], r=[PS[7]], w=["decsb"])
                        kb.tt("dve", mout[:].unsqueeze(2), Bc[k][:].rearrange("p (q t) -> p q t", t=8)[:, :, 7:8], MT, ALU.add,
                              r=["r_bc%d" % k, "r_mx%d" % k], w=["r_mout"])
                        kb.dma("sp", o_smm, mout[:], r=["r_mout"], w=())

                def mlstm_tile(i):
                    k = i % 2
                    tg = i // 4
                    hT = hTg[tg % 2]
                    tc0 = (i % 4) * 128
                    lc0 = (i % 4) * 128 if i < 16 else 0
                    cq = colq[k]
                    vx = vext[k]
                    grp = "hTg%d" % (tg % 2)
                    for bi, c0 in enumerate((512, 1024, 1536)):
                        bank = 2 + (bi % 2)
                        for kc in range(8):
                            kb.mm(ps[bank][:, :], lhsT=hT[:, kc, tc0:tc0 + 128], rhs=winA[:, kc, c0:c0 + 512], start=(kc == 0), stop=(kc == 7),
                                  r=["winA", grp], w=[PS[bank]])
                        if bi == 0:
                            kb.act(ktok[:].rearrange("p a b -> p (a b)"), ps[bank][:, :], AF.Identity, r=[PS[bank]], w=["ktok"], scale=DKS)
                            for h in range(4):
                                kb.ts("dve", kw[:, h, :], ktok[:, h, :], cq[:, h:h + 1], None, ALU.mult, None,
                                      r=["ktok", "colq%d" % k], w=["kw"])
                        elif bi == 1:
                            kb.cp("act", vx[:, :, 0:128], ps[bank][:, :].rearrange("p (h d) -> p h d", d=128), r=[PS[bank]], w=["vext%d" % k])
                        else:
                            kb.act(sigo[:], ps[bank][:, :], AF.Sigmoid, r=[PS[bank]], w=["tA0"])
                    if i == 0:
                        stop_at(31)
                    for h in range(4):
                        kb.mm(ps[4][:, h * 128:(h + 1) * 128], lhsT=qkT[:, 4 + h, lc0:lc0 + 128], rhs=qkT[:, h, lc0:lc0 + 128],
                              start=True, stop=True, r=["qkT"], w=[PS[4]], sig=(h == 3))
                    if i == 0:
                        stop_at(32)
                    msk = maskP if i < 16 else maskS
                    for h in range(4):
                        kb.stt(PT[:, h, :], ps[4][:, h * 128:(h + 1) * 128], cq[:, h:h + 1], msk, ALU.mult, ALU.mult,
                               r=[PS[4], "colq%d" % k, "C"], w=["PT"])
                    return cq, vx, lc0

                def numden_finish(i, cq):
                    for half in range(2):
                        bank = ps[5 + half]
                        den = bank[:, 0:260].rearrange("p (h d) -> p h d", d=130)[:, :, 128:129]
                        kb.act(dmax[:, 2 * half:2 * half + 2].unsqueeze(2), den, AF.Abs, r=[PS[5 + half]], w=["dmax"])
                        kb.tt("dve", dmax[:, 2 * half:2 * half + 2], dmax[:, 2 * half:2 * half + 2], cq[:, 4 + 2 * half:6 + 2 * half], ALU.max,
                              r=["dmax", "colq%d" % (i % 2)], w=["dmax"])
                    s.add("dve", lambda g_: g_.reciprocal(out=dmax[:], in_=dmax[:]), r=["dmax"], w=["dmax"], tag="recip")
                    for h in range(4):
                        bank = ps[5 + h // 2]
                        o0 = (h % 2) * 130
                        kb.act(hmf[:, h * 128:(h + 1) * 128], bank[:, o0:o0 + 128], AF.Identity, r=[PS[5 + h // 2], "dmax"], w=["tB0"],
                               scale=dmax[:, h:h + 1])
                    for h in range(4):
                        s.add("dve", lambda g_, h=h: g_.bn_stats(out=hst6[:, h, :], in_=hmf[:, h * 128:(h + 1) * 128]), r=["tB0"], w=["hst6"], tag="bnst")
                    for h in range(4):
                        s.add("dve", lambda g_, h=h: g_.bn_aggr(out=hmv[:, h, :], in_=hst6[:, h, :]), r=["hst6"], w=["hmv"], tag="bnag")
                    kb.act(hrs[:].unsqueeze(2), hmv[:, :, 1:2], AF.Sqrt, r=["hmv"], w=["hrs"], bias=1e-6, scale=1.0)
                    s.add("dve", lambda g_: g_.reciprocal(out=hrs[:], in_=hrs[:]), r=["hrs"], w=["hrs"], tag="recip")
                    kb.tt("dve", hnm[:].unsqueeze(2), hmv[:, :, 0:1], hrs[:].unsqueeze(2), ALU.mult, r=["hmv", "hrs"], w=["hnm"])
                    kb.ts("dve", hnm[:], hnm[:], -1.0, None, ALU.mult, None, r=["hnm"], w=["hnm"])
                    for h in range(4):
                        kb.act(hmf[:, h * 128:(h + 1) * 128], hmf[:, h * 128:(h + 1) * 128], AF.Identity, r=["tB0", "hrs", "hnm"], w=["tB0"],
                               bias=hnm[:, h:h + 1], scale=hrs[:, h:h + 1])
                    kb.tt("pool", hmf[:], hmf[:], mng[:], ALU.mult, r=["tB0", "mng"], w=["tB0"])
                    kb.tt("pool", hmf[:], hmf[:], sigo[:], ALU.mult, r=["tB0", "tA0"], w=["tB0"])
                    for h in range(4):
                        kb.tr(ps[4][:, h * 128:(h + 1) * 128], hmf[:, h * 128:(h + 1) * 128], ident, r=["tB0", "C"], w=[PS[4]], sig=(h == 3))
                    kb.cp("act", hmT[:, :, i * 128:(i + 1) * 128], ps[4][:, :].rearrange("p (h d) -> p h d", d=128), r=[PS[4]], w=["hmT%d" % i])

                for tg, (t0, nt_) in enumerate(TGS):
                    tiles = range(4 * tg, 4 * tg + 4) if tg < 4 else [16]
                    hT = hTg[tg % 2]
                    for i in tiles:
                        make_hT(hT, i, prescale=False, col0=(i % 4) * 128, res="hTg%d" % (tg % 2))
                    for j in range(8):
                        bank = j % 2
                        for kc in range(8):
                            kb.mm(ps[bank][:, 0:nt_], lhsT=winA[:, kc, j * 128:(j + 1) * 128], rhs=hT[:, kc, 0:nt_], start=(kc == 0), stop=(kc == 7),
                                  r=["winA", "hTg%d" % (tg % 2)], w=[PS[bank]])
                        if j < 4:
                            kb.cp("act", qkT[:, j, 0:nt_], ps[bank][:, 0:nt_], r=[PS[bank]], w=["qkT"])
                        else:
                            kb.ts("dve", qkT[:, j, 0:nt_], ps[bank][:, 0:nt_], DKS, None, ALU.mult, None, r=[PS[bank]], w=["qkT"])
                    for i in tiles:
                        if i == 0:
                            stop_at(1)
                        rows(i)
                        if i == 0:
                            stop_at(2)
                        cq, vx, lc0 = mlstm_tile(i)
                        if i == 0:
                            stop_at(3)
                        if i == 16:
                            stop_at(5)
                        if i < 16:
                            for h in range(4):
                                bank = ps[5 + h // 2]
                                o0 = (h % 2) * 130
                                kb.mm(bank[:, o0:o0 + 130], lhsT=PT[:, h, :], rhs=vx[:, h, :], start=True, stop=False, r=["PT", "vext%d" % (i % 2)], w=[PS[5 + h // 2]], sig=False)
                                kb.mm(bank[:, o0:o0 + 130], lhsT=qkT[:, h, lc0:lc0 + 128], rhs=Cb[:, h, :], start=False, stop=True, r=["qkT", "Cb"], w=[PS[5 + h // 2]], sig=True)
                            numden_finish(i, cq)
                            for h in range(4):
                                bank = ps[5 + h // 2]
                                o0 = (h % 2) * 130
                                kb.mm(bank[:, o0:o0 + 130], lhsT=kw[:, h, :], rhs=vx[:, h, :], start=True, stop=True, r=["kw", "vext%d" % (i % 2)], w=[PS[5 + h // 2]])
                            for h in range(4):
                                bank = ps[5 + h // 2]
                                o0 = (h % 2) * 130
                                kb.ts("dve", Cst[:, h, :], Cst[:, h, :], cq[:, 8 + h:9 + h], None, ALU.mult, None, r=["Cst", "colq%d" % (i % 2)], w=["Cst"])
                                kb.stt(Cst[:, h, :], bank[:, o0:o0 + 130], cq[:, 8 + h:9 + h], Cst[:, h, :], ALU.mult, ALU.add,
                                       r=[PS[5 + h // 2], "Cst", "colq%d" % (i % 2)], w=["Cst"])
                            kb.cp("act", Cb[:], Cst[:], r=["Cst"], w=["Cb"])
                            if i == 0:
                                stop_at(4)
                            if i == 15:
                                for h in range(4):
                                    kb.tr(ps[4][:, h * 128:(h + 1) * 128], Cst[:, h, 0:128], ident, r=["Cst", "C"], w=[PS[4]], sig=(h == 3))
                                kb.cp("act", hmf[:], ps[4][:, :], r=[PS[4]], w=["tB0"])
                                kb.dma("sp", o_pmC.rearrange("h v k -> v h k"), hmf[:].rearrange("p (h k) -> p h k", k=128), r=["tB0"], w=())
                                kb.dma("sp", o_pmn.rearrange("h k -> k h"), Cst[:, :, 128], r=["Cst"], w=(), allow_slow_non_contiguous=True)
                        else:
                            with contextlib.ExitStack() as psm:
                                Cin = kb.sb(psm, "Cin", [128, 16, 128], F32)
                                CsT = kb.sb(psm, "CsT", [128, 16, 130], BF16)
                                qTm = kb.sb(psm, "qTm", [128, 16, 128], BF16)
                                VWm = kb.sb(psm, "VWm", [128, 16, 128], BF16)
                                nin = kb.sb(psm, "nin", [16, 4, 128], F32)
                                ninT = kb.sb(psm, "ninT", [128, 4, 16], F32)
                                BMW = kb.sb(psm, "BMW", [128, 16], BF16)
                                decc = kb.sb(psm, "decc", [16, 4], F32)
                                nout = kb.sb(psm, "nout", [16, 4, 128], F32)
                                kb.memset("pool", qTm[:], 0.0, w=["qTm"])
                                kb.dma("sp", nin[:], smn, r=(), w=["nin"])
                                kb.tr(ps[7][0:16, 400:404], dd[0:4, :], C[0:4, 0:4], r=["r_dd", "C"], w=[PS[7]])
                                kb.cp("dve", decc[:], ps[7][0:16, 400:404], r=[PS[7]], w=["decc"])
                                for h in range(4):
                                    kb.tr(ps[7][:, 416 + h * 16:432 + h * 16], nin[0:16, h, :], C[0:16, 0:16], r=["nin", "C"], w=[PS[7]], sig=(h == 3))
                                kb.cp("dve", ninT[:].rearrange("p a b -> p (a b)"), ps[7][:, 416:480], r=[PS[7]], w=["ninT"])
                                for h in range(4):
                                    bank = ps[5 + h // 2]
                                    o0 = (h % 2) * 130
                                    kb.dma("sp", Cin[:], smC[:, h].rearrange("q v k -> v q k"), r=(), w=["Cin"])
                                    for q4 in range(4):
                                        pb = q4 % 2
                                        for qq in range(4):
                                            q = q4 * 4 + qq
                                            kb.tr(ps[pb][:, qq * 128:(qq + 1) * 128], Cin[:, q, :], ident, r=["Cin", "C"], w=[PS[pb]], sig=(qq == 3))
                                        kb.cp("act", CsT[:, q4 * 4:q4 * 4 + 4, 0:128], ps[pb][:, :].rearrange("p (a b) -> p a b", b=128), r=[PS[pb]], w=["CsT"])
                                    kb.cp("dve", CsT[:, :, 128:129], ninT[:, h, :].unsqueeze(2), r=["ninT"], w=["CsT"])
                                    kb.cp("pool", bass.AP(qTm, 0, [[2048, 128], [136, 16], [1, 8]]),
                                          qkT[:, h, 0:128].rearrange("p (q t) -> p q t", t=8), r=["qkT"], w=["qTm"])
                                    kb.mm(bank[:, o0:o0 + 130], lhsT=PT[:, h, :], rhs=vx[:, h, :], start=True, stop=False, r=["PT", "vext%d" % (i % 2)], w=[PS[5 + h // 2]], sig=False)
                                    for q in range(16):
                                        kb.mm(bank[:, o0:o0 + 130], lhsT=qTm[:, q, :], rhs=CsT[:, q, :], start=False, stop=(q == 15), r=["qTm", "CsT"], w=[PS[5 + h // 2]], sig=(q == 15))
                                    kb.ts("dve", BMW[:], bms, cq[:, 8 + h:9 + h], None, ALU.mult, None, r=["C", "colq%d" % (i % 2)], w=["BMW"])
                                    kb.tt("dve", VWm[:], vx[:, h, 0:128].unsqueeze(1).to_broadcast([128, 16, 128]), BMW[:].unsqueeze(2).to_broadcast([128, 16, 128]),
                                          ALU.mult, r=["vext%d" % (i % 2), "BMW"], w=["VWm"])
                                    for q4 in range(4):
                                        pb = q4 % 2
                                        for qq in range(4):
                                            q = q4 * 4 + qq
                                            kb.mm(ps[pb][:, qq * 128:(qq + 1) * 128], lhsT=VWm[:, q, :], rhs=ktok[:, h, :], start=True, stop=True,
                                                  r=["VWm", "ktok"], w=[PS[pb]], sig=(qq == 3))
                                        for qq in range(4):
                                            q = q4 * 4 + qq
                                            kb.stt(Cin[:, q, :], Cin[:, q, :], decsb[:, q * 4 + h:q * 4 + h + 1], ps[pb][:, qq * 128:(qq + 1) * 128], ALU.mult, ALU.add,
                                                   r=["Cin", "decsb", PS[pb]], w=["Cin"])
                                    kb.dma("sp", o_smC[:, h].rearrange("q v k -> v q k"), Cin[:], r=["Cin"], w=())
                                    kb.mm(ps[7][0:16, 0:128], lhsT=BMW[:], rhs=ktok[:, h, :], start=True, stop=True, r=["BMW", "ktok"], w=[PS[7]])
                                    kb.stt(nout[:, h, :], nin[:, h, :], decc[:, h:h + 1], ps[7][0:16, 0:128], ALU.mult, ALU.add, r=["nin", "decc", PS[7]], w=["nout"])
                                numden_finish(i, cq)
                                kb.dma("sp", o_smn, nout[:], r=["nout"], w=())
                                s.barrier()
            s.barrier()
            stop_at(6)
            hrT = kb.sb(ph, "hrT", [128, 4, NTOK], BF16)
            with contextlib.ExitStack() as pb_:
                winB = kb.sb(pb_, "winB", [128, 8, 1024], BF16)
                kb.dma("pool", winB[:], win_v[:, :, 2056:3080], r=(), w=["winB"])
                hTgB = [kb.sb(pb_, "hTgB%d" % i, [128, 8, 512], BF16) for i in range(2)]
                WA = kb.sb(pb_, "WA", [128, 4, 128], F32)
                WX = kb.sb(pb_, "WX", [128, 4, 128], F32)
                kb.memset("pool", WA[:], 0.0, w=["WA"])
                kb.memset("pool", WX[:], 0.0, w=["WX"])
                for c in range(4):
                    for hp in range(2):
                        kb.dma("sp", WA[hp * 64:(hp + 1) * 64, c, hp * 64:(hp + 1) * 64], rg_w_a[0, 2 * c + hp], r=(), w=["WA"])
                        kb.dma("sp", WX[hp * 64:(hp + 1) * 64, c, hp * 64:(hp + 1) * 64], rg_w_x[0, 2 * c + hp], r=(), w=["WX"])
                cl = kb.sb(pb_, "cl", [128, 4], F32)
                cl2 = kb.sb(pb_, "cl2", [128, 4], F32)
                kb.act(cl[:], vecA[:, 28:32], AF.Exp, r=["vecA"], w=["cl"], scale=-1.0)
                kb.act(cl[:], cl[:], AF.Ln, r=["cl"], w=["cl"], bias=1.0, scale=1.0)
                kb.ts("dve", cl2[:], cl[:], -16.0, None, ALU.mult, None, r=["cl"], w=["cl2"])
                kb.ts("dve", cl[:], cl[:], -8.0, None, ALU.mult, None, r=["cl", "cl2"], w=["cl"])
                xp = [kb.sb(pb_, "xp%d" % c, [128, 515], F32) for c in range(4)]
                xps = kb.sb(pb_, "xps", [128, 16, 11], F32)
                hst = kb.sb(pb_, "hst", [128, 4], F32)
                h0T = kb.sb(pb_, "h0T", [128, 4, 16], F32)
                cvT = kb.sb(pb_, "cvT", [128, 4, 48], F32)
                hl = kb.sb(pb_, "hl", [128, 4, 16], F32)
                srh_sb = kb.sb(pb_, "srh_sb", [16, 512], F32)
                src_sb = kb.sb(pb_, "src_sb", [48, 512], F32)
                F5 = lambda nm: kb.sb(pb_, nm, [128, 512], F32)
                xc, rr, ii, aa, a2, uu, hh_, t5 = F5("xc"), F5("rr"), F5("ii"), F5("aa"), F5("a2"), F5("uu"), F5("hh"), F5("t5")
                for c in range(4):
                    kb.memset("pool", xp[c][:, 0:3], 0.0, w=["xp%d" % c])
                kb.memset("pool", hst[:], 0.0, w=["hst"])
                kb.dma("sp", srh_sb[:], srh, r=(), w=["srh_sb"])
                kb.dma("sp", src_sb[:], srconv, r=(), w=["src_sb"])
                for c in range(4):
                    kb.tr(ps[6][:, c * 16:(c + 1) * 16], srh_sb[0:16, c * 128:(c + 1) * 128], C[0:16, 0:16], r=["srh_sb", "C"], w=[PS[6]], sig=(c == 3))
                kb.cp("dve", h0T[:].rearrange("p a b -> p (a b)"), ps[6][:, 0:64], r=[PS[6]], w=["h0T"])
                for c in range(4):
                    kb.tr(ps[6][:, 64 + c * 48:64 + (c + 1) * 48], src_sb[0:48, c * 128:(c + 1) * 128], C[0:48, 0:48], r=["src_sb", "C"], w=[PS[6]], sig=(c == 3))
                kb.cp("dve", cvT[:].rearrange("p a b -> p (a b)"), ps[6][:, 64:256], r=[PS[6]], w=["cvT"])
                u_ = 0
                for tg, (t0, n) in enumerate(TGS):
                    sample = tg == 4
                    hT = hTgB[tg % 2]
                    for i in (range(4 * tg, 4 * tg + 4) if tg < 4 else [16]):
                        make_hT(hT, i, prescale=True, col0=(i % 4) * 128, res="hTg%d" % (tg % 2))
                    for c in range(4):
                        px, pgr = ps[(2 * u_) % 4], ps[(2 * u_ + 1) % 4]
                        PX, PGR = PS[(2 * u_) % 4], PS[(2 * u_ + 1) % 4]
                        u_ += 1
                        for kc in range(8):
                            kb.mm(px[:, 0:n], lhsT=winB[:, kc, c * 128:(c + 1) * 128], rhs=hT[:, kc, 0:n], start=(kc == 0), stop=(kc == 7),
                                  r=["winB", "hTg%d" % (tg % 2)], w=[PX])
                        for kc in range(8):
                            kb.mm(pgr[:, 0:n], lhsT=winB[:, kc, 512 + c * 128:512 + (c + 1) * 128], rhs=hT[:, kc, 0:n], start=(kc == 0), stop=(kc == 7),
                                  r=["winB", "hTg%d" % (tg % 2)], w=[PGR])
                        cw = lambda j: vecA[:, c * 4 + j:c * 4 + j + 1]
                        cb = vecA[:, 16 + c:17 + c]
                        if not sample:
                            kb.cp("act", xp[c][:, 3:3 + n], px[:, 0:n], r=[PX], w=["xp%d" % c])
                            kb.ts("dve", xc[:, 0:n], xp[c][:, 0:n], cw(0), cb, ALU.mult, ALU.add, r=["xp%d" % c, "vecA"], w=["xc"])
                            for j in range(1, 4):
                                kb.stt(xc[:, 0:n], xp[c][:, j:j + n], cw(j), xc[:, 0:n], ALU.mult, ALU.add, r=["xp%d" % c, "vecA", "xc"], w=["xc"])
                            if tg == 3:
                                kb.tr(ps[6][0:3, c * 128:(c + 1) * 128], xp[c][:, n:n + 3], ident, r=["xp%d" % c, "C"], w=[PS[6]])
                            else:
                                kb.cp("pool", xp[c][:, 0:3], xp[c][:, n:n + 3], r=["xp%d" % c], w=["xp%d" % c])
                        else:
                            kb.cp("dve", xps[:, :, 0:3], cvT[:, c, :].rearrange("p (q j) -> p q j", j=3), r=["cvT"], w=["xps"])
                            kb.cp("act", xps[:, :, 3:11], px[:, 0:n].rearrange("p (q t) -> p q t", t=8), r=[PX], w=["xps"])
                            xc3 = xc[:, 0:n].rearrange("p (q t) -> p q t", t=8)
                            kb.ts("dve", xc3, xps[:, :, 0:8], cw(0), cb, ALU.mult, ALU.add, r=["xps", "vecA"], w=["xc"])
                            for j in range(1, 4):
                                kb.stt(xc3, xps[:, :, j:j + 8], cw(j), xc3, ALU.mult, ALU.add, r=["xps", "vecA", "xc"], w=["xc"])
                            for j in range(3):
                                kb.tr(ps[5][0:16, j * 128:(j + 1) * 128], xps[:, :, 8 + j], ident, r=["xps", "C"], w=[PS[5]], sig=(j == 2))
                            kb.cp("dve", t5[0:16, 0:384], ps[5][0:16, 0:384], r=[PS[5]], w=["t5"])
                            kb.dma("sp", o_srconv.rearrange("(q j) f -> q j f", j=3)[:, :, c * 128:(c + 1) * 128],
                                   t5[0:16, 0:384].rearrange("q (j f) -> q j f", f=128), r=["t5"], w=())
                        kb.mm(ps[4][:, 0:n], lhsT=WA[:, c, :], rhs=xc[:, 0:n], start=True, stop=True, r=["WA", "xc"], w=[PS[4]])
                        kb.mm(ps[5][:, 0:n], lhsT=WX[:, c, :], rhs=xc[:, 0:n], start=True, stop=True, r=["WX", "xc"], w=[PS[5]])
                        kb.act(rr[:, 0:n], ps[4][:, 0:n], AF.Sigmoid, r=[PS[4], "vecA"], w=["rr"], bias=vecA[:, 20 + c:21 + c], scale=1.0)
                        kb.act(ii[:, 0:n], ps[5][:, 0:n], AF.Sigmoid, r=[PS[5], "vecA"], w=["ii"], bias=vecA[:, 24 + c:25 + c], scale=1.0)
                        kb.act(aa[:, 0:n], rr[:, 0:n], AF.Exp, r=["rr", "cl"], w=["aa"], scale=cl[:, c:c + 1])
                        kb.act(a2[:, 0:n], rr[:, 0:n], AF.Exp, r=["rr", "cl2"], w=["a2"], scale=cl2[:, c:c + 1])
                        kb.act(a2[:, 0:n], a2[:, 0:n], AF.Sqrt, r=["a2"], w=["a2"], bias=1.0, scale=-1.0)
                        kb.tt("dve", uu[:, 0:n], a2[:, 0:n], ii[:, 0:n], ALU.mult, r=["a2", "ii"], w=["uu"])
                        kb.tt("dve", uu[:, 0:n], uu[:, 0:n], xc[:, 0:n], ALU.mult, r=["uu", "xc"], w=["uu"])
                        if not sample:
                            s.add("dve", lambda g_, c=c, n=n: g_.tensor_tensor_scan(out=hh_[:, 0:n], data0=aa[:, 0:n], data1=uu[:, 0:n], initial=hst[:, c:c + 1],
                                                                                     op0=ALU.mult, op1=ALU.add),
                                  r=["aa", "uu", "hst"], w=["hh"], tag="scanH")
                            kb.cp("dve", hst[:, c:c + 1], hh_[:, n - 1:n], r=["hh"], w=["hst"])
                        else:
                            aa3 = aa[:, 0:n].rearrange("p (q t) -> p q t", t=8)
                            uu3 = uu[:, 0:n].rearrange("p (q t) -> p q t", t=8)
                            kb.tt("dve", t5[:, 0:16].unsqueeze(2), aa3[:, :, 0:1], h0T[:, c, :].unsqueeze(2), ALU.mult, r=["aa", "h0T", "t5"], w=["t5"])
                            kb.tt("dve", uu3[:, :, 0:1], uu3[:, :, 0:1], t5[:, 0:16].unsqueeze(2), ALU.add, r=["uu", "t5"], w=["uu"])
                            kb.tt("dve", aa[:, 0:n], aa[:, 0:n], rst, ALU.mult, r=["aa", "C"], w=["aa"])
                            s.add("dve", lambda g_, n=n: g_.tensor_tensor_scan(out=hh_[:, 0:n], data0=aa[:, 0:n], data1=uu[:, 0:n], initial=0.0,
                                                                                op0=ALU.mult, op1=ALU.add),
                                  r=["aa", "uu"], w=["hh"], tag="scanH")
                            kb.cp("dve", hl[:, c, :].unsqueeze(2), hh_[:, 0:n].rearrange("p (q t) -> p q t", t=8)[:, :, 7:8], r=["hh"], w=["hl"])
                        kb.act(t5[:, 0:n], pgr[:, 0:n], AF.Square, r=[PGR, "t5"], w=["t5"])
                        kb.ts("dve", t5[:, 0:n], t5[:, 0:n], 0.044715, 1.0, ALU.mult, ALU.add, r=["t5"], w=["t5"])
                        kb.tt("dve", t5[:, 0:n], t5[:, 0:n], pgr[:, 0:n], ALU.mult, r=["t5", PGR], w=["t5"])
                        kb.act(t5[:, 0:n], t5[:, 0:n], AF.Tanh, r=["t5"], w=["t5"], scale=0.7978845608028654)
                        kb.ts("dve", t5[:, 0:n], t5[:, 0:n], 1.0, 0.5, ALU.add, ALU.mult, r=["t5"], w=["t5"])
                        kb.tt("dve", t5[:, 0:n], t5[:, 0:n], pgr[:, 0:n], ALU.mult, r=["t5", PGR], w=["t5"])
                        kb.tt("dve", hrT[:, c, t0:t0 + n], t5[:, 0:n], hh_[:, 0:n], ALU.mult, r=["t5", "hh"], w=["hrT_g%d" % tg])
                    if tg == 3:
                        kb.cp("dve", t5[0:3, :], ps[6][0:3, 0:512], r=[PS[6]], w=["t5"])
                        kb.dma("sp", o_prconv, t5[0:3, :], r=["t5"], w=())
                kb.tr(ps[7][0:4, 0:128], hst[:, 0:4], ident, r=["hst", "C"], w=[PS[7]])
                kb.cp("dve", rr[0:4, 0:128], ps[7][0:4, 0:128], r=[PS[7]], w=["rr"])
                kb.dma("sp", o_prh, rr[0:4, 0:128], r=["rr"], w=())
                for c in range(4):
                    kb.tr(ps[4][0:16, c * 128:(c + 1) * 128], hl[:, c, :], ident, r=["hl", "C"], w=[PS[4]], sig=(c == 3))
                kb.cp("dve", ii[0:16, :], ps[4][0:16, :], r=[PS[4]], w=["ii"])
                kb.dma("sp", o_srh, ii[0:16, :], r=["ii"], w=())
                s.barrier()
            stop_at(7)
            with contextlib.ExitStack() as pc_:
                alloc_gl(pc_)
                mod_prepare(l, 1, 1.0, blocks=[4, 5])
                Gp, Gs = gl["Gp"], gl["Gs"]
                wout = kb.sb(pc_, "wout", [128, 8, D], BF16)
                kb.dma("pool", wout[:], ab_w_out[0].rearrange("(kc p) n -> p kc n", p=128), r=(), w=["wout"])
                proj_acc(wout, "wout", 8, lambda i, kc: ((hmT[:, kc, i * 128:(i + 1) * 128], "hmT%d" % i) if kc < 4 else
                                                          (hrT[:, kc - 4, i * 128:(i + 1) * 128], "hrT_g%d" % (i // 4))), True, True)
                s.barrier()
        s.barrier()

    CW = -0.6065306597126334
    RT = F32

    def rwkv_mixer(l):
        rwkv_mixer_(l)
        s.skip = False
        s.barrier()

    def rwkv_mixer_(l):
        rw_mu = kb.din("muT", [128, 48])
        rw_wr = kb.din("rw_wr", [1, D, D])
        rw_wk = kb.din("rw_wk", [1, D, D])
        rw_wv = kb.din("rw_wv", [1, D, D])
        rw_wo = kb.din("rw_wo", [1, D, D])
        rw_w0 = kb.din("rw_w0", [1, D])
        rw_w1 = kb.din("rw_w1", [1, D, 64])
        rw_w2 = kb.din("rw_w2", [1, 64, D])
        rw_a0 = kb.din("rw_a0", [1, D])
        rw_a1 = kb.din("rw_a1", [1, D, 64])
        rw_a2 = kb.din("rw_a2", [1, 64, D])
        rw_g1 = kb.din("rw_g1", [1, D, 128])
        rw_g2 = kb.din("rw_g2", [1, 128, D])
        rw_kk = kb.din("rw_k_k", [1, D])
        rw_ka = kb.din("rw_k_a", [1, D])
        rw_rk = kb.din("rk_flat", [1, D])
        rw_lng = kb.din("rw_lnx_g", [1, D])
        rw_lnb = kb.din("rw_lnx_b", [1, D])
        swkv = kb.din("swkv", [16, 16, 64, 64])
        sshift = kb.din("sshift", [16, D])
        o_pwkv = kb.dout("o_pwkv", [16, 64, 64])
        o_pshift = kb.dout("o_pshift", [1, D])
        o_swkv = kb.dout("o_swkv", [16, 16, 64, 64])
        o_sshift = kb.dout("o_sshift", [16, D])

        cm = lambda nm: C[:, CST_OFF[nm]:CST_OFF[nm] + 128]
        maskP, maskS, upP, lowP, upS, lowS, blkS, ones, bms = cm("maskP"), cm("maskS"), cm("upP"), cm("lowP"), cm("upS"), cm("lowS"), cm("blkS"), cm("ones"), C[:, CST_OFF["bms"]:CST_OFF["bms"] + 16]

        mod_prepare(l, 1, 1.0, blocks=range(4))
        with contextlib.ExitStack() as ph:
            ygT = kb.sb(ph, "ygT", [128, 8, NTOK], BF16)
            muT = kb.sb(ph, "muT_sb", [128, 48], F32)
            kb.dma("sp", muT[:], rw_mu, r=(), w=["muT"])
            identR = kb.sb(ph, "identR", [128, 128], RT)
            kb.cp("dve", identR[:], ident, r=["C"], w=["identR"])
            w1b = kb.sb(ph, "w1b", [128, 8, 64], BF16)
            a1b = kb.sb(ph, "a1b", [128, 8, 64], BF16)
            g1b = kb.sb(ph, "g1b", [128, 8, 128], BF16)
            kb.dma("pool", w1b[:], rw_w1[0].rearrange("(kc p) n -> p kc n", p=128), r=(), w=["w1b"])
            kb.dma("pool", a1b[:], rw_a1[0].rearrange("(kc p) n -> p kc n", p=128), r=(), w=["a1b"])
            kb.dma("pool", g1b[:], rw_g1[0].rearrange("(kc p) n -> p kc n", p=128), r=(), w=["g1b"])
            hlast = kb.sb(ph, "hlast", [128, 8, 17], F32)
            sh0T = kb.sb(ph, "sh0T", [128, 8, 16], BF16)
            with contextlib.ExitStack() as p0:
                shs = kb.sb(p0, "shs", [16, D], F32)
                kb.dma("sp", shs[:], sshift, r=(), w=["shs"])
                for c in range(8):
                    kb.tr(ps[0][:, c * 16:(c + 1) * 16], shs[0:16, c * 128:(c + 1) * 128], C[0:16, 0:16], r=["shs", "C"], w=[PS[0]], sig=(c == 7))
                kb.cp("dve", sh0T[:].rearrange("p a b -> p (a b)"), ps[0][:, 0:128], r=[PS[0]], w=["sh0T"])
                s.barrier()

            def hook_last(i, c, src, psres):
                if i == 15:
                    kb.ts("dve", hlast[:, c, 0:1], src[:, 127:128], modT[:, 8 + c, 0:1], modT[:, c, 0:1], ALU.mult, ALU.add, r=["modT", psres], w=["hlast"])
                elif i == 16:
                    v3 = src.rearrange("p (q t) -> p q t", t=8)[:, :, 7:8]
                    kb.tt("dve", hlast[:, c, 1:17].unsqueeze(2), v3, modT[:, 8 + c, 1:NT].unsqueeze(2), ALU.mult, r=["modT", psres], w=["hlast"])
                    kb.tt("dve", hlast[:, c, 1:17], hlast[:, c, 1:17], modT[:, c, 1:NT], ALU.add, r=["hlast", "modT"], w=["hlast"])

            stop_at(51)
            for hg in range(4):
                c0 = hg * 256
                with contextlib.ExitStack() as pp:
                    wrs = kb.sb(pp, "wrs", [128, 8, 256], BF16)
                    wks = kb.sb(pp, "wks", [128, 8, 256], BF16)
                    wvs = kb.sb(pp, "wvs", [128, 8, 256], BF16)
                    for wt, src in ((wrs, rw_wr), (wks, rw_wk), (wvs, rw_wv)):
                        kb.dma("pool", wt[:], src[0].rearrange("(kc p) n -> p kc n", p=128)[:, :, c0:c0 + 256], r=(), w=["wqkv"])
                    w2s = kb.sb(pp, "w2s", [64, 256], BF16)
                    a2s = kb.sb(pp, "a2s", [64, 256], BF16)
                    g2s = kb.sb(pp, "g2s", [128, 256], BF16)
                    kb.dma("pool", w2s[:], rw_w2[0, :, c0:c0 + 256], r=(), w=["w2s"])
                    kb.dma("pool", a2s[:], rw_a2[0, :, c0:c0 + 256], r=(), w=["a2s"])
                    kb.dma("pool", g2s[:], rw_g2[0, :, c0:c0 + 256], r=(), w=["g2s"])
                    w0r = kb.sb(pp, "w0r", [1, 256], F32)
                    a0r = kb.sb(pp, "a0r", [1, 256], F32)
                    kb.dma("sp", w0r[:], rw_w0[0:1, c0:c0 + 256], r=(), w=["w0r"])
                    kb.dma("sp", a0r[:], rw_a0[0:1, c0:c0 + 256], r=(), w=["a0r"])
                    bcs = {}
                    alias = {"kkb": tA[2][:, 0:256], "kab": tA[2][:, 256:512], "rkb": tA[3][:, 0:256], "lngb": tA[3][:, 256:512]}
                    for nm, src in (("kkb", rw_kk), ("kab", rw_ka), ("rkb", rw_rk), ("lngb", rw_lng), ("lnbb", rw_lnb)):
                        bcs[nm] = alias[nm] if nm in alias else kb.sb(pp, nm, [128, 256], F32)
                        kb.dma("sp", bcs[nm][:], src[0:1, c0:c0 + 256].to_broadcast([128, 256]), r=(), w=[nm])
                    hTg = [kb.sb(pp, "hTr%d" % i, [128, 8, 130], BF16) for i in range(2)]
                    dx = kb.sb(pp, "dx", [128, 8, 128], BF16)
                    xj = [kb.sb(pp, "xj%d" % i, [128, 8, 128], BF16) for i in range(2)]
                    loT = kb.sb(pp, "loT", [128, 3, 128], BF16)
                    TB = lambda j: tB[j // 4][:, (j % 4) * 256:(j % 4) * 256 + 256]
                    Rr, Kk, KKn, Aa, SG, CSs, Ee, Tt = [TB(j) for j in range(8)]
                    tbres = lambda j: "tB%d" % (j // 4)
                    Gt = tA[0][:, 0:256]
                    small = tA[1][:, 256:320]
                    F3 = lambda nm: kb.sb(pp, nm, [128, 256], RT)
                    Vv, AL, KT, RB, KH, BH = F3("Vv"), F3("AL"), F3("KT"), F3("RB"), F3("KH"), F3("BH")
                    BT = tA[0][:, 256:512]
                    fmT = {nm: kb.sb(pp, nm, [64, 4, 128], RT) for nm in ("alT", "btT", "ktT", "rbT")}
                    chainall = kb.sb(pp, "chainall", [128, 6, 4, 128], RT)
                    chn = {nm: chainall[:, j] for j, nm in enumerate(("ApA", "ApB", "BpA", "BpB", "TTa", "TTb"))}
                    S0nat = kb.sb(pp, "S0nat", [64, 16, 64], F32)
                    S0Tq = kb.sb(pp, "S0Tq", [64, 16, 64], RT)
                    SLo = S0nat
                    AakT = kb.sb(pp, "AakT", [128, 4, 128], RT)
                    YTs = AakT[0:64, :, :]
                    ArbT = kb.sb(pp, "ArbT", [128, 4, 128], RT)
                    ArkT = kb.sb(pp, "ArkT", [128, 4, 128], RT)
                    Ahat = kb.sb(pp, "Ahat", [128, 4, 64], RT)
                    X1 = kb.sb(pp, "X1", [128, 4, 64], RT)
                    U0 = kb.sb(pp, "U0", [128, 4, 64], RT)
                    Gm = kb.sb(pp, "Gm", [64, 4, 64], RT)
                    RhT = kb.sb(pp, "RhT", [64, 4, 128], RT)
                    S0T = kb.sb(pp, "S0T", [64, 4, 64], RT)
                    PLc = tA[1][0:64, 320:384]
                    yb = tA[1][:, 0:256]
                    kb.ts("dve", S0T[:].rearrange("p a b -> p (a b)"), C[0:64, 0:256], 0.0, None, ALU.mult, None, r=["C"], w=["S0T0", "S0T1", "S0T2", "S0T3"])
                    kb.memset("pool", hTg[1][:, :, 128:129], 0.0, w=["hTr1"])

                    algb = [4]

                    def nb():
                        b_ = algb[0]
                        algb[0] = 4 + (algb[0] - 3) % 4
                        return b_

                    def grp4(mmf, n_cols, m_rows=128):
                        b_ = nb()
                        for hl in range(4):
                            items = mmf(hl)
                            for j, (lt, rh, rd) in enumerate(items):
                                kb.mm(ps[b_][0:m_rows, hl * n_cols:(hl + 1) * n_cols], lhsT=lt, rhs=rh, start=(j == 0), stop=(j == len(items) - 1),
                                      r=rd, w=[PS[b_]], sig=(hl == 3 and j == len(items) - 1))
                        return b_

                    fsl = lambda t_, hl: t_[:, hl, :]
                    tsl = lambda t_, hl: t_[:, hl * 64:(hl + 1) * 64]

                    def sample_states():
                        Bhm = chainall[:, 0:2].rearrange("p a h t -> p (a h t)").rearrange("p (q k) -> p q k", k=64)
                        Khm = chainall[:, 2:4].rearrange("p a h t -> p (a h t)").rearrange("p (q k) -> p q k", k=64)
                        Gq = chainall[0:64, 4:6].rearrange("p a h t -> p (a h t)").rearrange("p (q k) -> p q k", k=64)
                        bmq = bms.unsqueeze(2).to_broadcast([128, 16, 64])
                        for hl in range(4):
                            h = hg * 4 + hl
                            kb.tt("dve", Bhm, tsl(BH, hl).unsqueeze(1).to_broadcast([128, 16, 64]), bmq, ALU.mult, r=["BH", "C"], w=["ApA0", "ApA1", "ApA2", "ApA3"] + ["ApB0", "ApB1", "ApB2", "ApB3"])
                            kb.tt("pool", Khm, tsl(KH, hl).unsqueeze(1).to_broadcast([128, 16, 64]), bmq, ALU.mult, r=["KH", "C"], w=["BpA0", "BpA1", "BpA2", "BpA3"] + ["BpB0", "BpB1", "BpB2", "BpB3"])
                            kb.dma("sp", S0nat[:], swkv[:, h].rearrange("q v k -> v q k"), r=(), w=["S0nat"])
                            b0, b1 = nb(), nb()
                            for q in range(16):
                                bb = b0 if q < 8 else b1
                                kb.tr(ps[bb][0:64, (q % 8) * 64:(q % 8 + 1) * 64], S0nat[:, q, :], ident[0:64, 0:64], r=["S0nat", "C"], w=[PS[bb]], sig=(q % 8 == 7))
                            kb.cp("act", S0Tq[:, 0:8, :], ps[b0][0:64, :].rearrange("p (q k) -> p q k", k=64), r=[PS[b0]], w=["S0Tq"])
                            kb.cp("dve", S0Tq[:, 8:16, :], ps[b1][0:64, :].rearrange("p (q k) -> p q k", k=64), r=[PS[b1]], w=["S0Tq"])
                            b_ = nb()
                            for q in range(16):
                                kb.mm(ps[b_][0:64, q * 8:(q + 1) * 8], lhsT=S0Tq[:, q, :], rhs=RhT[:, hl, q * 8:(q + 1) * 8], start=True, stop=True,
                                      r=["S0Tq", "RhT%d" % hl], w=[PS[b_]], sig=(q == 15))
                            kb.cp("act", YTs[:, hl, :], ps[b_][0:64, 0:128], r=[PS[b_]], w=["AakT%d" % hl])
                            g0, g1 = nb(), nb()
                            for half, bb in ((0, g0), (1, g1)):
                                kb.mm(ps[bb][0:64, :], lhsT=Ahat[:, hl, :], rhs=Bhm[:, half * 8:(half + 1) * 8, :], start=True, stop=True,
                                      r=["Ahat%d" % hl] + ["ApA0", "ApA1", "ApA2", "ApA3"] + ["ApB0", "ApB1", "ApB2", "ApB3"], w=[PS[bb]])
                            for q in range(16):
                                bb = g0 if q < 8 else g1
                                kb.stt(Gq[:, q, :], ident[0:64, 0:64], PLc[:, hl * 16 + q:hl * 16 + q + 1], ps[bb][0:64, (q % 8) * 64:(q % 8 + 1) * 64], ALU.mult, ALU.add,
                                       r=["C", "tA1", PS[bb]], w=["TTa0", "TTa1", "TTa2", "TTa3"] + ["TTb0", "TTb1", "TTb2", "TTb3"])
                            for half in range(2):
                                bb = nb()
                                kb.mm(ps[bb][0:64, :], lhsT=tsl(Vv, hl), rhs=Khm[:, half * 8:(half + 1) * 8, :], start=True, stop=False, r=["Vv"] + ["BpA0", "BpA1", "BpA2", "BpA3"] + ["BpB0", "BpB1", "BpB2", "BpB3"], w=[PS[bb]], sig=False)
                                kb.mm(ps[bb][0:64, :], lhsT=U0[:, hl, :], rhs=Bhm[:, half * 8:(half + 1) * 8, :], start=False, stop=False, r=["U0%d" % hl] + ["ApA0", "ApA1", "ApA2", "ApA3"] + ["ApB0", "ApB1", "ApB2", "ApB3"], w=[PS[bb]], sig=False)
                                for qq in range(8):
                                    q = half * 8 + qq
                                    kb.mm(ps[bb][0:64, qq * 64:(qq + 1) * 64], lhsT=S0Tq[:, q, :], rhs=Gq[:, q, :], start=False, stop=(qq == 7),
                                          r=["S0Tq"] + ["TTa0", "TTa1", "TTa2", "TTa3"] + ["TTb0", "TTb1", "TTb2", "TTb3"], w=[PS[bb]], sig=(qq == 7))
                                kb.cp("act" if half else "dve", SLo[:, half * 8:(half + 1) * 8, :], ps[bb][0:64, :].rearrange("p (q k) -> p q k", k=64), r=[PS[bb]], w=["S0nat"])
                            kb.dma("sp", o_swkv[:, h].rearrange("q v k -> v q k"), SLo[:], r=["S0nat"], w=())
                        by = grp4(lambda hl: [(ArkT[:, hl, :], tsl(Vv, hl), ["ArkT%d" % hl, "Vv"]), (ArbT[:, hl, :], U0[:, hl, :], ["ArbT%d" % hl, "U0%d" % hl]),
                                               (YTs[:, hl, :], identR[0:64, 0:64], ["AakT%d" % hl, "identR"])], 64)
                        kb.cp("act", yb[:], ps[by][:, 0:256], r=[PS[by]], w=["tA1"])

                    for i in range(NT):
                        sample = i == 16
                        k_ = i % 2
                        hT = hTg[k_]
                        hres = "hTr%d" % k_
                        last_pass = hg == 3
                        if hg == 0 and i == 1:
                            stop_at(58)
                        if hg == 0 and i == 16:
                            stop_at(59)
                        make_hT(hT, i, prescale=last_pass, col0=1, res=hres, hook=(hook_last if hg == 0 else None))
                        if i == 0:
                            kb.memset("pool", hT[:, :, 0:1], 0.0, w=[hres])
                        elif not sample:
                            kb.cp("pool", hT[:, :, 0:1], hTg[1 - k_][:, :, 128:129], r=["hTr%d" % (1 - k_)], w=[hres])
                        cur = hT[:, :, 1:129]
                        if not sample:
                            kb.tt("pool", dx[:], hT[:, :, 0:128], cur, ALU.subtract, r=[hres], w=["dx"])
                        else:
                            kb.cp("pool", dx[:], hT[:, :, 0:128], r=[hres], w=["dx"])
                            kb.cp("pool", dx[:].rearrange("p c (q t) -> p c q t", t=8)[:, :, :, 0], sh0T[:], r=["sh0T"], w=["dx"])
                            kb.tt("pool", dx[:], dx[:], cur, ALU.subtract, r=["dx", hres], w=["dx"])
                        def mix(j, buf):
                            e = "dve" if j % 2 == 0 else "pool"
                            kb.tt(e, xj[buf][:], dx[:], muT[:, j * 8:(j + 1) * 8].unsqueeze(2).to_broadcast([128, 8, 128]), ALU.mult, r=["dx", "muT"], w=["xj%d" % buf])
                            kb.tt(e, xj[buf][:], xj[buf][:], cur, ALU.add, r=["xj%d" % buf, hres], w=["xj%d" % buf])
                            return xj[buf], "xj%d" % buf
                        for j, (wt, bank, off) in ((0, (wrs, 0, 0)), (2, (wks, 0, 256)), (3, (wvs, 1, 0))):
                            xx, xr_ = mix(j, j % 2)
                            for kc in range(8):
                                kb.mm(ps[bank][:, off:off + 256], lhsT=xx[:, kc, :], rhs=wt[:, kc, :], start=(kc == 0), stop=(kc == 7), r=[xr_, "wqkv"], w=[PS[bank]])
                        for j, (wt, wr_, m_, off3) in ((1, (w1b, "w1b", 64, 0)), (4, (a1b, "a1b", 64, 128)), (5, (g1b, "g1b", 128, 256))):
                            xx, xr_ = mix(j, j % 2)
                            for kc in range(8):
                                kb.mm(ps[3][0:m_, off3:off3 + 128], lhsT=wt[:, kc, :], rhs=xx[:, kc, :], start=(kc == 0), stop=(kc == 7), r=[xr_, wr_], w=[PS[3]])
                        kb.act(loT[0:64, 0, :], ps[3][0:64, 0:128], AF.Tanh, r=[PS[3]], w=["loT"])
                        kb.cp("act", loT[0:64, 1, :], ps[3][0:64, 128:256], r=[PS[3]], w=["loT"])
                        kb.act(loT[:, 2, :], ps[3][:, 256:384], AF.Sigmoid, r=[PS[3]], w=["loT"])
                        kb.mm(ps[1][:, 256:512], lhsT=loT[0:64, 0, :], rhs=w2s[:], start=True, stop=False, r=["loT", "w2s"], w=[PS[1]], sig=False)
                        kb.mm(ps[1][:, 256:512], lhsT=ones[0:1, :], rhs=w0r[:], start=False, stop=True, r=["C", "w0r"], w=[PS[1]])
                        kb.mm(ps[2][:, 0:256], lhsT=loT[0:64, 1, :], rhs=a2s[:], start=True, stop=False, r=["loT", "a2s"], w=[PS[2]], sig=False)
                        kb.mm(ps[2][:, 0:256], lhsT=ones[0:1, :], rhs=a0r[:], start=False, stop=True, r=["C", "a0r"], w=[PS[2]])
                        kb.mm(ps[2][:, 256:512], lhsT=loT[:, 2, :], rhs=g2s[:], start=True, stop=True, r=["loT", "g2s"], w=[PS[2]])
                        if hg == 0 and i == 0:
                            stop_at(52)
                        kb.cp("act", Rr, ps[0][:, 0:256], r=[PS[0]], w=[tbres(0)])
                        kb.cp("act", Kk, ps[0][:, 256:512], r=[PS[0]], w=[tbres(1)])
                        kb.cp("act", Vv[:], ps[1][:, 0:256], r=[PS[1]], w=["Vv"])
                        kb.act(SG, ps[1][:, 256:512], AF.Sigmoid, r=[PS[1]], w=[tbres(4)])
                        kb.act(Aa, ps[2][:, 0:256], AF.Sigmoid, r=[PS[2]], w=[tbres(3)])
                        kb.cp("act", Gt, ps[2][:, 256:512], r=[PS[2]], w=["tA0"])
                        kb.tt("dve", KKn, Kk, bcs["kkb"][:], ALU.mult, r=[tbres(1), "kkb"], w=[tbres(2)])
                        kb.tt("dve", Tt, KKn, KKn, ALU.mult, r=[tbres(2)], w=[tbres(7)])
                        s.add("dve", lambda g_: g_.tensor_reduce(out=small[:, 0:4], in_=Tt.rearrange("p (h k) -> p h k", k=64), axis=AX.X, op=ALU.add),
                              r=[tbres(7)], w=["tA1"], tag="red")
                        kb.act(small[:, 0:4], small[:, 0:4], AF.Sqrt, r=["tA1"], w=["tA1"])
                        kb.ts("dve", small[:, 0:4], small[:, 0:4], 1e-12, None, ALU.max, None, r=["tA1"], w=["tA1"])
                        s.add("dve", lambda g_: g_.reciprocal(out=small[:, 0:4], in_=small[:, 0:4]), r=["tA1"], w=["tA1"], tag="recip")
                        kb.tt("dve", KKn.rearrange("p (h k) -> p h k", k=64), KKn.rearrange("p (h k) -> p h k", k=64),
                              small[:, 0:4].unsqueeze(2).to_broadcast([128, 4, 64]), ALU.mult, r=[tbres(2), "tA1"], w=[tbres(2)])
                        kb.stt(Tt, Aa, -1.0, bcs["kab"][:], ALU.add, ALU.mult, r=[tbres(3), "kab"], w=[tbres(7)])
                        kb.tt("dve", Tt, Tt, Kk, ALU.mult, r=[tbres(7), tbres(1)], w=[tbres(7)])
                        kb.tt("dve", Kk, Kk, Tt, ALU.add, r=[tbres(1), tbres(7)], w=[tbres(1)])
                        kb.tt("pool", Tt, Rr, Kk, ALU.mult, r=[tbres(0), tbres(1)], w=[tbres(7)])
                        kb.tt("pool", Tt, Tt, bcs["rkb"][:], ALU.mult, r=[tbres(7), "rkb"], w=[tbres(7)])
                        s.add("dve", lambda g_: g_.tensor_reduce(out=small[:, 4:8], in_=Tt.rearrange("p (h k) -> p h k", k=64), axis=AX.X, op=ALU.add),
                              r=[tbres(7)], w=["tA1"], tag="red")
                        kb.tt("pool", Aa, KKn, Aa, ALU.mult, r=[tbres(2), tbres(3)], w=[tbres(3)])
                        if hg == 0 and i == 0:
                            stop_at(53)
                        Um, Jm = (maskP, ones) if not sample else (maskS, blkS)
                        kb.mm(ps[4][:, 0:256], lhsT=Um, rhs=SG, start=True, stop=True, r=["C", tbres(4)], w=[PS[4]])
                        kb.mm(ps[4][:, 256:512], lhsT=Jm, rhs=SG, start=True, stop=True, r=["C", tbres(4)], w=[PS[4]])
                        nq = 1 if not sample else 16
                        for hl in range(4):
                            kb.mm(ps[3][0:64, 384 + hl * nq:384 + (hl + 1) * nq], lhsT=SG[:, hl * 64:(hl + 1) * 64], rhs=(ones[:, 0:1] if not sample else bms),
                                  start=True, stop=True, r=[tbres(4), "C"], w=[PS[3]], sig=(hl == 3))
                        kb.act(PLc[:, 0:4 * nq], ps[3][0:64, 384:384 + 4 * nq], AF.Exp, r=[PS[3]], w=["tA1"], scale=CW)
                        kb.cp("act", CSs, ps[4][:, 0:256], r=[PS[4]], w=[tbres(5)])
                        kb.tt("dve", Tt, CSs, SG, ALU.subtract, r=[tbres(5), tbres(4)], w=[tbres(7)])
                        kb.act(Ee, Tt, AF.Exp, r=[tbres(7)], w=[tbres(6)], scale=CW)
                        kb.stt(AL[:], KKn, -1.0, Ee, ALU.mult, ALU.mult, r=[tbres(2), tbres(6)], w=["AL"])
                        kb.act(Ee, CSs, AF.Exp, r=[tbres(5), "AL"], w=[tbres(6)], scale=-CW)
                        kb.tt("dve", BT[:], Aa, Ee, ALU.mult, r=[tbres(3), tbres(6)], w=["tA0"])
                        kb.tt("pool", KT[:], Kk, Ee, ALU.mult, r=[tbres(1), tbres(6)], w=["KT"])
                        kb.act(Tt, CSs, AF.Exp, r=[tbres(5)], w=[tbres(7)], scale=CW)
                        kb.tt("dve", RB[:], Rr, Tt, ALU.mult, r=[tbres(0), tbres(7)], w=["RB"])
                        kb.tt("dve", Ee, ps[4][:, 256:512], CSs, ALU.subtract, r=[PS[4], tbres(5), "tA0", "KT"], w=[tbres(6)])
                        kb.act(Ee, Ee, AF.Exp, r=[tbres(6)], w=[tbres(6)], scale=CW)
                        kb.tt("dve", KH[:], Kk, Ee, ALU.mult, r=[tbres(1), tbres(6)], w=["KH"])
                        kb.tt("pool", BH[:], Aa, Ee, ALU.mult, r=[tbres(3), tbres(6)], w=["BH"])
                        for qi, (nm, src, sr) in enumerate((("alT", AL, "AL"), ("btT", BT, "tA0"), ("ktT", KT, "KT"), ("rbT", RB, "RB"))):
                            b_ = 5 + qi % 2
                            for hl in range(4):
                                sv = src[:, hl * 64:(hl + 1) * 64]
                                kb.tr(ps[b_][0:64, hl * 128:(hl + 1) * 128], sv.bitcast(F32) if nm != "btT" else sv, ident,
                                      r=[sr, "C"], w=[PS[b_]], sig=(hl == 3))
                            kb.cp("act" if qi % 2 else "dve", fmT[nm][:].rearrange("p a b -> p (a b)"), ps[b_][0:64, :], r=[PS[b_]], w=[nm])
                        if hg == 0 and i == 0:
                            stop_at(54)
                        alT, btT, ktT, rbT = fmT["alT"], fmT["btT"], fmT["ktT"], fmT["rbT"]
                        mlow, mup, minc = (lowP, upP, maskP) if not sample else (lowS, upS, maskS)
                        n_it = 6 if not sample else 2

                        def head_alg(hl):
                            B_ = 4 + hl
                            P_ = PS[B_]
                            rn = lambda nm: "%s%d" % (nm, hl)
                            pw = ps[B_][:, 0:128]

                            def mm1(lt, rh, rd, cols=128, rows=128, first=True, last=True):
                                kb.mm(ps[B_][0:rows, 0:cols], lhsT=lt, rhs=rh, start=first, stop=last, r=rd, w=[P_], sig=last)

                            mm1(fsl(alT, hl), fsl(btT, hl), ["alT", "btT"])
                            kb.tt("dve", chn["ApA"][:, hl, :], pw, mlow, ALU.mult, r=[P_, "C"], w=[rn("ApA")])
                            yield
                            mm1(fsl(btT, hl), fsl(alT, hl), ["alT", "btT"])
                            kb.tt("dve", chn["BpA"][:, hl, :], pw, mup, ALU.mult, r=[P_, "C"], w=[rn("BpA")])
                            yield
                            mm1(fsl(ktT, hl), fsl(alT, hl), ["alT", "ktT"])
                            kb.tt("dve", AakT[:, hl, :], pw, mup, ALU.mult, r=[P_, "C"], w=[rn("AakT")])
                            yield
                            mm1(fsl(btT, hl), fsl(rbT, hl), ["rbT", "btT"])
                            kb.tt("dve", ArbT[:, hl, :], pw, minc, ALU.mult, r=[P_, "C"], w=[rn("ArbT")])
                            yield
                            mm1(fsl(ktT, hl), fsl(rbT, hl), ["rbT", "ktT"])
                            kb.tt("dve", ArkT[:, hl, :], pw, minc, ALU.mult, r=[P_, "C"], w=[rn("ArkT")])
                            yield
                            kb.tt("pool", chn["TTa"][:, hl, :], chn["BpA"][:, hl, :], ident, ALU.add, r=[rn("BpA"), "C"], w=[rn("TTa")])
                            Ap, Bp, TT = "ApA", "BpA", "TTa"
                            for it in range(n_it):
                                Ap2 = "ApB" if Ap == "ApA" else "ApA"
                                Bp2 = "BpB" if Bp == "BpA" else "BpA"
                                TT2 = "TTb" if TT == "TTa" else "TTa"
                                mm1(chn[Bp][:, hl, :], chn[Ap][:, hl, :], [rn(Ap), rn(Bp)])
                                kb.cp("act", chn[Ap2][:, hl, :], pw, r=[P_], w=[rn(Ap2)])
                                yield
                                if it < n_it - 1:
                                    mm1(chn[Ap][:, hl, :], chn[Bp][:, hl, :], [rn(Ap), rn(Bp)])
                                    kb.cp("act", chn[Bp2][:, hl, :], pw, r=[P_], w=[rn(Bp2)])
                                    yield
                                mm1(chn[Ap2][:, hl, :], chn[TT][:, hl, :], [rn(Ap2), rn(TT)])
                                kb.tt("dve", chn[TT2][:, hl, :], pw, chn[TT][:, hl, :], ALU.add, r=[P_, rn(TT)], w=[rn(TT2)])
                                yield
                                Ap, Bp, TT = Ap2, Bp2, TT2
                            TTh = chn[TT][:, hl, :]
                            mm1(TTh, tsl(AL, hl), [rn(TT), "AL"], cols=64)
                            kb.cp("act", Ahat[:, hl, :], ps[B_][:, 0:64], r=[P_], w=[rn("Ahat")])
                            yield
                            mm1(AakT[:, hl, :], tsl(Vv, hl), [rn("AakT"), "Vv"], cols=64)
                            kb.cp("dve", X1[:, hl, :], ps[B_][:, 0:64], r=[P_], w=[rn("X1")])
                            yield
                            mm1(TTh, X1[:, hl, :], [rn(TT), rn("X1")], cols=64)
                            kb.cp("act", U0[:, hl, :], ps[B_][:, 0:64], r=[P_], w=[rn("U0")])
                            yield
                            mm1(tsl(RB, hl), identR[:], ["RB", "identR"], rows=64, last=False)
                            mm1(Ahat[:, hl, :], ArbT[:, hl, :], [rn("Ahat"), rn("ArbT")], rows=64, first=False)
                            kb.cp("dve", RhT[:, hl, :], ps[B_][0:64, 0:128], r=[P_], w=[rn("RhT")])
                            yield
                            if sample:
                                return
                            mm1(Ahat[:, hl, :], tsl(BH, hl), [rn("Ahat"), "BH"], cols=64, rows=64)
                            kb.stt(Gm[:, hl, :], ident[0:64, 0:64], PLc[:, hl:hl + 1], ps[B_][0:64, 0:64], ALU.mult, ALU.add, r=["C", "tA1", P_], w=[rn("Gm")])
                            yield
                            mm1(ArkT[:, hl, :], tsl(Vv, hl), [rn("ArkT"), "Vv"], cols=64, last=False)
                            mm1(ArbT[:, hl, :], U0[:, hl, :], [rn("ArbT"), rn("U0")], cols=64, first=False, last=False)
                            mm1(RhT[:, hl, :], S0T[:, hl, :], [rn("RhT"), rn("S0T")], cols=64, first=False)
                            kb.cp("act", yb[:, hl * 64:(hl + 1) * 64], ps[B_][:, 0:64], r=[P_], w=["tA1"])
                            yield
                            if i == 15:
                                mm1(tsl(Vv, hl), tsl(KH, hl), ["Vv", "KH"], cols=64, rows=64, last=False)
                                mm1(U0[:, hl, :], tsl(BH, hl), [rn("U0"), "BH"], cols=64, rows=64, first=False, last=False)
                                mm1(S0T[:, hl, :], Gm[:, hl, :], [rn("S0T"), rn("Gm")], cols=64, rows=64, first=False)
                                kb.cp("dve", SLo[:, hl, :], ps[B_][0:64, 0:64], r=[P_], w=["S0nat"])
                                yield
                            mm1(tsl(KH, hl), tsl(Vv, hl), ["Vv", "KH"], cols=64, rows=64, last=False)
                            mm1(tsl(BH, hl), U0[:, hl, :], [rn("U0"), "BH"], cols=64, rows=64, first=False, last=False)
                            mm1(Gm[:, hl, :], S0T[:, hl, :], [rn("S0T"), rn("Gm")], cols=64, rows=64, first=False)
                            kb.cp("dve", S0T[:, hl, :], ps[B_][0:64, 0:64], r=[P_], w=[rn("S0T")])
                            yield

                        gens = [head_alg(hl) for hl in range(4)]
                        while gens:
                            for g_ in list(gens):
                                try:
                                    next(g_)
                                except StopIteration:
                                    gens.remove(g_)
                        if not sample:
                            if i == 15:
                                kb.dma("sp", o_pwkv[hg * 4:hg * 4 + 4].rearrange("h v k -> v h k"), SLo[:, 0:4, :], r=["S0nat"], w=())
                        else:
                            sample_states()
                        if hg == 0 and i == 0:
                            stop_at(57)
                        if hg == 0 and i == 16:
                            stop_at(60)
                        for hl in range(4):
                            s.add("dve", lambda g_, hl=hl: g_.bn_stats(out=small[:, 8 + hl * 6:14 + hl * 6], in_=yb[:, hl * 64:(hl + 1) * 64]), r=["tA1"], w=["tA1"], tag="bnst")
                        for hl in range(4):
                            s.add("dve", lambda g_, hl=hl: g_.bn_aggr(out=small[:, 32 + hl * 2:34 + hl * 2], in_=small[:, 8 + hl * 6:14 + hl * 6]), r=["tA1"], w=["tA1"], tag="bnag")
                        mvv = small[:, 32:40].rearrange("p (h two) -> p h two", two=2)
                        kb.act(small[:, 40:44].unsqueeze(2), mvv[:, :, 1:2], AF.Sqrt, r=["tA1"], w=["tA1"], bias=64e-5, scale=1.0)
                        s.add("dve", lambda g_: g_.reciprocal(out=small[:, 40:44], in_=small[:, 40:44]), r=["tA1"], w=["tA1"], tag="recip")
                        kb.tt("dve", small[:, 44:48].unsqueeze(2), mvv[:, :, 0:1], small[:, 40:44].unsqueeze(2), ALU.mult, r=["tA1"], w=["tA1"])
                        kb.ts("dve", small[:, 44:48], small[:, 44:48], -1.0, None, ALU.mult, None, r=["tA1"], w=["tA1"])
                        for hl in range(4):
                            kb.act(yb[:, hl * 64:(hl + 1) * 64], yb[:, hl * 64:(hl + 1) * 64], AF.Identity, r=["tA1", "tA1"], w=["tA1"],
                                   bias=small[:, 44 + hl:45 + hl], scale=small[:, 40 + hl:41 + hl])
                        kb.tt("pool", yb[:], yb[:], bcs["lngb"][:], ALU.mult, r=["tA1", "lngb"], w=["tA1"])
                        kb.tt("pool", yb[:], yb[:], bcs["lnbb"][:], ALU.add, r=["tA1", "lnbb"], w=["tA1"])
                        kb.tt("dve", Tt.rearrange("p (h k) -> p h k", k=64), Vv[:].bitcast(F32).rearrange("p (h k) -> p h k", k=64),
                              small[:, 4:8].unsqueeze(2).to_broadcast([128, 4, 64]), ALU.mult, r=["Vv", "tA1"], w=[tbres(7)])
                        kb.tt("dve", yb[:], yb[:], Tt, ALU.add, r=["tA1", tbres(7)], w=["tA1"])
                        kb.tt("dve", yb[:], yb[:], Gt, ALU.mult, r=["tA1", "tA0"], w=["tA1"])
                        for cc in range(2):
                            kb.tr(ps[7][:, cc * 128:(cc + 1) * 128], yb[:, cc * 128:(cc + 1) * 128], ident, r=["tA1", "C"], w=[PS[7]], sig=(cc == 1))
                        kb.cp("act", ygT[:, 2 * hg:2 * hg + 2, i * 128:(i + 1) * 128], ps[7][:, 0:256].rearrange("p (c t) -> p c t", t=128), r=[PS[7]], w=["ygT%d" % i])
                    s.barrier()
            for c in range(8):
                kb.tr(ps[0][0:17, c * 128:(c + 1) * 128] if c < 4 else ps[1][0:17, (c - 4) * 128:(c - 3) * 128], hlast[:, c, :], ident, r=["hlast", "C"],
                      w=[PS[0] if c < 4 else PS[1]], sig=(c in (3, 7)))
            kb.cp("dve", tB[0][0:17, 0:512], ps[0][0:17, :], r=[PS[0]], w=["tB0"])
            kb.cp("dve", tB[0][0:17, 512:1024], ps[1][0:17, :], r=[PS[1]], w=["tB0"])
            kb.dma("sp", o_pshift, tB[0][0:1, :], r=["tB0"], w=())
            kb.dma("sp", o_sshift, tB[0][1:17, :], r=["tB0"], w=())
            s.barrier()
            with contextlib.ExitStack() as pc_:
                alloc_gl(pc_)
                mod_prepare(l, 1, 1.0, blocks=[4, 5])
                Gp, Gs = gl["Gp"], gl["Gs"]
                wout = kb.sb(pc_, "wo_sb", [128, 8, D], BF16)
                kb.dma("pool", wout[:], rw_wo[0].rearrange("(kc p) n -> p kc n", p=128), r=(), w=["wout"])
                proj_acc(wout, "wout", 8, lambda i, kc: (ygT[:, kc, i * 128:(i + 1) * 128], "ygT%d" % i), True, True)
                s.barrier()
        s.barrier()

    def dump_and_finish():
        for i in range(NT):
            kb.dma("sp", yout[i * 128:(i + 1) * 128, :], X[:, i, :], r=["X%d" % i], w=())

    stage = 0
    for l in range(2):
        for sub in range(3):
            if sub == 0:
                ffn(l, 0, 0, 0.5)
            elif sub == 2:
                ffn(l, 1, 2, 0.5)
            elif l == 0:
                ab_mixer(l)
            else:
                rwkv_mixer(l)
            stage += 1
            if stage >= upto:
                return dump_and_finish()
    dump_and_finish()


def build(upto=99, stop_point=None):
    kb = KB()
    kb.stop_point = stop_point
    with contextlib.ExitStack() as es:
        kb.es = es
        build_program(kb, upto)
        kb.s.emit(kb.nc, es)
    return kb


def core_inputs(inp, core, kb):
    sl = slice(16 * core, 16 * core + 16)
    xs = inp["x_sample"][sl].reshape(128, D)
    f32 = np.float32

    def fm(v):
        return np.ascontiguousarray(np.asarray(v, f32).reshape(-1, 128).T)

    m = {
        "x": np.concatenate([inp["x_prompt"][core], xs], axis=0),
        "c": np.concatenate([inp["c_prompt"][core:core + 1], inp["c_sample"][sl]], axis=0),
        "cst": CST_ARR,
    }
    cw = inp["rg_conv_w"][0]
    vecA = np.zeros((128, 32), f32)
    for c in range(4):
        for j in range(4):
            vecA[:, c * 4 + j] = cw[j, c * 128:(c + 1) * 128]
    vecA[:, 16:20] = fm(inp["rg_conv_b"][0])
    vecA[:, 20:24] = fm(inp["rg_b_a"][0])
    vecA[:, 24:28] = fm(inp["rg_b_x"][0])
    vecA[:, 28:32] = fm(inp["rg_lambda"][0])
    m["vecA"] = vecA
    m["bgT"] = np.ascontiguousarray(inp["mlstm_b_gates"][0].T)
    m["minitT"] = np.ascontiguousarray(inp["state_mlstm_m"][0, sl].T)
    m["smC"] = inp["state_mlstm_C"][0, sl]
    m["smn"] = inp["state_mlstm_n"][0, sl]
    m["srh"] = inp["state_rglru_h"][0, sl]
    m["srconv"] = inp["state_rglru_conv"][0, sl].reshape(48, 512)
    mu = inp["rw_mu"][0]
    muT = np.zeros((128, 48), f32)
    for j in range(6):
        muT[:, j * 8:(j + 1) * 8] = fm(mu[j])
    m["muT"] = muT
    m["rk_flat"] = inp["rw_r_k"].reshape(1, D)
    m["swkv"] = inp["state_rwkv_wkv"][0, sl]
    m["sshift"] = inp["state_rwkv_shift"][0, sl]
    for k in kb.dram:
        if k not in m and k in inp:
            m[k] = inp[k]
    return {k: np.ascontiguousarray(v, dtype=f32) for k, v in m.items() if k in kb.dram}


_CACHE = {}


def kernel(**inputs):
    inp = {k: np.asarray(v) for k, v in inputs.items()}
    if "kb" not in _CACHE:
        _CACHE["kb"] = build()
    kb = _CACHE["kb"]
    in_maps = [core_inputs(inp, c, kb) for c in range(NCORES)]
    res = run_bass_kernel_spmd(kb.nc, in_maps, core_ids=list(range(NCORES)))
    R = res.results
    f32 = np.float32
    cat = lambda key, f=(lambda a: a): np.stack([f(np.asarray(R[c][key], f32)) for c in range(NCORES)], axis=0)
    cats = lambda key, f=(lambda a: a): np.concatenate([f(np.asarray(R[c][key], f32)) for c in range(NCORES)], axis=0)
    y_prompt = cat("y", lambda a: a[:2048])
    y_sample = cats("y", lambda a: a[2048:].reshape(16, 8, D))
    outs = (
        y_prompt, y_sample,
        cat("o_pmC")[None], cat("o_pmn")[None], cat("o_pmm", lambda a: a[:, 0])[None],
        cat("o_prh", lambda a: a.reshape(512))[None], cat("o_prconv")[None],
        cat("o_pwkv")[None], cat("o_pshift", lambda a: a[0])[None],
        cats("o_smC")[None], cats("o_smn")[None], cats("o_smm", lambda a: a.T)[None],
        cats("o_srh")[None], cats("o_srconv", lambda a: a.reshape(16, 3, 512))[None],
        cats("o_swkv")[None], cats("o_sshift")[None],
    )
    return tuple(np.ascontiguousarray(o, dtype=f32) for o in outs)
```

```python
import contextlib
import numpy as np
import concourse.bass as bass
import concourse.mybir as mybir
from concourse.bass_utils import run_bass_kernel_spmd

F32 = mybir.dt.float32
BF16 = mybir.dt.bfloat16
F32R = mybir.dt.float32r
AF = mybir.ActivationFunctionType
ALU = mybir.AluOpType
AX = mybir.AxisListType

D = 1024
DFF = 2816
NT = 17
NTOK = NT * 128
ALPHA = 4.0 ** 0.25
LN_EPS = 1e-5
NCORES = 8


class Op:
    __slots__ = ("eng", "fn", "deps", "sig", "idx", "dma", "slot", "slot_total", "sigcount", "waits", "tag")


class Sched:
    ENGS = ["pe", "act", "dve", "pool", "sp"]

    def __init__(self, n_slots=40):
        self.q = {e: [] for e in self.ENGS}
        self.last_w = {}
        self.readers = {}
        self.n_slots = n_slots
        self.slot_rr = 0
        self.sw_rr = 0
        self.n_hw = n_slots - 12
        self.slot_total = [0] * n_slots
        self.slot_last = [None] * n_slots
        self.all_dma = []

    skip = False

    def add(self, eng, fn, r=(), w=(), sig=True, dma=False, tag=""):
        if self.skip:
            return None
        op = Op()
        op.eng, op.fn, op.sig, op.dma, op.tag = eng, fn, sig, dma, tag
        op.slot = None
        deps = []
        seen = set()

        def dep(o):
            if o is None or id(o) in seen:
                return
            seen.add(id(o))
            if (not dma) and eng == "pe" and o.eng == "pe" and not o.dma:
                return
            deps.append(o)

        for k in r:
            dep(self.last_w.get(k))
        for k in w:
            dep(self.last_w.get(k))
            for o in self.readers.get(k, {}).values():
                dep(o)
        if dma:
            if eng == "pool":
                slot = self.n_hw + (self.sw_rr % (self.n_slots - self.n_hw))
                self.sw_rr += 1
            else:
                slot = self.slot_rr % self.n_hw
                self.slot_rr += 1
            dep(self.slot_last[slot])
            self.slot_total[slot] += 16
            op.slot = slot
            op.slot_total = self.slot_total[slot]
            self.slot_last[slot] = op
            self.all_dma.append(op)
        op.deps = deps
        self.q[eng].append(op)
        op.idx = len(self.q[eng]) - 1
        key = ("dma", id(op)) if dma else eng
        for k in r:
            self.readers.setdefault(k, {})[key] = op
        for k in w:
            self.last_w[k] = op
            self.readers[k] = {}
        return op

    def barrier(self):
        if self.skip:
            return
        lasts = []
        for e in self.ENGS:
            comp = [o for o in self.q[e] if (not o.dma) and o.fn is not None]
            if comp:
                comp[-1].sig = True
                lasts.append(comp[-1])
        lasts += [o for o in self.slot_last if o is not None]
        for e in self.ENGS:
            op = Op()
            op.eng, op.sig, op.dma, op.tag, op.slot, op.fn = e, False, False, "barrier", None, None
            op.deps = list(lasts)
            self.q[e].append(op)
            op.idx = len(self.q[e]) - 1
        self.last_w = {}
        self.readers = {}

    def finalize(self):
        for e in self.ENGS:
            for o in reversed(self.q[e]):
                if not o.dma and o.fn is not None and e != "sp":
                    o.sig = True
                    break
        self.sigtot = {}
        for e in self.ENGS:
            cnt = 0
            ops = self.q[e]
            pref = []
            for o in ops:
                if (not o.dma) and o.sig:
                    cnt += 1
                pref.append(cnt)
            self.sigtot[e] = cnt
            nxt = None
            for i in range(len(ops) - 1, -1, -1):
                o = ops[i]
                if (not o.dma) and o.sig:
                    nxt = pref[i]
                o.sigcount = nxt if not o.dma else None
        for e in self.ENGS:
            known = {}
            for o in self.q[e]:
                need = {}
                for d in o.deps:
                    if d.dma:
                        key, val = ("slot", d.slot), d.slot_total
                    else:
                        if d.sigcount is None:
                            raise RuntimeError("dependency on op with no later signal: %s" % d.tag)
                        key, val = ("eng", d.eng), d.sigcount
                    if val > need.get(key, 0):
                        need[key] = val
                o.waits = []
                for key, val in need.items():
                    if known.get(key, 0) < val:
                        known[key] = val
                        o.waits.append((key, val))

    def simulate(self):
        pc = {e: 0 for e in self.ENGS}
        sem = {}
        sigc = {e: 0 for e in self.ENGS}
        progress = True
        while progress:
            progress = False
            for e in self.ENGS:
                while pc[e] < len(self.q[e]):
                    o = self.q[e][pc[e]]
                    ok = all(sem.get(k, 0) >= v for k, v in o.waits)
                    if not ok:
                        break
                    if o.dma:
                        sem[("slot", o.slot)] = sem.get(("slot", o.slot), 0) + 16
                    elif o.sig:
                        sem[("eng", e)] = sem.get(("eng", e), 0) + 1
                    pc[e] += 1
                    progress = True
        stuck = {e: (pc[e], len(self.q[e])) for e in self.ENGS if pc[e] < len(self.q[e])}
        if stuck:
            msg = []
            for e, (p, n) in stuck.items():
                o = self.q[e][p]
                msg.append("%s stuck at %d/%d tag=%s waits=%s" % (e, p, n, o.tag, [(k, v, sem.get(k, 0)) for k, v in o.waits]))
            raise RuntimeError("DEADLOCK in wait graph:\n" + "\n".join(msg))

    def emit(self, nc, es):
        self.finalize()
        self.simulate()
        engsem = {e: es.enter_context(nc.semaphore("sem_" + e)) for e in ["pe", "act", "dve", "pool"]}
        slotsem = [es.enter_context(nc.semaphore("slot%d" % i)) for i in range(self.n_slots)]

        def semof(key):
            return engsem[key[1]] if key[0] == "eng" else slotsem[key[1]]

        def run(e, g):
            for o in self.q[e]:
                for key, val in o.waits:
                    g.wait_ge(semof(key), val)
                if o.fn is None:
                    continue
                ins = o.fn(g)
                if o.dma:
                    ins.then_inc(slotsem[o.slot], 16)
                elif o.sig:
                    ins.then_inc(engsem[e], 1)
            if e == "sp":
                for s in range(self.n_slots):
                    if self.slot_total[s] > 0:
                        g.wait_ge(slotsem[s], self.slot_total[s])

        with nc.Block() as blk:
            blk.tensor(lambda g: run("pe", g))
            blk.scalar(lambda g: run("act", g))
            blk.vector(lambda g: run("dve", g))
            blk.gpsimd(lambda g: run("pool", g))
            blk.sync(lambda g: run("sp", g))


class KB:
    def __init__(self, stop_after=None, debug=False):
        self.nc = bass.Bass("TRN2", target_bir_lowering=False)
        self.s = Sched()
        self.stop_after = stop_after
        self.debug = debug
        self.dram = {}

    def din(self, name, shape, dt=F32):
        t = self.nc.dram_tensor(name, list(shape), dt, kind="ExternalInput")
        self.dram[name] = t
        return t.ap()

    def dout(self, name, shape, dt=F32):
        t = self.nc.dram_tensor(name, list(shape), dt, kind="ExternalOutput")
        self.dram[name] = t
        return t.ap()

    def sb(self, es, name, shape, dt=F32):
        self.uid = getattr(self, "uid", 0) + 1
        return es.enter_context(self.nc.sbuf_tensor("%s_%d" % (name, self.uid), list(shape), dt))

    def mm(self, out, lhsT, rhs, start, stop, r, w, sig=None, tag="mm"):
        if sig is None:
            sig = stop
        return self.s.add("pe", lambda g: g.matmul(out, lhsT=lhsT, rhs=rhs, start=start, stop=stop), r=r, w=w, sig=sig, tag=tag)

    def tr(self, out, in_, ident, r, w, sig=True, tag="tr"):
        return self.s.add("pe", lambda g: g.transpose(out, in_, ident), r=r, w=w, sig=sig, tag=tag)

    def act(self, out, in_, func, r, w, bias=None, scale=None, eng="act", tag="act"):
        kw = {}
        if bias is not None:
            kw["bias"] = bias
        if scale is not None:
            kw["scale"] = scale
        return self.s.add("act", lambda g: g.activation(out=out, in_=in_, func=func, **kw), r=r, w=w, tag=tag)

    def tt(self, eng, out, in0, in1, op, r, w, tag="tt"):
        return self.s.add(eng, lambda g: g.tensor_tensor(out=out, in0=in0, in1=in1, op=op), r=r, w=w, tag=tag)

    def ts(self, eng, out, in0, s1, s2, op0, op1, r, w, tag="ts"):
        if op1 is None:
            return self.s.add(eng, lambda g: g.tensor_scalar(out=out, in0=in0, scalar1=s1, scalar2=None, op0=op0), r=r, w=w, tag=tag)
        return self.s.add(eng, lambda g: g.tensor_scalar(out=out, in0=in0, scalar1=s1, scalar2=s2, op0=op0, op1=op1), r=r, w=w, tag=tag)

    def stt(self, out, in0, scalar, in1, op0, op1, r, w, tag="stt"):
        return self.s.add("dve", lambda g: g.scalar_tensor_tensor(out=out, in0=in0, scalar=scalar, in1=in1, op0=op0, op1=op1), r=r, w=w, tag=tag)

    def cp(self, eng, out, in_, r, w, tag="cp"):
        if eng == "act":
            return self.s.add("act", lambda g: g.copy(out=out, in_=in_), r=r, w=w, tag=tag)
        return self.s.add(eng, lambda g: g.tensor_copy(out=out, in_=in_), r=r, w=w, tag=tag)

    def memset(self, eng, ap, val, w, tag="memset"):
        return self.s.add(eng, lambda g: g.memset(ap, val), r=(), w=w, tag=tag)

    def dma(self, q, out, in_, r, w, tag="dma", **kw):
        return self.s.add(q, lambda g: g.dma_start(out=out, in_=in_, **kw), r=r, w=w, dma=True, tag=tag)


def make_consts():
    c = {}
    c["ident"] = np.eye(128, dtype=np.float32)
    selP = np.zeros((128, 128), np.float32)
    selP[0, :] = 1.0
    selS = np.zeros((128, 128), np.float32)
    for p in range(128):
        selS[1 + p // 8, p] = 1.0
    c["selP"] = selP
    c["selS"] = selS
    st = np.arange(128)
    c["maskP"] = (st[:, None] <= st[None, :]).astype(np.float32)
    c["maskS"] = ((st[:, None] <= st[None, :]) & (st[:, None] // 8 == st[None, :] // 8)).astype(np.float32)
    c["rst"] = np.tile((st % 8 != 0).astype(np.float32)[None, :], (128, 1))
    c["rstm"] = np.tile(np.where(st % 8 == 0, -1e30, 0.0).astype(np.float32)[None, :], (128, 1))
    bms = np.zeros((128, 128), np.float32)
    bms[st, st // 8] = 1.0
    c["bms"] = bms
    c["ones"] = np.ones((128, 128), np.float32)
    same = (st[:, None] // 8 == st[None, :] // 8)
    c["upP"] = (st[:, None] < st[None, :]).astype(np.float32)
    c["lowP"] = (st[:, None] > st[None, :]).astype(np.float32)
    c["upS"] = ((st[:, None] < st[None, :]) & same).astype(np.float32)
    c["lowS"] = ((st[:, None] > st[None, :]) & same).astype(np.float32)
    c["blkS"] = same.astype(np.float32)
    blk = lambda b: (st[:, None] // b == st[None, :] // b)
    c["low16"] = ((st[:, None] > st[None, :]) & blk(16)).astype(np.float32)
    c["up16"] = ((st[:, None] < st[None, :]) & blk(16)).astype(np.float32)
    for b in (16, 32, 64):
        c["m%d" % b] = (blk(2 * b) & ~blk(b)).astype(np.float32)
    names = list(c.keys())
    arr = np.concatenate([c[k] for k in names], axis=1)
    offs = {}
    o = 0
    for k in names:
        offs[k] = o
        o += c[k].shape[1]
    return arr, offs


CST_ARR, CST_OFF = make_consts()
NCST = CST_ARR.shape[1]

FFN_PARTS = [(0, 4), (4, 4), (8, 4), (12, 4), (16, 4), (20, 2)]
TGS = [(0, 512), (512, 512), (1024, 512), (1536, 512), (2048, 128)]


def build_program(kb, upto=99):
    nc, s = kb.nc, kb.s
    es = kb.es
    xin = kb.din("x", [NTOK, D])
    cin = kb.din("c", [NT, D])
    cst = kb.din("cst", [128, NCST])
    ada_w = kb.din("ada_w", [2, D, 9 * D])
    ada_b = kb.din("ada_b", [2, 9 * D])
    ln_g = kb.din("ln_g", [2, 3, D])
    ln_b = kb.din("ln_b", [2, 3, D])
    ffn_w1 = kb.din("ffn_w1", [2, 2, D, DFF])
    ffn_w3 = kb.din("ffn_w3", [2, 2, D, DFF])
    ffn_w2 = kb.din("ffn_w2", [2, 2, DFF, D])
    yout = kb.dout("y", [NTOK, D])

    X = kb.sb(es, "X", [128, NT, D], F32)
    C = kb.sb(es, "cst_sb", [128, NCST], F32)
    cT = kb.sb(es, "cT", [128, 8, NT], BF16)
    onesb = kb.sb(es, "onesb", [1, 32], F32)
    modT = kb.sb(es, "modT", [128, 16, NT], F32)
    gl = {}

    def alloc_gl(stack):
        gl["Gp"] = kb.sb(stack, "Gp", [128, D], F32)
        gl["Gs"] = kb.sb(stack, "Gs", [128, D], F32)
        gl["LNg"] = kb.sb(stack, "LNg", [128, D], F32)
        gl["LNb"] = kb.sb(stack, "LNb", [128, D], F32)
    tA = [kb.sb(es, "tA%d" % i, [128, 512], F32) for i in range(4)]
    tB = [kb.sb(es, "tB%d" % i, [128, D], F32) for i in range(2)]
    stt_ = [kb.sb(es, "bnst%d" % i, [128, 2, 6], F32) for i in range(2)]
    mv = [kb.sb(es, "mv%d" % i, [128, 2], F32) for i in range(2)]
    rstd = [kb.sb(es, "rstd%d" % i, [128, 1], F32) for i in range(2)]
    nmr = [kb.sb(es, "nmr%d" % i, [128, 1], F32) for i in range(2)]
    tmpS = kb.sb(es, "tmpS", [128, 128], F32)
    ps = [es.enter_context(nc.psum_tensor("ps%d" % i, [128, 512], F32)) for i in range(8)]
    PS = ["ps%d" % i for i in range(8)]
    psb = [p.bitcast(BF16) for p in ps]

    ident = C[:, CST_OFF["ident"]:CST_OFF["ident"] + 128]
    selP = C[0:NT, CST_OFF["selP"]:CST_OFF["selP"] + 128]
    selS = C[0:NT, CST_OFF["selS"]:CST_OFF["selS"] + 128]

    kb.dma("sp", C[:], cst, r=(), w=["C"])
    for i in range(NT):
        kb.dma("sp", X[:, i, :], xin[i * 128:(i + 1) * 128, :], r=(), w=["X%d" % i])
    kb.memset("pool", onesb[:], 1.0, w=["onesb"])

    with contextlib.ExitStack() as ph0:
        c_sb = kb.sb(ph0, "c_sb", [NT, D], F32)
        cs_sb = kb.sb(ph0, "cs_sb", [NT, D], F32)
        kb.dma("sp", c_sb[:], cin, r=(), w=["c_sb"])
        kb.act(cs_sb[:], c_sb[:], AF.Silu, r=["c_sb"], w=["cs_sb"])
        for kc in range(8):
            kb.tr(ps[0][:, kc * NT:(kc + 1) * NT], cs_sb[0:NT, kc * 128:(kc + 1) * 128], C[0:NT, 0:NT],
                  r=["cs_sb", "C"], w=[PS[0]], sig=(kc == 7))
        kb.cp("dve", cT[:].rearrange("p a b -> p (a b)"), ps[0][:, 0:8 * NT], r=[PS[0]], w=["cT"])
    s.barrier()

    state = {"ada_i": 0, "ada_i2": 0, "psr": 0}

    def mod_prepare(l, sub, res_w, blocks=range(6)):
        if 5 in blocks:
            Gp, Gs = gl["Gp"], gl["Gs"]
            kb.dma("sp", gl["LNg"][:], ln_g[l, sub:sub + 1, :].to_broadcast([128, D]), r=(), w=["LNg"])
            kb.dma("sp", gl["LNb"][:], ln_b[l, sub:sub + 1, :].to_broadcast([128, D]), r=(), w=["LNb"])
        phm = contextlib.ExitStack()
        modst = [kb.sb(phm, "modst%d" % i, [NT, 512], F32) for i in range(2)]
        adaw = [kb.sb(phm, "adaw%d" % i, [128, 8, 256], BF16) for i in range(2)]
        adab = [kb.sb(phm, "adab%d" % i, [1, 512], F32) for i in range(2)]
        for b in blocks:
            i = state["ada_i"]
            state["ada_i"] += 1
            buf = i % 2
            co = sub * 3 * D + b * 512
            kb.dma("sp", adab[buf][:], ada_b[l:l + 1, co:co + 512], r=(), w=["adab%d" % buf])
            pm = 4 + (i % 2)
            for sbk in range(2):
                i2 = state["ada_i2"]
                state["ada_i2"] += 1
                wb = i2 % 2
                kb.dma("pool", adaw[wb][:], ada_w[l].rearrange("(kc p) n -> p kc n", p=128)[:, :, co + sbk * 256:co + (sbk + 1) * 256],
                       r=(), w=["adaw%d" % wb])
                for kc in range(8):
                    kb.mm(ps[pm][0:NT, sbk * 256:(sbk + 1) * 256], lhsT=cT[:, kc, :], rhs=adaw[wb][:, kc, :], start=(kc == 0), stop=False,
                          r=["cT", "adaw%d" % wb], w=[PS[pm]], sig=False)
                kb.mm(ps[pm][0:NT, sbk * 256:(sbk + 1) * 256], lhsT=onesb[0:1, 0:NT], rhs=adab[buf][:, sbk * 256:(sbk + 1) * 256], start=False, stop=True,
                      r=["onesb", "adab%d" % buf], w=[PS[pm]], sig=True)
            kb.cp("act", modst[buf][:], ps[pm][0:NT, :], r=[PS[pm]], w=["modst%d" % buf])
            if b < 4:
                for cc in range(4):
                    j = b * 4 + cc
                    kb.tr(ps[6][:, j * NT:(j + 1) * NT], modst[buf][0:NT, cc * 128:(cc + 1) * 128], C[0:NT, 0:NT],
                          r=["modst%d" % buf, "C"], w=[PS[6]], sig=(cc == 3))
                if b == 1:
                    kb.cp("dve", modT[:, 0:8, :].rearrange("p a b -> p (a b)"), ps[6][:, 0:8 * NT], r=[PS[6]], w=["modT"])
                if b == 3:
                    kb.ts("dve", modT[:, 8:16, :].rearrange("p a b -> p (a b)"), ps[6][:, 8 * NT:16 * NT], 1.0, None,
                          ALU.add, None, r=[PS[6]], w=["modT"])
            else:
                h = b - 4
                kb.mm(ps[7][:, :], lhsT=selP, rhs=modst[buf][:], start=True, stop=True, r=["C", "modst%d" % buf], w=[PS[7]])
                kb.act(Gp[:, h * 512:(h + 1) * 512], ps[7][:, :], AF.Identity, r=[PS[7]], w=["Gp"], bias=float(res_w), scale=float(res_w))
                kb.mm(ps[7][:, :], lhsT=selS, rhs=modst[buf][:], start=True, stop=True, r=["C", "modst%d" % buf], w=[PS[7]])
                kb.act(Gs[:, h * 512:(h + 1) * 512], ps[7][:, :], AF.Identity, r=[PS[7]], w=["Gs"], bias=float(res_w), scale=float(res_w))
        s.barrier()
        phm.close()

    def make_hT(hT, i, prescale=True, col0=None, res=None, hook=None):
        g = res if res is not None else "hT_g%d" % (i // 4)
        if col0 is None:
            col0 = i * 128
        for half in range(2):
            pb = state["psr"] % 4
            state["psr"] += 1
            for cc in range(4):
                c = half * 4 + cc
                kb.tr(ps[pb][:, cc * 128:(cc + 1) * 128], X[:, i, c * 128:(c + 1) * 128], ident,
                      r=["X%d" % i, "C"], w=[PS[pb]], sig=(cc == 3))
            for cc in range(4):
                c = half * 4 + cc
                src = ps[pb][:, cc * 128:(cc + 1) * 128]
                if hook is not None:
                    hook(i, c, src, PS[pb])
                if i < 16:
                    if cc % 2 == 0:
                        kb.act(hT[:, c, col0:col0 + 128], src, AF.Identity, r=[PS[pb], "modT"], w=[g],
                               bias=modT[:, c, 0:1], scale=modT[:, 8 + c, 0:1])
                    else:
                        kb.ts("dve", hT[:, c, col0:col0 + 128], src, modT[:, 8 + c, 0:1], modT[:, c, 0:1],
                              ALU.mult, ALU.add, r=[PS[pb], "modT"], w=[g])
                else:
                    sc = modT[:, 8 + c, 1:NT].unsqueeze(2).to_broadcast([128, 16, 8])
                    sh = modT[:, c, 1:NT].unsqueeze(2).to_broadcast([128, 16, 8])
                    kb.tt("dve", tmpS[:].rearrange("p (q t) -> p q t", t=8), src.rearrange("p (q t) -> p q t", t=8), sc,
                          ALU.mult, r=[PS[pb], "modT"], w=["tmpS"])
                    kb.tt("dve", hT[:, c, col0:col0 + 128].rearrange("p (q t) -> p q t", t=8),
                          tmpS[:].rearrange("p (q t) -> p q t", t=8), sh, ALU.add, r=["tmpS", "modT"], w=[g])

    def layer_norm(i):
        k = i % 2
        for h in range(2):
            s.add("dve", lambda g_, h=h, k=k, i=i: g_.bn_stats(out=stt_[k][:, h, :], in_=X[:, i, h * 512:(h + 1) * 512]),
                  r=["X%d" % i], w=["bnst%d" % k], tag="bnstats")
        s.add("dve", lambda g_, k=k: g_.bn_aggr(out=mv[k][:], in_=stt_[k][:].rearrange("p a b -> p (a b)")),
              r=["bnst%d" % k], w=["mv%d" % k], tag="bnaggr")
        kb.act(rstd[k][:], mv[k][:, 1:2], AF.Sqrt, r=["mv%d" % k], w=["rstd%d" % k], bias=float(LN_EPS), scale=1.0)
        s.add("dve", lambda g_, k=k: g_.reciprocal(out=rstd[k][:], in_=rstd[k][:]), r=["rstd%d" % k], w=["rstd%d" % k], tag="recip")
        kb.ts("dve", nmr[k][:], mv[k][:, 0:1], rstd[k][:, 0:1], -1.0, ALU.mult, ALU.mult, r=["mv%d" % k, "rstd%d" % k], w=["nmr%d" % k])
        kb.act(tB[k][:], X[:, i, :], AF.Identity, r=["X%d" % i, "rstd%d" % k, "nmr%d" % k], w=["tB%d" % k],
               bias=nmr[k][:, 0:1], scale=rstd[k][:, 0:1])
        kb.tt("dve", tB[k][:], tB[k][:], gl["LNg"][:], ALU.mult, r=["tB%d" % k, "LNg"], w=["tB%d" % k])
        kb.tt("dve", X[:, i, :], tB[k][:], gl["LNb"][:], ALU.add, r=["tB%d" % k, "LNb"], w=["X%d" % i])

    pacc = {"v": 0}

    def proj_acc(wt, wres, nk, lhs_of, do_ln, first):
        Gp, Gs = gl["Gp"], gl["Gs"]
        for i in [16] + list(range(16)):
            if i == 0:
                kb.tt("dve", wt[:, 0:nk, :], wt[:, 0:nk, :], Gp[:].unsqueeze(1).to_broadcast([128, nk, D]), ALU.mult, r=[wres, "Gp"], w=[wres])
            for half in range(2):
                v = pacc["v"]
                pacc["v"] += 1
                py = 4 + (v % 4)
                for kc in range(nk):
                    lt, lres = lhs_of(i, kc)
                    kb.mm(ps[py][:, :], lhsT=lt, rhs=wt[:, kc, half * 512:(half + 1) * 512], start=(kc == 0), stop=(kc == nk - 1),
                          r=[lres, wres], w=[PS[py]])
                xs = X[:, i, half * 512:(half + 1) * 512]
                src = ps[py][:, :]
                rsrc = PS[py]
                if i == 16:
                    kb.tt("dve", tA[v % 4][:], ps[py][:, :], Gs[:, half * 512:(half + 1) * 512], ALU.mult, r=[PS[py], "Gs"], w=["tA%d" % (v % 4)])
                    src, rsrc = tA[v % 4][:], "tA%d" % (v % 4)
                if first:
                    kb.stt(xs, xs, float(ALPHA), src, ALU.mult, ALU.add, r=["X%d" % i, rsrc], w=["X%d" % i])
                else:
                    kb.tt("dve", xs, xs, src, ALU.add, r=["X%d" % i, rsrc], w=["X%d" % i])
            if do_ln:
                layer_norm(i)

    def ffn(l, f, sub, res_w):
        with contextlib.ExitStack() as ph:
            alloc_gl(ph)
            mod_prepare(l, sub, res_w)
            Gp, Gs = gl["Gp"], gl["Gs"]
            hT = kb.sb(ph, "hT", [128, 8, NTOK], BF16)
            w1p = kb.sb(ph, "w1p", [128, 8, 512], BF16)
            w3p = kb.sb(ph, "w3p", [128, 8, 512], BF16)
            w2p = kb.sb(ph, "w2p", [128, 4, D], BF16)
            gbuf = kb.sb(ph, "gbuf", [128, 4, NTOK], BF16)
            sil = [kb.sb(ph, "sil%d" % i, [128, 512], F32) for i in range(2)]
            w1v = ffn_w1[l, f].rearrange("(kc p) n -> p kc n", p=128)
            w3v = ffn_w3[l, f].rearrange("(kc p) n -> p kc n", p=128)
            w2v = ffn_w2[l, f].rearrange("(j p) n -> p j n", p=128)
            u = 0
            v = 0
            def load_up(pi_):
                j0_, n_ = FFN_PARTS[pi_]
                kb.dma("pool", w1p[:, :, 0:n_ * 128], w1v[:, :, j0_ * 128:(j0_ + n_) * 128], r=(), w=["w1p"])
                kb.dma("pool", w3p[:, :, 0:n_ * 128], w3v[:, :, j0_ * 128:(j0_ + n_) * 128], r=(), w=["w3p"])

            def load_dn(pi_):
                j0_, n_ = FFN_PARTS[pi_]
                kb.dma("pool", w2p[:, 0:n_, :], w2v[:, j0_:j0_ + n_, :], r=(), w=["w2p"])

            load_up(0)
            load_dn(0)
            for i in range(NT):
                make_hT(hT, i)
            for pi, (j0, ncn) in enumerate(FFN_PARTS):
                for tg, (t0, nt_) in enumerate(TGS):
                    for jj in range(ncn):
                        pa, pb = (2 * u) % 4, (2 * u + 1) % 4
                        for kc in range(8):
                            kb.mm(ps[pa][:, 0:nt_], lhsT=w1p[:, kc, jj * 128:(jj + 1) * 128], rhs=hT[:, kc, t0:t0 + nt_],
                                  start=(kc == 0), stop=(kc == 7), r=["w1p", "hT_g%d" % tg], w=[PS[pa]])
                        for kc in range(8):
                            kb.mm(ps[pb][:, 0:nt_], lhsT=w3p[:, kc, jj * 128:(jj + 1) * 128], rhs=hT[:, kc, t0:t0 + nt_],
                                  start=(kc == 0), stop=(kc == 7), r=["w3p", "hT_g%d" % tg], w=[PS[pb]])
                        kb.act(sil[u % 2][:, 0:nt_], ps[pa][:, 0:nt_], AF.Silu, r=[PS[pa]], w=["sil%d" % (u % 2)])
                        kb.tt("dve", gbuf[:, jj, t0:t0 + nt_], sil[u % 2][:, 0:nt_], ps[pb][:, 0:nt_], ALU.mult,
                              r=["sil%d" % (u % 2), PS[pb]], w=["g_g%d" % tg])
                        u += 1
                if pi + 1 < len(FFN_PARTS):
                    load_up(pi + 1)
                proj_acc(w2p, "w2p", ncn, lambda i, jj: (gbuf[:, jj, i * 128:(i + 1) * 128], "g_g%d" % (i // 4)), pi == len(FFN_PARTS) - 1, pi == 0)
                if pi + 1 < len(FFN_PARTS):
                    load_dn(pi + 1)
        s.barrier()

    DKS = float(128 ** -0.5)

    class _Stop(Exception):
        pass

    def stop_at(n):
        if getattr(kb, "stop_point", None) == n:
            s.barrier()
            s.skip = True

    def ab_mixer(l):
        ab_mixer_(l)
        s.skip = False
        s.barrier()

    def ab_mixer_(l):
        ab_w_in = kb.din("ab_w_in", [1, D, 3080])
        ab_w_out = kb.din("ab_w_out", [1, D, D])
        mnorm_g = kb.din("mlstm_norm_g", [1, 512])
        vecA_d = kb.din("vecA", [128, 32])
        bgT_d = kb.din("bgT", [4, 2])
        minitT_d = kb.din("minitT", [4, 16])
        rg_w_a = kb.din("rg_w_a", [1, 8, 64, 64])
        rg_w_x = kb.din("rg_w_x", [1, 8, 64, 64])
        smC = kb.din("smC", [16, 4, 128, 128])
        smn = kb.din("smn", [16, 4, 128])
        srh = kb.din("srh", [16, 512])
        srconv = kb.din("srconv", [48, 512])
        o_pmC = kb.dout("o_pmC", [4, 128, 128])
        o_pmn = kb.dout("o_pmn", [4, 128])
        o_pmm = kb.dout("o_pmm", [4, 1])
        o_prh = kb.dout("o_prh", [4, 128])
        o_prconv = kb.dout("o_prconv", [3, 512])
        o_smC = kb.dout("o_smC", [16, 4, 128, 128])
        o_smn = kb.dout("o_smn", [16, 4, 128])
        o_smm = kb.dout("o_smm", [4, 16])
        o_srh = kb.dout("o_srh", [16, 512])
        o_srconv = kb.dout("o_srconv", [48, 512])

        maskP = C[:, CST_OFF["maskP"]:CST_OFF["maskP"] + 128]
        maskS = C[:, CST_OFF["maskS"]:CST_OFF["maskS"] + 128]
        rst = C[:, CST_OFF["rst"]:CST_OFF["rst"] + 128]
        rstm = C[:, CST_OFF["rstm"]:CST_OFF["rstm"] + 128]
        bms = C[:, CST_OFF["bms"]:CST_OFF["bms"] + 16]
        ones = C[:, CST_OFF["ones"]:CST_OFF["ones"] + 128]

        mod_prepare(l, 1, 1.0, blocks=range(4))
        win_v = ab_w_in[0].rearrange("(kc p) n -> p kc n", p=128)
        with contextlib.ExitStack() as ph:
            hmT = kb.sb(ph, "hmT", [128, 4, NTOK], BF16)
            vecA = kb.sb(ph, "vecA_sb", [128, 32], F32)
            kb.dma("sp", vecA[:], vecA_d, r=(), w=["vecA"])
            sigo = tA[0]
            hmf = tB[0][:, 0:512]
            with contextlib.ExitStack() as pa:
                winA = kb.sb(pa, "winA", [128, 8, 2056], BF16)
                kb.dma("pool", winA[:, :, 0:1024], win_v[:, :, 0:1024], r=(), w=["winA"])
                kb.dma("pool", winA[:, :, 1024:2056], win_v[:, :, 1024:2056], r=(), w=["winA"])
                qkT = kb.sb(pa, "qkT", [128, 8, 512], BF16)
                hTg = [kb.sb(pa, "hTgA%d" % i, [128, 8, 512], BF16) for i in range(2)]
                bg = kb.sb(pa, "bg", [4, 2], F32)
                nbg1 = kb.sb(pa, "nbg1", [4, 1], F32)
                minitT = kb.sb(pa, "minitT_sb", [4, 16], F32)
                mng = kb.sb(pa, "mng", [128, 512], F32)
                kb.dma("sp", bg[:], bgT_d, r=(), w=["bg"])
                kb.dma("sp", minitT[:], minitT_d, r=(), w=["minitT"])
                kb.dma("sp", mng[:], mnorm_g[0:1, :].to_broadcast([128, 512]), r=(), w=["mng"])
                kb.ts("dve", nbg1[:], bg[:, 1:2], -1.0, None, ALU.mult, None, r=["bg"], w=["nbg1"])
                R4 = lambda nm: kb.sb(pa, nm, [4, 128], F32)
                t1, IGa, Rt, t3, t4 = R4("r_t1"), R4("r_ig"), R4("r_rt"), R4("r_t3"), R4("r_t4")
                Bc = [R4("r_bc0"), R4("r_bc1")]
                Mx = [R4("r_mx0"), R4("r_mx1")]
                dd = kb.sb(pa, "r_dd", [4, 16], F32)
                DDm = kb.sb(pa, "r_DD", [4, 64], F32)
                mout = kb.sb(pa, "r_mout", [4, 16], F32)
                colq = [kb.sb(pa, "colq%d" % i, [128, 16], F32) for i in range(2)]
                decsb = kb.sb(pa, "decsb", [128, 64], F32)
                kw = kb.sb(pa, "kw", [128, 4, 128], BF16)
                ktok = kb.sb(pa, "ktok", [128, 4, 128], BF16)
                vext = [kb.sb(pa, "vext%d" % i, [128, 4, 130], BF16) for i in range(2)]
                PT = kb.sb(pa, "PT", [128, 4, 128], BF16)
                Cst = kb.sb(pa, "Cst", [128, 4, 130], F32)
                Cb = kb.sb(pa, "Cb", [128, 4, 130], BF16)
                dmax = kb.sb(pa, "dmax", [128, 4], F32)
                hst6 = kb.sb(pa, "hst6", [128, 4, 6], F32)
                hmv = kb.sb(pa, "hmv", [128, 4, 2], F32)
                hrs = kb.sb(pa, "hrs", [128, 4], F32)
                hnm = kb.sb(pa, "hnm", [128, 4], F32)
                for vv in vext:
                    kb.memset("pool", vv[:], 1.0, w=["vext0", "vext1"])
                kb.memset("pool", Cst[:], 0.0, w=["Cst"])
                kb.memset("pool", Cb[:], 0.0, w=["Cb"])

                def rows(i):
                    k = i % 2
                    hb = (i // 4) % 2
                    hT = hTg[hb]
                    tc0 = (i % 4) * 128
                    pg = ps[7]
                    for kc in range(8):
                        kb.mm(pg[0:4, 0:128], lhsT=winA[:, kc, 2048:2052], rhs=hT[:, kc, tc0:tc0 + 128], start=(kc == 0), stop=(kc == 7),
                              r=["winA", "hTg%d" % hb], w=[PS[7]])
                    for kc in range(8):
                        kb.mm(pg[0:4, 128:256], lhsT=winA[:, kc, 2052:2056], rhs=hT[:, kc, tc0:tc0 + 128], start=(kc == 0), stop=(kc == 7),
                              r=["winA", "hTg%d" % hb], w=[PS[7]])
                    kb.act(IGa[:], pg[0:4, 0:128], AF.Identity, r=[PS[7], "bg"], w=["r_ig"], bias=bg[:, 0:1], scale=1.0)
                    kb.act(t1[:], pg[0:4, 128:256], AF.Exp, r=[PS[7], "nbg1"], w=["r_t1"], bias=nbg1[:, 0:1], scale=-1.0)
                    kb.act(t1[:], t1[:], AF.Ln, r=["r_t1"], w=["r_t1"], bias=1.0, scale=1.0)
                    kb.ts("dve", t1[:], t1[:], -1.0, None, ALU.mult, None, r=["r_t1"], w=["r_t1"])
                    prompt = i < 16
                    if prompt:
                        binit = 0.0 if i == 0 else Bc[1 - k][:, 127:128]
                        minit = 0.0 if i == 0 else Mx[1 - k][:, 127:128]
                        s.add("dve", lambda g_: g_.tensor_tensor_scan(out=Bc[k][:], data0=ones[0:4, :], data1=t1[:], initial=binit,
                                                                       op0=ALU.mult, op1=ALU.add),
                              r=["r_t1", "r_bc%d" % (1 - k), "C"], w=["r_bc%d" % k], tag="scanB")
                        kb.tt("dve", IGa[:], IGa[:], Bc[k][:], ALU.subtract, r=["r_ig", "r_bc%d" % k], w=["r_ig"])
                        kb.memset("dve", t3[:], 0.0, w=["r_t3"])
                        s.add("dve", lambda g_: g_.tensor_tensor_scan(out=Mx[k][:], data0=t3[:], data1=IGa[:], initial=minit,
                                                                       op0=ALU.add, op1=ALU.max),
                              r=["r_t3", "r_ig", "r_mx%d" % (1 - k)], w=["r_mx%d" % k], tag="scanM")
                        if i == 0:
                            kb.memset("dve", Rt[:], 0.0, w=["r_rt"])
                        else:
                            kb.cp("dve", Rt[:], Mx[1 - k][:, 127:128].to_broadcast([4, 128]), r=["r_mx%d" % (1 - k)], w=["r_rt"])
                    else:
                        s.add("dve", lambda g_: g_.tensor_tensor_scan(out=Bc[k][:], data0=rst[0:4, :], data1=t1[:], initial=0.0,
                                                                       op0=ALU.mult, op1=ALU.add),
                              r=["r_t1", "C"], w=["r_bc%d" % k], tag="scanB")
                        kb.tt("dve", IGa[:], IGa[:], Bc[k][:], ALU.subtract, r=["r_ig", "r_bc%d" % k], w=["r_ig"])
                        kb.cp("dve", t3[:], IGa[:], r=["r_ig"], w=["r_t3"])
                        kb.tt("dve", t3[:].rearrange("p (q t) -> p q t", t=8)[:, :, 0:1], IGa[:].rearrange("p (q t) -> p q t", t=8)[:, :, 0:1],
                              minitT[:].unsqueeze(2), ALU.max, r=["r_ig", "minitT"], w=["r_t3"])
                        s.add("dve", lambda g_: g_.tensor_tensor_scan(out=Mx[k][:], data0=rstm[0:4, :], data1=t3[:], initial=0.0,
                                                                       op0=ALU.add, op1=ALU.max),
                              r=["r_t3", "C"], w=["r_mx%d" % k], tag="scanM")
                        kb.cp("dve", Rt[:].rearrange("p (q t) -> p q t", t=8), minitT[:].unsqueeze(2).to_broadcast([4, 16, 8]),
                              r=["minitT"], w=["r_rt"])
                    kb.tt("dve", t3[:], IGa[:], Rt[:], ALU.subtract, r=["r_ig", "r_rt"], w=["r_t3"])
                    kb.act(t3[:], t3[:], AF.Exp, r=["r_t3"], w=["r_t3"])
                    kb.tt("dve", t4[:], Bc[k][:], Rt[:], ALU.add, r=["r_bc%d" % k, "r_rt"], w=["r_t4"])
                    kb.act(t4[:], t4[:], AF.Exp, r=["r_t4"], w=["r_t4"], scale=-1.0)
                    kb.tr(pg[:, 256:260], t3[0:4, :], C[0:4, 0:4], r=["r_t3", "C"], w=[PS[7]])
                    kb.tr(pg[:, 260:264], t4[0:4, :], C[0:4, 0:4], r=["r_t4", "C"], w=[PS[7]])
                    if prompt:
                        kb.tt("dve", dd[:, 0:1], Rt[:, 0:1], Mx[k][:, 127:128], ALU.subtract, r=["r_rt", "r_mx%d" % k], w=["r_dd"])
                        kb.act(dd[:, 0:1], dd[:, 0:1], AF.Exp, r=["r_dd"], w=["r_dd"])
                        kb.ts("dve", DDm[:, 0:4], C[0:4, 0:4], dd[:, 0:1], None, ALU.mult, None, r=["r_dd", "C"], w=["r_DD"])
                        kb.mm(pg[:, 264:268], lhsT=ones[0:4, :], rhs=DDm[:, 0:4], start=True, stop=True, r=["C", "r_DD"], w=[PS[7]])
                        kb.cp("dve", colq[k][:, 0:12], pg[:, 256:268], r=[PS[7]], w=["colq%d" % k])
                        if i == 15:
                            kb.tt("dve", mout[:, 0:1], Bc[k][:, 127:128], Mx[k][:, 127:128], ALU.add, r=["r_bc%d" % k, "r_mx%d" % k], w=["r_mout"])
                            kb.dma("sp", o_pmm, mout[:, 0:1], r=["r_mout"], w=())
                    else:
                        MT = Mx[k][:].rearrange("p (q t) -> p q t", t=8)[:, :, 7:8]
                        kb.tt("dve", t4[:].rearrange("p (q t) -> p q t", t=8), IGa[:].rearrange("p (q t) -> p q t", t=8),
                              MT.to_broadcast([4, 16, 8]), ALU.subtract, r=["r_ig", "r_mx%d" % k, PS[7]], w=["r_t4"])
                        kb.act(t4[:], t4[:], AF.Exp, r=["r_t4"], w=["r_t4"])
                        kb.tr(pg[:, 264:268], t4[0:4, :], C[0:4, 0:4], r=["r_t4", "C"], w=[PS[7]])
                        kb.cp("dve", colq[k][:, 0:12], pg[:, 256:268], r=[PS[7]], w=["colq%d" % k])
                        kb.tt("dve", dd[:].unsqueeze(2), minitT[:].unsqueeze(2), MT, ALU.subtract, r=["minitT", "r_mx%d" % k], w=["r_dd"])
                        kb.act(dd[:], dd[:], AF.Exp, r=["r_dd"], w=["r_dd"])
                        kb.tt("dve", DDm[:].rearrange("p (q h) -> p q h", h=4), dd[:].unsqueeze(2).to_broadcast([4, 16, 4]),
                              C[0:4, 0:4].unsqueeze(1).to_broadcast([4, 16, 4]), ALU.mult, r=["r_dd", "C"], w=["r_DD"])
                        kb.mm(pg[:, 272:336], lhsT=ones[0:4, :], rhs=DDm[:], start=True, stop=True, r=["C", "r_DD"], w=[PS[7]])
                        kb.cp("dve", decsb[:], pg[:, 272:336], r=[PS[7]], w=["decsb"])
                        kb.tt("dve", mout[:].unsqueeze(2), Bc[k][:].rearrange("p (q t) -> p q t", t=8)[:, :, 7:8], MT, ALU.add,
                              r=["r_bc%d" % k, "r_mx%d" % k], w=["r_mout"])
                        kb.dma("sp", o_smm, mout[:], r=["r_mout"], w=())

                def mlstm_tile(i):
                    k = i % 2
                    tg = i // 4
                    hT = hTg[tg % 2]
                    tc0 = (i % 4) * 128
                    lc0 = (i % 4) * 128 if i < 16 else 0
                    cq = colq[k]
                    vx = vext[k]
                    grp = "hTg%d" % (tg % 2)
                    for bi, c0 in enumerate((512, 1024, 1536)):
                        bank = 2 + (bi % 2)
                        for kc in range(8):
                            kb.mm(ps[bank][:, :], lhsT=hT[:, kc, tc0:tc0 + 128], rhs=winA[:, kc, c0:c0 + 512], start=(kc == 0), stop=(kc == 7),
                                  r=["winA", grp], w=[PS[bank]])
                        if bi == 0:
                            kb.act(ktok[:].rearrange("p a b -> p (a b)"), ps[bank][:, :], AF.Identity, r=[PS[bank]], w=["ktok"], scale=DKS)
                            for h in range(4):
                                kb.ts("dve", kw[:, h, :], ktok[:, h, :], cq[:, h:h + 1], None, ALU.mult, None,
                                      r=["ktok", "colq%d" % k], w=["kw"])
                        elif bi == 1:
                            kb.cp("act", vx[:, :, 0:128], ps[bank][:, :].rearrange("p (h d) -> p h d", d=128), r=[PS[bank]], w=["vext%d" % k])
                        else:
                            kb.act(sigo[:], ps[bank][:, :], AF.Sigmoid, r=[PS[bank]], w=["tA0"])
                    if i == 0:
                        stop_at(31)
                    for h in range(4):
                        kb.mm(ps[4][:, h * 128:(h + 1) * 128], lhsT=qkT[:, 4 + h, lc0:lc0 + 128], rhs=qkT[:, h, lc0:lc0 + 128],
                              start=True, stop=True, r=["qkT"], w=[PS[4]], sig=(h == 3))
                    if i == 0:
                        stop_at(32)
                    msk = maskP if i < 16 else maskS
                    for h in range(4):
                        kb.stt(PT[:, h, :], ps[4][:, h * 128:(h + 1) * 128], cq[:, h:h + 1], msk, ALU.mult, ALU.mult,
                               r=[PS[4], "colq%d" % k, "C"], w=["PT"])
                    return cq, vx, lc0

                def numden_finish(i, cq):
                    for half in range(2):
                        bank = ps[5 + half]
                        den = bank[:, 0:260].rearrange("p (h d) -> p h d", d=130)[:, :, 128:129]
                        kb.act(dmax[:, 2 * half:2 * half + 2].unsqueeze(2), den, AF.Abs, r=[PS[5 + half]], w=["dmax"])
                        kb.tt("dve", dmax[:, 2 * half:2 * half + 2], dmax[:, 2 * half:2 * half + 2], cq[:, 4 + 2 * half:6 + 2 * half], ALU.max,
                              r=["dmax", "colq%d" % (i % 2)], w=["dmax"])
                    s.add("dve", lambda g_: g_.reciprocal(out=dmax[:], in_=dmax[:]), r=["dmax"], w=["dmax"], tag="recip")
                    for h in range(4):
                        bank = ps[5 + h // 2]
                        o0 = (h % 2) * 130
                        kb.act(hmf[:, h * 128:(h + 1) * 128], bank[:, o0:o0 + 128], AF.Identity, r=[PS[5 + h // 2], "dmax"], w=["tB0"],
                               scale=dmax[:, h:h + 1])
                    for h in range(4):
                        s.add("dve", lambda g_, h=h: g_.bn_stats(out=hst6[:, h, :], in_=hmf[:, h * 128:(h + 1) * 128]), r=["tB0"], w=["hst6"], tag="bnst")
                    for h in range(4):
                        s.add("dve", lambda g_, h=h: g_.bn_aggr(out=hmv[:, h, :], in_=hst6[:, h, :]), r=["hst6"], w=["hmv"], tag="bnag")
                    kb.act(hrs[:].unsqueeze(2), hmv[:, :, 1:2], AF.Sqrt, r=["hmv"], w=["hrs"], bias=1e-6, scale=1.0)
                    s.add("dve", lambda g_: g_.reciprocal(out=hrs[:], in_=hrs[:]), r=["hrs"], w=["hrs"], tag="recip")
                    kb.tt("dve", hnm[:].unsqueeze(2), hmv[:, :, 0:1], hrs[:].unsqueeze(2), ALU.mult, r=["hmv", "hrs"], w=["hnm"])
                    kb.ts("dve", hnm[:], hnm[:], -1.0, None, ALU.mult, None, r=["hnm"], w=["hnm"])
                    for h in range(4):
                        kb.act(hmf[:, h * 128:(h + 1) * 128], hmf[:, h * 128:(h + 1) * 128], AF.Identity, r=["tB0", "hrs", "hnm"], w=["tB0"],
                               bias=hnm[:, h:h + 1], scale=hrs[:, h:h + 1])
                    kb.tt("pool", hmf[:], hmf[:], mng[:], ALU.mult, r=["tB0", "mng"], w=["tB0"])
                    kb.tt("pool", hmf[:], hmf[:], sigo[:], ALU.mult, r=["tB0", "tA0"], w=["tB0"])
                    for h in range(4):
                        kb.tr(ps[4][:, h * 128:(h + 1) * 128], hmf[:, h * 128:(h + 1) * 128], ident, r=["tB0", "C"], w=[PS[4]], sig=(h == 3))
                    kb.cp("act", hmT[:, :, i * 128:(i + 1) * 128], ps[4][:, :].rearrange("p (h d) -> p h d", d=128), r=[PS[4]], w=["hmT%d" % i])

                for tg, (t0, nt_) in enumerate(TGS):
                    tiles = range(4 * tg, 4 * tg + 4) if tg < 4 else [16]
                    hT = hTg[tg % 2]
                    for i in tiles:
                        make_hT(hT, i, prescale=False, col0=(i % 4) * 128, res="hTg%d" % (tg % 2))
                    for j in range(8):
                        bank = j % 2
                        for kc in range(8):
                            kb.mm(ps[bank][:, 0:nt_], lhsT=winA[:, kc, j * 128:(j + 1) * 128], rhs=hT[:, kc, 0:nt_], start=(kc == 0), stop=(kc == 7),
                                  r=["winA", "hTg%d" % (tg % 2)], w=[PS[bank]])
                        if j < 4:
                            kb.cp("act", qkT[:, j, 0:nt_], ps[bank][:, 0:nt_], r=[PS[bank]], w=["qkT"])
                        else:
                            kb.ts("dve", qkT[:, j, 0:nt_], ps[bank][:, 0:nt_], DKS, None, ALU.mult, None, r=[PS[bank]], w=["qkT"])
                    for i in tiles:
                        if i == 0:
                            stop_at(1)
                        rows(i)
                        if i == 0:
                            stop_at(2)
                        cq, vx, lc0 = mlstm_tile(i)
                        if i == 0:
                            stop_at(3)
                        if i == 16:
                            stop_at(5)
                        if i < 16:
                            for h in range(4):
                                bank = ps[5 + h // 2]
                                o0 = (h % 2) * 130
                                kb.mm(bank[:, o0:o0 + 130], lhsT=PT[:, h, :], rhs=vx[:, h, :], start=True, stop=False, r=["PT", "vext%d" % (i % 2)], w=[PS[5 + h // 2]], sig=False)
                                kb.mm(bank[:, o0:o0 + 130], lhsT=qkT[:, h, lc0:lc0 + 128], rhs=Cb[:, h, :], start=False, stop=True, r=["qkT", "Cb"], w=[PS[5 + h // 2]], sig=True)
                            numden_finish(i, cq)
                            for h in range(4):
                                bank = ps[5 + h // 2]
                                o0 = (h % 2) * 130
                                kb.mm(bank[:, o0:o0 + 130], lhsT=kw[:, h, :], rhs=vx[:, h, :], start=True, stop=True, r=["kw", "vext%d" % (i % 2)], w=[PS[5 + h // 2]])
                            for h in range(4):
                                bank = ps[5 + h // 2]
                                o0 = (h % 2) * 130
                                kb.ts("dve", Cst[:, h, :], Cst[:, h, :], cq[:, 8 + h:9 + h], None, ALU.mult, None, r=["Cst", "colq%d" % (i % 2)], w=["Cst"])
                                kb.stt(Cst[:, h, :], bank[:, o0:o0 + 130], cq[:, 8 + h:9 + h], Cst[:, h, :], ALU.mult, ALU.add,
                                       r=[PS[5 + h // 2], "Cst", "colq%d" % (i % 2)], w=["Cst"])
                            kb.cp("act", Cb[:], Cst[:], r=["Cst"], w=["Cb"])
                            if i == 0:
                                stop_at(4)
                            if i == 15:
                                for h in range(4):
                                    kb.tr(ps[4][:, h * 128:(h + 1) * 128], Cst[:, h, 0:128], ident, r=["Cst", "C"], w=[PS[4]], sig=(h == 3))
                                kb.cp("act", hmf[:], ps[4][:, :], r=[PS[4]], w=["tB0"])
                                kb.dma("sp", o_pmC.rearrange("h v k -> v h k"), hmf[:].rearrange("p (h k) -> p h k", k=128), r=["tB0"], w=())
                                kb.dma("sp", o_pmn.rearrange("h k -> k h"), Cst[:, :, 128], r=["Cst"], w=(), allow_slow_non_contiguous=True)
                        else:
                            with contextlib.ExitStack() as psm:
                                Cin = kb.sb(psm, "Cin", [128, 16, 128], F32)
                                CsT = kb.sb(psm, "CsT", [128, 16, 130], BF16)
                                qTm = kb.sb(psm, "qTm", [128, 16, 128], BF16)
                                VWm = kb.sb(psm, "VWm", [128, 16, 128], BF16)
                                nin = kb.sb(psm, "nin", [16, 4, 128], F32)
                                ninT = kb.sb(psm, "ninT", [128, 4, 16], F32)
                                BMW = kb.sb(psm, "BMW", [128, 16], BF16)
                                decc = kb.sb(psm, "decc", [16, 4], F32)
                                nout = nin
                                kb.memset("pool", qTm[:], 0.0, w=["qTm"])
                                kb.dma("sp", nin[:], smn, r=(), w=["nin"])
                                kb.tr(ps[7][0:16, 400:404], dd[0:4, :], C[0:4, 0:4], r=["r_dd", "C"], w=[PS[7]])
                                kb.cp("dve", decc[:], ps[7][0:16, 400:404], r=[PS[7]], w=["decc"])
                                for h in range(4):
                                    kb.tr(ps[7][:, 416 + h * 16:432 + h * 16], nin[0:16, h, :], C[0:16, 0:16], r=["nin", "C"], w=[PS[7]], sig=(h == 3))
                                kb.cp("dve", ninT[:].rearrange("p a b -> p (a b)"), ps[7][:, 416:480], r=[PS[7]], w=["ninT"])
                                for h in range(4):
                                    bank = ps[5 + h // 2]
                                    o0 = (h % 2) * 130
                                    kb.dma("sp", Cin[:], smC[:, h].rearrange("q v k -> v q k"), r=(), w=["Cin"])
                                    for q4 in range(4):
                                        pb = q4 % 2
                                        for qq in range(4):
                                            q = q4 * 4 + qq
                                            kb.tr(ps[pb][:, qq * 128:(qq + 1) * 128], Cin[:, q, :], ident, r=["Cin", "C"], w=[PS[pb]], sig=(qq == 3))
                                        kb.cp("act", CsT[:, q4 * 4:q4 * 4 + 4, 0:128], ps[pb][:, :].rearrange("p (a b) -> p a b", b=128), r=[PS[pb]], w=["CsT"])
                                    kb.cp("dve", CsT[:, :, 128:129], ninT[:, h, :].unsqueeze(2), r=["ninT"], w=["CsT"])
                                    kb.cp("pool", bass.AP(qTm, 0, [[2048, 128], [136, 16], [1, 8]]),
                                          qkT[:, h, 0:128].rearrange("p (q t) -> p q t", t=8), r=["qkT"], w=["qTm"])
                                    kb.mm(bank[:, o0:o0 + 130], lhsT=PT[:, h, :], rhs=vx[:, h, :], start=True, stop=False, r=["PT", "vext%d" % (i % 2)], w=[PS[5 + h // 2]], sig=False)
                                    for q in range(16):
                                        kb.mm(bank[:, o0:o0 + 130], lhsT=qTm[:, q, :], rhs=CsT[:, q, :], start=False, stop=(q == 15), r=["qTm", "CsT"], w=[PS[5 + h // 2]], sig=(q == 15))
                                    kb.ts("dve", BMW[:], bms, cq[:, 8 + h:9 + h], None, ALU.mult, None, r=["C", "colq%d" % (i % 2)], w=["BMW"])
                                    kb.tt("dve", VWm[:], vx[:, h, 0:128].unsqueeze(1).to_broadcast([128, 16, 128]), BMW[:].unsqueeze(2).to_broadcast([128, 16, 128]),
                                          ALU.mult, r=["vext%d" % (i % 2), "BMW"], w=["VWm"])
                                    for q4 in range(4):
                                        pb = q4 % 2
                                        for qq in range(4):
                                            q = q4 * 4 + qq
                                            kb.mm(ps[pb][:, qq * 128:(qq + 1) * 128], lhsT=VWm[:, q, :], rhs=ktok[:, h, :], start=True, stop=True,
                                                  r=["VWm", "ktok"], w=[PS[pb]], sig=(qq == 3))
                                        for qq in range(4):
                                            q = q4 * 4 + qq
                                            kb.stt(Cin[:, q, :], Cin[:, q, :], decsb[:, q * 4 + h:q * 4 + h + 1], ps[pb][:, qq * 128:(qq + 1) * 128], ALU.mult, ALU.add,
                                                   r=["Cin", "decsb", PS[pb]], w=["Cin"])
                                    kb.dma("sp", o_smC[:, h].rearrange("q v k -> v q k"), Cin[:], r=["Cin"], w=())
                                    kb.mm(ps[7][0:16, 0:128], lhsT=BMW[:], rhs=ktok[:, h, :], start=True, stop=True, r=["BMW", "ktok"], w=[PS[7]])
                                    kb.stt(nout[:, h, :], nin[:, h, :], decc[:, h:h + 1], ps[7][0:16, 0:128], ALU.mult, ALU.add, r=["nin", "decc", PS[7]], w=["nin"])
                                numden_finish(i, cq)
                                kb.dma("sp", o_smn, nout[:], r=["nin"], w=())
                                s.barrier()
            s.barrier()
            stop_at(6)
            hrT = kb.sb(ph, "hrT", [128, 4, NTOK], BF16)
            with contextlib.ExitStack() as pb_:
                winB = kb.sb(pb_, "winB", [128, 8, 1024], BF16)
                kb.dma("pool", winB[:], win_v[:, :, 2056:3080], r=(), w=["winB"])
                hTgB = [kb.sb(pb_, "hTgB%d" % i, [128, 8, 512], BF16) for i in range(2)]
                WA = kb.sb(pb_, "WA", [128, 4, 128], F32)
                WX = kb.sb(pb_, "WX", [128, 4, 128], F32)
                kb.memset("pool", WA[:], 0.0, w=["WA"])
                kb.memset("pool", WX[:], 0.0, w=["WX"])
                for c in range(4):
                    for hp in range(2):
                        kb.dma("sp", WA[hp * 64:(hp + 1) * 64, c, hp * 64:(hp + 1) * 64], rg_w_a[0, 2 * c + hp], r=(), w=["WA"])
                        kb.dma("sp", WX[hp * 64:(hp + 1) * 64, c, hp * 64:(hp + 1) * 64], rg_w_x[0, 2 * c + hp], r=(), w=["WX"])
                cl = kb.sb(pb_, "cl", [128, 4], F32)
                cl2 = kb.sb(pb_, "cl2", [128, 4], F32)
                kb.act(cl[:], vecA[:, 28:32], AF.Exp, r=["vecA"], w=["cl"], scale=-1.0)
                kb.act(cl[:], cl[:], AF.Ln, r=["cl"], w=["cl"], bias=1.0, scale=1.0)
                kb.ts("dve", cl2[:], cl[:], -16.0, None, ALU.mult, None, r=["cl"], w=["cl2"])
                kb.ts("dve", cl[:], cl[:], -8.0, None, ALU.mult, None, r=["cl", "cl2"], w=["cl"])
                xp = [kb.sb(pb_, "xp%d" % c, [128, 515], F32) for c in range(4)]
                xps = kb.sb(pb_, "xps", [128, 16, 11], F32)
                hst = kb.sb(pb_, "hst", [128, 4], F32)
                h0T = kb.sb(pb_, "h0T", [128, 4, 16], F32)
                cvT = kb.sb(pb_, "cvT", [128, 4, 48], F32)
                hl = kb.sb(pb_, "hl", [128, 4, 16], F32)
                srh_sb = kb.sb(pb_, "srh_sb", [16, 512], F32)
                src_sb = kb.sb(pb_, "src_sb", [48, 512], F32)
                F5 = lambda nm: kb.sb(pb_, nm, [128, 512], F32)
                xc, rr, ii, aa, a2, uu, hh_, t5 = F5("xc"), F5("rr"), F5("ii"), F5("aa"), F5("a2"), F5("uu"), F5("hh"), F5("t5")
                for c in range(4):
                    kb.memset("pool", xp[c][:, 0:3], 0.0, w=["xp%d" % c])
                kb.memset("pool", hst[:], 0.0, w=["hst"])
                kb.dma("sp", srh_sb[:], srh, r=(), w=["srh_sb"])
                kb.dma("sp", src_sb[:], srconv, r=(), w=["src_sb"])
                for c in range(4):
                    kb.tr(ps[6][:, c * 16:(c + 1) * 16], srh_sb[0:16, c * 128:(c + 1) * 128], C[0:16, 0:16], r=["srh_sb", "C"], w=[PS[6]], sig=(c == 3))
                kb.cp("dve", h0T[:].rearrange("p a b -> p (a b)"), ps[6][:, 0:64], r=[PS[6]], w=["h0T"])
                for c in range(4):
                    kb.tr(ps[6][:, 64 + c * 48:64 + (c + 1) * 48], src_sb[0:48, c * 128:(c + 1) * 128], C[0:48, 0:48], r=["src_sb", "C"], w=[PS[6]], sig=(c == 3))
                kb.cp("dve", cvT[:].rearrange("p a b -> p (a b)"), ps[6][:, 64:256], r=[PS[6]], w=["cvT"])
                u_ = 0
                for tg, (t0, n) in enumerate(TGS):
                    sample = tg == 4
                    hT = hTgB[tg % 2]
                    for i in (range(4 * tg, 4 * tg + 4) if tg < 4 else [16]):
                        make_hT(hT, i, prescale=True, col0=(i % 4) * 128, res="hTg%d" % (tg % 2))
                    for c in range(4):
                        px, pgr = ps[(2 * u_) % 4], ps[(2 * u_ + 1) % 4]
                        PX, PGR = PS[(2 * u_) % 4], PS[(2 * u_ + 1) % 4]
                        u_ += 1
                        for kc in range(8):
                            kb.mm(px[:, 0:n], lhsT=winB[:, kc, c * 128:(c + 1) * 128], rhs=hT[:, kc, 0:n], start=(kc == 0), stop=(kc == 7),
                                  r=["winB", "hTg%d" % (tg % 2)], w=[PX])
                        for kc in range(8):
                            kb.mm(pgr[:, 0:n], lhsT=winB[:, kc, 512 + c * 128:512 + (c + 1) * 128], rhs=hT[:, kc, 0:n], start=(kc == 0), stop=(kc == 7),
                                  r=["winB", "hTg%d" % (tg % 2)], w=[PGR])
                        cw = lambda j: vecA[:, c * 4 + j:c * 4 + j + 1]
                        cb = vecA[:, 16 + c:17 + c]
                        if not sample:
                            kb.cp("act", xp[c][:, 3:3 + n], px[:, 0:n], r=[PX], w=["xp%d" % c])
                            kb.ts("dve", xc[:, 0:n], xp[c][:, 0:n], cw(0), cb, ALU.mult, ALU.add, r=["xp%d" % c, "vecA"], w=["xc"])
                            for j in range(1, 4):
                                kb.stt(xc[:, 0:n], xp[c][:, j:j + n], cw(j), xc[:, 0:n], ALU.mult, ALU.add, r=["xp%d" % c, "vecA", "xc"], w=["xc"])
                            if tg == 3:
                                kb.tr(ps[6][0:3, c * 128:(c + 1) * 128], xp[c][:, n:n + 3], ident, r=["xp%d" % c, "C"], w=[PS[6]])
                            else:
                                kb.cp("pool", xp[c][:, 0:3], xp[c][:, n:n + 3], r=["xp%d" % c], w=["xp%d" % c])
                        else:
                            kb.cp("dve", xps[:, :, 0:3], cvT[:, c, :].rearrange("p (q j) -> p q j", j=3), r=["cvT"], w=["xps"])
                            kb.cp("act", xps[:, :, 3:11], px[:, 0:n].rearrange("p (q t) -> p q t", t=8), r=[PX], w=["xps"])
                            xc3 = xc[:, 0:n].rearrange("p (q t) -> p q t", t=8)
                            kb.ts("dve", xc3, xps[:, :, 0:8], cw(0), cb, ALU.mult, ALU.add, r=["xps", "vecA"], w=["xc"])
                            for j in range(1, 4):
                                kb.stt(xc3, xps[:, :, j:j + 8], cw(j), xc3, ALU.mult, ALU.add, r=["xps", "vecA", "xc"], w=["xc"])
                            for j in range(3):
                                kb.tr(ps[5][0:16, j * 128:(j + 1) * 128], xps[:, :, 8 + j], ident, r=["xps", "C"], w=[PS[5]], sig=(j == 2))
                            kb.cp("dve", t5[0:16, 0:384], ps[5][0:16, 0:384], r=[PS[5]], w=["t5"])
                            kb.dma("sp", o_srconv.rearrange("(q j) f -> q j f", j=3)[:, :, c * 128:(c + 1) * 128],
                                   t5[0:16, 0:384].rearrange("q (j f) -> q j f", f=128), r=["t5"], w=())
                        kb.mm(ps[4][:, 0:n], lhsT=WA[:, c, :], rhs=xc[:, 0:n], start=True, stop=True, r=["WA", "xc"], w=[PS[4]])
                        kb.mm(ps[5][:, 0:n], lhsT=WX[:, c, :], rhs=xc[:, 0:n], start=True, stop=True, r=["WX", "xc"], w=[PS[5]])
                        kb.act(rr[:, 0:n], ps[4][:, 0:n], AF.Sigmoid, r=[PS[4], "vecA"], w=["rr"], bias=vecA[:, 20 + c:21 + c], scale=1.0)
                        kb.act(ii[:, 0:n], ps[5][:, 0:n], AF.Sigmoid, r=[PS[5], "vecA"], w=["ii"], bias=vecA[:, 24 + c:25 + c], scale=1.0)
                        kb.act(aa[:, 0:n], rr[:, 0:n], AF.Exp, r=["rr", "cl"], w=["aa"], scale=cl[:, c:c + 1])
                        kb.act(a2[:, 0:n], rr[:, 0:n], AF.Exp, r=["rr", "cl2"], w=["a2"], scale=cl2[:, c:c + 1])
                        kb.act(a2[:, 0:n], a2[:, 0:n], AF.Sqrt, r=["a2"], w=["a2"], bias=1.0, scale=-1.0)
                        kb.tt("dve", uu[:, 0:n], a2[:, 0:n], ii[:, 0:n], ALU.mult, r=["a2", "ii"], w=["uu"])
                        kb.tt("dve", uu[:, 0:n], uu[:, 0:n], xc[:, 0:n], ALU.mult, r=["uu", "xc"], w=["uu"])
                        if not sample:
                            s.add("dve", lambda g_, c=c, n=n: g_.tensor_tensor_scan(out=hh_[:, 0:n], data0=aa[:, 0:n], data1=uu[:, 0:n], initial=hst[:, c:c + 1],
                                                                                     op0=ALU.mult, op1=ALU.add),
                                  r=["aa", "uu", "hst"], w=["hh"], tag="scanH")
                            kb.cp("dve", hst[:, c:c + 1], hh_[:, n - 1:n], r=["hh"], w=["hst"])
                        else:
                            aa3 = aa[:, 0:n].rearrange("p (q t) -> p q t", t=8)
                            uu3 = uu[:, 0:n].rearrange("p (q t) -> p q t", t=8)
                            kb.tt("dve", t5[:, 0:16].unsqueeze(2), aa3[:, :, 0:1], h0T[:, c, :].unsqueeze(2), ALU.mult, r=["aa", "h0T", "t5"], w=["t5"])
                            kb.tt("dve", uu3[:, :, 0:1], uu3[:, :, 0:1], t5[:, 0:16].unsqueeze(2), ALU.add, r=["uu", "t5"], w=["uu"])
                            kb.tt("dve", aa[:, 0:n], aa[:, 0:n], rst, ALU.mult, r=["aa", "C"], w=["aa"])
                            s.add("dve", lambda g_, n=n: g_.tensor_tensor_scan(out=hh_[:, 0:n], data0=aa[:, 0:n], data1=uu[:, 0:n], initial=0.0,
                                                                                op0=ALU.mult, op1=ALU.add),
                                  r=["aa", "uu"], w=["hh"], tag="scanH")
                            kb.cp("dve", hl[:, c, :].unsqueeze(2), hh_[:, 0:n].rearrange("p (q t) -> p q t", t=8)[:, :, 7:8], r=["hh"], w=["hl"])
                        kb.act(t5[:, 0:n], pgr[:, 0:n], AF.Square, r=[PGR, "t5"], w=["t5"])
                        kb.ts("dve", t5[:, 0:n], t5[:, 0:n], 0.044715, 1.0, ALU.mult, ALU.add, r=["t5"], w=["t5"])
                        kb.tt("dve", t5[:, 0:n], t5[:, 0:n], pgr[:, 0:n], ALU.mult, r=["t5", PGR], w=["t5"])
                        kb.act(t5[:, 0:n], t5[:, 0:n], AF.Tanh, r=["t5"], w=["t5"], scale=0.7978845608028654)
                        kb.ts("dve", t5[:, 0:n], t5[:, 0:n], 1.0, 0.5, ALU.add, ALU.mult, r=["t5"], w=["t5"])
                        kb.tt("dve", t5[:, 0:n], t5[:, 0:n], pgr[:, 0:n], ALU.mult, r=["t5", PGR], w=["t5"])
                        kb.tt("dve", hrT[:, c, t0:t0 + n], t5[:, 0:n], hh_[:, 0:n], ALU.mult, r=["t5", "hh"], w=["hrT_g%d" % tg])
                    if tg == 3:
                        kb.cp("dve", t5[0:3, :], ps[6][0:3, 0:512], r=[PS[6]], w=["t5"])
                        kb.dma("sp", o_prconv, t5[0:3, :], r=["t5"], w=())
                kb.tr(ps[7][0:4, 0:128], hst[:, 0:4], ident, r=["hst", "C"], w=[PS[7]])
                kb.cp("dve", rr[0:4, 0:128], ps[7][0:4, 0:128], r=[PS[7]], w=["rr"])
                kb.dma("sp", o_prh, rr[0:4, 0:128], r=["rr"], w=())
                for c in range(4):
                    kb.tr(ps[4][0:16, c * 128:(c + 1) * 128], hl[:, c, :], ident, r=["hl", "C"], w=[PS[4]], sig=(c == 3))
                kb.cp("dve", ii[0:16, :], ps[4][0:16, :], r=[PS[4]], w=["ii"])
                kb.dma("sp", o_srh, ii[0:16, :], r=["ii"], w=())
                s.barrier()
            stop_at(7)
            with contextlib.ExitStack() as pc_:
                alloc_gl(pc_)
                mod_prepare(l, 1, 1.0, blocks=[4, 5])
                Gp, Gs = gl["Gp"], gl["Gs"]
                wout = kb.sb(pc_, "wout", [128, 8, D], BF16)
                kb.dma("pool", wout[:], ab_w_out[0].rearrange("(kc p) n -> p kc n", p=128), r=(), w=["wout"])
                proj_acc(wout, "wout", 8, lambda i, kc: ((hmT[:, kc, i * 128:(i + 1) * 128], "hmT%d" % i) if kc < 4 else
                                                          (hrT[:, kc - 4, i * 128:(i + 1) * 128], "hrT_g%d" % (i // 4))), True, True)
                s.barrier()
        s.barrier()

    CW = -0.6065306597126334
    RT = BF16

    def rwkv_mixer(l):
        rwkv_mixer_(l)
        s.skip = False
        s.barrier()

    def rwkv_mixer_(l):
        rw_mu = kb.din("muT", [128, 48])
        rw_wr = kb.din("rw_wr", [1, D, D])
        rw_wk = kb.din("rw_wk", [1, D, D])
        rw_wv = kb.din("rw_wv", [1, D, D])
        rw_wo = kb.din("rw_wo", [1, D, D])
        rw_w0 = kb.din("rw_w0", [1, D])
        rw_w1 = kb.din("rw_w1", [1, D, 64])
        rw_w2 = kb.din("rw_w2", [1, 64, D])
        rw_a0 = kb.din("rw_a0", [1, D])
        rw_a1 = kb.din("rw_a1", [1, D, 64])
        rw_a2 = kb.din("rw_a2", [1, 64, D])
        rw_g1 = kb.din("rw_g1", [1, D, 128])
        rw_g2 = kb.din("rw_g2", [1, 128, D])
        rw_kk = kb.din("rw_k_k", [1, D])
        rw_ka = kb.din("rw_k_a", [1, D])
        rw_rk = kb.din("rk_flat", [1, D])
        rw_lng = kb.din("rw_lnx_g", [1, D])
        rw_lnb = kb.din("rw_lnx_b", [1, D])
        swkv = kb.din("swkv", [16, 16, 64, 64])
        sshift = kb.din("sshift", [16, D])
        o_pwkv = kb.dout("o_pwkv", [16, 64, 64])
        o_pshift = kb.dout("o_pshift", [1, D])
        o_swkv = kb.dout("o_swkv", [16, 16, 64, 64])
        o_sshift = kb.dout("o_sshift", [16, D])

        cm = lambda nm: C[:, CST_OFF[nm]:CST_OFF[nm] + 128]
        low16, up16, m16, m32, m64 = cm("low16"), cm("up16"), cm("m16"), cm("m32"), cm("m64")
        maskP, maskS, upP, lowP, upS, lowS, blkS, ones, bms = cm("maskP"), cm("maskS"), cm("upP"), cm("lowP"), cm("upS"), cm("lowS"), cm("blkS"), cm("ones"), C[:, CST_OFF["bms"]:CST_OFF["bms"] + 16]

        mod_prepare(l, 1, 1.0, blocks=range(4))
        with contextlib.ExitStack() as ph:
            ygT = kb.sb(ph, "ygT", [128, 8, NTOK], BF16)
            muT = kb.sb(ph, "muT_sb", [128, 48], F32)
            kb.dma("sp", muT[:], rw_mu, r=(), w=["muT"])
            identR = kb.sb(ph, "identR", [128, 128], RT)
            kb.cp("dve", identR[:], ident, r=["C"], w=["identR"])
            w1b = kb.sb(ph, "w1b", [128, 8, 64], BF16)
            a1b = kb.sb(ph, "a1b", [128, 8, 64], BF16)
            g1b = kb.sb(ph, "g1b", [128, 8, 128], BF16)
            kb.dma("pool", w1b[:], rw_w1[0].rearrange("(kc p) n -> p kc n", p=128), r=(), w=["w1b"])
            kb.dma("pool", a1b[:], rw_a1[0].rearrange("(kc p) n -> p kc n", p=128), r=(), w=["a1b"])
            kb.dma("pool", g1b[:], rw_g1[0].rearrange("(kc p) n -> p kc n", p=128), r=(), w=["g1b"])
            hlast = kb.sb(ph, "hlast", [128, 8, 17], F32)
            sh0T = kb.sb(ph, "sh0T", [128, 8, 16], BF16)
            with contextlib.ExitStack() as p0:
                shs = kb.sb(p0, "shs", [16, D], F32)
                kb.dma("sp", shs[:], sshift, r=(), w=["shs"])
                for c in range(8):
                    kb.tr(ps[0][:, c * 16:(c + 1) * 16], shs[0:16, c * 128:(c + 1) * 128], C[0:16, 0:16], r=["shs", "C"], w=[PS[0]], sig=(c == 7))
                kb.cp("dve", sh0T[:].rearrange("p a b -> p (a b)"), ps[0][:, 0:128], r=[PS[0]], w=["sh0T"])
                s.barrier()

            def hook_last(i, c, src, psres):
                if i == 15:
                    kb.ts("dve", hlast[:, c, 0:1], src[:, 127:128], modT[:, 8 + c, 0:1], modT[:, c, 0:1], ALU.mult, ALU.add, r=["modT", psres], w=["hlast"])
                elif i == 16:
                    v3 = src.rearrange("p (q t) -> p q t", t=8)[:, :, 7:8]
                    kb.tt("dve", hlast[:, c, 1:17].unsqueeze(2), v3, modT[:, 8 + c, 1:NT].unsqueeze(2), ALU.mult, r=["modT", psres], w=["hlast"])
                    kb.tt("dve", hlast[:, c, 1:17], hlast[:, c, 1:17], modT[:, c, 1:NT], ALU.add, r=["hlast", "modT"], w=["hlast"])

            stop_at(51)
            for hg in range(4):
                c0 = hg * 256
                with contextlib.ExitStack() as pp:
                    wrs = kb.sb(pp, "wrs", [128, 8, 256], BF16)
                    wks = kb.sb(pp, "wks", [128, 8, 256], BF16)
                    wvs = kb.sb(pp, "wvs", [128, 8, 256], BF16)
                    for wt, src in ((wrs, rw_wr), (wks, rw_wk), (wvs, rw_wv)):
                        kb.dma("pool", wt[:], src[0].rearrange("(kc p) n -> p kc n", p=128)[:, :, c0:c0 + 256], r=(), w=["wqkv"])
                    w2s = kb.sb(pp, "w2s", [64, 256], BF16)
                    a2s = kb.sb(pp, "a2s", [64, 256], BF16)
                    g2s = kb.sb(pp, "g2s", [128, 256], BF16)
                    kb.dma("pool", w2s[:], rw_w2[0, :, c0:c0 + 256], r=(), w=["w2s"])
                    kb.dma("pool", a2s[:], rw_a2[0, :, c0:c0 + 256], r=(), w=["a2s"])
                    kb.dma("pool", g2s[:], rw_g2[0, :, c0:c0 + 256], r=(), w=["g2s"])
                    w0r = kb.sb(pp, "w0r", [1, 256], F32)
                    a0r = kb.sb(pp, "a0r", [1, 256], F32)
                    kb.dma("sp", w0r[:], rw_w0[0:1, c0:c0 + 256], r=(), w=["w0r"])
                    kb.dma("sp", a0r[:], rw_a0[0:1, c0:c0 + 256], r=(), w=["a0r"])
                    bcs = {}
                    alias = {"kkb": tA[2][:, 0:256], "kab": tA[2][:, 256:512], "rkb": tA[3][:, 0:256], "lngb": tA[3][:, 256:512]}
                    for nm, src in (("kkb", rw_kk), ("kab", rw_ka), ("rkb", rw_rk), ("lngb", rw_lng), ("lnbb", rw_lnb)):
                        bcs[nm] = alias[nm] if nm in alias else kb.sb(pp, nm, [128, 256], F32)
                        kb.dma("sp", bcs[nm][:], src[0:1, c0:c0 + 256].to_broadcast([128, 256]), r=(), w=[nm])
                    hTg = [kb.sb(pp, "hTr%d" % i, [128, 8, 130], BF16) for i in range(2)]
                    dx = kb.sb(pp, "dx", [128, 8, 128], BF16)
                    xj = [kb.sb(pp, "xj0", [128, 8, 128], BF16)]
                    xj.append(xj[0])
                    loT = kb.sb(pp, "loT", [128, 3, 128], BF16)
                    TB = lambda j: tB[j // 4][:, (j % 4) * 256:(j % 4) * 256 + 256]
                    Rr, Kk, KKn, Aa, SG, CSs, Ee, Tt = [TB(j) for j in range(8)]
                    tbres = lambda j: "tB%d" % (j // 4)
                    Gt = tA[0][:, 0:256]
                    small = tA[1][:, 256:320]
                    F3 = lambda nm: kb.sb(pp, nm, [128, 256], RT)
                    Vv, AL, RB, KH, BH, BT = F3("Vv"), F3("AL"), F3("RB"), F3("KH"), F3("BH"), F3("BTt")
                    Vf = tA[0][:, 256:512]
                    fmall = kb.sb(pp, "fmall", [64, 4, 4, 128], RT)
                    fmT = {nm: fmall[:, j] for j, nm in enumerate(("alT", "btT", "ktT", "rbT"))}
                    chainall = kb.sb(pp, "chainall", [128, 7, 4, 128], RT)
                    chn = {nm: chainall[:, j] for j, nm in enumerate(("ApA", "ApB", "BpA", "BpB", "TTa", "TTb", "Am"))}
                    S0nat = kb.sb(pp, "S0nat", [64, 16, 64], F32)
                    S0Tq = fmall[:, 0:2].rearrange("p a h t -> p (a h t)").rearrange("p (q k) -> p q k", k=64)
                    SLo = S0nat
                    AakT = kb.sb(pp, "AakT", [128, 4, 128], RT)
                    YTs = AakT[0:64, :, :]
                    ArbT = kb.sb(pp, "ArbT", [128, 4, 128], RT)
                    ArkT = kb.sb(pp, "ArkT", [128, 4, 128], RT)
                    Ahat = kb.sb(pp, "Ahat", [128, 4, 64], RT)
                    X1 = kb.sb(pp, "X1", [128, 4, 64], RT)
                    KT = X1[:].rearrange("p h k -> p (h k)")
                    U0 = kb.sb(pp, "U0", [128, 4, 64], RT)
                    Gm = kb.sb(pp, "Gm", [64, 4, 64], RT)
                    RhT = kb.sb(pp, "RhT", [64, 4, 128], RT)
                    S0T = kb.sb(pp, "S0T", [64, 4, 64], RT)
                    PLc = tA[1][0:64, 320:384]
                    yb = tA[1][:, 0:256]
                    kb.ts("dve", S0T[:].rearrange("p a b -> p (a b)"), C[0:64, 0:256], 0.0, None, ALU.mult, None, r=["C"], w=["S0T0", "S0T1", "S0T2", "S0T3"])
                    kb.memset("pool", hTg[1][:, :, 128:129], 0.0, w=["hTr1"])

                    algb = [4]

                    def nb():
                        b_ = algb[0]
                        algb[0] = 4 + (algb[0] - 3) % 4
                        return b_

                    def grp4(mmf, n_cols, m_rows=128):
                        b_ = nb()
                        for hl in range(4):
                            items = mmf(hl)
                            for j, (lt, rh, rd) in enumerate(items):
                                kb.mm(ps[b_][0:m_rows, hl * n_cols:(hl + 1) * n_cols], lhsT=lt, rhs=rh, start=(j == 0), stop=(j == len(items) - 1),
                                      r=rd, w=[PS[b_]], sig=(hl == 3 and j == len(items) - 1))
                        return b_

                    fsl = lambda t_, hl: t_[:, hl, :]
                    tsl = lambda t_, hl: t_[:, hl * 64:(hl + 1) * 64]

                    def sample_states():
                        Bhm = chainall[:, 0:2].rearrange("p a h t -> p (a h t)").rearrange("p (q k) -> p q k", k=64)
                        Khm = chainall[:, 2:4].rearrange("p a h t -> p (a h t)").rearrange("p (q k) -> p q k", k=64)
                        Gq = chainall[0:64, 4:6].rearrange("p a h t -> p (a h t)").rearrange("p (q k) -> p q k", k=64)
                        bmq = bms.unsqueeze(2).to_broadcast([128, 16, 64])
                        for hl in range(4):
                            h = hg * 4 + hl
                            kb.tt("dve", Bhm, tsl(BH, hl).unsqueeze(1).to_broadcast([128, 16, 64]), bmq, ALU.mult, r=["BH", "C"], w=["ApA0", "ApA1", "ApA2", "ApA3"] + ["ApB0", "ApB1", "ApB2", "ApB3"])
                            kb.tt("pool", Khm, tsl(KH, hl).unsqueeze(1).to_broadcast([128, 16, 64]), bmq, ALU.mult, r=["KH", "C"], w=["BpA0", "BpA1", "BpA2", "BpA3"] + ["BpB0", "BpB1", "BpB2", "BpB3"])
                            kb.dma("sp", S0nat[:], swkv[:, h].rearrange("q v k -> v q k"), r=(), w=["S0nat"])
                            b0, b1 = nb(), nb()
                            for q in range(16):
                                bb = b0 if q < 8 else b1
                                kb.tr(ps[bb][0:64, (q % 8) * 64:(q % 8 + 1) * 64], S0nat[:, q, :], ident[0:64, 0:64], r=["S0nat", "C"], w=[PS[bb]], sig=(q % 8 == 7))
                            kb.cp("act", S0Tq[:, 0:8, :], ps[b0][0:64, :].rearrange("p (q k) -> p q k", k=64), r=[PS[b0]], w=["S0Tq", "alT", "btT"])
                            kb.cp("dve", S0Tq[:, 8:16, :], ps[b1][0:64, :].rearrange("p (q k) -> p q k", k=64), r=[PS[b1]], w=["S0Tq", "alT", "btT"])
                            b_ = nb()
                            for q in range(16):
                                kb.mm(ps[b_][0:64, q * 8:(q + 1) * 8], lhsT=S0Tq[:, q, :], rhs=RhT[:, hl, q * 8:(q + 1) * 8], start=True, stop=True,
                                      r=["S0Tq", "RhT%d" % hl], w=[PS[b_]], sig=(q == 15))
                            kb.cp("act", YTs[:, hl, :], ps[b_][0:64, 0:128], r=[PS[b_]], w=["AakT%d" % hl])
                            g0, g1 = nb(), nb()
                            for half, bb in ((0, g0), (1, g1)):
                                kb.mm(ps[bb][0:64, :], lhsT=Ahat[:, hl, :], rhs=Bhm[:, half * 8:(half + 1) * 8, :], start=True, stop=True,
                                      r=["Ahat%d" % hl] + ["ApA0", "ApA1", "ApA2", "ApA3"] + ["ApB0", "ApB1", "ApB2", "ApB3"], w=[PS[bb]])
                            for q in range(16):
                                bb = g0 if q < 8 else g1
                                kb.stt(Gq[:, q, :], ident[0:64, 0:64], PLc[:, hl * 16 + q:hl * 16 + q + 1], ps[bb][0:64, (q % 8) * 64:(q % 8 + 1) * 64], ALU.mult, ALU.add,
                                       r=["C", "tA1", PS[bb]], w=["TTa0", "TTa1", "TTa2", "TTa3"] + ["TTb0", "TTb1", "TTb2", "TTb3"])
                            for half in range(2):
                                bb = nb()
                                kb.mm(ps[bb][0:64, :], lhsT=tsl(Vv, hl), rhs=Khm[:, half * 8:(half + 1) * 8, :], start=True, stop=False, r=["Vv"] + ["BpA0", "BpA1", "BpA2", "BpA3"] + ["BpB0", "BpB1", "BpB2", "BpB3"], w=[PS[bb]], sig=False)
                                kb.mm(ps[bb][0:64, :], lhsT=U0[:, hl, :], rhs=Bhm[:, half * 8:(half + 1) * 8, :], start=False, stop=False, r=["U0%d" % hl] + ["ApA0", "ApA1", "ApA2", "ApA3"] + ["ApB0", "ApB1", "ApB2", "ApB3"], w=[PS[bb]], sig=False)
                                for qq in range(8):
                                    q = half * 8 + qq
                                    kb.mm(ps[bb][0:64, qq * 64:(qq + 1) * 64], lhsT=S0Tq[:, q, :], rhs=Gq[:, q, :], start=False, stop=(qq == 7),
                                          r=["S0Tq"] + ["TTa0", "TTa1", "TTa2", "TTa3"] + ["TTb0", "TTb1", "TTb2", "TTb3"], w=[PS[bb]], sig=(qq == 7))
                                kb.cp("act" if half else "dve", SLo[:, half * 8:(half + 1) * 8, :], ps[bb][0:64, :].rearrange("p (q k) -> p q k", k=64), r=[PS[bb]], w=["S0nat"])
                            kb.dma("sp", o_swkv[:, h].rearrange("q v k -> v q k"), SLo[:], r=["S0nat"], w=())
                        by = grp4(lambda hl: [(ArkT[:, hl, :], tsl(Vv, hl), ["ArkT%d" % hl, "Vv"]), (ArbT[:, hl, :], U0[:, hl, :], ["ArbT%d" % hl, "U0%d" % hl]),
                                               (YTs[:, hl, :], identR[0:64, 0:64], ["AakT%d" % hl, "identR"])], 64)
                        kb.cp("act", yb[:], ps[by][:, 0:256], r=[PS[by]], w=["tA1"])

                    for i in range(NT):
                        sample = i == 16
                        k_ = i % 2
                        hT = hTg[k_]
                        hres = "hTr%d" % k_
                        last_pass = hg == 3
                        if hg == 0 and i == 1:
                            stop_at(58)
                        if hg == 0 and i == 16:
                            stop_at(59)
                        make_hT(hT, i, prescale=last_pass, col0=1, res=hres, hook=(hook_last if hg == 0 else None))
                        if i == 0:
                            kb.memset("pool", hT[:, :, 0:1], 0.0, w=[hres])
                        elif not sample:
                            kb.cp("pool", hT[:, :, 0:1], hTg[1 - k_][:, :, 128:129], r=["hTr%d" % (1 - k_)], w=[hres])
                        cur = hT[:, :, 1:129]
                        if not sample:
                            kb.tt("pool", dx[:], hT[:, :, 0:128], cur, ALU.subtract, r=[hres], w=["dx"])
                        else:
                            kb.cp("pool", dx[:], hT[:, :, 0:128], r=[hres], w=["dx"])
                            kb.cp("pool", dx[:].rearrange("p c (q t) -> p c q t", t=8)[:, :, :, 0], sh0T[:], r=["sh0T"], w=["dx"])
                            kb.tt("pool", dx[:], dx[:], cur, ALU.subtract, r=["dx", hres], w=["dx"])
                        def mix(j, buf):
                            e = "dve" if j % 2 == 0 else "pool"
                            kb.tt(e, xj[buf][:], dx[:], muT[:, j * 8:(j + 1) * 8].unsqueeze(2).to_broadcast([128, 8, 128]), ALU.mult, r=["dx", "muT"], w=["xj%d" % buf])
                            kb.tt(e, xj[buf][:], xj[buf][:], cur, ALU.add, r=["xj%d" % buf, hres], w=["xj%d" % buf])
                            return xj[buf], "xj%d" % buf
                        for j, (wt, bank, off) in ((0, (wrs, 0, 0)), (2, (wks, 0, 256)), (3, (wvs, 1, 0))):
                            xx, xr_ = mix(j, 0)
                            for kc in range(8):
                                kb.mm(ps[bank][:, off:off + 256], lhsT=xx[:, kc, :], rhs=wt[:, kc, :], start=(kc == 0), stop=(kc == 7), r=[xr_, "wqkv"], w=[PS[bank]])
                        for j, (wt, wr_, m_, off3) in ((1, (w1b, "w1b", 64, 0)), (4, (a1b, "a1b", 64, 128)), (5, (g1b, "g1b", 128, 256))):
                            xx, xr_ = mix(j, 0)
                            for kc in range(8):
                                kb.mm(ps[3][0:m_, off3:off3 + 128], lhsT=wt[:, kc, :], rhs=xx[:, kc, :], start=(kc == 0), stop=(kc == 7), r=[xr_, wr_], w=[PS[3]])
                        kb.act(loT[0:64, 0, :], ps[3][0:64, 0:128], AF.Tanh, r=[PS[3]], w=["loT"])
                        kb.cp("act", loT[0:64, 1, :], ps[3][0:64, 128:256], r=[PS[3]], w=["loT"])
                        kb.act(loT[:, 2, :], ps[3][:, 256:384], AF.Sigmoid, r=[PS[3]], w=["loT"])
                        kb.mm(ps[1][:, 256:512], lhsT=loT[0:64, 0, :], rhs=w2s[:], start=True, stop=False, r=["loT", "w2s"], w=[PS[1]], sig=False)
                        kb.mm(ps[1][:, 256:512], lhsT=ones[0:1, :], rhs=w0r[:], start=False, stop=True, r=["C", "w0r"], w=[PS[1]])
                        kb.mm(ps[2][:, 0:256], lhsT=loT[0:64, 1, :], rhs=a2s[:], start=True, stop=False, r=["loT", "a2s"], w=[PS[2]], sig=False)
                        kb.mm(ps[2][:, 0:256], lhsT=ones[0:1, :], rhs=a0r[:], start=False, stop=True, r=["C", "a0r"], w=[PS[2]])
                        kb.mm(ps[2][:, 256:512], lhsT=loT[:, 2, :], rhs=g2s[:], start=True, stop=True, r=["loT", "g2s"], w=[PS[2]])
                        if hg == 0 and i == 0:
                            stop_at(52)
                        kb.cp("act", Rr, ps[0][:, 0:256], r=[PS[0]], w=[tbres(0)])
                        kb.cp("act", Kk, ps[0][:, 256:512], r=[PS[0]], w=[tbres(1)])
                        kb.cp("act", Vv[:], ps[1][:, 0:256], r=[PS[1]], w=["Vv"])
                        kb.cp("act", Vf, ps[1][:, 0:256], r=[PS[1]], w=["tA0"])
                        kb.act(SG, ps[1][:, 256:512], AF.Sigmoid, r=[PS[1]], w=[tbres(4)])
                        kb.act(Aa, ps[2][:, 0:256], AF.Sigmoid, r=[PS[2]], w=[tbres(3)])
                        kb.cp("act", Gt, ps[2][:, 256:512], r=[PS[2]], w=["tA0"])
                        kb.tt("dve", KKn, Kk, bcs["kkb"][:], ALU.mult, r=[tbres(1), "kkb"], w=[tbres(2)])
                        kb.tt("dve", Tt, KKn, KKn, ALU.mult, r=[tbres(2)], w=[tbres(7)])
                        s.add("dve", lambda g_: g_.tensor_reduce(out=small[:, 0:4], in_=Tt.rearrange("p (h k) -> p h k", k=64), axis=AX.X, op=ALU.add),
                              r=[tbres(7)], w=["tA1"], tag="red")
                        kb.act(small[:, 0:4], small[:, 0:4], AF.Sqrt, r=["tA1"], w=["tA1"])
                        kb.ts("dve", small[:, 0:4], small[:, 0:4], 1e-12, None, ALU.max, None, r=["tA1"], w=["tA1"])
                        s.add("dve", lambda g_: g_.reciprocal(out=small[:, 0:4], in_=small[:, 0:4]), r=["tA1"], w=["tA1"], tag="recip")
                        kb.tt("dve", KKn.rearrange("p (h k) -> p h k", k=64), KKn.rearrange("p (h k) -> p h k", k=64),
                              small[:, 0:4].unsqueeze(2).to_broadcast([128, 4, 64]), ALU.mult, r=[tbres(2), "tA1"], w=[tbres(2)])
                        kb.stt(Tt, Aa, -1.0, bcs["kab"][:], ALU.add, ALU.mult, r=[tbres(3), "kab"], w=[tbres(7)])
                        kb.tt("dve", Tt, Tt, Kk, ALU.mult, r=[tbres(7), tbres(1)], w=[tbres(7)])
                        kb.tt("dve", Kk, Kk, Tt, ALU.add, r=[tbres(1), tbres(7)], w=[tbres(1)])
                        kb.tt("pool", Tt, Rr, Kk, ALU.mult, r=[tbres(0), tbres(1)], w=[tbres(7)])
                        kb.tt("pool", Tt, Tt, bcs["rkb"][:], ALU.mult, r=[tbres(7), "rkb"], w=[tbres(7)])
                        s.add("dve", lambda g_: g_.tensor_reduce(out=small[:, 4:8], in_=Tt.rearrange("p (h k) -> p h k", k=64), axis=AX.X, op=ALU.add),
                              r=[tbres(7)], w=["tA1"], tag="red")
                        kb.tt("pool", Aa, KKn, Aa, ALU.mult, r=[tbres(2), tbres(3)], w=[tbres(3)])
                        if hg == 0 and i == 0:
                            stop_at(53)
                        Um, Jm = (maskP, ones) if not sample else (maskS, blkS)
                        kb.mm(ps[4][:, 0:256], lhsT=Um, rhs=SG, start=True, stop=True, r=["C", tbres(4)], w=[PS[4]])
                        kb.mm(ps[4][:, 256:512], lhsT=Jm, rhs=SG, start=True, stop=True, r=["C", tbres(4)], w=[PS[4]])
                        nq = 1 if not sample else 16
                        for hl in range(4):
                            kb.mm(ps[3][0:64, 384 + hl * nq:384 + (hl + 1) * nq], lhsT=SG[:, hl * 64:(hl + 1) * 64], rhs=(ones[:, 0:1] if not sample else bms),
                                  start=True, stop=True, r=[tbres(4), "C"], w=[PS[3]], sig=(hl == 3))
                        kb.act(PLc[:, 0:4 * nq], ps[3][0:64, 384:384 + 4 * nq], AF.Exp, r=[PS[3]], w=["tA1"], scale=CW)
                        kb.cp("act", CSs, ps[4][:, 0:256], r=[PS[4]], w=[tbres(5)])
                        kb.tt("dve", Tt, CSs, SG, ALU.subtract, r=[tbres(5), tbres(4)], w=[tbres(7)])
                        kb.act(Ee, Tt, AF.Exp, r=[tbres(7)], w=[tbres(6)], scale=CW)
                        kb.stt(AL[:], KKn, -1.0, Ee, ALU.mult, ALU.mult, r=[tbres(2), tbres(6)], w=["AL"])
                        kb.act(Ee, CSs, AF.Exp, r=[tbres(5), "AL"], w=[tbres(6)], scale=-CW)
                        kb.tt("dve", BT[:], Aa, Ee, ALU.mult, r=[tbres(3), tbres(6)], w=["BTt"])
                        kb.tt("pool", KT, Kk, Ee, ALU.mult, r=[tbres(1), tbres(6)], w=["X1kt", "X10", "X11", "X12", "X13"])
                        kb.act(Tt, CSs, AF.Exp, r=[tbres(5)], w=[tbres(7)], scale=CW)
                        kb.tt("dve", RB[:], Rr, Tt, ALU.mult, r=[tbres(0), tbres(7)], w=["RB"])
                        kb.tt("dve", Ee, ps[4][:, 256:512], CSs, ALU.subtract, r=[PS[4], tbres(5), "tA0", "X1kt"], w=[tbres(6)])
                        kb.act(Ee, Ee, AF.Exp, r=[tbres(6)], w=[tbres(6)], scale=CW)
                        kb.tt("dve", KH[:], Kk, Ee, ALU.mult, r=[tbres(1), tbres(6)], w=["KH"])
                        kb.tt("pool", BH[:], Aa, Ee, ALU.mult, r=[tbres(3), tbres(6)], w=["BH"])
                        for qi, (nm, src, sr) in enumerate((("alT", AL, "AL"), ("btT", BT, "BTt"), ("ktT", KT, "X1kt"), ("rbT", RB, "RB"))):
                            b_ = 5 + qi % 2
                            for hl in range(4):
                                kb.tr(psb[b_][0:64, hl * 128:(hl + 1) * 128], src[:, hl * 64:(hl + 1) * 64], identR[:],
                                      r=[sr, "identR"], w=[PS[b_]], sig=(hl == 3))
                            kb.cp("act" if qi % 2 else "dve", fmT[nm][:].rearrange("p a b -> p (a b)"), psb[b_][0:64, 0:512], r=[PS[b_]], w=[nm])
                        if hg == 0 and i == 0:
                            stop_at(54)
                        alT, btT, ktT, rbT = fmT["alT"], fmT["btT"], fmT["ktT"], fmT["rbT"]
                        mlow, mup, minc = (lowP, upP, maskP) if not sample else (lowS, upS, maskS)
                        n_it = 3 if not sample else 2

                        def head_alg(hl):
                            B_ = 4 + hl
                            P_ = PS[B_]
                            rn = lambda nm: "%s%d" % (nm, hl)
                            pw = ps[B_][:, 0:128]

                            def mm1(lt, rh, rd, cols=128, rows=128, first=True, last=True):
                                kb.mm(ps[B_][0:rows, 0:cols], lhsT=lt, rhs=rh, start=first, stop=last, r=rd, w=[P_], sig=last)

                            mm1(fsl(alT, hl), fsl(btT, hl), ["alT", "btT"])
                            if not sample:
                                kb.tt("dve", chn["Am"][:, hl, :], pw, lowP, ALU.mult, r=[P_, "C"], w=[rn("Am")])
                                kb.tt("dve", chn["ApA"][:, hl, :], pw, low16, ALU.mult, r=[P_, "C"], w=[rn("ApA")])
                            else:
                                kb.tt("dve", chn["ApA"][:, hl, :], pw, lowS, ALU.mult, r=[P_, "C"], w=[rn("ApA")])
                            yield
                            mm1(fsl(btT, hl), fsl(alT, hl), ["alT", "btT"])
                            kb.tt("dve", chn["BpA"][:, hl, :], pw, (up16 if not sample else upS), ALU.mult, r=[P_, "C"], w=[rn("BpA")])
                            yield
                            mm1(fsl(ktT, hl), fsl(alT, hl), ["alT", "ktT"])
                            kb.tt("dve", AakT[:, hl, :], pw, mup, ALU.mult, r=[P_, "C"], w=[rn("AakT")])
                            yield
                            mm1(fsl(btT, hl), fsl(rbT, hl), ["rbT", "btT"])
                            kb.tt("dve", ArbT[:, hl, :], pw, minc, ALU.mult, r=[P_, "C"], w=[rn("ArbT")])
                            yield
                            mm1(fsl(ktT, hl), fsl(rbT, hl), ["rbT", "ktT"])
                            kb.tt("dve", ArkT[:, hl, :], pw, minc, ALU.mult, r=[P_, "C"], w=[rn("ArkT")])
                            yield
                            kb.tt("dve", chn["TTa"][:, hl, :], chn["BpA"][:, hl, :], ident, ALU.add, r=[rn("BpA"), "C"], w=[rn("TTa")])
                            Ap, Bp, TT = "ApA", "BpA", "TTa"
                            for it in range(n_it):
                                Ap2 = "ApB" if Ap == "ApA" else "ApA"
                                Bp2 = "BpB" if Bp == "BpA" else "BpA"
                                TT2 = "TTb" if TT == "TTa" else "TTa"
                                mm1(chn[Bp][:, hl, :], chn[Ap][:, hl, :], [rn(Ap), rn(Bp)])
                                kb.cp("act", chn[Ap2][:, hl, :], pw, r=[P_], w=[rn(Ap2)])
                                yield
                                if it < n_it - 1:
                                    mm1(chn[Ap][:, hl, :], chn[Bp][:, hl, :], [rn(Ap), rn(Bp)])
                                    kb.cp("act", chn[Bp2][:, hl, :], pw, r=[P_], w=[rn(Bp2)])
                                    yield
                                mm1(chn[Ap2][:, hl, :], chn[TT][:, hl, :], [rn(Ap2), rn(TT)])
                                kb.tt("dve", chn[TT2][:, hl, :], pw, chn[TT][:, hl, :], ALU.add, r=[P_, rn(TT)], w=[rn(TT2)])
                                yield
                                Ap, Bp, TT = Ap2, Bp2, TT2
                            if not sample:
                                for mk in (m16, m32, m64):
                                    TT2 = "TTb" if TT == "TTa" else "TTa"
                                    kb.tr(psb[B_][:, 0:128], chn[TT][:, hl, :], identR[:], r=[rn(TT), "identR"], w=[P_])
                                    kb.cp("act", chn["ApB"][:, hl, :], psb[B_][:, 0:128], r=[P_], w=[rn("ApB")])
                                    kb.tt("dve", chn["ApA"][:, hl, :], chn["Am"][:, hl, :], mk, ALU.mult, r=[rn("Am"), "C"], w=[rn("ApA")])
                                    yield
                                    mm1(chn["ApA"][:, hl, :], chn[TT][:, hl, :], [rn("ApA"), rn(TT)])
                                    kb.cp("act", chn["BpA"][:, hl, :], pw, r=[P_], w=[rn("BpA")])
                                    yield
                                    mm1(chn["ApB"][:, hl, :], chn["BpA"][:, hl, :], [rn("ApB"), rn("BpA")])
                                    kb.tt("dve", chn[TT2][:, hl, :], pw, chn[TT][:, hl, :], ALU.add, r=[P_, rn(TT)], w=[rn(TT2)])
                                    yield
                                    TT = TT2
                            TTh = chn[TT][:, hl, :]
                            mm1(TTh, tsl(AL, hl), [rn(TT), "AL"], cols=64)
                            kb.cp("act", Ahat[:, hl, :], ps[B_][:, 0:64], r=[P_], w=[rn("Ahat")])
                            yield
                            mm1(AakT[:, hl, :], tsl(Vv, hl), [rn("AakT"), "Vv"], cols=64)
                            kb.cp("dve", X1[:, hl, :], ps[B_][:, 0:64], r=[P_], w=[rn("X1")])
                            yield
                            mm1(TTh, X1[:, hl, :], [rn(TT), rn("X1")], cols=64)
                            kb.cp("act", U0[:, hl, :], ps[B_][:, 0:64], r=[P_], w=[rn("U0")])
                            yield
                            mm1(tsl(RB, hl), identR[:], ["RB", "identR"], rows=64, last=False)
                            mm1(Ahat[:, hl, :], ArbT[:, hl, :], [rn("Ahat"), rn("ArbT")], rows=64, first=False)
                            kb.cp("dve", RhT[:, hl, :], ps[B_][0:64, 0:128], r=[P_], w=[rn("RhT")])
                            yield
                            if sample:
                                return
                            mm1(Ahat[:, hl, :], tsl(BH, hl), [rn("Ahat"), "BH"], cols=64, rows=64)
                            kb.stt(Gm[:, hl, :], ident[0:64, 0:64], PLc[:, hl:hl + 1], ps[B_][0:64, 0:64], ALU.mult, ALU.add, r=["C", "tA1", P_], w=[rn("Gm")])
                            yield
                            mm1(ArkT[:, hl, :], tsl(Vv, hl), [rn("ArkT"), "Vv"], cols=64, last=False)
                            mm1(ArbT[:, hl, :], U0[:, hl, :], [rn("ArbT"), rn("U0")], cols=64, first=False, last=False)
                            mm1(RhT[:, hl, :], S0T[:, hl, :], [rn("RhT"), rn("S0T")], cols=64, first=False)
                            kb.cp("act", yb[:, hl * 64:(hl + 1) * 64], ps[B_][:, 0:64], r=[P_], w=["tA1"])
                            yield
                            if i == 15:
                                mm1(tsl(Vv, hl), tsl(KH, hl), ["Vv", "KH"], cols=64, rows=64, last=False)
                                mm1(U0[:, hl, :], tsl(BH, hl), [rn("U0"), "BH"], cols=64, rows=64, first=False, last=False)
                                mm1(S0T[:, hl, :], Gm[:, hl, :], [rn("S0T"), rn("Gm")], cols=64, rows=64, first=False)
                                kb.cp("dve", SLo[:, hl, :], ps[B_][0:64, 0:64], r=[P_], w=["S0nat"])
                                yield
                            mm1(tsl(KH, hl), tsl(Vv, hl), ["Vv", "KH"], cols=64, rows=64, last=False)
                            mm1(tsl(BH, hl), U0[:, hl, :], [rn("U0"), "BH"], cols=64, rows=64, first=False, last=False)
                            mm1(Gm[:, hl, :], S0T[:, hl, :], [rn("S0T"), rn("Gm")], cols=64, rows=64, first=False)
                            kb.cp("dve", S0T[:, hl, :], ps[B_][0:64, 0:64], r=[P_], w=[rn("S0T")])
                            yield

                        gens = [head_alg(hl) for hl in range(4)]
                        while gens:
                            for g_ in list(gens):
                                try:
                                    next(g_)
                                except StopIteration:
                                    gens.remove(g_)
                        if not sample:
                            if i == 15:
                                kb.dma("sp", o_pwkv[hg * 4:hg * 4 + 4].rearrange("h v k -> v h k"), SLo[:, 0:4, :], r=["S0nat"], w=())
                        else:
                            sample_states()
                        if hg == 0 and i == 0:
                            stop_at(57)
                        if hg == 0 and i == 16:
                            stop_at(60)
                        for hl in range(4):
                            s.add("dve", lambda g_, hl=hl: g_.bn_stats(out=small[:, 8 + hl * 6:14 + hl * 6], in_=yb[:, hl * 64:(hl + 1) * 64]), r=["tA1"], w=["tA1"], tag="bnst")
                        for hl in range(4):
                            s.add("dve", lambda g_, hl=hl: g_.bn_aggr(out=small[:, 32 + hl * 2:34 + hl * 2], in_=small[:, 8 + hl * 6:14 + hl * 6]), r=["tA1"], w=["tA1"], tag="bnag")
                        mvv = small[:, 32:40].rearrange("p (h two) -> p h two", two=2)
                        kb.act(small[:, 40:44].unsqueeze(2), mvv[:, :, 1:2], AF.Sqrt, r=["tA1"], w=["tA1"], bias=64e-5, scale=1.0)
                        s.add("dve", lambda g_: g_.reciprocal(out=small[:, 40:44], in_=small[:, 40:44]), r=["tA1"], w=["tA1"], tag="recip")
                        kb.tt("dve", small[:, 44:48].unsqueeze(2), mvv[:, :, 0:1], small[:, 40:44].unsqueeze(2), ALU.mult, r=["tA1"], w=["tA1"])
                        kb.ts("dve", small[:, 44:48], small[:, 44:48], -1.0, None, ALU.mult, None, r=["tA1"], w=["tA1"])
                        for hl in range(4):
                            kb.act(yb[:, hl * 64:(hl + 1) * 64], yb[:, hl * 64:(hl + 1) * 64], AF.Identity, r=["tA1", "tA1"], w=["tA1"],
                                   bias=small[:, 44 + hl:45 + hl], scale=small[:, 40 + hl:41 + hl])
                        kb.tt("pool", yb[:], yb[:], bcs["lngb"][:], ALU.mult, r=["tA1", "lngb"], w=["tA1"])
                        kb.tt("pool", yb[:], yb[:], bcs["lnbb"][:], ALU.add, r=["tA1", "lnbb"], w=["tA1"])
                        kb.tt("dve", Tt.rearrange("p (h k) -> p h k", k=64), Vf.rearrange("p (h k) -> p h k", k=64),
                              small[:, 4:8].unsqueeze(2).to_broadcast([128, 4, 64]), ALU.mult, r=["tA0", "tA1"], w=[tbres(7)])
                        kb.tt("dve", yb[:], yb[:], Tt, ALU.add, r=["tA1", tbres(7)], w=["tA1"])
                        kb.tt("dve", yb[:], yb[:], Gt, ALU.mult, r=["tA1", "tA0"], w=["tA1"])
                        for cc in range(2):
                            kb.tr(ps[7][:, cc * 128:(cc + 1) * 128], yb[:, cc * 128:(cc + 1) * 128], ident, r=["tA1", "C"], w=[PS[7]], sig=(cc == 1))
                        kb.cp("act", ygT[:, 2 * hg:2 * hg + 2, i * 128:(i + 1) * 128], ps[7][:, 0:256].rearrange("p (c t) -> p c t", t=128), r=[PS[7]], w=["ygT%d" % i])
                    s.barrier()
            for c in range(8):
                kb.tr(ps[0][0:17, c * 128:(c + 1) * 128] if c < 4 else ps[1][0:17, (c - 4) * 128:(c - 3) * 128], hlast[:, c, :], ident, r=["hlast", "C"],
                      w=[PS[0] if c < 4 else PS[1]], sig=(c in (3, 7)))
            kb.cp("dve", tB[0][0:17, 0:512], ps[0][0:17, :], r=[PS[0]], w=["tB0"])
            kb.cp("dve", tB[0][0:17, 512:1024], ps[1][0:17, :], r=[PS[1]], w=["tB0"])
            kb.dma("sp", o_pshift, tB[0][0:1, :], r=["tB0"], w=())
            kb.dma("sp", o_sshift, tB[0][1:17, :], r=["tB0"], w=())
            s.barrier()
            with contextlib.ExitStack() as pc_:
                alloc_gl(pc_)
                mod_prepare(l, 1, 1.0, blocks=[4, 5])
                Gp, Gs = gl["Gp"], gl["Gs"]
                wout = kb.sb(pc_, "wo_sb", [128, 8, D], BF16)
                kb.dma("pool", wout[:], rw_wo[0].rearrange("(kc p) n -> p kc n", p=128), r=(), w=["wout"])
                proj_acc(wout, "wout", 8, lambda i, kc: (ygT[:, kc, i * 128:(i + 1) * 128], "ygT%d" % i), True, True)
                s.barrier()
        s.barrier()

    def dump_and_finish():
        for i in range(NT):
            kb.dma("sp", yout[i * 128:(i + 1) * 128, :], X[:, i, :], r=["X%d" % i], w=())

    stage = 0
    for l in range(2):
        for sub in range(3):
            if sub == 0:
                ffn(l, 0, 0, 0.5)
            elif sub == 2:
                ffn(l, 1, 2, 0.5)
            elif l == 0:
                ab_mixer(l)
            else:
                rwkv_mixer(l)
            stage += 1
            if stage >= upto:
                return dump_and_finish()
    dump_and_finish()


def build(upto=99, stop_point=None):
    kb = KB()
    kb.stop_point = stop_point
    with contextlib.ExitStack() as es:
        kb.es = es
        build_program(kb, upto)
        kb.s.emit(kb.nc, es)
    return kb


def core_inputs(inp, core, kb):
    sl = slice(16 * core, 16 * core + 16)
    xs = inp["x_sample"][sl].reshape(128, D)
    f32 = np.float32

    def fm(v):
        return np.ascontiguousarray(np.asarray(v, f32).reshape(-1, 128).T)

    m = {
        "x": np.concatenate([inp["x_prompt"][core], xs], axis=0),
        "c": np.concatenate([inp["c_prompt"][core:core + 1], inp["c_sample"][sl]], axis=0),
        "cst": CST_ARR,
    }
    cw = inp["rg_conv_w"][0]
    vecA = np.zeros((128, 32), f32)
    for c in range(4):
        for j in range(4):
            vecA[:, c * 4 + j] = cw[j, c * 128:(c + 1) * 128]
    vecA[:, 16:20] = fm(inp["rg_conv_b"][0])
    vecA[:, 20:24] = fm(inp["rg_b_a"][0])
    vecA[:, 24:28] = fm(inp["rg_b_x"][0])
    vecA[:, 28:32] = fm(inp["rg_lambda"][0])
    m["vecA"] = vecA
    m["bgT"] = np.ascontiguousarray(inp["mlstm_b_gates"][0].T)
    m["minitT"] = np.ascontiguousarray(inp["state_mlstm_m"][0, sl].T)
    m["smC"] = inp["state_mlstm_C"][0, sl]
    m["smn"] = inp["state_mlstm_n"][0, sl]
    m["srh"] = inp["state_rglru_h"][0, sl]
    m["srconv"] = inp["state_rglru_conv"][0, sl].reshape(48, 512)
    mu = inp["rw_mu"][0]
    muT = np.zeros((128, 48), f32)
    for j in range(6):
        muT[:, j * 8:(j + 1) * 8] = fm(mu[j])
    m["muT"] = muT
    m["rk_flat"] = inp["rw_r_k"].reshape(1, D)
    m["swkv"] = inp["state_rwkv_wkv"][0, sl]
    m["sshift"] = inp["state_rwkv_shift"][0, sl]
    for k in kb.dram:
        if k not in m and k in inp:
            m[k] = inp[k]
    return {k: np.ascontiguousarray(v, dtype=f32) for k, v in m.items() if k in kb.dram}


_CACHE = {}


def kernel(**inputs):
    inp = {k: np.asarray(v) for k, v in inputs.items()}
    if "kb" not in _CACHE:
        _CACHE["kb"] = build()
    kb = _CACHE["kb"]
    in_maps = [core_inputs(inp, c, kb) for c in range(NCORES)]
    res = run_bass_kernel_spmd(kb.nc, in_maps, core_ids=list(range(NCORES)))
    R = res.results
    f32 = np.float32
    cat = lambda key, f=(lambda a: a): np.stack([f(np.asarray(R[c][key], f32)) for c in range(NCORES)], axis=0)
    cats = lambda key, f=(lambda a: a): np.concatenate([f(np.asarray(R[c][key], f32)) for c in range(NCORES)], axis=0)
    y_prompt = cat("y", lambda a: a[:2048])
    y_sample = cats("y", lambda a: a[2048:].reshape(16, 8, D))
    outs = (
        y_prompt, y_sample,
        cat("o_pmC")[None], cat("o_pmn")[None], cat("o_pmm", lambda a: a[:, 0])[None],
        cat("o_prh", lambda a: a.reshape(512))[None], cat("o_prconv")[None],
        cat("o_pwkv")[None], cat("o_pshift", lambda a: a[0])[None],
        cats("o_smC")[None], cats("o_smn")[None], cats("o_smm", lambda a: a.T)[None],
        cats("o_srh")[None], cats("o_srconv", lambda a: a.reshape(16, 3, 512))[None],
        cats("o_swkv")[None], cats("o_sshift")[None],
    )
    return tuple(np.ascontiguousarray(o, dtype=f32) for o in outs)
```

```python
import contextlib
import numpy as np
import concourse.bass as bass
import concourse.mybir as mybir
from concourse.bass_utils import run_bass_kernel_spmd

F32 = mybir.dt.float32
BF16 = mybir.dt.bfloat16
F32R = mybir.dt.float32r
AF = mybir.ActivationFunctionType
ALU = mybir.AluOpType
AX = mybir.AxisListType

D = 1024
DFF = 2816
NT = 17
NTOK = NT * 128
ALPHA = 4.0 ** 0.25
LN_EPS = 1e-5
NCORES = 8


class Op:
    __slots__ = ("eng", "fn", "deps", "sig", "idx", "dma", "slot", "slot_total", "sigcount", "waits", "tag")


class Sched:
    ENGS = ["pe", "act", "dve", "pool", "sp"]

    def __init__(self, n_slots=40):
        self.q = {e: [] for e in self.ENGS}
        self.last_w = {}
        self.readers = {}
        self.n_slots = n_slots
        self.slot_rr = 0
        self.sw_rr = 0
        self.n_hw = n_slots - 12
        self.slot_total = [0] * n_slots
        self.slot_last = [None] * n_slots
        self.all_dma = []

    skip = False

    def add(self, eng, fn, r=(), w=(), sig=True, dma=False, tag=""):
        if self.skip:
            return None
        op = Op()
        op.eng, op.fn, op.sig, op.dma, op.tag = eng, fn, sig, dma, tag
        op.slot = None
        deps = []
        seen = set()

        def dep(o):
            if o is None or id(o) in seen:
                return
            seen.add(id(o))
            if (not dma) and eng == "pe" and o.eng == "pe" and not o.dma:
                return
            deps.append(o)

        for k in r:
            dep(self.last_w.get(k))
        for k in w:
            dep(self.last_w.get(k))
            for o in self.readers.get(k, {}).values():
                dep(o)
        if dma:
            if eng == "pool":
                slot = self.n_hw + (self.sw_rr % (self.n_slots - self.n_hw))
                self.sw_rr += 1
            else:
                slot = self.slot_rr % self.n_hw
                self.slot_rr += 1
            dep(self.slot_last[slot])
            self.slot_total[slot] += 16
            op.slot = slot
            op.slot_total = self.slot_total[slot]
            self.slot_last[slot] = op
            self.all_dma.append(op)
        op.deps = deps
        self.q[eng].append(op)
        op.idx = len(self.q[eng]) - 1
        key = ("dma", id(op)) if dma else eng
        for k in r:
            self.readers.setdefault(k, {})[key] = op
        for k in w:
            self.last_w[k] = op
            self.readers[k] = {}
        return op

    def barrier(self):
        if self.skip:
            return
        lasts = []
        for e in self.ENGS:
            comp = [o for o in self.q[e] if (not o.dma) and o.fn is not None]
            if comp:
                comp[-1].sig = True
                lasts.append(comp[-1])
        lasts += [o for o in self.slot_last if o is not None]
        for e in self.ENGS:
            op = Op()
            op.eng, op.sig, op.dma, op.tag, op.slot, op.fn = e, False, False, "barrier", None, None
            op.deps = list(lasts)
            self.q[e].append(op)
            op.idx = len(self.q[e]) - 1
        self.last_w = {}
        self.readers = {}

    def finalize(self):
        for e in self.ENGS:
            for o in reversed(self.q[e]):
                if not o.dma and o.fn is not None and e != "sp":
                    o.sig = True
                    break
        self.sigtot = {}
        for e in self.ENGS:
            cnt = 0
            ops = self.q[e]
            pref = []
            for o in ops:
                if (not o.dma) and o.sig:
                    cnt += 1
                pref.append(cnt)
            self.sigtot[e] = cnt
            nxt = None
            for i in range(len(ops) - 1, -1, -1):
                o = ops[i]
                if (not o.dma) and o.sig:
                    nxt = pref[i]
                o.sigcount = nxt if not o.dma else None
        for e in self.ENGS:
            known = {}
            for o in self.q[e]:
                need = {}
                for d in o.deps:
                    if d.dma:
                        key, val = ("slot", d.slot), d.slot_total
                    else:
                        if d.sigcount is None:
                            raise RuntimeError("dependency on op with no later signal: %s" % d.tag)
                        key, val = ("eng", d.eng), d.sigcount
                    if val > need.get(key, 0):
                        need[key] = val
                o.waits = []
                for key, val in need.items():
                    if known.get(key, 0) < val:
                        known[key] = val
                        o.waits.append((key, val))

    def simulate(self):
        pc = {e: 0 for e in self.ENGS}
        sem = {}
        sigc = {e: 0 for e in self.ENGS}
        progress = True
        while progress:
            progress = False
            for e in self.ENGS:
                while pc[e] < len(self.q[e]):
                    o = self.q[e][pc[e]]
                    ok = all(sem.get(k, 0) >= v for k, v in o.waits)
                    if not ok:
                        break
                    if o.dma:
                        sem[("slot", o.slot)] = sem.get(("slot", o.slot), 0) + 16
                    elif o.sig:
                        sem[("eng", e)] = sem.get(("eng", e), 0) + 1
                    pc[e] += 1
                    progress = True
        stuck = {e: (pc[e], len(self.q[e])) for e in self.ENGS if pc[e] < len(self.q[e])}
        if stuck:
            msg = []
            for e, (p, n) in stuck.items():
                o = self.q[e][p]
                msg.append("%s stuck at %d/%d tag=%s waits=%s" % (e, p, n, o.tag, [(k, v, sem.get(k, 0)) for k, v in o.waits]))
            raise RuntimeError("DEADLOCK in wait graph:\n" + "\n".join(msg))

    def emit(self, nc, es):
        self.finalize()
        self.simulate()
        engsem = {e: es.enter_context(nc.semaphore("sem_" + e)) for e in ["pe", "act", "dve", "pool"]}
        slotsem = [es.enter_context(nc.semaphore("slot%d" % i)) for i in range(self.n_slots)]

        def semof(key):
            return engsem[key[1]] if key[0] == "eng" else slotsem[key[1]]

        def run(e, g):
            for o in self.q[e]:
                for key, val in o.waits:
                    g.wait_ge(semof(key), val)
                if o.fn is None:
                    continue
                ins = o.fn(g)
                if o.dma:
                    ins.then_inc(slotsem[o.slot], 16)
                elif o.sig:
                    ins.then_inc(engsem[e], 1)
            if e == "sp":
                for s in range(self.n_slots):
                    if self.slot_total[s] > 0:
                        g.wait_ge(slotsem[s], self.slot_total[s])

        with nc.Block() as blk:
            blk.tensor(lambda g: run("pe", g))
            blk.scalar(lambda g: run("act", g))
            blk.vector(lambda g: run("dve", g))
            blk.gpsimd(lambda g: run("pool", g))
            blk.sync(lambda g: run("sp", g))


class KB:
    def __init__(self, stop_after=None, debug=False):
        self.nc = bass.Bass("TRN2", target_bir_lowering=False)
        self.s = Sched()
        self.stop_after = stop_after
        self.debug = debug
        self.dram = {}

    def din(self, name, shape, dt=F32):
        t = self.nc.dram_tensor(name, list(shape), dt, kind="ExternalInput")
        self.dram[name] = t
        return t.ap()

    def dout(self, name, shape, dt=F32):
        t = self.nc.dram_tensor(name, list(shape), dt, kind="ExternalOutput")
        self.dram[name] = t
        return t.ap()

    def sb(self, es, name, shape, dt=F32):
        self.uid = getattr(self, "uid", 0) + 1
        return es.enter_context(self.nc.sbuf_tensor("%s_%d" % (name, self.uid), list(shape), dt))

    def mm(self, out, lhsT, rhs, start, stop, r, w, sig=None, tag="mm"):
        if sig is None:
            sig = stop
        return self.s.add("pe", lambda g: g.matmul(out, lhsT=lhsT, rhs=rhs, start=start, stop=stop), r=r, w=w, sig=sig, tag=tag)

    def tr(self, out, in_, ident, r, w, sig=True, tag="tr"):
        return self.s.add("pe", lambda g: g.transpose(out, in_, ident), r=r, w=w, sig=sig, tag=tag)

    def act(self, out, in_, func, r, w, bias=None, scale=None, eng="act", tag="act"):
        kw = {}
        if bias is not None:
            kw["bias"] = bias
        if scale is not None:
            kw["scale"] = scale
        return self.s.add("act", lambda g: g.activation(out=out, in_=in_, func=func, **kw), r=r, w=w, tag=tag)

    def tt(self, eng, out, in0, in1, op, r, w, tag="tt"):
        return self.s.add(eng, lambda g: g.tensor_tensor(out=out, in0=in0, in1=in1, op=op), r=r, w=w, tag=tag)

    def ts(self, eng, out, in0, s1, s2, op0, op1, r, w, tag="ts"):
        if op1 is None:
            return self.s.add(eng, lambda g: g.tensor_scalar(out=out, in0=in0, scalar1=s1, scalar2=None, op0=op0), r=r, w=w, tag=tag)
        return self.s.add(eng, lambda g: g.tensor_scalar(out=out, in0=in0, scalar1=s1, scalar2=s2, op0=op0, op1=op1), r=r, w=w, tag=tag)

    def stt(self, out, in0, scalar, in1, op0, op1, r, w, tag="stt"):
        return self.s.add("dve", lambda g: g.scalar_tensor_tensor(out=out, in0=in0, scalar=scalar, in1=in1, op0=op0, op1=op1), r=r, w=w, tag=tag)

    def cp(self, eng, out, in_, r, w, tag="cp"):
        if eng == "act":
            return self.s.add("act", lambda g: g.copy(out=out, in_=in_), r=r, w=w, tag=tag)
        return self.s.add(eng, lambda g: g.tensor_copy(out=out, in_=in_), r=r, w=w, tag=tag)

    def memset(self, eng, ap, val, w, tag="memset"):
        return self.s.add(eng, lambda g: g.memset(ap, val), r=(), w=w, tag=tag)

    def dma(self, q, out, in_, r, w, tag="dma", **kw):
        return self.s.add(q, lambda g: g.dma_start(out=out, in_=in_, **kw), r=r, w=w, dma=True, tag=tag)


def make_consts():
    c = {}
    c["ident"] = np.eye(128, dtype=np.float32)
    selP = np.zeros((128, 128), np.float32)
    selP[0, :] = 1.0
    selS = np.zeros((128, 128), np.float32)
    for p in range(128):
        selS[1 + p // 8, p] = 1.0
    c["selP"] = selP
    c["selS"] = selS
    st = np.arange(128)
    c["maskP"] = (st[:, None] <= st[None, :]).astype(np.float32)
    c["maskS"] = ((st[:, None] <= st[None, :]) & (st[:, None] // 8 == st[None, :] // 8)).astype(np.float32)
    c["rst"] = np.tile((st % 8 != 0).astype(np.float32)[None, :], (128, 1))
    c["rstm"] = np.tile(np.where(st % 8 == 0, -1e30, 0.0).astype(np.float32)[None, :], (128, 1))
    bms = np.zeros((128, 128), np.float32)
    bms[st, st // 8] = 1.0
    c["bms"] = bms
    c["ones"] = np.ones((128, 128), np.float32)
    same = (st[:, None] // 8 == st[None, :] // 8)
    c["upP"] = (st[:, None] < st[None, :]).astype(np.float32)
    c["lowP"] = (st[:, None] > st[None, :]).astype(np.float32)
    c["upS"] = ((st[:, None] < st[None, :]) & same).astype(np.float32)
    c["lowS"] = ((st[:, None] > st[None, :]) & same).astype(np.float32)
    c["blkS"] = same.astype(np.float32)
    blk = lambda b: (st[:, None] // b == st[None, :] // b)
    c["low16"] = ((st[:, None] > st[None, :]) & blk(16)).astype(np.float32)
    c["up16"] = ((st[:, None] < st[None, :]) & blk(16)).astype(np.float32)
    for b in (16, 32, 64):
        c["m%d" % b] = (blk(2 * b) & ~blk(b)).astype(np.float32)
    names = list(c.keys())
    arr = np.concatenate([c[k] for k in names], axis=1)
    offs = {}
    o = 0
    for k in names:
        offs[k] = o
        o += c[k].shape[1]
    return arr, offs


CST_ARR, CST_OFF = make_consts()
NCST = CST_ARR.shape[1]

FFN_PARTS = [(0, 4), (4, 4), (8, 4), (12, 4), (16, 4), (20, 2)]
TGS = [(0, 512), (512, 512), (1024, 512), (1536, 512), (2048, 128)]


def build_program(kb, upto=99):
    nc, s = kb.nc, kb.s
    es = kb.es
    xin = kb.din("x", [NTOK, D])
    cin = kb.din("c", [NT, D])
    cst = kb.din("cst", [128, NCST])
    ada_w = kb.din("ada_w", [2, D, 9 * D])
    ada_b = kb.din("ada_b", [2, 9 * D])
    ln_g = kb.din("ln_g", [2, 3, D])
    ln_b = kb.din("ln_b", [2, 3, D])
    ffn_w1 = kb.din("ffn_w1", [2, 2, D, DFF])
    ffn_w3 = kb.din("ffn_w3", [2, 2, D, DFF])
    ffn_w2 = kb.din("ffn_w2", [2, 2, DFF, D])
    yout = kb.dout("y", [NTOK, D])

    X = kb.sb(es, "X", [128, NT, D], F32)
    C = kb.sb(es, "cst_sb", [128, NCST], F32)
    cT = kb.sb(es, "cT", [128, 8, NT], BF16)
    onesb = kb.sb(es, "onesb", [1, 32], F32)
    modT = kb.sb(es, "modT", [128, 16, NT], F32)
    gl = {}

    def alloc_gl(stack):
        gl["Gp"] = kb.sb(stack, "Gp", [128, D], F32)
        gl["Gs"] = kb.sb(stack, "Gs", [128, D], F32)
        gl["LNg"] = kb.sb(stack, "LNg", [128, D], F32)
        gl["LNb"] = kb.sb(stack, "LNb", [128, D], F32)
    tA = [kb.sb(es, "tA%d" % i, [128, 512], F32) for i in range(4)]
    tB = [kb.sb(es, "tB%d" % i, [128, D], F32) for i in range(2)]
    stt_ = [kb.sb(es, "bnst%d" % i, [128, 2, 6], F32) for i in range(2)]
    mv = [kb.sb(es, "mv%d" % i, [128, 2], F32) for i in range(2)]
    rstd = [kb.sb(es, "rstd%d" % i, [128, 1], F32) for i in range(2)]
    nmr = [kb.sb(es, "nmr%d" % i, [128, 1], F32) for i in range(2)]
    tmpS = kb.sb(es, "tmpS", [128, 128], F32)
    ps = [es.enter_context(nc.psum_tensor("ps%d" % i, [128, 512], F32)) for i in range(8)]
    PS = ["ps%d" % i for i in range(8)]
    psb = [p.bitcast(BF16) for p in ps]

    ident = C[:, CST_OFF["ident"]:CST_OFF["ident"] + 128]
    selP = C[0:NT, CST_OFF["selP"]:CST_OFF["selP"] + 128]
    selS = C[0:NT, CST_OFF["selS"]:CST_OFF["selS"] + 128]

    kb.dma("sp", C[:], cst, r=(), w=["C"])
    for i in range(NT):
        kb.dma("sp", X[:, i, :], xin[i * 128:(i + 1) * 128, :], r=(), w=["X%d" % i])
    kb.memset("pool", onesb[:], 1.0, w=["onesb"])

    with contextlib.ExitStack() as ph0:
        c_sb = kb.sb(ph0, "c_sb", [NT, D], F32)
        cs_sb = kb.sb(ph0, "cs_sb", [NT, D], F32)
        kb.dma("sp", c_sb[:], cin, r=(), w=["c_sb"])
        kb.act(cs_sb[:], c_sb[:], AF.Silu, r=["c_sb"], w=["cs_sb"])
        for kc in range(8):
            kb.tr(ps[0][:, kc * NT:(kc + 1) * NT], cs_sb[0:NT, kc * 128:(kc + 1) * 128], C[0:NT, 0:NT],
                  r=["cs_sb", "C"], w=[PS[0]], sig=(kc == 7))
        kb.cp("dve", cT[:].rearrange("p a b -> p (a b)"), ps[0][:, 0:8 * NT], r=[PS[0]], w=["cT"])
    s.barrier()

    state = {"ada_i": 0, "ada_i2": 0, "psr": 0}

    def mod_prepare(l, sub, res_w, blocks=range(6)):
        if 5 in blocks:
            Gp, Gs = gl["Gp"], gl["Gs"]
            kb.dma("sp", gl["LNg"][:], ln_g[l, sub:sub + 1, :].to_broadcast([128, D]), r=(), w=["LNg"])
            kb.dma("sp", gl["LNb"][:], ln_b[l, sub:sub + 1, :].to_broadcast([128, D]), r=(), w=["LNb"])
        phm = contextlib.ExitStack()
        modst = [kb.sb(phm, "modst%d" % i, [NT, 512], F32) for i in range(2)]
        adaw = [kb.sb(phm, "adaw%d" % i, [128, 8, 256], BF16) for i in range(2)]
        adab = [kb.sb(phm, "adab%d" % i, [1, 512], F32) for i in range(2)]
        for b in blocks:
            i = state["ada_i"]
            state["ada_i"] += 1
            buf = i % 2
            co = sub * 3 * D + b * 512
            kb.dma("sp", adab[buf][:], ada_b[l:l + 1, co:co + 512], r=(), w=["adab%d" % buf])
            pm = 4 + (i % 2)
            for sbk in range(2):
                i2 = state["ada_i2"]
                state["ada_i2"] += 1
                wb = i2 % 2
                kb.dma("pool", adaw[wb][:], ada_w[l].rearrange("(kc p) n -> p kc n", p=128)[:, :, co + sbk * 256:co + (sbk + 1) * 256],
                       r=(), w=["adaw%d" % wb])
                for kc in range(8):
                    kb.mm(ps[pm][0:NT, sbk * 256:(sbk + 1) * 256], lhsT=cT[:, kc, :], rhs=adaw[wb][:, kc, :], start=(kc == 0), stop=False,
                          r=["cT", "adaw%d" % wb], w=[PS[pm]], sig=False)
                kb.mm(ps[pm][0:NT, sbk * 256:(sbk + 1) * 256], lhsT=onesb[0:1, 0:NT], rhs=adab[buf][:, sbk * 256:(sbk + 1) * 256], start=False, stop=True,
                      r=["onesb", "adab%d" % buf], w=[PS[pm]], sig=True)
            kb.cp("act", modst[buf][:], ps[pm][0:NT, :], r=[PS[pm]], w=["modst%d" % buf])
            if b < 4:
                for cc in range(4):
                    j = b * 4 + cc
                    kb.tr(ps[6][:, j * NT:(j + 1) * NT], modst[buf][0:NT, cc * 128:(cc + 1) * 128], C[0:NT, 0:NT],
                          r=["modst%d" % buf, "C"], w=[PS[6]], sig=(cc == 3))
                if b == 1:
                    kb.cp("dve", modT[:, 0:8, :].rearrange("p a b -> p (a b)"), ps[6][:, 0:8 * NT], r=[PS[6]], w=["modT"])
                if b == 3:
                    kb.ts("dve", modT[:, 8:16, :].rearrange("p a b -> p (a b)"), ps[6][:, 8 * NT:16 * NT], 1.0, None,
                          ALU.add, None, r=[PS[6]], w=["modT"])
            else:
                h = b - 4
                kb.mm(ps[7][:, :], lhsT=selP, rhs=modst[buf][:], start=True, stop=True, r=["C", "modst%d" % buf], w=[PS[7]])
                kb.act(Gp[:, h * 512:(h + 1) * 512], ps[7][:, :], AF.Identity, r=[PS[7]], w=["Gp"], bias=float(res_w), scale=float(res_w))
                kb.mm(ps[7][:, :], lhsT=selS, rhs=modst[buf][:], start=True, stop=True, r=["C", "modst%d" % buf], w=[PS[7]])
                kb.act(Gs[:, h * 512:(h + 1) * 512], ps[7][:, :], AF.Identity, r=[PS[7]], w=["Gs"], bias=float(res_w), scale=float(res_w))
        s.barrier()
        phm.close()

    def make_hT(hT, i, prescale=True, col0=None, res=None, hook=None):
        g = res if res is not None else "hT_g%d" % (i // 4)
        if col0 is None:
            col0 = i * 128
        for half in range(2):
            pb = state["psr"] % 4
            state["psr"] += 1
            for cc in range(4):
                c = half * 4 + cc
                kb.tr(ps[pb][:, cc * 128:(cc + 1) * 128], X[:, i, c * 128:(c + 1) * 128], ident,
                      r=["X%d" % i, "C"], w=[PS[pb]], sig=(cc == 3))
            for cc in range(4):
                c = half * 4 + cc
                src = ps[pb][:, cc * 128:(cc + 1) * 128]
                if hook is not None:
                    hook(i, c, src, PS[pb])
                if i < 16:
                    if cc % 2 == 0:
                        kb.act(hT[:, c, col0:col0 + 128], src, AF.Identity, r=[PS[pb], "modT"], w=[g],
                               bias=modT[:, c, 0:1], scale=modT[:, 8 + c, 0:1])
                    else:
                        kb.ts("dve", hT[:, c, col0:col0 + 128], src, modT[:, 8 + c, 0:1], modT[:, c, 0:1],
                              ALU.mult, ALU.add, r=[PS[pb], "modT"], w=[g])
                else:
                    sc = modT[:, 8 + c, 1:NT].unsqueeze(2).to_broadcast([128, 16, 8])
                    sh = modT[:, c, 1:NT].unsqueeze(2).to_broadcast([128, 16, 8])
                    kb.tt("dve", tmpS[:].rearrange("p (q t) -> p q t", t=8), src.rearrange("p (q t) -> p q t", t=8), sc,
                          ALU.mult, r=[PS[pb], "modT"], w=["tmpS"])
                    kb.tt("dve", hT[:, c, col0:col0 + 128].rearrange("p (q t) -> p q t", t=8),
                          tmpS[:].rearrange("p (q t) -> p q t", t=8), sh, ALU.add, r=["tmpS", "modT"], w=[g])

    def layer_norm(i):
        k = i % 2
        for h in range(2):
            s.add("dve", lambda g_, h=h, k=k, i=i: g_.bn_stats(out=stt_[k][:, h, :], in_=X[:, i, h * 512:(h + 1) * 512]),
                  r=["X%d" % i], w=["bnst%d" % k], tag="bnstats")
        s.add("dve", lambda g_, k=k: g_.bn_aggr(out=mv[k][:], in_=stt_[k][:].rearrange("p a b -> p (a b)")),
              r=["bnst%d" % k], w=["mv%d" % k], tag="bnaggr")
        kb.act(rstd[k][:], mv[k][:, 1:2], AF.Sqrt, r=["mv%d" % k], w=["rstd%d" % k], bias=float(LN_EPS), scale=1.0)
        s.add("dve", lambda g_, k=k: g_.reciprocal(out=rstd[k][:], in_=rstd[k][:]), r=["rstd%d" % k], w=["rstd%d" % k], tag="recip")
        kb.ts("dve", nmr[k][:], mv[k][:, 0:1], rstd[k][:, 0:1], -1.0, ALU.mult, ALU.mult, r=["mv%d" % k, "rstd%d" % k], w=["nmr%d" % k])
        kb.act(tB[k][:], X[:, i, :], AF.Identity, r=["X%d" % i, "rstd%d" % k, "nmr%d" % k], w=["tB%d" % k],
               bias=nmr[k][:, 0:1], scale=rstd[k][:, 0:1])
        kb.tt("dve", tB[k][:], tB[k][:], gl["LNg"][:], ALU.mult, r=["tB%d" % k, "LNg"], w=["tB%d" % k])
        kb.tt("dve", X[:, i, :], tB[k][:], gl["LNb"][:], ALU.add, r=["tB%d" % k, "LNb"], w=["X%d" % i])

    pacc = {"v": 0}

    def proj_acc(wt, wres, nk, lhs_of, do_ln, first):
        Gp, Gs = gl["Gp"], gl["Gs"]
        for i in [16] + list(range(16)):
            if i == 0:
                kb.tt("dve", wt[:, 0:nk, :], wt[:, 0:nk, :], Gp[:].unsqueeze(1).to_broadcast([128, nk, D]), ALU.mult, r=[wres, "Gp"], w=[wres])
            for half in range(2):
                v = pacc["v"]
                pacc["v"] += 1
                py = 4 + (v % 4)
                for kc in range(nk):
                    lt, lres = lhs_of(i, kc)
                    kb.mm(ps[py][:, :], lhsT=lt, rhs=wt[:, kc, half * 512:(half + 1) * 512], start=(kc == 0), stop=(kc == nk - 1),
                          r=[lres, wres], w=[PS[py]])
                xs = X[:, i, half * 512:(half + 1) * 512]
                src = ps[py][:, :]
                rsrc = PS[py]
                if i == 16:
                    kb.tt("dve", tA[v % 4][:], ps[py][:, :], Gs[:, half * 512:(half + 1) * 512], ALU.mult, r=[PS[py], "Gs"], w=["tA%d" % (v % 4)])
                    src, rsrc = tA[v % 4][:], "tA%d" % (v % 4)
                if first:
                    kb.stt(xs, xs, float(ALPHA), src, ALU.mult, ALU.add, r=["X%d" % i, rsrc], w=["X%d" % i])
                else:
                    kb.tt("dve", xs, xs, src, ALU.add, r=["X%d" % i, rsrc], w=["X%d" % i])
            if do_ln:
                layer_norm(i)

    def ffn(l, f, sub, res_w):
        with contextlib.ExitStack() as ph:
            alloc_gl(ph)
            mod_prepare(l, sub, res_w)
            Gp, Gs = gl["Gp"], gl["Gs"]
            hT = kb.sb(ph, "hT", [128, 8, NTOK], BF16)
            w1p = kb.sb(ph, "w1p", [128, 8, 512], BF16)
            w3p = kb.sb(ph, "w3p", [128, 8, 512], BF16)
            w2p = kb.sb(ph, "w2p", [128, 4, D], BF16)
            gbuf = kb.sb(ph, "gbuf", [128, 4, NTOK], BF16)
            sil = [kb.sb(ph, "sil%d" % i, [128, 512], F32) for i in range(2)]
            w1v = ffn_w1[l, f].rearrange("(kc p) n -> p kc n", p=128)
            w3v = ffn_w3[l, f].rearrange("(kc p) n -> p kc n", p=128)
            w2v = ffn_w2[l, f].rearrange("(j p) n -> p j n", p=128)
            u = 0
            v = 0
            def load_up(pi_):
                j0_, n_ = FFN_PARTS[pi_]
                kb.dma("pool", w1p[:, :, 0:n_ * 128], w1v[:, :, j0_ * 128:(j0_ + n_) * 128], r=(), w=["w1p"])
                kb.dma("pool", w3p[:, :, 0:n_ * 128], w3v[:, :, j0_ * 128:(j0_ + n_) * 128], r=(), w=["w3p"])

            def load_dn(pi_):
                j0_, n_ = FFN_PARTS[pi_]
                kb.dma("pool", w2p[:, 0:n_, :], w2v[:, j0_:j0_ + n_, :], r=(), w=["w2p"])

            load_up(0)
            load_dn(0)
            for i in range(NT):
                make_hT(hT, i)
            for pi, (j0, ncn) in enumerate(FFN_PARTS):
                for tg, (t0, nt_) in enumerate(TGS):
                    for jj in range(ncn):
                        pa, pb = (2 * u) % 4, (2 * u + 1) % 4
                        for kc in range(8):
                            kb.mm(ps[pa][:, 0:nt_], lhsT=w1p[:, kc, jj * 128:(jj + 1) * 128], rhs=hT[:, kc, t0:t0 + nt_],
                                  start=(kc == 0), stop=(kc == 7), r=["w1p", "hT_g%d" % tg], w=[PS[pa]])
                        for kc in range(8):
                            kb.mm(ps[pb][:, 0:nt_], lhsT=w3p[:, kc, jj * 128:(jj + 1) * 128], rhs=hT[:, kc, t0:t0 + nt_],
                                  start=(kc == 0), stop=(kc == 7), r=["w3p", "hT_g%d" % tg], w=[PS[pb]])
                        kb.act(sil[u % 2][:, 0:nt_], ps[pa][:, 0:nt_], AF.Silu, r=[PS[pa]], w=["sil%d" % (u % 2)])
                        kb.tt("dve", gbuf[:, jj, t0:t0 + nt_], sil[u % 2][:, 0:nt_], ps[pb][:, 0:nt_], ALU.mult,
                              r=["sil%d" % (u % 2), PS[pb]], w=["g_g%d" % tg])
                        u += 1
                if pi + 1 < len(FFN_PARTS):
                    load_up(pi + 1)
                proj_acc(w2p, "w2p", ncn, lambda i, jj: (gbuf[:, jj, i * 128:(i + 1) * 128], "g_g%d" % (i // 4)), pi == len(FFN_PARTS) - 1, pi == 0)
                if pi + 1 < len(FFN_PARTS):
                    load_dn(pi + 1)
        s.barrier()

    DKS = float(128 ** -0.5)

    class _Stop(Exception):
        pass

    def stop_at(n):
        if getattr(kb, "stop_point", None) == n:
            s.barrier()
            s.skip = True

    def ab_mixer(l):
        ab_mixer_(l)
        s.skip = False
        s.barrier()

    def ab_mixer_(l):
        ab_w_in = kb.din("ab_w_in", [1, D, 3080])
        ab_w_out = kb.din("ab_w_out", [1, D, D])
        mnorm_g = kb.din("mlstm_norm_g", [1, 512])
        vecA_d = kb.din("vecA", [128, 32])
        bgT_d = kb.din("bgT", [4, 2])
        minitT_d = kb.din("minitT", [4, 16])
        rg_w_a = kb.din("rg_w_a", [1, 8, 64, 64])
        rg_w_x = kb.din("rg_w_x", [1, 8, 64, 64])
        smC = kb.din("smC", [16, 4, 128, 128])
        smn = kb.din("smn", [16, 4, 128])
        srh = kb.din("srh", [16, 512])
        srconv = kb.din("srconv", [48, 512])
        o_pmC = kb.dout("o_pmC", [4, 128, 128])
        o_pmn = kb.dout("o_pmn", [4, 128])
        o_pmm = kb.dout("o_pmm", [4, 1])
        o_prh = kb.dout("o_prh", [4, 128])
        o_prconv = kb.dout("o_prconv", [3, 512])
        o_smC = kb.dout("o_smC", [16, 4, 128, 128])
        o_smn = kb.dout("o_smn", [16, 4, 128])
        o_smm = kb.dout("o_smm", [4, 16])
        o_srh = kb.dout("o_srh", [16, 512])
        o_srconv = kb.dout("o_srconv", [48, 512])

        maskP = C[:, CST_OFF["maskP"]:CST_OFF["maskP"] + 128]
        maskS = C[:, CST_OFF["maskS"]:CST_OFF["maskS"] + 128]
        rst = C[:, CST_OFF["rst"]:CST_OFF["rst"] + 128]
        rstm = C[:, CST_OFF["rstm"]:CST_OFF["rstm"] + 128]
        bms = C[:, CST_OFF["bms"]:CST_OFF["bms"] + 16]
        ones = C[:, CST_OFF["ones"]:CST_OFF["ones"] + 128]

        mod_prepare(l, 1, 1.0, blocks=range(4))
        win_v = ab_w_in[0].rearrange("(kc p) n -> p kc n", p=128)
        with contextlib.ExitStack() as ph:
            hmT = kb.sb(ph, "hmT", [128, 4, NTOK], BF16)
            vecA = kb.sb(ph, "vecA_sb", [128, 32], F32)
            kb.dma("sp", vecA[:], vecA_d, r=(), w=["vecA"])
            sigo = tA[0]
            hmf = tB[0][:, 0:512]
            with contextlib.ExitStack() as pa:
                winA = kb.sb(pa, "winA", [128, 8, 2056], BF16)
                kb.dma("pool", winA[:, :, 0:1024], win_v[:, :, 0:1024], r=(), w=["winA"])
                kb.dma("pool", winA[:, :, 1024:2056], win_v[:, :, 1024:2056], r=(), w=["winA"])
                qkT = kb.sb(pa, "qkT", [128, 8, 512], BF16)
                hTg = [kb.sb(pa, "hTgA%d" % i, [128, 8, 512], BF16) for i in range(2)]
                bg = kb.sb(pa, "bg", [4, 2], F32)
                nbg1 = kb.sb(pa, "nbg1", [4, 1], F32)
                minitT = kb.sb(pa, "minitT_sb", [4, 16], F32)
                mng = kb.sb(pa, "mng", [128, 512], F32)
                kb.dma("sp", bg[:], bgT_d, r=(), w=["bg"])
                kb.dma("sp", minitT[:], minitT_d, r=(), w=["minitT"])
                kb.dma("sp", mng[:], mnorm_g[0:1, :].to_broadcast([128, 512]), r=(), w=["mng"])
                kb.ts("dve", nbg1[:], bg[:, 1:2], -1.0, None, ALU.mult, None, r=["bg"], w=["nbg1"])
                R4 = lambda nm: kb.sb(pa, nm, [4, 128], F32)
                t1, IGa, Rt, t3, t4 = R4("r_t1"), R4("r_ig"), R4("r_rt"), R4("r_t3"), R4("r_t4")
                Bc = [R4("r_bc0"), R4("r_bc1")]
                Mx = [R4("r_mx0"), R4("r_mx1")]
                dd = kb.sb(pa, "r_dd", [4, 16], F32)
                DDm = kb.sb(pa, "r_DD", [4, 64], F32)
                mout = kb.sb(pa, "r_mout", [4, 16], F32)
                colq = [kb.sb(pa, "colq%d" % i, [128, 16], F32) for i in range(2)]
                decsb = kb.sb(pa, "decsb", [128, 64], F32)
                kw = kb.sb(pa, "kw", [128, 4, 128], BF16)
                ktok = kb.sb(pa, "ktok", [128, 4, 128], BF16)
                vext = [kb.sb(pa, "vext%d" % i, [128, 4, 130], BF16) for i in range(2)]
                PT = kb.sb(pa, "PT", [128, 4, 128], BF16)
                Cst = kb.sb(pa, "Cst", [128, 4, 130], F32)
                Cb = kb.sb(pa, "Cb", [128, 4, 130], BF16)
                dmax = kb.sb(pa, "dmax", [128, 4], F32)
                hst6 = kb.sb(pa, "hst6", [128, 4, 6], F32)
                hmv = kb.sb(pa, "hmv", [128, 4, 2], F32)
                hrs = kb.sb(pa, "hrs", [128, 4], F32)
                hnm = kb.sb(pa, "hnm", [128, 4], F32)
                for vv in vext:
                    kb.memset("pool", vv[:], 1.0, w=["vext0", "vext1"])
                kb.memset("pool", Cst[:], 0.0, w=["Cst"])
                kb.memset("pool", Cb[:], 0.0, w=["Cb"])

                def rows(i):
                    k = i % 2
                    hb = (i // 4) % 2
                    hT = hTg[hb]
                    tc0 = (i % 4) * 128
                    pg = ps[7]
                    for kc in range(8):
                        kb.mm(pg[0:4, 0:128], lhsT=winA[:, kc, 2048:2052], rhs=hT[:, kc, tc0:tc0 + 128], start=(kc == 0), stop=(kc == 7),
                              r=["winA", "hTg%d" % hb], w=[PS[7]])
                    for kc in range(8):
                        kb.mm(pg[0:4, 128:256], lhsT=winA[:, kc, 2052:2056], rhs=hT[:, kc, tc0:tc0 + 128], start=(kc == 0), stop=(kc == 7),
                              r=["winA", "hTg%d" % hb], w=[PS[7]])
                    kb.act(IGa[:], pg[0:4, 0:128], AF.Identity, r=[PS[7], "bg"], w=["r_ig"], bias=bg[:, 0:1], scale=1.0)
                    kb.act(t1[:], pg[0:4, 128:256], AF.Exp, r=[PS[7], "nbg1"], w=["r_t1"], bias=nbg1[:, 0:1], scale=-1.0)
                    kb.act(t1[:], t1[:], AF.Ln, r=["r_t1"], w=["r_t1"], bias=1.0, scale=1.0)
                    kb.ts("dve", t1[:], t1[:], -1.0, None, ALU.mult, None, r=["r_t1"], w=["r_t1"])
                    prompt = i < 16
                    if prompt:
                        binit = 0.0 if i == 0 else Bc[1 - k][:, 127:128]
                        minit = 0.0 if i == 0 else Mx[1 - k][:, 127:128]
                        s.add("dve", lambda g_: g_.tensor_tensor_scan(out=Bc[k][:], data0=ones[0:4, :], data1=t1[:], initial=binit,
                                                                       op0=ALU.mult, op1=ALU.add),
                              r=["r_t1", "r_bc%d" % (1 - k), "C"], w=["r_bc%d" % k], tag="scanB")
                        kb.tt("dve", IGa[:], IGa[:], Bc[k][:], ALU.subtract, r=["r_ig", "r_bc%d" % k], w=["r_ig"])
                        kb.memset("dve", t3[:], 0.0, w=["r_t3"])
                        s.add("dve", lambda g_: g_.tensor_tensor_scan(out=Mx[k][:], data0=t3[:], data1=IGa[:], initial=minit,
                                                                       op0=ALU.add, op1=ALU.max),
                              r=["r_t3", "r_ig", "r_mx%d" % (1 - k)], w=["r_mx%d" % k], tag="scanM")
                        if i == 0:
                            kb.memset("dve", Rt[:], 0.0, w=["r_rt"])
                        else:
                            kb.cp("dve", Rt[:], Mx[1 - k][:, 127:128].to_broadcast([4, 128]), r=["r_mx%d" % (1 - k)], w=["r_rt"])
                    else:
                        s.add("dve", lambda g_: g_.tensor_tensor_scan(out=Bc[k][:], data0=rst[0:4, :], data1=t1[:], initial=0.0,
                                                                       op0=ALU.mult, op1=ALU.add),
                              r=["r_t1", "C"], w=["r_bc%d" % k], tag="scanB")
                        kb.tt("dve", IGa[:], IGa[:], Bc[k][:], ALU.subtract, r=["r_ig", "r_bc%d" % k], w=["r_ig"])
                        kb.cp("dve", t3[:], IGa[:], r=["r_ig"], w=["r_t3"])
                        kb.tt("dve", t3[:].rearrange("p (q t) -> p q t", t=8)[:, :, 0:1], IGa[:].rearrange("p (q t) -> p q t", t=8)[:, :, 0:1],
                              minitT[:].unsqueeze(2), ALU.max, r=["r_ig", "minitT"], w=["r_t3"])
                        s.add("dve", lambda g_: g_.tensor_tensor_scan(out=Mx[k][:], data0=rstm[0:4, :], data1=t3[:], initial=0.0,
                                                                       op0=ALU.add, op1=ALU.max),
                              r=["r_t3", "C"], w=["r_mx%d" % k], tag="scanM")
                        kb.cp("dve", Rt[:].rearrange("p (q t) -> p q t", t=8), minitT[:].unsqueeze(2).to_broadcast([4, 16, 8]),
                              r=["minitT"], w=["r_rt"])
                    kb.tt("dve", t3[:], IGa[:], Rt[:], ALU.subtract, r=["r_ig", "r_rt"], w=["r_t3"])
                    kb.act(t3[:], t3[:], AF.Exp, r=["r_t3"], w=["r_t3"])
                    kb.tt("dve", t4[:], Bc[k][:], Rt[:], ALU.add, r=["r_bc%d" % k, "r_rt"], w=["r_t4"])
                    kb.act(t4[:], t4[:], AF.Exp, r=["r_t4"], w=["r_t4"], scale=-1.0)
                    kb.tr(pg[:, 256:260], t3[0:4, :], C[0:4, 0:4], r=["r_t3", "C"], w=[PS[7]])
                    kb.tr(pg[:, 260:264], t4[0:4, :], C[0:4, 0:4], r=["r_t4", "C"], w=[PS[7]])
                    if prompt:
                        kb.tt("dve", dd[:, 0:1], Rt[:, 0:1], Mx[k][:, 127:128], ALU.subtract, r=["r_rt", "r_mx%d" % k], w=["r_dd"])
                        kb.act(dd[:, 0:1], dd[:, 0:1], AF.Exp, r=["r_dd"], w=["r_dd"])
                        kb.ts("dve", DDm[:, 0:4], C[0:4, 0:4], dd[:, 0:1], None, ALU.mult, None, r=["r_dd", "C"], w=["r_DD"])
                        kb.mm(pg[:, 264:268], lhsT=ones[0:4, :], rhs=DDm[:, 0:4], start=True, stop=True, r=["C", "r_DD"], w=[PS[7]])
                        kb.cp("dve", colq[k][:, 0:12], pg[:, 256:268], r=[PS[7]], w=["colq%d" % k])
                        if i == 15:
                            kb.tt("dve", mout[:, 0:1], Bc[k][:, 127:128], Mx[k][:, 127:128], ALU.add, r=["r_bc%d" % k, "r_mx%d" % k], w=["r_mout"])
                            kb.dma("sp", o_pmm, mout[:, 0:1], r=["r_mout"], w=())
                    else:
                        MT = Mx[k][:].rearrange("p (q t) -> p q t", t=8)[:, :, 7:8]
                        kb.tt("dve", t4[:].rearrange("p (q t) -> p q t", t=8), IGa[:].rearrange("p (q t) -> p q t", t=8),
                              MT.to_broadcast([4, 16, 8]), ALU.subtract, r=["r_ig", "r_mx%d" % k, PS[7]], w=["r_t4"])
                        kb.act(t4[:], t4[:], AF.Exp, r=["r_t4"], w=["r_t4"])
                        kb.tr(pg[:, 264:268], t4[0:4, :], C[0:4, 0:4], r=["r_t4", "C"], w=[PS[7]])
                        kb.cp("dve", colq[k][:, 0:12], pg[:, 256:268], r=[PS[7]], w=["colq%d" % k])
                        kb.tt("dve", dd[:].unsqueeze(2), minitT[:].unsqueeze(2), MT, ALU.subtract, r=["minitT", "r_mx%d" % k], w=["r_dd"])
                        kb.act(dd[:], dd[:], AF.Exp, r=["r_dd"], w=["r_dd"])
                        kb.tt("dve", DDm[:].rearrange("p (q h) -> p q h", h=4), dd[:].unsqueeze(2).to_broadcast([4, 16, 4]),
                              C[0:4, 0:4].unsqueeze(1).to_broadcast([4, 16, 4]), ALU.mult, r=["r_dd", "C"], w=["r_DD"])
                        kb.mm(pg[:, 272:336], lhsT=ones[0:4, :], rhs=DDm[:], start=True, stop=True, r=["C", "r_DD"], w=[PS[7]])
                        kb.cp("dve", decsb[:], pg[:, 272:336], r=[PS[7]], w=["decsb"])
                        kb.tt("dve", mout[:].unsqueeze(2), Bc[k][:].rearrange("p (q t) -> p q t", t=8)[:, :, 7:8], MT, ALU.add,
                              r=["r_bc%d" % k, "r_mx%d" % k], w=["r_mout"])
                        kb.dma("sp", o_smm, mout[:], r=["r_mout"], w=())

                def mlstm_tile(i):
                    k = i % 2
                    tg = i // 4
                    hT = hTg[tg % 2]
                    tc0 = (i % 4) * 128
                    lc0 = (i % 4) * 128 if i < 16 else 0
                    cq = colq[k]
                    vx = vext[k]
                    grp = "hTg%d" % (tg % 2)
                    for bi, c0 in enumerate((512, 1024, 1536)):
                        bank = 2 + (bi % 2)
                        for kc in range(8):
                            kb.mm(ps[bank][:, :], lhsT=hT[:, kc, tc0:tc0 + 128], rhs=winA[:, kc, c0:c0 + 512], start=(kc == 0), stop=(kc == 7),
                                  r=["winA", grp], w=[PS[bank]])
                        if bi == 0:
                            kb.act(ktok[:].rearrange("p a b -> p (a b)"), ps[bank][:, :], AF.Identity, r=[PS[bank]], w=["ktok"], scale=DKS)
                            for h in range(4):
                                kb.ts("dve", kw[:, h, :], ktok[:, h, :], cq[:, h:h + 1], None, ALU.mult, None,
                                      r=["ktok", "colq%d" % k], w=["kw"])
                        elif bi == 1:
                            kb.cp("act", vx[:, :, 0:128], ps[bank][:, :].rearrange("p (h d) -> p h d", d=128), r=[PS[bank]], w=["vext%d" % k])
                        else:
                            kb.act(sigo[:], ps[bank][:, :], AF.Sigmoid, r=[PS[bank]], w=["tA0"])
                    if i == 0:
                        stop_at(31)
                    for h in range(4):
                        kb.mm(ps[4][:, h * 128:(h + 1) * 128], lhsT=qkT[:, 4 + h, lc0:lc0 + 128], rhs=qkT[:, h, lc0:lc0 + 128],
                              start=True, stop=True, r=["qkT"], w=[PS[4]], sig=(h == 3))
                    if i == 0:
                        stop_at(32)
                    msk = maskP if i < 16 else maskS
                    for h in range(4):
                        kb.stt(PT[:, h, :], ps[4][:, h * 128:(h + 1) * 128], cq[:, h:h + 1], msk, ALU.mult, ALU.mult,
                               r=[PS[4], "colq%d" % k, "C"], w=["PT"])
                    return cq, vx, lc0

                def numden_finish(i, cq):
                    for half in range(2):
                        bank = ps[5 + half]
                        den = bank[:, 0:260].rearrange("p (h d) -> p h d", d=130)[:, :, 128:129]
                        kb.act(dmax[:, 2 * half:2 * half + 2].unsqueeze(2), den, AF.Abs, r=[PS[5 + half]], w=["dmax"])
                        kb.tt("dve", dmax[:, 2 * half:2 * half + 2], dmax[:, 2 * half:2 * half + 2], cq[:, 4 + 2 * half:6 + 2 * half], ALU.max,
                              r=["dmax", "colq%d" % (i % 2)], w=["dmax"])
                    s.add("dve", lambda g_: g_.reciprocal(out=dmax[:], in_=dmax[:]), r=["dmax"], w=["dmax"], tag="recip")
                    for h in range(4):
                        bank = ps[5 + h // 2]
                        o0 = (h % 2) * 130
                        kb.act(hmf[:, h * 128:(h + 1) * 128], bank[:, o0:o0 + 128], AF.Identity, r=[PS[5 + h // 2], "dmax"], w=["tB0"],
                               scale=dmax[:, h:h + 1])
                    for h in range(4):
                        s.add("dve", lambda g_, h=h: g_.bn_stats(out=hst6[:, h, :], in_=hmf[:, h * 128:(h + 1) * 128]), r=["tB0"], w=["hst6"], tag="bnst")
                    for h in range(4):
                        s.add("dve", lambda g_, h=h: g_.bn_aggr(out=hmv[:, h, :], in_=hst6[:, h, :]), r=["hst6"], w=["hmv"], tag="bnag")
                    kb.act(hrs[:].unsqueeze(2), hmv[:, :, 1:2], AF.Sqrt, r=["hmv"], w=["hrs"], bias=1e-6, scale=1.0)
                    s.add("dve", lambda g_: g_.reciprocal(out=hrs[:], in_=hrs[:]), r=["hrs"], w=["hrs"], tag="recip")
                    kb.tt("dve", hnm[:].unsqueeze(2), hmv[:, :, 0:1], hrs[:].unsqueeze(2), ALU.mult, r=["hmv", "hrs"], w=["hnm"])
                    kb.ts("dve", hnm[:], hnm[:], -1.0, None, ALU.mult, None, r=["hnm"], w=["hnm"])
                    for h in range(4):
                        kb.act(hmf[:, h * 128:(h + 1) * 128], hmf[:, h * 128:(h + 1) * 128], AF.Identity, r=["tB0", "hrs", "hnm"], w=["tB0"],
                               bias=hnm[:, h:h + 1], scale=hrs[:, h:h + 1])
                    kb.tt("pool", hmf[:], hmf[:], mng[:], ALU.mult, r=["tB0", "mng"], w=["tB0"])
                    kb.tt("pool", hmf[:], hmf[:], sigo[:], ALU.mult, r=["tB0", "tA0"], w=["tB0"])
                    for h in range(4):
                        kb.tr(ps[4][:, h * 128:(h + 1) * 128], hmf[:, h * 128:(h + 1) * 128], ident, r=["tB0", "C"], w=[PS[4]], sig=(h == 3))
                    kb.cp("act", hmT[:, :, i * 128:(i + 1) * 128], ps[4][:, :].rearrange("p (h d) -> p h d", d=128), r=[PS[4]], w=["hmT%d" % i])

                for tg, (t0, nt_) in enumerate(TGS):
                    tiles = range(4 * tg, 4 * tg + 4) if tg < 4 else [16]
                    hT = hTg[tg % 2]
                    for i in tiles:
                        make_hT(hT, i, prescale=False, col0=(i % 4) * 128, res="hTg%d" % (tg % 2))
                    for j in range(8):
                        bank = j % 2
                        for kc in range(8):
                            kb.mm(ps[bank][:, 0:nt_], lhsT=winA[:, kc, j * 128:(j + 1) * 128], rhs=hT[:, kc, 0:nt_], start=(kc == 0), stop=(kc == 7),
                                  r=["winA", "hTg%d" % (tg % 2)], w=[PS[bank]])
                        if j < 4:
                            kb.cp("act", qkT[:, j, 0:nt_], ps[bank][:, 0:nt_], r=[PS[bank]], w=["qkT"])
                        else:
                            kb.ts("dve", qkT[:, j, 0:nt_], ps[bank][:, 0:nt_], DKS, None, ALU.mult, None, r=[PS[bank]], w=["qkT"])
                    for i in tiles:
                        if i == 0:
                            stop_at(1)
                        rows(i)
                        if i == 0:
                            stop_at(2)
                        cq, vx, lc0 = mlstm_tile(i)
                        if i == 0:
                            stop_at(3)
                        if i == 16:
                            stop_at(5)
                        if i < 16:
                            for h in range(4):
                                bank = ps[5 + h // 2]
                                o0 = (h % 2) * 130
                                kb.mm(bank[:, o0:o0 + 130], lhsT=PT[:, h, :], rhs=vx[:, h, :], start=True, stop=False, r=["PT", "vext%d" % (i % 2)], w=[PS[5 + h // 2]], sig=False)
                                kb.mm(bank[:, o0:o0 + 130], lhsT=qkT[:, h, lc0:lc0 + 128], rhs=Cb[:, h, :], start=False, stop=True, r=["qkT", "Cb"], w=[PS[5 + h // 2]], sig=True)
                            numden_finish(i, cq)
                            for h in range(4):
                                bank = ps[5 + h // 2]
                                o0 = (h % 2) * 130
                                kb.mm(bank[:, o0:o0 + 130], lhsT=kw[:, h, :], rhs=vx[:, h, :], start=True, stop=True, r=["kw", "vext%d" % (i % 2)], w=[PS[5 + h // 2]])
                            for h in range(4):
                                bank = ps[5 + h // 2]
                                o0 = (h % 2) * 130
                                kb.ts("dve", Cst[:, h, :], Cst[:, h, :], cq[:, 8 + h:9 + h], None, ALU.mult, None, r=["Cst", "colq%d" % (i % 2)], w=["Cst"])
                                kb.stt(Cst[:, h, :], bank[:, o0:o0 + 130], cq[:, 8 + h:9 + h], Cst[:, h, :], ALU.mult, ALU.add,
                                       r=[PS[5 + h // 2], "Cst", "colq%d" % (i % 2)], w=["Cst"])
                            kb.cp("act", Cb[:], Cst[:], r=["Cst"], w=["Cb"])
                            if i == 0:
                                stop_at(4)
                            if i == 15:
                                for h in range(4):
                                    kb.tr(ps[4][:, h * 128:(h + 1) * 128], Cst[:, h, 0:128], ident, r=["Cst", "C"], w=[PS[4]], sig=(h == 3))
                                kb.cp("act", hmf[:], ps[4][:, :], r=[PS[4]], w=["tB0"])
                                kb.dma("sp", o_pmC.rearrange("h v k -> v h k"), hmf[:].rearrange("p (h k) -> p h k", k=128), r=["tB0"], w=())
                                kb.dma("sp", o_pmn.rearrange("h k -> k h"), Cst[:, :, 128], r=["Cst"], w=(), allow_slow_non_contiguous=True)
                        else:
                            with contextlib.ExitStack() as psm:
                                Cin = kb.sb(psm, "Cin", [128, 16, 128], F32)
                                CsT = kb.sb(psm, "CsT", [128, 16, 130], BF16)
                                qTm = kb.sb(psm, "qTm", [128, 16, 128], BF16)
                                VWm = kb.sb(psm, "VWm", [128, 16, 128], BF16)
                                nin = kb.sb(psm, "nin", [16, 4, 128], F32)
                                ninT = kb.sb(psm, "ninT", [128, 4, 16], F32)
                                BMW = kb.sb(psm, "BMW", [128, 16], BF16)
                                decc = kb.sb(psm, "decc", [16, 4], F32)
                                nout = nin
                                kb.memset("pool", qTm[:], 0.0, w=["qTm"])
                                kb.memset("pool", CsT[:], 0.0, w=["CsT"])
                                kb.dma("sp", nin[:], smn, r=(), w=["nin"])
                                kb.tr(ps[7][0:16, 400:404], dd[0:4, :], C[0:4, 0:4], r=["r_dd", "C"], w=[PS[7]])
                                kb.cp("dve", decc[:], ps[7][0:16, 400:404], r=[PS[7]], w=["decc"])
                                for h in range(4):
                                    kb.tr(ps[7][:, 416 + h * 16:432 + h * 16], nin[0:16, h, :], C[0:16, 0:16], r=["nin", "C"], w=[PS[7]], sig=(h == 3))
                                kb.cp("dve", ninT[:].rearrange("p a b -> p (a b)"), ps[7][:, 416:480], r=[PS[7]], w=["ninT"])
                                for h in range(4):
                                    bank = ps[5 + h // 2]
                                    o0 = (h % 2) * 130
                                    kb.dma("sp", Cin[:], smC[:, h].rearrange("q v k -> v q k"), r=(), w=["Cin"])
                                    for q4 in range(4):
                                        pb = q4 % 2
                                        for qq in range(4):
                                            q = q4 * 4 + qq
                                            kb.tr(ps[pb][:, qq * 128:(qq + 1) * 128], Cin[:, q, :], ident, r=["Cin", "C"], w=[PS[pb]], sig=(qq == 3))
                                        kb.cp("act", CsT[:, q4 * 4:q4 * 4 + 4, 0:128], ps[pb][:, :].rearrange("p (a b) -> p a b", b=128), r=[PS[pb]], w=["CsT"])
                                    kb.cp("dve", CsT[:, :, 128:129], ninT[:, h, :].unsqueeze(2), r=["ninT"], w=["CsT"])
                                    kb.cp("pool", bass.AP(qTm, 0, [[2048, 128], [136, 16], [1, 8]]),
                                          qkT[:, h, 0:128].rearrange("p (q t) -> p q t", t=8), r=["qkT"], w=["qTm"])
                                    kb.mm(bank[:, o0:o0 + 130], lhsT=PT[:, h, :], rhs=vx[:, h, :], start=True, stop=False, r=["PT", "vext%d" % (i % 2)], w=[PS[5 + h // 2]], sig=False)
                                    for q in range(16):
                                        kb.mm(bank[:, o0:o0 + 130], lhsT=qTm[:, q, :], rhs=CsT[:, q, :], start=False, stop=(q == 15), r=["qTm", "CsT"], w=[PS[5 + h // 2]], sig=(q == 15))
                                    kb.ts("dve", BMW[:], bms, cq[:, 8 + h:9 + h], None, ALU.mult, None, r=["C", "colq%d" % (i % 2)], w=["BMW"])
                                    kb.tt("dve", VWm[:], vx[:, h, 0:128].unsqueeze(1).to_broadcast([128, 16, 128]), BMW[:].unsqueeze(2).to_broadcast([128, 16, 128]),
                                          ALU.mult, r=["vext%d" % (i % 2), "BMW"], w=["VWm"])
                                    for q4 in range(4):
                                        pb = q4 % 2
                                        for qq in range(4):
                                            q = q4 * 4 + qq
                                            kb.mm(ps[pb][:, qq * 128:(qq + 1) * 128], lhsT=VWm[:, q, :], rhs=ktok[:, h, :], start=True, stop=True,
                                                  r=["VWm", "ktok"], w=[PS[pb]], sig=(qq == 3))
                                        for qq in range(4):
                                            q = q4 * 4 + qq
                                            kb.stt(Cin[:, q, :], Cin[:, q, :], decsb[:, q * 4 + h:q * 4 + h + 1], ps[pb][:, qq * 128:(qq + 1) * 128], ALU.mult, ALU.add,
                                                   r=["Cin", "decsb", PS[pb]], w=["Cin"])
                                    kb.dma("sp", o_smC[:, h].rearrange("q v k -> v q k"), Cin[:], r=["Cin"], w=())
                                    kb.mm(ps[7][0:16, 0:128], lhsT=BMW[:], rhs=ktok[:, h, :], start=True, stop=True, r=["BMW", "ktok"], w=[PS[7]])
                                    kb.stt(nout[:, h, :], nin[:, h, :], decc[:, h:h + 1], ps[7][0:16, 0:128], ALU.mult, ALU.add, r=["nin", "decc", PS[7]], w=["nin"])
                                numden_finish(i, cq)
                                kb.dma("sp", o_smn, nout[:], r=["nin"], w=())
                                s.barrier()
            s.barrier()
            stop_at(6)
            hrT = kb.sb(ph, "hrT", [128, 4, NTOK], BF16)
            with contextlib.ExitStack() as pb_:
                winB = kb.sb(pb_, "winB", [128, 8, 1024], BF16)
                kb.dma("pool", winB[:], win_v[:, :, 2056:3080], r=(), w=["winB"])
                hTgB = [kb.sb(pb_, "hTgB%d" % i, [128, 8, 512], BF16) for i in range(2)]
                WA = kb.sb(pb_, "WA", [128, 4, 128], F32)
                WX = kb.sb(pb_, "WX", [128, 4, 128], F32)
                kb.memset("pool", WA[:], 0.0, w=["WA"])
                kb.memset("pool", WX[:], 0.0, w=["WX"])
                for c in range(4):
                    for hp in range(2):
                        kb.dma("sp", WA[hp * 64:(hp + 1) * 64, c, hp * 64:(hp + 1) * 64], rg_w_a[0, 2 * c + hp], r=(), w=["WA"])
                        kb.dma("sp", WX[hp * 64:(hp + 1) * 64, c, hp * 64:(hp + 1) * 64], rg_w_x[0, 2 * c + hp], r=(), w=["WX"])
                cl = kb.sb(pb_, "cl", [128, 4], F32)
                cl2 = kb.sb(pb_, "cl2", [128, 4], F32)
                kb.act(cl[:], vecA[:, 28:32], AF.Exp, r=["vecA"], w=["cl"], scale=-1.0)
                kb.act(cl[:], cl[:], AF.Ln, r=["cl"], w=["cl"], bias=1.0, scale=1.0)
                kb.ts("dve", cl2[:], cl[:], -16.0, None, ALU.mult, None, r=["cl"], w=["cl2"])
                kb.ts("dve", cl[:], cl[:], -8.0, None, ALU.mult, None, r=["cl", "cl2"], w=["cl"])
                xp = [kb.sb(pb_, "xp%d" % c, [128, 515], F32) for c in range(4)]
                xps = kb.sb(pb_, "xps", [128, 16, 11], F32)
                hst = kb.sb(pb_, "hst", [128, 4], F32)
                h0T = kb.sb(pb_, "h0T", [128, 4, 16], F32)
                cvT = kb.sb(pb_, "cvT", [128, 4, 48], F32)
                hl = kb.sb(pb_, "hl", [128, 4, 16], F32)
                srh_sb = kb.sb(pb_, "srh_sb", [16, 512], F32)
                src_sb = kb.sb(pb_, "src_sb", [48, 512], F32)
                F5 = lambda nm: kb.sb(pb_, nm, [128, 512], F32)
                xc, rr, ii, aa, a2, uu, hh_, t5 = F5("xc"), F5("rr"), F5("ii"), F5("aa"), F5("a2"), F5("uu"), F5("hh"), F5("t5")
                for c in range(4):
                    kb.memset("pool", xp[c][:, 0:3], 0.0, w=["xp%d" % c])
                kb.memset("pool", hst[:], 0.0, w=["hst"])
                kb.dma("sp", srh_sb[:], srh, r=(), w=["srh_sb"])
                kb.dma("sp", src_sb[:], srconv, r=(), w=["src_sb"])
                for c in range(4):
                    kb.tr(ps[6][:, c * 16:(c + 1) * 16], srh_sb[0:16, c * 128:(c + 1) * 128], C[0:16, 0:16], r=["srh_sb", "C"], w=[PS[6]], sig=(c == 3))
                kb.cp("dve", h0T[:].rearrange("p a b -> p (a b)"), ps[6][:, 0:64], r=[PS[6]], w=["h0T"])
                for c in range(4):
                    kb.tr(ps[6][:, 64 + c * 48:64 + (c + 1) * 48], src_sb[0:48, c * 128:(c + 1) * 128], C[0:48, 0:48], r=["src_sb", "C"], w=[PS[6]], sig=(c == 3))
                kb.cp("dve", cvT[:].rearrange("p a b -> p (a b)"), ps[6][:, 64:256], r=[PS[6]], w=["cvT"])
                u_ = 0
                for tg, (t0, n) in enumerate(TGS):
                    sample = tg == 4
                    hT = hTgB[tg % 2]
                    for i in (range(4 * tg, 4 * tg + 4) if tg < 4 else [16]):
                        make_hT(hT, i, prescale=True, col0=(i % 4) * 128, res="hTg%d" % (tg % 2))
                    for c in range(4):
                        px, pgr = ps[(2 * u_) % 4], ps[(2 * u_ + 1) % 4]
                        PX, PGR = PS[(2 * u_) % 4], PS[(2 * u_ + 1) % 4]
                        u_ += 1
                        for kc in range(8):
                            kb.mm(px[:, 0:n], lhsT=winB[:, kc, c * 128:(c + 1) * 128], rhs=hT[:, kc, 0:n], start=(kc == 0), stop=(kc == 7),
                                  r=["winB", "hTg%d" % (tg % 2)], w=[PX])
                        for kc in range(8):
                            kb.mm(pgr[:, 0:n], lhsT=winB[:, kc, 512 + c * 128:512 + (c + 1) * 128], rhs=hT[:, kc, 0:n], start=(kc == 0), stop=(kc == 7),
                                  r=["winB", "hTg%d" % (tg % 2)], w=[PGR])
                        cw = lambda j: vecA[:, c * 4 + j:c * 4 + j + 1]
                        cb = vecA[:, 16 + c:17 + c]
                        if not sample:
                            kb.cp("act", xp[c][:, 3:3 + n], px[:, 0:n], r=[PX], w=["xp%d" % c])
                            kb.ts("dve", xc[:, 0:n], xp[c][:, 0:n], cw(0), cb, ALU.mult, ALU.add, r=["xp%d" % c, "vecA"], w=["xc"])
                            for j in range(1, 4):
                                kb.stt(xc[:, 0:n], xp[c][:, j:j + n], cw(j), xc[:, 0:n], ALU.mult, ALU.add, r=["xp%d" % c, "vecA", "xc"], w=["xc"])
                            if tg == 3:
                                kb.tr(ps[6][0:3, c * 128:(c + 1) * 128], xp[c][:, n:n + 3], ident, r=["xp%d" % c, "C"], w=[PS[6]])
                            else:
                                kb.cp("pool", xp[c][:, 0:3], xp[c][:, n:n + 3], r=["xp%d" % c], w=["xp%d" % c])
                        else:
                            kb.cp("dve", xps[:, :, 0:3], cvT[:, c, :].rearrange("p (q j) -> p q j", j=3), r=["cvT"], w=["xps"])
                            kb.cp("act", xps[:, :, 3:11], px[:, 0:n].rearrange("p (q t) -> p q t", t=8), r=[PX], w=["xps"])
                            xc3 = xc[:, 0:n].rearrange("p (q t) -> p q t", t=8)
                            kb.ts("dve", xc3, xps[:, :, 0:8], cw(0), cb, ALU.mult, ALU.add, r=["xps", "vecA"], w=["xc"])
                            for j in range(1, 4):
                                kb.stt(xc3, xps[:, :, j:j + 8], cw(j), xc3, ALU.mult, ALU.add, r=["xps", "vecA", "xc"], w=["xc"])
                            for j in range(3):
                                kb.tr(ps[5][0:16, j * 128:(j + 1) * 128], xps[:, :, 8 + j], ident, r=["xps", "C"], w=[PS[5]], sig=(j == 2))
                            kb.cp("dve", t5[0:16, 0:384], ps[5][0:16, 0:384], r=[PS[5]], w=["t5"])
                            kb.dma("sp", o_srconv.rearrange("(q j) f -> q j f", j=3)[:, :, c * 128:(c + 1) * 128],
                                   t5[0:16, 0:384].rearrange("q (j f) -> q j f", f=128), r=["t5"], w=())
                        kb.mm(ps[4][:, 0:n], lhsT=WA[:, c, :], rhs=xc[:, 0:n], start=True, stop=True, r=["WA", "xc"], w=[PS[4]])
                        kb.mm(ps[5][:, 0:n], lhsT=WX[:, c, :], rhs=xc[:, 0:n], start=True, stop=True, r=["WX", "xc"], w=[PS[5]])
                        kb.act(rr[:, 0:n], ps[4][:, 0:n], AF.Sigmoid, r=[PS[4], "vecA"], w=["rr"], bias=vecA[:, 20 + c:21 + c], scale=1.0)
                        kb.act(ii[:, 0:n], ps[5][:, 0:n], AF.Sigmoid, r=[PS[5], "vecA"], w=["ii"], bias=vecA[:, 24 + c:25 + c], scale=1.0)
                        kb.act(aa[:, 0:n], rr[:, 0:n], AF.Exp, r=["rr", "cl"], w=["aa"], scale=cl[:, c:c + 1])
                        kb.act(a2[:, 0:n], rr[:, 0:n], AF.Exp, r=["rr", "cl2"], w=["a2"], scale=cl2[:, c:c + 1])
                        kb.act(a2[:, 0:n], a2[:, 0:n], AF.Sqrt, r=["a2"], w=["a2"], bias=1.0, scale=-1.0)
                        kb.tt("dve", uu[:, 0:n], a2[:, 0:n], ii[:, 0:n], ALU.mult, r=["a2", "ii"], w=["uu"])
                        kb.tt("dve", uu[:, 0:n], uu[:, 0:n], xc[:, 0:n], ALU.mult, r=["uu", "xc"], w=["uu"])
                        if not sample:
                            s.add("dve", lambda g_, c=c, n=n: g_.tensor_tensor_scan(out=hh_[:, 0:n], data0=aa[:, 0:n], data1=uu[:, 0:n], initial=hst[:, c:c + 1],
                                                                                     op0=ALU.mult, op1=ALU.add),
                                  r=["aa", "uu", "hst"], w=["hh"], tag="scanH")
                            kb.cp("dve", hst[:, c:c + 1], hh_[:, n - 1:n], r=["hh"], w=["hst"])
                        else:
                            aa3 = aa[:, 0:n].rearrange("p (q t) -> p q t", t=8)
                            uu3 = uu[:, 0:n].rearrange("p (q t) -> p q t", t=8)
                            kb.tt("dve", t5[:, 0:16].unsqueeze(2), aa3[:, :, 0:1], h0T[:, c, :].unsqueeze(2), ALU.mult, r=["aa", "h0T", "t5"], w=["t5"])
                            kb.tt("dve", uu3[:, :, 0:1], uu3[:, :, 0:1], t5[:, 0:16].unsqueeze(2), ALU.add, r=["uu", "t5"], w=["uu"])
                            kb.tt("dve", aa[:, 0:n], aa[:, 0:n], rst, ALU.mult, r=["aa", "C"], w=["aa"])
                            s.add("dve", lambda g_, n=n: g_.tensor_tensor_scan(out=hh_[:, 0:n], data0=aa[:, 0:n], data1=uu[:, 0:n], initial=0.0,
                                                                                op0=ALU.mult, op1=ALU.add),
                                  r=["aa", "uu"], w=["hh"], tag="scanH")
                            kb.cp("dve", hl[:, c, :].unsqueeze(2), hh_[:, 0:n].rearrange("p (q t) -> p q t", t=8)[:, :, 7:8], r=["hh"], w=["hl"])
                        kb.act(t5[:, 0:n], pgr[:, 0:n], AF.Square, r=[PGR, "t5"], w=["t5"])
                        kb.ts("dve", t5[:, 0:n], t5[:, 0:n], 0.044715, 1.0, ALU.mult, ALU.add, r=["t5"], w=["t5"])
                        kb.tt("dve", t5[:, 0:n], t5[:, 0:n], pgr[:, 0:n], ALU.mult, r=["t5", PGR], w=["t5"])
                        kb.act(t5[:, 0:n], t5[:, 0:n], AF.Tanh, r=["t5"], w=["t5"], scale=0.7978845608028654)
                        kb.ts("dve", t5[:, 0:n], t5[:, 0:n], 1.0, 0.5, ALU.add, ALU.mult, r=["t5"], w=["t5"])
                        kb.tt("dve", t5[:, 0:n], t5[:, 0:n], pgr[:, 0:n], ALU.mult, r=["t5", PGR], w=["t5"])
                        kb.tt("dve", hrT[:, c, t0:t0 + n], t5[:, 0:n], hh_[:, 0:n], ALU.mult, r=["t5", "hh"], w=["hrT_g%d" % tg])
                    if tg == 3:
                        kb.cp("dve", t5[0:3, :], ps[6][0:3, 0:512], r=[PS[6]], w=["t5"])
                        kb.dma("sp", o_prconv, t5[0:3, :], r=["t5"], w=())
                kb.tr(ps[7][0:4, 0:128], hst[:, 0:4], ident, r=["hst", "C"], w=[PS[7]])
                kb.cp("dve", rr[0:4, 0:128], ps[7][0:4, 0:128], r=[PS[7]], w=["rr"])
                kb.dma("sp", o_prh, rr[0:4, 0:128], r=["rr"], w=())
                for c in range(4):
                    kb.tr(ps[4][0:16, c * 128:(c + 1) * 128], hl[:, c, :], ident, r=["hl", "C"], w=[PS[4]], sig=(c == 3))
                kb.cp("dve", ii[0:16, :], ps[4][0:16, :], r=[PS[4]], w=["ii"])
                kb.dma("sp", o_srh, ii[0:16, :], r=["ii"], w=())
                s.barrier()
            stop_at(7)
            with contextlib.ExitStack() as pc_:
                alloc_gl(pc_)
                mod_prepare(l, 1, 1.0, blocks=[4, 5])
                Gp, Gs = gl["Gp"], gl["Gs"]
                wout = kb.sb(pc_, "wout", [128, 8, D], BF16)
                kb.dma("pool", wout[:], ab_w_out[0].rearrange("(kc p) n -> p kc n", p=128), r=(), w=["wout"])
                proj_acc(wout, "wout", 8, lambda i, kc: ((hmT[:, kc, i * 128:(i + 1) * 128], "hmT%d" % i) if kc < 4 else
                                                          (hrT[:, kc - 4, i * 128:(i + 1) * 128], "hrT_g%d" % (i // 4))), True, True)
                s.barrier()
        s.barrier()

    CW = -0.6065306597126334
    RT = BF16

    def rwkv_mixer(l):
        rwkv_mixer_(l)
        s.skip = False
        s.barrier()

    def rwkv_mixer_(l):
        rw_mu = kb.din("muT", [128, 48])
        rw_wr = kb.din("rw_wr", [1, D, D])
        rw_wk = kb.din("rw_wk", [1, D, D])
        rw_wv = kb.din("rw_wv", [1, D, D])
        rw_wo = kb.din("rw_wo", [1, D, D])
        rw_w0 = kb.din("rw_w0", [1, D])
        rw_w1 = kb.din("rw_w1", [1, D, 64])
        rw_w2 = kb.din("rw_w2", [1, 64, D])
        rw_a0 = kb.din("rw_a0", [1, D])
        rw_a1 = kb.din("rw_a1", [1, D, 64])
        rw_a2 = kb.din("rw_a2", [1, 64, D])
        rw_g1 = kb.din("rw_g1", [1, D, 128])
        rw_g2 = kb.din("rw_g2", [1, 128, D])
        rw_kk = kb.din("rw_k_k", [1, D])
        rw_ka = kb.din("rw_k_a", [1, D])
        rw_rk = kb.din("rk_flat", [1, D])
        rw_lng = kb.din("rw_lnx_g", [1, D])
        rw_lnb = kb.din("rw_lnx_b", [1, D])
        swkv = kb.din("swkv", [16, 16, 64, 64])
        sshift = kb.din("sshift", [16, D])
        o_pwkv = kb.dout("o_pwkv", [16, 64, 64])
        o_pshift = kb.dout("o_pshift", [1, D])
        o_swkv = kb.dout("o_swkv", [16, 16, 64, 64])
        o_sshift = kb.dout("o_sshift", [16, D])

        cm = lambda nm: C[:, CST_OFF[nm]:CST_OFF[nm] + 128]
        low16, up16, m16, m32, m64 = cm("low16"), cm("up16"), cm("m16"), cm("m32"), cm("m64")
        maskP, maskS, upP, lowP, upS, lowS, blkS, ones, bms = cm("maskP"), cm("maskS"), cm("upP"), cm("lowP"), cm("upS"), cm("lowS"), cm("blkS"), cm("ones"), C[:, CST_OFF["bms"]:CST_OFF["bms"] + 16]

        mod_prepare(l, 1, 1.0, blocks=range(4))
        with contextlib.ExitStack() as ph:
            ygT = kb.sb(ph, "ygT", [128, 8, NTOK], BF16)
            muT = kb.sb(ph, "muT_sb", [128, 48], F32)
            kb.dma("sp", muT[:], rw_mu, r=(), w=["muT"])
            identR = kb.sb(ph, "identR", [128, 128], RT)
            kb.cp("dve", identR[:], ident, r=["C"], w=["identR"])
            w1b = kb.sb(ph, "w1b", [128, 8, 64], BF16)
            a1b = kb.sb(ph, "a1b", [128, 8, 64], BF16)
            g1b = kb.sb(ph, "g1b", [128, 8, 128], BF16)
            kb.dma("pool", w1b[:], rw_w1[0].rearrange("(kc p) n -> p kc n", p=128), r=(), w=["w1b"])
            kb.dma("pool", a1b[:], rw_a1[0].rearrange("(kc p) n -> p kc n", p=128), r=(), w=["a1b"])
            kb.dma("pool", g1b[:], rw_g1[0].rearrange("(kc p) n -> p kc n", p=128), r=(), w=["g1b"])
            w1m = kb.sb(ph, "w1m", [128, 8, 64], BF16)
            a1m = kb.sb(ph, "a1m", [128, 8, 64], BF16)
            g1m = kb.sb(ph, "g1m", [128, 8, 128], BF16)
            for wm_, wb_, rs_, j_, n_ in ((w1m, w1b, "w1b", 1, 64), (a1m, a1b, "a1b", 4, 64), (g1m, g1b, "g1b", 5, 128)):
                kb.tt("dve", wm_[:], wb_[:], muT[:, j_ * 8:(j_ + 1) * 8].unsqueeze(2).to_broadcast([128, 8, n_]), ALU.mult, r=[rs_, "muT"], w=[rs_ + "m"])
            hlast = kb.sb(ph, "hlast", [128, 8, 17], F32)
            sh0T = kb.sb(ph, "sh0T", [128, 8, 16], BF16)
            with contextlib.ExitStack() as p0:
                shs = kb.sb(p0, "shs", [16, D], F32)
                kb.dma("sp", shs[:], sshift, r=(), w=["shs"])
                for c in range(8):
                    kb.tr(ps[0][:, c * 16:(c + 1) * 16], shs[0:16, c * 128:(c + 1) * 128], C[0:16, 0:16], r=["shs", "C"], w=[PS[0]], sig=(c == 7))
                kb.cp("dve", sh0T[:].rearrange("p a b -> p (a b)"), ps[0][:, 0:128], r=[PS[0]], w=["sh0T"])
                s.barrier()

            def hook_last(i, c, src, psres):
                if i == 15:
                    kb.ts("dve", hlast[:, c, 0:1], src[:, 127:128], modT[:, 8 + c, 0:1], modT[:, c, 0:1], ALU.mult, ALU.add, r=["modT", psres], w=["hlast"])
                elif i == 16:
                    v3 = src.rearrange("p (q t) -> p q t", t=8)[:, :, 7:8]
                    kb.tt("dve", hlast[:, c, 1:17].unsqueeze(2), v3, modT[:, 8 + c, 1:NT].unsqueeze(2), ALU.mult, r=["modT", psres], w=["hlast"])
                    kb.tt("dve", hlast[:, c, 1:17], hlast[:, c, 1:17], modT[:, c, 1:NT], ALU.add, r=["hlast", "modT"], w=["hlast"])

            stop_at(51)
            for hg in range(4):
                c0 = hg * 256
                with contextlib.ExitStack() as pp:
                    wrs = kb.sb(pp, "wrs", [128, 8, 256], BF16)
                    wks = kb.sb(pp, "wks", [128, 8, 256], BF16)
                    wvs = kb.sb(pp, "wvs", [128, 8, 256], BF16)
                    for wt, src in ((wrs, rw_wr), (wks, rw_wk), (wvs, rw_wv)):
                        kb.dma("pool", wt[:], src[0].rearrange("(kc p) n -> p kc n", p=128)[:, :, c0:c0 + 256], r=(), w=["wqkv"])
                    wrm = kb.sb(pp, "wrm", [128, 8, 256], BF16)
                    wkm = kb.sb(pp, "wkm", [128, 8, 256], BF16)
                    wvm = kb.sb(pp, "wvm", [128, 8, 256], BF16)
                    for wm_, wt_, j_ in ((wrm, wrs, 0), (wkm, wks, 2), (wvm, wvs, 3)):
                        kb.tt("dve", wm_[:], wt_[:], muT[:, j_ * 8:(j_ + 1) * 8].unsqueeze(2).to_broadcast([128, 8, 256]), ALU.mult, r=["wqkv", "muT"], w=["wqkvm"])
                    w2s = kb.sb(pp, "w2s", [64, 256], BF16)
                    a2s = kb.sb(pp, "a2s", [64, 256], BF16)
                    g2s = kb.sb(pp, "g2s", [128, 256], BF16)
                    kb.dma("pool", w2s[:], rw_w2[0, :, c0:c0 + 256], r=(), w=["w2s"])
                    kb.dma("pool", a2s[:], rw_a2[0, :, c0:c0 + 256], r=(), w=["a2s"])
                    kb.dma("pool", g2s[:], rw_g2[0, :, c0:c0 + 256], r=(), w=["g2s"])
                    w0r = kb.sb(pp, "w0r", [1, 256], F32)
                    a0r = kb.sb(pp, "a0r", [1, 256], F32)
                    kb.dma("sp", w0r[:], rw_w0[0:1, c0:c0 + 256], r=(), w=["w0r"])
                    kb.dma("sp", a0r[:], rw_a0[0:1, c0:c0 + 256], r=(), w=["a0r"])
                    bcs = {}
                    alias = {"kkb": tA[2][:, 0:256], "kab": tA[2][:, 256:512], "rkb": tA[3][:, 0:256], "lngb": tA[3][:, 256:512]}
                    for nm, src in (("kkb", rw_kk), ("kab", rw_ka), ("rkb", rw_rk), ("lngb", rw_lng), ("lnbb", rw_lnb)):
                        bcs[nm] = alias[nm] if nm in alias else kb.sb(pp, nm, [128, 256], F32)
                        kb.dma("sp", bcs[nm][:], src[0:1, c0:c0 + 256].to_broadcast([128, 256]), r=(), w=[nm])
                    hTg = [kb.sb(pp, "hTr%d" % i, [128, 8, 130], BF16) for i in range(2)]
                    dx = kb.sb(pp, "dx", [128, 8, 128], BF16)
                    loT = kb.sb(pp, "loT", [128, 3, 128], BF16)
                    TB = lambda j: tB[j // 4][:, (j % 4) * 256:(j % 4) * 256 + 256]
                    Rr, Kk, KKn, Aa, SG, CSs, Ee, Tt = [TB(j) for j in range(8)]
                    tbres = lambda j: "tB%d" % (j // 4)
                    Gt = tA[0][:, 0:256]
                    small = tA[1][:, 256:320]
                    F3 = lambda nm: kb.sb(pp, nm, [128, 256], RT)
                    Vv, AL, RB, KH, BH, BT = F3("Vv"), F3("AL"), F3("RB"), F3("KH"), F3("BH"), F3("BTt")
                    Vf = tA[0][:, 256:512]
                    fmall = kb.sb(pp, "fmall", [64, 4, 4, 128], RT)
                    fmT = {nm: fmall[:, j] for j, nm in enumerate(("alT", "btT", "ktT", "rbT"))}
                    chainall = kb.sb(pp, "chainall", [128, 7, 4, 128], RT)
                    chn = {nm: chainall[:, j] for j, nm in enumerate(("ApA", "ApB", "BpA", "BpB", "TTa", "TTb", "Am"))}
                    S0nat = kb.sb(pp, "S0nat", [64, 16, 64], F32)
                    S0Tq = fmall[:, 0:2].rearrange("p a h t -> p (a h t)").rearrange("p (q k) -> p q k", k=64)
                    SLo = S0nat
                    AakT = kb.sb(pp, "AakT", [128, 4, 128], RT)
                    YTs = AakT[0:64, :, :]
                    ArbT = kb.sb(pp, "ArbT", [128, 4, 128], RT)
                    ArkT = kb.sb(pp, "ArkT", [128, 4, 128], RT)
                    Ahat = kb.sb(pp, "Ahat", [128, 4, 64], RT)
                    X1 = kb.sb(pp, "X1", [128, 4, 64], RT)
                    KT = X1[:].rearrange("p h k -> p (h k)")
                    U0 = kb.sb(pp, "U0", [128, 4, 64], RT)
                    Gm = kb.sb(pp, "Gm", [64, 4, 64], RT)
                    RhT = kb.sb(pp, "RhT", [64, 4, 128], RT)
                    S0T = kb.sb(pp, "S0T", [64, 4, 64], RT)
                    PLc = tA[1][0:64, 320:384]
                    yb = tA[1][:, 0:256]
                    kb.ts("dve", S0T[:].rearrange("p a b -> p (a b)"), C[0:64, 0:256], 0.0, None, ALU.mult, None, r=["C"], w=["S0T0", "S0T1", "S0T2", "S0T3"])
                    kb.memset("pool", hTg[1][:, :, 128:129], 0.0, w=["hTr1"])

                    algb = [4]

                    def nb():
                        b_ = algb[0]
                        algb[0] = 4 + (algb[0] - 3) % 4
                        return b_

                    def grp4(mmf, n_cols, m_rows=128):
                        b_ = nb()
                        for hl in range(4):
                            items = mmf(hl)
                            for j, (lt, rh, rd) in enumerate(items):
                                kb.mm(ps[b_][0:m_rows, hl * n_cols:(hl + 1) * n_cols], lhsT=lt, rhs=rh, start=(j == 0), stop=(j == len(items) - 1),
                                      r=rd, w=[PS[b_]], sig=(hl == 3 and j == len(items) - 1))
                        return b_

                    fsl = lambda t_, hl: t_[:, hl, :]
                    tsl = lambda t_, hl: t_[:, hl * 64:(hl + 1) * 64]

                    def sample_states():
                        Bhm = chainall[:, 0:2].rearrange("p a h t -> p (a h t)").rearrange("p (q k) -> p q k", k=64)
                        Khm = chainall[:, 2:4].rearrange("p a h t -> p (a h t)").rearrange("p (q k) -> p q k", k=64)
                        Gq = chainall[0:64, 4:6].rearrange("p a h t -> p (a h t)").rearrange("p (q k) -> p q k", k=64)
                        bmq = bms.unsqueeze(2).to_broadcast([128, 16, 64])
                        for hl in range(4):
                            h = hg * 4 + hl
                            kb.tt("dve", Bhm, tsl(BH, hl).unsqueeze(1).to_broadcast([128, 16, 64]), bmq, ALU.mult, r=["BH", "C"], w=["ApA0", "ApA1", "ApA2", "ApA3"] + ["ApB0", "ApB1", "ApB2", "ApB3"])
                            kb.tt("pool", Khm, tsl(KH, hl).unsqueeze(1).to_broadcast([128, 16, 64]), bmq, ALU.mult, r=["KH", "C"], w=["BpA0", "BpA1", "BpA2", "BpA3"] + ["BpB0", "BpB1", "BpB2", "BpB3"])
                            kb.dma("sp", S0nat[:], swkv[:, h].rearrange("q v k -> v q k"), r=(), w=["S0nat"])
                            b0, b1 = nb(), nb()
                            for q in range(16):
                                bb = b0 if q < 8 else b1
                                kb.tr(ps[bb][0:64, (q % 8) * 64:(q % 8 + 1) * 64], S0nat[:, q, :], ident[0:64, 0:64], r=["S0nat", "C"], w=[PS[bb]], sig=(q % 8 == 7))
                            kb.cp("act", S0Tq[:, 0:8, :], ps[b0][0:64, :].rearrange("p (q k) -> p q k", k=64), r=[PS[b0]], w=["S0Tq", "alT", "btT"])
                            kb.cp("dve", S0Tq[:, 8:16, :], ps[b1][0:64, :].rearrange("p (q k) -> p q k", k=64), r=[PS[b1]], w=["S0Tq", "alT", "btT"])
                            b_ = nb()
                            for q in range(16):
                                kb.mm(ps[b_][0:64, q * 8:(q + 1) * 8], lhsT=S0Tq[:, q, :], rhs=RhT[:, hl, q * 8:(q + 1) * 8], start=True, stop=True,
                                      r=["S0Tq", "RhT%d" % hl], w=[PS[b_]], sig=(q == 15))
                            kb.cp("act", YTs[:, hl, :], ps[b_][0:64, 0:128], r=[PS[b_]], w=["AakT%d" % hl])
                            g0, g1 = nb(), nb()
                            for half, bb in ((0, g0), (1, g1)):
                                kb.mm(ps[bb][0:64, :], lhsT=Ahat[:, hl, :], rhs=Bhm[:, half * 8:(half + 1) * 8, :], start=True, stop=True,
                                      r=["Ahat%d" % hl] + ["ApA0", "ApA1", "ApA2", "ApA3"] + ["ApB0", "ApB1", "ApB2", "ApB3"], w=[PS[bb]])
                            for q in range(16):
                                bb = g0 if q < 8 else g1
                                kb.stt(Gq[:, q, :], ident[0:64, 0:64], PLc[:, hl * 16 + q:hl * 16 + q + 1], ps[bb][0:64, (q % 8) * 64:(q % 8 + 1) * 64], ALU.mult, ALU.add,
                                       r=["C", "tA1", PS[bb]], w=["TTa0", "TTa1", "TTa2", "TTa3"] + ["TTb0", "TTb1", "TTb2", "TTb3"])
                            for half in range(2):
                                bb = nb()
                                kb.mm(ps[bb][0:64, :], lhsT=tsl(Vv, hl), rhs=Khm[:, half * 8:(half + 1) * 8, :], start=True, stop=False, r=["Vv"] + ["BpA0", "BpA1", "BpA2", "BpA3"] + ["BpB0", "BpB1", "BpB2", "BpB3"], w=[PS[bb]], sig=False)
                                kb.mm(ps[bb][0:64, :], lhsT=U0[:, hl, :], rhs=Bhm[:, half * 8:(half + 1) * 8, :], start=False, stop=False, r=["U0%d" % hl] + ["ApA0", "ApA1", "ApA2", "ApA3"] + ["ApB0", "ApB1", "ApB2", "ApB3"], w=[PS[bb]], sig=False)
                                for qq in range(8):
                                    q = half * 8 + qq
                                    kb.mm(ps[bb][0:64, qq * 64:(qq + 1) * 64], lhsT=S0Tq[:, q, :], rhs=Gq[:, q, :], start=False, stop=(qq == 7),
                                          r=["S0Tq"] + ["TTa0", "TTa1", "TTa2", "TTa3"] + ["TTb0", "TTb1", "TTb2", "TTb3"], w=[PS[bb]], sig=(qq == 7))
                                kb.cp("act" if half else "dve", SLo[:, half * 8:(half + 1) * 8, :], ps[bb][0:64, :].rearrange("p (q k) -> p q k", k=64), r=[PS[bb]], w=["S0nat"])
                            kb.dma("sp", o_swkv[:, h].rearrange("q v k -> v q k"), SLo[:], r=["S0nat"], w=())
                        by = grp4(lambda hl: [(ArkT[:, hl, :], tsl(Vv, hl), ["ArkT%d" % hl, "Vv"]), (ArbT[:, hl, :], U0[:, hl, :], ["ArbT%d" % hl, "U0%d" % hl]),
                                               (YTs[:, hl, :], identR[0:64, 0:64], ["AakT%d" % hl, "identR"])], 64)
                        kb.cp("act", yb[:], ps[by][:, 0:256], r=[PS[by]], w=["tA1"])

                    for i in range(NT):
                        sample = i == 16
                        k_ = i % 2
                        hT = hTg[k_]
                        hres = "hTr%d" % k_
                        last_pass = hg == 3
                        if hg == 0 and i == 1:
                            stop_at(58)
                        if hg == 0 and i == 16:
                            stop_at(59)
                        make_hT(hT, i, prescale=last_pass, col0=1, res=hres, hook=(hook_last if hg == 0 else None))
                        if i == 0:
                            kb.memset("pool", hT[:, :, 0:1], 0.0, w=[hres])
                        elif not sample:
                            kb.cp("pool", hT[:, :, 0:1], hTg[1 - k_][:, :, 128:129], r=["hTr%d" % (1 - k_)], w=[hres])
                        cur = hT[:, :, 1:129]
                        if not sample:
                            kb.tt("dve", dx[:], hT[:, :, 0:128], cur, ALU.subtract, r=[hres], w=["dx"])
                        else:
                            kb.cp("pool", dx[:], hT[:, :, 0:128], r=[hres], w=["dx"])
                            kb.cp("pool", dx[:].rearrange("p c (q t) -> p c q t", t=8)[:, :, :, 0], sh0T[:], r=["sh0T"], w=["dx"])
                            kb.tt("pool", dx[:], dx[:], cur, ALU.subtract, r=["dx", hres], w=["dx"])
                        for wt, wm_, bank, off in ((wrs, wrm, 0, 0), (wks, wkm, 0, 256), (wvs, wvm, 1, 0)):
                            for kc in range(8):
                                kb.mm(ps[bank][:, off:off + 256], lhsT=hT[:, kc, 1:129], rhs=wt[:, kc, :], start=(kc == 0), stop=False, r=[hres, "wqkv"], w=[PS[bank]], sig=False)
                            for kc in range(8):
                                kb.mm(ps[bank][:, off:off + 256], lhsT=dx[:, kc, :], rhs=wm_[:, kc, :], start=False, stop=(kc == 7), r=["dx", "wqkvm"], w=[PS[bank]])
                        for wt, wm_, wr_, m_, off3 in ((w1b, w1m, "w1b", 64, 0), (a1b, a1m, "a1b", 64, 128), (g1b, g1m, "g1b", 128, 256)):
                            for kc in range(8):
                                kb.mm(ps[3][0:m_, off3:off3 + 128], lhsT=wt[:, kc, :], rhs=hT[:, kc, 1:129], start=(kc == 0), stop=False, r=[hres, wr_], w=[PS[3]], sig=False)
                            for kc in range(8):
                                kb.mm(ps[3][0:m_, off3:off3 + 128], lhsT=wm_[:, kc, :], rhs=dx[:, kc, :], start=False, stop=(kc == 7), r=["dx", wr_ + "m"], w=[PS[3]])
                        kb.act(loT[0:64, 0, :], ps[3][0:64, 0:128], AF.Tanh, r=[PS[3]], w=["loT"])
                        kb.cp("act", loT[0:64, 1, :], ps[3][0:64, 128:256], r=[PS[3]], w=["loT"])
                        kb.act(loT[:, 2, :], ps[3][:, 256:384], AF.Sigmoid, r=[PS[3]], w=["loT"])
                        kb.mm(ps[1][:, 256:512], lhsT=loT[0:64, 0, :], rhs=w2s[:], start=True, stop=False, r=["loT", "w2s"], w=[PS[1]], sig=False)
                        kb.mm(ps[1][:, 256:512], lhsT=ones[0:1, :], rhs=w0r[:], start=False, stop=True, r=["C", "w0r"], w=[PS[1]])
                        kb.mm(ps[2][:, 0:256], lhsT=loT[0:64, 1, :], rhs=a2s[:], start=True, stop=False, r=["loT", "a2s"], w=[PS[2]], sig=False)
                        kb.mm(ps[2][:, 0:256], lhsT=ones[0:1, :], rhs=a0r[:], start=False, stop=True, r=["C", "a0r"], w=[PS[2]])
                        kb.mm(ps[2][:, 256:512], lhsT=loT[:, 2, :], rhs=g2s[:], start=True, stop=True, r=["loT", "g2s"], w=[PS[2]])
                        if hg == 0 and i == 0:
                            stop_at(52)
                        kb.cp("act", Rr, ps[0][:, 0:256], r=[PS[0]], w=[tbres(0)])
                        kb.cp("act", Kk, ps[0][:, 256:512], r=[PS[0]], w=[tbres(1)])
                        kb.cp("act", Vv[:], ps[1][:, 0:256], r=[PS[1]], w=["Vv"])
                        kb.cp("act", Vf, ps[1][:, 0:256], r=[PS[1]], w=["tA0"])
                        kb.act(SG, ps[1][:, 256:512], AF.Sigmoid, r=[PS[1]], w=[tbres(4)])
                        kb.act(Aa, ps[2][:, 0:256], AF.Sigmoid, r=[PS[2]], w=[tbres(3)])
                        kb.cp("act", Gt, ps[2][:, 256:512], r=[PS[2]], w=["tA0"])
                        kb.tt("dve", KKn, Kk, bcs["kkb"][:], ALU.mult, r=[tbres(1), "kkb"], w=[tbres(2)])
                        kb.tt("dve", Tt, KKn, KKn, ALU.mult, r=[tbres(2)], w=[tbres(7)])
                        s.add("dve", lambda g_: g_.tensor_reduce(out=small[:, 0:4], in_=Tt.rearrange("p (h k) -> p h k", k=64), axis=AX.X, op=ALU.add),
                              r=[tbres(7)], w=["tA1"], tag="red")
                        kb.act(small[:, 0:4], small[:, 0:4], AF.Sqrt, r=["tA1"], w=["tA1"])
                        kb.ts("dve", small[:, 0:4], small[:, 0:4], 1e-12, None, ALU.max, None, r=["tA1"], w=["tA1"])
                        s.add("dve", lambda g_: g_.reciprocal(out=small[:, 0:4], in_=small[:, 0:4]), r=["tA1"], w=["tA1"], tag="recip")
                        kb.tt("dve", KKn.rearrange("p (h k) -> p h k", k=64), KKn.rearrange("p (h k) -> p h k", k=64),
                              small[:, 0:4].unsqueeze(2).to_broadcast([128, 4, 64]), ALU.mult, r=[tbres(2), "tA1"], w=[tbres(2)])
                        kb.stt(Tt, Aa, -1.0, bcs["kab"][:], ALU.add, ALU.mult, r=[tbres(3), "kab"], w=[tbres(7)])
                        kb.tt("dve", Tt, Tt, Kk, ALU.mult, r=[tbres(7), tbres(1)], w=[tbres(7)])
                        kb.tt("dve", Kk, Kk, Tt, ALU.add, r=[tbres(1), tbres(7)], w=[tbres(1)])
                        kb.tt("dve", Tt, Rr, Kk, ALU.mult, r=[tbres(0), tbres(1)], w=[tbres(7)])
                        kb.tt("dve", Tt, Tt, bcs["rkb"][:], ALU.mult, r=[tbres(7), "rkb"], w=[tbres(7)])
                        s.add("dve", lambda g_: g_.tensor_reduce(out=small[:, 4:8], in_=Tt.rearrange("p (h k) -> p h k", k=64), axis=AX.X, op=ALU.add),
                              r=[tbres(7)], w=["tA1"], tag="red")
                        kb.tt("dve", Aa, KKn, Aa, ALU.mult, r=[tbres(2), tbres(3)], w=[tbres(3)])
                        if hg == 0 and i == 0:
                            stop_at(53)
                        Um, Jm = (maskP, ones) if not sample else (maskS, blkS)
                        kb.mm(ps[4][:, 0:256], lhsT=Um, rhs=SG, start=True, stop=True, r=["C", tbres(4)], w=[PS[4]])
                        kb.mm(ps[4][:, 256:512], lhsT=Jm, rhs=SG, start=True, stop=True, r=["C", tbres(4)], w=[PS[4]])
                        nq = 1 if not sample else 16
                        for hl in range(4):
                            kb.mm(ps[3][0:64, 384 + hl * nq:384 + (hl + 1) * nq], lhsT=SG[:, hl * 64:(hl + 1) * 64], rhs=(ones[:, 0:1] if not sample else bms),
                                  start=True, stop=True, r=[tbres(4), "C"], w=[PS[3]], sig=(hl == 3))
                        kb.act(PLc[:, 0:4 * nq], ps[3][0:64, 384:384 + 4 * nq], AF.Exp, r=[PS[3]], w=["tA1"], scale=CW)
                        kb.cp("act", CSs, ps[4][:, 0:256], r=[PS[4]], w=[tbres(5)])
                        kb.tt("dve", Tt, CSs, SG, ALU.subtract, r=[tbres(5), tbres(4)], w=[tbres(7)])
                        kb.act(Ee, Tt, AF.Exp, r=[tbres(7)], w=[tbres(6)], scale=CW)
                        kb.stt(AL[:], KKn, -1.0, Ee, ALU.mult, ALU.mult, r=[tbres(2), tbres(6)], w=["AL"])
                        kb.act(Ee, CSs, AF.Exp, r=[tbres(5), "AL"], w=[tbres(6)], scale=-CW)
                        kb.tt("dve", BT[:], Aa, Ee, ALU.mult, r=[tbres(3), tbres(6)], w=["BTt"])
                        kb.tt("dve", KT, Kk, Ee, ALU.mult, r=[tbres(1), tbres(6)], w=["X1kt", "X10", "X11", "X12", "X13"])
                        kb.act(Tt, CSs, AF.Exp, r=[tbres(5)], w=[tbres(7)], scale=CW)
                        kb.tt("dve", RB[:], Rr, Tt, ALU.mult, r=[tbres(0), tbres(7)], w=["RB"])
                        kb.tt("dve", Ee, ps[4][:, 256:512], CSs, ALU.subtract, r=[PS[4], tbres(5), "tA0", "X1kt"], w=[tbres(6)])
                        kb.act(Ee, Ee, AF.Exp, r=[tbres(6)], w=[tbres(6)], scale=CW)
                        kb.tt("dve", KH[:], Kk, Ee, ALU.mult, r=[tbres(1), tbres(6)], w=["KH"])
                        kb.tt("dve", BH[:], Aa, Ee, ALU.mult, r=[tbres(3), tbres(6)], w=["BH"])
                        for qi, (nm, src, sr) in enumerate((("alT", AL, "AL"), ("btT", BT, "BTt"), ("ktT", KT, "X1kt"), ("rbT", RB, "RB"))):
                            b_ = 5 + qi % 2
                            for hl in range(4):
                                kb.tr(psb[b_][0:64, hl * 128:(hl + 1) * 128], src[:, hl * 64:(hl + 1) * 64], identR[:],
                                      r=[sr, "identR"], w=[PS[b_]], sig=(hl == 3))
                            kb.cp("act" if qi % 2 else "dve", fmT[nm][:].rearrange("p a b -> p (a b)"), psb[b_][0:64, 0:512], r=[PS[b_]], w=[nm])
                        if hg == 0 and i == 0:
                            stop_at(54)
                        alT, btT, ktT, rbT = fmT["alT"], fmT["btT"], fmT["ktT"], fmT["rbT"]
                        mlow, mup, minc = (lowP, upP, maskP) if not sample else (lowS, upS, maskS)
                        n_it = 3 if not sample else 2

                        def head_alg(hl):
                            B_ = 4 + hl
                            P_ = PS[B_]
                            rn = lambda nm: "%s%d" % (nm, hl)
                            pw = ps[B_][:, 0:128]

                            def mm1(lt, rh, rd, cols=128, rows=128, first=True, last=True):
                                kb.mm(ps[B_][0:rows, 0:cols], lhsT=lt, rhs=rh, start=first, stop=last, r=rd, w=[P_], sig=last)

                            mm1(fsl(alT, hl), fsl(btT, hl), ["alT", "btT"])
                            if not sample:
                                kb.tt("dve", chn["Am"][:, hl, :], pw, lowP, ALU.mult, r=[P_, "C"], w=[rn("Am")])
                                kb.tt("dve", chn["ApA"][:, hl, :], pw, low16, ALU.mult, r=[P_, "C"], w=[rn("ApA")])
                            else:
                                kb.tt("dve", chn["ApA"][:, hl, :], pw, lowS, ALU.mult, r=[P_, "C"], w=[rn("ApA")])
                            yield
                            mm1(fsl(btT, hl), fsl(alT, hl), ["alT", "btT"])
                            kb.tt("dve", chn["BpA"][:, hl, :], pw, (up16 if not sample else upS), ALU.mult, r=[P_, "C"], w=[rn("BpA")])
                            yield
                            mm1(fsl(ktT, hl), fsl(alT, hl), ["alT", "ktT"])
                            kb.tt("dve", AakT[:, hl, :], pw, mup, ALU.mult, r=[P_, "C"], w=[rn("AakT")])
                            yield
                            mm1(fsl(btT, hl), fsl(rbT, hl), ["rbT", "btT"])
                            kb.tt("dve", ArbT[:, hl, :], pw, minc, ALU.mult, r=[P_, "C"], w=[rn("ArbT")])
                            yield
                            mm1(fsl(ktT, hl), fsl(rbT, hl), ["rbT", "ktT"])
                            kb.tt("dve", ArkT[:, hl, :], pw, minc, ALU.mult, r=[P_, "C"], w=[rn("ArkT")])
                            yield
                            kb.tt("dve", chn["TTa"][:, hl, :], chn["BpA"][:, hl, :], ident, ALU.add, r=[rn("BpA"), "C"], w=[rn("TTa")])
                            Ap, Bp, TT = "ApA", "BpA", "TTa"
                            for it in range(n_it):
                                Ap2 = "ApB" if Ap == "ApA" else "ApA"
                                Bp2 = "BpB" if Bp == "BpA" else "BpA"
                                TT2 = "TTb" if TT == "TTa" else "TTa"
                                mm1(chn[Bp][:, hl, :], chn[Ap][:, hl, :], [rn(Ap), rn(Bp)])
                                kb.cp("act", chn[Ap2][:, hl, :], pw, r=[P_], w=[rn(Ap2)])
                                yield
                                if it < n_it - 1:
                                    mm1(chn[Ap][:, hl, :], chn[Bp][:, hl, :], [rn(Ap), rn(Bp)])
                                    kb.cp("act", chn[Bp2][:, hl, :], pw, r=[P_], w=[rn(Bp2)])
                                    yield
                                mm1(chn[Ap2][:, hl, :], chn[TT][:, hl, :], [rn(Ap2), rn(TT)])
                                kb.tt("dve", chn[TT2][:, hl, :], pw, chn[TT][:, hl, :], ALU.add, r=[P_, rn(TT)], w=[rn(TT2)])
                                yield
                                Ap, Bp, TT = Ap2, Bp2, TT2
                            if not sample:
                                for mk in (m16, m32, m64):
                                    TT2 = "TTb" if TT == "TTa" else "TTa"
                                    kb.tr(psb[B_][:, 0:128], chn[TT][:, hl, :], identR[:], r=[rn(TT), "identR"], w=[P_])
                                    kb.cp("act", chn["ApB"][:, hl, :], psb[B_][:, 0:128], r=[P_], w=[rn("ApB")])
                                    kb.tt("dve", chn["ApA"][:, hl, :], chn["Am"][:, hl, :], mk, ALU.mult, r=[rn("Am"), "C"], w=[rn("ApA")])
                                    yield
                                    mm1(chn["ApA"][:, hl, :], chn[TT][:, hl, :], [rn("ApA"), rn(TT)])
                                    kb.cp("act", chn["BpA"][:, hl, :], pw, r=[P_], w=[rn("BpA")])
                                    yield
                                    mm1(chn["ApB"][:, hl, :], chn["BpA"][:, hl, :], [rn("ApB"), rn("BpA")])
                                    kb.tt("dve", chn[TT2][:, hl, :], pw, chn[TT][:, hl, :], ALU.add, r=[P_, rn(TT)], w=[rn(TT2)])
                                    yield
                                    TT = TT2
                            TTh = chn[TT][:, hl, :]
                            mm1(TTh, tsl(AL, hl), [rn(TT), "AL"], cols=64)
                            kb.cp("act", Ahat[:, hl, :], ps[B_][:, 0:64], r=[P_], w=[rn("Ahat")])
                            yield
                            mm1(AakT[:, hl, :], tsl(Vv, hl), [rn("AakT"), "Vv"], cols=64)
                            kb.cp("dve", X1[:, hl, :], ps[B_][:, 0:64], r=[P_], w=[rn("X1")])
                            yield
                            mm1(TTh, X1[:, hl, :], [rn(TT), rn("X1")], cols=64)
                            kb.cp("act", U0[:, hl, :], ps[B_][:, 0:64], r=[P_], w=[rn("U0")])
                            yield
                            mm1(tsl(RB, hl), identR[:], ["RB", "identR"], rows=64, last=False)
                            mm1(Ahat[:, hl, :], ArbT[:, hl, :], [rn("Ahat"), rn("ArbT")], rows=64, first=False)
                            kb.cp("dve", RhT[:, hl, :], ps[B_][0:64, 0:128], r=[P_], w=[rn("RhT")])
                            yield
                            if sample:
                                return
                            mm1(Ahat[:, hl, :], tsl(BH, hl), [rn("Ahat"), "BH"], cols=64, rows=64)
                            kb.stt(Gm[:, hl, :], ident[0:64, 0:64], PLc[:, hl:hl + 1], ps[B_][0:64, 0:64], ALU.mult, ALU.add, r=["C", "tA1", P_], w=[rn("Gm")])
                            yield
                            mm1(ArkT[:, hl, :], tsl(Vv, hl), [rn("ArkT"), "Vv"], cols=64, last=False)
                            mm1(ArbT[:, hl, :], U0[:, hl, :], [rn("ArbT"), rn("U0")], cols=64, first=False, last=False)
                            mm1(RhT[:, hl, :], S0T[:, hl, :], [rn("RhT"), rn("S0T")], cols=64, first=False)
                            kb.cp("act", yb[:, hl * 64:(hl + 1) * 64], ps[B_][:, 0:64], r=[P_], w=["tA1"])
                            yield
                            if i == 15:
                                mm1(tsl(Vv, hl), tsl(KH, hl), ["Vv", "KH"], cols=64, rows=64, last=False)
                                mm1(U0[:, hl, :], tsl(BH, hl), [rn("U0"), "BH"], cols=64, rows=64, first=False, last=False)
                                mm1(S0T[:, hl, :], Gm[:, hl, :], [rn("S0T"), rn("Gm")], cols=64, rows=64, first=False)
                                kb.cp("dve", SLo[:, hl, :], ps[B_][0:64, 0:64], r=[P_], w=["S0nat"])
                                yield
                            mm1(tsl(KH, hl), tsl(Vv, hl), ["Vv", "KH"], cols=64, rows=64, last=False)
                            mm1(tsl(BH, hl), U0[:, hl, :], [rn("U0"), "BH"], cols=64, rows=64, first=False, last=False)
                            mm1(Gm[:, hl, :], S0T[:, hl, :], [rn("S0T"), rn("Gm")], cols=64, rows=64, first=False)
                            kb.cp("dve", S0T[:, hl, :], ps[B_][0:64, 0:64], r=[P_], w=[rn("S0T")])
                            yield

                        gens = [head_alg(hl) for hl in range(4)]
                        while gens:
                            for g_ in list(gens):
                                try:
                                    next(g_)
                                except StopIteration:
                                    gens.remove(g_)
                        if not sample:
                            if i == 15:
                                kb.dma("sp", o_pwkv[hg * 4:hg * 4 + 4].rearrange("h v k -> v h k"), SLo[:, 0:4, :], r=["S0nat"], w=())
                        else:
                            sample_states()
                        if hg == 0 and i == 0:
                            stop_at(57)
                        if hg == 0 and i == 16:
                            stop_at(60)
                        for hl in range(4):
                            s.add("dve", lambda g_, hl=hl: g_.bn_stats(out=small[:, 8 + hl * 6:14 + hl * 6], in_=yb[:, hl * 64:(hl + 1) * 64]), r=["tA1"], w=["tA1"], tag="bnst")
                        for hl in range(4):
                            s.add("dve", lambda g_, hl=hl: g_.bn_aggr(out=small[:, 32 + hl * 2:34 + hl * 2], in_=small[:, 8 + hl * 6:14 + hl * 6]), r=["tA1"], w=["tA1"], tag="bnag")
                        mvv = small[:, 32:40].rearrange("p (h two) -> p h two", two=2)
                        kb.act(small[:, 40:44].unsqueeze(2), mvv[:, :, 1:2], AF.Sqrt, r=["tA1"], w=["tA1"], bias=64e-5, scale=1.0)
                        s.add("dve", lambda g_: g_.reciprocal(out=small[:, 40:44], in_=small[:, 40:44]), r=["tA1"], w=["tA1"], tag="recip")
                        kb.tt("dve", small[:, 44:48].unsqueeze(2), mvv[:, :, 0:1], small[:, 40:44].unsqueeze(2), ALU.mult, r=["tA1"], w=["tA1"])
                        kb.ts("dve", small[:, 44:48], small[:, 44:48], -1.0, None, ALU.mult, None, r=["tA1"], w=["tA1"])
                        for hl in range(4):
                            kb.act(yb[:, hl * 64:(hl + 1) * 64], yb[:, hl * 64:(hl + 1) * 64], AF.Identity, r=["tA1", "tA1"], w=["tA1"],
                                   bias=small[:, 44 + hl:45 + hl], scale=small[:, 40 + hl:41 + hl])
                        kb.tt("dve", yb[:], yb[:], bcs["lngb"][:], ALU.mult, r=["tA1", "lngb"], w=["tA1"])
                        kb.tt("dve", yb[:], yb[:], bcs["lnbb"][:], ALU.add, r=["tA1", "lnbb"], w=["tA1"])
                        kb.tt("dve", Tt.rearrange("p (h k) -> p h k", k=64), Vf.rearrange("p (h k) -> p h k", k=64),
                              small[:, 4:8].unsqueeze(2).to_broadcast([128, 4, 64]), ALU.mult, r=["tA0", "tA1"], w=[tbres(7)])
                        kb.tt("dve", yb[:], yb[:], Tt, ALU.add, r=["tA1", tbres(7)], w=["tA1"])
                        kb.tt("dve", yb[:], yb[:], Gt, ALU.mult, r=["tA1", "tA0"], w=["tA1"])
                        for cc in range(2):
                            kb.tr(ps[7][:, cc * 128:(cc + 1) * 128], yb[:, cc * 128:(cc + 1) * 128], ident, r=["tA1", "C"], w=[PS[7]], sig=(cc == 1))
                        kb.cp("act", ygT[:, 2 * hg:2 * hg + 2, i * 128:(i + 1) * 128], ps[7][:, 0:256].rearrange("p (c t) -> p c t", t=128), r=[PS[7]], w=["ygT%d" % i])
                    s.barrier()
            for c in range(8):
                kb.tr(ps[0][0:17, c * 128:(c + 1) * 128] if c < 4 else ps[1][0:17, (c - 4) * 128:(c - 3) * 128], hlast[:, c, :], ident, r=["hlast", "C"],
                      w=[PS[0] if c < 4 else PS[1]], sig=(c in (3, 7)))
            kb.cp("dve", tB[0][0:17, 0:512], ps[0][0:17, :], r=[PS[0]], w=["tB0"])
            kb.cp("dve", tB[0][0:17, 512:1024], ps[1][0:17, :], r=[PS[1]], w=["tB0"])
            kb.dma("sp", o_pshift, tB[0][0:1, :], r=["tB0"], w=())
            kb.dma("sp", o_sshift, tB[0][1:17, :], r=["tB0"], w=())
            s.barrier()
            with contextlib.ExitStack() as pc_:
                alloc_gl(pc_)
                mod_prepare(l, 1, 1.0, blocks=[4, 5])
                Gp, Gs = gl["Gp"], gl["Gs"]
                wout = kb.sb(pc_, "wo_sb", [128, 8, D], BF16)
                kb.dma("pool", wout[:], rw_wo[0].rearrange("(kc p) n -> p kc n", p=128), r=(), w=["wout"])
                proj_acc(wout, "wout", 8, lambda i, kc: (ygT[:, kc, i * 128:(i + 1) * 128], "ygT%d" % i), True, True)
                s.barrier()
        s.barrier()

    def dump_and_finish():
        for i in range(NT):
            kb.dma("sp", yout[i * 128:(i + 1) * 128, :], X[:, i, :], r=["X%d" % i], w=())

    stage = 0
    for l in range(2):
        for sub in range(3):
            if sub == 0:
                ffn(l, 0, 0, 0.5)
            elif sub == 2:
                ffn(l, 1, 2, 0.5)
            elif l == 0:
                ab_mixer(l)
            else:
                rwkv_mixer(l)
            stage += 1
            if stage >= upto:
                return dump_and_finish()
    dump_and_finish()


def build(upto=99, stop_point=None):
    kb = KB()
    kb.stop_point = stop_point
    with contextlib.ExitStack() as es:
        kb.es = es
        build_program(kb, upto)
        kb.s.emit(kb.nc, es)
    return kb


def core_inputs(inp, core, kb):
    sl = slice(16 * core, 16 * core + 16)
    xs = inp["x_sample"][sl].reshape(128, D)
    f32 = np.float32

    def fm(v):
        return np.ascontiguousarray(np.asarray(v, f32).reshape(-1, 128).T)

    m = {
        "x": np.concatenate([inp["x_prompt"][core], xs], axis=0),
        "c": np.concatenate([inp["c_prompt"][core:core + 1], inp["c_sample"][sl]], axis=0),
        "cst": CST_ARR,
    }
    cw = inp["rg_conv_w"][0]
    vecA = np.zeros((128, 32), f32)
    for c in range(4):
        for j in range(4):
            vecA[:, c * 4 + j] = cw[j, c * 128:(c + 1) * 128]
    vecA[:, 16:20] = fm(inp["rg_conv_b"][0])
    vecA[:, 20:24] = fm(inp["rg_b_a"][0])
    vecA[:, 24:28] = fm(inp["rg_b_x"][0])
    vecA[:, 28:32] = fm(inp["rg_lambda"][0])
    m["vecA"] = vecA
    m["bgT"] = np.ascontiguousarray(inp["mlstm_b_gates"][0].T)
    m["minitT"] = np.ascontiguousarray(inp["state_mlstm_m"][0, sl].T)
    m["smC"] = inp["state_mlstm_C"][0, sl]
    m["smn"] = inp["state_mlstm_n"][0, sl]
    m["srh"] = inp["state_rglru_h"][0, sl]
    m["srconv"] = inp["state_rglru_conv"][0, sl].reshape(48, 512)
    mu = inp["rw_mu"][0]
    muT = np.zeros((128, 48), f32)
    for j in range(6):
        muT[:, j * 8:(j + 1) * 8] = fm(mu[j])
    m["muT"] = muT
    m["rk_flat"] = inp["rw_r_k"].reshape(1, D)
    m["swkv"] = inp["state_rwkv_wkv"][0, sl]
    m["sshift"] = inp["state_rwkv_shift"][0, sl]
    for k in kb.dram:
        if k not in m and k in inp:
            m[k] = inp[k]
    return {k: np.ascontiguousarray(v, dtype=f32) for k, v in m.items() if k in kb.dram}


_CACHE = {}


def kernel(**inputs):
    inp = {k: np.asarray(v) for k, v in inputs.items()}
    if "kb" not in _CACHE:
        _CACHE["kb"] = build()
    kb = _CACHE["kb"]
    in_maps = [core_inputs(inp, c, kb) for c in range(NCORES)]
    res = run_bass_kernel_spmd(kb.nc, in_maps, core_ids=list(range(NCORES)))
    R = res.results
    f32 = np.float32
    cat = lambda key, f=(lambda a: a): np.stack([f(np.asarray(R[c][key], f32)) for c in range(NCORES)], axis=0)
    cats = lambda key, f=(lambda a: a): np.concatenate([f(np.asarray(R[c][key], f32)) for c in range(NCORES)], axis=0)
    y_prompt = cat("y", lambda a: a[:2048])
    y_sample = cats("y", lambda a: a[2048:].reshape(16, 8, D))
    outs = (
        y_prompt, y_sample,
        cat("o_pmC")[None], cat("o_pmn")[None], cat("o_pmm", lambda a: a[:, 0])[None],
        cat("o_prh", lambda a: a.reshape(512))[None], cat("o_prconv")[None],
        cat("o_pwkv")[None], cat("o_pshift", lambda a: a[0])[None],
        cats("o_smC")[None], cats("o_smn")[None], cats("o_smm", lambda a: a.T)[None],
        cats("o_srh")[None], cats("o_srconv", lambda a: a.reshape(16, 3, 512))[None],
        cats("o_swkv")[None], cats("o_sshift")[None],
    )
    return tuple(np.ascontiguousarray(o, dtype=f32) for o in outs)
```

```python
import contextlib
import numpy as np
import concourse.bass as bass
import concourse.mybir as mybir
from concourse.bass_utils import run_bass_kernel_spmd

F32 = mybir.dt.float32
BF16 = mybir.dt.bfloat16
F32R = mybir.dt.float32r
AF = mybir.ActivationFunctionType
ALU = mybir.AluOpType
AX = mybir.AxisListType

D = 1024
DFF = 2816
NT = 17
NTOK = NT * 128
ALPHA = 4.0 ** 0.25
LN_EPS = 1e-5
NCORES = 8


class Op:
    __slots__ = ("eng", "fn", "deps", "sig", "idx", "dma", "slot", "slot_total", "sigcount", "waits", "tag")


class Sched:
    ENGS = ["pe", "act", "dve", "pool", "sp"]

    def __init__(self, n_slots=40):
        self.q = {e: [] for e in self.ENGS}
        self.last_w = {}
        self.readers = {}
        self.n_slots = n_slots
        self.slot_rr = 0
        self.sw_rr = 0
        self.n_hw = n_slots - 12
        self.slot_total = [0] * n_slots
        self.slot_last = [None] * n_slots
        self.all_dma = []

    skip = False

    def add(self, eng, fn, r=(), w=(), sig=True, dma=False, tag=""):
        if self.skip:
            return None
        op = Op()
        op.eng, op.fn, op.sig, op.dma, op.tag = eng, fn, sig, dma, tag
        op.slot = None
        deps = []
        seen = set()

        def dep(o):
            if o is None or id(o) in seen:
                return
            seen.add(id(o))
            if (not dma) and eng == "pe" and o.eng == "pe" and not o.dma:
                return
            deps.append(o)

        for k in r:
            dep(self.last_w.get(k))
        for k in w:
            dep(self.last_w.get(k))
            for o in self.readers.get(k, {}).values():
                dep(o)
        if dma:
            if eng == "pool":
                slot = self.n_hw + (self.sw_rr % (self.n_slots - self.n_hw))
                self.sw_rr += 1
            else:
                slot = self.slot_rr % self.n_hw
                self.slot_rr += 1
            dep(self.slot_last[slot])
            self.slot_total[slot] += 16
            op.slot = slot
            op.slot_total = self.slot_total[slot]
            self.slot_last[slot] = op
            self.all_dma.append(op)
        op.deps = deps
        self.q[eng].append(op)
        op.idx = len(self.q[eng]) - 1
        key = ("dma", id(op)) if dma else eng
        for k in r:
            self.readers.setdefault(k, {})[key] = op
        for k in w:
            self.last_w[k] = op
            self.readers[k] = {}
        return op

    def barrier(self):
        if self.skip:
            return
        lasts = []
        for e in self.ENGS:
            comp = [o for o in self.q[e] if (not o.dma) and o.fn is not None]
            if comp:
                comp[-1].sig = True
                lasts.append(comp[-1])
        lasts += [o for o in self.slot_last if o is not None]
        for e in self.ENGS:
            op = Op()
            op.eng, op.sig, op.dma, op.tag, op.slot, op.fn = e, False, False, "barrier", None, None
            op.deps = list(lasts)
            self.q[e].append(op)
            op.idx = len(self.q[e]) - 1
        self.last_w = {}
        self.readers = {}

    def finalize(self):
        for e in self.ENGS:
            for o in reversed(self.q[e]):
                if not o.dma and o.fn is not None and e != "sp":
                    o.sig = True
                    break
        self.sigtot = {}
        for e in self.ENGS:
            cnt = 0
            ops = self.q[e]
            pref = []
            for o in ops:
                if (not o.dma) and o.sig:
                    cnt += 1
                pref.append(cnt)
            self.sigtot[e] = cnt
            nxt = None
            for i in range(len(ops) - 1, -1, -1):
                o = ops[i]
                if (not o.dma) and o.sig:
                    nxt = pref[i]
                o.sigcount = nxt if not o.dma else None
        for e in self.ENGS:
            known = {}
            for o in self.q[e]:
                need = {}
                for d in o.deps:
                    if d.dma:
                        key, val = ("slot", d.slot), d.slot_total
                    else:
                        if d.sigcount is None:
                            raise RuntimeError("dependency on op with no later signal: %s" % d.tag)
                        key, val = ("eng", d.eng), d.sigcount
                    if val > need.get(key, 0):
                        need[key] = val
                o.waits = []
                for key, val in need.items():
                    if known.get(key, 0) < val:
                        known[key] = val
                        o.waits.append((key, val))

    def simulate(self):
        pc = {e: 0 for e in self.ENGS}
        sem = {}
        sigc = {e: 0 for e in self.ENGS}
        progress = True
        while progress:
            progress = False
            for e in self.ENGS:
                while pc[e] < len(self.q[e]):
                    o = self.q[e][pc[e]]
                    ok = all(sem.get(k, 0) >= v for k, v in o.waits)
                    if not ok:
                        break
                    if o.dma:
                        sem[("slot", o.slot)] = sem.get(("slot", o.slot), 0) + 16
                    elif o.sig:
                        sem[("eng", e)] = sem.get(("eng", e), 0) + 1
                    pc[e] += 1
                    progress = True
        stuck = {e: (pc[e], len(self.q[e])) for e in self.ENGS if pc[e] < len(self.q[e])}
        if stuck:
            msg = []
            for e, (p, n) in stuck.items():
                o = self.q[e][p]
                msg.append("%s stuck at %d/%d tag=%s waits=%s" % (e, p, n, o.tag, [(k, v, sem.get(k, 0)) for k, v in o.waits]))
            raise RuntimeError("DEADLOCK in wait graph:\n" + "\n".join(msg))

    def emit(self, nc, es):
        self.finalize()
        self.simulate()
        engsem = {e: es.enter_context(nc.semaphore("sem_" + e)) for e in ["pe", "act", "dve", "pool"]}
        slotsem = [es.enter_context(nc.semaphore("slot%d" % i)) for i in range(self.n_slots)]

        def semof(key):
            return engsem[key[1]] if key[0] == "eng" else slotsem[key[1]]

        def run(e, g):
            for o in self.q[e]:
                for key, val in o.waits:
                    g.wait_ge(semof(key), val)
                if o.fn is None:
                    continue
                ins = o.fn(g)
                if o.dma:
                    ins.then_inc(slotsem[o.slot], 16)
                elif o.sig:
                    ins.then_inc(engsem[e], 1)
            if e == "sp":
                for s in range(self.n_slots):
                    if self.slot_total[s] > 0:
                        g.wait_ge(slotsem[s], self.slot_total[s])

        with nc.Block() as blk:
            blk.tensor(lambda g: run("pe", g))
            blk.scalar(lambda g: run("act", g))
            blk.vector(lambda g: run("dve", g))
            blk.gpsimd(lambda g: run("pool", g))
            blk.sync(lambda g: run("sp", g))


class KB:
    def __init__(self, stop_after=None, debug=False):
        self.nc = bass.Bass("TRN2", target_bir_lowering=False)
        self.s = Sched()
        self.stop_after = stop_after
        self.debug = debug
        self.dram = {}

    def din(self, name, shape, dt=F32):
        t = self.nc.dram_tensor(name, list(shape), dt, kind="ExternalInput")
        self.dram[name] = t
        return t.ap()

    def dout(self, name, shape, dt=F32):
        t = self.nc.dram_tensor(name, list(shape), dt, kind="ExternalOutput")
        self.dram[name] = t
        return t.ap()

    def sb(self, es, name, shape, dt=F32):
        self.uid = getattr(self, "uid", 0) + 1
        return es.enter_context(self.nc.sbuf_tensor("%s_%d" % (name, self.uid), list(shape), dt))

    def mm(self, out, lhsT, rhs, start, stop, r, w, sig=None, tag="mm"):
        if sig is None:
            sig = stop
        return self.s.add("pe", lambda g: g.matmul(out, lhsT=lhsT, rhs=rhs, start=start, stop=stop), r=r, w=w, sig=sig, tag=tag)

    def tr(self, out, in_, ident, r, w, sig=True, tag="tr"):
        return self.s.add("pe", lambda g: g.transpose(out, in_, ident), r=r, w=w, sig=sig, tag=tag)

    def act(self, out, in_, func, r, w, bias=None, scale=None, eng="act", tag="act"):
        kw = {}
        if bias is not None:
            kw["bias"] = bias
        if scale is not None:
            kw["scale"] = scale
        return self.s.add("act", lambda g: g.activation(out=out, in_=in_, func=func, **kw), r=r, w=w, tag=tag)

    def tt(self, eng, out, in0, in1, op, r, w, tag="tt"):
        return self.s.add(eng, lambda g: g.tensor_tensor(out=out, in0=in0, in1=in1, op=op), r=r, w=w, tag=tag)

    def ts(self, eng, out, in0, s1, s2, op0, op1, r, w, tag="ts"):
        if op1 is None:
            return self.s.add(eng, lambda g: g.tensor_scalar(out=out, in0=in0, scalar1=s1, scalar2=None, op0=op0), r=r, w=w, tag=tag)
        return self.s.add(eng, lambda g: g.tensor_scalar(out=out, in0=in0, scalar1=s1, scalar2=s2, op0=op0, op1=op1), r=r, w=w, tag=tag)

    def stt(self, out, in0, scalar, in1, op0, op1, r, w, tag="stt"):
        return self.s.add("dve", lambda g: g.scalar_tensor_tensor(out=out, in0=in0, scalar=scalar, in1=in1, op0=op0, op1=op1), r=r, w=w, tag=tag)

    def cp(self, eng, out, in_, r, w, tag="cp"):
        if eng == "act":
            return self.s.add("act", lambda g: g.copy(out=out, in_=in_), r=r, w=w, tag=tag)
        return self.s.add(eng, lambda g: g.tensor_copy(out=out, in_=in_), r=r, w=w, tag=tag)

    def memset(self, eng, ap, val, w, tag="memset"):
        return self.s.add(eng, lambda g: g.memset(ap, val), r=(), w=w, tag=tag)

    def dma(self, q, out, in_, r, w, tag="dma", **kw):
        return self.s.add(q, lambda g: g.dma_start(out=out, in_=in_, **kw), r=r, w=w, dma=True, tag=tag)


def make_consts():
    c = {}
    c["ident"] = np.eye(128, dtype=np.float32)
    selP = np.zeros((128, 128), np.float32)
    selP[0, :] = 1.0
    selS = np.zeros((128, 128), np.float32)
    for p in range(128):
        selS[1 + p // 8, p] = 1.0
    c["selP"] = selP
    c["selS"] = selS
    st = np.arange(128)
    c["maskP"] = (st[:, None] <= st[None, :]).astype(np.float32)
    c["maskS"] = ((st[:, None] <= st[None, :]) & (st[:, None] // 8 == st[None, :] // 8)).astype(np.float32)
    c["rst"] = np.tile((st % 8 != 0).astype(np.float32)[None, :], (128, 1))
    c["rstm"] = np.tile(np.where(st % 8 == 0, -1e30, 0.0).astype(np.float32)[None, :], (128, 1))
    bms = np.zeros((128, 128), np.float32)
    bms[st, st // 8] = 1.0
    c["bms"] = bms
    c["ones"] = np.ones((128, 128), np.float32)
    same = (st[:, None] // 8 == st[None, :] // 8)
    c["upP"] = (st[:, None] < st[None, :]).astype(np.float32)
    c["lowP"] = (st[:, None] > st[None, :]).astype(np.float32)
    c["upS"] = ((st[:, None] < st[None, :]) & same).astype(np.float32)
    c["lowS"] = ((st[:, None] > st[None, :]) & same).astype(np.float32)
    c["blkS"] = same.astype(np.float32)
    blk = lambda b: (st[:, None] // b == st[None, :] // b)
    c["low16"] = ((st[:, None] > st[None, :]) & blk(16)).astype(np.float32)
    c["up16"] = ((st[:, None] < st[None, :]) & blk(16)).astype(np.float32)
    for b in (16, 32, 64):
        c["m%d" % b] = (blk(2 * b) & ~blk(b)).astype(np.float32)
    names = list(c.keys())
    arr = np.concatenate([c[k] for k in names], axis=1)
    offs = {}
    o = 0
    for k in names:
        offs[k] = o
        o += c[k].shape[1]
    return arr, offs


CST_ARR, CST_OFF = make_consts()
NCST = CST_ARR.shape[1]

FFN_PARTS = [(0, 4), (4, 4), (8, 4), (12, 4), (16, 4), (20, 2)]
TGS = [(0, 512), (512, 512), (1024, 512), (1536, 512), (2048, 128)]


def build_program(kb, upto=99):
    nc, s = kb.nc, kb.s
    es = kb.es
    xin = kb.din("x", [NTOK, D])
    cin = kb.din("c", [NT, D])
    cst = kb.din("cst", [128, NCST])
    ada_w = kb.din("ada_w", [2, D, 9 * D])
    ada_b = kb.din("ada_b", [2, 9 * D])
    ln_g = kb.din("ln_g", [2, 3, D])
    ln_b = kb.din("ln_b", [2, 3, D])
    ffn_w1 = kb.din("ffn_w1", [2, 2, D, DFF])
    ffn_w3 = kb.din("ffn_w3", [2, 2, D, DFF])
    ffn_w2 = kb.din("ffn_w2", [2, 2, DFF, D])
    yout = kb.dout("y", [NTOK, D])

    X = kb.sb(es, "X", [128, NT, D], F32)
    C = kb.sb(es, "cst_sb", [128, NCST], F32)
    cT = kb.sb(es, "cT", [128, 8, NT], BF16)
    onesb = kb.sb(es, "onesb", [1, 32], F32)
    modT = kb.sb(es, "modT", [128, 16, NT], F32)
    gl = {}

    def alloc_gl(stack):
        gl["Gp"] = kb.sb(stack, "Gp", [128, D], F32)
        gl["Gs"] = kb.sb(stack, "Gs", [128, D], F32)
        gl["LNg"] = kb.sb(stack, "LNg", [128, D], F32)
        gl["LNb"] = kb.sb(stack, "LNb", [128, D], F32)
    tA = [kb.sb(es, "tA%d" % i, [128, 512], F32) for i in range(4)]
    tB = [kb.sb(es, "tB%d" % i, [128, D], F32) for i in range(2)]
    stt_ = [kb.sb(es, "bnst%d" % i, [128, 2, 6], F32) for i in range(2)]
    mv = [kb.sb(es, "mv%d" % i, [128, 2], F32) for i in range(2)]
    rstd = [kb.sb(es, "rstd%d" % i, [128, 1], F32) for i in range(2)]
    nmr = [kb.sb(es, "nmr%d" % i, [128, 1], F32) for i in range(2)]
    tmpS = kb.sb(es, "tmpS", [128, 128], F32)
    ps = [es.enter_context(nc.psum_tensor("ps%d" % i, [128, 512], F32)) for i in range(8)]
    PS = ["ps%d" % i for i in range(8)]
    psb = [p.bitcast(BF16) for p in ps]

    ident = C[:, CST_OFF["ident"]:CST_OFF["ident"] + 128]
    selP = C[0:NT, CST_OFF["selP"]:CST_OFF["selP"] + 128]
    selS = C[0:NT, CST_OFF["selS"]:CST_OFF["selS"] + 128]

    kb.dma("sp", C[:], cst, r=(), w=["C"])
    for i in range(NT):
        kb.dma("sp", X[:, i, :], xin[i * 128:(i + 1) * 128, :], r=(), w=["X%d" % i])
    kb.memset("pool", onesb[:], 1.0, w=["onesb"])

    with contextlib.ExitStack() as ph0:
        c_sb = kb.sb(ph0, "c_sb", [NT, D], F32)
        cs_sb = kb.sb(ph0, "cs_sb", [NT, D], F32)
        kb.dma("sp", c_sb[:], cin, r=(), w=["c_sb"])
        kb.act(cs_sb[:], c_sb[:], AF.Silu, r=["c_sb"], w=["cs_sb"])
        for kc in range(8):
            kb.tr(ps[0][:, kc * NT:(kc + 1) * NT], cs_sb[0:NT, kc * 128:(kc + 1) * 128], C[0:NT, 0:NT],
                  r=["cs_sb", "C"], w=[PS[0]], sig=(kc == 7))
        kb.cp("dve", cT[:].rearrange("p a b -> p (a b)"), ps[0][:, 0:8 * NT], r=[PS[0]], w=["cT"])
    s.barrier()

    state = {"ada_i": 0, "ada_i2": 0, "psr": 0}

    def mod_prepare(l, sub, res_w, blocks=range(6)):
        if 5 in blocks:
            Gp, Gs = gl["Gp"], gl["Gs"]
            kb.dma("sp", gl["LNg"][:], ln_g[l, sub:sub + 1, :].to_broadcast([128, D]), r=(), w=["LNg"])
            kb.dma("sp", gl["LNb"][:], ln_b[l, sub:sub + 1, :].to_broadcast([128, D]), r=(), w=["LNb"])
        phm = contextlib.ExitStack()
        modst = [kb.sb(phm, "modst%d" % i, [NT, 512], F32) for i in range(2)]
        adaw = [kb.sb(phm, "adaw%d" % i, [128, 8, 256], BF16) for i in range(2)]
        adab = [kb.sb(phm, "adab%d" % i, [1, 512], F32) for i in range(2)]
        for b in blocks:
            i = state["ada_i"]
            state["ada_i"] += 1
            buf = i % 2
            co = sub * 3 * D + b * 512
            kb.dma("sp", adab[buf][:], ada_b[l:l + 1, co:co + 512], r=(), w=["adab%d" % buf])
            pm = 4 + (i % 2)
            for sbk in range(2):
                i2 = state["ada_i2"]
                state["ada_i2"] += 1
                wb = i2 % 2
                kb.dma("pool", adaw[wb][:], ada_w[l].rearrange("(kc p) n -> p kc n", p=128)[:, :, co + sbk * 256:co + (sbk + 1) * 256],
                       r=(), w=["adaw%d" % wb])
                for kc in range(8):
                    kb.mm(ps[pm][0:NT, sbk * 256:(sbk + 1) * 256], lhsT=cT[:, kc, :], rhs=adaw[wb][:, kc, :], start=(kc == 0), stop=False,
                          r=["cT", "adaw%d" % wb], w=[PS[pm]], sig=False)
                kb.mm(ps[pm][0:NT, sbk * 256:(sbk + 1) * 256], lhsT=onesb[0:1, 0:NT], rhs=adab[buf][:, sbk * 256:(sbk + 1) * 256], start=False, stop=True,
                      r=["onesb", "adab%d" % buf], w=[PS[pm]], sig=True)
            kb.cp("act", modst[buf][:], ps[pm][0:NT, :], r=[PS[pm]], w=["modst%d" % buf])
            if b < 4:
                for cc in range(4):
                    j = b * 4 + cc
                    kb.tr(ps[6][:, j * NT:(j + 1) * NT], modst[buf][0:NT, cc * 128:(cc + 1) * 128], C[0:NT, 0:NT],
                          r=["modst%d" % buf, "C"], w=[PS[6]], sig=(cc == 3))
                if b == 1:
                    kb.cp("dve", modT[:, 0:8, :].rearrange("p a b -> p (a b)"), ps[6][:, 0:8 * NT], r=[PS[6]], w=["modT"])
                if b == 3:
                    kb.ts("dve", modT[:, 8:16, :].rearrange("p a b -> p (a b)"), ps[6][:, 8 * NT:16 * NT], 1.0, None,
                          ALU.add, None, r=[PS[6]], w=["modT"])
            else:
                h = b - 4
                kb.mm(ps[7][:, :], lhsT=selP, rhs=modst[buf][:], start=True, stop=True, r=["C", "modst%d" % buf], w=[PS[7]])
                kb.act(Gp[:, h * 512:(h + 1) * 512], ps[7][:, :], AF.Identity, r=[PS[7]], w=["Gp"], bias=float(res_w), scale=float(res_w))
                kb.mm(ps[7][:, :], lhsT=selS, rhs=modst[buf][:], start=True, stop=True, r=["C", "modst%d" % buf], w=[PS[7]])
                kb.act(Gs[:, h * 512:(h + 1) * 512], ps[7][:, :], AF.Identity, r=[PS[7]], w=["Gs"], bias=float(res_w), scale=float(res_w))
        s.barrier()
        phm.close()

    def make_hT(hT, i, prescale=True, col0=None, res=None, hook=None):
        g = res if res is not None else "hT_g%d" % (i // 4)
        if col0 is None:
            col0 = i * 128
        for half in range(2):
            pb = state["psr"] % 4
            state["psr"] += 1
            for cc in range(4):
                c = half * 4 + cc
                kb.tr(ps[pb][:, cc * 128:(cc + 1) * 128], X[:, i, c * 128:(c + 1) * 128], ident,
                      r=["X%d" % i, "C"], w=[PS[pb]], sig=(cc == 3))
            for cc in range(4):
                c = half * 4 + cc
                src = ps[pb][:, cc * 128:(cc + 1) * 128]
                if hook is not None:
                    hook(i, c, src, PS[pb])
                if i < 16:
                    if cc % 2 == 0:
                        kb.act(hT[:, c, col0:col0 + 128], src, AF.Identity, r=[PS[pb], "modT"], w=[g],
                               bias=modT[:, c, 0:1], scale=modT[:, 8 + c, 0:1])
                    else:
                        kb.ts("dve", hT[:, c, col0:col0 + 128], src, modT[:, 8 + c, 0:1], modT[:, c, 0:1],
                              ALU.mult, ALU.add, r=[PS[pb], "modT"], w=[g])
                else:
                    sc = modT[:, 8 + c, 1:NT].unsqueeze(2).to_broadcast([128, 16, 8])
                    sh = modT[:, c, 1:NT].unsqueeze(2).to_broadcast([128, 16, 8])
                    kb.tt("dve", tmpS[:].rearrange("p (q t) -> p q t", t=8), src.rearrange("p (q t) -> p q t", t=8), sc,
                          ALU.mult, r=[PS[pb], "modT"], w=["tmpS"])
                    kb.tt("dve", hT[:, c, col0:col0 + 128].rearrange("p (q t) -> p q t", t=8),
                          tmpS[:].rearrange("p (q t) -> p q t", t=8), sh, ALU.add, r=["tmpS", "modT"], w=[g])

    def layer_norm(i):
        k = i % 2
        for h in range(2):
            s.add("dve", lambda g_, h=h, k=k, i=i: g_.bn_stats(out=stt_[k][:, h, :], in_=X[:, i, h * 512:(h + 1) * 512]),
                  r=["X%d" % i], w=["bnst%d" % k], tag="bnstats")
        s.add("dve", lambda g_, k=k: g_.bn_aggr(out=mv[k][:], in_=stt_[k][:].rearrange("p a b -> p (a b)")),
              r=["bnst%d" % k], w=["mv%d" % k], tag="bnaggr")
        kb.act(rstd[k][:], mv[k][:, 1:2], AF.Sqrt, r=["mv%d" % k], w=["rstd%d" % k], bias=float(LN_EPS), scale=1.0)
        s.add("dve", lambda g_, k=k: g_.reciprocal(out=rstd[k][:], in_=rstd[k][:]), r=["rstd%d" % k], w=["rstd%d" % k], tag="recip")
        kb.ts("dve", nmr[k][:], mv[k][:, 0:1], rstd[k][:, 0:1], -1.0, ALU.mult, ALU.mult, r=["mv%d" % k, "rstd%d" % k], w=["nmr%d" % k])
        kb.act(tB[k][:], X[:, i, :], AF.Identity, r=["X%d" % i, "rstd%d" % k, "nmr%d" % k], w=["tB%d" % k],
               bias=nmr[k][:, 0:1], scale=rstd[k][:, 0:1])
        kb.tt("dve", tB[k][:], tB[k][:], gl["LNg"][:], ALU.mult, r=["tB%d" % k, "LNg"], w=["tB%d" % k])
        kb.tt("dve", X[:, i, :], tB[k][:], gl["LNb"][:], ALU.add, r=["tB%d" % k, "LNb"], w=["X%d" % i])

    pacc = {"v": 0}

    def proj_acc(wt, wres, nk, lhs_of, do_ln, first):
        Gp, Gs = gl["Gp"], gl["Gs"]
        for i in [16] + list(range(16)):
            if i == 0:
                kb.tt("dve", wt[:, 0:nk, :], wt[:, 0:nk, :], Gp[:].unsqueeze(1).to_broadcast([128, nk, D]), ALU.mult, r=[wres, "Gp"], w=[wres])
            for half in range(2):
                v = pacc["v"]
                pacc["v"] += 1
                py = 4 + (v % 4)
                for kc in range(nk):
                    lt, lres = lhs_of(i, kc)
                    kb.mm(ps[py][:, :], lhsT=lt, rhs=wt[:, kc, half * 512:(half + 1) * 512], start=(kc == 0), stop=(kc == nk - 1),
                          r=[lres, wres], w=[PS[py]])
                xs = X[:, i, half * 512:(half + 1) * 512]
                src = ps[py][:, :]
                rsrc = PS[py]
                if i == 16:
                    kb.tt("dve", tA[v % 4][:], ps[py][:, :], Gs[:, half * 512:(half + 1) * 512], ALU.mult, r=[PS[py], "Gs"], w=["tA%d" % (v % 4)])
                    src, rsrc = tA[v % 4][:], "tA%d" % (v % 4)
                if first:
                    kb.stt(xs, xs, float(ALPHA), src, ALU.mult, ALU.add, r=["X%d" % i, rsrc], w=["X%d" % i])
                else:
                    kb.tt("dve", xs, xs, src, ALU.add, r=["X%d" % i, rsrc], w=["X%d" % i])
            if do_ln:
                layer_norm(i)

    def ffn(l, f, sub, res_w):
        with contextlib.ExitStack() as ph:
            alloc_gl(ph)
            mod_prepare(l, sub, res_w)
            Gp, Gs = gl["Gp"], gl["Gs"]
            hT = kb.sb(ph, "hT", [128, 8, NTOK], BF16)
            w1p = kb.sb(ph, "w1p", [128, 8, 512], BF16)
            w3p = kb.sb(ph, "w3p", [128, 8, 512], BF16)
            w2p = kb.sb(ph, "w2p", [128, 4, D], BF16)
            gbuf = kb.sb(ph, "gbuf", [128, 4, NTOK], BF16)
            sil = [kb.sb(ph, "sil%d" % i, [128, 512], F32) for i in range(2)]
            w1v = ffn_w1[l, f].rearrange("(kc p) n -> p kc n", p=128)
            w3v = ffn_w3[l, f].rearrange("(kc p) n -> p kc n", p=128)
            w2v = ffn_w2[l, f].rearrange("(j p) n -> p j n", p=128)
            u = 0
            v = 0
            def load_up(pi_):
                j0_, n_ = FFN_PARTS[pi_]
                kb.dma("pool", w1p[:, :, 0:n_ * 128], w1v[:, :, j0_ * 128:(j0_ + n_) * 128], r=(), w=["w1p"])
                kb.dma("pool", w3p[:, :, 0:n_ * 128], w3v[:, :, j0_ * 128:(j0_ + n_) * 128], r=(), w=["w3p"])

            def load_dn(pi_):
                j0_, n_ = FFN_PARTS[pi_]
                kb.dma("pool", w2p[:, 0:n_, :], w2v[:, j0_:j0_ + n_, :], r=(), w=["w2p"])

            load_up(0)
            load_dn(0)
            for i in range(NT):
                make_hT(hT, i)
            for pi, (j0, ncn) in enumerate(FFN_PARTS):
                for tg, (t0, nt_) in enumerate(TGS):
                    for jj in range(ncn):
                        pa, pb = (2 * u) % 4, (2 * u + 1) % 4
                        for kc in range(8):
                            kb.mm(ps[pa][:, 0:nt_], lhsT=w1p[:, kc, jj * 128:(jj + 1) * 128], rhs=hT[:, kc, t0:t0 + nt_],
                                  start=(kc == 0), stop=(kc == 7), r=["w1p", "hT_g%d" % tg], w=[PS[pa]])
                        for kc in range(8):
                            kb.mm(ps[pb][:, 0:nt_], lhsT=w3p[:, kc, jj * 128:(jj + 1) * 128], rhs=hT[:, kc, t0:t0 + nt_],
                                  start=(kc == 0), stop=(kc == 7), r=["w3p", "hT_g%d" % tg], w=[PS[pb]])
                        kb.act(sil[u % 2][:, 0:nt_], ps[pa][:, 0:nt_], AF.Silu, r=[PS[pa]], w=["sil%d" % (u % 2)])
                        kb.tt("dve", gbuf[:, jj, t0:t0 + nt_], sil[u % 2][:, 0:nt_], ps[pb][:, 0:nt_], ALU.mult,
                              r=["sil%d" % (u % 2), PS[pb]], w=["g_g%d" % tg])
                        u += 1
                if pi + 1 < len(FFN_PARTS):
                    load_up(pi + 1)
                proj_acc(w2p, "w2p", ncn, lambda i, jj: (gbuf[:, jj, i * 128:(i + 1) * 128], "g_g%d" % (i // 4)), pi == len(FFN_PARTS) - 1, pi == 0)
                if pi + 1 < len(FFN_PARTS):
                    load_dn(pi + 1)
        s.barrier()

    DKS = float(128 ** -0.5)

    class _Stop(Exception):
        pass

    def stop_at(n):
        if getattr(kb, "stop_point", None) == n:
            s.barrier()
            s.skip = True

    def ab_mixer(l):
        ab_mixer_(l)
        s.skip = False
        s.barrier()

    def ab_mixer_(l):
        ab_w_in = kb.din("ab_w_in", [1, D, 3080])
        ab_w_out = kb.din("ab_w_out", [1, D, D])
        mnorm_g = kb.din("mlstm_norm_g", [1, 512])
        vecA_d = kb.din("vecA", [128, 32])
        bgT_d = kb.din("bgT", [4, 2])
        minitT_d = kb.din("minitT", [4, 16])
        rg_w_a = kb.din("rg_w_a", [1, 8, 64, 64])
        rg_w_x = kb.din("rg_w_x", [1, 8, 64, 64])
        smC = kb.din("smC", [16, 4, 128, 128])
        smn = kb.din("smn", [16, 4, 128])
        srh = kb.din("srh", [16, 512])
        srconv = kb.din("srconv", [48, 512])
        o_pmC = kb.dout("o_pmC", [4, 128, 128])
        o_pmn = kb.dout("o_pmn", [4, 128])
        o_pmm = kb.dout("o_pmm", [4, 1])
        o_prh = kb.dout("o_prh", [4, 128])
        o_prconv = kb.dout("o_prconv", [3, 512])
        o_smC = kb.dout("o_smC", [16, 4, 128, 128])
        o_smn = kb.dout("o_smn", [16, 4, 128])
        o_smm = kb.dout("o_smm", [4, 16])
        o_srh = kb.dout("o_srh", [16, 512])
        o_srconv = kb.dout("o_srconv", [48, 512])

        maskP = C[:, CST_OFF["maskP"]:CST_OFF["maskP"] + 128]
        maskS = C[:, CST_OFF["maskS"]:CST_OFF["maskS"] + 128]
        rst = C[:, CST_OFF["rst"]:CST_OFF["rst"] + 128]
        rstm = C[:, CST_OFF["rstm"]:CST_OFF["rstm"] + 128]
        bms = C[:, CST_OFF["bms"]:CST_OFF["bms"] + 16]
        ones = C[:, CST_OFF["ones"]:CST_OFF["ones"] + 128]

        mod_prepare(l, 1, 1.0, blocks=range(4))
        win_v = ab_w_in[0].rearrange("(kc p) n -> p kc n", p=128)
        with contextlib.ExitStack() as ph:
            hmT = kb.sb(ph, "hmT", [128, 4, NTOK], BF16)
            vecA = kb.sb(ph, "vecA_sb", [128, 32], F32)
            kb.dma("sp", vecA[:], vecA_d, r=(), w=["vecA"])
            sigo = tA[0]
            hmf = tB[0][:, 0:512]
            with contextlib.ExitStack() as pa:
                winA = kb.sb(pa, "winA", [128, 8, 2056], BF16)
                kb.dma("pool", winA[:, :, 0:1024], win_v[:, :, 0:1024], r=(), w=["winA"])
                kb.dma("pool", winA[:, :, 1024:2056], win_v[:, :, 1024:2056], r=(), w=["winA"])
                qkT = kb.sb(pa, "qkT", [128, 8, 512], BF16)
                hTg = [kb.sb(pa, "hTgA%d" % i, [128, 8, 512], BF16) for i in range(2)]
                bg = kb.sb(pa, "bg", [4, 2], F32)
                nbg1 = kb.sb(pa, "nbg1", [4, 1], F32)
                minitT = kb.sb(pa, "minitT_sb", [4, 16], F32)
                mng = kb.sb(pa, "mng", [128, 512], F32)
                kb.dma("sp", bg[:], bgT_d, r=(), w=["bg"])
                kb.dma("sp", minitT[:], minitT_d, r=(), w=["minitT"])
                kb.dma("sp", mng[:], mnorm_g[0:1, :].to_broadcast([128, 512]), r=(), w=["mng"])
                kb.ts("dve", nbg1[:], bg[:, 1:2], -1.0, None, ALU.mult, None, r=["bg"], w=["nbg1"])
                R4 = lambda nm: kb.sb(pa, nm, [4, 128], F32)
                t1, IGa, Rt, t3, t4 = R4("r_t1"), R4("r_ig"), R4("r_rt"), R4("r_t3"), R4("r_t4")
                Bc = [R4("r_bc0"), R4("r_bc1")]
                Mx = [R4("r_mx0"), R4("r_mx1")]
                dd = kb.sb(pa, "r_dd", [4, 16], F32)
                DDm = kb.sb(pa, "r_DD", [4, 64], F32)
                mout = kb.sb(pa, "r_mout", [4, 16], F32)
                colq = [kb.sb(pa, "colq%d" % i, [128, 16], F32) for i in range(2)]
                decsb = kb.sb(pa, "decsb", [128, 64], F32)
                kw = kb.sb(pa, "kw", [128, 4, 128], BF16)
                ktok = kb.sb(pa, "ktok", [128, 4, 128], BF16)
                vext = [kb.sb(pa, "vext%d" % i, [128, 4, 130], BF16) for i in range(2)]
                PT = kb.sb(pa, "PT", [128, 4, 128], BF16)
                Cst = kb.sb(pa, "Cst", [128, 4, 130], F32)
                Cb = kb.sb(pa, "Cb", [128, 4, 130], BF16)
                dmax = kb.sb(pa, "dmax", [128, 4], F32)
                hst6 = kb.sb(pa, "hst6", [128, 4, 6], F32)
                hmv = kb.sb(pa, "hmv", [128, 4, 2], F32)
                hrs = kb.sb(pa, "hrs", [128, 4], F32)
                hnm = kb.sb(pa, "hnm", [128, 4], F32)
                for vv in vext:
                    kb.memset("pool", vv[:], 1.0, w=["vext0", "vext1"])
                kb.memset("pool", Cst[:], 0.0, w=["Cst"])
                kb.memset("pool", Cb[:], 0.0, w=["Cb"])

                def rows(i):
                    k = i % 2
                    hb = (i // 4) % 2
                    hT = hTg[hb]
                    tc0 = (i % 4) * 128
                    pg = ps[7]
                    for kc in range(8):
                        kb.mm(pg[0:4, 0:128], lhsT=winA[:, kc, 2048:2052], rhs=hT[:, kc, tc0:tc0 + 128], start=(kc == 0), stop=(kc == 7),
                              r=["winA", "hTg%d" % hb], w=[PS[7]])
                    for kc in range(8):
                        kb.mm(pg[0:4, 128:256], lhsT=winA[:, kc, 2052:2056], rhs=hT[:, kc, tc0:tc0 + 128], start=(kc == 0), stop=(kc == 7),
                              r=["winA", "hTg%d" % hb], w=[PS[7]])
                    kb.act(IGa[:], pg[0:4, 0:128], AF.Identity, r=[PS[7], "bg"], w=["r_ig"], bias=bg[:, 0:1], scale=1.0)
                    kb.act(t1[:], pg[0:4, 128:256], AF.Exp, r=[PS[7], "nbg1"], w=["r_t1"], bias=nbg1[:, 0:1], scale=-1.0)
                    kb.act(t1[:], t1[:], AF.Ln, r=["r_t1"], w=["r_t1"], bias=1.0, scale=1.0)
                    kb.ts("dve", t1[:], t1[:], -1.0, None, ALU.mult, None, r=["r_t1"], w=["r_t1"])
                    prompt = i < 16
                    if prompt:
                        binit = 0.0 if i == 0 else Bc[1 - k][:, 127:128]
                        minit = 0.0 if i == 0 else Mx[1 - k][:, 127:128]
                        s.add("dve", lambda g_: g_.tensor_tensor_scan(out=Bc[k][:], data0=ones[0:4, :], data1=t1[:], initial=binit,
                                                                       op0=ALU.mult, op1=ALU.add),
                              r=["r_t1", "r_bc%d" % (1 - k), "C"], w=["r_bc%d" % k], tag="scanB")
                        kb.tt("dve", IGa[:], IGa[:], Bc[k][:], ALU.subtract, r=["r_ig", "r_bc%d" % k], w=["r_ig"])
                        kb.memset("dve", t3[:], 0.0, w=["r_t3"])
                        s.add("dve", lambda g_: g_.tensor_tensor_scan(out=Mx[k][:], data0=t3[:], data1=IGa[:], initial=minit,
                                                                       op0=ALU.add, op1=ALU.max),
                              r=["r_t3", "r_ig", "r_mx%d" % (1 - k)], w=["r_mx%d" % k], tag="scanM")
                        if i == 0:
                            kb.memset("dve", Rt[:], 0.0, w=["r_rt"])
                        else:
                            kb.cp("dve", Rt[:], Mx[1 - k][:, 127:128].to_broadcast([4, 128]), r=["r_mx%d" % (1 - k)], w=["r_rt"])
                    else:
                        s.add("dve", lambda g_: g_.tensor_tensor_scan(out=Bc[k][:], data0=rst[0:4, :], data1=t1[:], initial=0.0,
                                                                       op0=ALU.mult, op1=ALU.add),
                              r=["r_t1", "C"], w=["r_bc%d" % k], tag="scanB")
                        kb.tt("dve", IGa[:], IGa[:], Bc[k][:], ALU.subtract, r=["r_ig", "r_bc%d" % k], w=["r_ig"])
                        kb.cp("dve", t3[:], IGa[:], r=["r_ig"], w=["r_t3"])
                        kb.tt("dve", t3[:].rearrange("p (q t) -> p q t", t=8)[:, :, 0:1], IGa[:].rearrange("p (q t) -> p q t", t=8)[:, :, 0:1],
                              minitT[:].unsqueeze(2), ALU.max, r=["r_ig", "minitT"], w=["r_t3"])
                        s.add("dve", lambda g_: g_.tensor_tensor_scan(out=Mx[k][:], data0=rstm[0:4, :], data1=t3[:], initial=0.0,
                                                                       op0=ALU.add, op1=ALU.max),
                              r=["r_t3", "C"], w=["r_mx%d" % k], tag="scanM")
                        kb.cp("dve", Rt[:].rearrange("p (q t) -> p q t", t=8), minitT[:].unsqueeze(2).to_broadcast([4, 16, 8]),
                              r=["minitT"], w=["r_rt"])
                    kb.tt("dve", t3[:], IGa[:], Rt[:], ALU.subtract, r=["r_ig", "r_rt"], w=["r_t3"])
                    kb.act(t3[:], t3[:], AF.Exp, r=["r_t3"], w=["r_t3"])
                    kb.tt("dve", t4[:], Bc[k][:], Rt[:], ALU.add, r=["r_bc%d" % k, "r_rt"], w=["r_t4"])
                    kb.act(t4[:], t4[:], AF.Exp, r=["r_t4"], w=["r_t4"], scale=-1.0)
                    kb.tr(pg[:, 256:260], t3[0:4, :], C[0:4, 0:4], r=["r_t3", "C"], w=[PS[7]])
                    kb.tr(pg[:, 260:264], t4[0:4, :], C[0:4, 0:4], r=["r_t4", "C"], w=[PS[7]])
                    if prompt:
                        kb.tt("dve", dd[:, 0:1], Rt[:, 0:1], Mx[k][:, 127:128], ALU.subtract, r=["r_rt", "r_mx%d" % k], w=["r_dd"])
                        kb.act(dd[:, 0:1], dd[:, 0:1], AF.Exp, r=["r_dd"], w=["r_dd"])
                        kb.ts("dve", DDm[:, 0:4], C[0:4, 0:4], dd[:, 0:1], None, ALU.mult, None, r=["r_dd", "C"], w=["r_DD"])
                        kb.mm(pg[:, 264:268], lhsT=ones[0:4, :], rhs=DDm[:, 0:4], start=True, stop=True, r=["C", "r_DD"], w=[PS[7]])
                        kb.cp("dve", colq[k][:, 0:12], pg[:, 256:268], r=[PS[7]], w=["colq%d" % k])
                        if i == 15:
                            kb.tt("dve", mout[:, 0:1], Bc[k][:, 127:128], Mx[k][:, 127:128], ALU.add, r=["r_bc%d" % k, "r_mx%d" % k], w=["r_mout"])
                            kb.dma("sp", o_pmm, mout[:, 0:1], r=["r_mout"], w=())
                    else:
                        MT = Mx[k][:].rearrange("p (q t) -> p q t", t=8)[:, :, 7:8]
                        kb.tt("dve", t4[:].rearrange("p (q t) -> p q t", t=8), IGa[:].rearrange("p (q t) -> p q t", t=8),
                              MT.to_broadcast([4, 16, 8]), ALU.subtract, r=["r_ig", "r_mx%d" % k, PS[7]], w=["r_t4"])
                        kb.act(t4[:], t4[:], AF.Exp, r=["r_t4"], w=["r_t4"])
                        kb.tr(pg[:, 264:268], t4[0:4, :], C[0:4, 0:4], r=["r_t4", "C"], w=[PS[7]])
                        kb.cp("dve", colq[k][:, 0:12], pg[:, 256:268], r=[PS[7]], w=["colq%d" % k])
                        kb.tt("dve", dd[:].unsqueeze(2), minitT[:].unsqueeze(2), MT, ALU.subtract, r=["minitT", "r_mx%d" % k], w=["r_dd"])
                        kb.act(dd[:], dd[:], AF.Exp, r=["r_dd"], w=["r_dd"])
                        kb.tt("dve", DDm[:].rearrange("p (q h) -> p q h", h=4), dd[:].unsqueeze(2).to_broadcast([4, 16, 4]),
                              C[0:4, 0:4].unsqueeze(1).to_broadcast([4, 16, 4]), ALU.mult, r=["r_dd", "C"], w=["r_DD"])
                        kb.mm(pg[:, 272:336], lhsT=ones[0:4, :], rhs=DDm[:], start=True, stop=True, r=["C", "r_DD"], w=[PS[7]])
                        kb.cp("dve", decsb[:], pg[:, 272:336], r=[PS[7]], w=["decsb"])
                        kb.tt("dve", mout[:].unsqueeze(2), Bc[k][:].rearrange("p (q t) -> p q t", t=8)[:, :, 7:8], MT, ALU.add,
                              r=["r_bc%d" % k, "r_mx%d" % k], w=["r_mout"])
                        kb.dma("sp", o_smm, mout[:], r=["r_mout"], w=())

                def mlstm_tile(i):
                    k = i % 2
                    tg = i // 4
                    hT = hTg[tg % 2]
                    tc0 = (i % 4) * 128
                    lc0 = (i % 4) * 128 if i < 16 else 0
                    cq = colq[k]
                    vx = vext[k]
                    grp = "hTg%d" % (tg % 2)
                    for bi, c0 in enumerate((512, 1024, 1536)):
                        bank = 2 + (bi % 2)
                        for kc in range(8):
                            kb.mm(ps[bank][:, :], lhsT=hT[:, kc, tc0:tc0 + 128], rhs=winA[:, kc, c0:c0 + 512], start=(kc == 0), stop=(kc == 7),
                                  r=["winA", grp], w=[PS[bank]])
                        if bi == 0:
                            kb.act(ktok[:].rearrange("p a b -> p (a b)"), ps[bank][:, :], AF.Identity, r=[PS[bank]], w=["ktok"], scale=DKS)
                            for h in range(4):
                                kb.ts("dve", kw[:, h, :], ktok[:, h, :], cq[:, h:h + 1], None, ALU.mult, None,
                                      r=["ktok", "colq%d" % k], w=["kw"])
                        elif bi == 1:
                            kb.cp("act", vx[:, :, 0:128], ps[bank][:, :].rearrange("p (h d) -> p h d", d=128), r=[PS[bank]], w=["vext%d" % k])
                        else:
                            kb.act(sigo[:], ps[bank][:, :], AF.Sigmoid, r=[PS[bank]], w=["tA0"])
                    if i == 0:
                        stop_at(31)
                    for h in range(4):
                        kb.mm(ps[4][:, h * 128:(h + 1) * 128], lhsT=qkT[:, 4 + h, lc0:lc0 + 128], rhs=qkT[:, h, lc0:lc0 + 128],
                              start=True, stop=True, r=["qkT"], w=[PS[4]], sig=(h == 3))
                    if i == 0:
                        stop_at(32)
                    msk = maskP if i < 16 else maskS
                    for h in range(4):
                        kb.stt(PT[:, h, :], ps[4][:, h * 128:(h + 1) * 128], cq[:, h:h + 1], msk, ALU.mult, ALU.mult,
                               r=[PS[4], "colq%d" % k, "C"], w=["PT"])
                    return cq, vx, lc0

                def numden_finish(i, cq):
                    for half in range(2):
                        bank = ps[5 + half]
                        den = bank[:, 0:260].rearrange("p (h d) -> p h d", d=130)[:, :, 128:129]
                        kb.act(dmax[:, 2 * half:2 * half + 2].unsqueeze(2), den, AF.Abs, r=[PS[5 + half]], w=["dmax"])
                        kb.tt("dve", dmax[:, 2 * half:2 * half + 2], dmax[:, 2 * half:2 * half + 2], cq[:, 4 + 2 * half:6 + 2 * half], ALU.max,
                              r=["dmax", "colq%d" % (i % 2)], w=["dmax"])
                    s.add("dve", lambda g_: g_.reciprocal(out=dmax[:], in_=dmax[:]), r=["dmax"], w=["dmax"], tag="recip")
                    for h in range(4):
                        bank = ps[5 + h // 2]
                        o0 = (h % 2) * 130
                        kb.act(hmf[:, h * 128:(h + 1) * 128], bank[:, o0:o0 + 128], AF.Identity, r=[PS[5 + h // 2], "dmax"], w=["tB0"],
                               scale=dmax[:, h:h + 1])
                    for h in range(4):
                        s.add("dve", lambda g_, h=h: g_.bn_stats(out=hst6[:, h, :], in_=hmf[:, h * 128:(h + 1) * 128]), r=["tB0"], w=["hst6"], tag="bnst")
                    for h in range(4):
                        s.add("dve", lambda g_, h=h: g_.bn_aggr(out=hmv[:, h, :], in_=hst6[:, h, :]), r=["hst6"], w=["hmv"], tag="bnag")
                    kb.act(hrs[:].unsqueeze(2), hmv[:, :, 1:2], AF.Sqrt, r=["hmv"], w=["hrs"], bias=1e-6, scale=1.0)
                    s.add("dve", lambda g_: g_.reciprocal(out=hrs[:], in_=hrs[:]), r=["hrs"], w=["hrs"], tag="recip")
                    kb.tt("dve", hnm[:].unsqueeze(2), hmv[:, :, 0:1], hrs[:].unsqueeze(2), ALU.mult, r=["hmv", "hrs"], w=["hnm"])
                    kb.ts("dve", hnm[:], hnm[:], -1.0, None, ALU.mult, None, r=["hnm"], w=["hnm"])
                    for h in range(4):
                        kb.act(hmf[:, h * 128:(h + 1) * 128], hmf[:, h * 128:(h + 1) * 128], AF.Identity, r=["tB0", "hrs", "hnm"], w=["tB0"],
                               bias=hnm[:, h:h + 1], scale=hrs[:, h:h + 1])
                    kb.tt("pool", hmf[:], hmf[:], mng[:], ALU.mult, r=["tB0", "mng"], w=["tB0"])
                    kb.tt("pool", hmf[:], hmf[:], sigo[:], ALU.mult, r=["tB0", "tA0"], w=["tB0"])
                    for h in range(4):
                        kb.tr(ps[4][:, h * 128:(h + 1) * 128], hmf[:, h * 128:(h + 1) * 128], ident, r=["tB0", "C"], w=[PS[4]], sig=(h == 3))
                    kb.cp("act", hmT[:, :, i * 128:(i + 1) * 128], ps[4][:, :].rearrange("p (h d) -> p h d", d=128), r=[PS[4]], w=["hmT%d" % i])

                for tg, (t0, nt_) in enumerate(TGS):
                    tiles = range(4 * tg, 4 * tg + 4) if tg < 4 else [16]
                    hT = hTg[tg % 2]
                    for i in tiles:
                        make_hT(hT, i, prescale=False, col0=(i % 4) * 128, res="hTg%d" % (tg % 2))
                    for j in range(8):
                        bank = j % 2
                        for kc in range(8):
                            kb.mm(ps[bank][:, 0:nt_], lhsT=winA[:, kc, j * 128:(j + 1) * 128], rhs=hT[:, kc, 0:nt_], start=(kc == 0), stop=(kc == 7),
                                  r=["winA", "hTg%d" % (tg % 2)], w=[PS[bank]])
                        if j < 4:
                            kb.cp("act", qkT[:, j, 0:nt_], ps[bank][:, 0:nt_], r=[PS[bank]], w=["qkT"])
                        else:
                            kb.ts("dve", qkT[:, j, 0:nt_], ps[bank][:, 0:nt_], DKS, None, ALU.mult, None, r=[PS[bank]], w=["qkT"])
                    for i in tiles:
                        if i == 0:
                            stop_at(1)
                        rows(i)
                        if i == 0:
                            stop_at(2)
                        cq, vx, lc0 = mlstm_tile(i)
                        if i == 0:
                            stop_at(3)
                        if i == 16:
                            stop_at(5)
                        if i < 16:
                            for h in range(4):
                                bank = ps[5 + h // 2]
                                o0 = (h % 2) * 130
                                kb.mm(bank[:, o0:o0 + 130], lhsT=PT[:, h, :], rhs=vx[:, h, :], start=True, stop=False, r=["PT", "vext%d" % (i % 2)], w=[PS[5 + h // 2]], sig=False)
                                kb.mm(bank[:, o0:o0 + 130], lhsT=qkT[:, h, lc0:lc0 + 128], rhs=Cb[:, h, :], start=False, stop=True, r=["qkT", "Cb"], w=[PS[5 + h // 2]], sig=True)
                            numden_finish(i, cq)
                            for h in range(4):
                                bank = ps[5 + h // 2]
                                o0 = (h % 2) * 130
                                kb.mm(bank[:, o0:o0 + 130], lhsT=kw[:, h, :], rhs=vx[:, h, :], start=True, stop=True, r=["kw", "vext%d" % (i % 2)], w=[PS[5 + h // 2]])
                            for h in range(4):
                                bank = ps[5 + h // 2]
                                o0 = (h % 2) * 130
                                kb.ts("dve", Cst[:, h, :], Cst[:, h, :], cq[:, 8 + h:9 + h], None, ALU.mult, None, r=["Cst", "colq%d" % (i % 2)], w=["Cst"])
                                kb.stt(Cst[:, h, :], bank[:, o0:o0 + 130], cq[:, 8 + h:9 + h], Cst[:, h, :], ALU.mult, ALU.add,
                                       r=[PS[5 + h // 2], "Cst", "colq%d" % (i % 2)], w=["Cst"])
                            kb.cp("act", Cb[:], Cst[:], r=["Cst"], w=["Cb"])
                            if i == 0:
                                stop_at(4)
                            if i == 15:
                                for h in range(4):
                                    kb.tr(ps[4][:, h * 128:(h + 1) * 128], Cst[:, h, 0:128], ident, r=["Cst", "C"], w=[PS[4]], sig=(h == 3))
                                kb.cp("act", hmf[:], ps[4][:, :], r=[PS[4]], w=["tB0"])
                                kb.dma("sp", o_pmC.rearrange("h v k -> v h k"), hmf[:].rearrange("p (h k) -> p h k", k=128), r=["tB0"], w=())
                                kb.dma("sp", o_pmn.rearrange("h k -> k h"), Cst[:, :, 128], r=["Cst"], w=(), allow_slow_non_contiguous=True)
                        else:
                            with contextlib.ExitStack() as psm:
                                Cin = kb.sb(psm, "Cin", [128, 16, 128], F32)
                                CsT = kb.sb(psm, "CsT", [128, 16, 130], BF16)
                                qTm = kb.sb(psm, "qTm", [128, 16, 128], BF16)
                                VWm = kb.sb(psm, "VWm", [128, 16, 128], BF16)
                                nin = kb.sb(psm, "nin", [16, 4, 128], F32)
                                ninT = kb.sb(psm, "ninT", [128, 4, 16], F32)
                                BMW = kb.sb(psm, "BMW", [128, 16], BF16)
                                decc = kb.sb(psm, "decc", [16, 4], F32)
                                nout = nin
                                kb.memset("pool", qTm[:], 0.0, w=["qTm"])
                                kb.memset("pool", CsT[:], 0.0, w=["CsT"])
                                kb.dma("sp", nin[:], smn, r=(), w=["nin"])
                                kb.tr(ps[7][0:16, 400:404], dd[0:4, :], C[0:4, 0:4], r=["r_dd", "C"], w=[PS[7]])
                                kb.cp("dve", decc[:], ps[7][0:16, 400:404], r=[PS[7]], w=["decc"])
                                for h in range(4):
                                    kb.tr(ps[7][:, 416 + h * 16:432 + h * 16], nin[0:16, h, :], C[0:16, 0:16], r=["nin", "C"], w=[PS[7]], sig=(h == 3))
                                kb.cp("dve", ninT[:].rearrange("p a b -> p (a b)"), ps[7][:, 416:480], r=[PS[7]], w=["ninT"])
                                for h in range(4):
                                    bank = ps[5 + h // 2]
                                    o0 = (h % 2) * 130
                                    kb.dma("sp", Cin[:], smC[:, h].rearrange("q v k -> v q k"), r=(), w=["Cin"])
                                    for q4 in range(4):
                                        pb = q4 % 2
                                        for qq in range(4):
                                            q = q4 * 4 + qq
                                            kb.tr(ps[pb][:, qq * 128:(qq + 1) * 128], Cin[:, q, :], ident, r=["Cin", "C"], w=[PS[pb]], sig=(qq == 3))
                                        kb.cp("act", CsT[:, q4 * 4:q4 * 4 + 4, 0:128], ps[pb][:, :].rearrange("p (a b) -> p a b", b=128), r=[PS[pb]], w=["CsT"])
                                    kb.cp("dve", CsT[:, :, 128:129], ninT[:, h, :].unsqueeze(2), r=["ninT"], w=["CsT"])
                                    kb.cp("pool", bass.AP(qTm, 0, [[2048, 128], [136, 16], [1, 8]]),
                                          qkT[:, h, 0:128].rearrange("p (q t) -> p q t", t=8), r=["qkT"], w=["qTm"])
                                    kb.mm(bank[:, o0:o0 + 130], lhsT=PT[:, h, :], rhs=vx[:, h, :], start=True, stop=False, r=["PT", "vext%d" % (i % 2)], w=[PS[5 + h // 2]], sig=False)
                                    for q in range(16):
                                        kb.mm(bank[:, o0:o0 + 130], lhsT=qTm[:, q, :], rhs=CsT[:, q, :], start=False, stop=(q == 15), r=["qTm", "CsT"], w=[PS[5 + h // 2]], sig=(q == 15))
                                    kb.ts("dve", BMW[:], bms, cq[:, 8 + h:9 + h], None, ALU.mult, None, r=["C", "colq%d" % (i % 2)], w=["BMW"])
                                    kb.tt("dve", VWm[:], vx[:, h, 0:128].unsqueeze(1).to_broadcast([128, 16, 128]), BMW[:].unsqueeze(2).to_broadcast([128, 16, 128]),
                                          ALU.mult, r=["vext%d" % (i % 2), "BMW"], w=["VWm"])
                                    for q4 in range(4):
                                        pb = q4 % 2
                                        for qq in range(4):
                                            q = q4 * 4 + qq
                                            kb.mm(ps[pb][:, qq * 128:(qq + 1) * 128], lhsT=VWm[:, q, :], rhs=ktok[:, h, :], start=True, stop=True,
                                                  r=["VWm", "ktok"], w=[PS[pb]], sig=(qq == 3))
                                        for qq in range(4):
                                            q = q4 * 4 + qq
                                            kb.stt(Cin[:, q, :], Cin[:, q, :], decsb[:, q * 4 + h:q * 4 + h + 1], ps[pb][:, qq * 128:(qq + 1) * 128], ALU.mult, ALU.add,
                                                   r=["Cin", "decsb", PS[pb]], w=["Cin"])
                                    kb.dma("sp", o_smC[:, h].rearrange("q v k -> v q k"), Cin[:], r=["Cin"], w=())
                                    kb.mm(ps[7][0:16, 0:128], lhsT=BMW[:], rhs=ktok[:, h, :], start=True, stop=True, r=["BMW", "ktok"], w=[PS[7]])
                                    kb.stt(nout[:, h, :], nin[:, h, :], decc[:, h:h + 1], ps[7][0:16, 0:128], ALU.mult, ALU.add, r=["nin", "decc", PS[7]], w=["nin"])
                                numden_finish(i, cq)
                                kb.dma("sp", o_smn, nout[:], r=["nin"], w=())
                                s.barrier()
            s.barrier()
            stop_at(6)
            hrT = kb.sb(ph, "hrT", [128, 4, NTOK], BF16)
            with contextlib.ExitStack() as pb_:
                winB = kb.sb(pb_, "winB", [128, 8, 1024], BF16)
                kb.dma("pool", winB[:], win_v[:, :, 2056:3080], r=(), w=["winB"])
                hTgB = [kb.sb(pb_, "hTgB%d" % i, [128, 8, 512], BF16) for i in range(2)]
                WA = kb.sb(pb_, "WA", [128, 4, 128], F32)
                WX = kb.sb(pb_, "WX", [128, 4, 128], F32)
                kb.memset("pool", WA[:], 0.0, w=["WA"])
                kb.memset("pool", WX[:], 0.0, w=["WX"])
                for c in range(4):
                    for hp in range(2):
                        kb.dma("sp", WA[hp * 64:(hp + 1) * 64, c, hp * 64:(hp + 1) * 64], rg_w_a[0, 2 * c + hp], r=(), w=["WA"])
                        kb.dma("sp", WX[hp * 64:(hp + 1) * 64, c, hp * 64:(hp + 1) * 64], rg_w_x[0, 2 * c + hp], r=(), w=["WX"])
                cl = kb.sb(pb_, "cl", [128, 4], F32)
                cl2 = kb.sb(pb_, "cl2", [128, 4], F32)
                kb.act(cl[:], vecA[:, 28:32], AF.Exp, r=["vecA"], w=["cl"], scale=-1.0)
                kb.act(cl[:], cl[:], AF.Ln, r=["cl"], w=["cl"], bias=1.0, scale=1.0)
                kb.ts("dve", cl2[:], cl[:], -16.0, None, ALU.mult, None, r=["cl"], w=["cl2"])
                kb.ts("dve", cl[:], cl[:], -8.0, None, ALU.mult, None, r=["cl", "cl2"], w=["cl"])
                xp = [kb.sb(pb_, "xp%d" % c, [128, 515], F32) for c in range(4)]
                xps = kb.sb(pb_, "xps", [128, 16, 11], F32)
                hst = kb.sb(pb_, "hst", [128, 4], F32)
                h0T = kb.sb(pb_, "h0T", [128, 4, 16], F32)
                cvT = kb.sb(pb_, "cvT", [128, 4, 48], F32)
                hl = kb.sb(pb_, "hl", [128, 4, 16], F32)
                srh_sb = kb.sb(pb_, "srh_sb", [16, 512], F32)
                src_sb = kb.sb(pb_, "src_sb", [48, 512], F32)
                F5 = lambda nm: kb.sb(pb_, nm, [128, 512], F32)
                xc, rr, ii, aa, a2, uu, hh_, t5 = F5("xc"), F5("rr"), F5("ii"), F5("aa"), F5("a2"), F5("uu"), F5("hh"), F5("t5")
                for c in range(4):
                    kb.memset("pool", xp[c][:, 0:3], 0.0, w=["xp%d" % c])
                kb.memset("pool", hst[:], 0.0, w=["hst"])
                kb.dma("sp", srh_sb[:], srh, r=(), w=["srh_sb"])
                kb.dma("sp", src_sb[:], srconv, r=(), w=["src_sb"])
                for c in range(4):
                    kb.tr(ps[6][:, c * 16:(c + 1) * 16], srh_sb[0:16, c * 128:(c + 1) * 128], C[0:16, 0:16], r=["srh_sb", "C"], w=[PS[6]], sig=(c == 3))
                kb.cp("dve", h0T[:].rearrange("p a b -> p (a b)"), ps[6][:, 0:64], r=[PS[6]], w=["h0T"])
                for c in range(4):
                    kb.tr(ps[6][:, 64 + c * 48:64 + (c + 1) * 48], src_sb[0:48, c * 128:(c + 1) * 128], C[0:48, 0:48], r=["src_sb", "C"], w=[PS[6]], sig=(c == 3))
                kb.cp("dve", cvT[:].rearrange("p a b -> p (a b)"), ps[6][:, 64:256], r=[PS[6]], w=["cvT"])
                u_ = 0
                for tg, (t0, n) in enumerate(TGS):
                    sample = tg == 4
                    hT = hTgB[tg % 2]
                    for i in (range(4 * tg, 4 * tg + 4) if tg < 4 else [16]):
                        make_hT(hT, i, prescale=True, col0=(i % 4) * 128, res="hTg%d" % (tg % 2))
                    for c in range(4):
                        px, pgr = ps[(2 * u_) % 4], ps[(2 * u_ + 1) % 4]
                        PX, PGR = PS[(2 * u_) % 4], PS[(2 * u_ + 1) % 4]
                        u_ += 1
                        for kc in range(8):
                            kb.mm(px[:, 0:n], lhsT=winB[:, kc, c * 128:(c + 1) * 128], rhs=hT[:, kc, 0:n], start=(kc == 0), stop=(kc == 7),
                                  r=["winB", "hTg%d" % (tg % 2)], w=[PX])
                        for kc in range(8):
                            kb.mm(pgr[:, 0:n], lhsT=winB[:, kc, 512 + c * 128:512 + (c + 1) * 128], rhs=hT[:, kc, 0:n], start=(kc == 0), stop=(kc == 7),
                                  r=["winB", "hTg%d" % (tg % 2)], w=[PGR])
                        cw = lambda j: vecA[:, c * 4 + j:c * 4 + j + 1]
                        cb = vecA[:, 16 + c:17 + c]
                        if not sample:
                            kb.cp("act", xp[c][:, 3:3 + n], px[:, 0:n], r=[PX], w=["xp%d" % c])
                            kb.ts("dve", xc[:, 0:n], xp[c][:, 0:n], cw(0), cb, ALU.mult, ALU.add, r=["xp%d" % c, "vecA"], w=["xc"])
                            for j in range(1, 4):
                                kb.stt(xc[:, 0:n], xp[c][:, j:j + n], cw(j), xc[:, 0:n], ALU.mult, ALU.add, r=["xp%d" % c, "vecA", "xc"], w=["xc"])
                            if tg == 3:
                                kb.tr(ps[6][0:3, c * 128:(c + 1) * 128], xp[c][:, n:n + 3], ident, r=["xp%d" % c, "C"], w=[PS[6]])
                            else:
                                kb.cp("pool", xp[c][:, 0:3], xp[c][:, n:n + 3], r=["xp%d" % c], w=["xp%d" % c])
                        else:
                            kb.cp("dve", xps[:, :, 0:3], cvT[:, c, :].rearrange("p (q j) -> p q j", j=3), r=["cvT"], w=["xps"])
                            kb.cp("act", xps[:, :, 3:11], px[:, 0:n].rearrange("p (q t) -> p q t", t=8), r=[PX], w=["xps"])
                            xc3 = xc[:, 0:n].rearrange("p (q t) -> p q t", t=8)
                            kb.ts("dve", xc3, xps[:, :, 0:8], cw(0), cb, ALU.mult, ALU.add, r=["xps", "vecA"], w=["xc"])
                            for j in range(1, 4):
                                kb.stt(xc3, xps[:, :, j:j + 8], cw(j), xc3, ALU.mult, ALU.add, r=["xps", "vecA", "xc"], w=["xc"])
                            for j in range(3):
                                kb.tr(ps[5][0:16, j * 128:(j + 1) * 128], xps[:, :, 8 + j], ident, r=["xps", "C"], w=[PS[5]], sig=(j == 2))
                            kb.cp("dve", t5[0:16, 0:384], ps[5][0:16, 0:384], r=[PS[5]], w=["t5"])
                            kb.dma("sp", o_srconv.rearrange("(q j) f -> q j f", j=3)[:, :, c * 128:(c + 1) * 128],
                                   t5[0:16, 0:384].rearrange("q (j f) -> q j f", f=128), r=["t5"], w=())
                        kb.mm(ps[4][:, 0:n], lhsT=WA[:, c, :], rhs=xc[:, 0:n], start=True, stop=True, r=["WA", "xc"], w=[PS[4]])
                        kb.mm(ps[5][:, 0:n], lhsT=WX[:, c, :], rhs=xc[:, 0:n], start=True, stop=True, r=["WX", "xc"], w=[PS[5]])
                        kb.act(rr[:, 0:n], ps[4][:, 0:n], AF.Sigmoid, r=[PS[4], "vecA"], w=["rr"], bias=vecA[:, 20 + c:21 + c], scale=1.0)
                        kb.act(ii[:, 0:n], ps[5][:, 0:n], AF.Sigmoid, r=[PS[5], "vecA"], w=["ii"], bias=vecA[:, 24 + c:25 + c], scale=1.0)
                        kb.act(aa[:, 0:n], rr[:, 0:n], AF.Exp, r=["rr", "cl"], w=["aa"], scale=cl[:, c:c + 1])
                        kb.act(a2[:, 0:n], rr[:, 0:n], AF.Exp, r=["rr", "cl2"], w=["a2"], scale=cl2[:, c:c + 1])
                        kb.act(a2[:, 0:n], a2[:, 0:n], AF.Sqrt, r=["a2"], w=["a2"], bias=1.0, scale=-1.0)
                        kb.tt("dve", uu[:, 0:n], a2[:, 0:n], ii[:, 0:n], ALU.mult, r=["a2", "ii"], w=["uu"])
                        kb.tt("dve", uu[:, 0:n], uu[:, 0:n], xc[:, 0:n], ALU.mult, r=["uu", "xc"], w=["uu"])
                        if not sample:
                            s.add("dve", lambda g_, c=c, n=n: g_.tensor_tensor_scan(out=hh_[:, 0:n], data0=aa[:, 0:n], data1=uu[:, 0:n], initial=hst[:, c:c + 1],
                                                                                     op0=ALU.mult, op1=ALU.add),
                                  r=["aa", "uu", "hst"], w=["hh"], tag="scanH")
                            kb.cp("dve", hst[:, c:c + 1], hh_[:, n - 1:n], r=["hh"], w=["hst"])
                        else:
                            aa3 = aa[:, 0:n].rearrange("p (q t) -> p q t", t=8)
                            uu3 = uu[:, 0:n].rearrange("p (q t) -> p q t", t=8)
                            kb.tt("dve", t5[:, 0:16].unsqueeze(2), aa3[:, :, 0:1], h0T[:, c, :].unsqueeze(2), ALU.mult, r=["aa", "h0T", "t5"], w=["t5"])
                            kb.tt("dve", uu3[:, :, 0:1], uu3[:, :, 0:1], t5[:, 0:16].unsqueeze(2), ALU.add, r=["uu", "t5"], w=["uu"])
                            kb.tt("dve", aa[:, 0:n], aa[:, 0:n], rst, ALU.mult, r=["aa", "C"], w=["aa"])
                            s.add("dve", lambda g_, n=n: g_.tensor_tensor_scan(out=hh_[:, 0:n], data0=aa[:, 0:n], data1=uu[:, 0:n], initial=0.0,
                                                                                op0=ALU.mult, op1=ALU.add),
                                  r=["aa", "uu"], w=["hh"], tag="scanH")
                            kb.cp("dve", hl[:, c, :].unsqueeze(2), hh_[:, 0:n].rearrange("p (q t) -> p q t", t=8)[:, :, 7:8], r=["hh"], w=["hl"])
                        kb.act(t5[:, 0:n], pgr[:, 0:n], AF.Square, r=[PGR, "t5"], w=["t5"])
                        kb.ts("dve", t5[:, 0:n], t5[:, 0:n], 0.044715, 1.0, ALU.mult, ALU.add, r=["t5"], w=["t5"])
                        kb.tt("dve", t5[:, 0:n], t5[:, 0:n], pgr[:, 0:n], ALU.mult, r=["t5", PGR], w=["t5"])
                        kb.act(t5[:, 0:n], t5[:, 0:n], AF.Tanh, r=["t5"], w=["t5"], scale=0.7978845608028654)
                        kb.ts("dve", t5[:, 0:n], t5[:, 0:n], 1.0, 0.5, ALU.add, ALU.mult, r=["t5"], w=["t5"])
                        kb.tt("dve", t5[:, 0:n], t5[:, 0:n], pgr[:, 0:n], ALU.mult, r=["t5", PGR], w=["t5"])
                        kb.tt("dve", hrT[:, c, t0:t0 + n], t5[:, 0:n], hh_[:, 0:n], ALU.mult, r=["t5", "hh"], w=["hrT_g%d" % tg])
                    if tg == 3:
                        kb.cp("dve", t5[0:3, :], ps[6][0:3, 0:512], r=[PS[6]], w=["t5"])
                        kb.dma("sp", o_prconv, t5[0:3, :], r=["t5"], w=())
                kb.tr(ps[7][0:4, 0:128], hst[:, 0:4], ident, r=["hst", "C"], w=[PS[7]])
                kb.cp("dve", rr[0:4, 0:128], ps[7][0:4, 0:128], r=[PS[7]], w=["rr"])
                kb.dma("sp", o_prh, rr[0:4, 0:128], r=["rr"], w=())
                for c in range(4):
                    kb.tr(ps[4][0:16, c * 128:(c + 1) * 128], hl[:, c, :], ident, r=["hl", "C"], w=[PS[4]], sig=(c == 3))
                kb.cp("dve", ii[0:16, :], ps[4][0:16, :], r=[PS[4]], w=["ii"])
                kb.dma("sp", o_srh, ii[0:16, :], r=["ii"], w=())
                s.barrier()
            stop_at(7)
            with contextlib.ExitStack() as pc_:
                alloc_gl(pc_)
                mod_prepare(l, 1, 1.0, blocks=[4, 5])
                Gp, Gs = gl["Gp"], gl["Gs"]
                wout = kb.sb(pc_, "wout", [128, 8, D], BF16)
                kb.dma("pool", wout[:], ab_w_out[0].rearrange("(kc p) n -> p kc n", p=128), r=(), w=["wout"])
                proj_acc(wout, "wout", 8, lambda i, kc: ((hmT[:, kc, i * 128:(i + 1) * 128], "hmT%d" % i) if kc < 4 else
                                                          (hrT[:, kc - 4, i * 128:(i + 1) * 128], "hrT_g%d" % (i // 4))), True, True)
                s.barrier()
        s.barrier()

    CW = -0.6065306597126334
    RT = BF16

    def rwkv_mixer(l):
        rwkv_mixer_(l)
        s.skip = False
        s.barrier()

    def rwkv_mixer_(l):
        rw_mu = kb.din("muT", [128, 48])
        rw_wr = kb.din("rw_wr", [1, D, D])
        rw_wk = kb.din("rw_wk", [1, D, D])
        rw_wv = kb.din("rw_wv", [1, D, D])
        rw_wo = kb.din("rw_wo", [1, D, D])
        rw_w0 = kb.din("rw_w0", [1, D])
        rw_w1 = kb.din("rw_w1", [1, D, 64])
        rw_w2 = kb.din("rw_w2", [1, 64, D])
        rw_a0 = kb.din("rw_a0", [1, D])
        rw_a1 = kb.din("rw_a1", [1, D, 64])
        rw_a2 = kb.din("rw_a2", [1, 64, D])
        rw_g1 = kb.din("rw_g1", [1, D, 128])
        rw_g2 = kb.din("rw_g2", [1, 128, D])
        rw_kk = kb.din("rw_k_k", [1, D])
        rw_ka = kb.din("rw_k_a", [1, D])
        rw_rk = kb.din("rk_flat", [1, D])
        rw_lng = kb.din("rw_lnx_g", [1, D])
        rw_lnb = kb.din("rw_lnx_b", [1, D])
        swkv = kb.din("swkv", [16, 16, 64, 64])
        sshift = kb.din("sshift", [16, D])
        o_pwkv = kb.dout("o_pwkv", [16, 64, 64])
        o_pshift = kb.dout("o_pshift", [1, D])
        o_swkv = kb.dout("o_swkv", [16, 16, 64, 64])
        o_sshift = kb.dout("o_sshift", [16, D])

        cm = lambda nm: C[:, CST_OFF[nm]:CST_OFF[nm] + 128]
        low16, up16, m16, m32, m64 = cm("low16"), cm("up16"), cm("m16"), cm("m32"), cm("m64")
        maskP, maskS, upP, lowP, upS, lowS, blkS, ones, bms = cm("maskP"), cm("maskS"), cm("upP"), cm("lowP"), cm("upS"), cm("lowS"), cm("blkS"), cm("ones"), C[:, CST_OFF["bms"]:CST_OFF["bms"] + 16]

        mod_prepare(l, 1, 1.0, blocks=range(4))
        with contextlib.ExitStack() as ph:
            ygT = kb.sb(ph, "ygT", [128, 8, NTOK], BF16)
            muT = kb.sb(ph, "muT_sb", [128, 48], F32)
            kb.dma("sp", muT[:], rw_mu, r=(), w=["muT"])
            identR = kb.sb(ph, "identR", [128, 128], RT)
            kb.cp("dve", identR[:], ident, r=["C"], w=["identR"])
            w1b = kb.sb(ph, "w1b", [128, 8, 64], BF16)
            a1b = kb.sb(ph, "a1b", [128, 8, 64], BF16)
            g1b = kb.sb(ph, "g1b", [128, 8, 128], BF16)
            kb.dma("pool", w1b[:], rw_w1[0].rearrange("(kc p) n -> p kc n", p=128), r=(), w=["w1b"])
            kb.dma("pool", a1b[:], rw_a1[0].rearrange("(kc p) n -> p kc n", p=128), r=(), w=["a1b"])
            kb.dma("pool", g1b[:], rw_g1[0].rearrange("(kc p) n -> p kc n", p=128), r=(), w=["g1b"])
            w1m = kb.sb(ph, "w1m", [128, 8, 64], BF16)
            a1m = kb.sb(ph, "a1m", [128, 8, 64], BF16)
            g1m = kb.sb(ph, "g1m", [128, 8, 128], BF16)
            for wm_, wb_, rs_, j_, n_ in ((w1m, w1b, "w1b", 1, 64), (a1m, a1b, "a1b", 4, 64), (g1m, g1b, "g1b", 5, 128)):
                kb.tt("dve", wm_[:], wb_[:], muT[:, j_ * 8:(j_ + 1) * 8].unsqueeze(2).to_broadcast([128, 8, n_]), ALU.mult, r=[rs_, "muT"], w=[rs_ + "m"])
            hlast = kb.sb(ph, "hlast", [128, 8, 17], F32)
            sh0T = kb.sb(ph, "sh0T", [128, 8, 16], BF16)
            with contextlib.ExitStack() as p0:
                shs = kb.sb(p0, "shs", [16, D], F32)
                kb.dma("sp", shs[:], sshift, r=(), w=["shs"])
                for c in range(8):
                    kb.tr(ps[0][:, c * 16:(c + 1) * 16], shs[0:16, c * 128:(c + 1) * 128], C[0:16, 0:16], r=["shs", "C"], w=[PS[0]], sig=(c == 7))
                kb.cp("dve", sh0T[:].rearrange("p a b -> p (a b)"), ps[0][:, 0:128], r=[PS[0]], w=["sh0T"])
                s.barrier()

            def hook_last(i, c, src, psres):
                if i == 15:
                    kb.ts("dve", hlast[:, c, 0:1], src[:, 127:128], modT[:, 8 + c, 0:1], modT[:, c, 0:1], ALU.mult, ALU.add, r=["modT", psres], w=["hlast"])
                elif i == 16:
                    v3 = src.rearrange("p (q t) -> p q t", t=8)[:, :, 7:8]
                    kb.tt("dve", hlast[:, c, 1:17].unsqueeze(2), v3, modT[:, 8 + c, 1:NT].unsqueeze(2), ALU.mult, r=["modT", psres], w=["hlast"])
                    kb.tt("dve", hlast[:, c, 1:17], hlast[:, c, 1:17], modT[:, c, 1:NT], ALU.add, r=["hlast", "modT"], w=["hlast"])

            stop_at(51)
            for hg in range(4):
                c0 = hg * 256
                with contextlib.ExitStack() as pp:
                    wrs = kb.sb(pp, "wrs", [128, 8, 256], BF16)
                    wks = kb.sb(pp, "wks", [128, 8, 256], BF16)
                    wvs = kb.sb(pp, "wvs", [128, 8, 256], BF16)
                    for wt, src in ((wrs, rw_wr), (wks, rw_wk), (wvs, rw_wv)):
                        kb.dma("pool", wt[:], src[0].rearrange("(kc p) n -> p kc n", p=128)[:, :, c0:c0 + 256], r=(), w=["wqkv"])
                    wrm = kb.sb(pp, "wrm", [128, 8, 256], BF16)
                    wkm = kb.sb(pp, "wkm", [128, 8, 256], BF16)
                    wvm = kb.sb(pp, "wvm", [128, 8, 256], BF16)
                    for wm_, wt_, j_ in ((wrm, wrs, 0), (wkm, wks, 2), (wvm, wvs, 3)):
                        kb.tt("dve", wm_[:], wt_[:], muT[:, j_ * 8:(j_ + 1) * 8].unsqueeze(2).to_broadcast([128, 8, 256]), ALU.mult, r=["wqkv", "muT"], w=["wqkvm"])
                    w2s = kb.sb(pp, "w2s", [64, 256], BF16)
                    a2s = kb.sb(pp, "a2s", [64, 256], BF16)
                    g2s = kb.sb(pp, "g2s", [128, 256], BF16)
                    kb.dma("pool", w2s[:], rw_w2[0, :, c0:c0 + 256], r=(), w=["w2s"])
                    kb.dma("pool", a2s[:], rw_a2[0, :, c0:c0 + 256], r=(), w=["a2s"])
                    kb.dma("pool", g2s[:], rw_g2[0, :, c0:c0 + 256], r=(), w=["g2s"])
                    w0r = kb.sb(pp, "w0r", [1, 256], F32)
                    a0r = kb.sb(pp, "a0r", [1, 256], F32)
                    kb.dma("sp", w0r[:], rw_w0[0:1, c0:c0 + 256], r=(), w=["w0r"])
                    kb.dma("sp", a0r[:], rw_a0[0:1, c0:c0 + 256], r=(), w=["a0r"])
                    bcs = {}
                    alias = {"kkb": tA[2][:, 0:256], "kab": tA[2][:, 256:512], "rkb": tA[3][:, 0:256], "lngb": tA[3][:, 256:512], "lnbb": tA[0][:, 256:512]}
                    for nm, src in (("kkb", rw_kk), ("kab", rw_ka), ("rkb", rw_rk), ("lngb", rw_lng), ("lnbb", rw_lnb)):
                        bcs[nm] = alias[nm] if nm in alias else kb.sb(pp, nm, [128, 256], F32)
                        kb.dma("sp", bcs[nm][:], src[0:1, c0:c0 + 256].to_broadcast([128, 256]), r=(), w=[nm])
                    hTg = [kb.sb(pp, "hTr%d" % i, [128, 8, 130], BF16) for i in range(2)]
                    dx = kb.sb(pp, "dx", [128, 8, 128], BF16)
                    loT = kb.sb(pp, "loT", [128, 3, 128], BF16)
                    TB = lambda j: tB[j // 4][:, (j % 4) * 256:(j % 4) * 256 + 256]
                    Rr, Kk, KKn, Aa, SG, CSs, Ee, Tt = [TB(j) for j in range(8)]
                    tbres = lambda j: "tB%d" % (j // 4)
                    small = kb.sb(pp, "rwsmall", [128, 64], F32)
                    F3 = lambda nm: kb.sb(pp, nm, [128, 256], RT)
                    Vvs, ALs, RBs, KHs, BHs = [[F3("%s%d" % (nm, k)) for k in range(2)] for nm in ("Vv", "AL", "RB", "KH", "BH")]
                    BT, KT = F3("BTt"), F3("KTt")
                    GVs = [tA[0], kb.sb(pp, "GV1", [128, 256], F32)]
                    PLcs = [kb.sb(pp, "PLc%d" % k, [64, 64], F32) for k in range(2)]
                    smallfs = [kb.sb(pp, "smallf%d" % k, [128, 8], F32) for k in range(2)]
                    Tbk = tA[1][:, 256:512]
                    fmalls = [kb.sb(pp, "fmall%d" % k, [64, 4, 4, 128], RT) for k in range(2)]
                    fmTs = [{nm: fm_[:, j] for j, nm in enumerate(("alT", "btT", "ktT", "rbT"))} for fm_ in fmalls]
                    fmall = fmalls[0]
                    chainall = kb.sb(pp, "chainall", [128, 7, 4, 128], RT)
                    chn = {nm: chainall[:, j] for j, nm in enumerate(("ApA", "ApB", "BpA", "BpB", "TTa", "TTb", "Am"))}
                    S0nat = kb.sb(pp, "S0nat", [64, 16, 64], F32)
                    S0Tq = fmall[:, 0:2].rearrange("p a h t -> p (a h t)").rearrange("p (q k) -> p q k", k=64)
                    SLo = S0nat
                    AakT = kb.sb(pp, "AakT", [128, 4, 128], RT)
                    YTs = AakT[0:64, :, :]
                    ArbT = kb.sb(pp, "ArbT", [128, 4, 128], RT)
                    ArkT = kb.sb(pp, "ArkT", [128, 4, 128], RT)
                    Ahat = kb.sb(pp, "Ahat", [128, 4, 64], RT)
                    X1 = kb.sb(pp, "X1", [128, 4, 64], RT)
                    U0 = kb.sb(pp, "U0", [128, 4, 64], RT)
                    Gm = kb.sb(pp, "Gm", [64, 4, 64], RT)
                    RhT = kb.sb(pp, "RhT", [64, 4, 128], RT)
                    S0T = kb.sb(pp, "S0T", [64, 4, 64], RT)
                    yb = tA[1][:, 0:256]
                    kb.ts("dve", S0T[:].rearrange("p a b -> p (a b)"), C[0:64, 0:256], 0.0, None, ALU.mult, None, r=["C"], w=["S0T0", "S0T1", "S0T2", "S0T3"])
                    kb.memset("pool", hTg[1][:, :, 128:129], 0.0, w=["hTr1"])

                    algb = [4]

                    def nb():
                        b_ = algb[0]
                        algb[0] = 4 + (algb[0] - 3) % 4
                        return b_

                    def grp4(mmf, n_cols, m_rows=128):
                        b_ = nb()
                        for hl in range(4):
                            items = mmf(hl)
                            for j, (lt, rh, rd) in enumerate(items):
                                kb.mm(ps[b_][0:m_rows, hl * n_cols:(hl + 1) * n_cols], lhsT=lt, rhs=rh, start=(j == 0), stop=(j == len(items) - 1),
                                      r=rd, w=[PS[b_]], sig=(hl == 3 and j == len(items) - 1))
                        return b_

                    fsl = lambda t_, hl: t_[:, hl, :]
                    tsl = lambda t_, hl: t_[:, hl * 64:(hl + 1) * 64]

                    def sample_states():
                        BH, KH, Vv, PLc = BHs[0], KHs[0], Vvs[0], PLcs[0]
                        Bhm = chainall[:, 0:2].rearrange("p a h t -> p (a h t)").rearrange("p (q k) -> p q k", k=64)
                        Khm = chainall[:, 2:4].rearrange("p a h t -> p (a h t)").rearrange("p (q k) -> p q k", k=64)
                        Gq = chainall[0:64, 4:6].rearrange("p a h t -> p (a h t)").rearrange("p (q k) -> p q k", k=64)
                        bmq = bms.unsqueeze(2).to_broadcast([128, 16, 64])
                        for hl in range(4):
                            h = hg * 4 + hl
                            kb.tt("dve", Bhm, tsl(BH, hl).unsqueeze(1).to_broadcast([128, 16, 64]), bmq, ALU.mult, r=["BH0", "C"], w=["ApA0", "ApA1", "ApA2", "ApA3"] + ["ApB0", "ApB1", "ApB2", "ApB3"])
                            kb.tt("pool", Khm, tsl(KH, hl).unsqueeze(1).to_broadcast([128, 16, 64]), bmq, ALU.mult, r=["KH0", "C"], w=["BpA0", "BpA1", "BpA2", "BpA3"] + ["BpB0", "BpB1", "BpB2", "BpB3"])
                            kb.dma("sp", S0nat[:], swkv[:, h].rearrange("q v k -> v q k"), r=(), w=["S0nat"])
                            b0, b1 = nb(), nb()
                            for q in range(16):
                                bb = b0 if q < 8 else b1
                                kb.tr(ps[bb][0:64, (q % 8) * 64:(q % 8 + 1) * 64], S0nat[:, q, :], ident[0:64, 0:64], r=["S0nat", "C"], w=[PS[bb]], sig=(q % 8 == 7))
                            kb.cp("act", S0Tq[:, 0:8, :], ps[b0][0:64, :].rearrange("p (q k) -> p q k", k=64), r=[PS[b0]], w=["S0Tq", "alT0", "btT0"])
                            kb.cp("dve", S0Tq[:, 8:16, :], ps[b1][0:64, :].rearrange("p (q k) -> p q k", k=64), r=[PS[b1]], w=["S0Tq", "alT0", "btT0"])
                            b_ = nb()
                            for q in range(16):
                                kb.mm(ps[b_][0:64, q * 8:(q + 1) * 8], lhsT=S0Tq[:, q, :], rhs=RhT[:, hl, q * 8:(q + 1) * 8], start=True, stop=True,
                                      r=["S0Tq", "RhT%d" % hl], w=[PS[b_]], sig=(q == 15))
                            kb.cp("act", YTs[:, hl, :], ps[b_][0:64, 0:128], r=[PS[b_]], w=["AakT%d" % hl])
                            g0, g1 = nb(), nb()
                            for half, bb in ((0, g0), (1, g1)):
                                kb.mm(ps[bb][0:64, :], lhsT=Ahat[:, hl, :], rhs=Bhm[:, half * 8:(half + 1) * 8, :], start=True, stop=True,
                                      r=["Ahat%d" % hl] + ["ApA0", "ApA1", "ApA2", "ApA3"] + ["ApB0", "ApB1", "ApB2", "ApB3"], w=[PS[bb]])
                            for q in range(16):
                                bb = g0 if q < 8 else g1
                                kb.stt(Gq[:, q, :], ident[0:64, 0:64], PLc[:, hl * 16 + q:hl * 16 + q + 1], ps[bb][0:64, (q % 8) * 64:(q % 8 + 1) * 64], ALU.mult, ALU.add,
                                       r=["C", "PLc0", PS[bb]], w=["TTa0", "TTa1", "TTa2", "TTa3"] + ["TTb0", "TTb1", "TTb2", "TTb3"])
                            for half in range(2):
                                bb = nb()
                                kb.mm(ps[bb][0:64, :], lhsT=tsl(Vv, hl), rhs=Khm[:, half * 8:(half + 1) * 8, :], start=True, stop=False, r=["Vv0"] + ["BpA0", "BpA1", "BpA2", "BpA3"] + ["BpB0", "BpB1", "BpB2", "BpB3"], w=[PS[bb]], sig=False)
                                kb.mm(ps[bb][0:64, :], lhsT=U0[:, hl, :], rhs=Bhm[:, half * 8:(half + 1) * 8, :], start=False, stop=False, r=["U0%d" % hl] + ["ApA0", "ApA1", "ApA2", "ApA3"] + ["ApB0", "ApB1", "ApB2", "ApB3"], w=[PS[bb]], sig=False)
                                for qq in range(8):
                                    q = half * 8 + qq
                                    kb.mm(ps[bb][0:64, qq * 64:(qq + 1) * 64], lhsT=S0Tq[:, q, :], rhs=Gq[:, q, :], start=False, stop=(qq == 7),
                                          r=["S0Tq"] + ["TTa0", "TTa1", "TTa2", "TTa3"] + ["TTb0", "TTb1", "TTb2", "TTb3"], w=[PS[bb]], sig=(qq == 7))
                                kb.cp("act" if half else "dve", SLo[:, half * 8:(half + 1) * 8, :], ps[bb][0:64, :].rearrange("p (q k) -> p q k", k=64), r=[PS[bb]], w=["S0nat"])
                            kb.dma("sp", o_swkv[:, h].rearrange("q v k -> v q k"), SLo[:], r=["S0nat"], w=())
                        by = grp4(lambda hl: [(ArkT[:, hl, :], tsl(Vv, hl), ["ArkT%d" % hl, "Vv0"]), (ArbT[:, hl, :], U0[:, hl, :], ["ArbT%d" % hl, "U0%d" % hl]),
                                               (YTs[:, hl, :], identR[0:64, 0:64], ["AakT%d" % hl, "identR"])], 64)
                        kb.cp("act", yb[:], ps[by][:, 0:256], r=[PS[by]], w=["tA1"])

                    def front(i):
                        sample = i == 16
                        p_ = i % 2
                        AL, Vv, RB, KH, BH = ALs[p_], Vvs[p_], RBs[p_], KHs[p_], BHs[p_]
                        fmT, PLc, smallf = fmTs[p_], PLcs[p_], smallfs[p_]
                        Gt = GVs[p_][:, 0:256]
                        rAL, rVv, rRB, rKH, rBH = "AL%d" % p_, "Vv%d" % p_, "RB%d" % p_, "KH%d" % p_, "BH%d" % p_
                        ralT, rbtT, rktT, rrbT = "alT%d" % p_, "btT%d" % p_, "ktT%d" % p_, "rbT%d" % p_
                        rPL, rsf, rGV = "PLc%d" % p_, "smallf%d" % p_, "GV%d" % p_
                        sample = i == 16
                        k_ = i % 2
                        hT = hTg[k_]
                        hres = "hTr%d" % k_
                        last_pass = hg == 3
                        if hg == 0 and i == 1:
                            stop_at(58)
                        if hg == 0 and i == 16:
                            stop_at(59)
                        make_hT(hT, i, prescale=last_pass, col0=1, res=hres, hook=(hook_last if hg == 0 else None))
                        yield
                        if i == 0:
                            kb.memset("pool", hT[:, :, 0:1], 0.0, w=[hres])
                        elif not sample:
                            kb.cp("pool", hT[:, :, 0:1], hTg[1 - k_][:, :, 128:129], r=["hTr%d" % (1 - k_)], w=[hres])
                        cur = hT[:, :, 1:129]
                        if not sample:
                            kb.tt("dve", dx[:], hT[:, :, 0:128], cur, ALU.subtract, r=[hres], w=["dx"])
                        else:
                            kb.cp("pool", dx[:], hT[:, :, 0:128], r=[hres], w=["dx"])
                            kb.cp("pool", dx[:].rearrange("p c (q t) -> p c q t", t=8)[:, :, :, 0], sh0T[:], r=["sh0T"], w=["dx"])
                            kb.tt("pool", dx[:], dx[:], cur, ALU.subtract, r=["dx", hres], w=["dx"])
                        yield
                        for wt, wm_, bank, off in ((wrs, wrm, 0, 0), (wks, wkm, 0, 256), (wvs, wvm, 1, 0)):
                            for kc in range(8):
                                kb.mm(ps[bank][:, off:off + 256], lhsT=hT[:, kc, 1:129], rhs=wt[:, kc, :], start=(kc == 0), stop=False, r=[hres, "wqkv"], w=[PS[bank]], sig=False)
                            for kc in range(8):
                                kb.mm(ps[bank][:, off:off + 256], lhsT=dx[:, kc, :], rhs=wm_[:, kc, :], start=False, stop=(kc == 7), r=["dx", "wqkvm"], w=[PS[bank]])
                        for wt, wm_, wr_, m_, off3 in ((w1b, w1m, "w1b", 64, 0), (a1b, a1m, "a1b", 64, 128), (g1b, g1m, "g1b", 128, 256)):
                            for kc in range(8):
                                kb.mm(ps[3][0:m_, off3:off3 + 128], lhsT=wt[:, kc, :], rhs=hT[:, kc, 1:129], start=(kc == 0), stop=False, r=[hres, wr_], w=[PS[3]], sig=False)
                            for kc in range(8):
                                kb.mm(ps[3][0:m_, off3:off3 + 128], lhsT=wm_[:, kc, :], rhs=dx[:, kc, :], start=False, stop=(kc == 7), r=["dx", wr_ + "m"], w=[PS[3]])
                        kb.act(loT[0:64, 0, :], ps[3][0:64, 0:128], AF.Tanh, r=[PS[3]], w=["loT"])
                        yield
                        kb.cp("act", loT[0:64, 1, :], ps[3][0:64, 128:256], r=[PS[3]], w=["loT"])
                        kb.act(loT[:, 2, :], ps[3][:, 256:384], AF.Sigmoid, r=[PS[3]], w=["loT"])
                        kb.mm(ps[1][:, 256:512], lhsT=loT[0:64, 0, :], rhs=w2s[:], start=True, stop=False, r=["loT", "w2s"], w=[PS[1]], sig=False)
                        yield
                        kb.mm(ps[1][:, 256:512], lhsT=ones[0:1, :], rhs=w0r[:], start=False, stop=True, r=["C", "w0r"], w=[PS[1]])
                        kb.mm(ps[2][:, 0:256], lhsT=loT[0:64, 1, :], rhs=a2s[:], start=True, stop=False, r=["loT", "a2s"], w=[PS[2]], sig=False)
                        kb.mm(ps[2][:, 0:256], lhsT=ones[0:1, :], rhs=a0r[:], start=False, stop=True, r=["C", "a0r"], w=[PS[2]])
                        yield
                        kb.mm(ps[2][:, 256:512], lhsT=loT[:, 2, :], rhs=g2s[:], start=True, stop=True, r=["loT", "g2s"], w=[PS[2]])
                        if hg == 0 and i == 0:
                            stop_at(52)
                        kb.cp("act", Rr, ps[0][:, 0:256], r=[PS[0]], w=[tbres(0)])
                        yield
                        kb.cp("act", Kk, ps[0][:, 256:512], r=[PS[0]], w=[tbres(1)])
                        kb.cp("act", Vv[:], ps[1][:, 0:256], r=[PS[1]], w=[rVv])
                        yield
                        kb.act(SG, ps[1][:, 256:512], AF.Sigmoid, r=[PS[1]], w=[tbres(4)])
                        kb.act(Aa, ps[2][:, 0:256], AF.Sigmoid, r=[PS[2]], w=[tbres(3)])
                        kb.cp("act", Gt, ps[2][:, 256:512], r=[PS[2]], w=[rGV])
                        yield
                        kb.tt("dve", KKn, Kk, bcs["kkb"][:], ALU.mult, r=[tbres(1), "kkb"], w=[tbres(2)])
                        kb.tt("dve", Tt, KKn, KKn, ALU.mult, r=[tbres(2)], w=[tbres(7)])
                        s.add("dve", lambda g_: g_.tensor_reduce(out=smallf[:, 0:4], in_=Tt.rearrange("p (h k) -> p h k", k=64), axis=AX.X, op=ALU.add),
                              r=[tbres(7)], w=[rsf], tag="red")
                        yield
                        kb.act(smallf[:, 0:4], smallf[:, 0:4], AF.Sqrt, r=[rsf], w=[rsf])
                        kb.ts("dve", smallf[:, 0:4], smallf[:, 0:4], 1e-12, None, ALU.max, None, r=[rsf], w=[rsf])
                        s.add("dve", lambda g_: g_.reciprocal(out=smallf[:, 0:4], in_=smallf[:, 0:4]), r=[rsf], w=[rsf], tag="recip")
                        yield
                        kb.tt("dve", KKn.rearrange("p (h k) -> p h k", k=64), KKn.rearrange("p (h k) -> p h k", k=64),
                              smallf[:, 0:4].unsqueeze(2).to_broadcast([128, 4, 64]), ALU.mult, r=[tbres(2), rsf], w=[tbres(2)])
                        kb.stt(Tt, Aa, -1.0, bcs["kab"][:], ALU.add, ALU.mult, r=[tbres(3), "kab"], w=[tbres(7)])
                        kb.tt("dve", Tt, Tt, Kk, ALU.mult, r=[tbres(7), tbres(1)], w=[tbres(7)])
                        yield
                        kb.tt("dve", Kk, Kk, Tt, ALU.add, r=[tbres(1), tbres(7)], w=[tbres(1)])
                        kb.tt("dve", Tt, Rr, Kk, ALU.mult, r=[tbres(0), tbres(1)], w=[tbres(7)])
                        kb.tt("dve", Tt, Tt, bcs["rkb"][:], ALU.mult, r=[tbres(7), "rkb"], w=[tbres(7)])
                        yield
                        s.add("dve", lambda g_: g_.tensor_reduce(out=smallf[:, 4:8], in_=Tt.rearrange("p (h k) -> p h k", k=64), axis=AX.X, op=ALU.add),
                              r=[tbres(7)], w=[rsf], tag="red")
                        kb.tt("dve", Aa, KKn, Aa, ALU.mult, r=[tbres(2), tbres(3)], w=[tbres(3)])
                        if hg == 0 and i == 0:
                            stop_at(53)
                        yield
                        Um, Jm = (maskP, ones) if not sample else (maskS, blkS)
                        kb.mm(ps[0][:, 0:256], lhsT=Um, rhs=SG, start=True, stop=True, r=["C", tbres(4)], w=[PS[0]])
                        kb.mm(ps[0][:, 256:512], lhsT=Jm, rhs=SG, start=True, stop=True, r=["C", tbres(4)], w=[PS[0]])
                        yield
                        nq = 1 if not sample else 16
                        for hl in range(4):
                            kb.mm(ps[3][0:64, 384 + hl * nq:384 + (hl + 1) * nq], lhsT=SG[:, hl * 64:(hl + 1) * 64], rhs=(ones[:, 0:1] if not sample else bms),
                                  start=True, stop=True, r=[tbres(4), "C"], w=[PS[3]], sig=(hl == 3))
                        kb.act(PLc[:, 0:4 * nq], ps[3][0:64, 384:384 + 4 * nq], AF.Exp, r=[PS[3]], w=[rPL], scale=CW)
                        yield
                        kb.cp("act", CSs, ps[0][:, 0:256], r=[PS[0]], w=[tbres(5)])
                        kb.tt("dve", Tt, CSs, SG, ALU.subtract, r=[tbres(5), tbres(4)], w=[tbres(7)])
                        kb.act(Ee, Tt, AF.Exp, r=[tbres(7)], w=[tbres(6)], scale=CW)
                        yield
                        kb.stt(AL[:], KKn, -1.0, Ee, ALU.mult, ALU.mult, r=[tbres(2), tbres(6)], w=[rAL])
                        kb.act(Ee, CSs, AF.Exp, r=[tbres(5), rAL], w=[tbres(6)], scale=-CW)
                        kb.tt("dve", BT[:], Aa, Ee, ALU.mult, r=[tbres(3), tbres(6)], w=["BTt"])
                        yield
                        kb.tt("dve", KT[:], Kk, Ee, ALU.mult, r=[tbres(1), tbres(6)], w=["X1kt"])
                        kb.act(Tt, CSs, AF.Exp, r=[tbres(5)], w=[tbres(7)], scale=CW)
                        kb.tt("dve", RB[:], Rr, Tt, ALU.mult, r=[tbres(0), tbres(7)], w=[rRB])
                        yield
                        kb.tt("dve", Ee, ps[0][:, 256:512], CSs, ALU.subtract, r=[PS[0], tbres(5), rGV, "X1kt"], w=[tbres(6)])
                        kb.act(Ee, Ee, AF.Exp, r=[tbres(6)], w=[tbres(6)], scale=CW)
                        kb.tt("dve", KH[:], Kk, Ee, ALU.mult, r=[tbres(1), tbres(6)], w=[rKH])
                        yield
                        kb.tt("dve", BH[:], Aa, Ee, ALU.mult, r=[tbres(3), tbres(6)], w=[rBH])
                        for qi, (nm, src, sr) in enumerate((("alT", AL, rAL), ("btT", BT, "BTt"), ("ktT", KT, "X1kt"), ("rbT", RB, rRB))):
                            b_ = 1 + qi % 2
                            for hl in range(4):
                                kb.tr(psb[b_][0:64, hl * 128:(hl + 1) * 128], src[:, hl * 64:(hl + 1) * 64], identR[:],
                                      r=[sr, "identR"], w=[PS[b_]], sig=(hl == 3))
                            kb.cp("act" if qi % 2 else "dve", fmT[nm][:].rearrange("p a b -> p (a b)"), psb[b_][0:64, 0:512], r=[PS[b_]], w=["%s%d" % (nm, p_)])
                        if hg == 0 and i == 0:
                            stop_at(54)

                    def back(i, fg):
                        sample = i == 16
                        p_ = i % 2
                        AL, Vv, RB, KH, BH = ALs[p_], Vvs[p_], RBs[p_], KHs[p_], BHs[p_]
                        fmT, PLc, smallf = fmTs[p_], PLcs[p_], smallfs[p_]
                        Gt = GVs[p_][:, 0:256]
                        rAL, rVv, rRB, rKH, rBH = "AL%d" % p_, "Vv%d" % p_, "RB%d" % p_, "KH%d" % p_, "BH%d" % p_
                        ralT, rbtT, rktT, rrbT = "alT%d" % p_, "btT%d" % p_, "ktT%d" % p_, "rbT%d" % p_
                        rPL, rsf, rGV = "PLc%d" % p_, "smallf%d" % p_, "GV%d" % p_
                        alT, btT, ktT, rbT = fmT["alT"], fmT["btT"], fmT["ktT"], fmT["rbT"]
                        mlow, mup, minc = (lowP, upP, maskP) if not sample else (lowS, upS, maskS)
                        n_it = 3 if not sample else 2

                        def head_alg(hl):
                            B_ = 4 + hl
                            P_ = PS[B_]
                            rn = lambda nm: "%s%d" % (nm, hl)
                            pw = ps[B_][:, 0:128]

                            def mm1(lt, rh, rd, cols=128, rows=128, first=True, last=True):
                                kb.mm(ps[B_][0:rows, 0:cols], lhsT=lt, rhs=rh, start=first, stop=last, r=rd, w=[P_], sig=last)

                            mm1(fsl(alT, hl), fsl(btT, hl), [ralT, rbtT])
                            if not sample:
                                kb.tt("dve", chn["Am"][:, hl, :], pw, lowP, ALU.mult, r=[P_, "C"], w=[rn("Am")])
                                kb.tt("dve", chn["ApA"][:, hl, :], pw, low16, ALU.mult, r=[P_, "C"], w=[rn("ApA")])
                            else:
                                kb.tt("dve", chn["ApA"][:, hl, :], pw, lowS, ALU.mult, r=[P_, "C"], w=[rn("ApA")])
                            yield
                            mm1(fsl(btT, hl), fsl(alT, hl), [ralT, rbtT])
                            kb.tt("dve", chn["BpA"][:, hl, :], pw, (up16 if not sample else upS), ALU.mult, r=[P_, "C"], w=[rn("BpA")])
                            yield
                            mm1(fsl(ktT, hl), fsl(alT, hl), [ralT, rktT])
                            kb.tt("dve", AakT[:, hl, :], pw, mup, ALU.mult, r=[P_, "C"], w=[rn("AakT")])
                            yield
                            mm1(fsl(btT, hl), fsl(rbT, hl), [rrbT, rbtT])
                            kb.tt("dve", ArbT[:, hl, :], pw, minc, ALU.mult, r=[P_, "C"], w=[rn("ArbT")])
                            yield
                            mm1(fsl(ktT, hl), fsl(rbT, hl), [rrbT, rktT])
                            kb.tt("dve", ArkT[:, hl, :], pw, minc, ALU.mult, r=[P_, "C"], w=[rn("ArkT")])
                            yield
                            kb.tt("dve", chn["TTa"][:, hl, :], chn["BpA"][:, hl, :], ident, ALU.add, r=[rn("BpA"), "C"], w=[rn("TTa")])
                            Ap, Bp, TT = "ApA", "BpA", "TTa"
                            for it in range(n_it):
                                Ap2 = "ApB" if Ap == "ApA" else "ApA"
                                Bp2 = "BpB" if Bp == "BpA" else "BpA"
                                TT2 = "TTb" if TT == "TTa" else "TTa"
                                mm1(chn[Bp][:, hl, :], chn[Ap][:, hl, :], [rn(Ap), rn(Bp)])
                                kb.cp("act", chn[Ap2][:, hl, :], pw, r=[P_], w=[rn(Ap2)])
                                yield
                                if it < n_it - 1:
                                    mm1(chn[Ap][:, hl, :], chn[Bp][:, hl, :], [rn(Ap), rn(Bp)])
                                    kb.cp("act", chn[Bp2][:, hl, :], pw, r=[P_], w=[rn(Bp2)])
                                    yield
                                mm1(chn[Ap2][:, hl, :], chn[TT][:, hl, :], [rn(Ap2), rn(TT)])
                                kb.tt("dve", chn[TT2][:, hl, :], pw, chn[TT][:, hl, :], ALU.add, r=[P_, rn(TT)], w=[rn(TT2)])
                                yield
                                Ap, Bp, TT = Ap2, Bp2, TT2
                            if not sample:
                                for mk in (m16, m32, m64):
                                    TT2 = "TTb" if TT == "TTa" else "TTa"
                                    kb.tr(psb[B_][:, 0:128], chn[TT][:, hl, :], identR[:], r=[rn(TT), "identR"], w=[P_])
                                    kb.cp("act", chn["ApB"][:, hl, :], psb[B_][:, 0:128], r=[P_], w=[rn("ApB")])
                                    kb.tt("dve", chn["ApA"][:, hl, :], chn["Am"][:, hl, :], mk, ALU.mult, r=[rn("Am"), "C"], w=[rn("ApA")])
                                    yield
                                    mm1(chn["ApA"][:, hl, :], chn[TT][:, hl, :], [rn("ApA"), rn(TT)])
                                    kb.cp("act", chn["BpA"][:, hl, :], pw, r=[P_], w=[rn("BpA")])
                                    yield
                                    mm1(chn["ApB"][:, hl, :], chn["BpA"][:, hl, :], [rn("ApB"), rn("BpA")])
                                    kb.tt("dve", chn[TT2][:, hl, :], pw, chn[TT][:, hl, :], ALU.add, r=[P_, rn(TT)], w=[rn(TT2)])
                                    yield
                                    TT = TT2
                            TTh = chn[TT][:, hl, :]
                            mm1(TTh, tsl(AL, hl), [rn(TT), rAL], cols=64)
                            kb.cp("act", Ahat[:, hl, :], ps[B_][:, 0:64], r=[P_], w=[rn("Ahat")])
                            yield
                            mm1(AakT[:, hl, :], tsl(Vv, hl), [rn("AakT"), rVv], cols=64)
                            kb.cp("dve", X1[:, hl, :], ps[B_][:, 0:64], r=[P_], w=[rn("X1")])
                            yield
                            mm1(TTh, X1[:, hl, :], [rn(TT), rn("X1")], cols=64)
                            kb.cp("act", U0[:, hl, :], ps[B_][:, 0:64], r=[P_], w=[rn("U0")])
                            yield
                            mm1(tsl(RB, hl), identR[:], [rRB, "identR"], rows=64, last=False)
                            mm1(Ahat[:, hl, :], ArbT[:, hl, :], [rn("Ahat"), rn("ArbT")], rows=64, first=False)
                            kb.cp("dve", RhT[:, hl, :], ps[B_][0:64, 0:128], r=[P_], w=[rn("RhT")])
                            yield
                            if sample:
                                return
                            mm1(Ahat[:, hl, :], tsl(BH, hl), [rn("Ahat"), rBH], cols=64, rows=64)
                            kb.stt(Gm[:, hl, :], ident[0:64, 0:64], PLc[:, hl:hl + 1], ps[B_][0:64, 0:64], ALU.mult, ALU.add, r=["C", rPL, P_], w=[rn("Gm")])
                            yield
                            mm1(ArkT[:, hl, :], tsl(Vv, hl), [rn("ArkT"), rVv], cols=64, last=False)
                            mm1(ArbT[:, hl, :], U0[:, hl, :], [rn("ArbT"), rn("U0")], cols=64, first=False, last=False)
                            mm1(RhT[:, hl, :], S0T[:, hl, :], [rn("RhT"), rn("S0T")], cols=64, first=False)
                            kb.cp("act", yb[:, hl * 64:(hl + 1) * 64], ps[B_][:, 0:64], r=[P_], w=["tA1"])
                            yield
                            if i == 15:
                                mm1(tsl(Vv, hl), tsl(KH, hl), [rVv, rKH], cols=64, rows=64, last=False)
                                mm1(U0[:, hl, :], tsl(BH, hl), [rn("U0"), rBH], cols=64, rows=64, first=False, last=False)
                                mm1(S0T[:, hl, :], Gm[:, hl, :], [rn("S0T"), rn("Gm")], cols=64, rows=64, first=False)
                                kb.cp("dve", SLo[:, hl, :], ps[B_][0:64, 0:64], r=[P_], w=["S0nat"])
                                yield
                            mm1(tsl(KH, hl), tsl(Vv, hl), [rVv, rKH], cols=64, rows=64, last=False)
                            mm1(tsl(BH, hl), U0[:, hl, :], [rn("U0"), rBH], cols=64, rows=64, first=False, last=False)
                            mm1(Gm[:, hl, :], S0T[:, hl, :], [rn("S0T"), rn("Gm")], cols=64, rows=64, first=False)
                            kb.cp("dve", S0T[:, hl, :], ps[B_][0:64, 0:64], r=[P_], w=[rn("S0T")])
                            yield

                        gens = [head_alg(hl) for hl in range(4)] + ([fg] if fg is not None else [])
                        while gens:
                            for g_ in list(gens):
                                try:
                                    next(g_)
                                except StopIteration:
                                    gens.remove(g_)
                        if not sample:
                            if i == 15:
                                kb.dma("sp", o_pwkv[hg * 4:hg * 4 + 4].rearrange("h v k -> v h k"), SLo[:, 0:4, :], r=["S0nat"], w=())
                        else:
                            sample_states()
                        if hg == 0 and i == 0:
                            stop_at(57)
                        if hg == 0 and i == 16:
                            stop_at(60)
                        for hl in range(4):
                            s.add("dve", lambda g_, hl=hl: g_.bn_stats(out=small[:, 8 + hl * 6:14 + hl * 6], in_=yb[:, hl * 64:(hl + 1) * 64]), r=["tA1"], w=["tA1"], tag="bnst")
                        for hl in range(4):
                            s.add("dve", lambda g_, hl=hl: g_.bn_aggr(out=small[:, 32 + hl * 2:34 + hl * 2], in_=small[:, 8 + hl * 6:14 + hl * 6]), r=["tA1"], w=["tA1"], tag="bnag")
                        mvv = small[:, 32:40].rearrange("p (h two) -> p h two", two=2)
                        kb.act(small[:, 40:44].unsqueeze(2), mvv[:, :, 1:2], AF.Sqrt, r=["tA1"], w=["tA1"], bias=64e-5, scale=1.0)
                        s.add("dve", lambda g_: g_.reciprocal(out=small[:, 40:44], in_=small[:, 40:44]), r=["tA1"], w=["tA1"], tag="recip")
                        kb.tt("dve", small[:, 44:48].unsqueeze(2), mvv[:, :, 0:1], small[:, 40:44].unsqueeze(2), ALU.mult, r=["tA1"], w=["tA1"])
                        kb.ts("dve", small[:, 44:48], small[:, 44:48], -1.0, None, ALU.mult, None, r=["tA1"], w=["tA1"])
                        for hl in range(4):
                            kb.act(yb[:, hl * 64:(hl + 1) * 64], yb[:, hl * 64:(hl + 1) * 64], AF.Identity, r=["tA1", "tA1"], w=["tA1"],
                                   bias=small[:, 44 + hl:45 + hl], scale=small[:, 40 + hl:41 + hl])
                        kb.tt("dve", yb[:], yb[:], bcs["lngb"][:], ALU.mult, r=["tA1", "lngb"], w=["tA1"])
                        kb.tt("dve", yb[:], yb[:], bcs["lnbb"][:], ALU.add, r=["tA1", "lnbb"], w=["tA1"])
                        kb.tt("dve", Tbk[:].rearrange("p (h k) -> p h k", k=64), Vv[:].rearrange("p (h k) -> p h k", k=64),
                              smallf[:, 4:8].unsqueeze(2).to_broadcast([128, 4, 64]), ALU.mult, r=[rVv, rsf], w=["Tbk"])
                        kb.tt("dve", yb[:], yb[:], Tbk[:], ALU.add, r=["tA1", "Tbk"], w=["tA1"])
                        kb.tt("dve", yb[:], yb[:], Gt, ALU.mult, r=["tA1", rGV], w=["tA1"])
                        for cc in range(2):
                            kb.tr(ps[7][:, cc * 128:(cc + 1) * 128], yb[:, cc * 128:(cc + 1) * 128], ident, r=["tA1", "C"], w=[PS[7]], sig=(cc == 1))
                        kb.cp("act", ygT[:, 2 * hg:2 * hg + 2, i * 128:(i + 1) * 128], ps[7][:, 0:256].rearrange("p (c t) -> p c t", t=128), r=[PS[7]], w=["ygT%d" % i])
                    def run_gen(g_):
                        for _ in g_:
                            pass

                    run_gen(front(0))
                    for i in range(NT):
                        back(i, front(i + 1) if i + 1 < NT else None)
                    s.barrier()
            for c in range(8):
                kb.tr(ps[0][0:17, c * 128:(c + 1) * 128] if c < 4 else ps[1][0:17, (c - 4) * 128:(c - 3) * 128], hlast[:, c, :], ident, r=["hlast", "C"],
                      w=[PS[0] if c < 4 else PS[1]], sig=(c in (3, 7)))
            kb.cp("dve", tB[0][0:17, 0:512], ps[0][0:17, :], r=[PS[0]], w=["tB0"])
            kb.cp("dve", tB[0][0:17, 512:1024], ps[1][0:17, :], r=[PS[1]], w=["tB0"])
            kb.dma("sp", o_pshift, tB[0][0:1, :], r=["tB0"], w=())
            kb.dma("sp", o_sshift, tB[0][1:17, :], r=["tB0"], w=())
            s.barrier()
            with contextlib.ExitStack() as pc_:
                alloc_gl(pc_)
                mod_prepare(l, 1, 1.0, blocks=[4, 5])
                Gp, Gs = gl["Gp"], gl["Gs"]
                wout = kb.sb(pc_, "wo_sb", [128, 8, D], BF16)
                kb.dma("pool", wout[:], rw_wo[0].rearrange("(kc p) n -> p kc n", p=128), r=(), w=["wout"])
                proj_acc(wout, "wout", 8, lambda i, kc: (ygT[:, kc, i * 128:(i + 1) * 128], "ygT%d" % i), True, True)
                s.barrier()
        s.barrier()

    def dump_and_finish():
        for i in range(NT):
            kb.dma("sp", yout[i * 128:(i + 1) * 128, :], X[:, i, :], r=["X%d" % i], w=())

    stage = 0
    for l in range(2):
        for sub in range(3):
            if sub == 0:
                ffn(l, 0, 0, 0.5)
            elif sub == 2:
                ffn(l, 1, 2, 0.5)
            elif l == 0:
                ab_mixer(l)
            else:
                rwkv_mixer(l)
            stage += 1
            if stage >= upto:
                return dump_and_finish()
    dump_and_finish()


def build(upto=99, stop_point=None):
    kb = KB()
    kb.stop_point = stop_point
    with contextlib.ExitStack() as es:
        kb.es = es
        build_program(kb, upto)
        kb.s.emit(kb.nc, es)
    return kb


def core_inputs(inp, core, kb):
    sl = slice(16 * core, 16 * core + 16)
    xs = inp["x_sample"][sl].reshape(128, D)
    f32 = np.float32

    def fm(v):
        return np.ascontiguousarray(np.asarray(v, f32).reshape(-1, 128).T)

    m = {
        "x": np.concatenate([inp["x_prompt"][core], xs], axis=0),
        "c": np.concatenate([inp["c_prompt"][core:core + 1], inp["c_sample"][sl]], axis=0),
        "cst": CST_ARR,
    }
    cw = inp["rg_conv_w"][0]
    vecA = np.zeros((128, 32), f32)
    for c in range(4):
        for j in range(4):
            vecA[:, c * 4 + j] = cw[j, c * 128:(c + 1) * 128]
    vecA[:, 16:20] = fm(inp["rg_conv_b"][0])
    vecA[:, 20:24] = fm(inp["rg_b_a"][0])
    vecA[:, 24:28] = fm(inp["rg_b_x"][0])
    vecA[:, 28:32] = fm(inp["rg_lambda"][0])
    m["vecA"] = vecA
    m["bgT"] = np.ascontiguousarray(inp["mlstm_b_gates"][0].T)
    m["minitT"] = np.ascontiguousarray(inp["state_mlstm_m"][0, sl].T)
    m["smC"] = inp["state_mlstm_C"][0, sl]
    m["smn"] = inp["state_mlstm_n"][0, sl]
    m["srh"] = inp["state_rglru_h"][0, sl]
    m["srconv"] = inp["state_rglru_conv"][0, sl].reshape(48, 512)
    mu = inp["rw_mu"][0]
    muT = np.zeros((128, 48), f32)
    for j in range(6):
        muT[:, j * 8:(j + 1) * 8] = fm(mu[j])
    m["muT"] = muT
    m["rk_flat"] = inp["rw_r_k"].reshape(1, D)
    m["swkv"] = inp["state_rwkv_wkv"][0, sl]
    m["sshift"] = inp["state_rwkv_shift"][0, sl]
    for k in kb.dram:
        if k not in m and k in inp:
            m[k] = inp[k]
    return {k: np.ascontiguousarray(v, dtype=f32) for k, v in m.items() if k in kb.dram}


_CACHE = {}


def kernel(**inputs):
    inp = {k: np.asarray(v) for k, v in inputs.items()}
    if "kb" not in _CACHE:
        _CACHE["kb"] = build()
    kb = _CACHE["kb"]
    in_maps = [core_inputs(inp, c, kb) for c in range(NCORES)]
    res = run_bass_kernel_spmd(kb.nc, in_maps, core_ids=list(range(NCORES)))
    R = res.results
    f32 = np.float32
    cat = lambda key, f=(lambda a: a): np.stack([f(np.asarray(R[c][key], f32)) for c in range(NCORES)], axis=0)
    cats = lambda key, f=(lambda a: a): np.concatenate([f(np.asarray(R[c][key], f32)) for c in range(NCORES)], axis=0)
    y_prompt = cat("y", lambda a: a[:2048])
    y_sample = cats("y", lambda a: a[2048:].reshape(16, 8, D))
    outs = (
        y_prompt, y_sample,
        cat("o_pmC")[None], cat("o_pmn")[None], cat("o_pmm", lambda a: a[:, 0])[None],
        cat("o_prh", lambda a: a.reshape(512))[None], cat("o_prconv")[None],
        cat("o_pwkv")[None], cat("o_pshift", lambda a: a[0])[None],
        cats("o_smC")[None], cats("o_smn")[None], cats("o_smm", lambda a: a.T)[None],
        cats("o_srh")[None], cats("o_srconv", lambda a: a.reshape(16, 3, 512))[None],
        cats("o_swkv")[None], cats("o_sshift")[None],
    )
    return tuple(np.ascontiguousarray(o, dtype=f32) for o in outs)
```

```python
import contextlib
import numpy as np
import concourse.bass as bass
import concourse.mybir as mybir
from concourse.bass_utils import run_bass_kernel_spmd

F32 = mybir.dt.float32
BF16 = mybir.dt.bfloat16
F32R = mybir.dt.float32r
AF = mybir.ActivationFunctionType
ALU = mybir.AluOpType
AX = mybir.AxisListType

D = 1024
DFF = 2816
NT = 17
NTOK = NT * 128
ALPHA = 4.0 ** 0.25
LN_EPS = 1e-5
NCORES = 8


class Op:
    __slots__ = ("eng", "fn", "deps", "sig", "idx", "dma", "slot", "slot_total", "sigcount", "waits", "tag")


class Sched:
    ENGS = ["pe", "act", "dve", "pool", "sp"]

    def __init__(self, n_slots=40):
        self.q = {e: [] for e in self.ENGS}
        self.last_w = {}
        self.readers = {}
        self.n_slots = n_slots
        self.slot_rr = 0
        self.sw_rr = 0
        self.n_hw = n_slots - 12
        self.slot_total = [0] * n_slots
        self.slot_last = [None] * n_slots
        self.all_dma = []

    skip = False
    capture = None
    tick_fn = None
    _ticking = False

    def add(self, eng, fn, r=(), w=(), sig=True, dma=False, tag=""):
        if self.skip:
            return None
        if self.capture is not None:
            self.capture.append((eng, fn, tuple(r), tuple(w), sig, dma, tag))
            return None
        op = self._add(eng, fn, r, w, sig, dma, tag)
        if self.tick_fn is not None and not self._ticking:
            self._ticking = True
            try:
                self.tick_fn()
            finally:
                self._ticking = False
        return op

    def _add(self, eng, fn, r=(), w=(), sig=True, dma=False, tag=""):
        op = Op()
        op.eng, op.fn, op.sig, op.dma, op.tag = eng, fn, sig, dma, tag
        op.slot = None
        deps = []
        seen = set()

        def dep(o):
            if o is None or id(o) in seen:
                return
            seen.add(id(o))
            if (not dma) and eng == "pe" and o.eng == "pe" and not o.dma:
                return
            deps.append(o)

        for k in r:
            dep(self.last_w.get(k))
        for k in w:
            dep(self.last_w.get(k))
            for o in self.readers.get(k, {}).values():
                dep(o)
        if dma:
            if eng == "pool":
                slot = self.n_hw + (self.sw_rr % (self.n_slots - self.n_hw))
                self.sw_rr += 1
            else:
                slot = self.slot_rr % self.n_hw
                self.slot_rr += 1
            dep(self.slot_last[slot])
            self.slot_total[slot] += 16
            op.slot = slot
            op.slot_total = self.slot_total[slot]
            self.slot_last[slot] = op
            self.all_dma.append(op)
        op.deps = deps
        self.q[eng].append(op)
        op.idx = len(self.q[eng]) - 1
        key = ("dma", id(op)) if dma else eng
        for k in r:
            self.readers.setdefault(k, {})[key] = op
        for k in w:
            self.last_w[k] = op
            self.readers[k] = {}
        return op

    def barrier(self):
        if self.skip:
            return
        lasts = []
        for e in self.ENGS:
            comp = [o for o in self.q[e] if (not o.dma) and o.fn is not None]
            if comp:
                comp[-1].sig = True
                lasts.append(comp[-1])
        lasts += [o for o in self.slot_last if o is not None]
        for e in self.ENGS:
            op = Op()
            op.eng, op.sig, op.dma, op.tag, op.slot, op.fn = e, False, False, "barrier", None, None
            op.deps = list(lasts)
            self.q[e].append(op)
            op.idx = len(self.q[e]) - 1
        self.last_w = {}
        self.readers = {}

    def finalize(self):
        for e in self.ENGS:
            for o in reversed(self.q[e]):
                if not o.dma and o.fn is not None and e != "sp":
                    o.sig = True
                    break
        self.sigtot = {}
        for e in self.ENGS:
            cnt = 0
            ops = self.q[e]
            pref = []
            for o in ops:
                if (not o.dma) and o.sig:
                    cnt += 1
                pref.append(cnt)
            self.sigtot[e] = cnt
            nxt = None
            for i in range(len(ops) - 1, -1, -1):
                o = ops[i]
                if (not o.dma) and o.sig:
                    nxt = pref[i]
                o.sigcount = nxt if not o.dma else None
        for e in self.ENGS:
            known = {}
            for o in self.q[e]:
                need = {}
                for d in o.deps:
                    if d.dma:
                        key, val = ("slot", d.slot), d.slot_total
                    else:
                        if d.sigcount is None:
                            raise RuntimeError("dependency on op with no later signal: %s" % d.tag)
                        key, val = ("eng", d.eng), d.sigcount
                    if val > need.get(key, 0):
                        need[key] = val
                o.waits = []
                for key, val in need.items():
                    if known.get(key, 0) < val:
                        known[key] = val
                        o.waits.append((key, val))

    def simulate(self):
        pc = {e: 0 for e in self.ENGS}
        sem = {}
        sigc = {e: 0 for e in self.ENGS}
        progress = True
        while progress:
            progress = False
            for e in self.ENGS:
                while pc[e] < len(self.q[e]):
                    o = self.q[e][pc[e]]
                    ok = all(sem.get(k, 0) >= v for k, v in o.waits)
                    if not ok:
                        break
                    if o.dma:
                        sem[("slot", o.slot)] = sem.get(("slot", o.slot), 0) + 16
                    elif o.sig:
                        sem[("eng", e)] = sem.get(("eng", e), 0) + 1
                    pc[e] += 1
                    progress = True
        stuck = {e: (pc[e], len(self.q[e])) for e in self.ENGS if pc[e] < len(self.q[e])}
        if stuck:
            msg = []
            for e, (p, n) in stuck.items():
                o = self.q[e][p]
                msg.append("%s stuck at %d/%d tag=%s waits=%s" % (e, p, n, o.tag, [(k, v, sem.get(k, 0)) for k, v in o.waits]))
            raise RuntimeError("DEADLOCK in wait graph:\n" + "\n".join(msg))

    def emit(self, nc, es):
        self.finalize()
        self.simulate()
        engsem = {e: es.enter_context(nc.semaphore("sem_" + e)) for e in ["pe", "act", "dve", "pool"]}
        slotsem = [es.enter_context(nc.semaphore("slot%d" % i)) for i in range(self.n_slots)]

        def semof(key):
            return engsem[key[1]] if key[0] == "eng" else slotsem[key[1]]

        def run(e, g):
            for o in self.q[e]:
                for key, val in o.waits:
                    g.wait_ge(semof(key), val)
                if o.fn is None:
                    continue
                ins = o.fn(g)
                if o.dma:
                    ins.then_inc(slotsem[o.slot], 16)
                elif o.sig:
                    ins.then_inc(engsem[e], 1)
            if e == "sp":
                for s in range(self.n_slots):
                    if self.slot_total[s] > 0:
                        g.wait_ge(slotsem[s], self.slot_total[s])

        with nc.Block() as blk:
            blk.tensor(lambda g: run("pe", g))
            blk.scalar(lambda g: run("act", g))
            blk.vector(lambda g: run("dve", g))
            blk.gpsimd(lambda g: run("pool", g))
            blk.sync(lambda g: run("sp", g))


class KB:
    def __init__(self, stop_after=None, debug=False):
        self.nc = bass.Bass("TRN2", target_bir_lowering=False)
        self.s = Sched()
        self.stop_after = stop_after
        self.debug = debug
        self.dram = {}

    def din(self, name, shape, dt=F32):
        t = self.nc.dram_tensor(name, list(shape), dt, kind="ExternalInput")
        self.dram[name] = t
        return t.ap()

    def dout(self, name, shape, dt=F32):
        t = self.nc.dram_tensor(name, list(shape), dt, kind="ExternalOutput")
        self.dram[name] = t
        return t.ap()

    def sb(self, es, name, shape, dt=F32):
        self.uid = getattr(self, "uid", 0) + 1
        return es.enter_context(self.nc.sbuf_tensor("%s_%d" % (name, self.uid), list(shape), dt))

    def mm(self, out, lhsT, rhs, start, stop, r, w, sig=None, tag="mm"):
        if sig is None:
            sig = stop
        return self.s.add("pe", lambda g: g.matmul(out, lhsT=lhsT, rhs=rhs, start=start, stop=stop), r=r, w=w, sig=sig, tag=tag)

    def tr(self, out, in_, ident, r, w, sig=True, tag="tr"):
        return self.s.add("pe", lambda g: g.transpose(out, in_, ident), r=r, w=w, sig=sig, tag=tag)

    def act(self, out, in_, func, r, w, bias=None, scale=None, eng="act", tag="act"):
        kw = {}
        if bias is not None:
            kw["bias"] = bias
        if scale is not None:
            kw["scale"] = scale
        return self.s.add("act", lambda g: g.activation(out=out, in_=in_, func=func, **kw), r=r, w=w, tag=tag)

    def tt(self, eng, out, in0, in1, op, r, w, tag="tt"):
        return self.s.add(eng, lambda g: g.tensor_tensor(out=out, in0=in0, in1=in1, op=op), r=r, w=w, tag=tag)

    def ts(self, eng, out, in0, s1, s2, op0, op1, r, w, tag="ts"):
        if op1 is None:
            return self.s.add(eng, lambda g: g.tensor_scalar(out=out, in0=in0, scalar1=s1, scalar2=None, op0=op0), r=r, w=w, tag=tag)
        return self.s.add(eng, lambda g: g.tensor_scalar(out=out, in0=in0, scalar1=s1, scalar2=s2, op0=op0, op1=op1), r=r, w=w, tag=tag)

    def stt(self, out, in0, scalar, in1, op0, op1, r, w, tag="stt"):
        return self.s.add("dve", lambda g: g.scalar_tensor_tensor(out=out, in0=in0, scalar=scalar, in1=in1, op0=op0, op1=op1), r=r, w=w, tag=tag)

    def cp(self, eng, out, in_, r, w, tag="cp"):
        if eng == "act":
            return self.s.add("act", lambda g: g.copy(out=out, in_=in_), r=r, w=w, tag=tag)
        return self.s.add(eng, lambda g: g.tensor_copy(out=out, in_=in_), r=r, w=w, tag=tag)

    def memset(self, eng, ap, val, w, tag="memset"):
        return self.s.add(eng, lambda g: g.memset(ap, val), r=(), w=w, tag=tag)

    def dma(self, q, out, in_, r, w, tag="dma", **kw):
        return self.s.add(q, lambda g: g.dma_start(out=out, in_=in_, **kw), r=r, w=w, dma=True, tag=tag)


def make_consts():
    c = {}
    c["ident"] = np.eye(128, dtype=np.float32)
    selP = np.zeros((128, 128), np.float32)
    selP[0, :] = 1.0
    selS = np.zeros((128, 128), np.float32)
    for p in range(128):
        selS[1 + p // 8, p] = 1.0
    c["selP"] = selP
    c["selS"] = selS
    st = np.arange(128)
    c["maskP"] = (st[:, None] <= st[None, :]).astype(np.float32)
    c["maskS"] = ((st[:, None] <= st[None, :]) & (st[:, None] // 8 == st[None, :] // 8)).astype(np.float32)
    c["rst"] = np.tile((st % 8 != 0).astype(np.float32)[None, :], (128, 1))
    c["rstm"] = np.tile(np.where(st % 8 == 0, -1e30, 0.0).astype(np.float32)[None, :], (128, 1))
    bms = np.zeros((128, 128), np.float32)
    bms[st, st // 8] = 1.0
    c["bms"] = bms
    c["ones"] = np.ones((128, 128), np.float32)
    same = (st[:, None] // 8 == st[None, :] // 8)
    c["upP"] = (st[:, None] < st[None, :]).astype(np.float32)
    c["lowP"] = (st[:, None] > st[None, :]).astype(np.float32)
    c["upS"] = ((st[:, None] < st[None, :]) & same).astype(np.float32)
    c["lowS"] = ((st[:, None] > st[None, :]) & same).astype(np.float32)
    c["blkS"] = same.astype(np.float32)
    blk = lambda b: (st[:, None] // b == st[None, :] // b)
    c["low16"] = ((st[:, None] > st[None, :]) & blk(16)).astype(np.float32)
    c["up16"] = ((st[:, None] < st[None, :]) & blk(16)).astype(np.float32)
    for b in (16, 32, 64):
        c["m%d" % b] = (blk(2 * b) & ~blk(b)).astype(np.float32)
    names = list(c.keys())
    arr = np.concatenate([c[k] for k in names], axis=1)
    offs = {}
    o = 0
    for k in names:
        offs[k] = o
        o += c[k].shape[1]
    return arr, offs


CST_ARR, CST_OFF = make_consts()
NCST = CST_ARR.shape[1]

FFN_PARTS = [(0, 4), (4, 4), (8, 4), (12, 4), (16, 4), (20, 2)]
TGS = [(0, 512), (512, 512), (1024, 512), (1536, 512), (2048, 128)]


def build_program(kb, upto=99):
    nc, s = kb.nc, kb.s
    es = kb.es
    xin = kb.din("x", [NTOK, D])
    cin = kb.din("c", [NT, D])
    cst = kb.din("cst", [128, NCST])
    ada_w = kb.din("ada_w", [2, D, 9 * D])
    ada_b = kb.din("ada_b", [2, 9 * D])
    ln_g = kb.din("ln_g", [2, 3, D])
    ln_b = kb.din("ln_b", [2, 3, D])
    ffn_w1 = kb.din("ffn_w1", [2, 2, D, DFF])
    ffn_w3 = kb.din("ffn_w3", [2, 2, D, DFF])
    ffn_w2 = kb.din("ffn_w2", [2, 2, DFF, D])
    yout = kb.dout("y", [NTOK, D])

    X = kb.sb(es, "X", [128, NT, D], F32)
    C = kb.sb(es, "cst_sb", [128, NCST], F32)
    cT = kb.sb(es, "cT", [128, 8, NT], BF16)
    onesb = kb.sb(es, "onesb", [1, 32], F32)
    modT = kb.sb(es, "modT", [128, 16, NT], F32)
    gl = {}

    def alloc_gl(stack):
        gl["Gp"] = kb.sb(stack, "Gp", [128, D], F32)
        gl["Gs"] = kb.sb(stack, "Gs", [128, D], F32)
        gl["LNg"] = kb.sb(stack, "LNg", [128, D], F32)
        gl["LNb"] = kb.sb(stack, "LNb", [128, D], F32)
    tA = [kb.sb(es, "tA%d" % i, [128, 512], F32) for i in range(4)]
    tB = [kb.sb(es, "tB%d" % i, [128, D], F32) for i in range(2)]
    stt_ = [kb.sb(es, "bnst%d" % i, [128, 2, 6], F32) for i in range(2)]
    mv = [kb.sb(es, "mv%d" % i, [128, 2], F32) for i in range(2)]
    rstd = [kb.sb(es, "rstd%d" % i, [128, 1], F32) for i in range(2)]
    nmr = [kb.sb(es, "nmr%d" % i, [128, 1], F32) for i in range(2)]
    tmpS = kb.sb(es, "tmpS", [128, 128], F32)
    ps = [es.enter_context(nc.psum_tensor("ps%d" % i, [128, 512], F32)) for i in range(8)]
    PS = ["ps%d" % i for i in range(8)]
    psb = [p.bitcast(BF16) for p in ps]

    ident = C[:, CST_OFF["ident"]:CST_OFF["ident"] + 128]
    selP = C[0:NT, CST_OFF["selP"]:CST_OFF["selP"] + 128]
    selS = C[0:NT, CST_OFF["selS"]:CST_OFF["selS"] + 128]

    kb.dma("sp", C[:], cst, r=(), w=["C"])
    for i in range(NT):
        kb.dma("sp", X[:, i, :], xin[i * 128:(i + 1) * 128, :], r=(), w=["X%d" % i])
    kb.memset("pool", onesb[:], 1.0, w=["onesb"])

    with contextlib.ExitStack() as ph0:
        c_sb = kb.sb(ph0, "c_sb", [NT, D], F32)
        cs_sb = kb.sb(ph0, "cs_sb", [NT, D], F32)
        kb.dma("sp", c_sb[:], cin, r=(), w=["c_sb"])
        kb.act(cs_sb[:], c_sb[:], AF.Silu, r=["c_sb"], w=["cs_sb"])
        for kc in range(8):
            kb.tr(ps[0][:, kc * NT:(kc + 1) * NT], cs_sb[0:NT, kc * 128:(kc + 1) * 128], C[0:NT, 0:NT],
                  r=["cs_sb", "C"], w=[PS[0]], sig=(kc == 7))
        kb.cp("dve", cT[:].rearrange("p a b -> p (a b)"), ps[0][:, 0:8 * NT], r=[PS[0]], w=["cT"])
    s.barrier()

    state = {"ada_i": 0, "ada_i2": 0, "psr": 0}

    def mod_prepare(l, sub, res_w, blocks=range(6)):
        if 5 in blocks:
            Gp, Gs = gl["Gp"], gl["Gs"]
            kb.dma("sp", gl["LNg"][:], ln_g[l, sub:sub + 1, :].to_broadcast([128, D]), r=(), w=["LNg"])
            kb.dma("sp", gl["LNb"][:], ln_b[l, sub:sub + 1, :].to_broadcast([128, D]), r=(), w=["LNb"])
        phm = contextlib.ExitStack()
        modst = [kb.sb(phm, "modst%d" % i, [NT, 512], F32) for i in range(2)]
        adaw = [kb.sb(phm, "adaw%d" % i, [128, 8, 256], BF16) for i in range(2)]
        adab = [kb.sb(phm, "adab%d" % i, [1, 512], F32) for i in range(2)]
        for b in blocks:
            i = state["ada_i"]
            state["ada_i"] += 1
            buf = i % 2
            co = sub * 3 * D + b * 512
            kb.dma("sp", adab[buf][:], ada_b[l:l + 1, co:co + 512], r=(), w=["adab%d" % buf])
            pm = 4 + (i % 2)
            for sbk in range(2):
                i2 = state["ada_i2"]
                state["ada_i2"] += 1
                wb = i2 % 2
                kb.dma("pool", adaw[wb][:], ada_w[l].rearrange("(kc p) n -> p kc n", p=128)[:, :, co + sbk * 256:co + (sbk + 1) * 256],
                       r=(), w=["adaw%d" % wb])
                for kc in range(8):
                    kb.mm(ps[pm][0:NT, sbk * 256:(sbk + 1) * 256], lhsT=cT[:, kc, :], rhs=adaw[wb][:, kc, :], start=(kc == 0), stop=False,
                          r=["cT", "adaw%d" % wb], w=[PS[pm]], sig=False)
                kb.mm(ps[pm][0:NT, sbk * 256:(sbk + 1) * 256], lhsT=onesb[0:1, 0:NT], rhs=adab[buf][:, sbk * 256:(sbk + 1) * 256], start=False, stop=True,
                      r=["onesb", "adab%d" % buf], w=[PS[pm]], sig=True)
            kb.cp("act", modst[buf][:], ps[pm][0:NT, :], r=[PS[pm]], w=["modst%d" % buf])
            if b < 4:
                for cc in range(4):
                    j = b * 4 + cc
                    kb.tr(ps[6][:, j * NT:(j + 1) * NT], modst[buf][0:NT, cc * 128:(cc + 1) * 128], C[0:NT, 0:NT],
                          r=["modst%d" % buf, "C"], w=[PS[6]], sig=(cc == 3))
                if b == 1:
                    kb.cp("dve", modT[:, 0:8, :].rearrange("p a b -> p (a b)"), ps[6][:, 0:8 * NT], r=[PS[6]], w=["modT"])
                if b == 3:
                    kb.ts("dve", modT[:, 8:16, :].rearrange("p a b -> p (a b)"), ps[6][:, 8 * NT:16 * NT], 1.0, None,
                          ALU.add, None, r=[PS[6]], w=["modT"])
            else:
                h = b - 4
                kb.mm(ps[7][:, :], lhsT=selP, rhs=modst[buf][:], start=True, stop=True, r=["C", "modst%d" % buf], w=[PS[7]])
                kb.act(Gp[:, h * 512:(h + 1) * 512], ps[7][:, :], AF.Identity, r=[PS[7]], w=["Gp"], bias=float(res_w), scale=float(res_w))
                kb.mm(ps[7][:, :], lhsT=selS, rhs=modst[buf][:], start=True, stop=True, r=["C", "modst%d" % buf], w=[PS[7]])
                kb.act(Gs[:, h * 512:(h + 1) * 512], ps[7][:, :], AF.Identity, r=[PS[7]], w=["Gs"], bias=float(res_w), scale=float(res_w))
        s.barrier()
        phm.close()

    def make_hT(hT, i, prescale=True, col0=None, res=None, hook=None):
        g = res if res is not None else "hT_g%d" % (i // 4)
        if col0 is None:
            col0 = i * 128
        for half in range(2):
            pb = state["psr"] % 4
            state["psr"] += 1
            for cc in range(4):
                c = half * 4 + cc
                kb.tr(ps[pb][:, cc * 128:(cc + 1) * 128], X[:, i, c * 128:(c + 1) * 128], ident,
                      r=["X%d" % i, "C"], w=[PS[pb]], sig=(cc == 3))
            for cc in range(4):
                c = half * 4 + cc
                src = ps[pb][:, cc * 128:(cc + 1) * 128]
                if hook is not None:
                    hook(i, c, src, PS[pb])
                if i < 16:
                    if cc % 2 == 0:
                        kb.act(hT[:, c, col0:col0 + 128], src, AF.Identity, r=[PS[pb], "modT"], w=[g],
                               bias=modT[:, c, 0:1], scale=modT[:, 8 + c, 0:1])
                    else:
                        kb.ts("dve", hT[:, c, col0:col0 + 128], src, modT[:, 8 + c, 0:1], modT[:, c, 0:1],
                              ALU.mult, ALU.add, r=[PS[pb], "modT"], w=[g])
                else:
                    sc = modT[:, 8 + c, 1:NT].unsqueeze(2).to_broadcast([128, 16, 8])
                    sh = modT[:, c, 1:NT].unsqueeze(2).to_broadcast([128, 16, 8])
                    kb.tt("dve", tmpS[:].rearrange("p (q t) -> p q t", t=8), src.rearrange("p (q t) -> p q t", t=8), sc,
                          ALU.mult, r=[PS[pb], "modT"], w=["tmpS"])
                    kb.tt("dve", hT[:, c, col0:col0 + 128].rearrange("p (q t) -> p q t", t=8),
                          tmpS[:].rearrange("p (q t) -> p q t", t=8), sh, ALU.add, r=["tmpS", "modT"], w=[g])

    def layer_norm(i):
        k = i % 2
        for h in range(2):
            s.add("dve", lambda g_, h=h, k=k, i=i: g_.bn_stats(out=stt_[k][:, h, :], in_=X[:, i, h * 512:(h + 1) * 512]),
                  r=["X%d" % i], w=["bnst%d" % k], tag="bnstats")
        s.add("dve", lambda g_, k=k: g_.bn_aggr(out=mv[k][:], in_=stt_[k][:].rearrange("p a b -> p (a b)")),
              r=["bnst%d" % k], w=["mv%d" % k], tag="bnaggr")
        kb.act(rstd[k][:], mv[k][:, 1:2], AF.Sqrt, r=["mv%d" % k], w=["rstd%d" % k], bias=float(LN_EPS), scale=1.0)
        s.add("dve", lambda g_, k=k: g_.reciprocal(out=rstd[k][:], in_=rstd[k][:]), r=["rstd%d" % k], w=["rstd%d" % k], tag="recip")
        kb.ts("dve", nmr[k][:], mv[k][:, 0:1], rstd[k][:, 0:1], -1.0, ALU.mult, ALU.mult, r=["mv%d" % k, "rstd%d" % k], w=["nmr%d" % k])
        kb.act(tB[k][:], X[:, i, :], AF.Identity, r=["X%d" % i, "rstd%d" % k, "nmr%d" % k], w=["tB%d" % k],
               bias=nmr[k][:, 0:1], scale=rstd[k][:, 0:1])
        ea = "pool" if i % 3 == 2 else "dve"
        kb.tt(ea, tB[k][:], tB[k][:], gl["LNg"][:], ALU.mult, r=["tB%d" % k, "LNg"], w=["tB%d" % k])
        kb.tt(ea, X[:, i, :], tB[k][:], gl["LNb"][:], ALU.add, r=["tB%d" % k, "LNb"], w=["X%d" % i])

    pacc = {"v": 0}

    def proj_acc(wt, wres, nk, lhs_of, do_ln, first, after_ln=None):
        Gp, Gs = gl["Gp"], gl["Gs"]
        for i in [16] + list(range(16)):
            if i == 0:
                kb.tt("dve", wt[:, 0:nk, :], wt[:, 0:nk, :], Gp[:].unsqueeze(1).to_broadcast([128, nk, D]), ALU.mult, r=[wres, "Gp"], w=[wres])
            for half in range(2):
                v = pacc["v"]
                pacc["v"] += 1
                py = 4 + (v % 4)
                for kc in range(nk):
                    lt, lres = lhs_of(i, kc)
                    kb.mm(ps[py][:, :], lhsT=lt, rhs=wt[:, kc, half * 512:(half + 1) * 512], start=(kc == 0), stop=(kc == nk - 1),
                          r=[lres, wres], w=[PS[py]])
                xs = X[:, i, half * 512:(half + 1) * 512]
                src = ps[py][:, :]
                rsrc = PS[py]
                if i == 16:
                    kb.tt("dve", tA[v % 4][:], ps[py][:, :], Gs[:, half * 512:(half + 1) * 512], ALU.mult, r=[PS[py], "Gs"], w=["tA%d" % (v % 4)])
                    src, rsrc = tA[v % 4][:], "tA%d" % (v % 4)
                if first:
                    kb.stt(xs, xs, float(ALPHA), src, ALU.mult, ALU.add, r=["X%d" % i, rsrc], w=["X%d" % i])
                else:
                    kb.tt("dve", xs, xs, src, ALU.add, r=["X%d" % i, rsrc], w=["X%d" % i])
            if do_ln:
                layer_norm(i)
                if after_ln is not None:
                    after_ln(i)

    def ffn(l, f, sub, res_w):
        with contextlib.ExitStack() as ph:
            alloc_gl(ph)
            mod_prepare(l, sub, res_w)
            Gp, Gs = gl["Gp"], gl["Gs"]
            hT = kb.sb(ph, "hT", [128, 8, NTOK], BF16)
            w1p = kb.sb(ph, "w1p", [128, 8, 512], BF16)
            w3p = kb.sb(ph, "w3p", [128, 8, 512], BF16)
            w2p = kb.sb(ph, "w2p", [128, 4, D], BF16)
            gbuf = kb.sb(ph, "gbuf", [128, 4, NTOK], BF16)
            sil = [kb.sb(ph, "sil%d" % i, [128, 512], F32) for i in range(2)]
            w1v = ffn_w1[l, f].rearrange("(kc p) n -> p kc n", p=128)
            w3v = ffn_w3[l, f].rearrange("(kc p) n -> p kc n", p=128)
            w2v = ffn_w2[l, f].rearrange("(j p) n -> p j n", p=128)
            u = 0
            v = 0
            def load_up(pi_):
                j0_, n_ = FFN_PARTS[pi_]
                kb.dma("pool", w1p[:, :, 0:n_ * 128], w1v[:, :, j0_ * 128:(j0_ + n_) * 128], r=(), w=["w1p"])
                kb.dma("pool", w3p[:, :, 0:n_ * 128], w3v[:, :, j0_ * 128:(j0_ + n_) * 128], r=(), w=["w3p"])

            def load_dn(pi_):
                j0_, n_ = FFN_PARTS[pi_]
                kb.dma("pool", w2p[:, 0:n_, :], w2v[:, j0_:j0_ + n_, :], r=(), w=["w2p"])

            load_up(0)
            load_dn(0)
            for i in range(NT):
                make_hT(hT, i)
            for pi, (j0, ncn) in enumerate(FFN_PARTS):
                for tg, (t0, nt_) in enumerate(TGS):
                    for jj in range(ncn):
                        pa, pb = (2 * u) % 4, (2 * u + 1) % 4
                        for kc in range(8):
                            kb.mm(ps[pa][:, 0:nt_], lhsT=w1p[:, kc, jj * 128:(jj + 1) * 128], rhs=hT[:, kc, t0:t0 + nt_],
                                  start=(kc == 0), stop=(kc == 7), r=["w1p", "hT_g%d" % tg], w=[PS[pa]])
                        for kc in range(8):
                            kb.mm(ps[pb][:, 0:nt_], lhsT=w3p[:, kc, jj * 128:(jj + 1) * 128], rhs=hT[:, kc, t0:t0 + nt_],
                                  start=(kc == 0), stop=(kc == 7), r=["w3p", "hT_g%d" % tg], w=[PS[pb]])
                        kb.act(sil[u % 2][:, 0:nt_], ps[pa][:, 0:nt_], AF.Silu, r=[PS[pa]], w=["sil%d" % (u % 2)])
                        kb.tt("dve", gbuf[:, jj, t0:t0 + nt_], sil[u % 2][:, 0:nt_], ps[pb][:, 0:nt_], ALU.mult,
                              r=["sil%d" % (u % 2), PS[pb]], w=["g_g%d" % tg])
                        u += 1
                if pi + 1 < len(FFN_PARTS):
                    load_up(pi + 1)
                proj_acc(w2p, "w2p", ncn, lambda i, jj: (gbuf[:, jj, i * 128:(i + 1) * 128], "g_g%d" % (i // 4)), pi == len(FFN_PARTS) - 1, pi == 0,
                         after_ln=(store_tile if (l == 1 and sub == 2 and upto >= 6) else None))
                if pi + 1 < len(FFN_PARTS):
                    load_dn(pi + 1)
        s.barrier()

    DKS = float(128 ** -0.5)

    class _Stop(Exception):
        pass

    def stop_at(n):
        if getattr(kb, "stop_point", None) == n:
            s.barrier()
            s.skip = True

    def ab_mixer(l):
        ab_mixer_(l)
        s.skip = False
        s.barrier()

    def ab_mixer_(l):
        ab_w_in = kb.din("ab_w_in", [1, D, 3080])
        ab_w_out = kb.din("ab_w_out", [1, D, D])
        mnorm_g = kb.din("mlstm_norm_g", [1, 512])
        vecA_d = kb.din("vecA", [128, 32])
        bgT_d = kb.din("bgT", [4, 2])
        minitT_d = kb.din("minitT", [4, 16])
        rg_w_a = kb.din("rg_w_a", [1, 8, 64, 64])
        rg_w_x = kb.din("rg_w_x", [1, 8, 64, 64])
        smC = kb.din("smC", [16, 4, 128, 128])
        smn = kb.din("smn", [16, 4, 128])
        srh = kb.din("srh", [16, 512])
        srconv = kb.din("srconv", [48, 512])
        o_pmC = kb.dout("o_pmC", [4, 128, 128])
        o_pmn = kb.dout("o_pmn", [4, 128])
        o_pmm = kb.dout("o_pmm", [4, 1])
        o_prh = kb.dout("o_prh", [4, 128])
        o_prconv = kb.dout("o_prconv", [3, 512])
        o_smC = kb.dout("o_smC", [16, 4, 128, 128])
        o_smn = kb.dout("o_smn", [16, 4, 128])
        o_smm = kb.dout("o_smm", [4, 16])
        o_srh = kb.dout("o_srh", [16, 512])
        o_srconv = kb.dout("o_srconv", [48, 512])

        maskP = C[:, CST_OFF["maskP"]:CST_OFF["maskP"] + 128]
        maskS = C[:, CST_OFF["maskS"]:CST_OFF["maskS"] + 128]
        rst = C[:, CST_OFF["rst"]:CST_OFF["rst"] + 128]
        rstm = C[:, CST_OFF["rstm"]:CST_OFF["rstm"] + 128]
        bms = C[:, CST_OFF["bms"]:CST_OFF["bms"] + 16]
        ones = C[:, CST_OFF["ones"]:CST_OFF["ones"] + 128]

        mod_prepare(l, 1, 1.0, blocks=range(4))
        win_v = ab_w_in[0].rearrange("(kc p) n -> p kc n", p=128)
        with contextlib.ExitStack() as ph:
            hmT = kb.sb(ph, "hmT", [128, 4, NTOK], BF16)
            vecA = kb.sb(ph, "vecA_sb", [128, 32], F32)
            kb.dma("sp", vecA[:], vecA_d, r=(), w=["vecA"])
            sigo = tA[0]
            hmf = tB[0][:, 0:512]
            with contextlib.ExitStack() as pa:
                winA = kb.sb(pa, "winA", [128, 8, 2056], BF16)
                kb.dma("pool", winA[:, :, 0:1024], win_v[:, :, 0:1024], r=(), w=["winA"])
                kb.dma("pool", winA[:, :, 1024:2056], win_v[:, :, 1024:2056], r=(), w=["winA"])
                qkT = kb.sb(pa, "qkT", [128, 8, 512], BF16)
                hTg = [kb.sb(pa, "hTgA%d" % i, [128, 8, 512], BF16) for i in range(2)]
                bg = kb.sb(pa, "bg", [4, 2], F32)
                nbg1 = kb.sb(pa, "nbg1", [4, 1], F32)
                minitT = kb.sb(pa, "minitT_sb", [4, 16], F32)
                mng = kb.sb(pa, "mng", [128, 512], F32)
                kb.dma("sp", bg[:], bgT_d, r=(), w=["bg"])
                kb.dma("sp", minitT[:], minitT_d, r=(), w=["minitT"])
                kb.dma("sp", mng[:], mnorm_g[0:1, :].to_broadcast([128, 512]), r=(), w=["mng"])
                kb.ts("dve", nbg1[:], bg[:, 1:2], -1.0, None, ALU.mult, None, r=["bg"], w=["nbg1"])
                R4 = lambda nm: kb.sb(pa, nm, [4, 128], F32)
                t1, IGa, Rt, t3, t4 = R4("r_t1"), R4("r_ig"), R4("r_rt"), R4("r_t3"), R4("r_t4")
                Bc = [R4("r_bc0"), R4("r_bc1")]
                Mx = [R4("r_mx0"), R4("r_mx1")]
                dd = kb.sb(pa, "r_dd", [4, 16], F32)
                DDm = kb.sb(pa, "r_DD", [4, 64], F32)
                mout = kb.sb(pa, "r_mout", [4, 16], F32)
                colq = [kb.sb(pa, "colq%d" % i, [128, 16], F32) for i in range(2)]
                decsb = kb.sb(pa, "decsb", [128, 64], F32)
                kw = kb.sb(pa, "kw", [128, 4, 128], BF16)
                ktok = kb.sb(pa, "ktok", [128, 4, 128], BF16)
                vext = [kb.sb(pa, "vext%d" % i, [128, 4, 130], BF16) for i in range(2)]
                PT = kb.sb(pa, "PT", [128, 4, 128], BF16)
                Cst = kb.sb(pa, "Cst", [128, 4, 130], F32)
                Cb = kb.sb(pa, "Cb", [128, 4, 130], BF16)
                dmax = kb.sb(pa, "dmax", [128, 4], F32)
                hst6 = kb.sb(pa, "hst6", [128, 4, 6], F32)
                hmv = kb.sb(pa, "hmv", [128, 4, 2], F32)
                hrs = kb.sb(pa, "hrs", [128, 4], F32)
                hnm = kb.sb(pa, "hnm", [128, 4], F32)
                for vv in vext:
                    kb.memset("pool", vv[:], 1.0, w=["vext0", "vext1"])
                kb.memset("pool", Cst[:], 0.0, w=["Cst"])
                kb.memset("pool", Cb[:], 0.0, w=["Cb"])

                def rows(i):
                    k = i % 2
                    hb = (i // 4) % 2
                    hT = hTg[hb]
                    tc0 = (i % 4) * 128
                    pg = ps[7]
                    for kc in range(8):
                        kb.mm(pg[0:4, 0:128], lhsT=winA[:, kc, 2048:2052], rhs=hT[:, kc, tc0:tc0 + 128], start=(kc == 0), stop=(kc == 7),
                              r=["winA", "hTg%d" % hb], w=[PS[7]])
                    for kc in range(8):
                        kb.mm(pg[0:4, 128:256], lhsT=winA[:, kc, 2052:2056], rhs=hT[:, kc, tc0:tc0 + 128], start=(kc == 0), stop=(kc == 7),
                              r=["winA", "hTg%d" % hb], w=[PS[7]])
                    kb.act(IGa[:], pg[0:4, 0:128], AF.Identity, r=[PS[7], "bg"], w=["r_ig"], bias=bg[:, 0:1], scale=1.0)
                    kb.act(t1[:], pg[0:4, 128:256], AF.Exp, r=[PS[7], "nbg1"], w=["r_t1"], bias=nbg1[:, 0:1], scale=-1.0)
                    kb.act(t1[:], t1[:], AF.Ln, r=["r_t1"], w=["r_t1"], bias=1.0, scale=1.0)
                    kb.ts("dve", t1[:], t1[:], -1.0, None, ALU.mult, None, r=["r_t1"], w=["r_t1"])
                    prompt = i < 16
                    if prompt:
                        binit = 0.0 if i == 0 else Bc[1 - k][:, 127:128]
                        minit = 0.0 if i == 0 else Mx[1 - k][:, 127:128]
                        s.add("dve", lambda g_: g_.tensor_tensor_scan(out=Bc[k][:], data0=ones[0:4, :], data1=t1[:], initial=binit,
                                                                       op0=ALU.mult, op1=ALU.add),
                              r=["r_t1", "r_bc%d" % (1 - k), "C"], w=["r_bc%d" % k], tag="scanB")
                        kb.tt("dve", IGa[:], IGa[:], Bc[k][:], ALU.subtract, r=["r_ig", "r_bc%d" % k], w=["r_ig"])
                        kb.memset("dve", t3[:], 0.0, w=["r_t3"])
                        s.add("dve", lambda g_: g_.tensor_tensor_scan(out=Mx[k][:], data0=t3[:], data1=IGa[:], initial=minit,
                                                                       op0=ALU.add, op1=ALU.max),
                              r=["r_t3", "r_ig", "r_mx%d" % (1 - k)], w=["r_mx%d" % k], tag="scanM")
                        if i == 0:
                            kb.memset("dve", Rt[:], 0.0, w=["r_rt"])
                        else:
                            kb.cp("dve", Rt[:], Mx[1 - k][:, 127:128].to_broadcast([4, 128]), r=["r_mx%d" % (1 - k)], w=["r_rt"])
                    else:
                        s.add("dve", lambda g_: g_.tensor_tensor_scan(out=Bc[k][:], data0=rst[0:4, :], data1=t1[:], initial=0.0,
                                                                       op0=ALU.mult, op1=ALU.add),
                              r=["r_t1", "C"], w=["r_bc%d" % k], tag="scanB")
                        kb.tt("dve", IGa[:], IGa[:], Bc[k][:], ALU.subtract, r=["r_ig", "r_bc%d" % k], w=["r_ig"])
                        kb.cp("dve", t3[:], IGa[:], r=["r_ig"], w=["r_t3"])
                        kb.tt("dve", t3[:].rearrange("p (q t) -> p q t", t=8)[:, :, 0:1], IGa[:].rearrange("p (q t) -> p q t", t=8)[:, :, 0:1],
                              minitT[:].unsqueeze(2), ALU.max, r=["r_ig", "minitT"], w=["r_t3"])
                        s.add("dve", lambda g_: g_.tensor_tensor_scan(out=Mx[k][:], data0=rstm[0:4, :], data1=t3[:], initial=0.0,
                                                                       op0=ALU.add, op1=ALU.max),
                              r=["r_t3", "C"], w=["r_mx%d" % k], tag="scanM")
                        kb.cp("dve", Rt[:].rearrange("p (q t) -> p q t", t=8), minitT[:].unsqueeze(2).to_broadcast([4, 16, 8]),
                              r=["minitT"], w=["r_rt"])
                    kb.tt("dve", t3[:], IGa[:], Rt[:], ALU.subtract, r=["r_ig", "r_rt"], w=["r_t3"])
                    kb.act(t3[:], t3[:], AF.Exp, r=["r_t3"], w=["r_t3"])
                    kb.tt("dve", t4[:], Bc[k][:], Rt[:], ALU.add, r=["r_bc%d" % k, "r_rt"], w=["r_t4"])
                    kb.act(t4[:], t4[:], AF.Exp, r=["r_t4"], w=["r_t4"], scale=-1.0)
                    kb.tr(pg[:, 256:260], t3[0:4, :], C[0:4, 0:4], r=["r_t3", "C"], w=[PS[7]])
                    kb.tr(pg[:, 260:264], t4[0:4, :], C[0:4, 0:4], r=["r_t4", "C"], w=[PS[7]])
                    if prompt:
                        kb.tt("dve", dd[:, 0:1], Rt[:, 0:1], Mx[k][:, 127:128], ALU.subtract, r=["r_rt", "r_mx%d" % k], w=["r_dd"])
                        kb.act(dd[:, 0:1], dd[:, 0:1], AF.Exp, r=["r_dd"], w=["r_dd"])
                        kb.ts("dve", DDm[:, 0:4], C[0:4, 0:4], dd[:, 0:1], None, ALU.mult, None, r=["r_dd", "C"], w=["r_DD"])
                        kb.mm(pg[:, 264:268], lhsT=ones[0:4, :], rhs=DDm[:, 0:4], start=True, stop=True, r=["C", "r_DD"], w=[PS[7]])
                        kb.cp("dve", colq[k][:, 0:12], pg[:, 256:268], r=[PS[7]], w=["colq%d" % k])
                        if i == 15:
                            kb.tt("dve", mout[:, 0:1], Bc[k][:, 127:128], Mx[k][:, 127:128], ALU.add, r=["r_bc%d" % k, "r_mx%d" % k], w=["r_mout"])
                            kb.dma("sp", o_pmm, mout[:, 0:1], r=["r_mout"], w=())
                    else:
                        MT = Mx[k][:].rearrange("p (q t) -> p q t", t=8)[:, :, 7:8]
                        kb.tt("dve", t4[:].rearrange("p (q t) -> p q t", t=8), IGa[:].rearrange("p (q t) -> p q t", t=8),
                              MT.to_broadcast([4, 16, 8]), ALU.subtract, r=["r_ig", "r_mx%d" % k, PS[7]], w=["r_t4"])
                        kb.act(t4[:], t4[:], AF.Exp, r=["r_t4"], w=["r_t4"])
                        kb.tr(pg[:, 264:268], t4[0:4, :], C[0:4, 0:4], r=["r_t4", "C"], w=[PS[7]])
                        kb.cp("dve", colq[k][:, 0:12], pg[:, 256:268], r=[PS[7]], w=["colq%d" % k])
                        kb.tt("dve", dd[:].unsqueeze(2), minitT[:].unsqueeze(2), MT, ALU.subtract, r=["minitT", "r_mx%d" % k], w=["r_dd"])
                        kb.act(dd[:], dd[:], AF.Exp, r=["r_dd"], w=["r_dd"])
                        kb.tt("dve", DDm[:].rearrange("p (q h) -> p q h", h=4), dd[:].unsqueeze(2).to_broadcast([4, 16, 4]),
                              C[0:4, 0:4].unsqueeze(1).to_broadcast([4, 16, 4]), ALU.mult, r=["r_dd", "C"], w=["r_DD"])
                        kb.mm(pg[:, 272:336], lhsT=ones[0:4, :], rhs=DDm[:], start=True, stop=True, r=["C", "r_DD"], w=[PS[7]])
                        kb.cp("dve", decsb[:], pg[:, 272:336], r=[PS[7]], w=["decsb"])
                        kb.tt("dve", mout[:].unsqueeze(2), Bc[k][:].rearrange("p (q t) -> p q t", t=8)[:, :, 7:8], MT, ALU.add,
                              r=["r_bc%d" % k, "r_mx%d" % k], w=["r_mout"])
                        kb.dma("sp", o_smm, mout[:], r=["r_mout"], w=())

                def mlstm_tile(i):
                    k = i % 2
                    tg = i // 4
                    hT = hTg[tg % 2]
                    tc0 = (i % 4) * 128
                    lc0 = (i % 4) * 128 if i < 16 else 0
                    cq = colq[k]
                    vx = vext[k]
                    grp = "hTg%d" % (tg % 2)
                    for bi, c0 in enumerate((512, 1024, 1536)):
                        bank = 2 + (bi % 2)
                        for kc in range(8):
                            kb.mm(ps[bank][:, :], lhsT=hT[:, kc, tc0:tc0 + 128], rhs=winA[:, kc, c0:c0 + 512], start=(kc == 0), stop=(kc == 7),
                                  r=["winA", grp], w=[PS[bank]])
                        if bi == 0:
                            kb.act(ktok[:].rearrange("p a b -> p (a b)"), ps[bank][:, :], AF.Identity, r=[PS[bank]], w=["ktok"], scale=DKS)
                            for h in range(4):
                                kb.ts("dve", kw[:, h, :], ktok[:, h, :], cq[:, h:h + 1], None, ALU.mult, None,
                                      r=["ktok", "colq%d" % k], w=["kw"])
                        elif bi == 1:
                            kb.cp("act", vx[:, :, 0:128], ps[bank][:, :].rearrange("p (h d) -> p h d", d=128), r=[PS[bank]], w=["vext%d" % k])
                        else:
                            kb.act(sigo[:], ps[bank][:, :], AF.Sigmoid, r=[PS[bank]], w=["tA0"])
                    if i == 0:
                        stop_at(31)
                    for h in range(4):
                        kb.mm(ps[4][:, h * 128:(h + 1) * 128], lhsT=qkT[:, 4 + h, lc0:lc0 + 128], rhs=qkT[:, h, lc0:lc0 + 128],
                              start=True, stop=True, r=["qkT"], w=[PS[4]], sig=(h == 3))
                    if i == 0:
                        stop_at(32)
                    msk = maskP if i < 16 else maskS
                    for h in range(4):
                        kb.stt(PT[:, h, :], ps[4][:, h * 128:(h + 1) * 128], cq[:, h:h + 1], msk, ALU.mult, ALU.mult,
                               r=[PS[4], "colq%d" % k, "C"], w=["PT"])
                    return cq, vx, lc0

                def numden_finish(i, cq):
                    for half in range(2):
                        bank = ps[5 + half]
                        den = bank[:, 0:260].rearrange("p (h d) -> p h d", d=130)[:, :, 128:129]
                        kb.act(dmax[:, 2 * half:2 * half + 2].unsqueeze(2), den, AF.Abs, r=[PS[5 + half]], w=["dmax"])
                        kb.tt("dve", dmax[:, 2 * half:2 * half + 2], dmax[:, 2 * half:2 * half + 2], cq[:, 4 + 2 * half:6 + 2 * half], ALU.max,
                              r=["dmax", "colq%d" % (i % 2)], w=["dmax"])
                    s.add("dve", lambda g_: g_.reciprocal(out=dmax[:], in_=dmax[:]), r=["dmax"], w=["dmax"], tag="recip")
                    for h in range(4):
                        bank = ps[5 + h // 2]
                        o0 = (h % 2) * 130
                        kb.act(hmf[:, h * 128:(h + 1) * 128], bank[:, o0:o0 + 128], AF.Identity, r=[PS[5 + h // 2], "dmax"], w=["tB0"],
                               scale=dmax[:, h:h + 1])
                    for h in range(4):
                        s.add("dve", lambda g_, h=h: g_.bn_stats(out=hst6[:, h, :], in_=hmf[:, h * 128:(h + 1) * 128]), r=["tB0"], w=["hst6"], tag="bnst")
                    for h in range(4):
                        s.add("dve", lambda g_, h=h: g_.bn_aggr(out=hmv[:, h, :], in_=hst6[:, h, :]), r=["hst6"], w=["hmv"], tag="bnag")
                    kb.act(hrs[:].unsqueeze(2), hmv[:, :, 1:2], AF.Sqrt, r=["hmv"], w=["hrs"], bias=1e-6, scale=1.0)
                    s.add("dve", lambda g_: g_.reciprocal(out=hrs[:], in_=hrs[:]), r=["hrs"], w=["hrs"], tag="recip")
                    kb.tt("dve", hnm[:].unsqueeze(2), hmv[:, :, 0:1], hrs[:].unsqueeze(2), ALU.mult, r=["hmv", "hrs"], w=["hnm"])
                    kb.ts("dve", hnm[:], hnm[:], -1.0, None, ALU.mult, None, r=["hnm"], w=["hnm"])
                    for h in range(4):
                        kb.act(hmf[:, h * 128:(h + 1) * 128], hmf[:, h * 128:(h + 1) * 128], AF.Identity, r=["tB0", "hrs", "hnm"], w=["tB0"],
                               bias=hnm[:, h:h + 1], scale=hrs[:, h:h + 1])
                    kb.tt("dve", hmf[:], hmf[:], mng[:], ALU.mult, r=["tB0", "mng"], w=["tB0"])
                    kb.tt("dve", hmf[:], hmf[:], sigo[:], ALU.mult, r=["tB0", "tA0"], w=["tB0"])
                    for h in range(4):
                        kb.tr(ps[4][:, h * 128:(h + 1) * 128], hmf[:, h * 128:(h + 1) * 128], ident, r=["tB0", "C"], w=[PS[4]], sig=(h == 3))
                    kb.cp("act", hmT[:, :, i * 128:(i + 1) * 128], ps[4][:, :].rearrange("p (h d) -> p h d", d=128), r=[PS[4]], w=["hmT%d" % i])

                pend_rows = []
                for tg, (t0, nt_) in enumerate(TGS):
                    tiles = range(4 * tg, 4 * tg + 4) if tg < 4 else [16]
                    hT = hTg[tg % 2]
                    for i in tiles:
                        make_hT(hT, i, prescale=False, col0=(i % 4) * 128, res="hTg%d" % (tg % 2))
                    for j in range(8):
                        bank = j % 2
                        for kc in range(8):
                            kb.mm(ps[bank][:, 0:nt_], lhsT=winA[:, kc, j * 128:(j + 1) * 128], rhs=hT[:, kc, 0:nt_], start=(kc == 0), stop=(kc == 7),
                                  r=["winA", "hTg%d" % (tg % 2)], w=[PS[bank]])
                        if j < 4:
                            kb.cp("act", qkT[:, j, 0:nt_], ps[bank][:, 0:nt_], r=[PS[bank]], w=["qkT"])
                        else:
                            kb.ts("dve", qkT[:, j, 0:nt_], ps[bank][:, 0:nt_], DKS, None, ALU.mult, None, r=[PS[bank]], w=["qkT"])
                    for i in tiles:
                        if i == 0:
                            stop_at(1)
                        if i % 4 == 0 or i == 16:
                            rows(i)
                        else:
                            while pend_rows:
                                s.add(*pend_rows.pop(0))
                        if i < 16 and i % 4 != 3:
                            s.capture = []
                            rows(i + 1)
                            pend_rows.extend(s.capture)
                            s.capture = None
                            s.tick_fn = lambda: (s.add(*pend_rows.pop(0)) if pend_rows else None)
                        else:
                            s.tick_fn = None
                        if i == 0:
                            stop_at(2)
                        cq, vx, lc0 = mlstm_tile(i)
                        if i == 0:
                            stop_at(3)
                        if i == 16:
                            stop_at(5)
                        if i < 16:
                            for h in range(4):
                                bank = ps[5 + h // 2]
                                o0 = (h % 2) * 130
                                kb.mm(bank[:, o0:o0 + 130], lhsT=PT[:, h, :], rhs=vx[:, h, :], start=True, stop=False, r=["PT", "vext%d" % (i % 2)], w=[PS[5 + h // 2]], sig=False)
                                kb.mm(bank[:, o0:o0 + 130], lhsT=qkT[:, h, lc0:lc0 + 128], rhs=Cb[:, h, :], start=False, stop=True, r=["qkT", "Cb"], w=[PS[5 + h // 2]], sig=True)
                            numden_finish(i, cq)
                            for h in range(4):
                                bank = ps[5 + h // 2]
                                o0 = (h % 2) * 130
                                kb.mm(bank[:, o0:o0 + 130], lhsT=kw[:, h, :], rhs=vx[:, h, :], start=True, stop=True, r=["kw", "vext%d" % (i % 2)], w=[PS[5 + h // 2]])
                            for h in range(4):
                                bank = ps[5 + h // 2]
                                o0 = (h % 2) * 130
                                kb.ts("dve", Cst[:, h, :], Cst[:, h, :], cq[:, 8 + h:9 + h], None, ALU.mult, None, r=["Cst", "colq%d" % (i % 2)], w=["Cst"])
                                kb.stt(Cst[:, h, :], bank[:, o0:o0 + 130], cq[:, 8 + h:9 + h], Cst[:, h, :], ALU.mult, ALU.add,
                                       r=[PS[5 + h // 2], "Cst", "colq%d" % (i % 2)], w=["Cst"])
                            kb.cp("act", Cb[:], Cst[:], r=["Cst"], w=["Cb"])
                            if i == 0:
                                stop_at(4)
                            if i == 15:
                                for h in range(4):
                                    kb.tr(ps[4][:, h * 128:(h + 1) * 128], Cst[:, h, 0:128], ident, r=["Cst", "C"], w=[PS[4]], sig=(h == 3))
                                kb.cp("act", hmf[:], ps[4][:, :], r=[PS[4]], w=["tB0"])
                                kb.dma("sp", o_pmC.rearrange("h v k -> v h k"), hmf[:].rearrange("p (h k) -> p h k", k=128), r=["tB0"], w=())
                                kb.dma("sp", o_pmn.rearrange("h k -> k h"), Cst[:, :, 128], r=["Cst"], w=(), allow_slow_non_contiguous=True)
                        else:
                            with contextlib.ExitStack() as psm:
                                Cin = kb.sb(psm, "Cin", [128, 16, 128], F32)
                                CsT = kb.sb(psm, "CsT", [128, 16, 130], BF16)
                                qTm = kb.sb(psm, "qTm", [128, 16, 128], BF16)
                                VWm = kb.sb(psm, "VWm", [128, 16, 128], BF16)
                                nin = kb.sb(psm, "nin", [16, 4, 128], F32)
                                ninT = kb.sb(psm, "ninT", [128, 4, 16], F32)
                                BMW = kb.sb(psm, "BMW", [128, 16], BF16)
                                decc = kb.sb(psm, "decc", [16, 4], F32)
                                nout = nin
                                kb.memset("pool", qTm[:], 0.0, w=["qTm"])
                                kb.memset("pool", CsT[:], 0.0, w=["CsT"])
                                kb.dma("sp", nin[:], smn, r=(), w=["nin"])
                                kb.tr(ps[7][0:16, 400:404], dd[0:4, :], C[0:4, 0:4], r=["r_dd", "C"], w=[PS[7]])
                                kb.cp("dve", decc[:], ps[7][0:16, 400:404], r=[PS[7]], w=["decc"])
                                for h in range(4):
                                    kb.tr(ps[7][:, 416 + h * 16:432 + h * 16], nin[0:16, h, :], C[0:16, 0:16], r=["nin", "C"], w=[PS[7]], sig=(h == 3))
                                kb.cp("dve", ninT[:].rearrange("p a b -> p (a b)"), ps[7][:, 416:480], r=[PS[7]], w=["ninT"])
                                for h in range(4):
                                    bank = ps[5 + h // 2]
                                    o0 = (h % 2) * 130
                                    kb.dma("sp", Cin[:], smC[:, h].rearrange("q v k -> v q k"), r=(), w=["Cin"])
                                    for q4 in range(4):
                                        pb = q4 % 2
                                        for qq in range(4):
                                            q = q4 * 4 + qq
                                            kb.tr(ps[pb][:, qq * 128:(qq + 1) * 128], Cin[:, q, :], ident, r=["Cin", "C"], w=[PS[pb]], sig=(qq == 3))
                                        kb.cp("act", CsT[:, q4 * 4:q4 * 4 + 4, 0:128], ps[pb][:, :].rearrange("p (a b) -> p a b", b=128), r=[PS[pb]], w=["CsT"])
                                    kb.cp("dve", CsT[:, :, 128:129], ninT[:, h, :].unsqueeze(2), r=["ninT"], w=["CsT"])
                                    kb.cp("pool", bass.AP(qTm, 0, [[2048, 128], [136, 16], [1, 8]]),
                                          qkT[:, h, 0:128].rearrange("p (q t) -> p q t", t=8), r=["qkT"], w=["qTm"])
                                    kb.mm(bank[:, o0:o0 + 130], lhsT=PT[:, h, :], rhs=vx[:, h, :], start=True, stop=False, r=["PT", "vext%d" % (i % 2)], w=[PS[5 + h // 2]], sig=False)
                                    for q in range(16):
                                        kb.mm(bank[:, o0:o0 + 130], lhsT=qTm[:, q, :], rhs=CsT[:, q, :], start=False, stop=(q == 15), r=["qTm", "CsT"], w=[PS[5 + h // 2]], sig=(q == 15))
                                    kb.ts("dve", BMW[:], bms, cq[:, 8 + h:9 + h], None, ALU.mult, None, r=["C", "colq%d" % (i % 2)], w=["BMW"])
                                    kb.tt("dve", VWm[:], vx[:, h, 0:128].unsqueeze(1).to_broadcast([128, 16, 128]), BMW[:].unsqueeze(2).to_broadcast([128, 16, 128]),
                                          ALU.mult, r=["vext%d" % (i % 2), "BMW"], w=["VWm"])
                                    for q4 in range(4):
                                        pb = q4 % 2
                                        for qq in range(4):
                                            q = q4 * 4 + qq
                                            kb.mm(ps[pb][:, qq * 128:(qq + 1) * 128], lhsT=VWm[:, q, :], rhs=ktok[:, h, :], start=True, stop=True,
                                                  r=["VWm", "ktok"], w=[PS[pb]], sig=(qq == 3))
                                        for qq in range(4):
                                            q = q4 * 4 + qq
                                            kb.stt(Cin[:, q, :], Cin[:, q, :], decsb[:, q * 4 + h:q * 4 + h + 1], ps[pb][:, qq * 128:(qq + 1) * 128], ALU.mult, ALU.add,
                                                   r=["Cin", "decsb", PS[pb]], w=["Cin"])
                                    kb.dma("sp", o_smC[:, h].rearrange("q v k -> v q k"), Cin[:], r=["Cin"], w=())
                                    kb.mm(ps[7][0:16, 0:128], lhsT=BMW[:], rhs=ktok[:, h, :], start=True, stop=True, r=["BMW", "ktok"], w=[PS[7]])
                                    kb.stt(nout[:, h, :], nin[:, h, :], decc[:, h:h + 1], ps[7][0:16, 0:128], ALU.mult, ALU.add, r=["nin", "decc", PS[7]], w=["nin"])
                                numden_finish(i, cq)
                                kb.dma("sp", o_smn, nout[:], r=["nin"], w=())
                                s.barrier()
            s.barrier()
            s.tick_fn = None
            stop_at(6)
            hrT = kb.sb(ph, "hrT", [128, 4, NTOK], BF16)
            with contextlib.ExitStack() as pb_:
                winB = kb.sb(pb_, "winB", [128, 8, 1024], BF16)
                kb.dma("pool", winB[:], win_v[:, :, 2056:3080], r=(), w=["winB"])
                hTgB = [kb.sb(pb_, "hTgB%d" % i, [128, 8, 512], BF16) for i in range(2)]
                WA = kb.sb(pb_, "WA", [128, 4, 128], F32)
                WX = kb.sb(pb_, "WX", [128, 4, 128], F32)
                kb.memset("pool", WA[:], 0.0, w=["WA"])
                kb.memset("pool", WX[:], 0.0, w=["WX"])
                for c in range(4):
                    for hp in range(2):
                        kb.dma("sp", WA[hp * 64:(hp + 1) * 64, c, hp * 64:(hp + 1) * 64], rg_w_a[0, 2 * c + hp], r=(), w=["WA"])
                        kb.dma("sp", WX[hp * 64:(hp + 1) * 64, c, hp * 64:(hp + 1) * 64], rg_w_x[0, 2 * c + hp], r=(), w=["WX"])
                cl = kb.sb(pb_, "cl", [128, 4], F32)
                cl2 = kb.sb(pb_, "cl2", [128, 4], F32)
                kb.act(cl[:], vecA[:, 28:32], AF.Exp, r=["vecA"], w=["cl"], scale=-1.0)
                kb.act(cl[:], cl[:], AF.Ln, r=["cl"], w=["cl"], bias=1.0, scale=1.0)
                kb.ts("dve", cl2[:], cl[:], -16.0, None, ALU.mult, None, r=["cl"], w=["cl2"])
                kb.ts("dve", cl[:], cl[:], -8.0, None, ALU.mult, None, r=["cl", "cl2"], w=["cl"])
                xp = [kb.sb(pb_, "xp%d" % c, [128, 515], F32) for c in range(4)]
                xpss = [kb.sb(pb_, "xps%d" % k, [128, 16, 11], F32) for k in range(2)]
                hst = kb.sb(pb_, "hst", [128, 4], F32)
                h0T = kb.sb(pb_, "h0T", [128, 4, 16], F32)
                cvT = kb.sb(pb_, "cvT", [128, 4, 48], F32)
                hl = kb.sb(pb_, "hl", [128, 4, 16], F32)
                srh_sb = kb.sb(pb_, "srh_sb", [16, 512], F32)
                src_sb = kb.sb(pb_, "src_sb", [48, 512], F32)
                F5 = lambda nm: kb.sb(pb_, nm, [128, 512], F32)
                rgsets = [{nm: F5("%s%d" % (nm, k)) for nm in ("xc", "rr", "ii", "aa", "a2", "t5")} for k in range(2)]
                for c in range(4):
                    kb.memset("pool", xp[c][:, 0:3], 0.0, w=["xp%d" % c])
                kb.memset("pool", hst[:], 0.0, w=["hst0", "hst1", "hst2", "hst3"])
                kb.dma("sp", srh_sb[:], srh, r=(), w=["srh_sb"])
                kb.dma("sp", src_sb[:], srconv, r=(), w=["src_sb"])
                for c in range(4):
                    kb.tr(ps[6][:, c * 16:(c + 1) * 16], srh_sb[0:16, c * 128:(c + 1) * 128], C[0:16, 0:16], r=["srh_sb", "C"], w=[PS[6]], sig=(c == 3))
                kb.cp("dve", h0T[:].rearrange("p a b -> p (a b)"), ps[6][:, 0:64], r=[PS[6]], w=["h0T"])
                for c in range(4):
                    kb.tr(ps[6][:, 64 + c * 48:64 + (c + 1) * 48], src_sb[0:48, c * 128:(c + 1) * 128], C[0:48, 0:48], r=["src_sb", "C"], w=[PS[6]], sig=(c == 3))
                kb.cp("dve", cvT[:].rearrange("p a b -> p (a b)"), ps[6][:, 64:256], r=[PS[6]], w=["cvT"])
                def rg_unit(tg, t0, n, sample, hT, c, k):
                    u_ = tg * 4 + c
                    px, pgr = ps[(2 * u_) % 4], ps[(2 * u_ + 1) % 4]
                    PX, PGR = PS[(2 * u_) % 4], PS[(2 * u_ + 1) % 4]
                    S_ = rgsets[k]
                    xc, rr, ii, aa, a2, t5 = S_["xc"], S_["rr"], S_["ii"], S_["aa"], S_["a2"], S_["t5"]
                    uu, hh_ = a2, rr
                    gA, gX = (4, 5) if k == 0 else (6, 7)
                    xps = xpss[k]
                    nxps = "xps%d" % k
                    nx, nr, ni, na, n2, nt, nh = "xc%d" % k, "rr%d" % k, "ii%d" % k, "aa%d" % k, "a2%d" % k, "t5%d" % k, "hst%d" % c
                    for kc in range(8):
                        kb.mm(px[:, 0:n], lhsT=winB[:, kc, c * 128:(c + 1) * 128], rhs=hT[:, kc, 0:n], start=(kc == 0), stop=(kc == 7),
                              r=["winB", "hTg%d" % (tg % 2)], w=[PX])
                    for kc in range(8):
                        kb.mm(pgr[:, 0:n], lhsT=winB[:, kc, 512 + c * 128:512 + (c + 1) * 128], rhs=hT[:, kc, 0:n], start=(kc == 0), stop=(kc == 7),
                              r=["winB", "hTg%d" % (tg % 2)], w=[PGR])
                    cw = lambda j: vecA[:, c * 4 + j:c * 4 + j + 1]
                    cb = vecA[:, 16 + c:17 + c]
                    if not sample:
                        kb.cp("act", xp[c][:, 3:3 + n], px[:, 0:n], r=[PX], w=["xp%d" % c])
                        kb.ts("dve", xc[:, 0:n], xp[c][:, 0:n], cw(0), cb, ALU.mult, ALU.add, r=["xp%d" % c, "vecA"], w=[nx])
                        for j in range(1, 4):
                            kb.stt(xc[:, 0:n], xp[c][:, j:j + n], cw(j), xc[:, 0:n], ALU.mult, ALU.add, r=["xp%d" % c, "vecA", nx], w=[nx])
                        if tg != 3:
                            kb.cp("pool", xp[c][:, 0:3], xp[c][:, n:n + 3], r=["xp%d" % c], w=["xp%d" % c])
                    else:
                        kb.cp("dve", xps[:, :, 0:3], cvT[:, c, :].rearrange("p (q j) -> p q j", j=3), r=["cvT"], w=[nxps])
                        kb.cp("act", xps[:, :, 3:11], px[:, 0:n].rearrange("p (q t) -> p q t", t=8), r=[PX], w=[nxps])
                        xc3 = xc[:, 0:n].rearrange("p (q t) -> p q t", t=8)
                        kb.ts("dve", xc3, xps[:, :, 0:8], cw(0), cb, ALU.mult, ALU.add, r=[nxps, "vecA"], w=[nx])
                        for j in range(1, 4):
                            kb.stt(xc3, xps[:, :, j:j + 8], cw(j), xc3, ALU.mult, ALU.add, r=[nxps, "vecA", nx], w=[nx])
                        for j in range(3):
                            kb.tr(ps[gX][0:16, j * 128:(j + 1) * 128], xps[:, :, 8 + j], ident, r=[nxps, "C"], w=[PS[gX]], sig=(j == 2))
                        kb.cp("dve", t5[0:16, 0:384], ps[gX][0:16, 0:384], r=[PS[gX]], w=[nt])
                        kb.dma("sp", o_srconv.rearrange("(q j) f -> q j f", j=3)[:, :, c * 128:(c + 1) * 128],
                               t5[0:16, 0:384].rearrange("q (j f) -> q j f", f=128), r=[nt], w=())
                    kb.mm(ps[gA][:, 0:n], lhsT=WA[:, c, :], rhs=xc[:, 0:n], start=True, stop=True, r=["WA", nx], w=[PS[gA]])
                    kb.mm(ps[gX][:, 0:n], lhsT=WX[:, c, :], rhs=xc[:, 0:n], start=True, stop=True, r=["WX", nx], w=[PS[gX]])
                    kb.act(rr[:, 0:n], ps[gA][:, 0:n], AF.Sigmoid, r=[PS[gA], "vecA"], w=[nr], bias=vecA[:, 20 + c:21 + c], scale=1.0)
                    kb.act(ii[:, 0:n], ps[gX][:, 0:n], AF.Sigmoid, r=[PS[gX], "vecA"], w=[ni], bias=vecA[:, 24 + c:25 + c], scale=1.0)
                    kb.act(aa[:, 0:n], rr[:, 0:n], AF.Exp, r=[nr, "cl"], w=[na], scale=cl[:, c:c + 1])
                    kb.act(a2[:, 0:n], rr[:, 0:n], AF.Exp, r=[nr, "cl2"], w=[n2], scale=cl2[:, c:c + 1])
                    kb.act(a2[:, 0:n], a2[:, 0:n], AF.Sqrt, r=[n2], w=[n2], bias=1.0, scale=-1.0)
                    kb.tt("dve", uu[:, 0:n], a2[:, 0:n], ii[:, 0:n], ALU.mult, r=[n2, ni], w=[n2])
                    kb.tt("dve", uu[:, 0:n], uu[:, 0:n], xc[:, 0:n], ALU.mult, r=[n2, nx], w=[n2])
                    if not sample:
                        s.add("dve", lambda g_, c=c, n=n: g_.tensor_tensor_scan(out=hh_[:, 0:n], data0=aa[:, 0:n], data1=uu[:, 0:n], initial=hst[:, c:c + 1],
                                                                                 op0=ALU.mult, op1=ALU.add),
                              r=[na, n2, nh], w=[nr], tag="scanH")
                        kb.cp("dve", hst[:, c:c + 1], hh_[:, n - 1:n], r=[nr], w=[nh])
                    else:
                        aa3 = aa[:, 0:n].rearrange("p (q t) -> p q t", t=8)
                        uu3 = uu[:, 0:n].rearrange("p (q t) -> p q t", t=8)
                        kb.tt("dve", t5[:, 0:16].unsqueeze(2), aa3[:, :, 0:1], h0T[:, c, :].unsqueeze(2), ALU.mult, r=[na, "h0T", nt], w=[nt])
                        kb.tt("dve", uu3[:, :, 0:1], uu3[:, :, 0:1], t5[:, 0:16].unsqueeze(2), ALU.add, r=[n2, nt], w=[n2])
                        kb.tt("dve", aa[:, 0:n], aa[:, 0:n], rst, ALU.mult, r=[na, "C"], w=[na])
                        s.add("dve", lambda g_, n=n: g_.tensor_tensor_scan(out=hh_[:, 0:n], data0=aa[:, 0:n], data1=uu[:, 0:n], initial=0.0,
                                                                            op0=ALU.mult, op1=ALU.add),
                              r=[na, n2], w=[nr], tag="scanH")
                        kb.cp("dve", hl[:, c, :].unsqueeze(2), hh_[:, 0:n].rearrange("p (q t) -> p q t", t=8)[:, :, 7:8], r=[nr], w=["hl"])
                    kb.act(t5[:, 0:n], pgr[:, 0:n], AF.Square, r=[PGR, nt], w=[nt])
                    kb.ts("dve", t5[:, 0:n], t5[:, 0:n], 0.044715, 1.0, ALU.mult, ALU.add, r=[nt], w=[nt])
                    kb.tt("dve", t5[:, 0:n], t5[:, 0:n], pgr[:, 0:n], ALU.mult, r=[nt, PGR], w=[nt])
                    kb.act(t5[:, 0:n], t5[:, 0:n], AF.Tanh, r=[nt], w=[nt], scale=0.7978845608028654)
                    kb.ts("dve", t5[:, 0:n], t5[:, 0:n], 1.0, 0.5, ALU.add, ALU.mult, r=[nt], w=[nt])
                    kb.tt("dve", t5[:, 0:n], t5[:, 0:n], pgr[:, 0:n], ALU.mult, r=[nt, PGR], w=[nt])
                    kb.tt("dve", hrT[:, c, t0:t0 + n], t5[:, 0:n], hh_[:, 0:n], ALU.mult, r=[nt, nr], w=["hrT_g%d" % tg])

                for tg, (t0, n) in enumerate(TGS):
                    sample = tg == 4
                    hT = hTgB[tg % 2]
                    for i in (range(4 * tg, 4 * tg + 4) if tg < 4 else [16]):
                        make_hT(hT, i, prescale=True, col0=(i % 4) * 128, res="hTg%d" % (tg % 2))
                    for c0 in (0, 2):
                        s.capture = []
                        rg_unit(tg, t0, n, sample, hT, c0 + 1, 1)
                        pend_rg = s.capture
                        s.capture = None
                        s.tick_fn = lambda pr=pend_rg: (s.add(*pr.pop(0)) if pr else None)
                        rg_unit(tg, t0, n, sample, hT, c0, 0)
                        s.tick_fn = None
                        while pend_rg:
                            s.add(*pend_rg.pop(0))
                    if tg == 3:
                        for c in range(4):
                            kb.tr(ps[6][0:3, c * 128:(c + 1) * 128], xp[c][:, 512:515], ident, r=["xp%d" % c, "C"], w=[PS[6]], sig=(c == 3))
                        t5 = rgsets[0]["t5"]
                        kb.cp("dve", t5[0:3, :], ps[6][0:3, 0:512], r=[PS[6]], w=["t50"])
                        kb.dma("sp", o_prconv, t5[0:3, :], r=["t50"], w=())
                rr, ii = rgsets[0]["rr"], rgsets[0]["ii"]
                kb.tr(ps[7][0:4, 0:128], hst[:, 0:4], ident, r=["hst0", "hst1", "hst2", "hst3", "C"], w=[PS[7]])
                kb.cp("dve", rr[0:4, 0:128], ps[7][0:4, 0:128], r=[PS[7]], w=["rr0"])
                kb.dma("sp", o_prh, rr[0:4, 0:128], r=["rr0"], w=())
                for c in range(4):
                    kb.tr(ps[4][0:16, c * 128:(c + 1) * 128], hl[:, c, :], ident, r=["hl", "C"], w=[PS[4]], sig=(c == 3))
                kb.cp("dve", ii[0:16, :], ps[4][0:16, :], r=[PS[4]], w=["ii0"])
                kb.dma("sp", o_srh, ii[0:16, :], r=["ii0"], w=())
                s.barrier()
            stop_at(7)
            with contextlib.ExitStack() as pc_:
                alloc_gl(pc_)
                mod_prepare(l, 1, 1.0, blocks=[4, 5])
                Gp, Gs = gl["Gp"], gl["Gs"]
                wout = kb.sb(pc_, "wout", [128, 8, D], BF16)
                kb.dma("pool", wout[:], ab_w_out[0].rearrange("(kc p) n -> p kc n", p=128), r=(), w=["wout"])
                proj_acc(wout, "wout", 8, lambda i, kc: ((hmT[:, kc, i * 128:(i + 1) * 128], "hmT%d" % i) if kc < 4 else
                                                          (hrT[:, kc - 4, i * 128:(i + 1) * 128], "hrT_g%d" % (i // 4))), True, True)
                s.barrier()
        s.barrier()

    CW = -0.6065306597126334
    RT = BF16

    def rwkv_mixer(l):
        rwkv_mixer_(l)
        s.skip = False
        s.barrier()

    def rwkv_mixer_(l):
        rw_mu = kb.din("muT", [128, 48])
        rw_wr = kb.din("rw_wr", [1, D, D])
        rw_wk = kb.din("rw_wk", [1, D, D])
        rw_wv = kb.din("rw_wv", [1, D, D])
        rw_wo = kb.din("rw_wo", [1, D, D])
        rw_w0 = kb.din("rw_w0", [1, D])
        rw_w1 = kb.din("rw_w1", [1, D, 64])
        rw_w2 = kb.din("rw_w2", [1, 64, D])
        rw_a0 = kb.din("rw_a0", [1, D])
        rw_a1 = kb.din("rw_a1", [1, D, 64])
        rw_a2 = kb.din("rw_a2", [1, 64, D])
        rw_g1 = kb.din("rw_g1", [1, D, 128])
        rw_g2 = kb.din("rw_g2", [1, 128, D])
        rw_kk = kb.din("rw_k_k", [1, D])
        rw_ka = kb.din("rw_k_a", [1, D])
        rw_rk = kb.din("rk_flat", [1, D])
        rw_lng = kb.din("rw_lnx_g", [1, D])
        rw_lnb = kb.din("rw_lnx_b", [1, D])
        swkv = kb.din("swkv", [16, 16, 64, 64])
        sshift = kb.din("sshift", [16, D])
        o_pwkv = kb.dout("o_pwkv", [16, 64, 64])
        o_pshift = kb.dout("o_pshift", [1, D])
        o_swkv = kb.dout("o_swkv", [16, 16, 64, 64])
        o_sshift = kb.dout("o_sshift", [16, D])

        cm = lambda nm: C[:, CST_OFF[nm]:CST_OFF[nm] + 128]
        low16, up16, m16, m32, m64 = cm("low16"), cm("up16"), cm("m16"), cm("m32"), cm("m64")
        maskP, maskS, upP, lowP, upS, lowS, blkS, ones, bms = cm("maskP"), cm("maskS"), cm("upP"), cm("lowP"), cm("upS"), cm("lowS"), cm("blkS"), cm("ones"), C[:, CST_OFF["bms"]:CST_OFF["bms"] + 16]

        mod_prepare(l, 1, 1.0, blocks=range(4))
        with contextlib.ExitStack() as ph:
            ygT = kb.sb(ph, "ygT", [128, 8, NTOK], BF16)
            muT = kb.sb(ph, "muT_sb", [128, 48], F32)
            kb.dma("sp", muT[:], rw_mu, r=(), w=["muT"])
            identR = kb.sb(ph, "identR", [128, 128], RT)
            kb.cp("dve", identR[:], ident, r=["C"], w=["identR"])
            w1b = kb.sb(ph, "w1b", [128, 8, 64], BF16)
            a1b = kb.sb(ph, "a1b", [128, 8, 64], BF16)
            g1b = kb.sb(ph, "g1b", [128, 8, 128], BF16)
            kb.dma("pool", w1b[:], rw_w1[0].rearrange("(kc p) n -> p kc n", p=128), r=(), w=["w1b"])
            kb.dma("pool", a1b[:], rw_a1[0].rearrange("(kc p) n -> p kc n", p=128), r=(), w=["a1b"])
            kb.dma("pool", g1b[:], rw_g1[0].rearrange("(kc p) n -> p kc n", p=128), r=(), w=["g1b"])
            w1m = kb.sb(ph, "w1m", [128, 8, 64], BF16)
            a1m = kb.sb(ph, "a1m", [128, 8, 64], BF16)
            g1m = kb.sb(ph, "g1m", [128, 8, 128], BF16)
            for wm_, wb_, rs_, j_, n_ in ((w1m, w1b, "w1b", 1, 64), (a1m, a1b, "a1b", 4, 64), (g1m, g1b, "g1b", 5, 128)):
                kb.tt("dve", wm_[:], wb_[:], muT[:, j_ * 8:(j_ + 1) * 8].unsqueeze(2).to_broadcast([128, 8, n_]), ALU.mult, r=[rs_, "muT"], w=[rs_ + "m"])
            hlast = kb.sb(ph, "hlast", [128, 8, 17], F32)
            sh0T = kb.sb(ph, "sh0T", [128, 8, 16], BF16)
            with contextlib.ExitStack() as p0:
                shs = kb.sb(p0, "shs", [16, D], F32)
                kb.dma("sp", shs[:], sshift, r=(), w=["shs"])
                for c in range(8):
                    kb.tr(ps[0][:, c * 16:(c + 1) * 16], shs[0:16, c * 128:(c + 1) * 128], C[0:16, 0:16], r=["shs", "C"], w=[PS[0]], sig=(c == 7))
                kb.cp("dve", sh0T[:].rearrange("p a b -> p (a b)"), ps[0][:, 0:128], r=[PS[0]], w=["sh0T"])
                s.barrier()

            def hook_last(i, c, src, psres):
                if i == 15:
                    kb.ts("dve", hlast[:, c, 0:1], src[:, 127:128], modT[:, 8 + c, 0:1], modT[:, c, 0:1], ALU.mult, ALU.add, r=["modT", psres], w=["hlast"])
                elif i == 16:
                    v3 = src.rearrange("p (q t) -> p q t", t=8)[:, :, 7:8]
                    kb.tt("dve", hlast[:, c, 1:17].unsqueeze(2), v3, modT[:, 8 + c, 1:NT].unsqueeze(2), ALU.mult, r=["modT", psres], w=["hlast"])
                    kb.tt("dve", hlast[:, c, 1:17], hlast[:, c, 1:17], modT[:, c, 1:NT], ALU.add, r=["hlast", "modT"], w=["hlast"])

            stop_at(51)
            for hg in range(4):
                c0 = hg * 256
                with contextlib.ExitStack() as pp:
                    wrs = kb.sb(pp, "wrs", [128, 8, 256], BF16)
                    wks = kb.sb(pp, "wks", [128, 8, 256], BF16)
                    wvs = kb.sb(pp, "wvs", [128, 8, 256], BF16)
                    for wt, src in ((wrs, rw_wr), (wks, rw_wk), (wvs, rw_wv)):
                        kb.dma("pool", wt[:], src[0].rearrange("(kc p) n -> p kc n", p=128)[:, :, c0:c0 + 256], r=(), w=["wqkv"])
                    wrm = kb.sb(pp, "wrm", [128, 8, 256], BF16)
                    wkm = kb.sb(pp, "wkm", [128, 8, 256], BF16)
                    wvm = kb.sb(pp, "wvm", [128, 8, 256], BF16)
                    for wm_, wt_, j_ in ((wrm, wrs, 0), (wkm, wks, 2), (wvm, wvs, 3)):
                        kb.tt("dve", wm_[:], wt_[:], muT[:, j_ * 8:(j_ + 1) * 8].unsqueeze(2).to_broadcast([128, 8, 256]), ALU.mult, r=["wqkv", "muT"], w=["wqkvm"])
                    w2s = kb.sb(pp, "w2s", [64, 256], BF16)
                    a2s = kb.sb(pp, "a2s", [64, 256], BF16)
                    g2s = kb.sb(pp, "g2s", [128, 256], BF16)
                    kb.dma("pool", w2s[:], rw_w2[0, :, c0:c0 + 256], r=(), w=["w2s"])
                    kb.dma("pool", a2s[:], rw_a2[0, :, c0:c0 + 256], r=(), w=["a2s"])
                    kb.dma("pool", g2s[:], rw_g2[0, :, c0:c0 + 256], r=(), w=["g2s"])
                    w0r = kb.sb(pp, "w0r", [1, 256], F32)
                    a0r = kb.sb(pp, "a0r", [1, 256], F32)
                    kb.dma("sp", w0r[:], rw_w0[0:1, c0:c0 + 256], r=(), w=["w0r"])
                    kb.dma("sp", a0r[:], rw_a0[0:1, c0:c0 + 256], r=(), w=["a0r"])
                    bcs = {}
                    alias = {"kkb": tA[2][:, 0:256], "kab": tA[2][:, 256:512], "rkb": tA[3][:, 0:256], "lngb": tA[3][:, 256:512], "lnbb": tA[0][:, 256:512]}
                    for nm, src in (("kkb", rw_kk), ("kab", rw_ka), ("rkb", rw_rk), ("lngb", rw_lng), ("lnbb", rw_lnb)):
                        bcs[nm] = alias[nm] if nm in alias else kb.sb(pp, nm, [128, 256], F32)
                        kb.dma("sp", bcs[nm][:], src[0:1, c0:c0 + 256].to_broadcast([128, 256]), r=(), w=[nm])
                    hTg = [kb.sb(pp, "hTr%d" % i, [128, 8, 130], BF16) for i in range(2)]
                    dx = kb.sb(pp, "dx", [128, 8, 128], BF16)
                    loT = kb.sb(pp, "loT", [128, 3, 128], BF16)
                    TB = lambda j: tB[j // 4][:, (j % 4) * 256:(j % 4) * 256 + 256]
                    Rr, Kk, KKn, Aa, SG, CSs, Ee, Tt = [TB(j) for j in range(8)]
                    tbres = lambda j: "tB%d" % (j // 4)
                    small = kb.sb(pp, "rwsmall", [128, 64], F32)
                    F3 = lambda nm: kb.sb(pp, nm, [128, 256], RT)
                    Vvs, ALs, RBs, KHs, BHs = [[F3("%s%d" % (nm, k)) for k in range(2)] for nm in ("Vv", "AL", "RB", "KH", "BH")]
                    BT, KT = F3("BTt"), F3("KTt")
                    GVs = [tA[0], kb.sb(pp, "GV1", [128, 256], F32)]
                    PLcs = [kb.sb(pp, "PLc%d" % k, [64, 64], F32) for k in range(2)]
                    smallfs = [kb.sb(pp, "smallf%d" % k, [128, 8], F32) for k in range(2)]
                    Tbk = tA[1][:, 256:512]
                    fmalls = [kb.sb(pp, "fmall%d" % k, [64, 4, 4, 128], RT) for k in range(2)]
                    fmTs = [{nm: fm_[:, j] for j, nm in enumerate(("alT", "btT", "ktT", "rbT"))} for fm_ in fmalls]
                    fmall = fmalls[0]
                    chainall = kb.sb(pp, "chainall", [128, 7, 4, 128], RT)
                    chn = {nm: chainall[:, j] for j, nm in enumerate(("ApA", "ApB", "BpA", "BpB", "TTa", "TTb", "Am"))}
                    S0nat = kb.sb(pp, "S0nat", [64, 16, 64], F32)
                    S0Tq = fmall[:, 0:2].rearrange("p a h t -> p (a h t)").rearrange("p (q k) -> p q k", k=64)
                    SLo = S0nat
                    AakT = kb.sb(pp, "AakT", [128, 4, 128], RT)
                    YTs = AakT[0:64, :, :]
                    ArbT = kb.sb(pp, "ArbT", [128, 4, 128], RT)
                    ArkT = kb.sb(pp, "ArkT", [128, 4, 128], RT)
                    Ahat = kb.sb(pp, "Ahat", [128, 4, 64], RT)
                    X1 = kb.sb(pp, "X1", [128, 4, 64], RT)
                    U0 = kb.sb(pp, "U0", [128, 4, 64], RT)
                    Gm = kb.sb(pp, "Gm", [64, 4, 64], RT)
                    RhT = kb.sb(pp, "RhT", [64, 4, 128], RT)
                    S0T = kb.sb(pp, "S0T", [64, 4, 64], RT)
                    yb = tA[1][:, 0:256]
                    kb.ts("dve", S0T[:].rearrange("p a b -> p (a b)"), C[0:64, 0:256], 0.0, None, ALU.mult, None, r=["C"], w=["S0T0", "S0T1", "S0T2", "S0T3"])
                    kb.memset("pool", hTg[1][:, :, 128:129], 0.0, w=["hTr1"])

                    algb = [4]

                    def nb():
                        b_ = algb[0]
                        algb[0] = 4 + (algb[0] - 3) % 4
                        return b_

                    def grp4(mmf, n_cols, m_rows=128):
                        b_ = nb()
                        for hl in range(4):
                            items = mmf(hl)
                            for j, (lt, rh, rd) in enumerate(items):
                                kb.mm(ps[b_][0:m_rows, hl * n_cols:(hl + 1) * n_cols], lhsT=lt, rhs=rh, start=(j == 0), stop=(j == len(items) - 1),
                                      r=rd, w=[PS[b_]], sig=(hl == 3 and j == len(items) - 1))
                        return b_

                    fsl = lambda t_, hl: t_[:, hl, :]
                    tsl = lambda t_, hl: t_[:, hl * 64:(hl + 1) * 64]

                    def sample_states():
                        BH, KH, Vv, PLc = BHs[0], KHs[0], Vvs[0], PLcs[0]
                        Bhm = chainall[:, 0:2].rearrange("p a h t -> p (a h t)").rearrange("p (q k) -> p q k", k=64)
                        Khm = chainall[:, 2:4].rearrange("p a h t -> p (a h t)").rearrange("p (q k) -> p q k", k=64)
                        Gq = chainall[0:64, 4:6].rearrange("p a h t -> p (a h t)").rearrange("p (q k) -> p q k", k=64)
                        bmq = bms.unsqueeze(2).to_broadcast([128, 16, 64])
                        for hl in range(4):
                            h = hg * 4 + hl
                            kb.tt("dve", Bhm, tsl(BH, hl).unsqueeze(1).to_broadcast([128, 16, 64]), bmq, ALU.mult, r=["BH0", "C"], w=["ApA0", "ApA1", "ApA2", "ApA3"] + ["ApB0", "ApB1", "ApB2", "ApB3"])
                            kb.tt("pool", Khm, tsl(KH, hl).unsqueeze(1).to_broadcast([128, 16, 64]), bmq, ALU.mult, r=["KH0", "C"], w=["BpA0", "BpA1", "BpA2", "BpA3"] + ["BpB0", "BpB1", "BpB2", "BpB3"])
                            kb.dma("sp", S0nat[:], swkv[:, h].rearrange("q v k -> v q k"), r=(), w=["S0nat"])
                            b0, b1 = nb(), nb()
                            for q in range(16):
                                bb = b0 if q < 8 else b1
                                kb.tr(ps[bb][0:64, (q % 8) * 64:(q % 8 + 1) * 64], S0nat[:, q, :], ident[0:64, 0:64], r=["S0nat", "C"], w=[PS[bb]], sig=(q % 8 == 7))
                            kb.cp("act", S0Tq[:, 0:8, :], ps[b0][0:64, :].rearrange("p (q k) -> p q k", k=64), r=[PS[b0]], w=["S0Tq", "alT0", "btT0"])
                            kb.cp("dve", S0Tq[:, 8:16, :], ps[b1][0:64, :].rearrange("p (q k) -> p q k", k=64), r=[PS[b1]], w=["S0Tq", "alT0", "btT0"])
                            b_ = nb()
                            for q in range(16):
                                kb.mm(ps[b_][0:64, q * 8:(q + 1) * 8], lhsT=S0Tq[:, q, :], rhs=RhT[:, hl, q * 8:(q + 1) * 8], start=True, stop=True,
                                      r=["S0Tq", "RhT%d" % hl], w=[PS[b_]], sig=(q == 15))
                            kb.cp("act", YTs[:, hl, :], ps[b_][0:64, 0:128], r=[PS[b_]], w=["AakT%d" % hl])
                            g0, g1 = nb(), nb()
                            for half, bb in ((0, g0), (1, g1)):
                                kb.mm(ps[bb][0:64, :], lhsT=Ahat[:, hl, :], rhs=Bhm[:, half * 8:(half + 1) * 8, :], start=True, stop=True,
                                      r=["Ahat%d" % hl] + ["ApA0", "ApA1", "ApA2", "ApA3"] + ["ApB0", "ApB1", "ApB2", "ApB3"], w=[PS[bb]])
                            for q in range(16):
                                bb = g0 if q < 8 else g1
                                kb.stt(Gq[:, q, :], ident[0:64, 0:64], PLc[:, hl * 16 + q:hl * 16 + q + 1], ps[bb][0:64, (q % 8) * 64:(q % 8 + 1) * 64], ALU.mult, ALU.add,
                                       r=["C", "PLc0", PS[bb]], w=["TTa0", "TTa1", "TTa2", "TTa3"] + ["TTb0", "TTb1", "TTb2", "TTb3"])
                            for half in range(2):
                                bb = nb()
                                kb.mm(ps[bb][0:64, :], lhsT=tsl(Vv, hl), rhs=Khm[:, half * 8:(half + 1) * 8, :], start=True, stop=False, r=["Vv0"] + ["BpA0", "BpA1", "BpA2", "BpA3"] + ["BpB0", "BpB1", "BpB2", "BpB3"], w=[PS[bb]], sig=False)
                                kb.mm(ps[bb][0:64, :], lhsT=U0[:, hl, :], rhs=Bhm[:, half * 8:(half + 1) * 8, :], start=False, stop=False, r=["U0%d" % hl] + ["ApA0", "ApA1", "ApA2", "ApA3"] + ["ApB0", "ApB1", "ApB2", "ApB3"], w=[PS[bb]], sig=False)
                                for qq in range(8):
                                    q = half * 8 + qq
                                    kb.mm(ps[bb][0:64, qq * 64:(qq + 1) * 64], lhsT=S0Tq[:, q, :], rhs=Gq[:, q, :], start=False, stop=(qq == 7),
                                          r=["S0Tq"] + ["TTa0", "TTa1", "TTa2", "TTa3"] + ["TTb0", "TTb1", "TTb2", "TTb3"], w=[PS[bb]], sig=(qq == 7))
                                kb.cp("act" if half else "dve", SLo[:, half * 8:(half + 1) * 8, :], ps[bb][0:64, :].rearrange("p (q k) -> p q k", k=64), r=[PS[bb]], w=["S0nat"])
                            kb.dma("sp", o_swkv[:, h].rearrange("q v k -> v q k"), SLo[:], r=["S0nat"], w=())
                        by = grp4(lambda hl: [(ArkT[:, hl, :], tsl(Vv, hl), ["ArkT%d" % hl, "Vv0"]), (ArbT[:, hl, :], U0[:, hl, :], ["ArbT%d" % hl, "U0%d" % hl]),
                                               (YTs[:, hl, :], identR[0:64, 0:64], ["AakT%d" % hl, "identR"])], 64)
                        kb.cp("act", yb[:], ps[by][:, 0:256], r=[PS[by]], w=["tA1"])

                    def front(i):
                        sample = i == 16
                        p_ = i % 2
                        AL, Vv, RB, KH, BH = ALs[p_], Vvs[p_], RBs[p_], KHs[p_], BHs[p_]
                        fmT, PLc, smallf = fmTs[p_], PLcs[p_], smallfs[p_]
                        Gt = GVs[p_][:, 0:256]
                        rAL, rVv, rRB, rKH, rBH = "AL%d" % p_, "Vv%d" % p_, "RB%d" % p_, "KH%d" % p_, "BH%d" % p_
                        ralT, rbtT, rktT, rrbT = "alT%d" % p_, "btT%d" % p_, "ktT%d" % p_, "rbT%d" % p_
                        rPL, rsf, rGV = "PLc%d" % p_, "smallf%d" % p_, "GV%d" % p_
                        sample = i == 16
                        k_ = i % 2
                        hT = hTg[k_]
                        hres = "hTr%d" % k_
                        last_pass = hg == 3
                        if hg == 0 and i == 1:
                            stop_at(58)
                        if hg == 0 and i == 16:
                            stop_at(59)
                        make_hT(hT, i, prescale=last_pass, col0=1, res=hres, hook=(hook_last if hg == 0 else None))
                        yield
                        if i == 0:
                            kb.memset("pool", hT[:, :, 0:1], 0.0, w=[hres])
                        elif not sample:
                            kb.cp("pool", hT[:, :, 0:1], hTg[1 - k_][:, :, 128:129], r=["hTr%d" % (1 - k_)], w=[hres])
                        cur = hT[:, :, 1:129]
                        if not sample:
                            kb.tt("dve", dx[:], hT[:, :, 0:128], cur, ALU.subtract, r=[hres], w=["dx"])
                        else:
                            kb.cp("pool", dx[:], hT[:, :, 0:128], r=[hres], w=["dx"])
                            kb.cp("pool", dx[:].rearrange("p c (q t) -> p c q t", t=8)[:, :, :, 0], sh0T[:], r=["sh0T"], w=["dx"])
                            kb.tt("pool", dx[:], dx[:], cur, ALU.subtract, r=["dx", hres], w=["dx"])
                        yield
                        for wt, wm_, bank, off in ((wrs, wrm, 0, 0), (wks, wkm, 0, 256), (wvs, wvm, 1, 0)):
                            for kc in range(8):
                                kb.mm(ps[bank][:, off:off + 256], lhsT=hT[:, kc, 1:129], rhs=wt[:, kc, :], start=(kc == 0), stop=False, r=[hres, "wqkv"], w=[PS[bank]], sig=False)
                            for kc in range(8):
                                kb.mm(ps[bank][:, off:off + 256], lhsT=dx[:, kc, :], rhs=wm_[:, kc, :], start=False, stop=(kc == 7), r=["dx", "wqkvm"], w=[PS[bank]])
                        for wt, wm_, wr_, m_, off3 in ((w1b, w1m, "w1b", 64, 0), (a1b, a1m, "a1b", 64, 128), (g1b, g1m, "g1b", 128, 256)):
                            for kc in range(8):
                                kb.mm(ps[3][0:m_, off3:off3 + 128], lhsT=wt[:, kc, :], rhs=hT[:, kc, 1:129], start=(kc == 0), stop=False, r=[hres, wr_], w=[PS[3]], sig=False)
                            for kc in range(8):
                                kb.mm(ps[3][0:m_, off3:off3 + 128], lhsT=wm_[:, kc, :], rhs=dx[:, kc, :], start=False, stop=(kc == 7), r=["dx", wr_ + "m"], w=[PS[3]])
                        kb.act(loT[0:64, 0, :], ps[3][0:64, 0:128], AF.Tanh, r=[PS[3]], w=["loT"])
                        yield
                        kb.cp("act", loT[0:64, 1, :], ps[3][0:64, 128:256], r=[PS[3]], w=["loT"])
                        kb.act(loT[:, 2, :], ps[3][:, 256:384], AF.Sigmoid, r=[PS[3]], w=["loT"])
                        kb.mm(ps[1][:, 256:512], lhsT=loT[0:64, 0, :], rhs=w2s[:], start=True, stop=False, r=["loT", "w2s"], w=[PS[1]], sig=False)
                        yield
                        kb.mm(ps[1][:, 256:512], lhsT=ones[0:1, :], rhs=w0r[:], start=False, stop=True, r=["C", "w0r"], w=[PS[1]])
                        kb.mm(ps[2][:, 0:256], lhsT=loT[0:64, 1, :], rhs=a2s[:], start=True, stop=False, r=["loT", "a2s"], w=[PS[2]], sig=False)
                        kb.mm(ps[2][:, 0:256], lhsT=ones[0:1, :], rhs=a0r[:], start=False, stop=True, r=["C", "a0r"], w=[PS[2]])
                        yield
                        kb.mm(ps[2][:, 256:512], lhsT=loT[:, 2, :], rhs=g2s[:], start=True, stop=True, r=["loT", "g2s"], w=[PS[2]])
                        if hg == 0 and i == 0:
                            stop_at(52)
                        kb.cp("act", Rr, ps[0][:, 0:256], r=[PS[0]], w=[tbres(0)])
                        yield
                        kb.cp("act", Kk, ps[0][:, 256:512], r=[PS[0]], w=[tbres(1)])
                        kb.cp("act", Vv[:], ps[1][:, 0:256], r=[PS[1]], w=[rVv])
                        yield
                        kb.act(SG, ps[1][:, 256:512], AF.Sigmoid, r=[PS[1]], w=[tbres(4)])
                        kb.act(Aa, ps[2][:, 0:256], AF.Sigmoid, r=[PS[2]], w=[tbres(3)])
                        kb.cp("act", Gt, ps[2][:, 256:512], r=[PS[2]], w=[rGV])
                        yield
                        kb.tt("dve", KKn, Kk, bcs["kkb"][:], ALU.mult, r=[tbres(1), "kkb"], w=[tbres(2)])
                        kb.tt("dve", Tt, KKn, KKn, ALU.mult, r=[tbres(2)], w=[tbres(7)])
                        s.add("dve", lambda g_, o_=smallf[:, 0:4], i_=Tt.rearrange("p (h k) -> p h k", k=64): g_.tensor_reduce(out=o_, in_=i_, axis=AX.X, op=ALU.add),
                              r=[tbres(7)], w=[rsf], tag="red")
                        yield
                        kb.ts("dve", smallf[:, 0:4], smallf[:, 0:4], 1e-24, None, ALU.max, None, r=[rsf], w=[rsf])
                        kb.act(smallf[:, 0:4], smallf[:, 0:4], AF.Ln, r=[rsf], w=[rsf])
                        kb.act(smallf[:, 0:4], smallf[:, 0:4], AF.Exp, r=[rsf], w=[rsf], scale=-0.5)
                        yield
                        kb.tt("dve", KKn.rearrange("p (h k) -> p h k", k=64), KKn.rearrange("p (h k) -> p h k", k=64),
                              smallf[:, 0:4].unsqueeze(2).to_broadcast([128, 4, 64]), ALU.mult, r=[tbres(2), rsf], w=[tbres(2)])
                        kb.stt(Tt, Aa, -1.0, bcs["kab"][:], ALU.add, ALU.mult, r=[tbres(3), "kab"], w=[tbres(7)])
                        kb.tt("dve", Tt, Tt, Kk, ALU.mult, r=[tbres(7), tbres(1)], w=[tbres(7)])
                        yield
                        kb.tt("dve", Kk, Kk, Tt, ALU.add, r=[tbres(1), tbres(7)], w=[tbres(1)])
                        kb.tt("dve", Tt, Rr, Kk, ALU.mult, r=[tbres(0), tbres(1)], w=[tbres(7)])
                        kb.tt("dve", Tt, Tt, bcs["rkb"][:], ALU.mult, r=[tbres(7), "rkb"], w=[tbres(7)])
                        yield
                        s.add("dve", lambda g_, o_=smallf[:, 4:8], i_=Tt.rearrange("p (h k) -> p h k", k=64): g_.tensor_reduce(out=o_, in_=i_, axis=AX.X, op=ALU.add),
                              r=[tbres(7)], w=[rsf], tag="red")
                        kb.tt("dve", Aa, KKn, Aa, ALU.mult, r=[tbres(2), tbres(3)], w=[tbres(3)])
                        if hg == 0 and i == 0:
                            stop_at(53)
                        yield
                        Um, Jm = (maskP, ones) if not sample else (maskS, blkS)
                        kb.mm(ps[0][:, 0:256], lhsT=Um, rhs=SG, start=True, stop=True, r=["C", tbres(4)], w=[PS[0]])
                        kb.mm(ps[0][:, 256:512], lhsT=Jm, rhs=SG, start=True, stop=True, r=["C", tbres(4)], w=[PS[0]])
                        yield
                        nq = 1 if not sample else 16
                        for hl in range(4):
                            kb.mm(ps[3][0:64, 384 + hl * nq:384 + (hl + 1) * nq], lhsT=SG[:, hl * 64:(hl + 1) * 64], rhs=(ones[:, 0:1] if not sample else bms),
                                  start=True, stop=True, r=[tbres(4), "C"], w=[PS[3]], sig=(hl == 3))
                        kb.act(PLc[:, 0:4 * nq], ps[3][0:64, 384:384 + 4 * nq], AF.Exp, r=[PS[3]], w=[rPL], scale=CW)
                        yield
                        kb.cp("act", CSs, ps[0][:, 0:256], r=[PS[0]], w=[tbres(5)])
                        kb.tt("dve", Tt, CSs, SG, ALU.subtract, r=[tbres(5), tbres(4)], w=[tbres(7)])
                        kb.act(Ee, Tt, AF.Exp, r=[tbres(7)], w=[tbres(6)], scale=CW)
                        yield
                        kb.stt(AL[:], KKn, -1.0, Ee, ALU.mult, ALU.mult, r=[tbres(2), tbres(6)], w=[rAL])
                        kb.act(Ee, CSs, AF.Exp, r=[tbres(5), rAL], w=[tbres(6)], scale=-CW)
                        kb.tt("dve", BT[:], Aa, Ee, ALU.mult, r=[tbres(3), tbres(6)], w=["BTt"])
                        yield
                        kb.tt("dve", KT[:], Kk, Ee, ALU.mult, r=[tbres(1), tbres(6)], w=["X1kt"])
                        kb.act(Tt, CSs, AF.Exp, r=[tbres(5)], w=[tbres(7)], scale=CW)
                        kb.tt("dve", RB[:], Rr, Tt, ALU.mult, r=[tbres(0), tbres(7)], w=[rRB])
                        yield
                        kb.tt("dve", Ee, ps[0][:, 256:512], CSs, ALU.subtract, r=[PS[0], tbres(5), rGV, "X1kt"], w=[tbres(6)])
                        kb.act(Ee, Ee, AF.Exp, r=[tbres(6)], w=[tbres(6)], scale=CW)
                        kb.tt("dve", KH[:], Kk, Ee, ALU.mult, r=[tbres(1), tbres(6)], w=[rKH])
                        yield
                        kb.tt("dve", BH[:], Aa, Ee, ALU.mult, r=[tbres(3), tbres(6)], w=[rBH])
                        for qi, (nm, src, sr) in enumerate((("alT", AL, rAL), ("btT", BT, "BTt"), ("ktT", KT, "X1kt"), ("rbT", RB, rRB))):
                            b_ = 1 + qi % 2
                            for hl in range(4):
                                kb.tr(psb[b_][0:64, hl * 128:(hl + 1) * 128], src[:, hl * 64:(hl + 1) * 64], identR[:],
                                      r=[sr, "identR"], w=[PS[b_]], sig=(hl == 3))
                            kb.cp("act" if qi % 2 else "dve", fmT[nm][:].rearrange("p a b -> p (a b)"), psb[b_][0:64, 0:512], r=[PS[b_]], w=["%s%d" % (nm, p_)])
                        if hg == 0 and i == 0:
                            stop_at(54)

                    def back(i, fg):
                        sample = i == 16
                        p_ = i % 2
                        AL, Vv, RB, KH, BH = ALs[p_], Vvs[p_], RBs[p_], KHs[p_], BHs[p_]
                        fmT, PLc, smallf = fmTs[p_], PLcs[p_], smallfs[p_]
                        Gt = GVs[p_][:, 0:256]
                        rAL, rVv, rRB, rKH, rBH = "AL%d" % p_, "Vv%d" % p_, "RB%d" % p_, "KH%d" % p_, "BH%d" % p_
                        ralT, rbtT, rktT, rrbT = "alT%d" % p_, "btT%d" % p_, "ktT%d" % p_, "rbT%d" % p_
                        rPL, rsf, rGV = "PLc%d" % p_, "smallf%d" % p_, "GV%d" % p_
                        alT, btT, ktT, rbT = fmT["alT"], fmT["btT"], fmT["ktT"], fmT["rbT"]
                        mlow, mup, minc = (lowP, upP, maskP) if not sample else (lowS, upS, maskS)
                        n_it = 3 if not sample else 2

                        def head_alg(hl):
                            B_ = 4 + hl
                            P_ = PS[B_]
                            rn = lambda nm: "%s%d" % (nm, hl)
                            pw = ps[B_][:, 0:128]

                            def mm1(lt, rh, rd, cols=128, rows=128, first=True, last=True):
                                kb.mm(ps[B_][0:rows, 0:cols], lhsT=lt, rhs=rh, start=first, stop=last, r=rd, w=[P_], sig=last)

                            mm1(fsl(alT, hl), fsl(btT, hl), [ralT, rbtT])
                            if not sample:
                                kb.tt("dve", chn["Am"][:, hl, :], pw, lowP, ALU.mult, r=[P_, "C"], w=[rn("Am")])
                                kb.tt("dve", chn["ApA"][:, hl, :], pw, low16, ALU.mult, r=[P_, "C"], w=[rn("ApA")])
                            else:
                                kb.tt("dve", chn["ApA"][:, hl, :], pw, lowS, ALU.mult, r=[P_, "C"], w=[rn("ApA")])
                            yield
                            mm1(fsl(btT, hl), fsl(alT, hl), [ralT, rbtT])
                            kb.tt("dve", chn["BpA"][:, hl, :], pw, (up16 if not sample else upS), ALU.mult, r=[P_, "C"], w=[rn("BpA")])
                            yield
                            mm1(fsl(ktT, hl), fsl(alT, hl), [ralT, rktT])
                            kb.tt("dve", AakT[:, hl, :], pw, mup, ALU.mult, r=[P_, "C"], w=[rn("AakT")])
                            yield
                            mm1(fsl(btT, hl), fsl(rbT, hl), [rrbT, rbtT])
                            kb.tt("dve", ArbT[:, hl, :], pw, minc, ALU.mult, r=[P_, "C"], w=[rn("ArbT")])
                            yield
                            mm1(fsl(ktT, hl), fsl(rbT, hl), [rrbT, rktT])
                            kb.tt("dve", ArkT[:, hl, :], pw, minc, ALU.mult, r=[P_, "C"], w=[rn("ArkT")])
                            yield
                            kb.tt("dve", chn["TTa"][:, hl, :], chn["BpA"][:, hl, :], ident, ALU.add, r=[rn("BpA"), "C"], w=[rn("TTa")])
                            Ap, Bp, TT = "ApA", "BpA", "TTa"
                            for it in range(n_it):
                                Ap2 = "ApB" if Ap == "ApA" else "ApA"
                                Bp2 = "BpB" if Bp == "BpA" else "BpA"
                                TT2 = "TTb" if TT == "TTa" else "TTa"
                                mm1(chn[Bp][:, hl, :], chn[Ap][:, hl, :], [rn(Ap), rn(Bp)])
                                kb.cp("act", chn[Ap2][:, hl, :], pw, r=[P_], w=[rn(Ap2)])
                                yield
                                if it < n_it - 1:
                                    mm1(chn[Ap][:, hl, :], chn[Bp][:, hl, :], [rn(Ap), rn(Bp)])
                                    kb.cp("act", chn[Bp2][:, hl, :], pw, r=[P_], w=[rn(Bp2)])
                                    yield
                                mm1(chn[Ap2][:, hl, :], chn[TT][:, hl, :], [rn(Ap2), rn(TT)])
                                kb.tt("dve", chn[TT2][:, hl, :], pw, chn[TT][:, hl, :], ALU.add, r=[P_, rn(TT)], w=[rn(TT2)])
                                yield
                                Ap, Bp, TT = Ap2, Bp2, TT2
                            if not sample:
                                for mk in (m16, m32, m64):
                                    TT2 = "TTb" if TT == "TTa" else "TTa"
                                    kb.tr(psb[B_][:, 0:128], chn[TT][:, hl, :], identR[:], r=[rn(TT), "identR"], w=[P_])
                                    kb.cp("act", chn["ApB"][:, hl, :], psb[B_][:, 0:128], r=[P_], w=[rn("ApB")])
                                    kb.tt("dve", chn["ApA"][:, hl, :], chn["Am"][:, hl, :], mk, ALU.mult, r=[rn("Am"), "C"], w=[rn("ApA")])
                                    yield
                                    mm1(chn["ApA"][:, hl, :], chn[TT][:, hl, :], [rn("ApA"), rn(TT)])
                                    kb.cp("act", chn["BpA"][:, hl, :], pw, r=[P_], w=[rn("BpA")])
                                    yield
                                    mm1(chn["ApB"][:, hl, :], chn["BpA"][:, hl, :], [rn("ApB"), rn("BpA")])
                                    kb.tt("dve", chn[TT2][:, hl, :], pw, chn[TT][:, hl, :], ALU.add, r=[P_, rn(TT)], w=[rn(TT2)])
                                    yield
                                    TT = TT2
                            TTh = chn[TT][:, hl, :]
                            mm1(TTh, tsl(AL, hl), [rn(TT), rAL], cols=64)
                            kb.cp("act", Ahat[:, hl, :], ps[B_][:, 0:64], r=[P_], w=[rn("Ahat")])
                            yield
                            mm1(AakT[:, hl, :], tsl(Vv, hl), [rn("AakT"), rVv], cols=64)
                            kb.cp("dve", X1[:, hl, :], ps[B_][:, 0:64], r=[P_], w=[rn("X1")])
                            yield
                            mm1(TTh, X1[:, hl, :], [rn(TT), rn("X1")], cols=64)
                            kb.cp("act", U0[:, hl, :], ps[B_][:, 0:64], r=[P_], w=[rn("U0")])
                            yield
                            mm1(tsl(RB, hl), identR[:], [rRB, "identR"], rows=64, last=False)
                            mm1(Ahat[:, hl, :], ArbT[:, hl, :], [rn("Ahat"), rn("ArbT")], rows=64, first=False)
                            kb.cp("dve", RhT[:, hl, :], ps[B_][0:64, 0:128], r=[P_], w=[rn("RhT")])
                            yield
                            if sample:
                                return
                            mm1(Ahat[:, hl, :], tsl(BH, hl), [rn("Ahat"), rBH], cols=64, rows=64)
                            kb.stt(Gm[:, hl, :], ident[0:64, 0:64], PLc[:, hl:hl + 1], ps[B_][0:64, 0:64], ALU.mult, ALU.add, r=["C", rPL, P_], w=[rn("Gm")])
                            yield
                            mm1(ArkT[:, hl, :], tsl(Vv, hl), [rn("ArkT"), rVv], cols=64, last=False)
                            mm1(ArbT[:, hl, :], U0[:, hl, :], [rn("ArbT"), rn("U0")], cols=64, first=False, last=False)
                            mm1(RhT[:, hl, :], S0T[:, hl, :], [rn("RhT"), rn("S0T")], cols=64, first=False)
                            kb.cp("act", yb[:, hl * 64:(hl + 1) * 64], ps[B_][:, 0:64], r=[P_], w=["tA1"])
                            yield
                            if i == 15:
                                mm1(tsl(Vv, hl), tsl(KH, hl), [rVv, rKH], cols=64, rows=64, last=False)
                                mm1(U0[:, hl, :], tsl(BH, hl), [rn("U0"), rBH], cols=64, rows=64, first=False, last=False)
                                mm1(S0T[:, hl, :], Gm[:, hl, :], [rn("S0T"), rn("Gm")], cols=64, rows=64, first=False)
                                kb.cp("dve", SLo[:, hl, :], ps[B_][0:64, 0:64], r=[P_], w=["S0nat"])
                                yield
                            mm1(tsl(KH, hl), tsl(Vv, hl), [rVv, rKH], cols=64, rows=64, last=False)
                            mm1(tsl(BH, hl), U0[:, hl, :], [rn("U0"), rBH], cols=64, rows=64, first=False, last=False)
                            mm1(Gm[:, hl, :], S0T[:, hl, :], [rn("S0T"), rn("Gm")], cols=64, rows=64, first=False)
                            kb.cp("dve", S0T[:, hl, :], ps[B_][0:64, 0:64], r=[P_], w=[rn("S0T")])
                            yield

                        gens = [head_alg(hl) for hl in range(4)] + ([fg] if fg is not None else [])
                        while gens:
                            for g_ in list(gens):
                                try:
                                    next(g_)
                                except StopIteration:
                                    gens.remove(g_)
                        if not sample:
                            if i == 15:
                                kb.dma("sp", o_pwkv[hg * 4:hg * 4 + 4].rearrange("h v k -> v h k"), SLo[:, 0:4, :], r=["S0nat"], w=())
                        else:
                            sample_states()
                        if hg == 0 and i == 0:
                            stop_at(57)
                        if hg == 0 and i == 16:
                            stop_at(60)
                        for hl in range(4):
                            s.add("dve", lambda g_, o_=small[:, 8 + hl * 6:14 + hl * 6], i_=yb[:, hl * 64:(hl + 1) * 64]: g_.bn_stats(out=o_, in_=i_), r=["tA1"], w=["tA1"], tag="bnst")
                        for hl in range(4):
                            s.add("dve", lambda g_, o_=small[:, 32 + hl * 2:34 + hl * 2], i_=small[:, 8 + hl * 6:14 + hl * 6]: g_.bn_aggr(out=o_, in_=i_), r=["tA1"], w=["tA1"], tag="bnag")
                        mvv = small[:, 32:40].rearrange("p (h two) -> p h two", two=2)
                        kb.act(small[:, 40:44].unsqueeze(2), mvv[:, :, 1:2], AF.Ln, r=["tA1"], w=["tA1"], bias=64e-5, scale=1.0)
                        kb.act(small[:, 40:44], small[:, 40:44], AF.Exp, r=["tA1"], w=["tA1"], scale=-0.5)
                        kb.tt("dve", small[:, 44:48].unsqueeze(2), mvv[:, :, 0:1], small[:, 40:44].unsqueeze(2), ALU.mult, r=["tA1"], w=["tA1"])
                        kb.ts("dve", small[:, 44:48], small[:, 44:48], -1.0, None, ALU.mult, None, r=["tA1"], w=["tA1"])
                        for hl in range(4):
                            kb.act(yb[:, hl * 64:(hl + 1) * 64], yb[:, hl * 64:(hl + 1) * 64], AF.Identity, r=["tA1", "tA1"], w=["tA1"],
                                   bias=small[:, 44 + hl:45 + hl], scale=small[:, 40 + hl:41 + hl])
                        kb.tt("dve", yb[:], yb[:], bcs["lngb"][:], ALU.mult, r=["tA1", "lngb"], w=["tA1"])
                        kb.tt("dve", yb[:], yb[:], bcs["lnbb"][:], ALU.add, r=["tA1", "lnbb"], w=["tA1"])
                        kb.tt("dve", Tbk[:].rearrange("p (h k) -> p h k", k=64), Vv[:].rearrange("p (h k) -> p h k", k=64),
                              smallf[:, 4:8].unsqueeze(2).to_broadcast([128, 4, 64]), ALU.mult, r=[rVv, rsf], w=["Tbk"])
                        kb.tt("dve", yb[:], yb[:], Tbk[:], ALU.add, r=["tA1", "Tbk"], w=["tA1"])
                        kb.tt("dve", yb[:], yb[:], Gt, ALU.mult, r=["tA1", rGV], w=["tA1"])
                        for cc in range(2):
                            kb.tr(ps[7][:, cc * 128:(cc + 1) * 128], yb[:, cc * 128:(cc + 1) * 128], ident, r=["tA1", "C"], w=[PS[7]], sig=(cc == 1))
                        kb.cp("act", ygT[:, 2 * hg:2 * hg + 2, i * 128:(i + 1) * 128], ps[7][:, 0:256].rearrange("p (c t) -> p c t", t=128), r=[PS[7]], w=["ygT%d" % i])
                    def run_gen(g_):
                        for _ in g_:
                            pass

                    run_gen(front(0))
                    for i in range(NT):
                        back(i, front(i + 1) if i + 1 < NT else None)
                    s.barrier()
            for c in range(8):
                kb.tr(ps[0][0:17, c * 128:(c + 1) * 128] if c < 4 else ps[1][0:17, (c - 4) * 128:(c - 3) * 128], hlast[:, c, :], ident, r=["hlast", "C"],
                      w=[PS[0] if c < 4 else PS[1]], sig=(c in (3, 7)))
            kb.cp("dve", tB[0][0:17, 0:512], ps[0][0:17, :], r=[PS[0]], w=["tB0"])
            kb.cp("dve", tB[0][0:17, 512:1024], ps[1][0:17, :], r=[PS[1]], w=["tB0"])
            kb.dma("sp", o_pshift, tB[0][0:1, :], r=["tB0"], w=())
            kb.dma("sp", o_sshift, tB[0][1:17, :], r=["tB0"], w=())
            s.barrier()
            with contextlib.ExitStack() as pc_:
                alloc_gl(pc_)
                mod_prepare(l, 1, 1.0, blocks=[4, 5])
                Gp, Gs = gl["Gp"], gl["Gs"]
                wout = kb.sb(pc_, "wo_sb", [128, 8, D], BF16)
                kb.dma("pool", wout[:], rw_wo[0].rearrange("(kc p) n -> p kc n", p=128), r=(), w=["wout"])
                proj_acc(wout, "wout", 8, lambda i, kc: (ygT[:, kc, i * 128:(i + 1) * 128], "ygT%d" % i), True, True)
                s.barrier()
        s.barrier()

    stored = set()

    def store_tile(i):
        stored.add(i)
        kb.dma("sp", yout[i * 128:(i + 1) * 128, :], X[:, i, :], r=["X%d" % i], w=())

    def dump_and_finish():
        for i in range(NT):
            if i not in stored:
                kb.dma("sp", yout[i * 128:(i + 1) * 128, :], X[:, i, :], r=["X%d" % i], w=())

    stage = 0
    for l in range(2):
        for sub in range(3):
            if sub == 0:
                ffn(l, 0, 0, 0.5)
            elif sub == 2:
                ffn(l, 1, 2, 0.5)
            elif l == 0:
                ab_mixer(l)
            else:
                rwkv_mixer(l)
            stage += 1
            if stage >= upto:
                return dump_and_finish()
    dump_and_finish()


def build(upto=99, stop_point=None):
    kb = KB()
    kb.stop_point = stop_point
    with contextlib.ExitStack() as es:
        kb.es = es
        build_program(kb, upto)
        kb.s.emit(kb.nc, es)
    return kb


def core_inputs(inp, core, kb):
    sl = slice(16 * core, 16 * core + 16)
    xs = inp["x_sample"][sl].reshape(128, D)
    f32 = np.float32

    def fm(v):
        return np.ascontiguousarray(np.asarray(v, f32).reshape(-1, 128).T)

    m = {
        "x": np.concatenate([inp["x_prompt"][core], xs], axis=0),
        "c": np.concatenate([inp["c_prompt"][core:core + 1], inp["c_sample"][sl]], axis=0),
        "cst": CST_ARR,
    }
    cw = inp["rg_conv_w"][0]
    vecA = np.zeros((128, 32), f32)
    for c in range(4):
        for j in range(4):
            vecA[:, c * 4 + j] = cw[j, c * 128:(c + 1) * 128]
    vecA[:, 16:20] = fm(inp["rg_conv_b"][0])
    vecA[:, 20:24] = fm(inp["rg_b_a"][0])
    vecA[:, 24:28] = fm(inp["rg_b_x"][0])
    vecA[:, 28:32] = fm(inp["rg_lambda"][0])
    m["vecA"] = vecA
    m["bgT"] = np.ascontiguousarray(inp["mlstm_b_gates"][0].T)
    m["minitT"] = np.ascontiguousarray(inp["state_mlstm_m"][0, sl].T)
    m["smC"] = inp["state_mlstm_C"][0, sl]
    m["smn"] = inp["state_mlstm_n"][0, sl]
    m["srh"] = inp["state_rglru_h"][0, sl]
    m["srconv"] = inp["state_rglru_conv"][0, sl].reshape(48, 512)
    mu = inp["rw_mu"][0]
    muT = np.zeros((128, 48), f32)
    for j in range(6):
        muT[:, j * 8:(j + 1) * 8] = fm(mu[j])
    m["muT"] = muT
    m["rk_flat"] = inp["rw_r_k"].reshape(1, D)
    m["swkv"] = inp["state_rwkv_wkv"][0, sl]
    m["sshift"] = inp["state_rwkv_shift"][0, sl]
    for k in kb.dram:
        if k not in m and k in inp:
            m[k] = inp[k]
    return {k: np.ascontiguousarray(v, dtype=f32) for k, v in m.items() if k in kb.dram}


_CACHE = {}


def kernel(**inputs):
    inp = {k: np.asarray(v) for k, v in inputs.items()}
    if "kb" not in _CACHE:
        _CACHE["kb"] = build()
    kb = _CACHE["kb"]
    in_maps = [core_inputs(inp, c, kb) for c in range(NCORES)]
    res = run_bass_kernel_spmd(kb.nc, in_maps, core_ids=list(range(NCORES)))
    R = res.results
    f32 = np.float32
    cat = lambda key, f=(lambda a: a): np.stack([f(np.asarray(R[c][key], f32)) for c in range(NCORES)], axis=0)
    cats = lambda key, f=(lambda a: a): np.concatenate([f(np.asarray(R[c][key], f32)) for c in range(NCORES)], axis=0)
    y_prompt = cat("y", lambda a: a[:2048])
    y_sample = cats("y", lambda a: a[2048:].reshape(16, 8, D))
    outs = (
        y_prompt, y_sample,
        cat("o_pmC")[None], cat("o_pmn")[None], cat("o_pmm", lambda a: a[:, 0])[None],
        cat("o_prh", lambda a: a.reshape(512))[None], cat("o_prconv")[None],
        cat("o_pwkv")[None], cat("o_pshift", lambda a: a[0])[None],
        cats("o_smC")[None], cats("o_smn")[None], cats("o_smm", lambda a: a.T)[None],
        cats("o_srh")[None], cats("o_srconv", lambda a: a.reshape(16, 3, 512))[None],
        cats("o_swkv")[None], cats("o_sshift")[None],
    )
    return tuple(np.ascontiguousarray(o, dtype=f32) for o in outs)
```

```python
import contextlib
import numpy as np
import concourse.bass as bass
import concourse.mybir as mybir
from concourse.bass_utils import run_bass_kernel_spmd

F32 = mybir.dt.float32
BF16 = mybir.dt.bfloat16
F32R = mybir.dt.float32r
AF = mybir.ActivationFunctionType
ALU = mybir.AluOpType
AX = mybir.AxisListType

D = 1024
DFF = 2816
NT = 17
NTOK = NT * 128
ALPHA = 4.0 ** 0.25
LN_EPS = 1e-5
NCORES = 8


class Op:
    __slots__ = ("eng", "fn", "deps", "sig", "idx", "dma", "slot", "slot_total", "sigcount", "waits", "tag")


class Sched:
    ENGS = ["pe", "act", "dve", "pool", "sp"]

    def __init__(self, n_slots=40):
        self.q = {e: [] for e in self.ENGS}
        self.last_w = {}
        self.readers = {}
        self.n_slots = n_slots
        self.slot_rr = 0
        self.sw_rr = 0
        self.n_hw = n_slots - 12
        self.slot_total = [0] * n_slots
        self.slot_last = [None] * n_slots
        self.all_dma = []

    skip = False
    capture = None
    tick_fn = None
    _ticking = False

    def add(self, eng, fn, r=(), w=(), sig=True, dma=False, tag=""):
        if self.skip:
            return None
        if self.capture is not None:
            self.capture.append((eng, fn, tuple(r), tuple(w), sig, dma, tag))
            return None
        op = self._add(eng, fn, r, w, sig, dma, tag)
        if self.tick_fn is not None and not self._ticking:
            self._ticking = True
            try:
                self.tick_fn()
            finally:
                self._ticking = False
        return op

    def _add(self, eng, fn, r=(), w=(), sig=True, dma=False, tag=""):
        op = Op()
        op.eng, op.fn, op.sig, op.dma, op.tag = eng, fn, sig, dma, tag
        op.slot = None
        deps = []
        seen = set()

        def dep(o):
            if o is None or id(o) in seen:
                return
            seen.add(id(o))
            if (not dma) and eng == "pe" and o.eng == "pe" and not o.dma:
                return
            deps.append(o)

        for k in r:
            dep(self.last_w.get(k))
        for k in w:
            dep(self.last_w.get(k))
            for o in self.readers.get(k, {}).values():
                dep(o)
        if dma:
            if eng == "pool":
                slot = self.n_hw + (self.sw_rr % (self.n_slots - self.n_hw))
                self.sw_rr += 1
            else:
                slot = self.slot_rr % self.n_hw
                self.slot_rr += 1
            dep(self.slot_last[slot])
            self.slot_total[slot] += 16
            op.slot = slot
            op.slot_total = self.slot_total[slot]
            self.slot_last[slot] = op
            self.all_dma.append(op)
        op.deps = deps
        self.q[eng].append(op)
        op.idx = len(self.q[eng]) - 1
        key = ("dma", id(op)) if dma else eng
        for k in r:
            self.readers.setdefault(k, {})[key] = op
        for k in w:
            self.last_w[k] = op
            self.readers[k] = {}
        return op

    def barrier(self):
        if self.skip:
            return
        lasts = []
        for e in self.ENGS:
            comp = [o for o in self.q[e] if (not o.dma) and o.fn is not None]
            if comp:
                comp[-1].sig = True
                lasts.append(comp[-1])
        lasts += [o for o in self.slot_last if o is not None]
        for e in self.ENGS:
            op = Op()
            op.eng, op.sig, op.dma, op.tag, op.slot, op.fn = e, False, False, "barrier", None, None
            op.deps = list(lasts)
            self.q[e].append(op)
            op.idx = len(self.q[e]) - 1
        self.last_w = {}
        self.readers = {}

    def finalize(self):
        for e in self.ENGS:
            for o in reversed(self.q[e]):
                if not o.dma and o.fn is not None and e != "sp":
                    o.sig = True
                    break
        self.sigtot = {}
        for e in self.ENGS:
            cnt = 0
            ops = self.q[e]
            pref = []
            for o in ops:
                if (not o.dma) and o.sig:
                    cnt += 1
                pref.append(cnt)
            self.sigtot[e] = cnt
            nxt = None
            for i in range(len(ops) - 1, -1, -1):
                o = ops[i]
                if (not o.dma) and o.sig:
                    nxt = pref[i]
                o.sigcount = nxt if not o.dma else None
        for e in self.ENGS:
            known = {}
            for o in self.q[e]:
                need = {}
                for d in o.deps:
                    if d.dma:
                        key, val = ("slot", d.slot), d.slot_total
                    else:
                        if d.sigcount is None:
                            raise RuntimeError("dependency on op with no later signal: %s" % d.tag)
                        key, val = ("eng", d.eng), d.sigcount
                    if val > need.get(key, 0):
                        need[key] = val
                o.waits = []
                for key, val in need.items():
                    if known.get(key, 0) < val:
                        known[key] = val
                        o.waits.append((key, val))

    def simulate(self):
        pc = {e: 0 for e in self.ENGS}
        sem = {}
        sigc = {e: 0 for e in self.ENGS}
        progress = True
        while progress:
            progress = False
            for e in self.ENGS:
                while pc[e] < len(self.q[e]):
                    o = self.q[e][pc[e]]
                    ok = all(sem.get(k, 0) >= v for k, v in o.waits)
                    if not ok:
                        break
                    if o.dma:
                        sem[("slot", o.slot)] = sem.get(("slot", o.slot), 0) + 16
                    elif o.sig:
                        sem[("eng", e)] = sem.get(("eng", e), 0) + 1
                    pc[e] += 1
                    progress = True
        stuck = {e: (pc[e], len(self.q[e])) for e in self.ENGS if pc[e] < len(self.q[e])}
        if stuck:
            msg = []
            for e, (p, n) in stuck.items():
                o = self.q[e][p]
                msg.append("%s stuck at %d/%d tag=%s waits=%s" % (e, p, n, o.tag, [(k, v, sem.get(k, 0)) for k, v in o.waits]))
            raise RuntimeError("DEADLOCK in wait graph:\n" + "\n".join(msg))

    def emit(self, nc, es):
        self.finalize()
        self.simulate()
        engsem = {e: es.enter_context(nc.semaphore("sem_" + e)) for e in ["pe", "act", "dve", "pool"]}
        slotsem = [es.enter_context(nc.semaphore("slot%d" % i)) for i in range(self.n_slots)]

        def semof(key):
            return engsem[key[1]] if key[0] == "eng" else slotsem[key[1]]

        def run(e, g):
            for o in self.q[e]:
                for key, val in o.waits:
                    g.wait_ge(semof(key), val)
                if o.fn is None:
                    continue
                ins = o.fn(g)
                if o.dma:
                    ins.then_inc(slotsem[o.slot], 16)
                elif o.sig:
                    ins.then_inc(engsem[e], 1)
            if e == "sp":
                for s in range(self.n_slots):
                    if self.slot_total[s] > 0:
                        g.wait_ge(slotsem[s], self.slot_total[s])

        with nc.Block() as blk:
            blk.tensor(lambda g: run("pe", g))
            blk.scalar(lambda g: run("act", g))
            blk.vector(lambda g: run("dve", g))
            blk.gpsimd(lambda g: run("pool", g))
            blk.sync(lambda g: run("sp", g))


class KB:
    def __init__(self, stop_after=None, debug=False):
        self.nc = bass.Bass("TRN2", target_bir_lowering=False)
        self.s = Sched()
        self.stop_after = stop_after
        self.debug = debug
        self.dram = {}

    def din(self, name, shape, dt=F32):
        t = self.nc.dram_tensor(name, list(shape), dt, kind="ExternalInput")
        self.dram[name] = t
        return t.ap()

    def dout(self, name, shape, dt=F32):
        t = self.nc.dram_tensor(name, list(shape), dt, kind="ExternalOutput")
        self.dram[name] = t
        return t.ap()

    def sb(self, es, name, shape, dt=F32):
        self.uid = getattr(self, "uid", 0) + 1
        return es.enter_context(self.nc.sbuf_tensor("%s_%d" % (name, self.uid), list(shape), dt))

    def mm(self, out, lhsT, rhs, start, stop, r, w, sig=None, tag="mm"):
        if sig is None:
            sig = stop
        return self.s.add("pe", lambda g: g.matmul(out, lhsT=lhsT, rhs=rhs, start=start, stop=stop), r=r, w=w, sig=sig, tag=tag)

    def tr(self, out, in_, ident, r, w, sig=True, tag="tr"):
        return self.s.add("pe", lambda g: g.transpose(out, in_, ident), r=r, w=w, sig=sig, tag=tag)

    def act(self, out, in_, func, r, w, bias=None, scale=None, eng="act", tag="act"):
        kw = {}
        if bias is not None:
            kw["bias"] = bias
        if scale is not None:
            kw["scale"] = scale
        return self.s.add("act", lambda g: g.activation(out=out, in_=in_, func=func, **kw), r=r, w=w, tag=tag)

    def tt(self, eng, out, in0, in1, op, r, w, tag="tt"):
        return self.s.add(eng, lambda g: g.tensor_tensor(out=out, in0=in0, in1=in1, op=op), r=r, w=w, tag=tag)

    def ts(self, eng, out, in0, s1, s2, op0, op1, r, w, tag="ts"):
        if op1 is None:
            return self.s.add(eng, lambda g: g.tensor_scalar(out=out, in0=in0, scalar1=s1, scalar2=None, op0=op0), r=r, w=w, tag=tag)
        return self.s.add(eng, lambda g: g.tensor_scalar(out=out, in0=in0, scalar1=s1, scalar2=s2, op0=op0, op1=op1), r=r, w=w, tag=tag)

    def stt(self, out, in0, scalar, in1, op0, op1, r, w, tag="stt"):
        return self.s.add("dve", lambda g: g.scalar_tensor_tensor(out=out, in0=in0, scalar=scalar, in1=in1, op0=op0, op1=op1), r=r, w=w, tag=tag)

    def cp(self, eng, out, in_, r, w, tag="cp"):
        if eng == "act":
            return self.s.add("act", lambda g: g.copy(out=out, in_=in_), r=r, w=w, tag=tag)
        return self.s.add(eng, lambda g: g.tensor_copy(out=out, in_=in_), r=r, w=w, tag=tag)

    def memset(self, eng, ap, val, w, tag="memset"):
        return self.s.add(eng, lambda g: g.memset(ap, val), r=(), w=w, tag=tag)

    def dma(self, q, out, in_, r, w, tag="dma", **kw):
        return self.s.add(q, lambda g: g.dma_start(out=out, in_=in_, **kw), r=r, w=w, dma=True, tag=tag)


def make_consts():
    c = {}
    c["ident"] = np.eye(128, dtype=np.float32)
    selP = np.zeros((128, 128), np.float32)
    selP[0, :] = 1.0
    selS = np.zeros((128, 128), np.float32)
    for p in range(128):
        selS[1 + p // 8, p] = 1.0
    c["selP"] = selP
    c["selS"] = selS
    st = np.arange(128)
    c["maskP"] = (st[:, None] <= st[None, :]).astype(np.float32)
    c["maskS"] = ((st[:, None] <= st[None, :]) & (st[:, None] // 8 == st[None, :] // 8)).astype(np.float32)
    c["rst"] = np.tile((st % 8 != 0).astype(np.float32)[None, :], (128, 1))
    c["rstm"] = np.tile(np.where(st % 8 == 0, -1e30, 0.0).astype(np.float32)[None, :], (128, 1))
    bms = np.zeros((128, 128), np.float32)
    bms[st, st // 8] = 1.0
    c["bms"] = bms
    c["ones"] = np.ones((128, 128), np.float32)
    same = (st[:, None] // 8 == st[None, :] // 8)
    c["upP"] = (st[:, None] < st[None, :]).astype(np.float32)
    c["lowP"] = (st[:, None] > st[None, :]).astype(np.float32)
    c["upS"] = ((st[:, None] < st[None, :]) & same).astype(np.float32)
    c["lowS"] = ((st[:, None] > st[None, :]) & same).astype(np.float32)
    c["blkS"] = same.astype(np.float32)
    blk = lambda b: (st[:, None] // b == st[None, :] // b)
    c["low16"] = ((st[:, None] > st[None, :]) & blk(16)).astype(np.float32)
    c["up16"] = ((st[:, None] < st[None, :]) & blk(16)).astype(np.float32)
    for b in (16, 32, 64):
        c["m%d" % b] = (blk(2 * b) & ~blk(b)).astype(np.float32)
    names = list(c.keys())
    arr = np.concatenate([c[k] for k in names], axis=1)
    offs = {}
    o = 0
    for k in names:
        offs[k] = o
        o += c[k].shape[1]
    return arr, offs


CST_ARR, CST_OFF = make_consts()
NCST = CST_ARR.shape[1]

FFN_PARTS = [(0, 4), (4, 4), (8, 4), (12, 4), (16, 4), (20, 2)]
TGS = [(0, 512), (512, 512), (1024, 512), (1536, 512), (2048, 128)]


def build_program(kb, upto=99):
    nc, s = kb.nc, kb.s
    es = kb.es
    xin = kb.din("x", [NTOK, D])
    cin = kb.din("c", [NT, D])
    cst = kb.din("cst", [128, NCST])
    ada_w = kb.din("ada_w", [2, D, 9 * D])
    ada_b = kb.din("ada_b", [2, 9 * D])
    ln_g = kb.din("ln_g", [2, 3, D])
    ln_b = kb.din("ln_b", [2, 3, D])
    ffn_w1 = kb.din("ffn_w1", [2, 2, D, DFF])
    ffn_w3 = kb.din("ffn_w3", [2, 2, D, DFF])
    ffn_w2 = kb.din("ffn_w2", [2, 2, DFF, D])
    yout = kb.dout("y", [NTOK, D])

    X = kb.sb(es, "X", [128, NT, D], F32)
    C = kb.sb(es, "cst_sb", [128, NCST], F32)
    cT = kb.sb(es, "cT", [128, 8, NT], BF16)
    onesb = kb.sb(es, "onesb", [1, 32], F32)
    modT = kb.sb(es, "modT", [128, 16, NT], F32)
    gl = {}

    def alloc_gl(stack):
        gl["Gp"] = kb.sb(stack, "Gp", [128, D], F32)
        gl["Gs"] = kb.sb(stack, "Gs", [128, D], F32)
        gl["LNg"] = kb.sb(stack, "LNg", [128, D], F32)
        gl["LNb"] = kb.sb(stack, "LNb", [128, D], F32)
    tA = [kb.sb(es, "tA%d" % i, [128, 512], F32) for i in range(4)]
    tB = [kb.sb(es, "tB%d" % i, [128, D], F32) for i in range(2)]
    stt_ = [kb.sb(es, "bnst%d" % i, [128, 2, 6], F32) for i in range(2)]
    mv = [kb.sb(es, "mv%d" % i, [128, 2], F32) for i in range(2)]
    rstd = [kb.sb(es, "rstd%d" % i, [128, 1], F32) for i in range(2)]
    nmr = [kb.sb(es, "nmr%d" % i, [128, 1], F32) for i in range(2)]
    tmpS = kb.sb(es, "tmpS", [128, 128], F32)
    ps = [es.enter_context(nc.psum_tensor("ps%d" % i, [128, 512], F32)) for i in range(8)]
    PS = ["ps%d" % i for i in range(8)]
    psb = [p.bitcast(BF16) for p in ps]

    ident = C[:, CST_OFF["ident"]:CST_OFF["ident"] + 128]
    selP = C[0:NT, CST_OFF["selP"]:CST_OFF["selP"] + 128]
    selS = C[0:NT, CST_OFF["selS"]:CST_OFF["selS"] + 128]

    kb.dma("sp", C[:], cst, r=(), w=["C"])
    for i in range(NT):
        kb.dma("sp", X[:, i, :], xin[i * 128:(i + 1) * 128, :], r=(), w=["X%d" % i])
    kb.memset("pool", onesb[:], 1.0, w=["onesb"])

    with contextlib.ExitStack() as ph0:
        c_sb = kb.sb(ph0, "c_sb", [NT, D], F32)
        cs_sb = kb.sb(ph0, "cs_sb", [NT, D], F32)
        kb.dma("sp", c_sb[:], cin, r=(), w=["c_sb"])
        kb.act(cs_sb[:], c_sb[:], AF.Silu, r=["c_sb"], w=["cs_sb"])
        for kc in range(8):
            kb.tr(ps[0][:, kc * NT:(kc + 1) * NT], cs_sb[0:NT, kc * 128:(kc + 1) * 128], C[0:NT, 0:NT],
                  r=["cs_sb", "C"], w=[PS[0]], sig=(kc == 7))
        kb.cp("dve", cT[:].rearrange("p a b -> p (a b)"), ps[0][:, 0:8 * NT], r=[PS[0]], w=["cT"])
    s.barrier()

    state = {"ada_i": 0, "ada_i2": 0, "psr": 0}

    def mod_prepare(l, sub, res_w, blocks=range(6)):
        if 5 in blocks:
            Gp, Gs = gl["Gp"], gl["Gs"]
            kb.dma("sp", gl["LNg"][:], ln_g[l, sub:sub + 1, :].to_broadcast([128, D]), r=(), w=["LNg"])
            kb.dma("sp", gl["LNb"][:], ln_b[l, sub:sub + 1, :].to_broadcast([128, D]), r=(), w=["LNb"])
        phm = contextlib.ExitStack()
        modst = [kb.sb(phm, "modst%d" % i, [NT, 512], F32) for i in range(2)]
        adaw = [kb.sb(phm, "adaw%d" % i, [128, 8, 256], BF16) for i in range(2)]
        adab = [kb.sb(phm, "adab%d" % i, [1, 512], F32) for i in range(2)]
        for b in blocks:
            i = state["ada_i"]
            state["ada_i"] += 1
            buf = i % 2
            co = sub * 3 * D + b * 512
            kb.dma("sp", adab[buf][:], ada_b[l:l + 1, co:co + 512], r=(), w=["adab%d" % buf])
            pm = 4 + (i % 2)
            for sbk in range(2):
                i2 = state["ada_i2"]
                state["ada_i2"] += 1
                wb = i2 % 2
                kb.dma("pool", adaw[wb][:], ada_w[l].rearrange("(kc p) n -> p kc n", p=128)[:, :, co + sbk * 256:co + (sbk + 1) * 256],
                       r=(), w=["adaw%d" % wb])
                for kc in range(8):
                    kb.mm(ps[pm][0:NT, sbk * 256:(sbk + 1) * 256], lhsT=cT[:, kc, :], rhs=adaw[wb][:, kc, :], start=(kc == 0), stop=False,
                          r=["cT", "adaw%d" % wb], w=[PS[pm]], sig=False)
                kb.mm(ps[pm][0:NT, sbk * 256:(sbk + 1) * 256], lhsT=onesb[0:1, 0:NT], rhs=adab[buf][:, sbk * 256:(sbk + 1) * 256], start=False, stop=True,
                      r=["onesb", "adab%d" % buf], w=[PS[pm]], sig=True)
            kb.cp("act", modst[buf][:], ps[pm][0:NT, :], r=[PS[pm]], w=["modst%d" % buf])
            if b < 4:
                for cc in range(4):
                    j = b * 4 + cc
                    kb.tr(ps[6][:, j * NT:(j + 1) * NT], modst[buf][0:NT, cc * 128:(cc + 1) * 128], C[0:NT, 0:NT],
                          r=["modst%d" % buf, "C"], w=[PS[6]], sig=(cc == 3))
                if b == 1:
                    kb.cp("dve", modT[:, 0:8, :].rearrange("p a b -> p (a b)"), ps[6][:, 0:8 * NT], r=[PS[6]], w=["modT"])
                if b == 3:
                    kb.ts("dve", modT[:, 8:16, :].rearrange("p a b -> p (a b)"), ps[6][:, 8 * NT:16 * NT], 1.0, None,
                          ALU.add, None, r=[PS[6]], w=["modT"])
            else:
                h = b - 4
                kb.mm(ps[7][:, :], lhsT=selP, rhs=modst[buf][:], start=True, stop=True, r=["C", "modst%d" % buf], w=[PS[7]])
                kb.act(Gp[:, h * 512:(h + 1) * 512], ps[7][:, :], AF.Identity, r=[PS[7]], w=["Gp"], bias=float(res_w), scale=float(res_w))
                kb.mm(ps[7][:, :], lhsT=selS, rhs=modst[buf][:], start=True, stop=True, r=["C", "modst%d" % buf], w=[PS[7]])
                kb.act(Gs[:, h * 512:(h + 1) * 512], ps[7][:, :], AF.Identity, r=[PS[7]], w=["Gs"], bias=float(res_w), scale=float(res_w))
        s.barrier()
        phm.close()

    def make_hT(hT, i, prescale=True, col0=None, res=None, hook=None):
        g = res if res is not None else "hT_g%d" % (i // 4)
        if col0 is None:
            col0 = i * 128
        for half in range(2):
            pb = state["psr"] % 4
            state["psr"] += 1
            for cc in range(4):
                c = half * 4 + cc
                kb.tr(ps[pb][:, cc * 128:(cc + 1) * 128], X[:, i, c * 128:(c + 1) * 128], ident,
                      r=["X%d" % i, "C"], w=[PS[pb]], sig=(cc == 3))
            for cc in range(4):
                c = half * 4 + cc
                src = ps[pb][:, cc * 128:(cc + 1) * 128]
                if hook is not None:
                    hook(i, c, src, PS[pb])
                if i < 16:
                    if cc % 2 == 0:
                        kb.act(hT[:, c, col0:col0 + 128], src, AF.Identity, r=[PS[pb], "modT"], w=[g],
                               bias=modT[:, c, 0:1], scale=modT[:, 8 + c, 0:1])
                    else:
                        kb.ts("dve", hT[:, c, col0:col0 + 128], src, modT[:, 8 + c, 0:1], modT[:, c, 0:1],
                              ALU.mult, ALU.add, r=[PS[pb], "modT"], w=[g])
                else:
                    sc = modT[:, 8 + c, 1:NT].unsqueeze(2).to_broadcast([128, 16, 8])
                    sh = modT[:, c, 1:NT].unsqueeze(2).to_broadcast([128, 16, 8])
                    kb.tt("dve", tmpS[:].rearrange("p (q t) -> p q t", t=8), src.rearrange("p (q t) -> p q t", t=8), sc,
                          ALU.mult, r=[PS[pb], "modT"], w=["tmpS"])
                    kb.tt("dve", hT[:, c, col0:col0 + 128].rearrange("p (q t) -> p q t", t=8),
                          tmpS[:].rearrange("p (q t) -> p q t", t=8), sh, ALU.add, r=["tmpS", "modT"], w=[g])

    def layer_norm(i):
        k = i % 2
        for h in range(2):
            s.add("dve", lambda g_, h=h, k=k, i=i: g_.bn_stats(out=stt_[k][:, h, :], in_=X[:, i, h * 512:(h + 1) * 512]),
                  r=["X%d" % i], w=["bnst%d" % k], tag="bnstats")
        s.add("dve", lambda g_, k=k: g_.bn_aggr(out=mv[k][:], in_=stt_[k][:].rearrange("p a b -> p (a b)")),
              r=["bnst%d" % k], w=["mv%d" % k], tag="bnaggr")
        kb.act(rstd[k][:], mv[k][:, 1:2], AF.Sqrt, r=["mv%d" % k], w=["rstd%d" % k], bias=float(LN_EPS), scale=1.0)
        s.add("dve", lambda g_, k=k: g_.reciprocal(out=rstd[k][:], in_=rstd[k][:]), r=["rstd%d" % k], w=["rstd%d" % k], tag="recip")
        kb.ts("dve", nmr[k][:], mv[k][:, 0:1], rstd[k][:, 0:1], -1.0, ALU.mult, ALU.mult, r=["mv%d" % k, "rstd%d" % k], w=["nmr%d" % k])
        kb.act(tB[k][:], X[:, i, :], AF.Identity, r=["X%d" % i, "rstd%d" % k, "nmr%d" % k], w=["tB%d" % k],
               bias=nmr[k][:, 0:1], scale=rstd[k][:, 0:1])
        ea = "pool" if i % 3 == 2 else "dve"
        kb.tt(ea, tB[k][:], tB[k][:], gl["LNg"][:], ALU.mult, r=["tB%d" % k, "LNg"], w=["tB%d" % k])
        kb.tt(ea, X[:, i, :], tB[k][:], gl["LNb"][:], ALU.add, r=["tB%d" % k, "LNb"], w=["X%d" % i])

    pacc = {"v": 0}

    def proj_acc(wt, wres, nk, lhs_of, do_ln, first, after_ln=None, wts=None, wsres=None):
        Gp, Gs = gl["Gp"], gl["Gs"]
        for i in [16] + list(range(16)):
            if i == 0:
                if wts is None:
                    kb.tt("dve", wt[:, 0:nk, :], wt[:, 0:nk, :], Gp[:].unsqueeze(1).to_broadcast([128, nk, D]), ALU.mult, r=[wres, "Gp"], w=[wres])
                else:
                    wt, wres = wts, wsres
            for half in range(2):
                v = pacc["v"]
                pacc["v"] += 1
                py = 4 + (v % 4)
                for kc in range(nk):
                    lt, lres = lhs_of(i, kc)
                    kb.mm(ps[py][:, :], lhsT=lt, rhs=wt[:, kc, half * 512:(half + 1) * 512], start=(kc == 0), stop=(kc == nk - 1),
                          r=[lres, wres], w=[PS[py]])
                xs = X[:, i, half * 512:(half + 1) * 512]
                src = ps[py][:, :]
                rsrc = PS[py]
                if i == 16:
                    kb.tt("dve", tA[v % 4][:], ps[py][:, :], Gs[:, half * 512:(half + 1) * 512], ALU.mult, r=[PS[py], "Gs"], w=["tA%d" % (v % 4)])
                    src, rsrc = tA[v % 4][:], "tA%d" % (v % 4)
                if first:
                    kb.stt(xs, xs, float(ALPHA), src, ALU.mult, ALU.add, r=["X%d" % i, rsrc], w=["X%d" % i])
                else:
                    kb.tt("dve", xs, xs, src, ALU.add, r=["X%d" % i, rsrc], w=["X%d" % i])
            if do_ln:
                layer_norm(i)
                if after_ln is not None:
                    after_ln(i)

    def ffn(l, f, sub, res_w):
        with contextlib.ExitStack() as ph:
            alloc_gl(ph)
            mod_prepare(l, sub, res_w)
            Gp, Gs = gl["Gp"], gl["Gs"]
            hT = kb.sb(ph, "hT", [128, 8, NTOK], BF16)
            w1p = kb.sb(ph, "w1p", [128, 8, 512], BF16)
            w3p = kb.sb(ph, "w3p", [128, 8, 512], BF16)
            w2p = kb.sb(ph, "w2p", [128, 4, D], BF16)
            w2s = kb.sb(ph, "w2s", [128, 4, D], BF16)
            gbuf = kb.sb(ph, "gbuf", [128, 4, NTOK], BF16)
            sil = [kb.sb(ph, "sil%d" % i, [128, 512], F32) for i in range(2)]
            w1v = ffn_w1[l, f].rearrange("(kc p) n -> p kc n", p=128)
            w3v = ffn_w3[l, f].rearrange("(kc p) n -> p kc n", p=128)
            w2v = ffn_w2[l, f].rearrange("(j p) n -> p j n", p=128)
            u = 0
            v = 0
            def load_up(pi_):
                j0_, n_ = FFN_PARTS[pi_]
                kb.dma("pool", w1p[:, :, 0:n_ * 128], w1v[:, :, j0_ * 128:(j0_ + n_) * 128], r=(), w=["w1p"])
                kb.dma("pool", w3p[:, :, 0:n_ * 128], w3v[:, :, j0_ * 128:(j0_ + n_) * 128], r=(), w=["w3p"])

            def load_dn(pi_):
                j0_, n_ = FFN_PARTS[pi_]
                kb.dma("pool", w2p[:, 0:n_, :], w2v[:, j0_:j0_ + n_, :], r=(), w=["w2p"])

            load_up(0)
            load_dn(0)
            for i in range(NT):
                make_hT(hT, i)
            for pi, (j0, ncn) in enumerate(FFN_PARTS):
                for tg, (t0, nt_) in enumerate(TGS):
                    for jj in range(ncn):
                        pa, pb = (2 * u) % 4, (2 * u + 1) % 4
                        for kc in range(8):
                            kb.mm(ps[pa][:, 0:nt_], lhsT=w1p[:, kc, jj * 128:(jj + 1) * 128], rhs=hT[:, kc, t0:t0 + nt_],
                                  start=(kc == 0), stop=(kc == 7), r=["w1p", "hT_g%d" % tg], w=[PS[pa]])
                        for kc in range(8):
                            kb.mm(ps[pb][:, 0:nt_], lhsT=w3p[:, kc, jj * 128:(jj + 1) * 128], rhs=hT[:, kc, t0:t0 + nt_],
                                  start=(kc == 0), stop=(kc == 7), r=["w3p", "hT_g%d" % tg], w=[PS[pb]])
                        kb.act(sil[u % 2][:, 0:nt_], ps[pa][:, 0:nt_], AF.Silu, r=[PS[pa]], w=["sil%d" % (u % 2)])
                        kb.tt("dve", gbuf[:, jj, t0:t0 + nt_], sil[u % 2][:, 0:nt_], ps[pb][:, 0:nt_], ALU.mult,
                              r=["sil%d" % (u % 2), PS[pb]], w=["g_g%d" % tg])
                        u += 1
                    if tg == 1:
                        kb.tt("dve", w2s[:, 0:ncn, :], w2p[:, 0:ncn, :], Gp[:].unsqueeze(1).to_broadcast([128, ncn, D]), ALU.mult,
                              r=["w2p", "Gp"], w=["w2s"])
                if pi + 1 < len(FFN_PARTS):
                    load_up(pi + 1)
                proj_acc(w2p, "w2p", ncn, lambda i, jj: (gbuf[:, jj, i * 128:(i + 1) * 128], "g_g%d" % (i // 4)), pi == len(FFN_PARTS) - 1, pi == 0,
                         after_ln=(store_tile if (l == 1 and sub == 2 and upto >= 6) else None), wts=w2s, wsres="w2s")
                if pi + 1 < len(FFN_PARTS):
                    load_dn(pi + 1)
        s.barrier()

    DKS = float(128 ** -0.5)

    class _Stop(Exception):
        pass

    def stop_at(n):
        if getattr(kb, "stop_point", None) == n:
            s.barrier()
            s.skip = True

    def ab_mixer(l):
        ab_mixer_(l)
        s.skip = False
        s.barrier()

    def ab_mixer_(l):
        ab_w_in = kb.din("ab_w_in", [1, D, 3080])
        ab_w_out = kb.din("ab_w_out", [1, D, D])
        mnorm_g = kb.din("mlstm_norm_g", [1, 512])
        vecA_d = kb.din("vecA", [128, 32])
        bgT_d = kb.din("bgT", [4, 2])
        minitT_d = kb.din("minitT", [4, 16])
        rg_w_a = kb.din("rg_w_a", [1, 8, 64, 64])
        rg_w_x = kb.din("rg_w_x", [1, 8, 64, 64])
        smC = kb.din("smC", [16, 4, 128, 128])
        smn = kb.din("smn", [16, 4, 128])
        srh = kb.din("srh", [16, 512])
        srconv = kb.din("srconv", [48, 512])
        o_pmC = kb.dout("o_pmC", [4, 128, 128])
        o_pmn = kb.dout("o_pmn", [4, 128])
        o_pmm = kb.dout("o_pmm", [4, 1])
        o_prh = kb.dout("o_prh", [4, 128])
        o_prconv = kb.dout("o_prconv", [3, 512])
        o_smC = kb.dout("o_smC", [16, 4, 128, 128])
        o_smn = kb.dout("o_smn", [16, 4, 128])
        o_smm = kb.dout("o_smm", [4, 16])
        o_srh = kb.dout("o_srh", [16, 512])
        o_srconv = kb.dout("o_srconv", [48, 512])

        maskP = C[:, CST_OFF["maskP"]:CST_OFF["maskP"] + 128]
        maskS = C[:, CST_OFF["maskS"]:CST_OFF["maskS"] + 128]
        rst = C[:, CST_OFF["rst"]:CST_OFF["rst"] + 128]
        rstm = C[:, CST_OFF["rstm"]:CST_OFF["rstm"] + 128]
        bms = C[:, CST_OFF["bms"]:CST_OFF["bms"] + 16]
        ones = C[:, CST_OFF["ones"]:CST_OFF["ones"] + 128]

        mod_prepare(l, 1, 1.0, blocks=range(4))
        win_v = ab_w_in[0].rearrange("(kc p) n -> p kc n", p=128)
        with contextlib.ExitStack() as ph:
            hmT = kb.sb(ph, "hmT", [128, 4, NTOK], BF16)
            vecA = kb.sb(ph, "vecA_sb", [128, 32], F32)
            kb.dma("sp", vecA[:], vecA_d, r=(), w=["vecA"])
            sigo = tA[0]
            hmf = tB[0][:, 0:512]
            with contextlib.ExitStack() as pa:
                winA = kb.sb(pa, "winA", [128, 8, 2056], BF16)
                kb.dma("pool", winA[:, :, 0:1024], win_v[:, :, 0:1024], r=(), w=["winA"])
                kb.dma("pool", winA[:, :, 1024:2056], win_v[:, :, 1024:2056], r=(), w=["winA"])
                qkT = kb.sb(pa, "qkT", [128, 8, 512], BF16)
                hTg = [kb.sb(pa, "hTgA%d" % i, [128, 8, 512], BF16) for i in range(2)]
                bg = kb.sb(pa, "bg", [4, 2], F32)
                nbg1 = kb.sb(pa, "nbg1", [4, 1], F32)
                minitT = kb.sb(pa, "minitT_sb", [4, 16], F32)
                mng = kb.sb(pa, "mng", [128, 512], F32)
                kb.dma("sp", bg[:], bgT_d, r=(), w=["bg"])
                kb.dma("sp", minitT[:], minitT_d, r=(), w=["minitT"])
                kb.dma("sp", mng[:], mnorm_g[0:1, :].to_broadcast([128, 512]), r=(), w=["mng"])
                kb.ts("dve", nbg1[:], bg[:, 1:2], -1.0, None, ALU.mult, None, r=["bg"], w=["nbg1"])
                R4 = lambda nm: kb.sb(pa, nm, [4, 128], F32)
                t1, IGa, Rt, t3, t4 = R4("r_t1"), R4("r_ig"), R4("r_rt"), R4("r_t3"), R4("r_t4")
                Bc = [R4("r_bc0"), R4("r_bc1")]
                Mx = [R4("r_mx0"), R4("r_mx1")]
                dd = kb.sb(pa, "r_dd", [4, 16], F32)
                DDm = kb.sb(pa, "r_DD", [4, 64], F32)
                mout = kb.sb(pa, "r_mout", [4, 16], F32)
                colq = [kb.sb(pa, "colq%d" % i, [128, 16], F32) for i in range(2)]
                decsb = kb.sb(pa, "decsb", [128, 64], F32)
                kw = kb.sb(pa, "kw", [128, 4, 128], BF16)
                ktok = kb.sb(pa, "ktok", [128, 4, 128], BF16)
                vext = [kb.sb(pa, "vext%d" % i, [128, 4, 130], BF16) for i in range(2)]
                PT = kb.sb(pa, "PT", [128, 4, 128], BF16)
                Cst = kb.sb(pa, "Cst", [128, 4, 130], F32)
                Cb = kb.sb(pa, "Cb", [128, 4, 130], BF16)
                dmax = kb.sb(pa, "dmax", [128, 4], F32)
                hst6 = kb.sb(pa, "hst6", [128, 4, 6], F32)
                hmv = kb.sb(pa, "hmv", [128, 4, 2], F32)
                hrs = kb.sb(pa, "hrs", [128, 4], F32)
                hnm = kb.sb(pa, "hnm", [128, 4], F32)
                for vv in vext:
                    kb.memset("pool", vv[:], 1.0, w=["vext0", "vext1"])
                kb.memset("pool", Cst[:], 0.0, w=["Cst"])
                kb.memset("pool", Cb[:], 0.0, w=["Cb"])

                def rows(i):
                    k = i % 2
                    hb = (i // 4) % 2
                    hT = hTg[hb]
                    tc0 = (i % 4) * 128
                    pg = ps[7]
                    for kc in range(8):
                        kb.mm(pg[0:4, 0:128], lhsT=winA[:, kc, 2048:2052], rhs=hT[:, kc, tc0:tc0 + 128], start=(kc == 0), stop=(kc == 7),
                              r=["winA", "hTg%d" % hb], w=[PS[7]])
                    for kc in range(8):
                        kb.mm(pg[0:4, 128:256], lhsT=winA[:, kc, 2052:2056], rhs=hT[:, kc, tc0:tc0 + 128], start=(kc == 0), stop=(kc == 7),
                              r=["winA", "hTg%d" % hb], w=[PS[7]])
                    kb.act(IGa[:], pg[0:4, 0:128], AF.Identity, r=[PS[7], "bg"], w=["r_ig"], bias=bg[:, 0:1], scale=1.0)
                    kb.act(t1[:], pg[0:4, 128:256], AF.Exp, r=[PS[7], "nbg1"], w=["r_t1"], bias=nbg1[:, 0:1], scale=-1.0)
                    kb.act(t1[:], t1[:], AF.Ln, r=["r_t1"], w=["r_t1"], bias=1.0, scale=1.0)
                    kb.ts("dve", t1[:], t1[:], -1.0, None, ALU.mult, None, r=["r_t1"], w=["r_t1"])
                    prompt = i < 16
                    if prompt:
                        binit = 0.0 if i == 0 else Bc[1 - k][:, 127:128]
                        minit = 0.0 if i == 0 else Mx[1 - k][:, 127:128]
                        s.add("dve", lambda g_: g_.tensor_tensor_scan(out=Bc[k][:], data0=ones[0:4, :], data1=t1[:], initial=binit,
                                                                       op0=ALU.mult, op1=ALU.add),
                              r=["r_t1", "r_bc%d" % (1 - k), "C"], w=["r_bc%d" % k], tag="scanB")
                        kb.tt("dve", IGa[:], IGa[:], Bc[k][:], ALU.subtract, r=["r_ig", "r_bc%d" % k], w=["r_ig"])
                        kb.memset("dve", t3[:], 0.0, w=["r_t3"])
                        s.add("dve", lambda g_: g_.tensor_tensor_scan(out=Mx[k][:], data0=t3[:], data1=IGa[:], initial=minit,
                                                                       op0=ALU.add, op1=ALU.max),
                              r=["r_t3", "r_ig", "r_mx%d" % (1 - k)], w=["r_mx%d" % k], tag="scanM")
                        if i == 0:
                            kb.memset("dve", Rt[:], 0.0, w=["r_rt"])
                        else:
                            kb.cp("dve", Rt[:], Mx[1 - k][:, 127:128].to_broadcast([4, 128]), r=["r_mx%d" % (1 - k)], w=["r_rt"])
                    else:
                        s.add("dve", lambda g_: g_.tensor_tensor_scan(out=Bc[k][:], data0=rst[0:4, :], data1=t1[:], initial=0.0,
                                                                       op0=ALU.mult, op1=ALU.add),
                              r=["r_t1", "C"], w=["r_bc%d" % k], tag="scanB")
                        kb.tt("dve", IGa[:], IGa[:], Bc[k][:], ALU.subtract, r=["r_ig", "r_bc%d" % k], w=["r_ig"])
                        kb.cp("dve", t3[:], IGa[:], r=["r_ig"], w=["r_t3"])
                        kb.tt("dve", t3[:].rearrange("p (q t) -> p q t", t=8)[:, :, 0:1], IGa[:].rearrange("p (q t) -> p q t", t=8)[:, :, 0:1],
                              minitT[:].unsqueeze(2), ALU.max, r=["r_ig", "minitT"], w=["r_t3"])
                        s.add("dve", lambda g_: g_.tensor_tensor_scan(out=Mx[k][:], data0=rstm[0:4, :], data1=t3[:], initial=0.0,
                                                                       op0=ALU.add, op1=ALU.max),
                              r=["r_t3", "C"], w=["r_mx%d" % k], tag="scanM")
                        kb.cp("dve", Rt[:].rearrange("p (q t) -> p q t", t=8), minitT[:].unsqueeze(2).to_broadcast([4, 16, 8]),
                              r=["minitT"], w=["r_rt"])
                    kb.tt("dve", t3[:], IGa[:], Rt[:], ALU.subtract, r=["r_ig", "r_rt"], w=["r_t3"])
                    kb.act(t3[:], t3[:], AF.Exp, r=["r_t3"], w=["r_t3"])
                    kb.tt("dve", t4[:], Bc[k][:], Rt[:], ALU.add, r=["r_bc%d" % k, "r_rt"], w=["r_t4"])
                    kb.act(t4[:], t4[:], AF.Exp, r=["r_t4"], w=["r_t4"], scale=-1.0)
                    kb.tr(pg[:, 256:260], t3[0:4, :], C[0:4, 0:4], r=["r_t3", "C"], w=[PS[7]])
                    kb.tr(pg[:, 260:264], t4[0:4, :], C[0:4, 0:4], r=["r_t4", "C"], w=[PS[7]])
                    if prompt:
                        kb.tt("dve", dd[:, 0:1], Rt[:, 0:1], Mx[k][:, 127:128], ALU.subtract, r=["r_rt", "r_mx%d" % k], w=["r_dd"])
                        kb.act(dd[:, 0:1], dd[:, 0:1], AF.Exp, r=["r_dd"], w=["r_dd"])
                        kb.ts("dve", DDm[:, 0:4], C[0:4, 0:4], dd[:, 0:1], None, ALU.mult, None, r=["r_dd", "C"], w=["r_DD"])
                        kb.mm(pg[:, 264:268], lhsT=ones[0:4, :], rhs=DDm[:, 0:4], start=True, stop=True, r=["C", "r_DD"], w=[PS[7]])
                        kb.cp("dve", colq[k][:, 0:12], pg[:, 256:268], r=[PS[7]], w=["colq%d" % k])
                        if i == 15:
                            kb.tt("dve", mout[:, 0:1], Bc[k][:, 127:128], Mx[k][:, 127:128], ALU.add, r=["r_bc%d" % k, "r_mx%d" % k], w=["r_mout"])
                            kb.dma("sp", o_pmm, mout[:, 0:1], r=["r_mout"], w=())
                    else:
                        MT = Mx[k][:].rearrange("p (q t) -> p q t", t=8)[:, :, 7:8]
                        kb.tt("dve", t4[:].rearrange("p (q t) -> p q t", t=8), IGa[:].rearrange("p (q t) -> p q t", t=8),
                              MT.to_broadcast([4, 16, 8]), ALU.subtract, r=["r_ig", "r_mx%d" % k, PS[7]], w=["r_t4"])
                        kb.act(t4[:], t4[:], AF.Exp, r=["r_t4"], w=["r_t4"])
                        kb.tr(pg[:, 264:268], t4[0:4, :], C[0:4, 0:4], r=["r_t4", "C"], w=[PS[7]])
                        kb.cp("dve", colq[k][:, 0:12], pg[:, 256:268], r=[PS[7]], w=["colq%d" % k])
                        kb.tt("dve", dd[:].unsqueeze(2), minitT[:].unsqueeze(2), MT, ALU.subtract, r=["minitT", "r_mx%d" % k], w=["r_dd"])
                        kb.act(dd[:], dd[:], AF.Exp, r=["r_dd"], w=["r_dd"])
                        kb.tt("dve", DDm[:].rearrange("p (q h) -> p q h", h=4), dd[:].unsqueeze(2).to_broadcast([4, 16, 4]),
                              C[0:4, 0:4].unsqueeze(1).to_broadcast([4, 16, 4]), ALU.mult, r=["r_dd", "C"], w=["r_DD"])
                        kb.mm(pg[:, 272:336], lhsT=ones[0:4, :], rhs=DDm[:], start=True, stop=True, r=["C", "r_DD"], w=[PS[7]])
                        kb.cp("dve", decsb[:], pg[:, 272:336], r=[PS[7]], w=["decsb"])
                        kb.tt("dve", mout[:].unsqueeze(2), Bc[k][:].rearrange("p (q t) -> p q t", t=8)[:, :, 7:8], MT, ALU.add,
                              r=["r_bc%d" % k, "r_mx%d" % k], w=["r_mout"])
                        kb.dma("sp", o_smm, mout[:], r=["r_mout"], w=())

                def mlstm_tile(i):
                    k = i % 2
                    tg = i // 4
                    hT = hTg[tg % 2]
                    tc0 = (i % 4) * 128
                    lc0 = (i % 4) * 128 if i < 16 else 0
                    cq = colq[k]
                    vx = vext[k]
                    grp = "hTg%d" % (tg % 2)
                    for bi, c0 in enumerate((512, 1024, 1536)):
                        bank = 2 + (bi % 2)
                        for kc in range(8):
                            kb.mm(ps[bank][:, :], lhsT=hT[:, kc, tc0:tc0 + 128], rhs=winA[:, kc, c0:c0 + 512], start=(kc == 0), stop=(kc == 7),
                                  r=["winA", grp], w=[PS[bank]])
                        if bi == 0:
                            kb.act(ktok[:].rearrange("p a b -> p (a b)"), ps[bank][:, :], AF.Identity, r=[PS[bank]], w=["ktok"], scale=DKS)
                            for h in range(4):
                                kb.ts("dve", kw[:, h, :], ktok[:, h, :], cq[:, h:h + 1], None, ALU.mult, None,
                                      r=["ktok", "colq%d" % k], w=["kw"])
                        elif bi == 1:
                            kb.cp("act", vx[:, :, 0:128], ps[bank][:, :].rearrange("p (h d) -> p h d", d=128), r=[PS[bank]], w=["vext%d" % k])
                        else:
                            kb.act(sigo[:], ps[bank][:, :], AF.Sigmoid, r=[PS[bank]], w=["tA0"])
                    if i == 0:
                        stop_at(31)
                    for h in range(4):
                        kb.mm(ps[4][:, h * 128:(h + 1) * 128], lhsT=qkT[:, 4 + h, lc0:lc0 + 128], rhs=qkT[:, h, lc0:lc0 + 128],
                              start=True, stop=True, r=["qkT"], w=[PS[4]], sig=(h == 3))
                    if i == 0:
                        stop_at(32)
                    msk = maskP if i < 16 else maskS
                    for h in range(4):
                        kb.stt(PT[:, h, :], ps[4][:, h * 128:(h + 1) * 128], cq[:, h:h + 1], msk, ALU.mult, ALU.mult,
                               r=[PS[4], "colq%d" % k, "C"], w=["PT"])
                    return cq, vx, lc0

                def numden_finish(i, cq):
                    for half in range(2):
                        bank = ps[5 + half]
                        den = bank[:, 0:260].rearrange("p (h d) -> p h d", d=130)[:, :, 128:129]
                        kb.act(dmax[:, 2 * half:2 * half + 2].unsqueeze(2), den, AF.Abs, r=[PS[5 + half]], w=["dmax"])
                        kb.tt("dve", dmax[:, 2 * half:2 * half + 2], dmax[:, 2 * half:2 * half + 2], cq[:, 4 + 2 * half:6 + 2 * half], ALU.max,
                              r=["dmax", "colq%d" % (i % 2)], w=["dmax"])
                    s.add("dve", lambda g_: g_.reciprocal(out=dmax[:], in_=dmax[:]), r=["dmax"], w=["dmax"], tag="recip")
                    for h in range(4):
                        bank = ps[5 + h // 2]
                        o0 = (h % 2) * 130
                        kb.act(hmf[:, h * 128:(h + 1) * 128], bank[:, o0:o0 + 128], AF.Identity, r=[PS[5 + h // 2], "dmax"], w=["tB0"],
                               scale=dmax[:, h:h + 1])
                    for h in range(4):
                        s.add("dve", lambda g_, h=h: g_.bn_stats(out=hst6[:, h, :], in_=hmf[:, h * 128:(h + 1) * 128]), r=["tB0"], w=["hst6"], tag="bnst")
                    for h in range(4):
                        s.add("dve", lambda g_, h=h: g_.bn_aggr(out=hmv[:, h, :], in_=hst6[:, h, :]), r=["hst6"], w=["hmv"], tag="bnag")
                    kb.act(hrs[:].unsqueeze(2), hmv[:, :, 1:2], AF.Sqrt, r=["hmv"], w=["hrs"], bias=1e-6, scale=1.0)
                    s.add("dve", lambda g_: g_.reciprocal(out=hrs[:], in_=hrs[:]), r=["hrs"], w=["hrs"], tag="recip")
                    kb.tt("dve", hnm[:].unsqueeze(2), hmv[:, :, 0:1], hrs[:].unsqueeze(2), ALU.mult, r=["hmv", "hrs"], w=["hnm"])
                    kb.ts("dve", hnm[:], hnm[:], -1.0, None, ALU.mult, None, r=["hnm"], w=["hnm"])
                    for h in range(4):
                        kb.act(hmf[:, h * 128:(h + 1) * 128], hmf[:, h * 128:(h + 1) * 128], AF.Identity, r=["tB0", "hrs", "hnm"], w=["tB0"],
                               bias=hnm[:, h:h + 1], scale=hrs[:, h:h + 1])
                    kb.tt("dve", hmf[:], hmf[:], mng[:], ALU.mult, r=["tB0", "mng"], w=["tB0"])
                    kb.tt("dve", hmf[:], hmf[:], sigo[:], ALU.mult, r=["tB0", "tA0"], w=["tB0"])
                    for h in range(4):
                        kb.tr(ps[4][:, h * 128:(h + 1) * 128], hmf[:, h * 128:(h + 1) * 128], ident, r=["tB0", "C"], w=[PS[4]], sig=(h == 3))
                    kb.cp("act", hmT[:, :, i * 128:(i + 1) * 128], ps[4][:, :].rearrange("p (h d) -> p h d", d=128), r=[PS[4]], w=["hmT%d" % i])

                pend_rows = []
                for tg, (t0, nt_) in enumerate(TGS):
                    tiles = range(4 * tg, 4 * tg + 4) if tg < 4 else [16]
                    hT = hTg[tg % 2]
                    for i in tiles:
                        make_hT(hT, i, prescale=False, col0=(i % 4) * 128, res="hTg%d" % (tg % 2))
                    for j in range(8):
                        bank = j % 2
                        for kc in range(8):
                            kb.mm(ps[bank][:, 0:nt_], lhsT=winA[:, kc, j * 128:(j + 1) * 128], rhs=hT[:, kc, 0:nt_], start=(kc == 0), stop=(kc == 7),
                                  r=["winA", "hTg%d" % (tg % 2)], w=[PS[bank]])
                        if j < 4:
                            kb.cp("act", qkT[:, j, 0:nt_], ps[bank][:, 0:nt_], r=[PS[bank]], w=["qkT"])
                        else:
                            kb.ts("dve", qkT[:, j, 0:nt_], ps[bank][:, 0:nt_], DKS, None, ALU.mult, None, r=[PS[bank]], w=["qkT"])
                    for i in tiles:
                        if i == 0:
                            stop_at(1)
                        if i % 4 == 0 or i == 16:
                            rows(i)
                        else:
                            while pend_rows:
                                s.add(*pend_rows.pop(0))
                        if i < 16 and i % 4 != 3:
                            s.capture = []
                            rows(i + 1)
                            pend_rows.extend(s.capture)
                            s.capture = None
                            s.tick_fn = lambda: (s.add(*pend_rows.pop(0)) if pend_rows else None)
                        else:
                            s.tick_fn = None
                        if i == 0:
                            stop_at(2)
                        cq, vx, lc0 = mlstm_tile(i)
                        if i == 0:
                            stop_at(3)
                        if i == 16:
                            stop_at(5)
                        if i < 16:
                            for h in range(4):
                                bank = ps[5 + h // 2]
                                o0 = (h % 2) * 130
                                kb.mm(bank[:, o0:o0 + 130], lhsT=PT[:, h, :], rhs=vx[:, h, :], start=True, stop=False, r=["PT", "vext%d" % (i % 2)], w=[PS[5 + h // 2]], sig=False)
                                kb.mm(bank[:, o0:o0 + 130], lhsT=qkT[:, h, lc0:lc0 + 128], rhs=Cb[:, h, :], start=False, stop=True, r=["qkT", "Cb"], w=[PS[5 + h // 2]], sig=True)
                            numden_finish(i, cq)
                            for h in range(4):
                                bank = ps[5 + h // 2]
                                o0 = (h % 2) * 130
                                kb.mm(bank[:, o0:o0 + 130], lhsT=kw[:, h, :], rhs=vx[:, h, :], start=True, stop=True, r=["kw", "vext%d" % (i % 2)], w=[PS[5 + h // 2]])
                            for h in range(4):
                                bank = ps[5 + h // 2]
                                o0 = (h % 2) * 130
                                kb.ts("dve", Cst[:, h, :], Cst[:, h, :], cq[:, 8 + h:9 + h], None, ALU.mult, None, r=["Cst", "colq%d" % (i % 2)], w=["Cst"])
                                kb.stt(Cst[:, h, :], bank[:, o0:o0 + 130], cq[:, 8 + h:9 + h], Cst[:, h, :], ALU.mult, ALU.add,
                                       r=[PS[5 + h // 2], "Cst", "colq%d" % (i % 2)], w=["Cst"])
                            kb.cp("act", Cb[:], Cst[:], r=["Cst"], w=["Cb"])
                            if i == 0:
                                stop_at(4)
                            if i == 15:
                                for h in range(4):
                                    kb.tr(ps[4][:, h * 128:(h + 1) * 128], Cst[:, h, 0:128], ident, r=["Cst", "C"], w=[PS[4]], sig=(h == 3))
                                kb.cp("act", hmf[:], ps[4][:, :], r=[PS[4]], w=["tB0"])
                                kb.dma("sp", o_pmC.rearrange("h v k -> v h k"), hmf[:].rearrange("p (h k) -> p h k", k=128), r=["tB0"], w=())
                                kb.dma("sp", o_pmn.rearrange("h k -> k h"), Cst[:, :, 128], r=["Cst"], w=(), allow_slow_non_contiguous=True)
                        else:
                            with contextlib.ExitStack() as psm:
                                Cin = kb.sb(psm, "Cin", [128, 16, 128], F32)
                                CsT = kb.sb(psm, "CsT", [128, 16, 130], BF16)
                                qTm = kb.sb(psm, "qTm", [128, 16, 128], BF16)
                                VWm = kb.sb(psm, "VWm", [128, 16, 128], BF16)
                                nin = kb.sb(psm, "nin", [16, 4, 128], F32)
                                ninT = kb.sb(psm, "ninT", [128, 4, 16], F32)
                                BMW = kb.sb(psm, "BMW", [128, 16], BF16)
                                decc = kb.sb(psm, "decc", [16, 4], F32)
                                nout = nin
                                kb.memset("pool", qTm[:], 0.0, w=["qTm"])
                                kb.memset("pool", CsT[:], 0.0, w=["CsT"])
                                kb.dma("sp", nin[:], smn, r=(), w=["nin"])
                                kb.tr(ps[7][0:16, 400:404], dd[0:4, :], C[0:4, 0:4], r=["r_dd", "C"], w=[PS[7]])
                                kb.cp("dve", decc[:], ps[7][0:16, 400:404], r=[PS[7]], w=["decc"])
                                for h in range(4):
                                    kb.tr(ps[7][:, 416 + h * 16:432 + h * 16], nin[0:16, h, :], C[0:16, 0:16], r=["nin", "C"], w=[PS[7]], sig=(h == 3))
                                kb.cp("dve", ninT[:].rearrange("p a b -> p (a b)"), ps[7][:, 416:480], r=[PS[7]], w=["ninT"])
                                for h in range(4):
                                    bank = ps[5 + h // 2]
                                    o0 = (h % 2) * 130
                                    kb.dma("sp", Cin[:], smC[:, h].rearrange("q v k -> v q k"), r=(), w=["Cin"])
                                    for q4 in range(4):
                                        pb = q4 % 2
                                        for qq in range(4):
                                            q = q4 * 4 + qq
                                            kb.tr(ps[pb][:, qq * 128:(qq + 1) * 128], Cin[:, q, :], ident, r=["Cin", "C"], w=[PS[pb]], sig=(qq == 3))
                                        kb.cp("act", CsT[:, q4 * 4:q4 * 4 + 4, 0:128], ps[pb][:, :].rearrange("p (a b) -> p a b", b=128), r=[PS[pb]], w=["CsT"])
                                    kb.cp("dve", CsT[:, :, 128:129], ninT[:, h, :].unsqueeze(2), r=["ninT"], w=["CsT"])
                                    kb.cp("pool", bass.AP(qTm, 0, [[2048, 128], [136, 16], [1, 8]]),
                                          qkT[:, h, 0:128].rearrange("p (q t) -> p q t", t=8), r=["qkT"], w=["qTm"])
                                    kb.mm(bank[:, o0:o0 + 130], lhsT=PT[:, h, :], rhs=vx[:, h, :], start=True, stop=False, r=["PT", "vext%d" % (i % 2)], w=[PS[5 + h // 2]], sig=False)
                                    for q in range(16):
                                        kb.mm(bank[:, o0:o0 + 130], lhsT=qTm[:, q, :], rhs=CsT[:, q, :], start=False, stop=(q == 15), r=["qTm", "CsT"], w=[PS[5 + h // 2]], sig=(q == 15))
                                    kb.ts("dve", BMW[:], bms, cq[:, 8 + h:9 + h], None, ALU.mult, None, r=["C", "colq%d" % (i % 2)], w=["BMW"])
                                    kb.tt("dve", VWm[:], vx[:, h, 0:128].unsqueeze(1).to_broadcast([128, 16, 128]), BMW[:].unsqueeze(2).to_broadcast([128, 16, 128]),
                                          ALU.mult, r=["vext%d" % (i % 2), "BMW"], w=["VWm"])
                                    for q4 in range(4):
                                        pb = q4 % 2
                                        for qq in range(4):
                                            q = q4 * 4 + qq
                                            kb.mm(ps[pb][:, qq * 128:(qq + 1) * 128], lhsT=VWm[:, q, :], rhs=ktok[:, h, :], start=True, stop=True,
                                                  r=["VWm", "ktok"], w=[PS[pb]], sig=(qq == 3))
                                        for qq in range(4):
                                            q = q4 * 4 + qq
                                            kb.stt(Cin[:, q, :], Cin[:, q, :], decsb[:, q * 4 + h:q * 4 + h + 1], ps[pb][:, qq * 128:(qq + 1) * 128], ALU.mult, ALU.add,
                                                   r=["Cin", "decsb", PS[pb]], w=["Cin"])
                                    kb.dma("sp", o_smC[:, h].rearrange("q v k -> v q k"), Cin[:], r=["Cin"], w=())
                                    kb.mm(ps[7][0:16, 0:128], lhsT=BMW[:], rhs=ktok[:, h, :], start=True, stop=True, r=["BMW", "ktok"], w=[PS[7]])
                                    kb.stt(nout[:, h, :], nin[:, h, :], decc[:, h:h + 1], ps[7][0:16, 0:128], ALU.mult, ALU.add, r=["nin", "decc", PS[7]], w=["nin"])
                                numden_finish(i, cq)
                                kb.dma("sp", o_smn, nout[:], r=["nin"], w=())
                                s.barrier()
            s.barrier()
            s.tick_fn = None
            stop_at(6)
            hrT = kb.sb(ph, "hrT", [128, 4, NTOK], BF16)
            with contextlib.ExitStack() as pb_:
                winB = kb.sb(pb_, "winB", [128, 8, 1024], BF16)
                kb.dma("pool", winB[:], win_v[:, :, 2056:3080], r=(), w=["winB"])
                hTgB = [kb.sb(pb_, "hTgB%d" % i, [128, 8, 512], BF16) for i in range(2)]
                WA = kb.sb(pb_, "WA", [128, 4, 128], F32)
                WX = kb.sb(pb_, "WX", [128, 4, 128], F32)
                kb.memset("pool", WA[:], 0.0, w=["WA"])
                kb.memset("pool", WX[:], 0.0, w=["WX"])
                for c in range(4):
                    for hp in range(2):
                        kb.dma("sp", WA[hp * 64:(hp + 1) * 64, c, hp * 64:(hp + 1) * 64], rg_w_a[0, 2 * c + hp], r=(), w=["WA"])
                        kb.dma("sp", WX[hp * 64:(hp + 1) * 64, c, hp * 64:(hp + 1) * 64], rg_w_x[0, 2 * c + hp], r=(), w=["WX"])
                cl = kb.sb(pb_, "cl", [128, 4], F32)
                cl2 = kb.sb(pb_, "cl2", [128, 4], F32)
                kb.act(cl[:], vecA[:, 28:32], AF.Exp, r=["vecA"], w=["cl"], scale=-1.0)
                kb.act(cl[:], cl[:], AF.Ln, r=["cl"], w=["cl"], bias=1.0, scale=1.0)
                kb.ts("dve", cl2[:], cl[:], -16.0, None, ALU.mult, None, r=["cl"], w=["cl2"])
                kb.ts("dve", cl[:], cl[:], -8.0, None, ALU.mult, None, r=["cl", "cl2"], w=["cl"])
                xp = [kb.sb(pb_, "xp%d" % c, [128, 515], F32) for c in range(4)]
                xpss = [kb.sb(pb_, "xps%d" % k, [128, 16, 11], F32) for k in range(2)]
                hst = kb.sb(pb_, "hst", [128, 4], F32)
                h0T = kb.sb(pb_, "h0T", [128, 4, 16], F32)
                cvT = kb.sb(pb_, "cvT", [128, 4, 48], F32)
                hl = kb.sb(pb_, "hl", [128, 4, 16], F32)
                srh_sb = kb.sb(pb_, "srh_sb", [16, 512], F32)
                src_sb = kb.sb(pb_, "src_sb", [48, 512], F32)
                F5 = lambda nm: kb.sb(pb_, nm, [128, 512], F32)
                rgsets = [{nm: F5("%s%d" % (nm, k)) for nm in ("xc", "rr", "ii", "aa", "a2", "t5")} for k in range(2)]
                for c in range(4):
                    kb.memset("pool", xp[c][:, 0:3], 0.0, w=["xp%d" % c])
                kb.memset("pool", hst[:], 0.0, w=["hst0", "hst1", "hst2", "hst3"])
                kb.dma("sp", srh_sb[:], srh, r=(), w=["srh_sb"])
                kb.dma("sp", src_sb[:], srconv, r=(), w=["src_sb"])
                for c in range(4):
                    kb.tr(ps[6][:, c * 16:(c + 1) * 16], srh_sb[0:16, c * 128:(c + 1) * 128], C[0:16, 0:16], r=["srh_sb", "C"], w=[PS[6]], sig=(c == 3))
                kb.cp("dve", h0T[:].rearrange("p a b -> p (a b)"), ps[6][:, 0:64], r=[PS[6]], w=["h0T"])
                for c in range(4):
                    kb.tr(ps[6][:, 64 + c * 48:64 + (c + 1) * 48], src_sb[0:48, c * 128:(c + 1) * 128], C[0:48, 0:48], r=["src_sb", "C"], w=[PS[6]], sig=(c == 3))
                kb.cp("dve", cvT[:].rearrange("p a b -> p (a b)"), ps[6][:, 64:256], r=[PS[6]], w=["cvT"])
                def rg_unit(tg, t0, n, sample, hT, c, k):
                    u_ = tg * 4 + c
                    px, pgr = ps[(2 * u_) % 4], ps[(2 * u_ + 1) % 4]
                    PX, PGR = PS[(2 * u_) % 4], PS[(2 * u_ + 1) % 4]
                    S_ = rgsets[k]
                    xc, rr, ii, aa, a2, t5 = S_["xc"], S_["rr"], S_["ii"], S_["aa"], S_["a2"], S_["t5"]
                    uu, hh_ = a2, rr
                    gA, gX = (4, 5) if k == 0 else (6, 7)
                    xps = xpss[k]
                    nxps = "xps%d" % k
                    nx, nr, ni, na, n2, nt, nh = "xc%d" % k, "rr%d" % k, "ii%d" % k, "aa%d" % k, "a2%d" % k, "t5%d" % k, "hst%d" % c
                    for kc in range(8):
                        kb.mm(px[:, 0:n], lhsT=winB[:, kc, c * 128:(c + 1) * 128], rhs=hT[:, kc, 0:n], start=(kc == 0), stop=(kc == 7),
                              r=["winB", "hTg%d" % (tg % 2)], w=[PX])
                    for kc in range(8):
                        kb.mm(pgr[:, 0:n], lhsT=winB[:, kc, 512 + c * 128:512 + (c + 1) * 128], rhs=hT[:, kc, 0:n], start=(kc == 0), stop=(kc == 7),
                              r=["winB", "hTg%d" % (tg % 2)], w=[PGR])
                    cw = lambda j: vecA[:, c * 4 + j:c * 4 + j + 1]
                    cb = vecA[:, 16 + c:17 + c]
                    if not sample:
                        kb.cp("act", xp[c][:, 3:3 + n], px[:, 0:n], r=[PX], w=["xp%d" % c])
                        kb.ts("dve", xc[:, 0:n], xp[c][:, 0:n], cw(0), cb, ALU.mult, ALU.add, r=["xp%d" % c, "vecA"], w=[nx])
                        for j in range(1, 4):
                            kb.stt(xc[:, 0:n], xp[c][:, j:j + n], cw(j), xc[:, 0:n], ALU.mult, ALU.add, r=["xp%d" % c, "vecA", nx], w=[nx])
                        if tg != 3:
                            kb.cp("pool", xp[c][:, 0:3], xp[c][:, n:n + 3], r=["xp%d" % c], w=["xp%d" % c])
                    else:
                        kb.cp("dve", xps[:, :, 0:3], cvT[:, c, :].rearrange("p (q j) -> p q j", j=3), r=["cvT"], w=[nxps])
                        kb.cp("act", xps[:, :, 3:11], px[:, 0:n].rearrange("p (q t) -> p q t", t=8), r=[PX], w=[nxps])
                        xc3 = xc[:, 0:n].rearrange("p (q t) -> p q t", t=8)
                        kb.ts("dve", xc3, xps[:, :, 0:8], cw(0), cb, ALU.mult, ALU.add, r=[nxps, "vecA"], w=[nx])
                        for j in range(1, 4):
                            kb.stt(xc3, xps[:, :, j:j + 8], cw(j), xc3, ALU.mult, ALU.add, r=[nxps, "vecA", nx], w=[nx])
                        for j in range(3):
                            kb.tr(ps[gX][0:16, j * 128:(j + 1) * 128], xps[:, :, 8 + j], ident, r=[nxps, "C"], w=[PS[gX]], sig=(j == 2))
                        kb.cp("dve", t5[0:16, 0:384], ps[gX][0:16, 0:384], r=[PS[gX]], w=[nt])
                        kb.dma("sp", o_srconv.rearrange("(q j) f -> q j f", j=3)[:, :, c * 128:(c + 1) * 128],
                               t5[0:16, 0:384].rearrange("q (j f) -> q j f", f=128), r=[nt], w=())
                    kb.mm(ps[gA][:, 0:n], lhsT=WA[:, c, :], rhs=xc[:, 0:n], start=True, stop=True, r=["WA", nx], w=[PS[gA]])
                    kb.mm(ps[gX][:, 0:n], lhsT=WX[:, c, :], rhs=xc[:, 0:n], start=True, stop=True, r=["WX", nx], w=[PS[gX]])
                    kb.act(rr[:, 0:n], ps[gA][:, 0:n], AF.Sigmoid, r=[PS[gA], "vecA"], w=[nr], bias=vecA[:, 20 + c:21 + c], scale=1.0)
                    kb.act(ii[:, 0:n], ps[gX][:, 0:n], AF.Sigmoid, r=[PS[gX], "vecA"], w=[ni], bias=vecA[:, 24 + c:25 + c], scale=1.0)
                    kb.act(aa[:, 0:n], rr[:, 0:n], AF.Exp, r=[nr, "cl"], w=[na], scale=cl[:, c:c + 1])
                    kb.act(a2[:, 0:n], rr[:, 0:n], AF.Exp, r=[nr, "cl2"], w=[n2], scale=cl2[:, c:c + 1])
                    kb.act(a2[:, 0:n], a2[:, 0:n], AF.Sqrt, r=[n2], w=[n2], bias=1.0, scale=-1.0)
                    kb.tt("dve", uu[:, 0:n], a2[:, 0:n], ii[:, 0:n], ALU.mult, r=[n2, ni], w=[n2])
                    kb.tt("dve", uu[:, 0:n], uu[:, 0:n], xc[:, 0:n], ALU.mult, r=[n2, nx], w=[n2])
                    if not sample:
                        s.add("dve", lambda g_, c=c, n=n: g_.tensor_tensor_scan(out=hh_[:, 0:n], data0=aa[:, 0:n], data1=uu[:, 0:n], initial=hst[:, c:c + 1],
                                                                                 op0=ALU.mult, op1=ALU.add),
                              r=[na, n2, nh], w=[nr], tag="scanH")
                        kb.cp("dve", hst[:, c:c + 1], hh_[:, n - 1:n], r=[nr], w=[nh])
                    else:
                        aa3 = aa[:, 0:n].rearrange("p (q t) -> p q t", t=8)
                        uu3 = uu[:, 0:n].rearrange("p (q t) -> p q t", t=8)
                        kb.tt("dve", t5[:, 0:16].unsqueeze(2), aa3[:, :, 0:1], h0T[:, c, :].unsqueeze(2), ALU.mult, r=[na, "h0T", nt], w=[nt])
                        kb.tt("dve", uu3[:, :, 0:1], uu3[:, :, 0:1], t5[:, 0:16].unsqueeze(2), ALU.add, r=[n2, nt], w=[n2])
                        kb.tt("dve", aa[:, 0:n], aa[:, 0:n], rst, ALU.mult, r=[na, "C"], w=[na])
                        s.add("dve", lambda g_, n=n: g_.tensor_tensor_scan(out=hh_[:, 0:n], data0=aa[:, 0:n], data1=uu[:, 0:n], initial=0.0,
                                                                            op0=ALU.mult, op1=ALU.add),
                              r=[na, n2], w=[nr], tag="scanH")
                        kb.cp("dve", hl[:, c, :].unsqueeze(2), hh_[:, 0:n].rearrange("p (q t) -> p q t", t=8)[:, :, 7:8], r=[nr], w=["hl"])
                    kb.act(t5[:, 0:n], pgr[:, 0:n], AF.Square, r=[PGR, nt], w=[nt])
                    kb.ts("dve", t5[:, 0:n], t5[:, 0:n], 0.044715, 1.0, ALU.mult, ALU.add, r=[nt], w=[nt])
                    kb.tt("dve", t5[:, 0:n], t5[:, 0:n], pgr[:, 0:n], ALU.mult, r=[nt, PGR], w=[nt])
                    kb.act(t5[:, 0:n], t5[:, 0:n], AF.Tanh, r=[nt], w=[nt], scale=0.7978845608028654)
                    kb.ts("dve", t5[:, 0:n], t5[:, 0:n], 1.0, 0.5, ALU.add, ALU.mult, r=[nt], w=[nt])
                    kb.tt("dve", t5[:, 0:n], t5[:, 0:n], pgr[:, 0:n], ALU.mult, r=[nt, PGR], w=[nt])
                    kb.tt("dve", hrT[:, c, t0:t0 + n], t5[:, 0:n], hh_[:, 0:n], ALU.mult, r=[nt, nr], w=["hrT_g%d" % tg])

                for tg, (t0, n) in enumerate(TGS):
                    sample = tg == 4
                    hT = hTgB[tg % 2]
                    for i in (range(4 * tg, 4 * tg + 4) if tg < 4 else [16]):
                        make_hT(hT, i, prescale=True, col0=(i % 4) * 128, res="hTg%d" % (tg % 2))
                    for c0 in (0, 2):
                        s.capture = []
                        rg_unit(tg, t0, n, sample, hT, c0 + 1, 1)
                        pend_rg = s.capture
                        s.capture = None
                        s.tick_fn = lambda pr=pend_rg: (s.add(*pr.pop(0)) if pr else None)
                        rg_unit(tg, t0, n, sample, hT, c0, 0)
                        s.tick_fn = None
                        while pend_rg:
                            s.add(*pend_rg.pop(0))
                    if tg == 3:
                        for c in range(4):
                            kb.tr(ps[6][0:3, c * 128:(c + 1) * 128], xp[c][:, 512:515], ident, r=["xp%d" % c, "C"], w=[PS[6]], sig=(c == 3))
                        t5 = rgsets[0]["t5"]
                        kb.cp("dve", t5[0:3, :], ps[6][0:3, 0:512], r=[PS[6]], w=["t50"])
                        kb.dma("sp", o_prconv, t5[0:3, :], r=["t50"], w=())
                rr, ii = rgsets[0]["rr"], rgsets[0]["ii"]
                kb.tr(ps[7][0:4, 0:128], hst[:, 0:4], ident, r=["hst0", "hst1", "hst2", "hst3", "C"], w=[PS[7]])
                kb.cp("dve", rr[0:4, 0:128], ps[7][0:4, 0:128], r=[PS[7]], w=["rr0"])
                kb.dma("sp", o_prh, rr[0:4, 0:128], r=["rr0"], w=())
                for c in range(4):
                    kb.tr(ps[4][0:16, c * 128:(c + 1) * 128], hl[:, c, :], ident, r=["hl", "C"], w=[PS[4]], sig=(c == 3))
                kb.cp("dve", ii[0:16, :], ps[4][0:16, :], r=[PS[4]], w=["ii0"])
                kb.dma("sp", o_srh, ii[0:16, :], r=["ii0"], w=())
                s.barrier()
            stop_at(7)
            with contextlib.ExitStack() as pc_:
                alloc_gl(pc_)
                mod_prepare(l, 1, 1.0, blocks=[4, 5])
                Gp, Gs = gl["Gp"], gl["Gs"]
                wout = kb.sb(pc_, "wout", [128, 8, D], BF16)
                kb.dma("pool", wout[:], ab_w_out[0].rearrange("(kc p) n -> p kc n", p=128), r=(), w=["wout"])
                proj_acc(wout, "wout", 8, lambda i, kc: ((hmT[:, kc, i * 128:(i + 1) * 128], "hmT%d" % i) if kc < 4 else
                                                          (hrT[:, kc - 4, i * 128:(i + 1) * 128], "hrT_g%d" % (i // 4))), True, True)
                s.barrier()
        s.barrier()

    CW = -0.6065306597126334
    RT = BF16

    def rwkv_mixer(l):
        rwkv_mixer_(l)
        s.skip = False
        s.barrier()

    def rwkv_mixer_(l):
        rw_mu = kb.din("muT", [128, 48])
        rw_wr = kb.din("rw_wr", [1, D, D])
        rw_wk = kb.din("rw_wk", [1, D, D])
        rw_wv = kb.din("rw_wv", [1, D, D])
        rw_wo = kb.din("rw_wo", [1, D, D])
        rw_w0 = kb.din("rw_w0", [1, D])
        rw_w1 = kb.din("rw_w1", [1, D, 64])
        rw_w2 = kb.din("rw_w2", [1, 64, D])
        rw_a0 = kb.din("rw_a0", [1, D])
        rw_a1 = kb.din("rw_a1", [1, D, 64])
        rw_a2 = kb.din("rw_a2", [1, 64, D])
        rw_g1 = kb.din("rw_g1", [1, D, 128])
        rw_g2 = kb.din("rw_g2", [1, 128, D])
        rw_kk = kb.din("rw_k_k", [1, D])
        rw_ka = kb.din("rw_k_a", [1, D])
        rw_rk = kb.din("rk_flat", [1, D])
        rw_lng = kb.din("rw_lnx_g", [1, D])
        rw_lnb = kb.din("rw_lnx_b", [1, D])
        swkv = kb.din("swkv", [16, 16, 64, 64])
        sshift = kb.din("sshift", [16, D])
        o_pwkv = kb.dout("o_pwkv", [16, 64, 64])
        o_pshift = kb.dout("o_pshift", [1, D])
        o_swkv = kb.dout("o_swkv", [16, 16, 64, 64])
        o_sshift = kb.dout("o_sshift", [16, D])

        cm = lambda nm: C[:, CST_OFF[nm]:CST_OFF[nm] + 128]
        low16, up16, m16, m32, m64 = cm("low16"), cm("up16"), cm("m16"), cm("m32"), cm("m64")
        maskP, maskS, upP, lowP, upS, lowS, blkS, ones, bms = cm("maskP"), cm("maskS"), cm("upP"), cm("lowP"), cm("upS"), cm("lowS"), cm("blkS"), cm("ones"), C[:, CST_OFF["bms"]:CST_OFF["bms"] + 16]

        mod_prepare(l, 1, 1.0, blocks=range(4))
        with contextlib.ExitStack() as ph:
            ygT = kb.sb(ph, "ygT", [128, 8, NTOK], BF16)
            muT = kb.sb(ph, "muT_sb", [128, 48], F32)
            kb.dma("sp", muT[:], rw_mu, r=(), w=["muT"])
            identR = kb.sb(ph, "identR", [128, 128], RT)
            kb.cp("dve", identR[:], ident, r=["C"], w=["identR"])
            w1b = kb.sb(ph, "w1b", [128, 8, 64], BF16)
            a1b = kb.sb(ph, "a1b", [128, 8, 64], BF16)
            g1b = kb.sb(ph, "g1b", [128, 8, 128], BF16)
            kb.dma("pool", w1b[:], rw_w1[0].rearrange("(kc p) n -> p kc n", p=128), r=(), w=["w1b"])
            kb.dma("pool", a1b[:], rw_a1[0].rearrange("(kc p) n -> p kc n", p=128), r=(), w=["a1b"])
            kb.dma("pool", g1b[:], rw_g1[0].rearrange("(kc p) n -> p kc n", p=128), r=(), w=["g1b"])
            w1m = kb.sb(ph, "w1m", [128, 8, 64], BF16)
            a1m = kb.sb(ph, "a1m", [128, 8, 64], BF16)
            g1m = kb.sb(ph, "g1m", [128, 8, 128], BF16)
            for wm_, wb_, rs_, j_, n_ in ((w1m, w1b, "w1b", 1, 64), (a1m, a1b, "a1b", 4, 64), (g1m, g1b, "g1b", 5, 128)):
                kb.tt("dve", wm_[:], wb_[:], muT[:, j_ * 8:(j_ + 1) * 8].unsqueeze(2).to_broadcast([128, 8, n_]), ALU.mult, r=[rs_, "muT"], w=[rs_ + "m"])
            hlast = kb.sb(ph, "hlast", [128, 8, 17], F32)
            sh0T = kb.sb(ph, "sh0T", [128, 8, 16], BF16)
            with contextlib.ExitStack() as p0:
                shs = kb.sb(p0, "shs", [16, D], F32)
                kb.dma("sp", shs[:], sshift, r=(), w=["shs"])
                for c in range(8):
                    kb.tr(ps[0][:, c * 16:(c + 1) * 16], shs[0:16, c * 128:(c + 1) * 128], C[0:16, 0:16], r=["shs", "C"], w=[PS[0]], sig=(c == 7))
                kb.cp("dve", sh0T[:].rearrange("p a b -> p (a b)"), ps[0][:, 0:128], r=[PS[0]], w=["sh0T"])
                s.barrier()

            def hook_last(i, c, src, psres):
                if i == 15:
                    kb.ts("dve", hlast[:, c, 0:1], src[:, 127:128], modT[:, 8 + c, 0:1], modT[:, c, 0:1], ALU.mult, ALU.add, r=["modT", psres], w=["hlast"])
                elif i == 16:
                    v3 = src.rearrange("p (q t) -> p q t", t=8)[:, :, 7:8]
                    kb.tt("dve", hlast[:, c, 1:17].unsqueeze(2), v3, modT[:, 8 + c, 1:NT].unsqueeze(2), ALU.mult, r=["modT", psres], w=["hlast"])
                    kb.tt("dve", hlast[:, c, 1:17], hlast[:, c, 1:17], modT[:, c, 1:NT], ALU.add, r=["hlast", "modT"], w=["hlast"])

            stop_at(51)
            for hg in range(4):
                c0 = hg * 256
                with contextlib.ExitStack() as pp:
                    wrs = kb.sb(pp, "wrs", [128, 8, 256], BF16)
                    wks = kb.sb(pp, "wks", [128, 8, 256], BF16)
                    wvs = kb.sb(pp, "wvs", [128, 8, 256], BF16)
                    for wt, src in ((wrs, rw_wr), (wks, rw_wk), (wvs, rw_wv)):
                        kb.dma("pool", wt[:], src[0].rearrange("(kc p) n -> p kc n", p=128)[:, :, c0:c0 + 256], r=(), w=["wqkv"])
                    wrm = kb.sb(pp, "wrm", [128, 8, 256], BF16)
                    wkm = kb.sb(pp, "wkm", [128, 8, 256], BF16)
                    wvm = kb.sb(pp, "wvm", [128, 8, 256], BF16)
                    for wm_, wt_, j_ in ((wrm, wrs, 0), (wkm, wks, 2), (wvm, wvs, 3)):
                        kb.tt("dve", wm_[:], wt_[:], muT[:, j_ * 8:(j_ + 1) * 8].unsqueeze(2).to_broadcast([128, 8, 256]), ALU.mult, r=["wqkv", "muT"], w=["wqkvm"])
                    w2s = kb.sb(pp, "w2s", [64, 256], BF16)
                    a2s = kb.sb(pp, "a2s", [64, 256], BF16)
                    g2s = kb.sb(pp, "g2s", [128, 256], BF16)
                    kb.dma("pool", w2s[:], rw_w2[0, :, c0:c0 + 256], r=(), w=["w2s"])
                    kb.dma("pool", a2s[:], rw_a2[0, :, c0:c0 + 256], r=(), w=["a2s"])
                    kb.dma("pool", g2s[:], rw_g2[0, :, c0:c0 + 256], r=(), w=["g2s"])
                    w0r = kb.sb(pp, "w0r", [1, 256], F32)
                    a0r = kb.sb(pp, "a0r", [1, 256], F32)
                    kb.dma("sp", w0r[:], rw_w0[0:1, c0:c0 + 256], r=(), w=["w0r"])
                    kb.dma("sp", a0r[:], rw_a0[0:1, c0:c0 + 256], r=(), w=["a0r"])
                    bcs = {}
                    alias = {"kkb": tA[2][:, 0:256], "kab": tA[2][:, 256:512], "rkb": tA[3][:, 0:256], "lngb": tA[3][:, 256:512], "lnbb": tA[0][:, 256:512]}
                    for nm, src in (("kkb", rw_kk), ("kab", rw_ka), ("rkb", rw_rk), ("lngb", rw_lng), ("lnbb", rw_lnb)):
                        bcs[nm] = alias[nm] if nm in alias else kb.sb(pp, nm, [128, 256], F32)
                        kb.dma("sp", bcs[nm][:], src[0:1, c0:c0 + 256].to_broadcast([128, 256]), r=(), w=[nm])
                    hTg = [kb.sb(pp, "hTr%d" % i, [128, 8, 130], BF16) for i in range(2)]
                    dx = kb.sb(pp, "dx", [128, 8, 128], BF16)
                    loT = kb.sb(pp, "loT", [128, 3, 128], BF16)
                    TB = lambda j: tB[j // 4][:, (j % 4) * 256:(j % 4) * 256 + 256]
                    Rr, Kk, KKn, Aa, SG, CSs, Ee, Tt = [TB(j) for j in range(8)]
                    tbres = lambda j: "tB%d" % (j // 4)
                    small = kb.sb(pp, "rwsmall", [128, 64], F32)
                    F3 = lambda nm: kb.sb(pp, nm, [128, 256], RT)
                    Vvs, ALs, RBs, KHs, BHs = [[F3("%s%d" % (nm, k)) for k in range(2)] for nm in ("Vv", "AL", "RB", "KH", "BH")]
                    BT, KT = F3("BTt"), F3("KTt")
                    GVs = [tA[0], kb.sb(pp, "GV1", [128, 256], F32)]
                    PLcs = [kb.sb(pp, "PLc%d" % k, [64, 64], F32) for k in range(2)]
                    smallfs = [kb.sb(pp, "smallf%d" % k, [128, 8], F32) for k in range(2)]
                    Tbk = tA[1][:, 256:512]
                    fmalls = [kb.sb(pp, "fmall%d" % k, [64, 4, 4, 128], RT) for k in range(2)]
                    fmTs = [{nm: fm_[:, j] for j, nm in enumerate(("alT", "btT", "ktT", "rbT"))} for fm_ in fmalls]
                    fmall = fmalls[0]
                    chainall = kb.sb(pp, "chainall", [128, 7, 4, 128], RT)
                    chn = {nm: chainall[:, j] for j, nm in enumerate(("ApA", "ApB", "BpA", "BpB", "TTa", "TTb", "Am"))}
                    S0nat = kb.sb(pp, "S0nat", [64, 16, 64], F32)
                    S0Tq = fmall[:, 0:2].rearrange("p a h t -> p (a h t)").rearrange("p (q k) -> p q k", k=64)
                    SLo = S0nat
                    AakT = kb.sb(pp, "AakT", [128, 4, 128], RT)
                    YTs = AakT[0:64, :, :]
                    ArbT = kb.sb(pp, "ArbT", [128, 4, 128], RT)
                    ArkT = kb.sb(pp, "ArkT", [128, 4, 128], RT)
                    Ahat = kb.sb(pp, "Ahat", [128, 4, 64], RT)
                    X1 = kb.sb(pp, "X1", [128, 4, 64], RT)
                    U0 = kb.sb(pp, "U0", [128, 4, 64], RT)
                    Gm = kb.sb(pp, "Gm", [64, 4, 64], RT)
                    RhT = kb.sb(pp, "RhT", [64, 4, 128], RT)
                    S0T = kb.sb(pp, "S0T", [64, 4, 64], RT)
                    yb = tA[1][:, 0:256]
                    kb.ts("dve", S0T[:].rearrange("p a b -> p (a b)"), C[0:64, 0:256], 0.0, None, ALU.mult, None, r=["C"], w=["S0T0", "S0T1", "S0T2", "S0T3"])
                    kb.memset("pool", hTg[1][:, :, 128:129], 0.0, w=["hTr1"])

                    algb = [4]

                    def nb():
                        b_ = algb[0]
                        algb[0] = 4 + (algb[0] - 3) % 4
                        return b_

                    def grp4(mmf, n_cols, m_rows=128):
                        b_ = nb()
                        for hl in range(4):
                            items = mmf(hl)
                            for j, (lt, rh, rd) in enumerate(items):
                                kb.mm(ps[b_][0:m_rows, hl * n_cols:(hl + 1) * n_cols], lhsT=lt, rhs=rh, start=(j == 0), stop=(j == len(items) - 1),
                                      r=rd, w=[PS[b_]], sig=(hl == 3 and j == len(items) - 1))
                        return b_

                    fsl = lambda t_, hl: t_[:, hl, :]
                    tsl = lambda t_, hl: t_[:, hl * 64:(hl + 1) * 64]

                    def sample_states():
                        BH, KH, Vv, PLc = BHs[0], KHs[0], Vvs[0], PLcs[0]
                        Bhm = chainall[:, 0:2].rearrange("p a h t -> p (a h t)").rearrange("p (q k) -> p q k", k=64)
                        Khm = chainall[:, 2:4].rearrange("p a h t -> p (a h t)").rearrange("p (q k) -> p q k", k=64)
                        Gq = chainall[0:64, 4:6].rearrange("p a h t -> p (a h t)").rearrange("p (q k) -> p q k", k=64)
                        bmq = bms.unsqueeze(2).to_broadcast([128, 16, 64])
                        for hl in range(4):
                            h = hg * 4 + hl
                            kb.tt("dve", Bhm, tsl(BH, hl).unsqueeze(1).to_broadcast([128, 16, 64]), bmq, ALU.mult, r=["BH0", "C"], w=["ApA0", "ApA1", "ApA2", "ApA3"] + ["ApB0", "ApB1", "ApB2", "ApB3"])
                            kb.tt("pool", Khm, tsl(KH, hl).unsqueeze(1).to_broadcast([128, 16, 64]), bmq, ALU.mult, r=["KH0", "C"], w=["BpA0", "BpA1", "BpA2", "BpA3"] + ["BpB0", "BpB1", "BpB2", "BpB3"])
                            kb.dma("sp", S0nat[:], swkv[:, h].rearrange("q v k -> v q k"), r=(), w=["S0nat"])
                            b0, b1 = nb(), nb()
                            for q in range(16):
                                bb = b0 if q < 8 else b1
                                kb.tr(ps[bb][0:64, (q % 8) * 64:(q % 8 + 1) * 64], S0nat[:, q, :], ident[0:64, 0:64], r=["S0nat", "C"], w=[PS[bb]], sig=(q % 8 == 7))
                            kb.cp("act", S0Tq[:, 0:8, :], ps[b0][0:64, :].rearrange("p (q k) -> p q k", k=64), r=[PS[b0]], w=["S0Tq", "alT0", "btT0"])
                            kb.cp("dve", S0Tq[:, 8:16, :], ps[b1][0:64, :].rearrange("p (q k) -> p q k", k=64), r=[PS[b1]], w=["S0Tq", "alT0", "btT0"])
                            b_ = nb()
                            for q in range(16):
                                kb.mm(ps[b_][0:64, q * 8:(q + 1) * 8], lhsT=S0Tq[:, q, :], rhs=RhT[:, hl, q * 8:(q + 1) * 8], start=True, stop=True,
                                      r=["S0Tq", "RhT%d" % hl], w=[PS[b_]], sig=(q == 15))
                            kb.cp("act", YTs[:, hl, :], ps[b_][0:64, 0:128], r=[PS[b_]], w=["AakT%d" % hl])
                            g0, g1 = nb(), nb()
                            for half, bb in ((0, g0), (1, g1)):
                                kb.mm(ps[bb][0:64, :], lhsT=Ahat[:, hl, :], rhs=Bhm[:, half * 8:(half + 1) * 8, :], start=True, stop=True,
                                      r=["Ahat%d" % hl] + ["ApA0", "ApA1", "ApA2", "ApA3"] + ["ApB0", "ApB1", "ApB2", "ApB3"], w=[PS[bb]])
                            for q in range(16):
                                bb = g0 if q < 8 else g1
                                kb.stt(Gq[:, q, :], ident[0:64, 0:64], PLc[:, hl * 16 + q:hl * 16 + q + 1], ps[bb][0:64, (q % 8) * 64:(q % 8 + 1) * 64], ALU.mult, ALU.add,
                                       r=["C", "PLc0", PS[bb]], w=["TTa0", "TTa1", "TTa2", "TTa3"] + ["TTb0", "TTb1", "TTb2", "TTb3"])
                            for half in range(2):
                                bb = nb()
                                kb.mm(ps[bb][0:64, :], lhsT=tsl(Vv, hl), rhs=Khm[:, half * 8:(half + 1) * 8, :], start=True, stop=False, r=["Vv0"] + ["BpA0", "BpA1", "BpA2", "BpA3"] + ["BpB0", "BpB1", "BpB2", "BpB3"], w=[PS[bb]], sig=False)
                                kb.mm(ps[bb][0:64, :], lhsT=U0[:, hl, :], rhs=Bhm[:, half * 8:(half + 1) * 8, :], start=False, stop=False, r=["U0%d" % hl] + ["ApA0", "ApA1", "ApA2", "ApA3"] + ["ApB0", "ApB1", "ApB2", "ApB3"], w=[PS[bb]], sig=False)
                                for qq in range(8):
                                    q = half * 8 + qq
                                    kb.mm(ps[bb][0:64, qq * 64:(qq + 1) * 64], lhsT=S0Tq[:, q, :], rhs=Gq[:, q, :], start=False, stop=(qq == 7),
                                          r=["S0Tq"] + ["TTa0", "TTa1", "TTa2", "TTa3"] + ["TTb0", "TTb1", "TTb2", "TTb3"], w=[PS[bb]], sig=(qq == 7))
                                kb.cp("act" if half else "dve", SLo[:, half * 8:(half + 1) * 8, :], ps[bb][0:64, :].rearrange("p (q k) -> p q k", k=64), r=[PS[bb]], w=["S0nat"])
                            kb.dma("sp", o_swkv[:, h].rearrange("q v k -> v q k"), SLo[:], r=["S0nat"], w=())
                        by = grp4(lambda hl: [(ArkT[:, hl, :], tsl(Vv, hl), ["ArkT%d" % hl, "Vv0"]), (ArbT[:, hl, :], U0[:, hl, :], ["ArbT%d" % hl, "U0%d" % hl]),
                                               (YTs[:, hl, :], identR[0:64, 0:64], ["AakT%d" % hl, "identR"])], 64)
                        kb.cp("act", yb[:], ps[by][:, 0:256], r=[PS[by]], w=["tA1"])

                    def front(i):
                        sample = i == 16
                        p_ = i % 2
                        AL, Vv, RB, KH, BH = ALs[p_], Vvs[p_], RBs[p_], KHs[p_], BHs[p_]
                        fmT, PLc, smallf = fmTs[p_], PLcs[p_], smallfs[p_]
                        Gt = GVs[p_][:, 0:256]
                        rAL, rVv, rRB, rKH, rBH = "AL%d" % p_, "Vv%d" % p_, "RB%d" % p_, "KH%d" % p_, "BH%d" % p_
                        ralT, rbtT, rktT, rrbT = "alT%d" % p_, "btT%d" % p_, "ktT%d" % p_, "rbT%d" % p_
                        rPL, rsf, rGV = "PLc%d" % p_, "smallf%d" % p_, "GV%d" % p_
                        sample = i == 16
                        k_ = i % 2
                        hT = hTg[k_]
                        hres = "hTr%d" % k_
                        last_pass = hg == 3
                        if hg == 0 and i == 1:
                            stop_at(58)
                        if hg == 0 and i == 16:
                            stop_at(59)
                        make_hT(hT, i, prescale=last_pass, col0=1, res=hres, hook=(hook_last if hg == 0 else None))
                        yield
                        if i == 0:
                            kb.memset("pool", hT[:, :, 0:1], 0.0, w=[hres])
                        elif not sample:
                            kb.cp("pool", hT[:, :, 0:1], hTg[1 - k_][:, :, 128:129], r=["hTr%d" % (1 - k_)], w=[hres])
                        cur = hT[:, :, 1:129]
                        if not sample:
                            kb.tt("dve", dx[:], hT[:, :, 0:128], cur, ALU.subtract, r=[hres], w=["dx"])
                        else:
                            kb.cp("pool", dx[:], hT[:, :, 0:128], r=[hres], w=["dx"])
                            kb.cp("pool", dx[:].rearrange("p c (q t) -> p c q t", t=8)[:, :, :, 0], sh0T[:], r=["sh0T"], w=["dx"])
                            kb.tt("pool", dx[:], dx[:], cur, ALU.subtract, r=["dx", hres], w=["dx"])
                        yield
                        for wt, wm_, bank, off in ((wrs, wrm, 0, 0), (wks, wkm, 0, 256), (wvs, wvm, 1, 0)):
                            for kc in range(8):
                                kb.mm(ps[bank][:, off:off + 256], lhsT=hT[:, kc, 1:129], rhs=wt[:, kc, :], start=(kc == 0), stop=False, r=[hres, "wqkv"], w=[PS[bank]], sig=False)
                            for kc in range(8):
                                kb.mm(ps[bank][:, off:off + 256], lhsT=dx[:, kc, :], rhs=wm_[:, kc, :], start=False, stop=(kc == 7), r=["dx", "wqkvm"], w=[PS[bank]])
                        for wt, wm_, wr_, m_, off3 in ((w1b, w1m, "w1b", 64, 0), (a1b, a1m, "a1b", 64, 128), (g1b, g1m, "g1b", 128, 256)):
                            for kc in range(8):
                                kb.mm(ps[3][0:m_, off3:off3 + 128], lhsT=wt[:, kc, :], rhs=hT[:, kc, 1:129], start=(kc == 0), stop=False, r=[hres, wr_], w=[PS[3]], sig=False)
                            for kc in range(8):
                                kb.mm(ps[3][0:m_, off3:off3 + 128], lhsT=wm_[:, kc, :], rhs=dx[:, kc, :], start=False, stop=(kc == 7), r=["dx", wr_ + "m"], w=[PS[3]])
                        kb.act(loT[0:64, 0, :], ps[3][0:64, 0:128], AF.Tanh, r=[PS[3]], w=["loT"])
                        yield
                        kb.cp("act", loT[0:64, 1, :], ps[3][0:64, 128:256], r=[PS[3]], w=["loT"])
                        kb.act(loT[:, 2, :], ps[3][:, 256:384], AF.Sigmoid, r=[PS[3]], w=["loT"])
                        kb.mm(ps[1][:, 256:512], lhsT=loT[0:64, 0, :], rhs=w2s[:], start=True, stop=False, r=["loT", "w2s"], w=[PS[1]], sig=False)
                        yield
                        kb.mm(ps[1][:, 256:512], lhsT=ones[0:1, :], rhs=w0r[:], start=False, stop=True, r=["C", "w0r"], w=[PS[1]])
                        kb.mm(ps[2][:, 0:256], lhsT=loT[0:64, 1, :], rhs=a2s[:], start=True, stop=False, r=["loT", "a2s"], w=[PS[2]], sig=False)
                        kb.mm(ps[2][:, 0:256], lhsT=ones[0:1, :], rhs=a0r[:], start=False, stop=True, r=["C", "a0r"], w=[PS[2]])
                        yield
                        kb.mm(ps[2][:, 256:512], lhsT=loT[:, 2, :], rhs=g2s[:], start=True, stop=True, r=["loT", "g2s"], w=[PS[2]])
                        if hg == 0 and i == 0:
                            stop_at(52)
                        kb.cp("act", Rr, ps[0][:, 0:256], r=[PS[0]], w=[tbres(0)])
                        yield
                        kb.cp("act", Kk, ps[0][:, 256:512], r=[PS[0]], w=[tbres(1)])
                        kb.cp("act", Vv[:], ps[1][:, 0:256], r=[PS[1]], w=[rVv])
                        yield
                        kb.act(SG, ps[1][:, 256:512], AF.Sigmoid, r=[PS[1]], w=[tbres(4)])
                        kb.act(Aa, ps[2][:, 0:256], AF.Sigmoid, r=[PS[2]], w=[tbres(3)])
                        kb.cp("act", Gt, ps[2][:, 256:512], r=[PS[2]], w=[rGV])
                        yield
                        kb.tt("dve", KKn, Kk, bcs["kkb"][:], ALU.mult, r=[tbres(1), "kkb"], w=[tbres(2)])
                        kb.tt("dve", Tt, KKn, KKn, ALU.mult, r=[tbres(2)], w=[tbres(7)])
                        s.add("dve", lambda g_, o_=smallf[:, 0:4], i_=Tt.rearrange("p (h k) -> p h k", k=64): g_.tensor_reduce(out=o_, in_=i_, axis=AX.X, op=ALU.add),
                              r=[tbres(7)], w=[rsf], tag="red")
                        yield
                        kb.ts("dve", smallf[:, 0:4], smallf[:, 0:4], 1e-24, None, ALU.max, None, r=[rsf], w=[rsf])
                        kb.act(smallf[:, 0:4], smallf[:, 0:4], AF.Ln, r=[rsf], w=[rsf])
                        kb.act(smallf[:, 0:4], smallf[:, 0:4], AF.Exp, r=[rsf], w=[rsf], scale=-0.5)
                        yield
                        kb.tt("dve", KKn.rearrange("p (h k) -> p h k", k=64), KKn.rearrange("p (h k) -> p h k", k=64),
                              smallf[:, 0:4].unsqueeze(2).to_broadcast([128, 4, 64]), ALU.mult, r=[tbres(2), rsf], w=[tbres(2)])
                        kb.stt(Tt, Aa, -1.0, bcs["kab"][:], ALU.add, ALU.mult, r=[tbres(3), "kab"], w=[tbres(7)])
                        kb.tt("dve", Tt, Tt, Kk, ALU.mult, r=[tbres(7), tbres(1)], w=[tbres(7)])
                        yield
                        kb.tt("dve", Kk, Kk, Tt, ALU.add, r=[tbres(1), tbres(7)], w=[tbres(1)])
                        kb.tt("dve", Tt, Rr, Kk, ALU.mult, r=[tbres(0), tbres(1)], w=[tbres(7)])
                        kb.tt("dve", Tt, Tt, bcs["rkb"][:], ALU.mult, r=[tbres(7), "rkb"], w=[tbres(7)])
                        yield
                        s.add("dve", lambda g_, o_=smallf[:, 4:8], i_=Tt.rearrange("p (h k) -> p h k", k=64): g_.tensor_reduce(out=o_, in_=i_, axis=AX.X, op=ALU.add),
                              r=[tbres(7)], w=[rsf], tag="red")
                        kb.tt("dve", Aa, KKn, Aa, ALU.mult, r=[tbres(2), tbres(3)], w=[tbres(3)])
                        if hg == 0 and i == 0:
                            stop_at(53)
                        yield
                        Um, Jm = (maskP, ones) if not sample else (maskS, blkS)
                        kb.mm(ps[0][:, 0:256], lhsT=Um, rhs=SG, start=True, stop=True, r=["C", tbres(4)], w=[PS[0]])
                        kb.mm(ps[0][:, 256:512], lhsT=Jm, rhs=SG, start=True, stop=True, r=["C", tbres(4)], w=[PS[0]])
                        yield
                        nq = 1 if not sample else 16
                        for hl in range(4):
                            kb.mm(ps[3][0:64, 384 + hl * nq:384 + (hl + 1) * nq], lhsT=SG[:, hl * 64:(hl + 1) * 64], rhs=(ones[:, 0:1] if not sample else bms),
                                  start=True, stop=True, r=[tbres(4), "C"], w=[PS[3]], sig=(hl == 3))
                        kb.act(PLc[:, 0:4 * nq], ps[3][0:64, 384:384 + 4 * nq], AF.Exp, r=[PS[3]], w=[rPL], scale=CW)
                        yield
                        kb.cp("act", CSs, ps[0][:, 0:256], r=[PS[0]], w=[tbres(5)])
                        kb.tt("dve", Tt, CSs, SG, ALU.subtract, r=[tbres(5), tbres(4)], w=[tbres(7)])
                        kb.act(Ee, Tt, AF.Exp, r=[tbres(7)], w=[tbres(6)], scale=CW)
                        yield
                        kb.stt(AL[:], KKn, -1.0, Ee, ALU.mult, ALU.mult, r=[tbres(2), tbres(6)], w=[rAL])
                        kb.act(Ee, CSs, AF.Exp, r=[tbres(5), rAL], w=[tbres(6)], scale=-CW)
                        kb.tt("dve", BT[:], Aa, Ee, ALU.mult, r=[tbres(3), tbres(6)], w=["BTt"])
                        yield
                        kb.tt("dve", KT[:], Kk, Ee, ALU.mult, r=[tbres(1), tbres(6)], w=["X1kt"])
                        kb.act(Tt, CSs, AF.Exp, r=[tbres(5)], w=[tbres(7)], scale=CW)
                        kb.tt("dve", RB[:], Rr, Tt, ALU.mult, r=[tbres(0), tbres(7)], w=[rRB])
                        yield
                        kb.tt("dve", Ee, ps[0][:, 256:512], CSs, ALU.subtract, r=[PS[0], tbres(5), rGV, "X1kt"], w=[tbres(6)])
                        kb.act(Ee, Ee, AF.Exp, r=[tbres(6)], w=[tbres(6)], scale=CW)
                        kb.tt("dve", KH[:], Kk, Ee, ALU.mult, r=[tbres(1), tbres(6)], w=[rKH])
                        yield
                        kb.tt("dve", BH[:], Aa, Ee, ALU.mult, r=[tbres(3), tbres(6)], w=[rBH])
                        for qi, (nm, src, sr) in enumerate((("alT", AL, rAL), ("btT", BT, "BTt"), ("ktT", KT, "X1kt"), ("rbT", RB, rRB))):
                            b_ = 1 + qi % 2
                            for hl in range(4):
                                kb.tr(psb[b_][0:64, hl * 128:(hl + 1) * 128], src[:, hl * 64:(hl + 1) * 64], identR[:],
                                      r=[sr, "identR"], w=[PS[b_]], sig=(hl == 3))
                            kb.cp("act" if qi % 2 else "dve", fmT[nm][:].rearrange("p a b -> p (a b)"), psb[b_][0:64, 0:512], r=[PS[b_]], w=["%s%d" % (nm, p_)])
                        if hg == 0 and i == 0:
                            stop_at(54)

                    def back(i, fg):
                        sample = i == 16
                        p_ = i % 2
                        AL, Vv, RB, KH, BH = ALs[p_], Vvs[p_], RBs[p_], KHs[p_], BHs[p_]
                        fmT, PLc, smallf = fmTs[p_], PLcs[p_], smallfs[p_]
                        Gt = GVs[p_][:, 0:256]
                        rAL, rVv, rRB, rKH, rBH = "AL%d" % p_, "Vv%d" % p_, "RB%d" % p_, "KH%d" % p_, "BH%d" % p_
                        ralT, rbtT, rktT, rrbT = "alT%d" % p_, "btT%d" % p_, "ktT%d" % p_, "rbT%d" % p_
                        rPL, rsf, rGV = "PLc%d" % p_, "smallf%d" % p_, "GV%d" % p_
                        alT, btT, ktT, rbT = fmT["alT"], fmT["btT"], fmT["ktT"], fmT["rbT"]
                        mlow, mup, minc = (lowP, upP, maskP) if not sample else (lowS, upS, maskS)
                        n_it = 3 if not sample else 2

                        def head_alg(hl):
                            B_ = 4 + hl
                            P_ = PS[B_]
                            rn = lambda nm: "%s%d" % (nm, hl)
                            pw = ps[B_][:, 0:128]

                            def mm1(lt, rh, rd, cols=128, rows=128, first=True, last=True):
                                kb.mm(ps[B_][0:rows, 0:cols], lhsT=lt, rhs=rh, start=first, stop=last, r=rd, w=[P_], sig=last)

                            mm1(fsl(alT, hl), fsl(btT, hl), [ralT, rbtT])
                            if not sample:
                                kb.tt("dve", chn["Am"][:, hl, :], pw, lowP, ALU.mult, r=[P_, "C"], w=[rn("Am")])
                                kb.tt("dve", chn["ApA"][:, hl, :], pw, low16, ALU.mult, r=[P_, "C"], w=[rn("ApA")])
                            else:
                                kb.tt("dve", chn["ApA"][:, hl, :], pw, lowS, ALU.mult, r=[P_, "C"], w=[rn("ApA")])
                            yield
                            mm1(fsl(btT, hl), fsl(alT, hl), [ralT, rbtT])
                            kb.tt("dve", chn["BpA"][:, hl, :], pw, (up16 if not sample else upS), ALU.mult, r=[P_, "C"], w=[rn("BpA")])
                            yield
                            mm1(fsl(ktT, hl), fsl(alT, hl), [ralT, rktT])
                            kb.tt("dve", AakT[:, hl, :], pw, mup, ALU.mult, r=[P_, "C"], w=[rn("AakT")])
                            yield
                            mm1(fsl(btT, hl), fsl(rbT, hl), [rrbT, rbtT])
                            kb.tt("dve", ArbT[:, hl, :], pw, minc, ALU.mult, r=[P_, "C"], w=[rn("ArbT")])
                            yield
                            mm1(fsl(ktT, hl), fsl(rbT, hl), [rrbT, rktT])
                            kb.tt("dve", ArkT[:, hl, :], pw, minc, ALU.mult, r=[P_, "C"], w=[rn("ArkT")])
                            yield
                            kb.tt("dve", chn["TTa"][:, hl, :], chn["BpA"][:, hl, :], ident, ALU.add, r=[rn("BpA"), "C"], w=[rn("TTa")])
                            Ap, Bp, TT = "ApA", "BpA", "TTa"
                            for it in range(n_it):
                                Ap2 = "ApB" if Ap == "ApA" else "ApA"
                                Bp2 = "BpB" if Bp == "BpA" else "BpA"
                                TT2 = "TTb" if TT == "TTa" else "TTa"
                                mm1(chn[Bp][:, hl, :], chn[Ap][:, hl, :], [rn(Ap), rn(Bp)])
                                kb.cp("act", chn[Ap2][:, hl, :], pw, r=[P_], w=[rn(Ap2)])
                                yield
                                if it < n_it - 1:
                                    mm1(chn[Ap][:, hl, :], chn[Bp][:, hl, :], [rn(Ap), rn(Bp)])
                                    kb.cp("act", chn[Bp2][:, hl, :], pw, r=[P_], w=[rn(Bp2)])
                                    yield
                                mm1(chn[Ap2][:, hl, :], chn[TT][:, hl, :], [rn(Ap2), rn(TT)])
                                kb.tt("dve", chn[TT2][:, hl, :], pw, chn[TT][:, hl, :], ALU.add, r=[P_, rn(TT)], w=[rn(TT2)])
                                yield
                                Ap, Bp, TT = Ap2, Bp2, TT2
                            if not sample:
                                for mk in (m16, m32, m64):
                                    TT2 = "TTb" if TT == "TTa" else "TTa"
                                    kb.tr(psb[B_][:, 0:128], chn[TT][:, hl, :], identR[:], r=[rn(TT), "identR"], w=[P_])
                                    kb.cp("act", chn["ApB"][:, hl, :], psb[B_][:, 0:128], r=[P_], w=[rn("ApB")])
                                    kb.tt("dve", chn["ApA"][:, hl, :], chn["Am"][:, hl, :], mk, ALU.mult, r=[rn("Am"), "C"], w=[rn("ApA")])
                                    yield
                                    mm1(chn["ApA"][:, hl, :], chn[TT][:, hl, :], [rn("ApA"), rn(TT)])
                                    kb.cp("act", chn["BpA"][:, hl, :], pw, r=[P_], w=[rn("BpA")])
                                    yield
                                    mm1(chn["ApB"][:, hl, :], chn["BpA"][:, hl, :], [rn("ApB"), rn("BpA")])
                                    kb.tt("dve", chn[TT2][:, hl, :], pw, chn[TT][:, hl, :], ALU.add, r=[P_, rn(TT)], w=[rn(TT2)])
                                    yield
                                    TT = TT2
                            TTh = chn[TT][:, hl, :]
                            mm1(TTh, tsl(AL, hl), [rn(TT), rAL], cols=64)
                            kb.cp("act", Ahat[:, hl, :], ps[B_][:, 0:64], r=[P_], w=[rn("Ahat")])
                            yield
                            mm1(AakT[:, hl, :], tsl(Vv, hl), [rn("AakT"), rVv], cols=64)
                            kb.cp("dve", X1[:, hl, :], ps[B_][:, 0:64], r=[P_], w=[rn("X1")])
                            yield
                            mm1(TTh, X1[:, hl, :], [rn(TT), rn("X1")], cols=64)
                            kb.cp("act", U0[:, hl, :], ps[B_][:, 0:64], r=[P_], w=[rn("U0")])
                            yield
                            mm1(tsl(RB, hl), identR[:], [rRB, "identR"], rows=64, last=False)
                            mm1(Ahat[:, hl, :], ArbT[:, hl, :], [rn("Ahat"), rn("ArbT")], rows=64, first=False)
                            kb.cp("dve", RhT[:, hl, :], ps[B_][0:64, 0:128], r=[P_], w=[rn("RhT")])
                            yield
                            if sample:
                                return
                            mm1(Ahat[:, hl, :], tsl(BH, hl), [rn("Ahat"), rBH], cols=64, rows=64)
                            kb.stt(Gm[:, hl, :], ident[0:64, 0:64], PLc[:, hl:hl + 1], ps[B_][0:64, 0:64], ALU.mult, ALU.add, r=["C", rPL, P_], w=[rn("Gm")])
                            yield
                            mm1(ArkT[:, hl, :], tsl(Vv, hl), [rn("ArkT"), rVv], cols=64, last=False)
                            mm1(ArbT[:, hl, :], U0[:, hl, :], [rn("ArbT"), rn("U0")], cols=64, first=False, last=False)
                            mm1(RhT[:, hl, :], S0T[:, hl, :], [rn("RhT"), rn("S0T")], cols=64, first=False)
                            kb.cp("act", yb[:, hl * 64:(hl + 1) * 64], ps[B_][:, 0:64], r=[P_], w=["tA1"])
                            yield
                            if i == 15:
                                mm1(tsl(Vv, hl), tsl(KH, hl), [rVv, rKH], cols=64, rows=64, last=False)
                                mm1(U0[:, hl, :], tsl(BH, hl), [rn("U0"), rBH], cols=64, rows=64, first=False, last=False)
                                mm1(S0T[:, hl, :], Gm[:, hl, :], [rn("S0T"), rn("Gm")], cols=64, rows=64, first=False)
                                kb.cp("dve", SLo[:, hl, :], ps[B_][0:64, 0:64], r=[P_], w=["S0nat"])
                                yield
                            mm1(tsl(KH, hl), tsl(Vv, hl), [rVv, rKH], cols=64, rows=64, last=False)
                            mm1(tsl(BH, hl), U0[:, hl, :], [rn("U0"), rBH], cols=64, rows=64, first=False, last=False)
                            mm1(Gm[:, hl, :], S0T[:, hl, :], [rn("S0T"), rn("Gm")], cols=64, rows=64, first=False)
                            kb.cp("dve", S0T[:, hl, :], ps[B_][0:64, 0:64], r=[P_], w=[rn("S0T")])
                            yield

                        gens = [head_alg(hl) for hl in range(4)] + ([fg] if fg is not None else [])
                        while gens:
                            for g_ in list(gens):
                                try:
                                    next(g_)
                                except StopIteration:
                                    gens.remove(g_)
                        if not sample:
                            if i == 15:
                                kb.dma("sp", o_pwkv[hg * 4:hg * 4 + 4].rearrange("h v k -> v h k"), SLo[:, 0:4, :], r=["S0nat"], w=())
                        else:
                            sample_states()
                        if hg == 0 and i == 0:
                            stop_at(57)
                        if hg == 0 and i == 16:
                            stop_at(60)
                        for hl in range(4):
                            s.add("dve", lambda g_, o_=small[:, 8 + hl * 6:14 + hl * 6], i_=yb[:, hl * 64:(hl + 1) * 64]: g_.bn_stats(out=o_, in_=i_), r=["tA1"], w=["tA1"], tag="bnst")
                        for hl in range(4):
                            s.add("dve", lambda g_, o_=small[:, 32 + hl * 2:34 + hl * 2], i_=small[:, 8 + hl * 6:14 + hl * 6]: g_.bn_aggr(out=o_, in_=i_), r=["tA1"], w=["tA1"], tag="bnag")
                        mvv = small[:, 32:40].rearrange("p (h two) -> p h two", two=2)
                        kb.act(small[:, 40:44].unsqueeze(2), mvv[:, :, 1:2], AF.Ln, r=["tA1"], w=["tA1"], bias=64e-5, scale=1.0)
                        kb.act(small[:, 40:44], small[:, 40:44], AF.Exp, r=["tA1"], w=["tA1"], scale=-0.5)
                        kb.tt("dve", small[:, 44:48].unsqueeze(2), mvv[:, :, 0:1], small[:, 40:44].unsqueeze(2), ALU.mult, r=["tA1"], w=["tA1"])
                        kb.ts("dve", small[:, 44:48], small[:, 44:48], -1.0, None, ALU.mult, None, r=["tA1"], w=["tA1"])
                        for hl in range(4):
                            kb.act(yb[:, hl * 64:(hl + 1) * 64], yb[:, hl * 64:(hl + 1) * 64], AF.Identity, r=["tA1", "tA1"], w=["tA1"],
                                   bias=small[:, 44 + hl:45 + hl], scale=small[:, 40 + hl:41 + hl])
                        kb.tt("dve", yb[:], yb[:], bcs["lngb"][:], ALU.mult, r=["tA1", "lngb"], w=["tA1"])
                        kb.tt("dve", yb[:], yb[:], bcs["lnbb"][:], ALU.add, r=["tA1", "lnbb"], w=["tA1"])
                        kb.tt("dve", Tbk[:].rearrange("p (h k) -> p h k", k=64), Vv[:].rearrange("p (h k) -> p h k", k=64),
                              smallf[:, 4:8].unsqueeze(2).to_broadcast([128, 4, 64]), ALU.mult, r=[rVv, rsf], w=["Tbk"])
                        kb.tt("dve", yb[:], yb[:], Tbk[:], ALU.add, r=["tA1", "Tbk"], w=["tA1"])
                        kb.tt("dve", yb[:], yb[:], Gt, ALU.mult, r=["tA1", rGV], w=["tA1"])
                        for cc in range(2):
                            kb.tr(ps[7][:, cc * 128:(cc + 1) * 128], yb[:, cc * 128:(cc + 1) * 128], ident, r=["tA1", "C"], w=[PS[7]], sig=(cc == 1))
                        kb.cp("act", ygT[:, 2 * hg:2 * hg + 2, i * 128:(i + 1) * 128], ps[7][:, 0:256].rearrange("p (c t) -> p c t", t=128), r=[PS[7]], w=["ygT%d" % i])
                    def run_gen(g_):
                        for _ in g_:
                            pass

                    run_gen(front(0))
                    for i in range(NT):
                        back(i, front(i + 1) if i + 1 < NT else None)
                    s.barrier()
            for c in range(8):
                kb.tr(ps[0][0:17, c * 128:(c + 1) * 128] if c < 4 else ps[1][0:17, (c - 4) * 128:(c - 3) * 128], hlast[:, c, :], ident, r=["hlast", "C"],
                      w=[PS[0] if c < 4 else PS[1]], sig=(c in (3, 7)))
            kb.cp("dve", tB[0][0:17, 0:512], ps[0][0:17, :], r=[PS[0]], w=["tB0"])
            kb.cp("dve", tB[0][0:17, 512:1024], ps[1][0:17, :], r=[PS[1]], w=["tB0"])
            kb.dma("sp", o_pshift, tB[0][0:1, :], r=["tB0"], w=())
            kb.dma("sp", o_sshift, tB[0][1:17, :], r=["tB0"], w=())
            s.barrier()
            with contextlib.ExitStack() as pc_:
                alloc_gl(pc_)
                mod_prepare(l, 1, 1.0, blocks=[4, 5])
                Gp, Gs = gl["Gp"], gl["Gs"]
                wout = kb.sb(pc_, "wo_sb", [128, 8, D], BF16)
                kb.dma("pool", wout[:], rw_wo[0].rearrange("(kc p) n -> p kc n", p=128), r=(), w=["wout"])
                proj_acc(wout, "wout", 8, lambda i, kc: (ygT[:, kc, i * 128:(i + 1) * 128], "ygT%d" % i), True, True)
                s.barrier()
        s.barrier()

    stored = set()

    def store_tile(i):
        stored.add(i)
        kb.dma("sp", yout[i * 128:(i + 1) * 128, :], X[:, i, :], r=["X%d" % i], w=())

    def dump_and_finish():
        for i in range(NT):
            if i not in stored:
                kb.dma("sp", yout[i * 128:(i + 1) * 128, :], X[:, i, :], r=["X%d" % i], w=())

    stage = 0
    for l in range(2):
        for sub in range(3):
            if sub == 0:
                ffn(l, 0, 0, 0.5)
            elif sub == 2:
                ffn(l, 1, 2, 0.5)
            elif l == 0:
                ab_mixer(l)
            else:
                rwkv_mixer(l)
            stage += 1
            if stage >= upto:
                return dump_and_finish()
    dump_and_finish()


def build(upto=99, stop_point=None):
    kb = KB()
    kb.stop_point = stop_point
    with contextlib.ExitStack() as es:
        kb.es = es
        build_program(kb, upto)
        kb.s.emit(kb.nc, es)
    return kb


def core_inputs(inp, core, kb):
    sl = slice(16 * core, 16 * core + 16)
    xs = inp["x_sample"][sl].reshape(128, D)
    f32 = np.float32

    def fm(v):
        return np.ascontiguousarray(np.asarray(v, f32).reshape(-1, 128).T)

    m = {
        "x": np.concatenate([inp["x_prompt"][core], xs], axis=0),
        "c": np.concatenate([inp["c_prompt"][core:core + 1], inp["c_sample"][sl]], axis=0),
        "cst": CST_ARR,
    }
    cw = inp["rg_conv_w"][0]
    vecA = np.zeros((128, 32), f32)
    for c in range(4):
        for j in range(4):
            vecA[:, c * 4 + j] = cw[j, c * 128:(c + 1) * 128]
    vecA[:, 16:20] = fm(inp["rg_conv_b"][0])
    vecA[:, 20:24] = fm(inp["rg_b_a"][0])
    vecA[:, 24:28] = fm(inp["rg_b_x"][0])
    vecA[:, 28:32] = fm(inp["rg_lambda"][0])
    m["vecA"] = vecA
    m["bgT"] = np.ascontiguousarray(inp["mlstm_b_gates"][0].T)
    m["minitT"] = np.ascontiguousarray(inp["state_mlstm_m"][0, sl].T)
    m["smC"] = inp["state_mlstm_C"][0, sl]
    m["smn"] = inp["state_mlstm_n"][0, sl]
    m["srh"] = inp["state_rglru_h"][0, sl]
    m["srconv"] = inp["state_rglru_conv"][0, sl].reshape(48, 512)
    mu = inp["rw_mu"][0]
    muT = np.zeros((128, 48), f32)
    for j in range(6):
        muT[:, j * 8:(j + 1) * 8] = fm(mu[j])
    m["muT"] = muT
    m["rk_flat"] = inp["rw_r_k"].reshape(1, D)
    m["swkv"] = inp["state_rwkv_wkv"][0, sl]
    m["sshift"] = inp["state_rwkv_shift"][0, sl]
    for k in kb.dram:
        if k not in m and k in inp:
            m[k] = inp[k]
    return {k: np.ascontiguousarray(v, dtype=f32) for k, v in m.items() if k in kb.dram}


_CACHE = {}


def kernel(**inputs):
    inp = {k: np.asarray(v) for k, v in inputs.items()}
    if "kb" not in _CACHE:
        _CACHE["kb"] = build()
    kb = _CACHE["kb"]
    in_maps = [core_inputs(inp, c, kb) for c in range(NCORES)]
    res = run_bass_kernel_spmd(kb.nc, in_maps, core_ids=list(range(NCORES)))
    R = res.results
    f32 = np.float32
    cat = lambda key, f=(lambda a: a): np.stack([f(np.asarray(R[c][key], f32)) for c in range(NCORES)], axis=0)
    cats = lambda key, f=(lambda a: a): np.concatenate([f(np.asarray(R[c][key], f32)) for c in range(NCORES)], axis=0)
    y_prompt = cat("y", lambda a: a[:2048])
    y_sample = cats("y", lambda a: a[2048:].reshape(16, 8, D))
    outs = (
        y_prompt, y_sample,
        cat("o_pmC")[None], cat("o_pmn")[None], cat("o_pmm", lambda a: a[:, 0])[None],
        cat("o_prh", lambda a: a.reshape(512))[None], cat("o_prconv")[None],
        cat("o_pwkv")[None], cat("o_pshift", lambda a: a[0])[None],
        cats("o_smC")[None], cats("o_smn")[None], cats("o_smm", lambda a: a.T)[None],
        cats("o_srh")[None], cats("o_srconv", lambda a: a.reshape(16, 3, 512))[None],
        cats("o_swkv")[None], cats("o_sshift")[None],
    )
    return tuple(np.ascontiguousarray(o, dtype=f32) for o in outs)
```

```python
import contextlib
import numpy as np
import concourse.bass as bass
import concourse.mybir as mybir
from concourse.bass_utils import run_bass_kernel_spmd

F32 = mybir.dt.float32
BF16 = mybir.dt.bfloat16
F32R = mybir.dt.float32r
AF = mybir.ActivationFunctionType
ALU = mybir.AluOpType
AX = mybir.AxisListType

D = 1024
DFF = 2816
NT = 17
NTOK = NT * 128
ALPHA = 4.0 ** 0.25
LN_EPS = 1e-5
NCORES = 8


class Op:
    __slots__ = ("eng", "fn", "deps", "sig", "idx", "dma", "slot", "slot_total", "sigcount", "waits", "tag")


class Sched:
    ENGS = ["pe", "act", "dve", "pool", "sp"]

    def __init__(self, n_slots=40):
        self.q = {e: [] for e in self.ENGS}
        self.last_w = {}
        self.readers = {}
        self.n_slots = n_slots
        self.slot_rr = 0
        self.sw_rr = 0
        self.n_hw = n_slots - 12
        self.slot_total = [0] * n_slots
        self.slot_last = [None] * n_slots
        self.all_dma = []

    skip = False
    capture = None
    tick_fn = None
    _ticking = False

    def add(self, eng, fn, r=(), w=(), sig=True, dma=False, tag=""):
        if self.skip:
            return None
        if self.capture is not None:
            self.capture.append((eng, fn, tuple(r), tuple(w), sig, dma, tag))
            return None
        op = self._add(eng, fn, r, w, sig, dma, tag)
        if self.tick_fn is not None and not self._ticking:
            self._ticking = True
            try:
                self.tick_fn()
            finally:
                self._ticking = False
        return op

    def _add(self, eng, fn, r=(), w=(), sig=True, dma=False, tag=""):
        op = Op()
        op.eng, op.fn, op.sig, op.dma, op.tag = eng, fn, sig, dma, tag
        op.slot = None
        deps = []
        seen = set()

        def dep(o):
            if o is None or id(o) in seen:
                return
            seen.add(id(o))
            if (not dma) and eng == "pe" and o.eng == "pe" and not o.dma:
                return
            deps.append(o)

        for k in r:
            dep(self.last_w.get(k))
        for k in w:
            dep(self.last_w.get(k))
            for o in self.readers.get(k, {}).values():
                dep(o)
        if dma:
            if eng == "pool":
                slot = self.n_hw + (self.sw_rr % (self.n_slots - self.n_hw))
                self.sw_rr += 1
            else:
                slot = self.slot_rr % self.n_hw
                self.slot_rr += 1
            dep(self.slot_last[slot])
            self.slot_total[slot] += 16
            op.slot = slot
            op.slot_total = self.slot_total[slot]
            self.slot_last[slot] = op
            self.all_dma.append(op)
        op.deps = deps
        self.q[eng].append(op)
        op.idx = len(self.q[eng]) - 1
        key = ("dma", id(op)) if dma else eng
        for k in r:
            self.readers.setdefault(k, {})[key] = op
        for k in w:
            self.last_w[k] = op
            self.readers[k] = {}
        return op

    def barrier(self):
        if self.skip:
            return
        lasts = []
        for e in self.ENGS:
            comp = [o for o in self.q[e] if (not o.dma) and o.fn is not None]
            if comp:
                comp[-1].sig = True
                lasts.append(comp[-1])
        lasts += [o for o in self.slot_last if o is not None]
        for e in self.ENGS:
            op = Op()
            op.eng, op.sig, op.dma, op.tag, op.slot, op.fn = e, False, False, "barrier", None, None
            op.deps = list(lasts)
            self.q[e].append(op)
            op.idx = len(self.q[e]) - 1
        self.last_w = {}
        self.readers = {}

    def finalize(self):
        for e in self.ENGS:
            for o in reversed(self.q[e]):
                if not o.dma and o.fn is not None and e != "sp":
                    o.sig = True
                    break
        self.sigtot = {}
        for e in self.ENGS:
            cnt = 0
            ops = self.q[e]
            pref = []
            for o in ops:
                if (not o.dma) and o.sig:
                    cnt += 1
                pref.append(cnt)
            self.sigtot[e] = cnt
            nxt = None
            for i in range(len(ops) - 1, -1, -1):
                o = ops[i]
                if (not o.dma) and o.sig:
                    nxt = pref[i]
                o.sigcount = nxt if not o.dma else None
        for e in self.ENGS:
            known = {}
            for o in self.q[e]:
                need = {}
                for d in o.deps:
                    if d.dma:
                        key, val = ("slot", d.slot), d.slot_total
                    else:
                        if d.sigcount is None:
                            raise RuntimeError("dependency on op with no later signal: %s" % d.tag)
                        key, val = ("eng", d.eng), d.sigcount
                    if val > need.get(key, 0):
                        need[key] = val
                o.waits = []
                for key, val in need.items():
                    if known.get(key, 0) < val:
                        known[key] = val
                        o.waits.append((key, val))

    def simulate(self):
        pc = {e: 0 for e in self.ENGS}
        sem = {}
        sigc = {e: 0 for e in self.ENGS}
        progress = True
        while progress:
            progress = False
            for e in self.ENGS:
                while pc[e] < len(self.q[e]):
                    o = self.q[e][pc[e]]
                    ok = all(sem.get(k, 0) >= v for k, v in o.waits)
                    if not ok:
                        break
                    if o.dma:
                        sem[("slot", o.slot)] = sem.get(("slot", o.slot), 0) + 16
                    elif o.sig:
                        sem[("eng", e)] = sem.get(("eng", e), 0) + 1
                    pc[e] += 1
                    progress = True
        stuck = {e: (pc[e], len(self.q[e])) for e in self.ENGS if pc[e] < len(self.q[e])}
        if stuck:
            msg = []
            for e, (p, n) in stuck.items():
                o = self.q[e][p]
                msg.append("%s stuck at %d/%d tag=%s waits=%s" % (e, p, n, o.tag, [(k, v, sem.get(k, 0)) for k, v in o.waits]))
            raise RuntimeError("DEADLOCK in wait graph:\n" + "\n".join(msg))

    def emit(self, nc, es):
        self.finalize()
        self.simulate()
        engsem = {e: es.enter_context(nc.semaphore("sem_" + e)) for e in ["pe", "act", "dve", "pool"]}
        slotsem = [es.enter_context(nc.semaphore("slot%d" % i)) for i in range(self.n_slots)]

        def semof(key):
            return engsem[key[1]] if key[0] == "eng" else slotsem[key[1]]

        def run(e, g):
            for o in self.q[e]:
                for key, val in o.waits:
                    g.wait_ge(semof(key), val)
                if o.fn is None:
                    continue
                ins = o.fn(g)
                if o.dma:
                    ins.then_inc(slotsem[o.slot], 16)
                elif o.sig:
                    ins.then_inc(engsem[e], 1)
            if e == "sp":
                for s in range(self.n_slots):
                    if self.slot_total[s] > 0:
                        g.wait_ge(slotsem[s], self.slot_total[s])

        with nc.Block() as blk:
            blk.tensor(lambda g: run("pe", g))
            blk.scalar(lambda g: run("act", g))
            blk.vector(lambda g: run("dve", g))
            blk.gpsimd(lambda g: run("pool", g))
            blk.sync(lambda g: run("sp", g))


class KB:
    def __init__(self, stop_after=None, debug=False):
        self.nc = bass.Bass("TRN2", target_bir_lowering=False)
        self.s = Sched()
        self.stop_after = stop_after
        self.debug = debug
        self.dram = {}

    def din(self, name, shape, dt=F32):
        t = self.nc.dram_tensor(name, list(shape), dt, kind="ExternalInput")
        self.dram[name] = t
        return t.ap()

    def dout(self, name, shape, dt=F32):
        t = self.nc.dram_tensor(name, list(shape), dt, kind="ExternalOutput")
        self.dram[name] = t
        return t.ap()

    def sb(self, es, name, shape, dt=F32):
        self.uid = getattr(self, "uid", 0) + 1
        return es.enter_context(self.nc.sbuf_tensor("%s_%d" % (name, self.uid), list(shape), dt))

    def mm(self, out, lhsT, rhs, start, stop, r, w, sig=None, tag="mm"):
        if sig is None:
            sig = stop
        return self.s.add("pe", lambda g: g.matmul(out, lhsT=lhsT, rhs=rhs, start=start, stop=stop), r=r, w=w, sig=sig, tag=tag)

    def tr(self, out, in_, ident, r, w, sig=True, tag="tr"):
        return self.s.add("pe", lambda g: g.transpose(out, in_, ident), r=r, w=w, sig=sig, tag=tag)

    def act(self, out, in_, func, r, w, bias=None, scale=None, eng="act", tag="act"):
        kw = {}
        if bias is not None:
            kw["bias"] = bias
        if scale is not None:
            kw["scale"] = scale
        return self.s.add("act", lambda g: g.activation(out=out, in_=in_, func=func, **kw), r=r, w=w, tag=tag)

    def tt(self, eng, out, in0, in1, op, r, w, tag="tt"):
        return self.s.add(eng, lambda g: g.tensor_tensor(out=out, in0=in0, in1=in1, op=op), r=r, w=w, tag=tag)

    def ts(self, eng, out, in0, s1, s2, op0, op1, r, w, tag="ts"):
        if op1 is None:
            return self.s.add(eng, lambda g: g.tensor_scalar(out=out, in0=in0, scalar1=s1, scalar2=None, op0=op0), r=r, w=w, tag=tag)
        return self.s.add(eng, lambda g: g.tensor_scalar(out=out, in0=in0, scalar1=s1, scalar2=s2, op0=op0, op1=op1), r=r, w=w, tag=tag)

    def stt(self, out, in0, scalar, in1, op0, op1, r, w, tag="stt"):
        return self.s.add("dve", lambda g: g.scalar_tensor_tensor(out=out, in0=in0, scalar=scalar, in1=in1, op0=op0, op1=op1), r=r, w=w, tag=tag)

    def cp(self, eng, out, in_, r, w, tag="cp"):
        if eng == "act":
            return self.s.add("act", lambda g: g.copy(out=out, in_=in_), r=r, w=w, tag=tag)
        return self.s.add(eng, lambda g: g.tensor_copy(out=out, in_=in_), r=r, w=w, tag=tag)

    def memset(self, eng, ap, val, w, tag="memset"):
        return self.s.add(eng, lambda g: g.memset(ap, val), r=(), w=w, tag=tag)

    def dma(self, q, out, in_, r, w, tag="dma", **kw):
        return self.s.add(q, lambda g: g.dma_start(out=out, in_=in_, **kw), r=r, w=w, dma=True, tag=tag)


def make_consts():
    c = {}
    c["ident"] = np.eye(128, dtype=np.float32)
    selP = np.zeros((128, 128), np.float32)
    selP[0, :] = 1.0
    selS = np.zeros((128, 128), np.float32)
    for p in range(128):
        selS[1 + p // 8, p] = 1.0
    c["selP"] = selP
    c["selS"] = selS
    st = np.arange(128)
    c["maskP"] = (st[:, None] <= st[None, :]).astype(np.float32)
    c["maskS"] = ((st[:, None] <= st[None, :]) & (st[:, None] // 8 == st[None, :] // 8)).astype(np.float32)
    c["rst"] = np.tile((st % 8 != 0).astype(np.float32)[None, :], (128, 1))
    c["rstm"] = np.tile(np.where(st % 8 == 0, -1e30, 0.0).astype(np.float32)[None, :], (128, 1))
    bms = np.zeros((128, 128), np.float32)
    bms[st, st // 8] = 1.0
    c["bms"] = bms
    c["ones"] = np.ones((128, 128), np.float32)
    same = (st[:, None] // 8 == st[None, :] // 8)
    c["upP"] = (st[:, None] < st[None, :]).astype(np.float32)
    c["lowP"] = (st[:, None] > st[None, :]).astype(np.float32)
    c["upS"] = ((st[:, None] < st[None, :]) & same).astype(np.float32)
    c["lowS"] = ((st[:, None] > st[None, :]) & same).astype(np.float32)
    c["blkS"] = same.astype(np.float32)
    blk = lambda b: (st[:, None] // b == st[None, :] // b)
    c["low16"] = ((st[:, None] > st[None, :]) & blk(16)).astype(np.float32)
    c["up16"] = ((st[:, None] < st[None, :]) & blk(16)).astype(np.float32)
    for b in (16, 32, 64):
        c["m%d" % b] = (blk(2 * b) & ~blk(b)).astype(np.float32)
    names = list(c.keys())
    arr = np.concatenate([c[k] for k in names], axis=1)
    offs = {}
    o = 0
    for k in names:
        offs[k] = o
        o += c[k].shape[1]
    return arr, offs


CST_ARR, CST_OFF = make_consts()
NCST = CST_ARR.shape[1]

FFN_PARTS = [(0, 4), (4, 4), (8, 4), (12, 4), (16, 4), (20, 2)]
TGS = [(0, 512), (512, 512), (1024, 512), (1536, 512), (2048, 128)]


def build_program(kb, upto=99):
    nc, s = kb.nc, kb.s
    es = kb.es
    xin = kb.din("x", [NTOK, D])
    cin = kb.din("c", [NT, D])
    cst = kb.din("cst", [128, NCST])
    ada_w = kb.din("ada_w", [2, D, 9 * D])
    ada_b = kb.din("ada_b", [2, 9 * D])
    ln_g = kb.din("ln_g", [2, 3, D])
    ln_b = kb.din("ln_b", [2, 3, D])
    ffn_w1 = kb.din("ffn_w1", [2, 2, D, DFF])
    ffn_w3 = kb.din("ffn_w3", [2, 2, D, DFF])
    ffn_w2 = kb.din("ffn_w2", [2, 2, DFF, D])
    yout = kb.dout("y", [NTOK, D])

    X = kb.sb(es, "X", [128, NT, D], F32)
    C = kb.sb(es, "cst_sb", [128, NCST], F32)
    cT = kb.sb(es, "cT", [128, 8, NT], BF16)
    onesb = kb.sb(es, "onesb", [1, 32], F32)
    modT = kb.sb(es, "modT", [128, 16, NT], F32)
    gl = {}

    def alloc_gl(stack):
        gl["Gp"] = kb.sb(stack, "Gp", [128, D], F32)
        gl["Gs"] = kb.sb(stack, "Gs", [128, D], F32)
        gl["LNg"] = kb.sb(stack, "LNg", [128, D], F32)
        gl["LNb"] = kb.sb(stack, "LNb", [128, D], F32)
    tA = [kb.sb(es, "tA%d" % i, [128, 512], F32) for i in range(4)]
    tB = [kb.sb(es, "tB%d" % i, [128, D], F32) for i in range(2)]
    stt_ = [kb.sb(es, "bnst%d" % i, [128, 2, 6], F32) for i in range(2)]
    mv = [kb.sb(es, "mv%d" % i, [128, 2], F32) for i in range(2)]
    rstd = [kb.sb(es, "rstd%d" % i, [128, 1], F32) for i in range(2)]
    nmr = [kb.sb(es, "nmr%d" % i, [128, 1], F32) for i in range(2)]
    tmpS = kb.sb(es, "tmpS", [128, 128], F32)
    ps = [es.enter_context(nc.psum_tensor("ps%d" % i, [128, 512], F32)) for i in range(8)]
    PS = ["ps%d" % i for i in range(8)]
    psb = [p.bitcast(BF16) for p in ps]

    ident = C[:, CST_OFF["ident"]:CST_OFF["ident"] + 128]
    selP = C[0:NT, CST_OFF["selP"]:CST_OFF["selP"] + 128]
    selS = C[0:NT, CST_OFF["selS"]:CST_OFF["selS"] + 128]

    kb.dma("sp", C[:], cst, r=(), w=["C"])
    for i in range(NT):
        kb.dma("sp", X[:, i, :], xin[i * 128:(i + 1) * 128, :], r=(), w=["X%d" % i])
    kb.memset("pool", onesb[:], 1.0, w=["onesb"])

    with contextlib.ExitStack() as ph0:
        c_sb = kb.sb(ph0, "c_sb", [NT, D], F32)
        cs_sb = kb.sb(ph0, "cs_sb", [NT, D], F32)
        kb.dma("sp", c_sb[:], cin, r=(), w=["c_sb"])
        kb.act(cs_sb[:], c_sb[:], AF.Silu, r=["c_sb"], w=["cs_sb"])
        for kc in range(8):
            kb.tr(ps[0][:, kc * NT:(kc + 1) * NT], cs_sb[0:NT, kc * 128:(kc + 1) * 128], C[0:NT, 0:NT],
                  r=["cs_sb", "C"], w=[PS[0]], sig=(kc == 7))
        kb.cp("dve", cT[:].rearrange("p a b -> p (a b)"), ps[0][:, 0:8 * NT], r=[PS[0]], w=["cT"])
    s.barrier()

    state = {"ada_i": 0, "ada_i2": 0, "psr": 0}

    def mod_prepare(l, sub, res_w, blocks=range(6)):
        if 5 in blocks:
            Gp, Gs = gl["Gp"], gl["Gs"]
            kb.dma("sp", gl["LNg"][:], ln_g[l, sub:sub + 1, :].to_broadcast([128, D]), r=(), w=["LNg"])
            kb.dma("sp", gl["LNb"][:], ln_b[l, sub:sub + 1, :].to_broadcast([128, D]), r=(), w=["LNb"])
        phm = contextlib.ExitStack()
        modst = [kb.sb(phm, "modst%d" % i, [NT, 512], F32) for i in range(2)]
        adaw = [kb.sb(phm, "adaw%d" % i, [128, 8, 256], BF16) for i in range(2)]
        adab = [kb.sb(phm, "adab%d" % i, [1, 512], F32) for i in range(2)]
        for b in blocks:
            i = state["ada_i"]
            state["ada_i"] += 1
            buf = i % 2
            co = sub * 3 * D + b * 512
            kb.dma("sp", adab[buf][:], ada_b[l:l + 1, co:co + 512], r=(), w=["adab%d" % buf])
            pm = 4 + (i % 2)
            for sbk in range(2):
                i2 = state["ada_i2"]
                state["ada_i2"] += 1
                wb = i2 % 2
                kb.dma("pool", adaw[wb][:], ada_w[l].rearrange("(kc p) n -> p kc n", p=128)[:, :, co + sbk * 256:co + (sbk + 1) * 256],
                       r=(), w=["adaw%d" % wb])
                for kc in range(8):
                    kb.mm(ps[pm][0:NT, sbk * 256:(sbk + 1) * 256], lhsT=cT[:, kc, :], rhs=adaw[wb][:, kc, :], start=(kc == 0), stop=False,
                          r=["cT", "adaw%d" % wb], w=[PS[pm]], sig=False)
                kb.mm(ps[pm][0:NT, sbk * 256:(sbk + 1) * 256], lhsT=onesb[0:1, 0:NT], rhs=adab[buf][:, sbk * 256:(sbk + 1) * 256], start=False, stop=True,
                      r=["onesb", "adab%d" % buf], w=[PS[pm]], sig=True)
            kb.cp("act", modst[buf][:], ps[pm][0:NT, :], r=[PS[pm]], w=["modst%d" % buf])
            if b < 4:
                for cc in range(4):
                    j = b * 4 + cc
                    kb.tr(ps[6][:, j * NT:(j + 1) * NT], modst[buf][0:NT, cc * 128:(cc + 1) * 128], C[0:NT, 0:NT],
                          r=["modst%d" % buf, "C"], w=[PS[6]], sig=(cc == 3))
                if b == 1:
                    kb.cp("dve", modT[:, 0:8, :].rearrange("p a b -> p (a b)"), ps[6][:, 0:8 * NT], r=[PS[6]], w=["modT"])
                if b == 3:
                    kb.ts("dve", modT[:, 8:16, :].rearrange("p a b -> p (a b)"), ps[6][:, 8 * NT:16 * NT], 1.0, None,
                          ALU.add, None, r=[PS[6]], w=["modT"])
            else:
                h = b - 4
                kb.mm(ps[7][:, :], lhsT=selP, rhs=modst[buf][:], start=True, stop=True, r=["C", "modst%d" % buf], w=[PS[7]])
                kb.act(Gp[:, h * 512:(h + 1) * 512], ps[7][:, :], AF.Identity, r=[PS[7]], w=["Gp"], bias=float(res_w), scale=float(res_w))
                kb.mm(ps[7][:, :], lhsT=selS, rhs=modst[buf][:], start=True, stop=True, r=["C", "modst%d" % buf], w=[PS[7]])
                kb.act(Gs[:, h * 512:(h + 1) * 512], ps[7][:, :], AF.Identity, r=[PS[7]], w=["Gs"], bias=float(res_w), scale=float(res_w))
        s.barrier()
        phm.close()

    def make_hT(hT, i, prescale=True, col0=None, res=None, hook=None):
        g = res if res is not None else "hT_g%d" % (i // 4)
        if col0 is None:
            col0 = i * 128
        for half in range(2):
            pb = state["psr"] % 4
            state["psr"] += 1
            for cc in range(4):
                c = half * 4 + cc
                kb.tr(ps[pb][:, cc * 128:(cc + 1) * 128], X[:, i, c * 128:(c + 1) * 128], ident,
                      r=["X%d" % i, "C"], w=[PS[pb]], sig=(cc == 3))
            for cc in range(4):
                c = half * 4 + cc
                src = ps[pb][:, cc * 128:(cc + 1) * 128]
                if hook is not None:
                    hook(i, c, src, PS[pb])
                if i < 16:
                    if cc % 2 == 0:
                        kb.act(hT[:, c, col0:col0 + 128], src, AF.Identity, r=[PS[pb], "modT"], w=[g],
                               bias=modT[:, c, 0:1], scale=modT[:, 8 + c, 0:1])
                    else:
                        kb.ts("dve", hT[:, c, col0:col0 + 128], src, modT[:, 8 + c, 0:1], modT[:, c, 0:1],
                              ALU.mult, ALU.add, r=[PS[pb], "modT"], w=[g])
                else:
                    sc = modT[:, 8 + c, 1:NT].unsqueeze(2).to_broadcast([128, 16, 8])
                    sh = modT[:, c, 1:NT].unsqueeze(2).to_broadcast([128, 16, 8])
                    kb.tt("dve", tmpS[:].rearrange("p (q t) -> p q t", t=8), src.rearrange("p (q t) -> p q t", t=8), sc,
                          ALU.mult, r=[PS[pb], "modT"], w=["tmpS"])
                    kb.tt("dve", hT[:, c, col0:col0 + 128].rearrange("p (q t) -> p q t", t=8),
                          tmpS[:].rearrange("p (q t) -> p q t", t=8), sh, ALU.add, r=["tmpS", "modT"], w=[g])

    def layer_norm(i):
        k = i % 2
        for h in range(2):
            s.add("dve", lambda g_, h=h, k=k, i=i: g_.bn_stats(out=stt_[k][:, h, :], in_=X[:, i, h * 512:(h + 1) * 512]),
                  r=["X%d" % i], w=["bnst%d" % k], tag="bnstats")
        s.add("dve", lambda g_, k=k: g_.bn_aggr(out=mv[k][:], in_=stt_[k][:].rearrange("p a b -> p (a b)")),
              r=["bnst%d" % k], w=["mv%d" % k], tag="bnaggr")
        kb.act(rstd[k][:], mv[k][:, 1:2], AF.Sqrt, r=["mv%d" % k], w=["rstd%d" % k], bias=float(LN_EPS), scale=1.0)
        s.add("dve", lambda g_, k=k: g_.reciprocal(out=rstd[k][:], in_=rstd[k][:]), r=["rstd%d" % k], w=["rstd%d" % k], tag="recip")
        kb.ts("dve", nmr[k][:], mv[k][:, 0:1], rstd[k][:, 0:1], -1.0, ALU.mult, ALU.mult, r=["mv%d" % k, "rstd%d" % k], w=["nmr%d" % k])
        kb.act(tB[k][:], X[:, i, :], AF.Identity, r=["X%d" % i, "rstd%d" % k, "nmr%d" % k], w=["tB%d" % k],
               bias=nmr[k][:, 0:1], scale=rstd[k][:, 0:1])
        ea = "pool" if i % 2 == 1 else "dve"
        kb.tt(ea, tB[k][:], tB[k][:], gl["LNg"][:], ALU.mult, r=["tB%d" % k, "LNg"], w=["tB%d" % k])
        kb.tt(ea, X[:, i, :], tB[k][:], gl["LNb"][:], ALU.add, r=["tB%d" % k, "LNb"], w=["X%d" % i])

    pacc = {"v": 0}

    def proj_acc(wt, wres, nk, lhs_of, do_ln, first, after_ln=None, wts=None, wsres=None):
        Gp, Gs = gl["Gp"], gl["Gs"]
        for i in [16] + list(range(16)):
            if i == 0:
                if wts is None:
                    kb.tt("dve", wt[:, 0:nk, :], wt[:, 0:nk, :], Gp[:].unsqueeze(1).to_broadcast([128, nk, D]), ALU.mult, r=[wres, "Gp"], w=[wres])
                else:
                    wt, wres = wts, wsres
            for half in range(2):
                v = pacc["v"]
                pacc["v"] += 1
                py = 4 + (v % 4)
                for kc in range(nk):
                    lt, lres = lhs_of(i, kc)
                    kb.mm(ps[py][:, :], lhsT=lt, rhs=wt[:, kc, half * 512:(half + 1) * 512], start=(kc == 0), stop=(kc == nk - 1),
                          r=[lres, wres], w=[PS[py]])
                xs = X[:, i, half * 512:(half + 1) * 512]
                src = ps[py][:, :]
                rsrc = PS[py]
                if i == 16:
                    kb.tt("dve", tA[v % 4][:], ps[py][:, :], Gs[:, half * 512:(half + 1) * 512], ALU.mult, r=[PS[py], "Gs"], w=["tA%d" % (v % 4)])
                    src, rsrc = tA[v % 4][:], "tA%d" % (v % 4)
                if first:
                    kb.stt(xs, xs, float(ALPHA), src, ALU.mult, ALU.add, r=["X%d" % i, rsrc], w=["X%d" % i])
                else:
                    kb.tt("dve", xs, xs, src, ALU.add, r=["X%d" % i, rsrc], w=["X%d" % i])
            if do_ln:
                layer_norm(i)
                if after_ln is not None:
                    after_ln(i)

    def ffn(l, f, sub, res_w):
        with contextlib.ExitStack() as ph:
            alloc_gl(ph)
            mod_prepare(l, sub, res_w)
            Gp, Gs = gl["Gp"], gl["Gs"]
            hT = kb.sb(ph, "hT", [128, 8, NTOK], BF16)
            w1p = kb.sb(ph, "w1p", [128, 8, 512], BF16)
            w3p = kb.sb(ph, "w3p", [128, 8, 512], BF16)
            w2p = kb.sb(ph, "w2p", [128, 4, D], BF16)
            w2s = kb.sb(ph, "w2s", [128, 4, D], BF16)
            gbuf = kb.sb(ph, "gbuf", [128, 4, NTOK], BF16)
            sil = [kb.sb(ph, "sil%d" % i, [128, 512], F32) for i in range(2)]
            w1v = ffn_w1[l, f].rearrange("(kc p) n -> p kc n", p=128)
            w3v = ffn_w3[l, f].rearrange("(kc p) n -> p kc n", p=128)
            w2v = ffn_w2[l, f].rearrange("(j p) n -> p j n", p=128)
            u = 0
            v = 0
            def load_up(pi_):
                j0_, n_ = FFN_PARTS[pi_]
                kb.dma("pool", w1p[:, :, 0:n_ * 128], w1v[:, :, j0_ * 128:(j0_ + n_) * 128], r=(), w=["w1p"])
                kb.dma("pool", w3p[:, :, 0:n_ * 128], w3v[:, :, j0_ * 128:(j0_ + n_) * 128], r=(), w=["w3p"])

            def load_dn(pi_):
                j0_, n_ = FFN_PARTS[pi_]
                kb.dma("pool", w2p[:, 0:n_, :], w2v[:, j0_:j0_ + n_, :], r=(), w=["w2p"])

            load_up(0)
            load_dn(0)
            for i in range(NT):
                make_hT(hT, i)
            for pi, (j0, ncn) in enumerate(FFN_PARTS):
                for tg, (t0, nt_) in enumerate(TGS):
                    for jj in range(ncn):
                        pa, pb = (2 * u) % 4, (2 * u + 1) % 4
                        for kc in range(8):
                            kb.mm(ps[pa][:, 0:nt_], lhsT=w1p[:, kc, jj * 128:(jj + 1) * 128], rhs=hT[:, kc, t0:t0 + nt_],
                                  start=(kc == 0), stop=(kc == 7), r=["w1p", "hT_g%d" % tg], w=[PS[pa]])
                        for kc in range(8):
                            kb.mm(ps[pb][:, 0:nt_], lhsT=w3p[:, kc, jj * 128:(jj + 1) * 128], rhs=hT[:, kc, t0:t0 + nt_],
                                  start=(kc == 0), stop=(kc == 7), r=["w3p", "hT_g%d" % tg], w=[PS[pb]])
                        kb.act(sil[u % 2][:, 0:nt_], ps[pa][:, 0:nt_], AF.Silu, r=[PS[pa]], w=["sil%d" % (u % 2)])
                        kb.tt("dve", gbuf[:, jj, t0:t0 + nt_], sil[u % 2][:, 0:nt_], ps[pb][:, 0:nt_], ALU.mult,
                              r=["sil%d" % (u % 2), PS[pb]], w=["g_g%d" % tg])
                        u += 1
                    if tg == 1:
                        kb.tt("dve", w2s[:, 0:ncn, :], w2p[:, 0:ncn, :], Gp[:].unsqueeze(1).to_broadcast([128, ncn, D]), ALU.mult,
                              r=["w2p", "Gp"], w=["w2s"])
                if pi + 1 < len(FFN_PARTS):
                    load_up(pi + 1)
                proj_acc(w2p, "w2p", ncn, lambda i, jj: (gbuf[:, jj, i * 128:(i + 1) * 128], "g_g%d" % (i // 4)), pi == len(FFN_PARTS) - 1, pi == 0,
                         after_ln=(store_tile if (l == 1 and sub == 2 and upto >= 6) else None), wts=w2s, wsres="w2s")
                if pi + 1 < len(FFN_PARTS):
                    load_dn(pi + 1)
        s.barrier()

    DKS = float(128 ** -0.5)

    class _Stop(Exception):
        pass

    def stop_at(n):
        if getattr(kb, "stop_point", None) == n:
            s.barrier()
            s.skip = True

    def ab_mixer(l):
        ab_mixer_(l)
        s.skip = False
        s.barrier()

    def ab_mixer_(l):
        ab_w_in = kb.din("ab_w_in", [1, D, 3080])
        ab_w_out = kb.din("ab_w_out", [1, D, D])
        mnorm_g = kb.din("mlstm_norm_g", [1, 512])
        vecA_d = kb.din("vecA", [128, 32])
        bgT_d = kb.din("bgT", [4, 2])
        minitT_d = kb.din("minitT", [4, 16])
        rg_w_a = kb.din("rg_w_a", [1, 8, 64, 64])
        rg_w_x = kb.din("rg_w_x", [1, 8, 64, 64])
        smC = kb.din("smC", [16, 4, 128, 128])
        smn = kb.din("smn", [16, 4, 128])
        srh = kb.din("srh", [16, 512])
        srconv = kb.din("srconv", [48, 512])
        o_pmC = kb.dout("o_pmC", [4, 128, 128])
        o_pmn = kb.dout("o_pmn", [4, 128])
        o_pmm = kb.dout("o_pmm", [4, 1])
        o_prh = kb.dout("o_prh", [4, 128])
        o_prconv = kb.dout("o_prconv", [3, 512])
        o_smC = kb.dout("o_smC", [16, 4, 128, 128])
        o_smn = kb.dout("o_smn", [16, 4, 128])
        o_smm = kb.dout("o_smm", [4, 16])
        o_srh = kb.dout("o_srh", [16, 512])
        o_srconv = kb.dout("o_srconv", [48, 512])

        maskP = C[:, CST_OFF["maskP"]:CST_OFF["maskP"] + 128]
        maskS = C[:, CST_OFF["maskS"]:CST_OFF["maskS"] + 128]
        rst = C[:, CST_OFF["rst"]:CST_OFF["rst"] + 128]
        rstm = C[:, CST_OFF["rstm"]:CST_OFF["rstm"] + 128]
        bms = C[:, CST_OFF["bms"]:CST_OFF["bms"] + 16]
        ones = C[:, CST_OFF["ones"]:CST_OFF["ones"] + 128]

        mod_prepare(l, 1, 1.0, blocks=range(4))
        win_v = ab_w_in[0].rearrange("(kc p) n -> p kc n", p=128)
        with contextlib.ExitStack() as ph:
            hmT = kb.sb(ph, "hmT", [128, 4, NTOK], BF16)
            vecA = kb.sb(ph, "vecA_sb", [128, 32], F32)
            kb.dma("sp", vecA[:], vecA_d, r=(), w=["vecA"])
            sigo = tA[0]
            hmf = tB[0][:, 0:512]
            with contextlib.ExitStack() as pa:
                winA = kb.sb(pa, "winA", [128, 8, 2056], BF16)
                kb.dma("pool", winA[:, :, 0:1024], win_v[:, :, 0:1024], r=(), w=["winA"])
                kb.dma("pool", winA[:, :, 1024:2056], win_v[:, :, 1024:2056], r=(), w=["winA"])
                qkT = kb.sb(pa, "qkT", [128, 8, 512], BF16)
                hTg = [kb.sb(pa, "hTgA%d" % i, [128, 8, 512], BF16) for i in range(2)]
                bg = kb.sb(pa, "bg", [4, 2], F32)
                nbg1 = kb.sb(pa, "nbg1", [4, 1], F32)
                minitT = kb.sb(pa, "minitT_sb", [4, 16], F32)
                mng = kb.sb(pa, "mng", [128, 512], F32)
                kb.dma("sp", bg[:], bgT_d, r=(), w=["bg"])
                kb.dma("sp", minitT[:], minitT_d, r=(), w=["minitT"])
                kb.dma("sp", mng[:], mnorm_g[0:1, :].to_broadcast([128, 512]), r=(), w=["mng"])
                kb.ts("dve", nbg1[:], bg[:, 1:2], -1.0, None, ALU.mult, None, r=["bg"], w=["nbg1"])
                R4 = lambda nm: kb.sb(pa, nm, [4, 128], F32)
                t1, IGa, Rt, t3, t4 = R4("r_t1"), R4("r_ig"), R4("r_rt"), R4("r_t3"), R4("r_t4")
                Bc = [R4("r_bc0"), R4("r_bc1")]
                Mx = [R4("r_mx0"), R4("r_mx1")]
                dd = kb.sb(pa, "r_dd", [4, 16], F32)
                DDm = kb.sb(pa, "r_DD", [4, 64], F32)
                mout = kb.sb(pa, "r_mout", [4, 16], F32)
                colq = [kb.sb(pa, "colq%d" % i, [128, 16], F32) for i in range(2)]
                decsb = kb.sb(pa, "decsb", [128, 64], F32)
                kw = kb.sb(pa, "kw", [128, 4, 128], BF16)
                ktok = kb.sb(pa, "ktok", [128, 4, 128], BF16)
                vext = [kb.sb(pa, "vext%d" % i, [128, 4, 130], BF16) for i in range(2)]
                PT = kb.sb(pa, "PT", [128, 4, 128], BF16)
                Cst = kb.sb(pa, "Cst", [128, 4, 130], F32)
                Cb = kb.sb(pa, "Cb", [128, 4, 130], BF16)
                dmax = kb.sb(pa, "dmax", [128, 4], F32)
                hst6 = kb.sb(pa, "hst6", [128, 4, 6], F32)
                hmv = kb.sb(pa, "hmv", [128, 4, 2], F32)
                hrs = kb.sb(pa, "hrs", [128, 4], F32)
                hnm = kb.sb(pa, "hnm", [128, 4], F32)
                for vv in vext:
                    kb.memset("pool", vv[:], 1.0, w=["vext0", "vext1"])
                kb.memset("pool", Cst[:], 0.0, w=["Cst"])
                kb.memset("pool", Cb[:], 0.0, w=["Cb"])

                def rows(i):
                    k = i % 2
                    hb = (i // 4) % 2
                    hT = hTg[hb]
                    tc0 = (i % 4) * 128
                    pg = ps[7]
                    for kc in range(8):
                        kb.mm(pg[0:4, 0:128], lhsT=winA[:, kc, 2048:2052], rhs=hT[:, kc, tc0:tc0 + 128], start=(kc == 0), stop=(kc == 7),
                              r=["winA", "hTg%d" % hb], w=[PS[7]])
                    for kc in range(8):
                        kb.mm(pg[0:4, 128:256], lhsT=winA[:, kc, 2052:2056], rhs=hT[:, kc, tc0:tc0 + 128], start=(kc == 0), stop=(kc == 7),
                              r=["winA", "hTg%d" % hb], w=[PS[7]])
                    kb.act(IGa[:], pg[0:4, 0:128], AF.Identity, r=[PS[7], "bg"], w=["r_ig"], bias=bg[:, 0:1], scale=1.0)
                    kb.act(t1[:], pg[0:4, 128:256], AF.Exp, r=[PS[7], "nbg1"], w=["r_t1"], bias=nbg1[:, 0:1], scale=-1.0)
                    kb.act(t1[:], t1[:], AF.Ln, r=["r_t1"], w=["r_t1"], bias=1.0, scale=1.0)
                    kb.ts("dve", t1[:], t1[:], -1.0, None, ALU.mult, None, r=["r_t1"], w=["r_t1"])
                    prompt = i < 16
                    if prompt:
                        binit = 0.0 if i == 0 else Bc[1 - k][:, 127:128]
                        minit = 0.0 if i == 0 else Mx[1 - k][:, 127:128]
                        s.add("dve", lambda g_: g_.tensor_tensor_scan(out=Bc[k][:], data0=ones[0:4, :], data1=t1[:], initial=binit,
                                                                       op0=ALU.mult, op1=ALU.add),
                              r=["r_t1", "r_bc%d" % (1 - k), "C"], w=["r_bc%d" % k], tag="scanB")
                        kb.tt("dve", IGa[:], IGa[:], Bc[k][:], ALU.subtract, r=["r_ig", "r_bc%d" % k], w=["r_ig"])
                        kb.memset("dve", t3[:], 0.0, w=["r_t3"])
                        s.add("dve", lambda g_: g_.tensor_tensor_scan(out=Mx[k][:], data0=t3[:], data1=IGa[:], initial=minit,
                                                                       op0=ALU.add, op1=ALU.max),
                              r=["r_t3", "r_ig", "r_mx%d" % (1 - k)], w=["r_mx%d" % k], tag="scanM")
                        if i == 0:
                            kb.memset("dve", Rt[:], 0.0, w=["r_rt"])
                        else:
                            kb.cp("dve", Rt[:], Mx[1 - k][:, 127:128].to_broadcast([4, 128]), r=["r_mx%d" % (1 - k)], w=["r_rt"])
                    else:
                        s.add("dve", lambda g_: g_.tensor_tensor_scan(out=Bc[k][:], data0=rst[0:4, :], data1=t1[:], initial=0.0,
                                                                       op0=ALU.mult, op1=ALU.add),
                              r=["r_t1", "C"], w=["r_bc%d" % k], tag="scanB")
                        kb.tt("dve", IGa[:], IGa[:], Bc[k][:], ALU.subtract, r=["r_ig", "r_bc%d" % k], w=["r_ig"])
                        kb.cp("dve", t3[:], IGa[:], r=["r_ig"], w=["r_t3"])
                        kb.tt("dve", t3[:].rearrange("p (q t) -> p q t", t=8)[:, :, 0:1], IGa[:].rearrange("p (q t) -> p q t", t=8)[:, :, 0:1],
                              minitT[:].unsqueeze(2), ALU.max, r=["r_ig", "minitT"], w=["r_t3"])
                        s.add("dve", lambda g_: g_.tensor_tensor_scan(out=Mx[k][:], data0=rstm[0:4, :], data1=t3[:], initial=0.0,
                                                                       op0=ALU.add, op1=ALU.max),
                              r=["r_t3", "C"], w=["r_mx%d" % k], tag="scanM")
                        kb.cp("dve", Rt[:].rearrange("p (q t) -> p q t", t=8), minitT[:].unsqueeze(2).to_broadcast([4, 16, 8]),
                              r=["minitT"], w=["r_rt"])
                    kb.tt("dve", t3[:], IGa[:], Rt[:], ALU.subtract, r=["r_ig", "r_rt"], w=["r_t3"])
                    kb.act(t3[:], t3[:], AF.Exp, r=["r_t3"], w=["r_t3"])
                    kb.tt("dve", t4[:], Bc[k][:], Rt[:], ALU.add, r=["r_bc%d" % k, "r_rt"], w=["r_t4"])
                    kb.act(t4[:], t4[:], AF.Exp, r=["r_t4"], w=["r_t4"], scale=-1.0)
                    kb.tr(pg[:, 256:260], t3[0:4, :], C[0:4, 0:4], r=["r_t3", "C"], w=[PS[7]])
                    kb.tr(pg[:, 260:264], t4[0:4, :], C[0:4, 0:4], r=["r_t4", "C"], w=[PS[7]])
                    if prompt:
                        kb.tt("dve", dd[:, 0:1], Rt[:, 0:1], Mx[k][:, 127:128], ALU.subtract, r=["r_rt", "r_mx%d" % k], w=["r_dd"])
                        kb.act(dd[:, 0:1], dd[:, 0:1], AF.Exp, r=["r_dd"], w=["r_dd"])
                        kb.ts("dve", DDm[:, 0:4], C[0:4, 0:4], dd[:, 0:1], None, ALU.mult, None, r=["r_dd", "C"], w=["r_DD"])
                        kb.mm(pg[:, 264:268], lhsT=ones[0:4, :], rhs=DDm[:, 0:4], start=True, stop=True, r=["C", "r_DD"], w=[PS[7]])
                        kb.cp("dve", colq[k][:, 0:12], pg[:, 256:268], r=[PS[7]], w=["colq%d" % k])
                        if i == 15:
                            kb.tt("dve", mout[:, 0:1], Bc[k][:, 127:128], Mx[k][:, 127:128], ALU.add, r=["r_bc%d" % k, "r_mx%d" % k], w=["r_mout"])
                            kb.dma("sp", o_pmm, mout[:, 0:1], r=["r_mout"], w=())
                    else:
                        MT = Mx[k][:].rearrange("p (q t) -> p q t", t=8)[:, :, 7:8]
                        kb.tt("dve", t4[:].rearrange("p (q t) -> p q t", t=8), IGa[:].rearrange("p (q t) -> p q t", t=8),
                              MT.to_broadcast([4, 16, 8]), ALU.subtract, r=["r_ig", "r_mx%d" % k, PS[7]], w=["r_t4"])
                        kb.act(t4[:], t4[:], AF.Exp, r=["r_t4"], w=["r_t4"])
                        kb.tr(pg[:, 264:268], t4[0:4, :], C[0:4, 0:4], r=["r_t4", "C"], w=[PS[7]])
                        kb.cp("dve", colq[k][:, 0:12], pg[:, 256:268], r=[PS[7]], w=["colq%d" % k])
                        kb.tt("dve", dd[:].unsqueeze(2), minitT[:].unsqueeze(2), MT, ALU.subtract, r=["minitT", "r_mx%d" % k], w=["r_dd"])
                        kb.act(dd[:], dd[:], AF.Exp, r=["r_dd"], w=["r_dd"])
                        kb.tt("dve", DDm[:].rearrange("p (q h) -> p q h", h=4), dd[:].unsqueeze(2).to_broadcast([4, 16, 4]),
                              C[0:4, 0:4].unsqueeze(1).to_broadcast([4, 16, 4]), ALU.mult, r=["r_dd", "C"], w=["r_DD"])
                        kb.mm(pg[:, 272:336], lhsT=ones[0:4, :], rhs=DDm[:], start=True, stop=True, r=["C", "r_DD"], w=[PS[7]])
                        kb.cp("dve", decsb[:], pg[:, 272:336], r=[PS[7]], w=["decsb"])
                        kb.tt("dve", mout[:].unsqueeze(2), Bc[k][:].rearrange("p (q t) -> p q t", t=8)[:, :, 7:8], MT, ALU.add,
                              r=["r_bc%d" % k, "r_mx%d" % k], w=["r_mout"])
                        kb.dma("sp", o_smm, mout[:], r=["r_mout"], w=())

                def mlstm_tile(i):
                    k = i % 2
                    tg = i // 4
                    hT = hTg[tg % 2]
                    tc0 = (i % 4) * 128
                    lc0 = (i % 4) * 128 if i < 16 else 0
                    cq = colq[k]
                    vx = vext[k]
                    grp = "hTg%d" % (tg % 2)
                    for bi, c0 in enumerate((512, 1024, 1536)):
                        bank = 2 + (bi % 2)
                        for kc in range(8):
                            kb.mm(ps[bank][:, :], lhsT=hT[:, kc, tc0:tc0 + 128], rhs=winA[:, kc, c0:c0 + 512], start=(kc == 0), stop=(kc == 7),
                                  r=["winA", grp], w=[PS[bank]])
                        if bi == 0:
                            kb.act(ktok[:].rearrange("p a b -> p (a b)"), ps[bank][:, :], AF.Identity, r=[PS[bank]], w=["ktok"], scale=DKS)
                            for h in range(4):
                                kb.ts("dve", kw[:, h, :], ktok[:, h, :], cq[:, h:h + 1], None, ALU.mult, None,
                                      r=["ktok", "colq%d" % k], w=["kw"])
                        elif bi == 1:
                            kb.cp("act", vx[:, :, 0:128], ps[bank][:, :].rearrange("p (h d) -> p h d", d=128), r=[PS[bank]], w=["vext%d" % k])
                        else:
                            kb.act(sigo[:], ps[bank][:, :], AF.Sigmoid, r=[PS[bank]], w=["tA0"])
                    if i == 0:
                        stop_at(31)
                    for h in range(4):
                        kb.mm(ps[4][:, h * 128:(h + 1) * 128], lhsT=qkT[:, 4 + h, lc0:lc0 + 128], rhs=qkT[:, h, lc0:lc0 + 128],
                              start=True, stop=True, r=["qkT"], w=[PS[4]], sig=(h == 3))
                    if i == 0:
                        stop_at(32)
                    msk = maskP if i < 16 else maskS
                    for h in range(4):
                        kb.stt(PT[:, h, :], ps[4][:, h * 128:(h + 1) * 128], cq[:, h:h + 1], msk, ALU.mult, ALU.mult,
                               r=[PS[4], "colq%d" % k, "C"], w=["PT"])
                    return cq, vx, lc0

                def numden_finish(i, cq):
                    for half in range(2):
                        bank = ps[5 + half]
                        den = bank[:, 0:260].rearrange("p (h d) -> p h d", d=130)[:, :, 128:129]
                        kb.act(dmax[:, 2 * half:2 * half + 2].unsqueeze(2), den, AF.Abs, r=[PS[5 + half]], w=["dmax"])
                        kb.tt("dve", dmax[:, 2 * half:2 * half + 2], dmax[:, 2 * half:2 * half + 2], cq[:, 4 + 2 * half:6 + 2 * half], ALU.max,
                              r=["dmax", "colq%d" % (i % 2)], w=["dmax"])
                    s.add("dve", lambda g_: g_.reciprocal(out=dmax[:], in_=dmax[:]), r=["dmax"], w=["dmax"], tag="recip")
                    for h in range(4):
                        bank = ps[5 + h // 2]
                        o0 = (h % 2) * 130
                        kb.act(hmf[:, h * 128:(h + 1) * 128], bank[:, o0:o0 + 128], AF.Identity, r=[PS[5 + h // 2], "dmax"], w=["tB0"],
                               scale=dmax[:, h:h + 1])
                    for h in range(4):
                        s.add("dve", lambda g_, h=h: g_.bn_stats(out=hst6[:, h, :], in_=hmf[:, h * 128:(h + 1) * 128]), r=["tB0"], w=["hst6"], tag="bnst")
                    for h in range(4):
                        s.add("dve", lambda g_, h=h: g_.bn_aggr(out=hmv[:, h, :], in_=hst6[:, h, :]), r=["hst6"], w=["hmv"], tag="bnag")
                    kb.act(hrs[:].unsqueeze(2), hmv[:, :, 1:2], AF.Sqrt, r=["hmv"], w=["hrs"], bias=1e-6, scale=1.0)
                    s.add("dve", lambda g_: g_.reciprocal(out=hrs[:], in_=hrs[:]), r=["hrs"], w=["hrs"], tag="recip")
                    kb.tt("dve", hnm[:].unsqueeze(2), hmv[:, :, 0:1], hrs[:].unsqueeze(2), ALU.mult, r=["hmv", "hrs"], w=["hnm"])
                    kb.ts("dve", hnm[:], hnm[:], -1.0, None, ALU.mult, None, r=["hnm"], w=["hnm"])
                    for h in range(4):
                        kb.act(hmf[:, h * 128:(h + 1) * 128], hmf[:, h * 128:(h + 1) * 128], AF.Identity, r=["tB0", "hrs", "hnm"], w=["tB0"],
                               bias=hnm[:, h:h + 1], scale=hrs[:, h:h + 1])
                    kb.tt("dve", hmf[:], hmf[:], mng[:], ALU.mult, r=["tB0", "mng"], w=["tB0"])
                    kb.tt("dve", hmf[:], hmf[:], sigo[:], ALU.mult, r=["tB0", "tA0"], w=["tB0"])
                    for h in range(4):
                        kb.tr(ps[4][:, h * 128:(h + 1) * 128], hmf[:, h * 128:(h + 1) * 128], ident, r=["tB0", "C"], w=[PS[4]], sig=(h == 3))
                    kb.cp("act", hmT[:, :, i * 128:(i + 1) * 128], ps[4][:, :].rearrange("p (h d) -> p h d", d=128), r=[PS[4]], w=["hmT%d" % i])

                pend_rows = []
                for tg, (t0, nt_) in enumerate(TGS):
                    tiles = range(4 * tg, 4 * tg + 4) if tg < 4 else [16]
                    hT = hTg[tg % 2]
                    for i in tiles:
                        make_hT(hT, i, prescale=False, col0=(i % 4) * 128, res="hTg%d" % (tg % 2))
                    for j in range(8):
                        bank = j % 2
                        for kc in range(8):
                            kb.mm(ps[bank][:, 0:nt_], lhsT=winA[:, kc, j * 128:(j + 1) * 128], rhs=hT[:, kc, 0:nt_], start=(kc == 0), stop=(kc == 7),
                                  r=["winA", "hTg%d" % (tg % 2)], w=[PS[bank]])
                        if j < 4:
                            kb.cp("act", qkT[:, j, 0:nt_], ps[bank][:, 0:nt_], r=[PS[bank]], w=["qkT"])
                        else:
                            kb.ts("dve", qkT[:, j, 0:nt_], ps[bank][:, 0:nt_], DKS, None, ALU.mult, None, r=[PS[bank]], w=["qkT"])
                    for i in tiles:
                        if i == 0:
                            stop_at(1)
                        if i % 4 == 0 or i == 16:
                            rows(i)
                        else:
                            while pend_rows:
                                s.add(*pend_rows.pop(0))
                        if i < 16 and i % 4 != 3:
                            s.capture = []
                            rows(i + 1)
                            pend_rows.extend(s.capture)
                            s.capture = None
                            s.tick_fn = lambda: (s.add(*pend_rows.pop(0)) if pend_rows else None)
                        else:
                            s.tick_fn = None
                        if i == 0:
                            stop_at(2)
                        cq, vx, lc0 = mlstm_tile(i)
                        if i == 0:
                            stop_at(3)
                        if i == 16:
                            stop_at(5)
                        if i < 16:
                            for h in range(4):
                                bank = ps[5 + h // 2]
                                o0 = (h % 2) * 130
                                kb.mm(bank[:, o0:o0 + 130], lhsT=PT[:, h, :], rhs=vx[:, h, :], start=True, stop=False, r=["PT", "vext%d" % (i % 2)], w=[PS[5 + h // 2]], sig=False)
                                kb.mm(bank[:, o0:o0 + 130], lhsT=qkT[:, h, lc0:lc0 + 128], rhs=Cb[:, h, :], start=False, stop=True, r=["qkT", "Cb"], w=[PS[5 + h // 2]], sig=True)
                            numden_finish(i, cq)
                            for h in range(4):
                                bank = ps[5 + h // 2]
                                o0 = (h % 2) * 130
                                kb.mm(bank[:, o0:o0 + 130], lhsT=kw[:, h, :], rhs=vx[:, h, :], start=True, stop=True, r=["kw", "vext%d" % (i % 2)], w=[PS[5 + h // 2]])
                            for h in range(4):
                                bank = ps[5 + h // 2]
                                o0 = (h % 2) * 130
                                kb.ts("dve", Cst[:, h, :], Cst[:, h, :], cq[:, 8 + h:9 + h], None, ALU.mult, None, r=["Cst", "colq%d" % (i % 2)], w=["Cst"])
                                kb.stt(Cst[:, h, :], bank[:, o0:o0 + 130], cq[:, 8 + h:9 + h], Cst[:, h, :], ALU.mult, ALU.add,
                                       r=[PS[5 + h // 2], "Cst", "colq%d" % (i % 2)], w=["Cst"])
                            kb.cp("act", Cb[:], Cst[:], r=["Cst"], w=["Cb"])
                            if i == 0:
                                stop_at(4)
                            if i == 15:
                                for h in range(4):
                                    kb.tr(ps[4][:, h * 128:(h + 1) * 128], Cst[:, h, 0:128], ident, r=["Cst", "C"], w=[PS[4]], sig=(h == 3))
                                kb.cp("act", hmf[:], ps[4][:, :], r=[PS[4]], w=["tB0"])
                                kb.dma("sp", o_pmC.rearrange("h v k -> v h k"), hmf[:].rearrange("p (h k) -> p h k", k=128), r=["tB0"], w=())
                                kb.dma("sp", o_pmn.rearrange("h k -> k h"), Cst[:, :, 128], r=["Cst"], w=(), allow_slow_non_contiguous=True)
                        else:
                            with contextlib.ExitStack() as psm:
                                Cin = kb.sb(psm, "Cin", [128, 16, 128], F32)
                                CsT = kb.sb(psm, "CsT", [128, 16, 130], BF16)
                                qTm = kb.sb(psm, "qTm", [128, 16, 128], BF16)
                                VWm = kb.sb(psm, "VWm", [128, 16, 128], BF16)
                                nin = kb.sb(psm, "nin", [16, 4, 128], F32)
                                ninT = kb.sb(psm, "ninT", [128, 4, 16], F32)
                                BMW = kb.sb(psm, "BMW", [128, 16], BF16)
                                decc = kb.sb(psm, "decc", [16, 4], F32)
                                nout = nin
                                kb.memset("pool", qTm[:], 0.0, w=["qTm"])
                                kb.memset("pool", CsT[:], 0.0, w=["CsT"])
                                kb.dma("sp", nin[:], smn, r=(), w=["nin"])
                                kb.tr(ps[7][0:16, 400:404], dd[0:4, :], C[0:4, 0:4], r=["r_dd", "C"], w=[PS[7]])
                                kb.cp("dve", decc[:], ps[7][0:16, 400:404], r=[PS[7]], w=["decc"])
                                for h in range(4):
                                    kb.tr(ps[7][:, 416 + h * 16:432 + h * 16], nin[0:16, h, :], C[0:16, 0:16], r=["nin", "C"], w=[PS[7]], sig=(h == 3))
                                kb.cp("dve", ninT[:].rearrange("p a b -> p (a b)"), ps[7][:, 416:480], r=[PS[7]], w=["ninT"])
                                for h in range(4):
                                    bank = ps[5 + h // 2]
                                    o0 = (h % 2) * 130
                                    kb.dma("sp", Cin[:], smC[:, h].rearrange("q v k -> v q k"), r=(), w=["Cin"])
                                    for q4 in range(4):
                                        pb = q4 % 2
                                        for qq in range(4):
                                            q = q4 * 4 + qq
                                            kb.tr(ps[pb][:, qq * 128:(qq + 1) * 128], Cin[:, q, :], ident, r=["Cin", "C"], w=[PS[pb]], sig=(qq == 3))
                                        kb.cp("act", CsT[:, q4 * 4:q4 * 4 + 4, 0:128], ps[pb][:, :].rearrange("p (a b) -> p a b", b=128), r=[PS[pb]], w=["CsT"])
                                    kb.cp("dve", CsT[:, :, 128:129], ninT[:, h, :].unsqueeze(2), r=["ninT"], w=["CsT"])
                                    kb.cp("pool", bass.AP(qTm, 0, [[2048, 128], [136, 16], [1, 8]]),
                                          qkT[:, h, 0:128].rearrange("p (q t) -> p q t", t=8), r=["qkT"], w=["qTm"])
                                    kb.mm(bank[:, o0:o0 + 130], lhsT=PT[:, h, :], rhs=vx[:, h, :], start=True, stop=False, r=["PT", "vext%d" % (i % 2)], w=[PS[5 + h // 2]], sig=False)
                                    for q in range(16):
                                        kb.mm(bank[:, o0:o0 + 130], lhsT=qTm[:, q, :], rhs=CsT[:, q, :], start=False, stop=(q == 15), r=["qTm", "CsT"], w=[PS[5 + h // 2]], sig=(q == 15))
                                    kb.ts("dve", BMW[:], bms, cq[:, 8 + h:9 + h], None, ALU.mult, None, r=["C", "colq%d" % (i % 2)], w=["BMW"])
                                    kb.tt("dve", VWm[:], vx[:, h, 0:128].unsqueeze(1).to_broadcast([128, 16, 128]), BMW[:].unsqueeze(2).to_broadcast([128, 16, 128]),
                                          ALU.mult, r=["vext%d" % (i % 2), "BMW"], w=["VWm"])
                                    for q4 in range(4):
                                        pb = q4 % 2
                                        for qq in range(4):
                                            q = q4 * 4 + qq
                                            kb.mm(ps[pb][:, qq * 128:(qq + 1) * 128], lhsT=VWm[:, q, :], rhs=ktok[:, h, :], start=True, stop=True,
                                                  r=["VWm", "ktok"], w=[PS[pb]], sig=(qq == 3))
                                        for qq in range(4):
                                            q = q4 * 4 + qq
                                            kb.stt(Cin[:, q, :], Cin[:, q, :], decsb[:, q * 4 + h:q * 4 + h + 1], ps[pb][:, qq * 128:(qq + 1) * 128], ALU.mult, ALU.add,
                                                   r=["Cin", "decsb", PS[pb]], w=["Cin"])
                                    kb.dma("sp", o_smC[:, h].rearrange("q v k -> v q k"), Cin[:], r=["Cin"], w=())
                                    kb.mm(ps[7][0:16, 0:128], lhsT=BMW[:], rhs=ktok[:, h, :], start=True, stop=True, r=["BMW", "ktok"], w=[PS[7]])
                                    kb.stt(nout[:, h, :], nin[:, h, :], decc[:, h:h + 1], ps[7][0:16, 0:128], ALU.mult, ALU.add, r=["nin", "decc", PS[7]], w=["nin"])
                                numden_finish(i, cq)
                                kb.dma("sp", o_smn, nout[:], r=["nin"], w=())
                                s.barrier()
            s.barrier()
            s.tick_fn = None
            stop_at(6)
            hrT = kb.sb(ph, "hrT", [128, 4, NTOK], BF16)
            with contextlib.ExitStack() as pb_:
                winB = kb.sb(pb_, "winB", [128, 8, 1024], BF16)
                kb.dma("pool", winB[:], win_v[:, :, 2056:3080], r=(), w=["winB"])
                hTgB = [kb.sb(pb_, "hTgB%d" % i, [128, 8, 512], BF16) for i in range(2)]
                WA = kb.sb(pb_, "WA", [128, 4, 128], F32)
                WX = kb.sb(pb_, "WX", [128, 4, 128], F32)
                kb.memset("pool", WA[:], 0.0, w=["WA"])
                kb.memset("pool", WX[:], 0.0, w=["WX"])
                for c in range(4):
                    for hp in range(2):
                        kb.dma("sp", WA[hp * 64:(hp + 1) * 64, c, hp * 64:(hp + 1) * 64], rg_w_a[0, 2 * c + hp], r=(), w=["WA"])
                        kb.dma("sp", WX[hp * 64:(hp + 1) * 64, c, hp * 64:(hp + 1) * 64], rg_w_x[0, 2 * c + hp], r=(), w=["WX"])
                cl = kb.sb(pb_, "cl", [128, 4], F32)
                cl2 = kb.sb(pb_, "cl2", [128, 4], F32)
                kb.act(cl[:], vecA[:, 28:32], AF.Exp, r=["vecA"], w=["cl"], scale=-1.0)
                kb.act(cl[:], cl[:], AF.Ln, r=["cl"], w=["cl"], bias=1.0, scale=1.0)
                kb.ts("dve", cl2[:], cl[:], -16.0, None, ALU.mult, None, r=["cl"], w=["cl2"])
                kb.ts("dve", cl[:], cl[:], -8.0, None, ALU.mult, None, r=["cl", "cl2"], w=["cl"])
                xp = [kb.sb(pb_, "xp%d" % c, [128, 515], F32) for c in range(4)]
                xpss = [kb.sb(pb_, "xps%d" % k, [128, 16, 11], F32) for k in range(2)]
                hst = kb.sb(pb_, "hst", [128, 4], F32)
                h0T = kb.sb(pb_, "h0T", [128, 4, 16], F32)
                cvT = kb.sb(pb_, "cvT", [128, 4, 48], F32)
                hl = kb.sb(pb_, "hl", [128, 4, 16], F32)
                srh_sb = kb.sb(pb_, "srh_sb", [16, 512], F32)
                src_sb = kb.sb(pb_, "src_sb", [48, 512], F32)
                F5 = lambda nm: kb.sb(pb_, nm, [128, 512], F32)
                rgsets = [{nm: F5("%s%d" % (nm, k)) for nm in ("xc", "rr", "ii", "aa", "a2", "t5")} for k in range(2)]
                for c in range(4):
                    kb.memset("pool", xp[c][:, 0:3], 0.0, w=["xp%d" % c])
                kb.memset("pool", hst[:], 0.0, w=["hst0", "hst1", "hst2", "hst3"])
                kb.dma("sp", srh_sb[:], srh, r=(), w=["srh_sb"])
                kb.dma("sp", src_sb[:], srconv, r=(), w=["src_sb"])
                for c in range(4):
                    kb.tr(ps[6][:, c * 16:(c + 1) * 16], srh_sb[0:16, c * 128:(c + 1) * 128], C[0:16, 0:16], r=["srh_sb", "C"], w=[PS[6]], sig=(c == 3))
                kb.cp("dve", h0T[:].rearrange("p a b -> p (a b)"), ps[6][:, 0:64], r=[PS[6]], w=["h0T"])
                for c in range(4):
                    kb.tr(ps[6][:, 64 + c * 48:64 + (c + 1) * 48], src_sb[0:48, c * 128:(c + 1) * 128], C[0:48, 0:48], r=["src_sb", "C"], w=[PS[6]], sig=(c == 3))
                kb.cp("dve", cvT[:].rearrange("p a b -> p (a b)"), ps[6][:, 64:256], r=[PS[6]], w=["cvT"])
                def rg_unit(tg, t0, n, sample, hT, c, k):
                    u_ = tg * 4 + c
                    px, pgr = ps[(2 * u_) % 4], ps[(2 * u_ + 1) % 4]
                    PX, PGR = PS[(2 * u_) % 4], PS[(2 * u_ + 1) % 4]
                    S_ = rgsets[k]
                    xc, rr, ii, aa, a2, t5 = S_["xc"], S_["rr"], S_["ii"], S_["aa"], S_["a2"], S_["t5"]
                    uu, hh_ = a2, rr
                    gA, gX = (4, 5) if k == 0 else (6, 7)
                    xps = xpss[k]
                    nxps = "xps%d" % k
                    nx, nr, ni, na, n2, nt, nh = "xc%d" % k, "rr%d" % k, "ii%d" % k, "aa%d" % k, "a2%d" % k, "t5%d" % k, "hst%d" % c
                    for kc in range(8):
                        kb.mm(px[:, 0:n], lhsT=winB[:, kc, c * 128:(c + 1) * 128], rhs=hT[:, kc, 0:n], start=(kc == 0), stop=(kc == 7),
                              r=["winB", "hTg%d" % (tg % 2)], w=[PX])
                    for kc in range(8):
                        kb.mm(pgr[:, 0:n], lhsT=winB[:, kc, 512 + c * 128:512 + (c + 1) * 128], rhs=hT[:, kc, 0:n], start=(kc == 0), stop=(kc == 7),
                              r=["winB", "hTg%d" % (tg % 2)], w=[PGR])
                    cw = lambda j: vecA[:, c * 4 + j:c * 4 + j + 1]
                    cb = vecA[:, 16 + c:17 + c]
                    if not sample:
                        kb.cp("act", xp[c][:, 3:3 + n], px[:, 0:n], r=[PX], w=["xp%d" % c])
                        kb.ts("dve", xc[:, 0:n], xp[c][:, 0:n], cw(0), cb, ALU.mult, ALU.add, r=["xp%d" % c, "vecA"], w=[nx])
                        for j in range(1, 4):
                            kb.stt(xc[:, 0:n], xp[c][:, j:j + n], cw(j), xc[:, 0:n], ALU.mult, ALU.add, r=["xp%d" % c, "vecA", nx], w=[nx])
                        if tg != 3:
                            kb.cp("pool", xp[c][:, 0:3], xp[c][:, n:n + 3], r=["xp%d" % c], w=["xp%d" % c])
                    else:
                        kb.cp("dve", xps[:, :, 0:3], cvT[:, c, :].rearrange("p (q j) -> p q j", j=3), r=["cvT"], w=[nxps])
                        kb.cp("act", xps[:, :, 3:11], px[:, 0:n].rearrange("p (q t) -> p q t", t=8), r=[PX], w=[nxps])
                        xc3 = xc[:, 0:n].rearrange("p (q t) -> p q t", t=8)
                        kb.ts("dve", xc3, xps[:, :, 0:8], cw(0), cb, ALU.mult, ALU.add, r=[nxps, "vecA"], w=[nx])
                        for j in range(1, 4):
                            kb.stt(xc3, xps[:, :, j:j + 8], cw(j), xc3, ALU.mult, ALU.add, r=[nxps, "vecA", nx], w=[nx])
                        for j in range(3):
                            kb.tr(ps[gX][0:16, j * 128:(j + 1) * 128], xps[:, :, 8 + j], ident, r=[nxps, "C"], w=[PS[gX]], sig=(j == 2))
                        kb.cp("dve", t5[0:16, 0:384], ps[gX][0:16, 0:384], r=[PS[gX]], w=[nt])
                        kb.dma("sp", o_srconv.rearrange("(q j) f -> q j f", j=3)[:, :, c * 128:(c + 1) * 128],
                               t5[0:16, 0:384].rearrange("q (j f) -> q j f", f=128), r=[nt], w=())
                    kb.mm(ps[gA][:, 0:n], lhsT=WA[:, c, :], rhs=xc[:, 0:n], start=True, stop=True, r=["WA", nx], w=[PS[gA]])
                    kb.mm(ps[gX][:, 0:n], lhsT=WX[:, c, :], rhs=xc[:, 0:n], start=True, stop=True, r=["WX", nx], w=[PS[gX]])
                    kb.act(rr[:, 0:n], ps[gA][:, 0:n], AF.Sigmoid, r=[PS[gA], "vecA"], w=[nr], bias=vecA[:, 20 + c:21 + c], scale=1.0)
                    kb.act(ii[:, 0:n], ps[gX][:, 0:n], AF.Sigmoid, r=[PS[gX], "vecA"], w=[ni], bias=vecA[:, 24 + c:25 + c], scale=1.0)
                    kb.act(aa[:, 0:n], rr[:, 0:n], AF.Exp, r=[nr, "cl"], w=[na], scale=cl[:, c:c + 1])
                    kb.act(a2[:, 0:n], rr[:, 0:n], AF.Exp, r=[nr, "cl2"], w=[n2], scale=cl2[:, c:c + 1])
                    kb.act(a2[:, 0:n], a2[:, 0:n], AF.Sqrt, r=[n2], w=[n2], bias=1.0, scale=-1.0)
                    kb.tt("dve", uu[:, 0:n], a2[:, 0:n], ii[:, 0:n], ALU.mult, r=[n2, ni], w=[n2])
                    kb.tt("dve", uu[:, 0:n], uu[:, 0:n], xc[:, 0:n], ALU.mult, r=[n2, nx], w=[n2])
                    if not sample:
                        s.add("dve", lambda g_, c=c, n=n: g_.tensor_tensor_scan(out=hh_[:, 0:n], data0=aa[:, 0:n], data1=uu[:, 0:n], initial=hst[:, c:c + 1],
                                                                                 op0=ALU.mult, op1=ALU.add),
                              r=[na, n2, nh], w=[nr], tag="scanH")
                        kb.cp("dve", hst[:, c:c + 1], hh_[:, n - 1:n], r=[nr], w=[nh])
                    else:
                        aa3 = aa[:, 0:n].rearrange("p (q t) -> p q t", t=8)
                        uu3 = uu[:, 0:n].rearrange("p (q t) -> p q t", t=8)
                        kb.tt("dve", t5[:, 0:16].unsqueeze(2), aa3[:, :, 0:1], h0T[:, c, :].unsqueeze(2), ALU.mult, r=[na, "h0T", nt], w=[nt])
                        kb.tt("dve", uu3[:, :, 0:1], uu3[:, :, 0:1], t5[:, 0:16].unsqueeze(2), ALU.add, r=[n2, nt], w=[n2])
                        kb.tt("dve", aa[:, 0:n], aa[:, 0:n], rst, ALU.mult, r=[na, "C"], w=[na])
                        s.add("dve", lambda g_, n=n: g_.tensor_tensor_scan(out=hh_[:, 0:n], data0=aa[:, 0:n], data1=uu[:, 0:n], initial=0.0,
                                                                            op0=ALU.mult, op1=ALU.add),
                              r=[na, n2], w=[nr], tag="scanH")
                        kb.cp("dve", hl[:, c, :].unsqueeze(2), hh_[:, 0:n].rearrange("p (q t) -> p q t", t=8)[:, :, 7:8], r=[nr], w=["hl"])
                    kb.act(t5[:, 0:n], pgr[:, 0:n], AF.Square, r=[PGR, nt], w=[nt])
                    kb.ts("dve", t5[:, 0:n], t5[:, 0:n], 0.044715, 1.0, ALU.mult, ALU.add, r=[nt], w=[nt])
                    kb.tt("dve", t5[:, 0:n], t5[:, 0:n], pgr[:, 0:n], ALU.mult, r=[nt, PGR], w=[nt])
                    kb.act(t5[:, 0:n], t5[:, 0:n], AF.Tanh, r=[nt], w=[nt], scale=0.7978845608028654)
                    kb.ts("dve", t5[:, 0:n], t5[:, 0:n], 1.0, 0.5, ALU.add, ALU.mult, r=[nt], w=[nt])
                    kb.tt("dve", t5[:, 0:n], t5[:, 0:n], pgr[:, 0:n], ALU.mult, r=[nt, PGR], w=[nt])
                    kb.tt("dve", hrT[:, c, t0:t0 + n], t5[:, 0:n], hh_[:, 0:n], ALU.mult, r=[nt, nr], w=["hrT_g%d" % tg])

                for tg, (t0, n) in enumerate(TGS):
                    sample = tg == 4
                    hT = hTgB[tg % 2]
                    for i in (range(4 * tg, 4 * tg + 4) if tg < 4 else [16]):
                        make_hT(hT, i, prescale=True, col0=(i % 4) * 128, res="hTg%d" % (tg % 2))
                    for c0 in (0, 2):
                        s.capture = []
                        rg_unit(tg, t0, n, sample, hT, c0 + 1, 1)
                        pend_rg = s.capture
                        s.capture = None
                        s.tick_fn = lambda pr=pend_rg: (s.add(*pr.pop(0)) if pr else None)
                        rg_unit(tg, t0, n, sample, hT, c0, 0)
                        s.tick_fn = None
                        while pend_rg:
                            s.add(*pend_rg.pop(0))
                    if tg == 3:
                        for c in range(4):
                            kb.tr(ps[6][0:3, c * 128:(c + 1) * 128], xp[c][:, 512:515], ident, r=["xp%d" % c, "C"], w=[PS[6]], sig=(c == 3))
                        t5 = rgsets[0]["t5"]
                        kb.cp("dve", t5[0:3, :], ps[6][0:3, 0:512], r=[PS[6]], w=["t50"])
                        kb.dma("sp", o_prconv, t5[0:3, :], r=["t50"], w=())
                rr, ii = rgsets[0]["rr"], rgsets[0]["ii"]
                kb.tr(ps[7][0:4, 0:128], hst[:, 0:4], ident, r=["hst0", "hst1", "hst2", "hst3", "C"], w=[PS[7]])
                kb.cp("dve", rr[0:4, 0:128], ps[7][0:4, 0:128], r=[PS[7]], w=["rr0"])
                kb.dma("sp", o_prh, rr[0:4, 0:128], r=["rr0"], w=())
                for c in range(4):
                    kb.tr(ps[4][0:16, c * 128:(c + 1) * 128], hl[:, c, :], ident, r=["hl", "C"], w=[PS[4]], sig=(c == 3))
                kb.cp("dve", ii[0:16, :], ps[4][0:16, :], r=[PS[4]], w=["ii0"])
                kb.dma("sp", o_srh, ii[0:16, :], r=["ii0"], w=())
                s.barrier()
            stop_at(7)
            with contextlib.ExitStack() as pc_:
                alloc_gl(pc_)
                mod_prepare(l, 1, 1.0, blocks=[4, 5])
                Gp, Gs = gl["Gp"], gl["Gs"]
                wout = kb.sb(pc_, "wout", [128, 8, D], BF16)
                kb.dma("pool", wout[:], ab_w_out[0].rearrange("(kc p) n -> p kc n", p=128), r=(), w=["wout"])
                proj_acc(wout, "wout", 8, lambda i, kc: ((hmT[:, kc, i * 128:(i + 1) * 128], "hmT%d" % i) if kc < 4 else
                                                          (hrT[:, kc - 4, i * 128:(i + 1) * 128], "hrT_g%d" % (i // 4))), True, True)
                s.barrier()
        s.barrier()

    CW = -0.6065306597126334
    RT = BF16

    def rwkv_mixer(l):
        rwkv_mixer_(l)
        s.skip = False
        s.barrier()

    def rwkv_mixer_(l):
        rw_mu = kb.din("muT", [128, 48])
        rw_wr = kb.din("rw_wr", [1, D, D])
        rw_wk = kb.din("rw_wk", [1, D, D])
        rw_wv = kb.din("rw_wv", [1, D, D])
        rw_wo = kb.din("rw_wo", [1, D, D])
        rw_w0 = kb.din("rw_w0", [1, D])
        rw_w1 = kb.din("rw_w1", [1, D, 64])
        rw_w2 = kb.din("rw_w2", [1, 64, D])
        rw_a0 = kb.din("rw_a0", [1, D])
        rw_a1 = kb.din("rw_a1", [1, D, 64])
        rw_a2 = kb.din("rw_a2", [1, 64, D])
        rw_g1 = kb.din("rw_g1", [1, D, 128])
        rw_g2 = kb.din("rw_g2", [1, 128, D])
        rw_kk = kb.din("rw_k_k", [1, D])
        rw_ka = kb.din("rw_k_a", [1, D])
        rw_rk = kb.din("rk_flat", [1, D])
        rw_lng = kb.din("rw_lnx_g", [1, D])
        rw_lnb = kb.din("rw_lnx_b", [1, D])
        swkv = kb.din("swkv", [16, 16, 64, 64])
        sshift = kb.din("sshift", [16, D])
        o_pwkv = kb.dout("o_pwkv", [16, 64, 64])
        o_pshift = kb.dout("o_pshift", [1, D])
        o_swkv = kb.dout("o_swkv", [16, 16, 64, 64])
        o_sshift = kb.dout("o_sshift", [16, D])

        cm = lambda nm: C[:, CST_OFF[nm]:CST_OFF[nm] + 128]
        low16, up16, m16, m32, m64 = cm("low16"), cm("up16"), cm("m16"), cm("m32"), cm("m64")
        maskP, maskS, upP, lowP, upS, lowS, blkS, ones, bms = cm("maskP"), cm("maskS"), cm("upP"), cm("lowP"), cm("upS"), cm("lowS"), cm("blkS"), cm("ones"), C[:, CST_OFF["bms"]:CST_OFF["bms"] + 16]

        mod_prepare(l, 1, 1.0, blocks=range(4))
        with contextlib.ExitStack() as ph:
            ygT = kb.sb(ph, "ygT", [128, 8, NTOK], BF16)
            muT = kb.sb(ph, "muT_sb", [128, 48], F32)
            kb.dma("sp", muT[:], rw_mu, r=(), w=["muT"])
            identR = kb.sb(ph, "identR", [128, 128], RT)
            kb.cp("dve", identR[:], ident, r=["C"], w=["identR"])
            w1b = kb.sb(ph, "w1b", [128, 8, 64], BF16)
            a1b = kb.sb(ph, "a1b", [128, 8, 64], BF16)
            g1b = kb.sb(ph, "g1b", [128, 8, 128], BF16)
            kb.dma("pool", w1b[:], rw_w1[0].rearrange("(kc p) n -> p kc n", p=128), r=(), w=["w1b"])
            kb.dma("pool", a1b[:], rw_a1[0].rearrange("(kc p) n -> p kc n", p=128), r=(), w=["a1b"])
            kb.dma("pool", g1b[:], rw_g1[0].rearrange("(kc p) n -> p kc n", p=128), r=(), w=["g1b"])
            w1m = kb.sb(ph, "w1m", [128, 8, 64], BF16)
            a1m = kb.sb(ph, "a1m", [128, 8, 64], BF16)
            g1m = kb.sb(ph, "g1m", [128, 8, 128], BF16)
            for wm_, wb_, rs_, j_, n_ in ((w1m, w1b, "w1b", 1, 64), (a1m, a1b, "a1b", 4, 64), (g1m, g1b, "g1b", 5, 128)):
                kb.tt("dve", wm_[:], wb_[:], muT[:, j_ * 8:(j_ + 1) * 8].unsqueeze(2).to_broadcast([128, 8, n_]), ALU.mult, r=[rs_, "muT"], w=[rs_ + "m"])
            hlast = kb.sb(ph, "hlast", [128, 8, 17], F32)
            sh0T = kb.sb(ph, "sh0T", [128, 8, 16], BF16)
            with contextlib.ExitStack() as p0:
                shs = kb.sb(p0, "shs", [16, D], F32)
                kb.dma("sp", shs[:], sshift, r=(), w=["shs"])
                for c in range(8):
                    kb.tr(ps[0][:, c * 16:(c + 1) * 16], shs[0:16, c * 128:(c + 1) * 128], C[0:16, 0:16], r=["shs", "C"], w=[PS[0]], sig=(c == 7))
                kb.cp("dve", sh0T[:].rearrange("p a b -> p (a b)"), ps[0][:, 0:128], r=[PS[0]], w=["sh0T"])
                s.barrier()

            def hook_last(i, c, src, psres):
                if i == 15:
                    kb.ts("dve", hlast[:, c, 0:1], src[:, 127:128], modT[:, 8 + c, 0:1], modT[:, c, 0:1], ALU.mult, ALU.add, r=["modT", psres], w=["hlast"])
                elif i == 16:
                    v3 = src.rearrange("p (q t) -> p q t", t=8)[:, :, 7:8]
                    kb.tt("dve", hlast[:, c, 1:17].unsqueeze(2), v3, modT[:, 8 + c, 1:NT].unsqueeze(2), ALU.mult, r=["modT", psres], w=["hlast"])
                    kb.tt("dve", hlast[:, c, 1:17], hlast[:, c, 1:17], modT[:, c, 1:NT], ALU.add, r=["hlast", "modT"], w=["hlast"])

            stop_at(51)
            for hg in range(4):
                c0 = hg * 256
                with contextlib.ExitStack() as pp:
                    wrs = kb.sb(pp, "wrs", [128, 8, 256], BF16)
                    wks = kb.sb(pp, "wks", [128, 8, 256], BF16)
                    wvs = kb.sb(pp, "wvs", [128, 8, 256], BF16)
                    for wt, src in ((wrs, rw_wr), (wks, rw_wk), (wvs, rw_wv)):
                        kb.dma("pool", wt[:], src[0].rearrange("(kc p) n -> p kc n", p=128)[:, :, c0:c0 + 256], r=(), w=["wqkv"])
                    wrm = kb.sb(pp, "wrm", [128, 8, 256], BF16)
                    wkm = kb.sb(pp, "wkm", [128, 8, 256], BF16)
                    wvm = kb.sb(pp, "wvm", [128, 8, 256], BF16)
                    for wm_, wt_, j_ in ((wrm, wrs, 0), (wkm, wks, 2), (wvm, wvs, 3)):
                        kb.tt("dve", wm_[:], wt_[:], muT[:, j_ * 8:(j_ + 1) * 8].unsqueeze(2).to_broadcast([128, 8, 256]), ALU.mult, r=["wqkv", "muT"], w=["wqkvm"])
                    w2s = kb.sb(pp, "w2s", [64, 256], BF16)
                    a2s = kb.sb(pp, "a2s", [64, 256], BF16)
                    g2s = kb.sb(pp, "g2s", [128, 256], BF16)
                    kb.dma("pool", w2s[:], rw_w2[0, :, c0:c0 + 256], r=(), w=["w2s"])
                    kb.dma("pool", a2s[:], rw_a2[0, :, c0:c0 + 256], r=(), w=["a2s"])
                    kb.dma("pool", g2s[:], rw_g2[0, :, c0:c0 + 256], r=(), w=["g2s"])
                    w0r = kb.sb(pp, "w0r", [1, 256], F32)
                    a0r = kb.sb(pp, "a0r", [1, 256], F32)
                    kb.dma("sp", w0r[:], rw_w0[0:1, c0:c0 + 256], r=(), w=["w0r"])
                    kb.dma("sp", a0r[:], rw_a0[0:1, c0:c0 + 256], r=(), w=["a0r"])
                    bcs = {}
                    alias = {"kkb": tA[2][:, 0:256], "kab": tA[2][:, 256:512], "rkb": tA[3][:, 0:256], "lngb": tA[3][:, 256:512], "lnbb": tA[0][:, 256:512]}
                    for nm, src in (("kkb", rw_kk), ("kab", rw_ka), ("rkb", rw_rk), ("lngb", rw_lng), ("lnbb", rw_lnb)):
                        bcs[nm] = alias[nm] if nm in alias else kb.sb(pp, nm, [128, 256], F32)
                        kb.dma("sp", bcs[nm][:], src[0:1, c0:c0 + 256].to_broadcast([128, 256]), r=(), w=[nm])
                    hTg = [kb.sb(pp, "hTr%d" % i, [128, 8, 130], BF16) for i in range(2)]
                    dx = kb.sb(pp, "dx", [128, 8, 128], BF16)
                    loT = kb.sb(pp, "loT", [128, 3, 128], BF16)
                    TB = lambda j: tB[j // 4][:, (j % 4) * 256:(j % 4) * 256 + 256]
                    Rr, Kk, KKn, Aa, SG, CSs, Ee, Tt = [TB(j) for j in range(8)]
                    tbres = lambda j: "tB%d" % (j // 4)
                    small = kb.sb(pp, "rwsmall", [128, 64], F32)
                    F3 = lambda nm: kb.sb(pp, nm, [128, 256], RT)
                    Vvs, ALs, RBs, KHs, BHs = [[F3("%s%d" % (nm, k)) for k in range(2)] for nm in ("Vv", "AL", "RB", "KH", "BH")]
                    BT, KT = F3("BTt"), F3("KTt")
                    GVs = [tA[0], kb.sb(pp, "GV1", [128, 256], F32)]
                    PLcs = [kb.sb(pp, "PLc%d" % k, [64, 64], F32) for k in range(2)]
                    smallfs = [kb.sb(pp, "smallf%d" % k, [128, 8], F32) for k in range(2)]
                    Tbk = tA[1][:, 256:512]
                    fmalls = [kb.sb(pp, "fmall%d" % k, [64, 4, 4, 128], RT) for k in range(2)]
                    fmTs = [{nm: fm_[:, j] for j, nm in enumerate(("alT", "btT", "ktT", "rbT"))} for fm_ in fmalls]
                    fmall = fmalls[0]
                    chainall = kb.sb(pp, "chainall", [128, 7, 4, 128], RT)
                    chn = {nm: chainall[:, j] for j, nm in enumerate(("ApA", "ApB", "BpA", "BpB", "TTa", "TTb", "Am"))}
                    S0nat = kb.sb(pp, "S0nat", [64, 16, 64], F32)
                    S0Tq = fmall[:, 0:2].rearrange("p a h t -> p (a h t)").rearrange("p (q k) -> p q k", k=64)
                    SLo = S0nat
                    AakT = kb.sb(pp, "AakT", [128, 4, 128], RT)
                    YTs = AakT[0:64, :, :]
                    ArbT = kb.sb(pp, "ArbT", [128, 4, 128], RT)
                    ArkT = kb.sb(pp, "ArkT", [128, 4, 128], RT)
                    Ahat = kb.sb(pp, "Ahat", [128, 4, 64], RT)
                    X1 = kb.sb(pp, "X1", [128, 4, 64], RT)
                    U0 = kb.sb(pp, "U0", [128, 4, 64], RT)
                    Gm = kb.sb(pp, "Gm", [64, 4, 64], RT)
                    RhT = kb.sb(pp, "RhT", [64, 4, 128], RT)
                    S0T = kb.sb(pp, "S0T", [64, 4, 64], RT)
                    yb = tA[1][:, 0:256]
                    kb.ts("dve", S0T[:].rearrange("p a b -> p (a b)"), C[0:64, 0:256], 0.0, None, ALU.mult, None, r=["C"], w=["S0T0", "S0T1", "S0T2", "S0T3"])
                    kb.memset("pool", hTg[1][:, :, 128:129], 0.0, w=["hTr1"])

                    algb = [4]

                    def nb():
                        b_ = algb[0]
                        algb[0] = 4 + (algb[0] - 3) % 4
                        return b_

                    def grp4(mmf, n_cols, m_rows=128):
                        b_ = nb()
                        for hl in range(4):
                            items = mmf(hl)
                            for j, (lt, rh, rd) in enumerate(items):
                                kb.mm(ps[b_][0:m_rows, hl * n_cols:(hl + 1) * n_cols], lhsT=lt, rhs=rh, start=(j == 0), stop=(j == len(items) - 1),
                                      r=rd, w=[PS[b_]], sig=(hl == 3 and j == len(items) - 1))
                        return b_

                    fsl = lambda t_, hl: t_[:, hl, :]
                    tsl = lambda t_, hl: t_[:, hl * 64:(hl + 1) * 64]

                    def sample_states():
                        BH, KH, Vv, PLc = BHs[0], KHs[0], Vvs[0], PLcs[0]
                        Bhm = chainall[:, 0:2].rearrange("p a h t -> p (a h t)").rearrange("p (q k) -> p q k", k=64)
                        Khm = chainall[:, 2:4].rearrange("p a h t -> p (a h t)").rearrange("p (q k) -> p q k", k=64)
                        Gq = chainall[0:64, 4:6].rearrange("p a h t -> p (a h t)").rearrange("p (q k) -> p q k", k=64)
                        bmq = bms.unsqueeze(2).to_broadcast([128, 16, 64])
                        for hl in range(4):
                            h = hg * 4 + hl
                            kb.tt("dve", Bhm, tsl(BH, hl).unsqueeze(1).to_broadcast([128, 16, 64]), bmq, ALU.mult, r=["BH0", "C"], w=["ApA0", "ApA1", "ApA2", "ApA3"] + ["ApB0", "ApB1", "ApB2", "ApB3"])
                            kb.tt("pool", Khm, tsl(KH, hl).unsqueeze(1).to_broadcast([128, 16, 64]), bmq, ALU.mult, r=["KH0", "C"], w=["BpA0", "BpA1", "BpA2", "BpA3"] + ["BpB0", "BpB1", "BpB2", "BpB3"])
                            kb.dma("sp", S0nat[:], swkv[:, h].rearrange("q v k -> v q k"), r=(), w=["S0nat"])
                            b0, b1 = nb(), nb()
                            for q in range(16):
                                bb = b0 if q < 8 else b1
                                kb.tr(ps[bb][0:64, (q % 8) * 64:(q % 8 + 1) * 64], S0nat[:, q, :], ident[0:64, 0:64], r=["S0nat", "C"], w=[PS[bb]], sig=(q % 8 == 7))
                            kb.cp("act", S0Tq[:, 0:8, :], ps[b0][0:64, :].rearrange("p (q k) -> p q k", k=64), r=[PS[b0]], w=["S0Tq", "alT0", "btT0"])
                            kb.cp("dve", S0Tq[:, 8:16, :], ps[b1][0:64, :].rearrange("p (q k) -> p q k", k=64), r=[PS[b1]], w=["S0Tq", "alT0", "btT0"])
                            b_ = nb()
                            for q in range(16):
                                kb.mm(ps[b_][0:64, q * 8:(q + 1) * 8], lhsT=S0Tq[:, q, :], rhs=RhT[:, hl, q * 8:(q + 1) * 8], start=True, stop=True,
                                      r=["S0Tq", "RhT%d" % hl], w=[PS[b_]], sig=(q == 15))
                            kb.cp("act", YTs[:, hl, :], ps[b_][0:64, 0:128], r=[PS[b_]], w=["AakT%d" % hl])
                            g0, g1 = nb(), nb()
                            for half, bb in ((0, g0), (1, g1)):
                                kb.mm(ps[bb][0:64, :], lhsT=Ahat[:, hl, :], rhs=Bhm[:, half * 8:(half + 1) * 8, :], start=True, stop=True,
                                      r=["Ahat%d" % hl] + ["ApA0", "ApA1", "ApA2", "ApA3"] + ["ApB0", "ApB1", "ApB2", "ApB3"], w=[PS[bb]])
                            for q in range(16):
                                bb = g0 if q < 8 else g1
                                kb.stt(Gq[:, q, :], ident[0:64, 0:64], PLc[:, hl * 16 + q:hl * 16 + q + 1], ps[bb][0:64, (q % 8) * 64:(q % 8 + 1) * 64], ALU.mult, ALU.add,
                                       r=["C", "PLc0", PS[bb]], w=["TTa0", "TTa1", "TTa2", "TTa3"] + ["TTb0", "TTb1", "TTb2", "TTb3"])
                            for half in range(2):
                                bb = nb()
                                kb.mm(ps[bb][0:64, :], lhsT=tsl(Vv, hl), rhs=Khm[:, half * 8:(half + 1) * 8, :], start=True, stop=False, r=["Vv0"] + ["BpA0", "BpA1", "BpA2", "BpA3"] + ["BpB0", "BpB1", "BpB2", "BpB3"], w=[PS[bb]], sig=False)
                                kb.mm(ps[bb][0:64, :], lhsT=U0[:, hl, :], rhs=Bhm[:, half * 8:(half + 1) * 8, :], start=False, stop=False, r=["U0%d" % hl] + ["ApA0", "ApA1", "ApA2", "ApA3"] + ["ApB0", "ApB1", "ApB2", "ApB3"], w=[PS[bb]], sig=False)
                                for qq in range(8):
                                    q = half * 8 + qq
                                    kb.mm(ps[bb][0:64, qq * 64:(qq + 1) * 64], lhsT=S0Tq[:, q, :], rhs=Gq[:, q, :], start=False, stop=(qq == 7),
                                          r=["S0Tq"] + ["TTa0", "TTa1", "TTa2", "TTa3"] + ["TTb0", "TTb1", "TTb2", "TTb3"], w=[PS[bb]], sig=(qq == 7))
                                kb.cp("act" if half else "dve", SLo[:, half * 8:(half + 1) * 8, :], ps[bb][0:64, :].rearrange("p (q k) -> p q k", k=64), r=[PS[bb]], w=["S0nat"])
                            kb.dma("sp", o_swkv[:, h].rearrange("q v k -> v q k"), SLo[:], r=["S0nat"], w=())
                        by = grp4(lambda hl: [(ArkT[:, hl, :], tsl(Vv, hl), ["ArkT%d" % hl, "Vv0"]), (ArbT[:, hl, :], U0[:, hl, :], ["ArbT%d" % hl, "U0%d" % hl]),
                                               (YTs[:, hl, :], identR[0:64, 0:64], ["AakT%d" % hl, "identR"])], 64)
                        kb.cp("act", yb[:], ps[by][:, 0:256], r=[PS[by]], w=["tA1"])

                    def front(i):
                        sample = i == 16
                        p_ = i % 2
                        AL, Vv, RB, KH, BH = ALs[p_], Vvs[p_], RBs[p_], KHs[p_], BHs[p_]
                        fmT, PLc, smallf = fmTs[p_], PLcs[p_], smallfs[p_]
                        Gt = GVs[p_][:, 0:256]
                        rAL, rVv, rRB, rKH, rBH = "AL%d" % p_, "Vv%d" % p_, "RB%d" % p_, "KH%d" % p_, "BH%d" % p_
                        ralT, rbtT, rktT, rrbT = "alT%d" % p_, "btT%d" % p_, "ktT%d" % p_, "rbT%d" % p_
                        rPL, rsf, rGV = "PLc%d" % p_, "smallf%d" % p_, "GV%d" % p_
                        sample = i == 16
                        k_ = i % 2
                        hT = hTg[k_]
                        hres = "hTr%d" % k_
                        last_pass = hg == 3
                        if hg == 0 and i == 1:
                            stop_at(58)
                        if hg == 0 and i == 16:
                            stop_at(59)
                        make_hT(hT, i, prescale=last_pass, col0=1, res=hres, hook=(hook_last if hg == 0 else None))
                        yield
                        if i == 0:
                            kb.memset("pool", hT[:, :, 0:1], 0.0, w=[hres])
                        elif not sample:
                            kb.cp("pool", hT[:, :, 0:1], hTg[1 - k_][:, :, 128:129], r=["hTr%d" % (1 - k_)], w=[hres])
                        cur = hT[:, :, 1:129]
                        if not sample:
                            kb.tt("dve", dx[:], hT[:, :, 0:128], cur, ALU.subtract, r=[hres], w=["dx"])
                        else:
                            kb.cp("pool", dx[:], hT[:, :, 0:128], r=[hres], w=["dx"])
                            kb.cp("pool", dx[:].rearrange("p c (q t) -> p c q t", t=8)[:, :, :, 0], sh0T[:], r=["sh0T"], w=["dx"])
                            kb.tt("pool", dx[:], dx[:], cur, ALU.subtract, r=["dx", hres], w=["dx"])
                        yield
                        for wt, wm_, bank, off in ((wrs, wrm, 0, 0), (wks, wkm, 0, 256), (wvs, wvm, 1, 0)):
                            for kc in range(8):
                                kb.mm(ps[bank][:, off:off + 256], lhsT=hT[:, kc, 1:129], rhs=wt[:, kc, :], start=(kc == 0), stop=False, r=[hres, "wqkv"], w=[PS[bank]], sig=False)
                            for kc in range(8):
                                kb.mm(ps[bank][:, off:off + 256], lhsT=dx[:, kc, :], rhs=wm_[:, kc, :], start=False, stop=(kc == 7), r=["dx", "wqkvm"], w=[PS[bank]])
                        for wt, wm_, wr_, m_, off3 in ((w1b, w1m, "w1b", 64, 0), (a1b, a1m, "a1b", 64, 128), (g1b, g1m, "g1b", 128, 256)):
                            for kc in range(8):
                                kb.mm(ps[3][0:m_, off3:off3 + 128], lhsT=wt[:, kc, :], rhs=hT[:, kc, 1:129], start=(kc == 0), stop=False, r=[hres, wr_], w=[PS[3]], sig=False)
                            for kc in range(8):
                                kb.mm(ps[3][0:m_, off3:off3 + 128], lhsT=wm_[:, kc, :], rhs=dx[:, kc, :], start=False, stop=(kc == 7), r=["dx", wr_ + "m"], w=[PS[3]])
                        kb.act(loT[0:64, 0, :], ps[3][0:64, 0:128], AF.Tanh, r=[PS[3]], w=["loT"])
                        yield
                        kb.cp("act", loT[0:64, 1, :], ps[3][0:64, 128:256], r=[PS[3]], w=["loT"])
                        kb.act(loT[:, 2, :], ps[3][:, 256:384], AF.Sigmoid, r=[PS[3]], w=["loT"])
                        kb.mm(ps[1][:, 256:512], lhsT=loT[0:64, 0, :], rhs=w2s[:], start=True, stop=False, r=["loT", "w2s"], w=[PS[1]], sig=False)
                        yield
                        kb.mm(ps[1][:, 256:512], lhsT=ones[0:1, :], rhs=w0r[:], start=False, stop=True, r=["C", "w0r"], w=[PS[1]])
                        kb.mm(ps[2][:, 0:256], lhsT=loT[0:64, 1, :], rhs=a2s[:], start=True, stop=False, r=["loT", "a2s"], w=[PS[2]], sig=False)
                        kb.mm(ps[2][:, 0:256], lhsT=ones[0:1, :], rhs=a0r[:], start=False, stop=True, r=["C", "a0r"], w=[PS[2]])
                        yield
                        kb.mm(ps[2][:, 256:512], lhsT=loT[:, 2, :], rhs=g2s[:], start=True, stop=True, r=["loT", "g2s"], w=[PS[2]])
                        if hg == 0 and i == 0:
                            stop_at(52)
                        kb.cp("act", Rr, ps[0][:, 0:256], r=[PS[0]], w=[tbres(0)])
                        yield
                        kb.cp("act", Kk, ps[0][:, 256:512], r=[PS[0]], w=[tbres(1)])
                        kb.cp("act", Vv[:], ps[1][:, 0:256], r=[PS[1]], w=[rVv])
                        yield
                        kb.act(SG, ps[1][:, 256:512], AF.Sigmoid, r=[PS[1]], w=[tbres(4)])
                        kb.act(Aa, ps[2][:, 0:256], AF.Sigmoid, r=[PS[2]], w=[tbres(3)])
                        kb.cp("act", Gt, ps[2][:, 256:512], r=[PS[2]], w=[rGV])
                        yield
                        kb.tt("dve", KKn, Kk, bcs["kkb"][:], ALU.mult, r=[tbres(1), "kkb"], w=[tbres(2)])
                        kb.tt("dve", Tt, KKn, KKn, ALU.mult, r=[tbres(2)], w=[tbres(7)])
                        s.add("dve", lambda g_, o_=smallf[:, 0:4], i_=Tt.rearrange("p (h k) -> p h k", k=64): g_.tensor_reduce(out=o_, in_=i_, axis=AX.X, op=ALU.add),
                              r=[tbres(7)], w=[rsf], tag="red")
                        yield
                        kb.ts("dve", smallf[:, 0:4], smallf[:, 0:4], 1e-24, None, ALU.max, None, r=[rsf], w=[rsf])
                        kb.act(smallf[:, 0:4], smallf[:, 0:4], AF.Ln, r=[rsf], w=[rsf])
                        kb.act(smallf[:, 0:4], smallf[:, 0:4], AF.Exp, r=[rsf], w=[rsf], scale=-0.5)
                        yield
                        kb.tt("dve", KKn.rearrange("p (h k) -> p h k", k=64), KKn.rearrange("p (h k) -> p h k", k=64),
                              smallf[:, 0:4].unsqueeze(2).to_broadcast([128, 4, 64]), ALU.mult, r=[tbres(2), rsf], w=[tbres(2)])
                        kb.stt(Tt, Aa, -1.0, bcs["kab"][:], ALU.add, ALU.mult, r=[tbres(3), "kab"], w=[tbres(7)])
                        kb.tt("dve", Tt, Tt, Kk, ALU.mult, r=[tbres(7), tbres(1)], w=[tbres(7)])
                        yield
                        kb.tt("dve", Kk, Kk, Tt, ALU.add, r=[tbres(1), tbres(7)], w=[tbres(1)])
                        kb.tt("dve", Tt, Rr, Kk, ALU.mult, r=[tbres(0), tbres(1)], w=[tbres(7)])
                        kb.tt("dve", Tt, Tt, bcs["rkb"][:], ALU.mult, r=[tbres(7), "rkb"], w=[tbres(7)])
                        yield
                        s.add("dve", lambda g_, o_=smallf[:, 4:8], i_=Tt.rearrange("p (h k) -> p h k", k=64): g_.tensor_reduce(out=o_, in_=i_, axis=AX.X, op=ALU.add),
                              r=[tbres(7)], w=[rsf], tag="red")
                        kb.tt("dve", Aa, KKn, Aa, ALU.mult, r=[tbres(2), tbres(3)], w=[tbres(3)])
                        if hg == 0 and i == 0:
                            stop_at(53)
                        yield
                        Um, Jm = (maskP, ones) if not sample else (maskS, blkS)
                        kb.mm(ps[0][:, 0:256], lhsT=Um, rhs=SG, start=True, stop=True, r=["C", tbres(4)], w=[PS[0]])
                        kb.mm(ps[0][:, 256:512], lhsT=Jm, rhs=SG, start=True, stop=True, r=["C", tbres(4)], w=[PS[0]])
                        yield
                        nq = 1 if not sample else 16
                        for hl in range(4):
                            kb.mm(ps[3][0:64, 384 + hl * nq:384 + (hl + 1) * nq], lhsT=SG[:, hl * 64:(hl + 1) * 64], rhs=(ones[:, 0:1] if not sample else bms),
                                  start=True, stop=True, r=[tbres(4), "C"], w=[PS[3]], sig=(hl == 3))
                        kb.act(PLc[:, 0:4 * nq], ps[3][0:64, 384:384 + 4 * nq], AF.Exp, r=[PS[3]], w=[rPL], scale=CW)
                        yield
                        kb.cp("act", CSs, ps[0][:, 0:256], r=[PS[0]], w=[tbres(5)])
                        kb.tt("dve", Tt, CSs, SG, ALU.subtract, r=[tbres(5), tbres(4)], w=[tbres(7)])
                        kb.act(Ee, Tt, AF.Exp, r=[tbres(7)], w=[tbres(6)], scale=CW)
                        yield
                        kb.stt(AL[:], KKn, -1.0, Ee, ALU.mult, ALU.mult, r=[tbres(2), tbres(6)], w=[rAL])
                        kb.act(Ee, CSs, AF.Exp, r=[tbres(5), rAL], w=[tbres(6)], scale=-CW)
                        kb.tt("dve", BT[:], Aa, Ee, ALU.mult, r=[tbres(3), tbres(6)], w=["BTt"])
                        yield
                        kb.tt("dve", KT[:], Kk, Ee, ALU.mult, r=[tbres(1), tbres(6)], w=["X1kt"])
                        kb.act(Tt, CSs, AF.Exp, r=[tbres(5)], w=[tbres(7)], scale=CW)
                        kb.tt("dve", RB[:], Rr, Tt, ALU.mult, r=[tbres(0), tbres(7)], w=[rRB])
                        yield
                        kb.tt("dve", Ee, ps[0][:, 256:512], CSs, ALU.subtract, r=[PS[0], tbres(5), rGV, "X1kt"], w=[tbres(6)])
                        kb.act(Ee, Ee, AF.Exp, r=[tbres(6)], w=[tbres(6)], scale=CW)
                        kb.tt("dve", KH[:], Kk, Ee, ALU.mult, r=[tbres(1), tbres(6)], w=[rKH])
                        yield
                        kb.tt("dve", BH[:], Aa, Ee, ALU.mult, r=[tbres(3), tbres(6)], w=[rBH])
                        for qi, (nm, src, sr) in enumerate((("alT", AL, rAL), ("btT", BT, "BTt"), ("ktT", KT, "X1kt"), ("rbT", RB, rRB))):
                            b_ = 1 + qi % 2
                            for hl in range(4):
                                kb.tr(psb[b_][0:64, hl * 128:(hl + 1) * 128], src[:, hl * 64:(hl + 1) * 64], identR[:],
                                      r=[sr, "identR"], w=[PS[b_]], sig=(hl == 3))
                            kb.cp("act" if qi % 2 else "dve", fmT[nm][:].rearrange("p a b -> p (a b)"), psb[b_][0:64, 0:512], r=[PS[b_]], w=["%s%d" % (nm, p_)])
                        if hg == 0 and i == 0:
                            stop_at(54)

                    def back(i, fg):
                        sample = i == 16
                        p_ = i % 2
                        AL, Vv, RB, KH, BH = ALs[p_], Vvs[p_], RBs[p_], KHs[p_], BHs[p_]
                        fmT, PLc, smallf = fmTs[p_], PLcs[p_], smallfs[p_]
                        Gt = GVs[p_][:, 0:256]
                        rAL, rVv, rRB, rKH, rBH = "AL%d" % p_, "Vv%d" % p_, "RB%d" % p_, "KH%d" % p_, "BH%d" % p_
                        ralT, rbtT, rktT, rrbT = "alT%d" % p_, "btT%d" % p_, "ktT%d" % p_, "rbT%d" % p_
                        rPL, rsf, rGV = "PLc%d" % p_, "smallf%d" % p_, "GV%d" % p_
                        alT, btT, ktT, rbT = fmT["alT"], fmT["btT"], fmT["ktT"], fmT["rbT"]
                        mlow, mup, minc = (lowP, upP, maskP) if not sample else (lowS, upS, maskS)
                        n_it = 3 if not sample else 2

                        def head_alg(hl):
                            B_ = 4 + hl
                            P_ = PS[B_]
                            rn = lambda nm: "%s%d" % (nm, hl)
                            pw = ps[B_][:, 0:128]

                            def mm1(lt, rh, rd, cols=128, rows=128, first=True, last=True):
                                kb.mm(ps[B_][0:rows, 0:cols], lhsT=lt, rhs=rh, start=first, stop=last, r=rd, w=[P_], sig=last)

                            mm1(fsl(alT, hl), fsl(btT, hl), [ralT, rbtT])
                            if not sample:
                                kb.tt("dve", chn["Am"][:, hl, :], pw, lowP, ALU.mult, r=[P_, "C"], w=[rn("Am")])
                                kb.tt("dve", chn["ApA"][:, hl, :], pw, low16, ALU.mult, r=[P_, "C"], w=[rn("ApA")])
                            else:
                                kb.tt("dve", chn["ApA"][:, hl, :], pw, lowS, ALU.mult, r=[P_, "C"], w=[rn("ApA")])
                            yield
                            mm1(fsl(btT, hl), fsl(alT, hl), [ralT, rbtT])
                            kb.tt("dve", chn["BpA"][:, hl, :], pw, (up16 if not sample else upS), ALU.mult, r=[P_, "C"], w=[rn("BpA")])
                            yield
                            mm1(fsl(ktT, hl), fsl(alT, hl), [ralT, rktT])
                            kb.tt("dve", AakT[:, hl, :], pw, mup, ALU.mult, r=[P_, "C"], w=[rn("AakT")])
                            yield
                            mm1(fsl(btT, hl), fsl(rbT, hl), [rrbT, rbtT])
                            kb.tt("dve", ArbT[:, hl, :], pw, minc, ALU.mult, r=[P_, "C"], w=[rn("ArbT")])
                            yield
                            mm1(fsl(ktT, hl), fsl(rbT, hl), [rrbT, rktT])
                            kb.tt("dve", ArkT[:, hl, :], pw, minc, ALU.mult, r=[P_, "C"], w=[rn("ArkT")])
                            yield
                            kb.tt("dve", chn["TTa"][:, hl, :], chn["BpA"][:, hl, :], ident, ALU.add, r=[rn("BpA"), "C"], w=[rn("TTa")])
                            Ap, Bp, TT = "ApA", "BpA", "TTa"
                            for it in range(n_it):
                                Ap2 = "ApB" if Ap == "ApA" else "ApA"
                                Bp2 = "BpB" if Bp == "BpA" else "BpA"
                                TT2 = "TTb" if TT == "TTa" else "TTa"
                                mm1(chn[Bp][:, hl, :], chn[Ap][:, hl, :], [rn(Ap), rn(Bp)])
                                kb.cp("act", chn[Ap2][:, hl, :], pw, r=[P_], w=[rn(Ap2)])
                                yield
                                if it < n_it - 1:
                                    mm1(chn[Ap][:, hl, :], chn[Bp][:, hl, :], [rn(Ap), rn(Bp)])
                                    kb.cp("act", chn[Bp2][:, hl, :], pw, r=[P_], w=[rn(Bp2)])
                                    yield
                                mm1(chn[Ap2][:, hl, :], chn[TT][:, hl, :], [rn(Ap2), rn(TT)])
                                kb.tt("dve", chn[TT2][:, hl, :], pw, chn[TT][:, hl, :], ALU.add, r=[P_, rn(TT)], w=[rn(TT2)])
                                yield
                                Ap, Bp, TT = Ap2, Bp2, TT2
                            if not sample:
                                for mk in (m16, m32, m64):
                                    TT2 = "TTb" if TT == "TTa" else "TTa"
                                    kb.tr(psb[B_][:, 0:128], chn[TT][:, hl, :], identR[:], r=[rn(TT), "identR"], w=[P_])
                                    kb.cp("act", chn["ApB"][:, hl, :], psb[B_][:, 0:128], r=[P_], w=[rn("ApB")])
                                    kb.tt("dve", chn["ApA"][:, hl, :], chn["Am"][:, hl, :], mk, ALU.mult, r=[rn("Am"), "C"], w=[rn("ApA")])
                                    yield
                                    mm1(chn["ApA"][:, hl, :], chn[TT][:, hl, :], [rn("ApA"), rn(TT)])
                                    kb.cp("act", chn["BpA"][:, hl, :], pw, r=[P_], w=[rn("BpA")])
                                    yield
                                    mm1(chn["ApB"][:, hl, :], chn["BpA"][:, hl, :], [rn("ApB"), rn("BpA")])
                                    kb.tt("dve", chn[TT2][:, hl, :], pw, chn[TT][:, hl, :], ALU.add, r=[P_, rn(TT)], w=[rn(TT2)])
                                    yield
                                    TT = TT2
                            TTh = chn[TT][:, hl, :]
                            mm1(TTh, tsl(AL, hl), [rn(TT), rAL], cols=64)
                            kb.cp("act", Ahat[:, hl, :], ps[B_][:, 0:64], r=[P_], w=[rn("Ahat")])
                            yield
                            mm1(AakT[:, hl, :], tsl(Vv, hl), [rn("AakT"), rVv], cols=64)
                            kb.cp("dve", X1[:, hl, :], ps[B_][:, 0:64], r=[P_], w=[rn("X1")])
                            yield
                            mm1(TTh, X1[:, hl, :], [rn(TT), rn("X1")], cols=64)
                            kb.cp("act", U0[:, hl, :], ps[B_][:, 0:64], r=[P_], w=[rn("U0")])
                            yield
                            mm1(tsl(RB, hl), identR[:], [rRB, "identR"], rows=64, last=False)
                            mm1(Ahat[:, hl, :], ArbT[:, hl, :], [rn("Ahat"), rn("ArbT")], rows=64, first=False)
                            kb.cp("dve", RhT[:, hl, :], ps[B_][0:64, 0:128], r=[P_], w=[rn("RhT")])
                            yield
                            if sample:
                                return
                            mm1(Ahat[:, hl, :], tsl(BH, hl), [rn("Ahat"), rBH], cols=64, rows=64)
                            kb.stt(Gm[:, hl, :], ident[0:64, 0:64], PLc[:, hl:hl + 1], ps[B_][0:64, 0:64], ALU.mult, ALU.add, r=["C", rPL, P_], w=[rn("Gm")])
                            yield
                            mm1(ArkT[:, hl, :], tsl(Vv, hl), [rn("ArkT"), rVv], cols=64, last=False)
                            mm1(ArbT[:, hl, :], U0[:, hl, :], [rn("ArbT"), rn("U0")], cols=64, first=False, last=False)
                            mm1(RhT[:, hl, :], S0T[:, hl, :], [rn("RhT"), rn("S0T")], cols=64, first=False)
                            kb.cp("act", yb[:, hl * 64:(hl + 1) * 64], ps[B_][:, 0:64], r=[P_], w=["tA1"])
                            yield
                            if i == 15:
                                mm1(tsl(Vv, hl), tsl(KH, hl), [rVv, rKH], cols=64, rows=64, last=False)
                                mm1(U0[:, hl, :], tsl(BH, hl), [rn("U0"), rBH], cols=64, rows=64, first=False, last=False)
                                mm1(S0T[:, hl, :], Gm[:, hl, :], [rn("S0T"), rn("Gm")], cols=64, rows=64, first=False)
                                kb.cp("dve", SLo[:, hl, :], ps[B_][0:64, 0:64], r=[P_], w=["S0nat"])
                                yield
                            mm1(tsl(KH, hl), tsl(Vv, hl), [rVv, rKH], cols=64, rows=64, last=False)
                            mm1(tsl(BH, hl), U0[:, hl, :], [rn("U0"), rBH], cols=64, rows=64, first=False, last=False)
                            mm1(Gm[:, hl, :], S0T[:, hl, :], [rn("S0T"), rn("Gm")], cols=64, rows=64, first=False)
                            kb.cp("dve", S0T[:, hl, :], ps[B_][0:64, 0:64], r=[P_], w=[rn("S0T")])
                            yield

                        gens = [head_alg(hl) for hl in range(4)] + ([fg] if fg is not None else [])
                        while gens:
                            for g_ in list(gens):
                                try:
                                    next(g_)
                                except StopIteration:
                                    gens.remove(g_)
                        if not sample:
                            if i == 15:
                                kb.dma("sp", o_pwkv[hg * 4:hg * 4 + 4].rearrange("h v k -> v h k"), SLo[:, 0:4, :], r=["S0nat"], w=())
                        else:
                            sample_states()
                        if hg == 0 and i == 0:
                            stop_at(57)
                        if hg == 0 and i == 16:
                            stop_at(60)
                        for hl in range(4):
                            s.add("dve", lambda g_, o_=small[:, 8 + hl * 6:14 + hl * 6], i_=yb[:, hl * 64:(hl + 1) * 64]: g_.bn_stats(out=o_, in_=i_), r=["tA1"], w=["tA1"], tag="bnst")
                        for hl in range(4):
                            s.add("dve", lambda g_, o_=small[:, 32 + hl * 2:34 + hl * 2], i_=small[:, 8 + hl * 6:14 + hl * 6]: g_.bn_aggr(out=o_, in_=i_), r=["tA1"], w=["tA1"], tag="bnag")
                        mvv = small[:, 32:40].rearrange("p (h two) -> p h two", two=2)
                        kb.act(small[:, 40:44].unsqueeze(2), mvv[:, :, 1:2], AF.Ln, r=["tA1"], w=["tA1"], bias=64e-5, scale=1.0)
                        kb.act(small[:, 40:44], small[:, 40:44], AF.Exp, r=["tA1"], w=["tA1"], scale=-0.5)
                        kb.tt("dve", small[:, 44:48].unsqueeze(2), mvv[:, :, 0:1], small[:, 40:44].unsqueeze(2), ALU.mult, r=["tA1"], w=["tA1"])
                        kb.ts("dve", small[:, 44:48], small[:, 44:48], -1.0, None, ALU.mult, None, r=["tA1"], w=["tA1"])
                        for hl in range(4):
                            kb.act(yb[:, hl * 64:(hl + 1) * 64], yb[:, hl * 64:(hl + 1) * 64], AF.Identity, r=["tA1", "tA1"], w=["tA1"],
                                   bias=small[:, 44 + hl:45 + hl], scale=small[:, 40 + hl:41 + hl])
                        kb.tt("dve", yb[:], yb[:], bcs["lngb"][:], ALU.mult, r=["tA1", "lngb"], w=["tA1"])
                        kb.tt("dve", yb[:], yb[:], bcs["lnbb"][:], ALU.add, r=["tA1", "lnbb"], w=["tA1"])
                        kb.tt("dve", Tbk[:].rearrange("p (h k) -> p h k", k=64), Vv[:].rearrange("p (h k) -> p h k", k=64),
                              smallf[:, 4:8].unsqueeze(2).to_broadcast([128, 4, 64]), ALU.mult, r=[rVv, rsf], w=["Tbk"])
                        kb.tt("dve", yb[:], yb[:], Tbk[:], ALU.add, r=["tA1", "Tbk"], w=["tA1"])
                        kb.tt("dve", yb[:], yb[:], Gt, ALU.mult, r=["tA1", rGV], w=["tA1"])
                        for cc in range(2):
                            kb.tr(ps[7][:, cc * 128:(cc + 1) * 128], yb[:, cc * 128:(cc + 1) * 128], ident, r=["tA1", "C"], w=[PS[7]], sig=(cc == 1))
                        kb.cp("act", ygT[:, 2 * hg:2 * hg + 2, i * 128:(i + 1) * 128], ps[7][:, 0:256].rearrange("p (c t) -> p c t", t=128), r=[PS[7]], w=["ygT%d" % i])
                    def run_gen(g_):
                        for _ in g_:
                            pass

                    run_gen(front(0))
                    for i in range(NT):
                        back(i, front(i + 1) if i + 1 < NT else None)
                    s.barrier()
            for c in range(8):
                kb.tr(ps[0][0:17, c * 128:(c + 1) * 128] if c < 4 else ps[1][0:17, (c - 4) * 128:(c - 3) * 128], hlast[:, c, :], ident, r=["hlast", "C"],
                      w=[PS[0] if c < 4 else PS[1]], sig=(c in (3, 7)))
            kb.cp("dve", tB[0][0:17, 0:512], ps[0][0:17, :], r=[PS[0]], w=["tB0"])
            kb.cp("dve", tB[0][0:17, 512:1024], ps[1][0:17, :], r=[PS[1]], w=["tB0"])
            kb.dma("sp", o_pshift, tB[0][0:1, :], r=["tB0"], w=())
            kb.dma("sp", o_sshift, tB[0][1:17, :], r=["tB0"], w=())
            s.barrier()
            with contextlib.ExitStack() as pc_:
                alloc_gl(pc_)
                mod_prepare(l, 1, 1.0, blocks=[4, 5])
                Gp, Gs = gl["Gp"], gl["Gs"]
                wout = kb.sb(pc_, "wo_sb", [128, 8, D], BF16)
                kb.dma("pool", wout[:], rw_wo[0].rearrange("(kc p) n -> p kc n", p=128), r=(), w=["wout"])
                proj_acc(wout, "wout", 8, lambda i, kc: (ygT[:, kc, i * 128:(i + 1) * 128], "ygT%d" % i), True, True)
                s.barrier()
        s.barrier()

    stored = set()

    def store_tile(i):
        stored.add(i)
        kb.dma("sp", yout[i * 128:(i + 1) * 128, :], X[:, i, :], r=["X%d" % i], w=())

    def dump_and_finish():
        for i in range(NT):
            if i not in stored:
                kb.dma("sp", yout[i * 128:(i + 1) * 128, :], X[:, i, :], r=["X%d" % i], w=())

    stage = 0
    for l in range(2):
        for sub in range(3):
            if sub == 0:
                ffn(l, 0, 0, 0.5)
            elif sub == 2:
                ffn(l, 1, 2, 0.5)
            elif l == 0:
                ab_mixer(l)
            else:
                rwkv_mixer(l)
            stage += 1
            if stage >= upto:
                return dump_and_finish()
    dump_and_finish()


def build(upto=99, stop_point=None):
    kb = KB()
    kb.stop_point = stop_point
    with contextlib.ExitStack() as es:
        kb.es = es
        build_program(kb, upto)
        kb.s.emit(kb.nc, es)
    return kb


def core_inputs(inp, core, kb):
    sl = slice(16 * core, 16 * core + 16)
    xs = inp["x_sample"][sl].reshape(128, D)
    f32 = np.float32

    def fm(v):
        return np.ascontiguousarray(np.asarray(v, f32).reshape(-1, 128).T)

    m = {
        "x": np.concatenate([inp["x_prompt"][core], xs], axis=0),
        "c": np.concatenate([inp["c_prompt"][core:core + 1], inp["c_sample"][sl]], axis=0),
        "cst": CST_ARR,
    }
    cw = inp["rg_conv_w"][0]
    vecA = np.zeros((128, 32), f32)
    for c in range(4):
        for j in range(4):
            vecA[:, c * 4 + j] = cw[j, c * 128:(c + 1) * 128]
    vecA[:, 16:20] = fm(inp["rg_conv_b"][0])
    vecA[:, 20:24] = fm(inp["rg_b_a"][0])
    vecA[:, 24:28] = fm(inp["rg_b_x"][0])
    vecA[:, 28:32] = fm(inp["rg_lambda"][0])
    m["vecA"] = vecA
    m["bgT"] = np.ascontiguousarray(inp["mlstm_b_gates"][0].T)
    m["minitT"] = np.ascontiguousarray(inp["state_mlstm_m"][0, sl].T)
    m["smC"] = inp["state_mlstm_C"][0, sl]
    m["smn"] = inp["state_mlstm_n"][0, sl]
    m["srh"] = inp["state_rglru_h"][0, sl]
    m["srconv"] = inp["state_rglru_conv"][0, sl].reshape(48, 512)
    mu = inp["rw_mu"][0]
    muT = np.zeros((128, 48), f32)
    for j in range(6):
        muT[:, j * 8:(j + 1) * 8] = fm(mu[j])
    m["muT"] = muT
    m["rk_flat"] = inp["rw_r_k"].reshape(1, D)
    m["swkv"] = inp["state_rwkv_wkv"][0, sl]
    m["sshift"] = inp["state_rwkv_shift"][0, sl]
    for k in kb.dram:
        if k not in m and k in inp:
            m[k] = inp[k]
    return {k: np.ascontiguousarray(v, dtype=f32) for k, v in m.items() if k in kb.dram}


_CACHE = {}


def kernel(**inputs):
    inp = {k: np.asarray(v) for k, v in inputs.items()}
    if "kb" not in _CACHE:
        _CACHE["kb"] = build()
    kb = _CACHE["kb"]
    in_maps = [core_inputs(inp, c, kb) for c in range(NCORES)]
    res = run_bass_kernel_spmd(kb.nc, in_maps, core_ids=list(range(NCORES)))
    R = res.results
    f32 = np.float32
    cat = lambda key, f=(lambda a: a): np.stack([f(np.asarray(R[c][key], f32)) for c in range(NCORES)], axis=0)
    cats = lambda key, f=(lambda a: a): np.concatenate([f(np.asarray(R[c][key], f32)) for c in range(NCORES)], axis=0)
    y_prompt = cat("y", lambda a: a[:2048])
    y_sample = cats("y", lambda a: a[2048:].reshape(16, 8, D))
    outs = (
        y_prompt, y_sample,
        cat("o_pmC")[None], cat("o_pmn")[None], cat("o_pmm", lambda a: a[:, 0])[None],
        cat("o_prh", lambda a: a.reshape(512))[None], cat("o_prconv")[None],
        cat("o_pwkv")[None], cat("o_pshift", lambda a: a[0])[None],
        cats("o_smC")[None], cats("o_smn")[None], cats("o_smm", lambda a: a.T)[None],
        cats("o_srh")[None], cats("o_srconv", lambda a: a.reshape(16, 3, 512))[None],
        cats("o_swkv")[None], cats("o_sshift")[None],
    )
    return tuple(np.ascontiguousarray(o, dtype=f32) for o in outs)
```
